# Optimizing a Trainium2 kernel written in Bass

```python
import math
import jax
import jax.numpy as jnp
from jax import lax
import numpy as np

D_MODEL = 1024
BATCH = 16
SEQ = 256
DEPTH = 4
DEC_BATCH = 2
DEC_SEQ = 2048
PAST_LEN = 256

GRID_W = 64
GROUP_W = D_MODEL // 2
D_MIX = 4 * GROUP_W
CHUNK = 128
Q_BLOCK = 128
D_CONV = 5
ROPE_THETA = 10000.0
ROPE_DIM = 64
EPS = 1e-6

SSM_HEADS = 8
SSM_HEAD_DIM = GROUP_W // SSM_HEADS
SSM_GROUPS = 2
SSM_STATE = 64
SSM_CONV_CH = GROUP_W + 2 * SSM_GROUPS * SSM_STATE
A_COLS = GROUP_W + SSM_CONV_CH + 2 * SSM_HEADS
DIFF_HEADS = 4
DIFF_QK_DIM = 64
DIFF_V_DIM = GROUP_W // DIFF_HEADS
B_QK = DIFF_HEADS * 2 * DIFF_QK_DIM
B_COLS = 2 * B_QK + GROUP_W
MLSTM_HEADS = 4
MLSTM_HEAD_DIM = GROUP_W // MLSTM_HEADS
C_COLS = 4 * GROUP_W + 4 * MLSTM_HEADS
GQA_HEADS = 8
GQA_KV_HEADS = 2
GQA_HEAD_DIM = GROUP_W // GQA_HEADS
GQA_GROUP = GQA_HEADS // GQA_KV_HEADS
D_COLS = GROUP_W + 2 * GQA_KV_HEADS * GQA_HEAD_DIM

IN_COLS = A_COLS + B_COLS + C_COLS + D_COLS
D_FF = -(-8 * D_MODEL // (3 * 256)) * 256

kernel_name = 'hybrid_diffusion_prefix_trunk_step'


def rms_norm(x):
    xf = x.astype(jnp.float32)
    return (xf * lax.rsqrt(jnp.mean(xf * xf, axis=-1, keepdims=True) + EPS)).astype(x.dtype)


def grid_rope(n_tok, head_dim):
    rows = n_tok // GRID_W
    r, c = jnp.meshgrid(jnp.arange(rows, dtype=jnp.float32), jnp.arange(GRID_W, dtype=jnp.float32), indexing='ij')
    nf = head_dim // 4
    freqs = ROPE_THETA ** (-jnp.arange(nf, dtype=jnp.float32) / nf)
    ang = jnp.stack([r.reshape(-1)[:, None] * freqs, c.reshape(-1)[:, None] * freqs], axis=1)
    return jnp.cos(ang), jnp.sin(ang)


def apply_rope(x, rope):
    cos, sin = rope
    b, L, h, d = x.shape
    xs = x.reshape(b, L, h, 2, 2, d // 4)
    cs = cos[None, :, None].astype(x.dtype)
    sn = sin[None, :, None].astype(x.dtype)
    x1, x2 = xs[..., 0, :], xs[..., 1, :]
    return jnp.stack([x1 * cs - x2 * sn, x2 * cs + x1 * sn], axis=-2).reshape(b, L, h, d)


def dwconv(x, w, bias):
    L = x.shape[1]
    pad = D_CONV // 2
    xp = jnp.pad(x, ((0, 0), (pad, pad), (0, 0)))
    out = bias
    for tap in range(D_CONV):
        out = out + w[tap] * xp[:, tap:tap + L]
    return out


def flip(t):
    return jnp.flip(t, axis=1)


def sweep_q_blocks(fn, q):
    b, L = q.shape[:2]
    nb = L // Q_BLOCK
    qb = jnp.moveaxis(q.reshape(b, nb, Q_BLOCK, *q.shape[2:]), 1, 0)
    out = lax.map(fn, qb)
    return jnp.moveaxis(out, 0, 1).reshape(b, L, *out.shape[3:])


def ssd_scan(x, dt, a_neg, bm, cm, h0):
    b, L, H, P = x.shape
    nc = L // CHUNK
    rep = H // bm.shape[2]
    ch = lambda t: t.reshape(b, nc, CHUNK, *t.shape[2:])
    xc, dtc = ch(x), ch(dt)
    bc, cc = ch(jnp.repeat(bm, rep, axis=2)), ch(jnp.repeat(cm, rep, axis=2))
    cum = jnp.cumsum(dtc * a_neg, axis=2)
    causal = jnp.tril(jnp.ones((CHUNK, CHUNK), dtype=bool))[:, :, None]
    decay = jnp.exp(jnp.where(causal, cum[:, :, :, None] - cum[:, :, None], -jnp.inf))
    xdt = xc * dtc[..., None]
    scores = jnp.einsum('bcihn,bcjhn->bcijh', cc, bc) * decay
    y_diag = jnp.einsum('bcijh,bcjhp->bcihp', scores, xdt)
    last = cum[:, :, -1]
    w_end = jnp.exp(last[:, :, None] - cum)
    chunk_state = jnp.einsum('bcjh,bcjhn,bcjhp->bchpn', w_end, bc, xdt)

    def step(h, inp):
        st, dec = inp
        return jnp.exp(dec)[..., None, None] * h + st, h

    h_fin, h_prev = lax.scan(step, h0.astype(jnp.float32), (jnp.moveaxis(chunk_state, 1, 0), jnp.moveaxis(last, 1, 0)))
    h_prev = jnp.moveaxis(h_prev, 0, 1)
    y_off = jnp.einsum('bcihn,bchpn->bcihp', cc, h_prev) * jnp.exp(cum)[..., None]
    return (y_diag + y_off).reshape(b, L, H, P), h_fin


def mlstm_scan(q, k, v, li, lf, state):
    b, L, H, dh = q.shape
    nc = L // CHUNK
    ch = lambda t: t.reshape(b, nc, CHUNK, *t.shape[2:])
    qc, kc, vc, lic, lfc = ch(q), ch(k), ch(v), ch(li), ch(lf)
    cum = jnp.cumsum(lfc, axis=2)
    causal = jnp.tril(jnp.ones((CHUNK, CHUNK), dtype=bool))[:, :, None]
    dmat = jnp.where(causal, cum[:, :, :, None] - cum[:, :, None] + lic[:, :, None], -jnp.inf)
    last = cum[:, :, -1]
    g_end = last[:, :, None] - cum + lic
    m_loc = jnp.max(g_end, axis=2)
    w_end = jnp.exp(g_end - m_loc[:, :, None])
    c_loc = jnp.einsum('bcjh,bcjhd,bcjhe->bchde', w_end, kc, vc)
    n_loc = jnp.einsum('bcjh,bcjhd->bchd', w_end, kc)

    def step(carry, inp):
        c_p, n_p, m_p = carry
        c_l, n_l, m_l, a_l = inp
        m_new = jnp.maximum(a_l + m_p, m_l)
        s_p = jnp.exp(a_l + m_p - m_new)
        s_l = jnp.exp(m_l - m_new)
        new = (s_p[..., None, None] * c_p + s_l[..., None, None] * c_l, s_p[..., None] * n_p + s_l[..., None] * n_l, m_new)
        return new, carry

    init = tuple(s.astype(jnp.float32) for s in state)
    final, prev = lax.scan(step, init, tuple(jnp.moveaxis(t, 1, 0) for t in (c_loc, n_loc, m_loc, last)))
    c_p, n_p, m_p = (jnp.moveaxis(t, 0, 1) for t in prev)
    inter = cum + m_p[:, :, None]
    m_t = jnp.maximum(inter, jnp.max(dmat, axis=3))
    w_intra = jnp.exp(dmat - m_t[:, :, :, None])
    w_inter = jnp.exp(inter - m_t)
    s = jnp.einsum('bcihd,bcjhd->bcijh', qc, kc) * w_intra
    num = jnp.einsum('bcijh,bcjhe->bcihe', s, vc) + w_inter[..., None] * jnp.einsum('bcihd,bchde->bcihe', qc, c_p)
    den = jnp.sum(s, axis=3) + w_inter * jnp.einsum('bcihd,bchd->bcih', qc, n_p)
    h = num / jnp.maximum(jnp.abs(den), jnp.exp(-m_t))[..., None]
    return h.reshape(b, L, H, dh), final


def mixer_ssd(u, conv_w, conv_b, a_log, dt_bias, d_skip, norm_g, h0):
    b, L, _ = u.shape
    z, xbc, dt_raw = jnp.split(u, [GROUP_W, GROUP_W + SSM_CONV_CH], axis=-1)
    xbc = jax.nn.silu(dwconv(xbc, conv_w, conv_b))
    xs, bs, cs = jnp.split(xbc, [GROUP_W, GROUP_W + SSM_GROUPS * SSM_STATE], axis=-1)
    x = xs.reshape(b, L, SSM_HEADS, SSM_HEAD_DIM)
    bm = bs.reshape(b, L, SSM_GROUPS, SSM_STATE)
    cm = cs.reshape(b, L, SSM_GROUPS, SSM_STATE)
    dt = jax.nn.softplus((dt_raw.reshape(b, L, 2, SSM_HEADS) + dt_bias).astype(jnp.float32))
    a = -jnp.exp(a_log.astype(jnp.float32))
    y_f, h_f = ssd_scan(x, dt[:, :, 0], a[0], bm, cm, h0[:, 0])
    y_b, h_b = ssd_scan(flip(x), flip(dt[:, :, 1]), a[1], flip(bm), flip(cm), h0[:, 1])
    y = y_f + flip(y_b) + d_skip[:, None] * x
    y = y.reshape(b, L, GROUP_W).astype(u.dtype) * jax.nn.silu(z)
    return rms_norm(y) * norm_g, jnp.stack([h_f, h_b], axis=1).astype(u.dtype)


def mixer_diff(u, lq1, lk1, lq2, lk2, lam_init, rope, ctx):
    b, L, _ = u.shape
    q, k, v = jnp.split(u, [B_QK, 2 * B_QK], axis=-1)
    q = q.reshape(b, L, DIFF_HEADS, 2, DIFF_QK_DIM)
    k = k.reshape(b, L, DIFF_HEADS, 2, DIFF_QK_DIM)
    v = v.reshape(b, L, DIFF_HEADS, DIFF_V_DIM)
    new_ctx = (k, v)
    if rope is not None:
        q = apply_rope(q.reshape(b, L, 2 * DIFF_HEADS, DIFF_QK_DIM), rope).reshape(q.shape)
        k = apply_rope(k.reshape(b, L, 2 * DIFF_HEADS, DIFF_QK_DIM), rope).reshape(k.shape)
    if ctx is not None:
        k = jnp.concatenate([ctx[0].astype(k.dtype), k], axis=1)
        v = jnp.concatenate([ctx[1].astype(v.dtype), v], axis=1)
    lam = (jnp.exp(jnp.sum(lq1 * lk1)) - jnp.exp(jnp.sum(lq2 * lk2)) + lam_init).astype(jnp.float32)
    scale = DIFF_QK_DIM ** -0.5

    def block(qb):
        s = jnp.einsum('bqhrd,bkhrd->bhrqk', qb, k).astype(jnp.float32) * scale
        p = jax.nn.softmax(s, axis=-1)
        att = p[:, :, 0] - lam * p[:, :, 1]
        return jnp.einsum('bhqk,bkhe->bqhe', att.astype(v.dtype), v)

    o = rms_norm(sweep_q_blocks(block, q)) * (1.0 - lam_init)
    return o.reshape(b, L, GROUP_W), new_ctx


def mixer_mlstm(u, conv_w, conv_b, gate_b, norm_g, state):
    b, L, _ = u.shape
    qk, v, o, gates = jnp.split(u, [2 * GROUP_W, 3 * GROUP_W, 4 * GROUP_W], axis=-1)
    qk = jax.nn.silu(dwconv(qk, conv_w, conv_b))
    q, k = jnp.split(qk, 2, axis=-1)
    shp = (b, L, MLSTM_HEADS, MLSTM_HEAD_DIM)
    q = q.reshape(shp) * MLSTM_HEAD_DIM ** -0.5
    k = k.reshape(shp)
    v = v.reshape(shp)
    g = (gates.reshape(b, L, 2, 2, MLSTM_HEADS) + gate_b).astype(jnp.float32)
    li = g[:, :, :, 0]
    lf = jax.nn.log_sigmoid(g[:, :, :, 1])
    c0, n0, m0 = state
    h_f, (cf, nf, mf) = mlstm_scan(q, k, v, li[:, :, 0], lf[:, :, 0], (c0[:, 0], n0[:, 0], m0[:, 0]))
    h_b, (cb, nb, mb) = mlstm_scan(flip(q), flip(k), flip(v), flip(li[:, :, 1]), flip(lf[:, :, 1]), (c0[:, 1], n0[:, 1], m0[:, 1]))
    h = rms_norm((h_f + flip(h_b)).astype(u.dtype)).reshape(b, L, GROUP_W) * norm_g
    h = jax.nn.sigmoid(o) * h
    new_state = (jnp.stack([cf, cb], axis=1).astype(u.dtype), jnp.stack([nf, nb], axis=1).astype(u.dtype), jnp.stack([mf, mb], axis=1).astype(u.dtype))
    return h, new_state


def mixer_gqa(u, qn_g, kn_g, rope, ctx):
    b, L, _ = u.shape
    kvw = GQA_KV_HEADS * GQA_HEAD_DIM
    q, k, v = jnp.split(u, [GROUP_W, GROUP_W + kvw], axis=-1)
    q = rms_norm(q.reshape(b, L, GQA_HEADS, GQA_HEAD_DIM)) * qn_g
    k = rms_norm(k.reshape(b, L, GQA_KV_HEADS, GQA_HEAD_DIM)) * kn_g
    v = v.reshape(b, L, GQA_KV_HEADS, GQA_HEAD_DIM)
    new_ctx = (k, v)
    if rope is not None:
        q = apply_rope(q, rope)
        k = apply_rope(k, rope)
    if ctx is not None:
        k = jnp.concatenate([ctx[0].astype(k.dtype), k], axis=1)
        v = jnp.concatenate([ctx[1].astype(v.dtype), v], axis=1)
    q = q.reshape(b, L, GQA_KV_HEADS, GQA_GROUP, GQA_HEAD_DIM)
    scale = GQA_HEAD_DIM ** -0.5

    def block(qb):
        s = jnp.einsum('bqhgd,bkhd->bhgqk', qb, k).astype(jnp.float32) * scale
        p = jax.nn.softmax(s, axis=-1)
        return jnp.einsum('bhgqk,bkhd->bqhgd', p.astype(v.dtype), v)

    o = sweep_q_blocks(block, q)
    return o.reshape(b, L, GROUP_W), new_ctx


def trunk_layer(x, mod, p, l, rope, ctx):
    b = x.shape[0]
    sh1, sc1, g1, sh2, sc2, g2 = jnp.split(mod, 6, axis=-1)
    h = (rms_norm(x) * p['norm1'][l]) * (1.0 + sc1) + sh1
    u = h @ p['w_in'][l]
    u_a, u_b, u_c, u_d = jnp.split(u, [A_COLS, A_COLS + B_COLS, A_COLS + B_COLS + C_COLS], axis=-1)
    if ctx is None:
        ssm0 = jnp.zeros((b, 2, SSM_HEADS, SSM_HEAD_DIM, SSM_STATE), x.dtype)
        mst0 = (jnp.zeros((b, 2, MLSTM_HEADS, MLSTM_HEAD_DIM, MLSTM_HEAD_DIM), x.dtype),
                jnp.zeros((b, 2, MLSTM_HEADS, MLSTM_HEAD_DIM), x.dtype),
                jnp.zeros((b, 2, MLSTM_HEADS), x.dtype))
        diff_ctx = None
        gqa_ctx = None
    else:
        ssm0 = ctx['ssm']
        mst0 = (ctx['mlstm_c'], ctx['mlstm_n'], ctx['mlstm_m'])
        diff_ctx = (ctx['diff_k'], ctx['diff_v'])
        gqa_ctx = (ctx['gqa_k'], ctx['gqa_v'])
    lam_init = 0.8 - 0.6 * math.exp(-0.3 * l)
    y_a, ssm_t = mixer_ssd(u_a, p['conv_ssd_w'][l], p['conv_ssd_b'][l], p['ssd_a_log'][l], p['ssd_dt_bias'][l], p['ssd_d'][l], p['ssd_norm'][l], ssm0)
    y_b, diff_kv = mixer_diff(u_b, p['diff_lq1'][l], p['diff_lk1'][l], p['diff_lq2'][l], p['diff_lk2'][l], lam_init, rope, diff_ctx)
    y_c, mst_t = mixer_mlstm(u_c, p['conv_mlstm_w'][l], p['conv_mlstm_b'][l], p['mlstm_gate_b'][l], p['mlstm_norm'][l], mst0)
    y_d, gqa_kv = mixer_gqa(u_d, p['gqa_q_norm'][l], p['gqa_k_norm'][l], rope, gqa_ctx)
    y = jnp.concatenate([y_a, y_b, y_c, y_d], axis=-1) @ p['w_out'][l]
    x = x + g1 * y
    h = (rms_norm(x) * p['norm2'][l]) * (1.0 + sc2) + sh2
    gate, up = jnp.split(h @ p['w_ffn_in'][l], 2, axis=-1)
    x = x + g2 * ((jax.nn.silu(gate) * up) @ p['w_ffn_out'][l])
    return x, (diff_kv[0], diff_kv[1], gqa_kv[0], gqa_kv[1], ssm_t, mst_t[0], mst_t[1], mst_t[2])


def setup_inputs(seed: int = 0) -> dict:
    key = jax.random.key(seed)
    keys = iter(jax.random.split(key, 48))

    def nrm(shape, scale=1.0):
        return scale * jax.random.normal(next(keys), shape, jnp.float32)

    def unif(shape, lo, hi):
        return jax.random.uniform(next(keys), shape, jnp.float32, lo, hi)

    def gain(shape):
        return 1.0 + nrm(shape, 0.05)

    dt0 = jnp.exp(unif((DEPTH, 2, SSM_HEADS), math.log(1e-3), math.log(1e-1)))
    gate_b = jnp.stack([nrm((DEPTH, 2, MLSTM_HEADS), 0.1), unif((DEPTH, 2, MLSTM_HEADS), 3.0, 6.0)], axis=2)
    return {
        'x_prompt': nrm((BATCH, SEQ, D_MODEL)),
        'x_sample': nrm((DEC_BATCH, DEC_SEQ, D_MODEL)),
        'c': nrm((DEC_BATCH, D_MODEL)),
        'cache_diff_k': nrm((DEC_BATCH, DEPTH, PAST_LEN, DIFF_HEADS, 2, DIFF_QK_DIM)),
        'cache_diff_v': nrm((DEC_BATCH, DEPTH, PAST_LEN, DIFF_HEADS, DIFF_V_DIM)),
        'cache_gqa_k': nrm((DEC_BATCH, DEPTH, PAST_LEN, GQA_KV_HEADS, GQA_HEAD_DIM)),
        'cache_gqa_v': nrm((DEC_BATCH, DEPTH, PAST_LEN, GQA_KV_HEADS, GQA_HEAD_DIM)),
        'state_ssm': nrm((DEC_BATCH, DEPTH, 2, SSM_HEADS, SSM_HEAD_DIM, SSM_STATE), 0.5),
        'state_mlstm_c': nrm((DEC_BATCH, DEPTH, 2, MLSTM_HEADS, MLSTM_HEAD_DIM, MLSTM_HEAD_DIM), 0.3),
        'state_mlstm_n': nrm((DEC_BATCH, DEPTH, 2, MLSTM_HEADS, MLSTM_HEAD_DIM), 0.3),
        'state_mlstm_m': nrm((DEC_BATCH, DEPTH, 2, MLSTM_HEADS)),
        'c_ctx': nrm((D_MODEL,)),
        'w_ada': nrm((DEPTH, D_MODEL, 6 * D_MODEL), 0.5 * D_MODEL ** -0.5),
        'b_ada': nrm((DEPTH, 6 * D_MODEL), 0.02),
        'norm1': gain((DEPTH, D_MODEL)),
        'norm2': gain((DEPTH, D_MODEL)),
        'w_in': nrm((DEPTH, D_MODEL, IN_COLS), D_MODEL ** -0.5),
        'w_out': nrm((DEPTH, D_MIX, D_MODEL), D_MIX ** -0.5),
        'conv_ssd_w': nrm((DEPTH, D_CONV, SSM_CONV_CH), D_CONV ** -0.5),
        'conv_ssd_b': nrm((DEPTH, SSM_CONV_CH), 0.02),
        'ssd_a_log': jnp.log(unif((DEPTH, 2, SSM_HEADS), 1.0, 16.0)),
        'ssd_dt_bias': dt0 + jnp.log(-jnp.expm1(-dt0)),
        'ssd_d': gain((DEPTH, SSM_HEADS)),
        'ssd_norm': gain((DEPTH, GROUP_W)),
        'diff_lq1': nrm((DEPTH, DIFF_QK_DIM), 0.1),
        'diff_lk1': nrm((DEPTH, DIFF_QK_DIM), 0.1),
        'diff_lq2': nrm((DEPTH, DIFF_QK_DIM), 0.1),
        'diff_lk2': nrm((DEPTH, DIFF_QK_DIM), 0.1),
        'conv_mlstm_w': nrm((DEPTH, D_CONV, 2 * GROUP_W), D_CONV ** -0.5),
        'conv_mlstm_b': nrm((DEPTH, 2 * GROUP_W), 0.02),
        'mlstm_gate_b': gate_b,
        'mlstm_norm': gain((DEPTH, GROUP_W)),
        'gqa_q_norm': gain((DEPTH, GQA_HEAD_DIM)),
        'gqa_k_norm': gain((DEPTH, GQA_HEAD_DIM)),
        'w_ffn_in': nrm((DEPTH, D_MODEL, 2 * D_FF), D_MODEL ** -0.5),
        'w_ffn_out': nrm((DEPTH, D_FF, D_MODEL), D_FF ** -0.5),
        'norm_f': gain((D_MODEL,)),
    }


def reference(x_prompt, x_sample, c, cache_diff_k, cache_diff_v, cache_gqa_k, cache_gqa_v, state_ssm, state_mlstm_c, state_mlstm_n, state_mlstm_m, c_ctx, w_ada, b_ada, norm1, norm2, w_in, w_out, conv_ssd_w, conv_ssd_b, ssd_a_log, ssd_dt_bias, ssd_d, ssd_norm, diff_lq1, diff_lk1, diff_lq2, diff_lk2, conv_mlstm_w, conv_mlstm_b, mlstm_gate_b, mlstm_norm, gqa_q_norm, gqa_k_norm, w_ffn_in, w_ffn_out, norm_f):
    p = dict(norm1=norm1, norm2=norm2, w_in=w_in, w_out=w_out, conv_ssd_w=conv_ssd_w, conv_ssd_b=conv_ssd_b,
             ssd_a_log=ssd_a_log, ssd_dt_bias=ssd_dt_bias, ssd_d=ssd_d, ssd_norm=ssd_norm,
             diff_lq1=diff_lq1, diff_lk1=diff_lk1, diff_lq2=diff_lq2, diff_lk2=diff_lk2,
             conv_mlstm_w=conv_mlstm_w, conv_mlstm_b=conv_mlstm_b, mlstm_gate_b=mlstm_gate_b, mlstm_norm=mlstm_norm,
             gqa_q_norm=gqa_q_norm, gqa_k_norm=gqa_k_norm, w_ffn_in=w_ffn_in, w_ffn_out=w_ffn_out)

    x = x_prompt
    per_layer = []
    for l in range(DEPTH):
        mod = (jax.nn.silu(c_ctx) @ w_ada[l] + b_ada[l])[None, None, :]
        x, ctx_l = trunk_layer(x, mod, p, l, None, None)
        per_layer.append(ctx_l)
    y_prompt = rms_norm(x) * norm_f
    new_diff_k = jnp.stack([s[0] for s in per_layer], axis=1)
    new_diff_v = jnp.stack([s[1] for s in per_layer], axis=1)
    new_gqa_k = jnp.stack([s[2] for s in per_layer], axis=1)
    new_gqa_v = jnp.stack([s[3] for s in per_layer], axis=1)
    new_ssm = jnp.stack([s[4] for s in per_layer], axis=1)
    new_mlstm_c = jnp.stack([s[5] for s in per_layer], axis=1)
    new_mlstm_n = jnp.stack([s[6] for s in per_layer], axis=1)
    new_mlstm_m = jnp.stack([s[7] for s in per_layer], axis=1)

    rope = grid_rope(x_sample.shape[1], ROPE_DIM)
    x = x_sample
    for l in range(DEPTH):
        mod = (jax.nn.silu(c) @ w_ada[l] + b_ada[l])[:, None, :]
        ctx = dict(diff_k=cache_diff_k[:, l], diff_v=cache_diff_v[:, l], gqa_k=cache_gqa_k[:, l], gqa_v=cache_gqa_v[:, l],
                   ssm=state_ssm[:, l], mlstm_c=state_mlstm_c[:, l], mlstm_n=state_mlstm_n[:, l], mlstm_m=state_mlstm_m[:, l])
        x, _ = trunk_layer(x, mod, p, l, rope, ctx)
    y_sample = rms_norm(x) * norm_f
    return (y_prompt, y_sample, new_diff_k, new_diff_v, new_gqa_k, new_gqa_v, new_ssm, new_mlstm_c, new_mlstm_n, new_mlstm_m)
```

```python
import os
import math
from contextlib import ExitStack
import numpy as np
import concourse.bass as bass
import concourse.mybir as mybir
from concourse.bass_utils import run_bass_kernel_spmd

F32 = mybir.dt.float32
BF16 = mybir.dt.bfloat16
ALU = mybir.AluOpType
AF = mybir.ActivationFunctionType
AX = mybir.AxisListType

D_MODEL = 1024
DEPTH = 4
IN_COLS = 5664
D_FF = 2816
EPS = 1e-6
NEG = -30000.0

NL = int(os.environ.get("MK_NL", "4"))
MIXERS = os.environ.get("MK_MIX", "abcd")
SSD_PH = int(os.environ.get("MK_SSD_PH", "9"))
SSD_SUB = int(os.environ.get("MK_SSD_SUB", "9"))


class Buf:
    __slots__ = ("t", "w", "r")

    def __init__(self, t):
        self.t = t
        self.w = None
        self.r = []

    def __getitem__(self, idx):
        return self.t[idx]


class KB:
    NDMA_SEM = 8

    def __init__(self, nc):
        self.nc = nc
        self.engs = {"pe": nc.tensor, "act": nc.scalar, "dve": nc.vector, "pool": nc.gpsimd, "sp": nc.sync}
        self.sems = {}
        self.cnt = {}
        for k in ("pe", "act", "dve", "pool"):
            self.sems[k] = nc.alloc_semaphore(name="s_" + k)
            self.cnt[k] = 0
        self.dq = {}
        for q in ("sp", "pool", "act"):
            lst = []
            for i in range(self.NDMA_SEM):
                key = "d_%s%d" % (q, i)
                self.sems[key] = nc.alloc_semaphore(name=key)
                self.cnt[key] = 0
                lst.append(key)
            self.dq[q] = [lst, 0]
        self.seen = {e: {} for e in self.engs}
        self.ninst = 0

    def _wait(self, eng, k, v):
        seen = self.seen[eng]
        if seen.get(k, 0) >= v:
            return
        self.engs[eng].wait_ge(self.sems[k], v)
        self.ninst += 1
        seen[k] = v

    def _need(self, eng, reads, writes):
        need = {}

        def add(dep):
            if dep is None:
                return
            k, v = dep
            if need.get(k, 0) < v:
                need[k] = v
        for b in reads:
            add(b.w)
        for b in writes:
            add(b.w)
            for d in b.r:
                add(d)
        for k, v in need.items():
            if k == eng and eng == "pe":
                continue
            self._wait(eng, k, v)

    def _record(self, dep, reads, writes):
        for b in reads:
            b.r.append(dep)
            if len(b.r) > 64:
                mx = {}
                for k, v in b.r:
                    if mx.get(k, 0) < v:
                        mx[k] = v
                b.r = list(mx.items())
        for b in writes:
            b.w = dep
            b.r = []

    def op(self, eng, fn, reads=(), writes=(), inc=True):
        self._need(eng, reads, writes)
        ins = fn(self.engs[eng])
        self.ninst += 1
        val = self.cnt[eng] + 1
        if inc:
            ins.then_inc(self.sems[eng], 1)
            self.cnt[eng] = val
        self._record((eng, val), reads, writes)
        return ins

    def dma(self, q, out, in_, reads=(), writes=(), **kw):
        self._need(q, reads, writes)
        lst, i = self.dq[q]
        key = lst[i % len(lst)]
        self.dq[q][1] = i + 1
        if self.cnt[key]:
            self._wait(q, key, self.cnt[key])
        ins = self.engs[q].dma_start(out=out, in_=in_, **kw)
        self.ninst += 1
        self.cnt[key] += 16
        ins.then_inc(self.sems[key], 16)
        dep = (key, self.cnt[key])
        self._record(dep, reads, writes)
        return dep

    def barrier(self, include_pool_dma=False):
        keys = ["pe", "act", "dve", "pool"] + self.dq["sp"][0] + self.dq["act"][0]
        if include_pool_dma:
            keys += self.dq["pool"][0]
        for e in ("pe", "act", "dve", "pool", "sp"):
            for k in keys:
                if (k == e and e == "pe") or self.cnt[k] == 0:
                    continue
                self._wait(e, k, self.cnt[k])


def bcast(ap, shape, axis):
    return ap.unsqueeze(axis).broadcast_to(list(shape))


IN_SPECS = [
    ("xp", [512, 1024]), ("xs", [2048, 1024]), ("cvec", [2, 1024]),
    ("cdk", [4, 256, 512]), ("cdv", [4, 256, 512]), ("cgk", [4, 256, 128]), ("cgv", [4, 256, 128]),
    ("sssm", [4, 2, 8, 64, 64]), ("smc", [4, 2, 4, 128, 128]), ("smn", [4, 2, 4, 128]), ("smm", [4, 8]),
    ("w_ada", [4, 1024, 6144]), ("b_ada", [4, 6144]), ("norm1", [4, 1024]), ("norm2", [4, 1024]),
    ("w_in", [4, 1024, IN_COLS]), ("w_out", [4, 2048, 1024]),
    ("conv_ssd_w", [4, 5, 768]), ("conv_ssd_b", [4, 768]), ("ssd_a_log", [4, 16]), ("ssd_dt_bias", [4, 16]),
    ("ssd_d", [4, 8]), ("ssd_norm", [4, 512]),
    ("diff_lq1", [4, 64]), ("diff_lk1", [4, 64]), ("diff_lq2", [4, 64]), ("diff_lk2", [4, 64]),
    ("conv_mlstm_w", [4, 5, 1024]), ("conv_mlstm_b", [4, 1024]), ("mlstm_gate_b", [4, 16]), ("mlstm_norm", [4, 512]),
    ("gqa_q_norm", [4, 64]), ("gqa_k_norm", [4, 64]),
    ("w_ffn_in", [4, 1024, 2 * D_FF]), ("w_ffn_out", [4, D_FF, 1024]), ("norm_f", [1024]),
    ("consts", [128, 1152]), ("rope", [128, 2, 2048]),
]
OUT_SPECS = [
    ("yp", [512, 1024]), ("ys", [2048, 1024]),
    ("ndk", [2, 4, 256, 512]), ("ndv", [2, 4, 256, 512]), ("ngk", [2, 4, 256, 128]), ("ngv", [2, 4, 256, 128]),
    ("nssm", [2, 4, 2, 8, 64, 64]), ("nmc", [2, 4, 2, 4, 128, 128]), ("nmn", [2, 4, 2, 4, 128]), ("nmm", [2, 4, 8]),
]


def make_consts():
    c = np.zeros((128, 1152), np.float32)
    k = np.arange(128)
    c[:, 0:128] = np.eye(128)
    c[:, 128:256] = 1.0
    c[:, 256:384] = (k[:, None] <= k[None, :])
    c[:, 384:512] = (k[:, None] >= k[None, :])
    c[:, 512:640] = np.where(k[:, None] <= k[None, :], 0.0, NEG)
    c[:, 640:768] = np.where(k[:, None] >= k[None, :], 0.0, NEG)
    c[:, 768:896] = (k[:, None] // 64 == k[None, :] // 64)
    rm = np.zeros((128, 128), np.float32)
    for dp in range(128):
        half = (dp % 32) // 16
        if half == 0:
            rm[dp + 16, dp] = -1.0
        else:
            rm[dp - 16, dp] = 1.0
    c[:, 896:1024] = rm
    c[64, 1024:1088] = 1.0
    c[0, 1088:1152] = 1.0
    return c


def make_rope():
    t = np.arange(2048)
    r = (t // 64).astype(np.float32)
    cc = (t % 64).astype(np.float32)
    nf = 16
    freqs = (10000.0 ** (-np.arange(nf, dtype=np.float32) / nf)).astype(np.float32)
    ang = np.stack([r[:, None] * freqs, cc[:, None] * freqs], axis=1).astype(np.float32)
    out = np.zeros((128, 2, 2048), np.float32)
    for p in range(128):
        d = p % 64
        a = d // 32
        f = d % 16
        out[p, 0] = np.cos(ang[:, a, f])
        out[p, 1] = np.sin(ang[:, a, f])
    return out


def build_program():
    nc = bass.Bass("TRN2", target_bir_lowering=False)
    kb = KB(nc)
    D = {}
    for name, shape in IN_SPECS:
        if shape[0] == 4 and len(shape) > 1:
            shape = [NL] + list(shape[1:])
        D[name] = nc.dram_tensor(name, shape, F32, kind="ExternalInput").ap()
    for name, shape in OUT_SPECS:
        D[name] = nc.dram_tensor(name, shape, F32, kind="ExternalOutput").ap()
    mod_d = nc.dram_tensor("mod_scr", [4, 2, 6144], F32, kind="Internal").ap()

    top = ExitStack()

    uid = [0]

    def alloc(stack, name, shape, dt, psum=False):
        uid[0] += 1
        name = "%s_%d" % (name, uid[0])
        cm = nc.psum_tensor(name, shape, dt) if psum else nc.sbuf_tensor(name, shape, dt)
        return Buf(stack.enter_context(cm))

    xT = [alloc(top, "xT%d" % i, [128, 8, 512], F32) for i in range(4)]
    hT = [alloc(top, "hT%d" % i, [128, 8, 512], BF16) for i in range(4)]
    NW = 2
    wbufs = [alloc(top, "wb%d" % i, [128, 4096], BF16) for i in range(NW)]
    wstate = [0]
    psb = [alloc(top, "ps%d" % i, [128, 512], F32, psum=True) for i in range(8)]
    pstate = [0]
    cf = alloc(top, "cf", [128, 640], F32)
    cb = alloc(top, "cb", [128, 1024], BF16)
    modc = alloc(top, "modc", [128, 6, 8], F32)
    nrm = alloc(top, "nrm", [128, 2, 8], F32)
    AB = alloc(top, "AB", [128, 4, 8], F32)
    nfc = alloc(top, "nfc", [128, 8], F32)
    lnsb = alloc(top, "lnsb", [128, 1], F32)
    lns_col = lnsb.t

    wo_buf = alloc(top, "wo_buf", [128, 4, 1024], BF16)
    accstate = [0]

    def ps():
        b = psb[pstate[0] % 4]
        pstate[0] += 1
        return b

    def ps_acc():
        b = psb[4 + accstate[0] % 4]
        accstate[0] += 1
        return b

    def wload(pieces, kch):
        b = wbufs[wstate[0] % NW]
        wstate[0] += 1
        ntot = sum(n for _, n in pieces)
        assert kch * ntot <= 4096, (kch, ntot)
        view = b.t[:, 0:kch * ntot].rearrange("p (k n) -> p k n", k=kch)
        o = 0
        for ap, n in pieces:
            kb.dma("pool", view[:, :, o:o + n], ap.rearrange("(k p) n -> p k n", p=128), writes=[b])
            o += n
        return b, view

    ident_f = cf.t[:, 0:128]
    ones_f = cf.t[:, 128:256]
    ident_b = cb.t[:, 0:128]
    selm_f = cf.t[:, 512:640]
    ones_b = cb.t[:, 128:256]

    kb.dma("sp", cf[:, 0:512], D["consts"][:, 0:512], writes=[cf])
    kb.dma("sp", cf[:, 512:640], D["consts"][:, 1024:1152], writes=[cf])
    kb.dma("pool", cb[:], D["consts"][:, 0:1024], writes=[cb])
    kb.op("dve", lambda e: e.memset(lnsb[:], math.log(128 ** -0.5)), writes=[lnsb])
    kb.dma("sp", nfc[:], D["norm_f"].rearrange("(c p) -> p c", p=128), writes=[nfc], allow_slow_non_contiguous=True)

    modall = alloc(top, "modall", [128, NL, 48, 2], F32)
    ball = alloc(top, "ball", [128, NL, 48], F32)
    nrmall = alloc(top, "nrmall", [128, NL, 2, 8], F32)
    for l in range(NL):
        kb.dma("sp", ball[:, l, :], D["b_ada"][l].rearrange("(j p) -> p j", p=128), writes=[ball], allow_slow_non_contiguous=True)
        kb.dma("sp", nrmall[:, l, 0, :], D["norm1"][l].rearrange("(c p) -> p c", p=128), writes=[nrmall], allow_slow_non_contiguous=True)
        kb.dma("sp", nrmall[:, l, 1, :], D["norm2"][l].rearrange("(c p) -> p c", p=128), writes=[nrmall], allow_slow_non_contiguous=True)
    with ExitStack() as st:
        cT = alloc(st, "cT", [128, 2, 8], F32)
        cTb = alloc(st, "cTb", [128, 2, 8], BF16)
        sig = alloc(st, "csig", [128, 2, 8], F32)
        for g in range(2):
            kb.dma("sp", cT[:, g, :], D["cvec"][g].rearrange("(c p) -> p c", p=128), writes=[cT], allow_slow_non_contiguous=True)
        kb.op("act", lambda e: e.activation(out=sig[:], in_=cT[:], func=AF.Sigmoid), reads=[cT], writes=[sig])
        kb.op("dve", lambda e: e.tensor_tensor(out=cTb[:], in0=cT[:], in1=sig[:], op=ALU.mult), reads=[cT, sig], writes=[cTb])
        for l in range(NL):
            for blk in range(12):
                c0 = blk * 512
                wb, wv = wload([(D["w_ada"][l][:, c0:c0 + 512], 512)], 8)
                p = ps()
                for cc in range(4):
                    for k in range(8):
                        kb.op("pe", lambda e: e.matmul(p[:, 2 * cc:2 * cc + 2], lhsT=wv[:, k, cc * 128:(cc + 1) * 128], rhs=cTb[:, :, k], start=(k == 0), stop=(k == 7)),
                              reads=[cTb, wb], writes=[p], inc=(k == 7 and cc == 3))
                kb.op("dve", lambda e: e.tensor_tensor(out=modall[:, l, blk * 4:(blk + 1) * 4, :], in0=p[:, 0:8].rearrange("p (j g) -> p j g", g=2),
                                                       in1=bcast(ball[:, l, blk * 4:(blk + 1) * 4], [128, 4, 2], 2), op=ALU.add), reads=[p, ball], writes=[modall])
        kb.barrier()

    def load_x(src, nblk):
        with ExitStack() as st:
            xin = [alloc(st, "xin%d" % i, [128, 1024], F32) for i in range(2)]
            for tb in range(nblk):
                tiles = []
                for tl in range(4):
                    pass
                for tl in range(4):
                    xi = xin[(tb * 4 + tl) % 2]
                    t0 = (tb * 4 + tl) * 128
                    kb.dma("sp", xi[:], src[t0:t0 + 128, :], writes=[xi])
                    for half in range(2):
                        p = ps()
                        for cc in range(4):
                            c = half * 4 + cc
                            kb.op("pe", lambda e: e.matmul(p[:, cc * 128:(cc + 1) * 128], lhsT=xi[:, c * 128:(c + 1) * 128], rhs=ident_f,
                                                           start=True, stop=True), reads=[xi, cf], writes=[p], inc=(cc == 3))
                        kb.op("act", lambda e: e.activation(
                            out=xT[tb][:, half * 4:half * 4 + 4, tl * 128:(tl + 1) * 128],
                            in_=p[:, :].rearrange("p (c t) -> p c t", c=4), func=AF.Copy), reads=[p], writes=[xT[tb]])

    def norm_block(st_tmp, tb, Acol, Bcol, dst, dst_dt_is_bf16=True):
        sq, rstd, tmp2 = st_tmp
        if isinstance(sq, list):
            sq, rstd = sq[tb % 2], rstd[tb % 2]
        kb.op("act", lambda e: e.activation(out=sq[:], in_=xT[tb][:], func=AF.Square), reads=[xT[tb]], writes=[sq])
        p = ps()
        for c in range(8):
            kb.op("pe", lambda e: e.matmul(p[:, :], lhsT=ones_b, rhs=sq[:, c, :], start=(c == 0), stop=(c == 7)),
                  reads=[sq, cb], writes=[p], inc=(c == 7))
        kb.op("dve", lambda e: e.tensor_scalar(out=rstd[:], in0=p[:, :], scalar1=1.0 / D_MODEL, scalar2=EPS, op0=ALU.mult, op1=ALU.add),
              reads=[p], writes=[rstd])
        kb.op("act", lambda e: e.activation(out=rstd[:], in_=rstd[:], func=AF.Sqrt), reads=[rstd], writes=[rstd])
        kb.op("dve", lambda e: e.reciprocal(out=rstd[:], in_=rstd[:]), reads=[rstd], writes=[rstd])
        for c in range(8):
            t2 = tmp2[c % len(tmp2)]
            kb.op("dve", lambda e: e.tensor_tensor(out=t2[:], in0=xT[tb][:, c, :], in1=rstd[:], op=ALU.mult),
                  reads=[xT[tb], rstd], writes=[t2])
            if Bcol is not None:
                kb.op("act", lambda e: e.activation(out=dst[:, c, :], in_=t2[:], func=AF.Identity, bias=Bcol[:, c:c + 1], scale=Acol[:, c:c + 1]),
                      reads=[t2, AB], writes=[dst])
            else:
                kb.op("act", lambda e: e.activation(out=dst[:, c, :], in_=t2[:], func=AF.Identity, scale=Acol[:, c:c + 1]),
                      reads=[t2, nfc], writes=[dst])

    def load_mod(l, g):
        kb.op("dve", lambda e: e.tensor_copy(out=modc[:], in_=modall[:, l, :, g].rearrange("p (v c) -> p v c", v=6)), reads=[modall], writes=[modc])
        kb.op("dve", lambda e: e.tensor_copy(out=nrm[:], in_=nrmall[:, l, :, :]), reads=[nrmall], writes=[nrm])
        for j, (vs, vh) in enumerate(((1, 0), (4, 3))):
            kb.op("dve", lambda e: e.scalar_tensor_tensor(out=AB[:, 2 * j, :], in0=modc[:, vs, :], scalar=1.0, in1=nrm[:, j, :],
                                                          op0=ALU.add, op1=ALU.mult), reads=[modc, nrm], writes=[AB])
            kb.op("dve", lambda e: e.tensor_copy(out=AB[:, 2 * j + 1, :], in_=modc[:, vh, :]), reads=[modc], writes=[AB])

    def ffn(l, blocks):
        nb = len(blocks)
        with ExitStack() as st:
            actT = alloc(st, "actT", [128, 22, nb * 512], BF16)
            sg = [alloc(st, "sg%d" % i, [128, 512], F32) for i in range(2)]
            it = 0
            for jj in range(11):
                c0 = jj * 256
                wb, wv = wload([(D["w_ffn_in"][l][:, c0:c0 + 256], 256), (D["w_ffn_in"][l][:, D_FF + c0:D_FF + c0 + 256], 256)], 8)
                for j2 in range(2):
                    j = jj * 2 + j2
                    for bi, tb in enumerate(blocks):
                        pg = ps()
                        pu = ps()
                        for k in range(8):
                            kb.op("pe", lambda e: e.matmul(pg[:, :], lhsT=wv[:, k, j2 * 128:(j2 + 1) * 128], rhs=hT[tb][:, k, :],
                                                           start=(k == 0), stop=(k == 7)), reads=[wb, hT[tb]], writes=[pg], inc=(k == 7))
                        for k in range(8):
                            kb.op("pe", lambda e: e.matmul(pu[:, :], lhsT=wv[:, k, 256 + j2 * 128:256 + (j2 + 1) * 128], rhs=hT[tb][:, k, :],
                                                           start=(k == 0), stop=(k == 7)), reads=[wb, hT[tb]], writes=[pu], inc=(k == 7))
                        s = sg[it % 2]
                        it += 1
                        kb.op("act", lambda e: e.activation(out=s[:], in_=pg[:, :], func=AF.Silu), reads=[pg], writes=[s])
                        kb.op("dve", lambda e: e.tensor_tensor(out=actT[:, j, bi * 512:(bi + 1) * 512], in0=s[:], in1=pu[:, :], op=ALU.mult),
                              reads=[s, pu], writes=[actT])
            for c in range(8):
                wb, wv = wload([(D["w_ffn_out"][l][:, c * 128:(c + 1) * 128], 128)], 22)
                for bi, tb in enumerate(blocks):
                    p = ps()
                    for k in range(22):
                        kb.op("pe", lambda e: e.matmul(p[:, :], lhsT=wv[:, k, :], rhs=actT[:, k, bi * 512:(bi + 1) * 512],
                                                       start=(k == 0), stop=(k == 21)), reads=[wb, actT], writes=[p], inc=(k == 21))
                    kb.op("dve", lambda e: e.scalar_tensor_tensor(out=xT[tb][:, c, :], in0=p[:, :], scalar=modc[:, 5, c:c + 1], in1=xT[tb][:, c, :],
                                                                  op0=ALU.mult, op1=ALU.add), reads=[p, modc, xT[tb]], writes=[xT[tb]])
        kb.barrier()

    def final_out(nblk, dst):
        with ExitStack() as st:
            sq = alloc(st, "sq", [128, 8, 512], BF16)
            rstd = alloc(st, "rstd", [128, 512], F32)
            tmp2 = [alloc(st, "tmpn%d" % i, [128, 512], F32) for i in range(2)]
            xn = alloc(st, "xn", [128, 8, 512], F32)
            ot = [alloc(st, "ot%d" % i, [128, 1024], F32) for i in range(2)]
            for tb in range(nblk):
                norm_block((sq, rstd, tmp2), tb, nfc, None, xn)
                for tl in range(4):
                    o = ot[tl % 2]
                    for half in range(2):
                        p = ps()
                        for cc in range(4):
                            c = half * 4 + cc
                            kb.op("pe", lambda e: e.matmul(p[:, cc * 128:(cc + 1) * 128], lhsT=xn[:, c, tl * 128:(tl + 1) * 128], rhs=ident_f,
                                                           start=True, stop=True), reads=[xn, cf], writes=[p], inc=(cc == 3))
                        kb.op("act", lambda e: e.activation(out=o[:, half * 512:(half + 1) * 512], in_=p[:, :], func=AF.Copy), reads=[p], writes=[o])
                    t0 = (tb * 4 + tl) * 128
                    kb.dma("sp", dst[t0:t0 + 128, :], o[:], reads=[o])
        kb.barrier()


    bd64_b = cb.t[:, 768:896]
    rm_b = cb.t[:, 896:1024]

    def proj_tm(wb, wv, s0, n, tb, tl, p):
        for k in range(8):
            kb.op("pe", lambda e: e.matmul(p[:, 0:n], lhsT=hT[tb][:, k, tl * 128:(tl + 1) * 128], rhs=wv[:, k, s0:s0 + n],
                                           start=(k == 0), stop=(k == 7)), reads=[wb, hT[tb]], writes=[p], inc=(k == 7))

    def load_wo(l, row0):
        kb.dma("pool", wo_buf[:], D["w_out"][l][row0:row0 + 512, :].rearrange("(k p) n -> p k n", p=128), writes=[wo_buf])

    def mixer_out(st, ytm, tb, yT):
        for c in range(4):
            p = ps()
            for tl in range(4):
                kb.op("pe", lambda e: e.matmul(p[:, tl * 128:(tl + 1) * 128], lhsT=ytm[:, tl, c * 128:(c + 1) * 128], rhs=ident_b,
                                               start=True, stop=True), reads=[ytm, cb], writes=[p], inc=(tl == 3))
            kb.op("act", lambda e: e.activation(out=yT[:, c, :], in_=p[:, :], func=AF.Copy), reads=[p], writes=[yT])
        for c in range(8):
            p = ps()
            for k in range(4):
                kb.op("pe", lambda e: e.matmul(p[:, :], lhsT=wo_buf[:, k, c * 128:(c + 1) * 128], rhs=yT[:, k, :],
                                               start=(k == 0), stop=(k == 3)), reads=[wo_buf, yT], writes=[p], inc=(k == 3))
            kb.op("dve", lambda e: e.scalar_tensor_tensor(out=xT[tb][:, c, :], in0=p[:, :], scalar=modc[:, 2, c:c + 1], in1=xT[tb][:, c, :],
                                                          op0=ALU.mult, op1=ALU.add), reads=[p, modc, xT[tb]], writes=[xT[tb]])

    def attention(l, g, kind, nblk):
        sample = (g == 1)
        L = 2048 if sample else 256
        nseq = 1 if sample else 2
        nctx = 2 if sample else 0
        lt = L // 128
        nkt = lt + nctx
        ntok = nblk * 512
        if kind == "d":
            qc0, kc0, vc0, nkc, nvh, ve, orow0, nheads = 4896, 5408, 5536, 1, 2, 64, 1536, 8
            ck, cv, ok, ov = D["cgk"], D["cgv"], D["ngk"], D["ngv"]
        else:
            qc0, kc0, vc0, nkc, nvh, ve, orow0, nheads = 1296, 1808, 2320, 4, 4, 128, 512, 4
            ck, cv, ok, ov = D["cdk"], D["cdv"], D["ndk"], D["ndv"]
        scale = 64 ** -0.5
        kw = nkc * 128
        vw = nvh * ve
        lam_init = 0.8 - 0.6 * math.exp(-0.3 * l)
        with ExitStack() as st:
            qT = alloc(st, "qT", [128, 4, ntok], BF16)
            kT = alloc(st, "kT", [128, nkc, nseq * nkt * 128], BF16)
            vsw = ve + 1 if kind == "d" else ve
            vaug = alloc(st, "vaug", [128, nseq * nkt, nvh, vsw], BF16)
            vodd = alloc(st, "vodd", [128, nseq * nkt, nvh, 128], BF16) if kind == "d" else None
            yT = alloc(st, "yT", [128, 4, 512], BF16)
            pTs = [alloc(st, "pT%d" % i, [128, 512], BF16) for i in range(3)]
            sqb = alloc(st, "sqb", [128, 512], BF16)
            rs = alloc(st, "rs", [128, 512], F32)
            qn = alloc(st, "qn", [128, 512], BF16)
            t1 = alloc(st, "t1", [128, 512], F32)
            gcol = alloc(st, "gcol", [128, 2], F32)
            osb = alloc(st, "osb", [128, 512], F32)
            t2 = osb
            sm = alloc(st, "sm", [128, 16], F32)
            lamt = alloc(st, "lamt", [128, 4, 64], F32)
            kng = alloc(st, "kng", [128, 64], F32)
            ropeT = alloc(st, "ropeT", [128, 2, 2048], BF16) if sample else None
            load_wo(l, orow0)
            if kind == "d":
                kb.op("dve", lambda e: e.memset(vaug[:, :, :, ve:ve + 1], 1.0), writes=[vaug])
                kb.op("dve", lambda e: e.memset(vodd[:, :, :, 0:1], 1.0), writes=[vodd])
                kb.op("dve", lambda e: e.memset(vodd[:, :, :, 1:64], 0.0), writes=[vodd])
            if sample:
                kb.dma("pool", ropeT[:], D["rope"], writes=[ropeT])
            if kind == "d":
                for j, nm in enumerate(("gqa_q_norm", "gqa_k_norm")):
                    for hh in range(2):
                        kb.dma("sp", gcol[hh * 64:(hh + 1) * 64, j:j + 1], D[nm][l].rearrange("(d o) -> d o", o=1), writes=[gcol])
                kb.dma("sp", kng[:], D["gqa_k_norm"][l].partition_broadcast(128), writes=[kng])
            else:
                for j, nm in enumerate(("diff_lq1", "diff_lk1", "diff_lq2", "diff_lk2")):
                    kb.dma("sp", lamt[:, j, :], D[nm][l].partition_broadcast(128), writes=[lamt])
                kb.op("dve", lambda e: e.tensor_tensor(out=lamt[:, 0, :], in0=lamt[:, 0, :], in1=lamt[:, 1, :], op=ALU.mult), reads=[lamt], writes=[lamt])
                kb.op("dve", lambda e: e.tensor_tensor(out=lamt[:, 2, :], in0=lamt[:, 2, :], in1=lamt[:, 3, :], op=ALU.mult), reads=[lamt], writes=[lamt])
                kb.op("dve", lambda e: e.tensor_reduce(out=sm[:, 2:3], in_=lamt[:, 0, :], axis=AX.X, op=ALU.add), reads=[lamt], writes=[sm])
                kb.op("dve", lambda e: e.tensor_reduce(out=sm[:, 3:4], in_=lamt[:, 2, :], axis=AX.X, op=ALU.add), reads=[lamt], writes=[sm])
                kb.op("act", lambda e: e.activation(out=sm[:, 2:4], in_=sm[:, 2:4], func=AF.Exp), reads=[sm], writes=[sm])
                kb.op("dve", lambda e: e.tensor_tensor(out=sm[:, 0:1], in0=sm[:, 2:3], in1=sm[:, 3:4], op=ALU.subtract), reads=[sm], writes=[sm])
                kb.op("dve", lambda e: e.tensor_scalar(out=sm[:, 1:2], in0=sm[:, 0:1], scalar1=lam_init, scalar2=-1.0, op0=ALU.add, op1=ALU.mult), reads=[sm], writes=[sm])

            def qk_post(p, dst, tb, normj):
                src = p
                if kind == "d":
                    kb.op("act", lambda e: e.activation(out=sqb[:], in_=p[:, :], func=AF.Square), reads=[p], writes=[sqb])
                    pn = ps()
                    kb.op("pe", lambda e: e.matmul(pn[:, :], lhsT=bd64_b, rhs=sqb[:], start=True, stop=True), reads=[cb, sqb], writes=[pn])
                    kb.op("dve", lambda e: e.tensor_scalar(out=rs[:], in0=pn[:, :], scalar1=1.0 / 64, scalar2=EPS, op0=ALU.mult, op1=ALU.add), reads=[pn], writes=[rs])
                    kb.op("act", lambda e: e.activation(out=rs[:], in_=rs[:], func=AF.Sqrt), reads=[rs], writes=[rs])
                    kb.op("dve", lambda e: e.reciprocal(out=rs[:], in_=rs[:]), reads=[rs], writes=[rs])
                    tgt = qn if sample else None
                    o_ap = qn[:] if sample else dst
                    kb.op("dve", lambda e: e.scalar_tensor_tensor(out=o_ap, in0=p[:, :], scalar=gcol[:, normj:normj + 1], in1=rs[:], op0=ALU.mult, op1=ALU.mult),
                          reads=[p, gcol, rs], writes=[qn if sample else dst_buf[0]])
                else:
                    o_ap = qn[:] if sample else dst
                    kb.op("act", lambda e: e.activation(out=o_ap, in_=p[:, :], func=AF.Copy), reads=[p], writes=[qn if sample else dst_buf[0]])
                if sample:
                    pr = ps()
                    kb.op("pe", lambda e: e.matmul(pr[:, :], lhsT=rm_b, rhs=qn[:], start=True, stop=True), reads=[cb, qn], writes=[pr])
                    kb.op("dve", lambda e: e.tensor_tensor(out=t1[:], in0=qn[:], in1=ropeT[:, 0, tb * 512:(tb + 1) * 512], op=ALU.mult), reads=[qn, ropeT], writes=[t1])
                    kb.op("dve", lambda e: e.tensor_tensor(out=t2[:], in0=pr[:, :], in1=ropeT[:, 1, tb * 512:(tb + 1) * 512], op=ALU.mult), reads=[pr, ropeT], writes=[t2])
                    kb.op("dve", lambda e: e.tensor_tensor(out=dst, in0=t1[:], in1=t2[:], op=ALU.add), reads=[t1, t2], writes=[dst_buf[0]])

            dst_buf = [None]
            blocks = list(range(nblk))
            if kind == "d":
                pcs = []
                for j in range(4):
                    for hh in (j, 4 + j):
                        pcs.append((D["w_in"][l][:, qc0 + hh * 64:qc0 + (hh + 1) * 64], 64))
                wb, wv = wload(pcs, 8)
            else:
                wb, wv = wload([(D["w_in"][l][:, qc0:qc0 + 512], 512)], 8)
            dst_buf[0] = qT
            for j in range(4):
                for tb in blocks:
                    p = ps()
                    for k in range(8):
                        lh = wv[:, k, j * 128:(j + 1) * 128]
                        kb.op("pe", lambda e: e.matmul(p[:, :], lhsT=lh, rhs=hT[tb][:, k, :], start=(k == 0), stop=(k == 7)),
                              reads=[wb, hT[tb]], writes=[p], inc=(k == 7))
                    qk_post(p, qT[:, j, tb * 512:(tb + 1) * 512], tb, 0)
            wb, wv = wload([(D["w_in"][l][:, kc0:kc0 + kw], kw)], 8)
            dst_buf[0] = kT
            for j in range(nkc):
                for tb in blocks:
                    p = ps()
                    for k in range(8):
                        kb.op("pe", lambda e: e.matmul(p[:, :], lhsT=wv[:, k, j * 128:(j + 1) * 128], rhs=hT[tb][:, k, :], start=(k == 0), stop=(k == 7)),
                              reads=[wb, hT[tb]], writes=[p], inc=(k == 7))
                    qk_post(p, kT[:, j, nctx * 128 + tb * 512:nctx * 128 + (tb + 1) * 512], tb, 1)
            if not sample:
                for tt in range(ntok // 128):
                    sq_, r0 = divmod(tt, lt)
                    p = ps()
                    proj_tm(wb, wv, 0, kw, tt // 4, tt % 4, p)
                    kb.op("act", lambda e: e.activation(out=osb[:, 0:kw], in_=p[:, 0:kw], func=AF.Copy), reads=[p], writes=[osb])
                    if kind == "d":
                        kb.op("dve", lambda e: e.tensor_tensor(out=t1[:, 0:128], in0=osb[:, 0:128], in1=osb[:, 0:128], op=ALU.mult), reads=[osb], writes=[t1])
                        kb.op("dve", lambda e: e.tensor_reduce(out=sm[:, 8:10], in_=t1[:, 0:128].rearrange("p (h d) -> p h d", d=64), axis=AX.X, op=ALU.add), reads=[t1], writes=[sm])
                        kb.op("dve", lambda e: e.tensor_scalar(out=sm[:, 8:10], in0=sm[:, 8:10], scalar1=1.0 / 64, scalar2=EPS, op0=ALU.mult, op1=ALU.add), reads=[sm], writes=[sm])
                        kb.op("act", lambda e: e.activation(out=sm[:, 8:10], in_=sm[:, 8:10], func=AF.Sqrt), reads=[sm], writes=[sm])
                        kb.op("dve", lambda e: e.reciprocal(out=sm[:, 8:10], in_=sm[:, 8:10]), reads=[sm], writes=[sm])
                        kb.op("dve", lambda e: e.tensor_tensor(out=t1[:, 0:128].rearrange("p (h d) -> p h d", d=64), in0=osb[:, 0:128].rearrange("p (h d) -> p h d", d=64),
                                                               in1=bcast(sm[:, 8:10], [128, 2, 64], 2), op=ALU.mult), reads=[osb, sm], writes=[t1])
                        kb.op("dve", lambda e: e.tensor_tensor(out=t2[:, 0:128].rearrange("p (h d) -> p h d", d=64), in0=t1[:, 0:128].rearrange("p (h d) -> p h d", d=64),
                                                               in1=bcast(kng[:], [128, 2, 64], 1), op=ALU.mult), reads=[t1, kng], writes=[t2])
                        kb.dma("sp", ok[sq_, l, r0 * 128:(r0 + 1) * 128, :], t2[:, 0:128], reads=[t2])
                    else:
                        kb.dma("sp", ok[sq_, l, r0 * 128:(r0 + 1) * 128, :], osb[:, 0:kw], reads=[osb])
            wb, wv = wload([(D["w_in"][l][:, vc0:vc0 + vw], vw)], 8)
            for tt in range(ntok // 128):
                sq_, r0 = divmod(tt, lt)
                ktile = sq_ * nkt + nctx + r0
                p = ps()
                proj_tm(wb, wv, 0, vw, tt // 4, tt % 4, p)
                kb.op("act", lambda e: e.activation(out=vaug[:, ktile, :, 0:ve], in_=p[:, 0:vw].rearrange("p (h e) -> p h e", e=ve), func=AF.Copy), reads=[p], writes=[vaug])
                if kind == "d":
                    kb.op("act", lambda e: e.activation(out=vodd[:, ktile, :, 64:128], in_=p[:, 0:vw].rearrange("p (h e) -> p h e", e=ve), func=AF.Copy), reads=[p], writes=[vodd])
                if not sample:
                    kb.op("dve", lambda e: e.tensor_copy(out=osb[:, 0:vw], in_=p[:, 0:vw]), reads=[p], writes=[osb])
                    kb.dma("sp", ov[sq_, l, r0 * 128:(r0 + 1) * 128, :], osb[:, 0:vw], reads=[osb])
            if sample:
                with ExitStack() as st2:
                    ctxk = alloc(st2, "ctxk", [128, kw], F32)
                    for t in range(2):
                        kb.dma("sp", ctxk[:], ck[l][t * 128:(t + 1) * 128, :], writes=[ctxk])
                        for j in range(nkc):
                            p = ps()
                            kb.op("pe", lambda e: e.matmul(p[:, 0:128], lhsT=ctxk[:, j * 128:(j + 1) * 128], rhs=ident_f, start=True, stop=True),
                                  reads=[ctxk, cf], writes=[p])
                            kb.op("act", lambda e: e.activation(out=kT[:, j, t * 128:(t + 1) * 128], in_=p[:, 0:128], func=AF.Copy), reads=[p], writes=[kT])
                    for t in range(2):
                        kb.dma("pool", vaug[:, t, :, 0:ve], cv[l][t * 128:(t + 1) * 128, :].rearrange("p (h e) -> p h e", e=ve), writes=[vaug])
                        if kind == "d":
                            kb.dma("pool", vodd[:, t, :, 64:128], cv[l][t * 128:(t + 1) * 128, :].rearrange("p (h e) -> p h e", e=ve), writes=[vodd])
                    kb.barrier(include_pool_dma=True)
            qblk = min(L, 512)
            pti = 0
            for s_ in range(nseq):
                for qb in range(L // qblk):
                    q0 = s_ * L + qb * qblk
                    qs = slice(q0, q0 + qblk)
                    yc = slice(q0 % 512, q0 % 512 + qblk)
                    its = []
                    for h in range(nheads):
                        for kt in range(nkt):
                            if kind == "d":
                                its.append((h, kt, 0))
                            else:
                                its.append((h, kt, 0))
                                its.append((h, kt, 1))
                    hstate = {}
                    pts = {}
                    DEPTH = 2

                    def stageA(i):
                        h, kt, r = its[i]
                        kc = (s_ * nkt + kt) * 128
                        if kind == "d":
                            rows, qch = (h // 4) * 64, h % 4
                            lhs = kT[rows:rows + 64, 0, kc:kc + 128]
                            rh = qT[rows:rows + 64, qch, qs]
                        else:
                            lhs = kT[r * 64:(r + 1) * 64, h, kc:kc + 128]
                            rh = qT[r * 64:(r + 1) * 64, h, qs]
                        pS = ps()
                        kb.op("pe", lambda e: e.matmul(pS[:, 0:qblk], lhsT=lhs, rhs=rh, start=True, stop=True), reads=[kT, qT], writes=[pS])
                        pT = pTs[i % 3]
                        kb.op("act", lambda e: e.activation(out=pT[:, 0:qblk], in_=pS[:, 0:qblk], func=AF.Exp, scale=scale), reads=[pS], writes=[pT])
                        pts[i] = pT

                    def stageC(i):
                        h, kt, r = its[i]
                        pT = pts.pop(i)
                        last = (kt == nkt - 1)
                        if kind == "d":
                            vh, odd = h // 4, h % 2
                            if kt == 0:
                                hstate[h] = ps_acc()
                            accO = hstate[h]
                            mo = 128 if odd else ve + 1
                            lh = vodd[:, s_ * nkt + kt, vh, :] if odd else vaug[:, s_ * nkt + kt, vh, :]
                            kb.op("pe", lambda e: e.matmul(accO[0:mo, 0:qblk], lhsT=lh, rhs=pT[:, 0:qblk], start=(kt == 0), stop=last),
                                  reads=[pT, vodd if odd else vaug], writes=[accO], inc=True)
                            if last:
                                drow = 0 if odd else 64
                                orow = 64 if odd else 0
                                kb.op("act", lambda e: e.activation(out=rs[drow:drow + 1, 0:qblk], in_=accO[drow:drow + 1, 0:qblk], func=AF.Copy), reads=[accO], writes=[rs])
                                kb.op("dve", lambda e: e.reciprocal(out=rs[drow:drow + 1, 0:qblk], in_=rs[drow:drow + 1, 0:qblk]), reads=[rs], writes=[rs])
                                pB = ps()
                                kb.op("pe", lambda e: e.matmul(pB[:, 0:qblk], lhsT=selm_f[drow:drow + 1, :], rhs=rs[drow:drow + 1, 0:qblk], start=True, stop=True),
                                      reads=[cf, rs], writes=[pB])
                                kb.op("act", lambda e: e.activation(out=t1[orow:orow + 64, 0:qblk], in_=pB[orow:orow + 64, 0:qblk], func=AF.Copy), reads=[pB], writes=[t1])
                                kb.op("dve", lambda e: e.tensor_tensor(out=yT[orow:orow + 64, h // 2, yc], in0=accO[orow:orow + 64, 0:qblk], in1=t1[orow:orow + 64, 0:qblk], op=ALU.mult),
                                      reads=[accO, t1], writes=[yT])
                        else:
                            if kt == 0 and r == 0:
                                hstate[h] = ([ps_acc(), ps_acc()], [ps_acc(), ps_acc()])
                            accO, accD = hstate[h]
                            kb.op("pe", lambda e: e.matmul(accO[r][:, 0:qblk], lhsT=vaug[:, s_ * nkt + kt, h, :], rhs=pT[:, 0:qblk], start=(kt == 0), stop=last),
                                  reads=[pT, vaug], writes=[accO[r]], inc=False)
                            kb.op("pe", lambda e: e.matmul(accD[r][:, 0:qblk], lhsT=ones_b, rhs=pT[:, 0:qblk], start=(kt == 0), stop=last),
                                  reads=[pT, cb], writes=[accD[r]], inc=True)
                            if last and r == 1:
                                A, Bt, O = rs, t1, osb
                                kb.op("dve", lambda e: e.reciprocal(out=A[:, 0:qblk], in_=accD[0][:, 0:qblk]), reads=[accD[0]], writes=[A])
                                kb.op("dve", lambda e: e.reciprocal(out=Bt[:, 0:qblk], in_=accD[1][:, 0:qblk]), reads=[accD[1]], writes=[Bt])
                                kb.op("dve", lambda e: e.tensor_tensor(out=O[:, 0:qblk], in0=accO[0][:, 0:qblk], in1=A[:, 0:qblk], op=ALU.mult), reads=[accO[0], A], writes=[O])
                                kb.op("dve", lambda e: e.tensor_tensor(out=Bt[:, 0:qblk], in0=accO[1][:, 0:qblk], in1=Bt[:, 0:qblk], op=ALU.mult), reads=[accO[1], Bt], writes=[Bt])
                                kb.op("dve", lambda e: e.scalar_tensor_tensor(out=O[:, 0:qblk], in0=Bt[:, 0:qblk], scalar=sm[:, 1:2], in1=O[:, 0:qblk], op0=ALU.mult, op1=ALU.add),
                                      reads=[Bt, sm, O], writes=[O])
                                kb.op("act", lambda e: e.activation(out=sqb[:, 0:qblk], in_=O[:, 0:qblk], func=AF.Square), reads=[O], writes=[sqb])
                                pn = ps()
                                kb.op("pe", lambda e: e.matmul(pn[:, 0:qblk], lhsT=ones_b, rhs=sqb[:, 0:qblk], start=True, stop=True), reads=[cb, sqb], writes=[pn])
                                kb.op("dve", lambda e: e.tensor_scalar(out=A[:, 0:qblk], in0=pn[:, 0:qblk], scalar1=1.0 / 128, scalar2=EPS, op0=ALU.mult, op1=ALU.add), reads=[pn], writes=[A])
                                kb.op("act", lambda e: e.activation(out=A[:, 0:qblk], in_=A[:, 0:qblk], func=AF.Sqrt), reads=[A], writes=[A])
                                kb.op("dve", lambda e: e.reciprocal(out=A[:, 0:qblk], in_=A[:, 0:qblk]), reads=[A], writes=[A])
                                kb.op("dve", lambda e: e.scalar_tensor_tensor(out=yT[:, h, yc], in0=O[:, 0:qblk], scalar=1.0 - lam_init, in1=A[:, 0:qblk], op0=ALU.mult, op1=ALU.mult),
                                      reads=[O, A], writes=[yT])

                    n_it = len(its)
                    for i in range(n_it + DEPTH):
                        if i < n_it:
                            stageA(i)
                        if i >= DEPTH:
                            stageC(i - DEPTH)
                    if (q0 + qblk) % 512 == 0:
                        tb = (q0 + qblk) // 512 - 1
                        for c in range(8):
                            p = ps()
                            for k in range(4):
                                kb.op("pe", lambda e: e.matmul(p[:, :], lhsT=wo_buf[:, k, c * 128:(c + 1) * 128], rhs=yT[:, k, :],
                                                               start=(k == 0), stop=(k == 3)), reads=[wo_buf, yT], writes=[p], inc=(k == 3))
                            kb.op("dve", lambda e: e.scalar_tensor_tensor(out=xT[tb][:, c, :], in0=p[:, :], scalar=modc[:, 2, c:c + 1], in1=xT[tb][:, c, :],
                                                                          op0=ALU.mult, op1=ALU.add), reads=[p, modc, xT[tb]], writes=[xT[tb]])
        kb.barrier()

    triF_f = cf.t[:, 256:384]
    triB_f = cf.t[:, 384:512]
    maskF_b = cb.t[:, 512:640]
    maskB_b = cb.t[:, 640:768]

    def proj_fm(l, c0, ncol, blocks, fn):
        for t0 in range(0, ncol, 512):
            n = min(512, ncol - t0)
            wb, wv = wload([(D["w_in"][l][:, c0 + t0:c0 + t0 + n], n)], 8)
            for s0 in range(0, n, 128):
                w = min(128, n - s0)
                for tb in blocks:
                    p = ps()
                    for k in range(8):
                        kb.op("pe", lambda e: e.matmul(p[0:w, :], lhsT=wv[:, k, s0:s0 + w], rhs=hT[tb][:, k, :], start=(k == 0), stop=(k == 7)),
                              reads=[wb, hT[tb]], writes=[p], inc=(k == 7))
                    fn((t0 + s0) // 128, tb, p, w)

    def conv_chunk(convin, acc, cwt, cbt, cc, nseq, L, dst_ap_fn, dst_buf):
        for s_ in range(nseq):
            a = acc[:, s_ * L:(s_ + 1) * L]
            kb.op("dve", lambda e: e.tensor_scalar(out=a, in0=convin[:, s_, 0:L], scalar1=cwt[:, cc, 0:1], scalar2=None, op0=ALU.mult),
                  reads=[convin, cwt], writes=[acc])
            for tap in range(1, 5):
                kb.op("dve", lambda e: e.scalar_tensor_tensor(out=a, in0=convin[:, s_, tap:tap + L], scalar=cwt[:, cc, tap:tap + 1], in1=a,
                                                              op0=ALU.mult, op1=ALU.add), reads=[convin, cwt, acc], writes=[acc])
            if isinstance(dst_buf, list):
                for gg in range(2):
                    kb.op("act", lambda e: e.activation(out=dst_buf[gg][gg * 64:(gg + 1) * 64, s_ * L:(s_ + 1) * L], in_=acc[gg * 64:(gg + 1) * 64, s_ * L:(s_ + 1) * L],
                                                        func=AF.Silu, bias=cbt[gg * 64:(gg + 1) * 64, cc:cc + 1]), reads=[acc, cbt], writes=[dst_buf[gg]])
            else:
                kb.op("act", lambda e: e.activation(out=dst_ap_fn(s_), in_=a, func=AF.Silu, bias=cbt[:, cc:cc + 1]), reads=[acc, cbt], writes=[dst_buf])

    def evac_conv_in(convin, p, tb, nseq, L):
        if nseq == 1:
            kb.op("act", lambda e: e.activation(out=convin[:, 0, 2 + tb * 512:2 + (tb + 1) * 512], in_=p[:, :], func=AF.Copy), reads=[p], writes=[convin])
        else:
            for s_ in range(2):
                kb.op("act", lambda e: e.activation(out=convin[:, s_, 2:2 + L], in_=p[:, s_ * L:(s_ + 1) * L], func=AF.Copy), reads=[p], writes=[convin])

    def transpose_to_tm(src, dst_fn, dst_buf, T):
        for t0 in range(0, T, 4):
            p = ps()
            for j in range(4):
                kb.op("pe", lambda e: e.matmul(p[:, j * 128:(j + 1) * 128], lhsT=src[:, (t0 + j) * 128:(t0 + j + 1) * 128], rhs=ident_b, start=True, stop=True),
                      reads=[src, cb], writes=[p], inc=(j == 3))
            kb.op("act", lambda e: e.activation(out=dst_fn(t0, 4), in_=p[:, :].rearrange("p (t c) -> p t c", t=4), func=AF.Copy), reads=[p], writes=[dst_buf])

    def ssd(l, g, nblk):
        sample = (g == 1)
        L = 2048 if sample else 256
        nseq = 1 if sample else 2
        lt = L // 128
        ntok = nblk * 512
        T = ntok // 128
        blocks = list(range(nblk))
        with ExitStack() as st:
            xtm = alloc(st, "xtm", [128, T, 512], BF16)
            Btm = alloc(st, "Btm", [128, T, 128], BF16)
            BT = alloc(st, "BT", [128, ntok], BF16)
            CTz = [alloc(st, "CT%d" % i, [128, ntok], BF16) for i in range(2)]
            for i_ in range(2):
                kb.op("dve", lambda e: e.memset(CTz[i_][:], 0.0), writes=[CTz[i_]])
            dtt = alloc(st, "dtt", [128, T, 16], F32)
            dtA = alloc(st, "dtA", [128, T, 16], F32)
            cum = alloc(st, "cum", [128, T, 16], F32)
            tot = alloc(st, "tot", [128, T, 16], F32)
            ecum = alloc(st, "ecum", [128, T, 16], F32)
            wd = alloc(st, "wd", [128, T, 16], F32)
            bj = alloc(st, "bj", [128, T, 16], F32)
            dec = alloc(st, "dec", [128, T, 2, 4], F32)
            abc = alloc(st, "abc", [128, 16], F32)
            dtb = alloc(st, "dtb", [128, 16], F32)
            dsk = alloc(st, "dsk", [128, 8], F32)
            ng = alloc(st, "ng", [128, 512], F32)
            cwt = alloc(st, "cwt", [128, 6, 5], F32)
            cbt = alloc(st, "cbt", [128, 6], F32)
            load_wo(l, 0)
            kb.dma("sp", abc[:], D["ssd_a_log"][l].partition_broadcast(128), writes=[abc])
            kb.dma("sp", dtb[:], D["ssd_dt_bias"][l].partition_broadcast(128), writes=[dtb])
            kb.dma("sp", dsk[:], D["ssd_d"][l].partition_broadcast(128), writes=[dsk])
            kb.dma("sp", ng[:], D["ssd_norm"][l].partition_broadcast(128), writes=[ng])
            for tap in range(5):
                kb.dma("sp", cwt[:, :, tap], D["conv_ssd_w"][l, tap].rearrange("(c p) -> p c", p=128), writes=[cwt], allow_slow_non_contiguous=True)
            kb.dma("sp", cbt[:], D["conv_ssd_b"][l].rearrange("(c p) -> p c", p=128), writes=[cbt], allow_slow_non_contiguous=True)
            kb.op("act", lambda e: e.activation(out=abc[:], in_=abc[:], func=AF.Exp), reads=[abc], writes=[abc])
            kb.op("dve", lambda e: e.tensor_scalar(out=abc[:], in0=abc[:], scalar1=-1.0, scalar2=None, op0=ALU.mult), reads=[abc], writes=[abc])
            with ExitStack() as st1:
                convins = [alloc(st1, "convin", [128, nseq, L + 4], F32) for _ in range(2)]
                acc = alloc(st1, "cacc", [128, ntok], F32)
                xcT = alloc(st1, "xcT", [128, ntok], BF16)
                for cv_ in convins:
                    kb.op("dve", lambda e: e.memset(cv_[:], 0.0), writes=[cv_])

                def cb_fn(ci, tb, p, w):
                    convin = convins[ci % 2]
                    evac_conv_in(convin, p, tb, nseq, L)
                    if tb != blocks[-1]:
                        return
                    if ci < 4:
                        conv_chunk(convin, acc, cwt, cbt, ci, nseq, L, lambda s_: xcT[:, s_ * L:(s_ + 1) * L], xcT)
                        transpose_to_tm(xcT, lambda t0, n: xtm[:, t0:t0 + n, ci * 128:(ci + 1) * 128], xtm, T)
                    elif ci == 4:
                        conv_chunk(convin, acc, cwt, cbt, ci, nseq, L, lambda s_: BT[:, s_ * L:(s_ + 1) * L], BT)
                        transpose_to_tm(BT, lambda t0, n: Btm[:, t0:t0 + n, :], Btm, T)
                    else:
                        conv_chunk(convin, acc, cwt, cbt, ci, nseq, L, None, CTz)
                proj_fm(l, 512, 768, blocks, cb_fn)
            kb.barrier()
            if SSD_PH < 2:
                return
            wb, wv = wload([(D["w_in"][l][:, 1280:1296], 16)], 8)
            for tt in range(T):
                p = ps()
                proj_tm(wb, wv, 0, 16, tt // 4, tt % 4, p)
                kb.op("dve", lambda e: e.tensor_tensor(out=dtt[:, tt, :], in0=p[:, 0:16], in1=dtb[:], op=ALU.add), reads=[p, dtb], writes=[dtt])
            kb.op("act", lambda e: e.activation(out=dtt[:], in_=dtt[:], func=AF.Exp), reads=[dtt], writes=[dtt])
            kb.op("act", lambda e: e.activation(out=dtt[:], in_=dtt[:], func=AF.Ln, bias=1.0), reads=[dtt], writes=[dtt])
            kb.op("dve", lambda e: e.tensor_tensor(out=dtA[:], in0=dtt[:], in1=bcast(abc[:], [128, T, 16], 1), op=ALU.mult), reads=[dtt, abc], writes=[dtA])
            for tt in range(T):
                p = ps()
                kb.op("pe", lambda e: e.matmul(p[:, 0:16], lhsT=triF_f, rhs=dtA[:, tt, :], start=True, stop=True), reads=[cf, dtA], writes=[p], inc=False)
                kb.op("pe", lambda e: e.matmul(p[:, 16:32], lhsT=triB_f, rhs=dtA[:, tt, :], start=True, stop=True), reads=[cf, dtA], writes=[p], inc=False)
                kb.op("pe", lambda e: e.matmul(p[:, 32:48], lhsT=ones_f, rhs=dtA[:, tt, :], start=True, stop=True), reads=[cf, dtA], writes=[p])
                kb.op("dve", lambda e: e.tensor_copy(out=cum[:, tt, 0:8], in_=p[:, 0:8]), reads=[p], writes=[cum])
                kb.op("dve", lambda e: e.tensor_copy(out=cum[:, tt, 8:16], in_=p[:, 24:32]), reads=[p], writes=[cum])
                kb.op("dve", lambda e: e.tensor_copy(out=tot[:, tt, :], in_=p[:, 32:48]), reads=[p], writes=[tot])
            kb.op("act", lambda e: e.activation(out=ecum[:], in_=cum[:], func=AF.Exp), reads=[cum], writes=[ecum])
            kb.op("dve", lambda e: e.tensor_tensor(out=wd[:], in0=tot[:], in1=cum[:], op=ALU.subtract), reads=[tot, cum], writes=[wd])
            kb.op("act", lambda e: e.activation(out=wd[:], in_=wd[:], func=AF.Exp), reads=[wd], writes=[wd])
            kb.op("dve", lambda e: e.tensor_tensor(out=wd[:], in0=wd[:], in1=dtt[:], op=ALU.mult), reads=[wd, dtt], writes=[wd])
            kb.op("act", lambda e: e.activation(out=bj[:], in_=dtt[:], func=AF.Ln), reads=[dtt], writes=[bj])
            kb.op("dve", lambda e: e.tensor_tensor(out=bj[:], in0=bj[:], in1=cum[:], op=ALU.subtract), reads=[bj, cum], writes=[bj])
            tot4 = tot[:].rearrange("p t (d h) -> p t d h", d=2)
            for gg in range(2):
                kb.op("act", lambda e: e.activation(out=dec[gg * 64:(gg + 1) * 64], in_=tot4[gg * 64:(gg + 1) * 64, :, :, gg * 4:(gg + 1) * 4], func=AF.Exp),
                      reads=[tot], writes=[dec])
            if SSD_PH < 3:
                kb.barrier()
                return
            with ExitStack() as st2:
                Hprev = alloc(st2, "Hprev", [128, T, 2, 256], BF16)
                st3 = ExitStack()
                Hs = [alloc(st3, "Hs%d" % i, [128, 256], F32) for i in range(2)]
                xws = [alloc(st3, "xw%d" % i, [128, 512], BF16) for i in range(2)]
                hx = alloc(st3, "hx", [128, 2, 128], F32)
                ho = alloc(st3, "ho", [128, 128], F32)
                xi = 0
                for dr in range(2):
                    H = Hs[dr]
                    for s_ in range(nseq):
                        if sample:
                            for blk in range(2):
                                for two in range(2):
                                    kb.dma("sp", hx[two * 64:(two + 1) * 64, blk, :].rearrange("p (g n) -> p g n", g=2),
                                           D["sssm"][l, dr].rearrange("(g r) p n -> r p g n", g=2)[blk * 2 + two], writes=[hx])
                            for blk in range(2):
                                p = ps()
                                kb.op("pe", lambda e: e.matmul(p[:, 0:128], lhsT=hx[:, blk, :], rhs=ident_f, start=True, stop=True), reads=[hx, cf], writes=[p])
                                kb.op("act", lambda e: e.activation(out=H[:, blk * 128:(blk + 1) * 128], in_=p[:, 0:128], func=AF.Copy), reads=[p], writes=[H])
                        else:
                            kb.op("dve", lambda e: e.memset(H[:], 0.0), writes=[H])
                        order = range(lt) if dr == 0 else range(lt - 1, -1, -1)
                        for r0 in order:
                            tt = s_ * lt + r0
                            kb.op("act", lambda e: e.activation(out=Hprev[:, tt, dr, :], in_=H[:], func=AF.Copy), reads=[H], writes=[Hprev])
                            xw = xws[xi % 2]
                            xi += 1
                            kb.op("dve", lambda e: e.tensor_tensor(out=xw[:].rearrange("p (h d) -> p h d", d=64), in0=xtm[:, tt, :].rearrange("p (h d) -> p h d", d=64),
                                                                   in1=bcast(wd[:, tt, dr * 8:(dr + 1) * 8], [128, 8, 64], 2), op=ALU.mult), reads=[xtm, wd], writes=[xw])
                            p = ps()
                            kb.op("pe", lambda e: e.matmul(p[:, :], lhsT=Btm[:, tt, :], rhs=xw[:], start=True, stop=True), reads=[Btm, xw], writes=[p])
                            kb.op("dve", lambda e: e.tensor_tensor(out=H[:].rearrange("p (h d) -> p h d", d=64), in0=H[:].rearrange("p (h d) -> p h d", d=64),
                                                                   in1=bcast(dec[:, tt, dr, :], [128, 4, 64], 2), op=ALU.mult), reads=[H, dec], writes=[H])
                            for gg in range(2):
                                kb.op("dve", lambda e: e.tensor_tensor(out=H[gg * 64:(gg + 1) * 64, :], in0=H[gg * 64:(gg + 1) * 64, :],
                                                                       in1=p[gg * 64:(gg + 1) * 64, gg * 256:(gg + 1) * 256], op=ALU.add), reads=[H, p], writes=[H])
                        if not sample:
                            for blk in range(2):
                                p = ps()
                                kb.op("pe", lambda e: e.matmul(p[:, 0:128], lhsT=H[:, blk * 128:(blk + 1) * 128], rhs=ident_f, start=True, stop=True), reads=[H, cf], writes=[p])
                                kb.op("act", lambda e: e.activation(out=ho[:], in_=p[:, 0:128], func=AF.Copy), reads=[p], writes=[ho])
                                for two in range(2):
                                    kb.dma("sp", D["nssm"][s_, l, dr].rearrange("(g r) p n -> r p g n", g=2)[blk * 2 + two],
                                           ho[two * 64:(two + 1) * 64, :].rearrange("p (g n) -> p g n", g=2), reads=[ho])
                kb.barrier()
                st3.close()
                if SSD_PH < 4:
                    return
                Dg = alloc(st2, "Dg", [128, 16, 128], F32)
                Es = [alloc(st2, "E%d" % i, [128, 128], F32) for i in range(3)]
                Ms = [alloc(st2, "M%d" % i, [128, 128], BF16) for i in range(3)]
                ya = alloc(st2, "ya", [128, 512], F32)
                yu = alloc(st2, "yu", [128, 512], F32)
                zs = yu
                ytm = alloc(st2, "ytm", [128, 4, 512], BF16)
                yT = alloc(st2, "yT", [128, 4, 512], BF16)
                ss = alloc(st2, "ss", [128, 4], F32)
                wzb, wzv = wload([(D["w_in"][l][:, 0:512], 512)], 8)
                ei = 0
                for tt in range(T):
                    tk = slice(tt * 128, (tt + 1) * 128)
                    pz = ps_acc()
                    proj_tm(wzb, wzv, 0, 512, tt // 4, tt % 4, pz)
                    pBC = ps_acc()
                    for gg in range(2):
                        kb.op("pe", lambda e: e.matmul(pBC[:, gg * 128:(gg + 1) * 128], lhsT=BT[:, tk], rhs=CTz[gg][:, tk], start=True, stop=True),
                              reads=[BT, CTz[gg]], writes=[pBC], inc=(gg == 1))
                    kb.op("dve", lambda e: e.tensor_tensor(out=Dg[:], in0=bcast(ident_f, [128, 16, 128], 1), in1=bcast(cum[:, tt, :], [128, 16, 128], 2), op=ALU.mult),
                          reads=[cf, cum], writes=[Dg])
                    yint = ps_acc()
                    hd = [(h, dr) for h in range(8) for dr in range(2)]
                    mts = {}

                    def sA(i):
                        h, dr = hd[i]
                        pE = ps()
                        kb.op("pe", lambda e: e.matmul(pE[:, 0:128], lhsT=ones_f, rhs=Dg[:, dr * 8 + h, :], start=True, stop=False), reads=[cf, Dg], writes=[pE], inc=False)
                        kb.op("pe", lambda e: e.matmul(pE[:, 0:128], lhsT=ident_b, rhs=(maskF_b if dr == 0 else maskB_b), start=False, stop=True), reads=[cb], writes=[pE])
                        E = Es[i % 3]
                        M = Ms[i % 3]
                        kb.op("act", lambda e: e.activation(out=E[:], in_=pE[:, 0:128], func=AF.Exp, bias=bj[:, tt, dr * 8 + h:dr * 8 + h + 1]), reads=[pE, bj], writes=[E])
                        gg = h // 4
                        kb.op("dve", lambda e: e.tensor_tensor(out=M[:], in0=E[:], in1=pBC[:, gg * 128:(gg + 1) * 128], op=ALU.mult), reads=[E, pBC], writes=[M])
                        mts[i] = M

                    def sC(i):
                        h, dr = hd[i]
                        M = mts.pop(i)
                        kb.op("pe", lambda e: e.matmul(yint[:, h * 64:(h + 1) * 64], lhsT=M[:], rhs=xtm[:, tt, h * 64:(h + 1) * 64], start=(dr == 0), stop=(dr == 1)),
                              reads=[M, xtm], writes=[yint], inc=True)

                    for i in range(16 + 2):
                        if i < 16:
                            sA(i)
                        if i >= 2:
                            sC(i - 2)
                    if SSD_PH < 5:
                        continue
                    pY = [ps(), ps()]
                    for dr in range(2):
                        for gg in range(2):
                            kb.op("pe", lambda e: e.matmul(pY[dr][:, gg * 256:(gg + 1) * 256], lhsT=CTz[gg][:, tk], rhs=Hprev[:, tt, dr, :], start=True, stop=True),
                                  reads=[CTz[gg], Hprev], writes=[pY[dr]], inc=(gg == 1))
                    v3 = lambda ap: ap.rearrange("p (h d) -> p h d", d=64)
                    kb.op("dve", lambda e: e.tensor_tensor(out=v3(ya[:]), in0=v3(xtm[:, tt, :]), in1=bcast(dsk[:], [128, 8, 64], 2), op=ALU.mult), reads=[xtm, dsk], writes=[ya])
                    kb.op("dve", lambda e: e.tensor_tensor(out=ya[:], in0=ya[:], in1=yint[:, :], op=ALU.add), reads=[ya, yint], writes=[ya])
                    for dr in range(2):
                        kb.op("dve", lambda e: e.tensor_tensor(out=v3(yu[:]), in0=v3(pY[dr][:, :]), in1=bcast(ecum[:, tt, dr * 8:(dr + 1) * 8], [128, 8, 64], 2), op=ALU.mult),
                              reads=[pY[dr], ecum], writes=[yu])
                        kb.op("dve", lambda e: e.tensor_tensor(out=ya[:], in0=ya[:], in1=yu[:], op=ALU.add), reads=[ya, yu], writes=[ya])
                    if SSD_PH < 6:
                        continue
                    kb.op("act", lambda e: e.activation(out=zs[:], in_=pz[:, :], func=AF.Silu), reads=[pz], writes=[zs])
                    kb.op("dve", lambda e: e.tensor_tensor(out=ya[:], in0=ya[:], in1=zs[:], op=ALU.mult), reads=[ya, zs], writes=[ya])
                    kb.op("dve", lambda e: e.memset(ss[:, 0:1], 0.0), writes=[ss])
                    kb.op("act", lambda e: e.activation(out=yu[:], in_=ya[:], func=AF.Square, accum_out=ss[:, 0:1]), reads=[ya, ss], writes=[yu, ss])
                    kb.op("dve", lambda e: e.tensor_scalar(out=ss[:, 1:2], in0=ss[:, 0:1], scalar1=1.0 / 512, scalar2=EPS, op0=ALU.mult, op1=ALU.add), reads=[ss], writes=[ss])
                    kb.op("act", lambda e: e.activation(out=ss[:, 1:2], in_=ss[:, 1:2], func=AF.Sqrt), reads=[ss], writes=[ss])
                    kb.op("dve", lambda e: e.reciprocal(out=ss[:, 2:3], in_=ss[:, 1:2]), reads=[ss], writes=[ss])
                    kb.op("dve", lambda e: e.scalar_tensor_tensor(out=ytm[:, tt % 4, :], in0=ya[:], scalar=ss[:, 2:3], in1=ng[:], op0=ALU.mult, op1=ALU.mult),
                          reads=[ya, ss, ng], writes=[ytm])
                    if SSD_PH < 7:
                        continue
                    if tt % 4 == 3:
                        mixer_out(st2, ytm, tt // 4, yT)
        kb.barrier()


    def mlstm(l, g, nblk):
        sample = (g == 1)
        L = 2048 if sample else 256
        nseq = 1 if sample else 2
        lt = L // 128
        ntok = nblk * 512
        T = ntok // 128
        blocks = list(range(nblk))
        C0 = 2832
        lns = math.log(128 ** -0.5)
        for hg in range(2):
            h0 = hg * 2
            with ExitStack() as st:
                qT = alloc(st, "mqT", [128, 2, ntok], BF16)
                kT = alloc(st, "mkT", [128, 2, ntok], BF16)
                vaug = alloc(st, "mvaug", [128, T, 2, 129], BF16)
                Cpb = alloc(st, "Cpb", [128, T, 2, 129], BF16)
                li = alloc(st, "li", [128, T, 4], F32)
                lf = alloc(st, "lf", [128, T, 4], F32)
                G = alloc(st, "G", [128, T, 4], F32)
                tot = alloc(st, "mtot", [128, T, 4], F32)
                pj = alloc(st, "pj", [128, T, 4], F32)
                pjs = alloc(st, "pjs", [128, T, 4], F32)
                gend = alloc(st, "gend", [128, T, 4], F32)
                mlb = alloc(st, "mlb", [128, T, 4], F32)
                wend = alloc(st, "wend", [128, T, 4], F32)
                mprev = alloc(st, "mprev", [128, T, 4], F32)
                gb = alloc(st, "gb", [128, 8], F32)
                cwt = alloc(st, "mcwt", [128, 4, 5], F32)
                cbt = alloc(st, "mcbt", [128, 4], F32)
                ng = alloc(st, "mng", [128, 256], F32)
                kb.dma("pool", wo_buf[:, 0:2, :], D["w_out"][l][1024 + h0 * 128:1024 + (h0 + 2) * 128, :].rearrange("(k p) n -> p k n", p=128), writes=[wo_buf])
                kb.dma("sp", ng[:], D["mlstm_norm"][l][h0 * 128:(h0 + 2) * 128].partition_broadcast(128), writes=[ng])
                goffs = [0 * 8 + 0 * 4 + h0, 1 * 8 + 0 * 4 + h0, 0 * 8 + 1 * 4 + h0, 1 * 8 + 1 * 4 + h0]
                for i_, go in enumerate(goffs):
                    kb.dma("sp", gb[:, i_ * 2:(i_ + 1) * 2], D["mlstm_gate_b"][l][go:go + 2].partition_broadcast(128), writes=[gb])
                for ci, ch0 in enumerate((h0 * 128, (h0 + 1) * 128, 512 + h0 * 128, 512 + (h0 + 1) * 128)):
                    for tap in range(5):
                        kb.dma("sp", cwt[:, ci, tap:tap + 1], D["conv_mlstm_w"][l, tap][ch0:ch0 + 128].rearrange("(p o) -> p o", o=1), writes=[cwt])
                    kb.dma("sp", cbt[:, ci:ci + 1], D["conv_mlstm_b"][l][ch0:ch0 + 128].rearrange("(p o) -> p o", o=1), writes=[cbt])
                kb.op("dve", lambda e: e.memset(vaug[:, :, :, 128:129], 1.0), writes=[vaug])
                with ExitStack() as st1:
                    convins = [alloc(st1, "mconvin", [128, nseq, L + 4], F32) for _ in range(2)]
                    acc = alloc(st1, "mcacc", [128, ntok], F32)
                    for cv_ in convins:
                        kb.op("dve", lambda e: e.memset(cv_[:], 0.0), writes=[cv_])
                    wb, wv = wload([(D["w_in"][l][:, C0 + h0 * 128:C0 + (h0 + 2) * 128], 256),
                                    (D["w_in"][l][:, C0 + 512 + h0 * 128:C0 + 512 + (h0 + 2) * 128], 256)], 8)
                    for ci in range(4):
                        for tb in blocks:
                            p = ps()
                            for k in range(8):
                                kb.op("pe", lambda e: e.matmul(p[:, :], lhsT=wv[:, k, ci * 128:(ci + 1) * 128], rhs=hT[tb][:, k, :], start=(k == 0), stop=(k == 7)),
                                      reads=[wb, hT[tb]], writes=[p], inc=(k == 7))
                            evac_conv_in(convins[ci % 2], p, tb, nseq, L)
                        convin = convins[ci % 2]
                        dstb = qT if ci < 2 else kT
                        conv_chunk(convin, acc, cwt, cbt, ci, nseq, L, lambda s_: dstb[:, ci % 2, s_ * L:(s_ + 1) * L], dstb)
                kb.barrier()
                wb, wv = wload([(D["w_in"][l][:, C0 + 1024 + h0 * 128:C0 + 1024 + (h0 + 2) * 128], 256)], 8)
                for tt in range(T):
                    p = ps()
                    proj_tm(wb, wv, 0, 256, tt // 4, tt % 4, p)
                    kb.op("act", lambda e: e.activation(out=vaug[:, tt, :, 0:128], in_=p[:, 0:256].rearrange("p (h e) -> p h e", e=128), func=AF.Copy), reads=[p], writes=[vaug])
                gc = C0 + 2048
                wb, wv = wload([(D["w_in"][l][:, gc + go:gc + go + 2], 2) for go in goffs], 8)
                for tt in range(T):
                    p = ps()
                    proj_tm(wb, wv, 0, 8, tt // 4, tt % 4, p)
                    kb.op("dve", lambda e: e.tensor_tensor(out=li[:, tt, :], in0=p[:, 0:4], in1=gb[:, 0:4], op=ALU.add), reads=[p, gb], writes=[li])
                    kb.op("dve", lambda e: e.tensor_tensor(out=lf[:, tt, :], in0=p[:, 4:8], in1=gb[:, 4:8], op=ALU.add), reads=[p, gb], writes=[lf])
                kb.op("act", lambda e: e.activation(out=lf[:], in_=lf[:], func=AF.Exp, scale=-1.0), reads=[lf], writes=[lf])
                kb.op("act", lambda e: e.activation(out=lf[:], in_=lf[:], func=AF.Ln, bias=1.0), reads=[lf], writes=[lf])
                kb.op("dve", lambda e: e.tensor_scalar(out=lf[:], in0=lf[:], scalar1=-1.0, scalar2=None, op0=ALU.mult), reads=[lf], writes=[lf])
                for tt in range(T):
                    p = ps()
                    kb.op("pe", lambda e: e.matmul(p[:, 0:4], lhsT=triF_f, rhs=lf[:, tt, :], start=True, stop=True), reads=[cf, lf], writes=[p], inc=False)
                    kb.op("pe", lambda e: e.matmul(p[:, 4:8], lhsT=triB_f, rhs=lf[:, tt, :], start=True, stop=True), reads=[cf, lf], writes=[p], inc=False)
                    kb.op("pe", lambda e: e.matmul(p[:, 8:12], lhsT=ones_f, rhs=lf[:, tt, :], start=True, stop=True), reads=[cf, lf], writes=[p])
                    kb.op("dve", lambda e: e.tensor_copy(out=G[:, tt, 0:2], in_=p[:, 0:2]), reads=[p], writes=[G])
                    kb.op("dve", lambda e: e.tensor_copy(out=G[:, tt, 2:4], in_=p[:, 6:8]), reads=[p], writes=[G])
                    kb.op("dve", lambda e: e.tensor_copy(out=tot[:, tt, :], in_=p[:, 8:12]), reads=[p], writes=[tot])
                kb.op("dve", lambda e: e.tensor_tensor(out=pj[:], in0=li[:], in1=G[:], op=ALU.subtract), reads=[li, G], writes=[pj])
                kb.op("dve", lambda e: e.tensor_scalar(out=pjs[:], in0=pj[:], scalar1=lns, scalar2=None, op0=ALU.add), reads=[pj], writes=[pjs])
                kb.op("dve", lambda e: e.tensor_tensor(out=gend[:], in0=pj[:], in1=tot[:], op=ALU.add), reads=[pj, tot], writes=[gend])
                with ExitStack() as stt:
                    mrow = alloc(stt, "mrow8", [4, 1], F32)
                    d8 = alloc(stt, "d8", [4, 4], F32)
                    for tt in range(T):
                        p = ps()
                        kb.op("pe", lambda e: e.matmul(p[0:4, 0:128], lhsT=gend[:, tt, :], rhs=ident_f, start=True, stop=True), reads=[gend, cf], writes=[p])
                        kb.op("dve", lambda e: e.tensor_reduce(out=mrow[:], in_=p[0:4, 0:128], axis=AX.X, op=ALU.max), reads=[p], writes=[mrow])
                        kb.op("dve", lambda e: e.tensor_scalar(out=d8[:], in0=ident_f[0:4, 0:4], scalar1=mrow[:, 0:1], scalar2=None, op0=ALU.mult), reads=[cf, mrow], writes=[d8])
                        p2 = ps()
                        kb.op("pe", lambda e: e.matmul(p2[:, 0:4], lhsT=ones_f[0:4, :], rhs=d8[:], start=True, stop=True), reads=[cf, d8], writes=[p2])
                        kb.op("dve", lambda e: e.tensor_copy(out=mlb[:, tt, :], in_=p2[:, 0:4]), reads=[p2], writes=[mlb])
                    kb.barrier()
                kb.op("dve", lambda e: e.tensor_tensor(out=wend[:], in0=gend[:], in1=mlb[:], op=ALU.subtract), reads=[gend, mlb], writes=[wend])
                kb.op("act", lambda e: e.activation(out=wend[:], in_=wend[:], func=AF.Exp), reads=[wend], writes=[wend])
                with ExitStack() as st2:
                    Cst = [alloc(st2, "Cst%d" % i, [128, 129], F32) for i in range(4)]
                    mp = alloc(st2, "mp", [128, 4], F32)
                    mt8 = alloc(st2, "mt8", [128, 16], F32)
                    kwt = alloc(st2, "kwt", [128, 2, 128], BF16)
                    Dg = alloc(st2, "mDg", [128, 4, 128], F32)
                    Dr = alloc(st2, "mDr", [128, 4, 128], F32)
                    sc = alloc(st2, "msc", [128, 40], F32)
                    Es = [alloc(st2, "mE%d" % i, [128, 128], F32) for i in range(3)]
                    Ms = [alloc(st2, "mM%d" % i, [128, 128], BF16) for i in range(3)]
                    nds = [alloc(st2, "nd%d" % i, [128, 129], F32) for i in range(2)]
                    cbf = [alloc(st2, "cbf%d" % i, [128, 129], BF16) for i in range(2)]
                    hsum = alloc(st2, "hsum", [128, 256], F32)
                    ht = alloc(st2, "mht", [128, 256], F32)
                    sg = alloc(st2, "msg", [128, 256], F32)
                    ytm = alloc(st2, "mytm", [128, 4, 256], BF16)
                    yT = alloc(st2, "myT", [128, 2, 512], BF16)
                    ei = 0
                    ei0 = [0]

                    def init_state(dr, s_):
                        for hh in range(2):
                            C = Cst[dr * 2 + hh]
                            if sample:
                                kb.dma("sp", C[:, 0:128], D["smc"][l, dr, h0 + hh], writes=[C])
                                kb.dma("sp", C[:, 128:129], D["smn"][l, dr, h0 + hh].rearrange("(p o) -> p o", o=1), writes=[C])
                            else:
                                kb.op("dve", lambda e: e.memset(C[:], 0.0), writes=[C])
                        if sample:
                            kb.dma("sp", mp[:, dr * 2:dr * 2 + 2], D["smm"][l][dr * 4 + h0:dr * 4 + h0 + 2].partition_broadcast(128), writes=[mp])
                        else:
                            kb.op("dve", lambda e: e.memset(mp[:, dr * 2:dr * 2 + 2], 0.0), writes=[mp])

                    def local_update(dr, tt):
                        cs = slice(dr * 2, dr * 2 + 2)
                        pk = ps()
                        for hh in range(2):
                            kb.op("pe", lambda e: e.matmul(pk[:, hh * 128:(hh + 1) * 128], lhsT=kT[:, hh, tt * 128:(tt + 1) * 128], rhs=ident_b, start=True, stop=True),
                                  reads=[kT, cb], writes=[pk], inc=(hh == 1))
                        kb.op("dve", lambda e: e.tensor_tensor(out=kwt[:], in0=pk[:, 0:256].rearrange("p (h d) -> p h d", d=128),
                                                               in1=bcast(wend[:, tt, cs], [128, 2, 128], 2), op=ALU.mult), reads=[pk, wend], writes=[kwt])
                        a = sc[:, 0:2]
                        mn = sc[:, 2:4]
                        sp_ = sc[:, 4:6]
                        sl_ = sc[:, 6:8]
                        kb.op("dve", lambda e: e.tensor_tensor(out=a, in0=tot[:, tt, cs], in1=mp[:, cs], op=ALU.add), reads=[tot, mp], writes=[sc])
                        kb.op("dve", lambda e: e.tensor_tensor(out=mn, in0=a, in1=mlb[:, tt, cs], op=ALU.max), reads=[sc, mlb], writes=[sc])
                        kb.op("dve", lambda e: e.tensor_tensor(out=sp_, in0=a, in1=mn, op=ALU.subtract), reads=[sc], writes=[sc])
                        kb.op("dve", lambda e: e.tensor_tensor(out=sl_, in0=mlb[:, tt, cs], in1=mn, op=ALU.subtract), reads=[sc, mlb], writes=[sc])
                        kb.op("act", lambda e: e.activation(out=sc[:, 4:8], in_=sc[:, 4:8], func=AF.Exp), reads=[sc], writes=[sc])
                        kb.op("dve", lambda e: e.tensor_copy(out=mp[:, cs], in_=mn), reads=[sc], writes=[mp])
                        for hh in range(2):
                            C = Cst[dr * 2 + hh]
                            pc = ps()
                            kb.op("pe", lambda e: e.matmul(pc[:, 0:129], lhsT=kwt[:, hh, :], rhs=vaug[:, tt, hh, :], start=True, stop=True), reads=[kwt, vaug], writes=[pc])
                            kb.op("dve", lambda e: e.tensor_scalar(out=C[:], in0=C[:], scalar1=sc[:, 4 + hh:5 + hh], scalar2=None, op0=ALU.mult), reads=[C, sc], writes=[C])
                            kb.op("dve", lambda e: e.scalar_tensor_tensor(out=C[:], in0=pc[:, 0:129], scalar=sc[:, 6 + hh:7 + hh], in1=C[:], op0=ALU.mult, op1=ALU.add),
                                  reads=[pc, sc, C], writes=[C])

                    def final_state(dr, s_):
                        for hh in range(2):
                            C = Cst[dr * 2 + hh]
                            kb.dma("sp", D["nmc"][s_, l, dr, h0 + hh], C[:, 0:128], reads=[C])
                            kb.dma("sp", D["nmn"][s_, l, dr, h0 + hh].rearrange("(p o) -> p o", o=1), C[:, 128:129], reads=[C])
                        kb.dma("sp", D["nmm"][s_, l:l + 1, dr * 4 + h0:dr * 4 + h0 + 2], mp[0:1, dr * 2:dr * 2 + 2], reads=[mp])

                    for s_ in range(nseq):
                        init_state(1, s_)
                        for r0 in range(lt - 1, -1, -1):
                            tt = s_ * lt + r0
                            for hh in range(2):
                                kb.op("act", lambda e: e.activation(out=Cpb[:, tt, hh, :], in_=Cst[2 + hh][:], func=AF.Copy), reads=[Cst[2 + hh]], writes=[Cpb])
                            kb.op("dve", lambda e: e.tensor_copy(out=mprev[:, tt, 2:4], in_=mp[:, 2:4]), reads=[mp], writes=[mprev])
                            local_update(1, tt)
                        if not sample:
                            final_state(1, s_)
                    wob, wov = wload([(D["w_in"][l][:, C0 + 1536 + h0 * 128:C0 + 1536 + (h0 + 2) * 128], 256)], 8)
                    for s_ in range(nseq):
                        init_state(0, s_)
                        for r0 in range(lt):
                            tt = s_ * lt + r0
                            tk = slice(tt * 128, (tt + 1) * 128)
                            kb.op("dve", lambda e: e.tensor_copy(out=mprev[:, tt, 0:2], in_=mp[:, 0:2]), reads=[mp], writes=[mprev])
                            kb.op("dve", lambda e: e.tensor_tensor(out=sc[:, 8:12], in0=G[:, tt, :], in1=mprev[:, tt, :], op=ALU.add), reads=[G, mprev], writes=[sc])
                            kb.op("dve", lambda e: e.tensor_tensor(out=Dg[:], in0=bcast(ident_f, [128, 4, 128], 1), in1=bcast(pj[:, tt, :], [128, 4, 128], 2), op=ALU.mult),
                                  reads=[cf, pj], writes=[Dg])
                            po = ps_acc()
                            proj_tm(wob, wov, 0, 256, tt // 4, tt % 4, po)
                            pSs = []
                            for hh in range(2):
                                pS = ps_acc()
                                kb.op("pe", lambda e: e.matmul(pS[:, 0:128], lhsT=kT[:, hh, tk], rhs=qT[:, hh, tk], start=True, stop=True), reads=[kT, qT], writes=[pS])
                                pSs.append(pS)
                            pms = []
                            for c in range(4):
                                dr = c // 2
                                pm = ps()
                                kb.op("pe", lambda e: e.matmul(pm[:, 0:128], lhsT=ones_f, rhs=Dg[:, c, :], start=True, stop=False), reads=[cf, Dg], writes=[pm], inc=False)
                                kb.op("pe", lambda e: e.matmul(pm[:, 0:128], lhsT=ident_b, rhs=(maskB_b if dr == 0 else maskF_b), start=False, stop=True), reads=[cb], writes=[pm])
                                pms.append(pm)
                            for c in range(4):
                                kb.op("dve", lambda e: e.tensor_reduce(out=sc[:, 12 + c:13 + c], in_=pms[c][:, 0:128], axis=AX.X, op=ALU.max), reads=[pms[c]], writes=[sc])
                            kb.op("dve", lambda e: e.tensor_tensor(out=sc[:, 12:16], in0=sc[:, 12:16], in1=G[:, tt, :], op=ALU.add), reads=[sc, G], writes=[sc])
                            kb.op("dve", lambda e: e.tensor_tensor(out=sc[:, 16:20], in0=sc[:, 12:16], in1=sc[:, 8:12], op=ALU.max), reads=[sc], writes=[sc])
                            kb.op("dve", lambda e: e.tensor_tensor(out=sc[:, 20:24], in0=G[:, tt, :], in1=sc[:, 16:20], op=ALU.subtract), reads=[sc, G], writes=[sc])
                            kb.op("dve", lambda e: e.tensor_tensor(out=sc[:, 24:28], in0=sc[:, 8:12], in1=sc[:, 16:20], op=ALU.subtract), reads=[sc], writes=[sc])
                            kb.op("act", lambda e: e.activation(out=sc[:, 24:28], in_=sc[:, 24:28], func=AF.Exp, bias=lns_col[:, 0:1]), reads=[sc, cf], writes=[sc])
                            kb.op("act", lambda e: e.activation(out=sc[:, 28:32], in_=sc[:, 16:20], func=AF.Exp, scale=-1.0), reads=[sc], writes=[sc])
                            kb.op("dve", lambda e: e.tensor_tensor(out=Dr[:], in0=bcast(ident_f, [128, 4, 128], 1), in1=bcast(sc[:, 20:24], [128, 4, 128], 2), op=ALU.mult),
                                  reads=[cf, sc], writes=[Dr])
                            items = [(0, 0), (0, 1), (1, 0), (1, 1)]
                            mts = {}

                            def mA(i):
                                hh, dr = items[i]
                                c = dr * 2 + hh
                                pW = ps()
                                kb.op("pe", lambda e: e.matmul(pW[:, 0:128], lhsT=ones_f, rhs=Dr[:, c, :], start=True, stop=False), reads=[cf, Dr], writes=[pW], inc=False)
                                kb.op("pe", lambda e: e.matmul(pW[:, 0:128], lhsT=ident_b, rhs=(maskF_b if dr == 0 else maskB_b), start=False, stop=True), reads=[cb], writes=[pW])
                                E = Es[(ei0[0] + i) % 3]
                                M = Ms[(ei0[0] + i) % 3]
                                kb.op("act", lambda e: e.activation(out=E[:], in_=pW[:, 0:128], func=AF.Exp, bias=pjs[:, tt, c:c + 1]), reads=[pW, pjs], writes=[E])
                                kb.op("dve", lambda e: e.tensor_tensor(out=M[:], in0=E[:], in1=pSs[hh][:, 0:128], op=ALU.mult), reads=[E, pSs[hh]], writes=[M])
                                if dr == 0:
                                    kb.op("act", lambda e: e.activation(out=cbf[hh][:], in_=Cst[hh][:], func=AF.Copy), reads=[Cst[hh]], writes=[cbf[hh]])
                                mts[i] = M

                            def mC(i):
                                hh, dr = items[i]
                                c = dr * 2 + hh
                                M = mts.pop(i)
                                nd = nds[i % 2]
                                pN = ps()
                                kb.op("pe", lambda e: e.matmul(pN[:, 0:129], lhsT=M[:], rhs=vaug[:, tt, hh, :], start=True, stop=True), reads=[M, vaug], writes=[pN])
                                pI = ps()
                                if dr == 0:
                                    kb.op("pe", lambda e: e.matmul(pI[:, 0:129], lhsT=qT[:, hh, tk], rhs=cbf[hh][:], start=True, stop=True), reads=[qT, cbf[hh]], writes=[pI])
                                else:
                                    kb.op("pe", lambda e: e.matmul(pI[:, 0:129], lhsT=qT[:, hh, tk], rhs=Cpb[:, tt, hh, :], start=True, stop=True), reads=[qT, Cpb], writes=[pI])
                                kb.op("act", lambda e: e.activation(out=nd[:], in_=pN[:, 0:129], func=AF.Copy), reads=[pN], writes=[nd])
                                kb.op("dve", lambda e: e.scalar_tensor_tensor(out=nd[:], in0=pI[:, 0:129], scalar=sc[:, 24 + c:25 + c], in1=nd[:], op0=ALU.mult, op1=ALU.add),
                                      reads=[pI, sc, nd], writes=[nd])
                                kb.op("dve", lambda e: e.tensor_scalar(out=sc[:, 32:33], in0=nd[:, 128:129], scalar1=-1.0, scalar2=None, op0=ALU.mult), reads=[nd], writes=[sc])
                                kb.op("dve", lambda e: e.tensor_tensor(out=sc[:, 32:33], in0=sc[:, 32:33], in1=nd[:, 128:129], op=ALU.max), reads=[nd, sc], writes=[sc])
                                kb.op("dve", lambda e: e.tensor_tensor(out=sc[:, 32:33], in0=sc[:, 32:33], in1=sc[:, 28 + c:29 + c], op=ALU.max), reads=[sc], writes=[sc])
                                kb.op("dve", lambda e: e.reciprocal(out=sc[:, 33:34], in_=sc[:, 32:33]), reads=[sc], writes=[sc])
                                if dr == 0:
                                    kb.op("dve", lambda e: e.tensor_scalar(out=hsum[:, hh * 128:(hh + 1) * 128], in0=nd[:, 0:128], scalar1=sc[:, 33:34], scalar2=None, op0=ALU.mult),
                                          reads=[nd, sc], writes=[hsum])
                                else:
                                    kb.op("dve", lambda e: e.scalar_tensor_tensor(out=hsum[:, hh * 128:(hh + 1) * 128], in0=nd[:, 0:128], scalar=sc[:, 33:34],
                                                                                  in1=hsum[:, hh * 128:(hh + 1) * 128], op0=ALU.mult, op1=ALU.add), reads=[nd, sc, hsum], writes=[hsum])

                            for i in range(4 + 2):
                                if i < 4:
                                    mA(i)
                                if i >= 2:
                                    mC(i - 2)
                            ei0[0] += 4
                            local_update(0, tt)
                            h3 = hsum[:].rearrange("p (h d) -> p h d", d=128)
                            t3 = ht[:].rearrange("p (h d) -> p h d", d=128)
                            kb.op("dve", lambda e: e.tensor_tensor(out=ht[:], in0=hsum[:], in1=hsum[:], op=ALU.mult), reads=[hsum], writes=[ht])
                            kb.op("dve", lambda e: e.tensor_reduce(out=sc[:, 34:36], in_=t3, axis=AX.X, op=ALU.add), reads=[ht], writes=[sc])
                            kb.op("dve", lambda e: e.tensor_scalar(out=sc[:, 34:36], in0=sc[:, 34:36], scalar1=1.0 / 128, scalar2=EPS, op0=ALU.mult, op1=ALU.add), reads=[sc], writes=[sc])
                            kb.op("act", lambda e: e.activation(out=sc[:, 34:36], in_=sc[:, 34:36], func=AF.Sqrt), reads=[sc], writes=[sc])
                            kb.op("dve", lambda e: e.reciprocal(out=sc[:, 36:38], in_=sc[:, 34:36]), reads=[sc], writes=[sc])
                            kb.op("dve", lambda e: e.tensor_tensor(out=t3, in0=h3, in1=bcast(sc[:, 36:38], [128, 2, 128], 2), op=ALU.mult), reads=[hsum, sc], writes=[ht])
                            kb.op("dve", lambda e: e.tensor_tensor(out=ht[:], in0=ht[:], in1=ng[:], op=ALU.mult), reads=[ht, ng], writes=[ht])
                            kb.op("act", lambda e: e.activation(out=sg[:], in_=po[:, 0:256], func=AF.Sigmoid), reads=[po], writes=[sg])
                            kb.op("dve", lambda e: e.tensor_tensor(out=ytm[:, tt % 4, :], in0=ht[:], in1=sg[:], op=ALU.mult), reads=[ht, sg], writes=[ytm])
                            if tt % 4 == 3:
                                tb = tt // 4
                                for c2 in range(2):
                                    p = ps()
                                    for tl in range(4):
                                        kb.op("pe", lambda e: e.matmul(p[:, tl * 128:(tl + 1) * 128], lhsT=ytm[:, tl, c2 * 128:(c2 + 1) * 128], rhs=ident_b, start=True, stop=True),
                                              reads=[ytm, cb], writes=[p], inc=(tl == 3))
                                    kb.op("act", lambda e: e.activation(out=yT[:, c2, :], in_=p[:, :], func=AF.Copy), reads=[p], writes=[yT])
                                for c in range(8):
                                    p = ps()
                                    for k in range(2):
                                        kb.op("pe", lambda e: e.matmul(p[:, :], lhsT=wo_buf[:, k, c * 128:(c + 1) * 128], rhs=yT[:, k, :], start=(k == 0), stop=(k == 1)),
                                              reads=[wo_buf, yT], writes=[p], inc=(k == 1))
                                    kb.op("dve", lambda e: e.scalar_tensor_tensor(out=xT[tb][:, c, :], in0=p[:, :], scalar=modc[:, 2, c:c + 1], in1=xT[tb][:, c, :],
                                                                                  op0=ALU.mult, op1=ALU.add), reads=[p, modc, xT[tb]], writes=[xT[tb]])
                        if not sample:
                            final_state(0, s_)
            kb.barrier()

    def run_pass(g, src, dst, nblk):
        load_x(src, nblk)
        kb.barrier()
        for l in range(NL):
            load_mod(l, g)
            with ExitStack() as st:
                sq = [alloc(st, "sq", [128, 8, 512], BF16) for _ in range(2)]
                rstd = [alloc(st, "rstd", [128, 512], F32) for _ in range(2)]
                tmp2 = [alloc(st, "tmpn%d" % i, [128, 512], F32) for i in range(4)]
                for tb in range(nblk):
                    norm_block((sq, rstd, tmp2), tb, AB.t[:, 0, :], AB.t[:, 1, :], hT[tb])
            kb.barrier()
            for mk in MIXERS:
                if mk in "bd":
                    attention(l, g, mk, nblk)
                elif mk == "a":
                    ssd(l, g, nblk)
                elif mk == "c":
                    mlstm(l, g, nblk)
            with ExitStack() as st:
                sq = [alloc(st, "sq", [128, 8, 512], BF16) for _ in range(2)]
                rstd = [alloc(st, "rstd", [128, 512], F32) for _ in range(2)]
                tmp2 = [alloc(st, "tmpn%d" % i, [128, 512], F32) for i in range(4)]
                for tb in range(nblk):
                    norm_block((sq, rstd, tmp2), tb, AB.t[:, 2, :], AB.t[:, 3, :], hT[tb])
            kb.barrier()
            for b0 in range(0, nblk, 2):
                ffn(l, list(range(b0, min(b0 + 2, nblk))))
        final_out(nblk, dst)

    run_pass(0, D["xp"], D["yp"], 1)
    run_pass(1, D["xs"], D["ys"], 4)

    kb.barrier(include_pool_dma=True)
    top.close()
    print("instructions:", kb.ninst, flush=True)
    return nc


_CACHE = {}


def prep_inputs(inp):
    f = lambda a: np.ascontiguousarray(np.asarray(a, dtype=np.float32))
    consts = make_consts()
    rope = make_rope()
    shared = {}
    for name in ("w_ada", "b_ada", "norm1", "norm2", "w_in", "w_out", "conv_ssd_w", "conv_ssd_b", "ssd_d", "ssd_norm",
                 "diff_lq1", "diff_lk1", "diff_lq2", "diff_lk2", "conv_mlstm_w", "conv_mlstm_b", "mlstm_norm",
                 "gqa_q_norm", "gqa_k_norm", "w_ffn_in", "w_ffn_out", "norm_f"):
        shared[name] = f(inp[name])
    shared["ssd_a_log"] = f(inp["ssd_a_log"]).reshape(4, 16)
    shared["ssd_dt_bias"] = f(inp["ssd_dt_bias"]).reshape(4, 16)
    shared["mlstm_gate_b"] = f(inp["mlstm_gate_b"]).reshape(4, 16)
    shared["consts"] = consts
    shared["rope"] = rope
    xp = f(inp["x_prompt"])
    xs = f(inp["x_sample"])
    in_maps = []
    for c in range(8):
        b = c // 4
        m = dict(shared)
        m["xp"] = xp[2 * c:2 * c + 2].reshape(512, 1024)
        m["xs"] = xs[b]
        m["cvec"] = np.stack([f(inp["c_ctx"]), f(inp["c"])[b]], axis=0)
        m["cdk"] = f(inp["cache_diff_k"])[b].reshape(4, 256, 512)
        m["cdv"] = f(inp["cache_diff_v"])[b].reshape(4, 256, 512)
        m["cgk"] = f(inp["cache_gqa_k"])[b].reshape(4, 256, 128)
        m["cgv"] = f(inp["cache_gqa_v"])[b].reshape(4, 256, 128)
        m["sssm"] = f(inp["state_ssm"])[b]
        m["smc"] = f(inp["state_mlstm_c"])[b]
        m["smn"] = f(inp["state_mlstm_n"])[b]
        m["smm"] = f(inp["state_mlstm_m"])[b].reshape(4, 8)
        in_maps.append(m)
    if NL < 4:
        spec = dict(IN_SPECS)
        for m in in_maps:
            for k_ in list(m.keys()):
                if spec[k_][0] == 4 and len(spec[k_]) > 1:
                    m[k_] = np.ascontiguousarray(m[k_][:NL])
    return in_maps


def kernel(**inp):
    if "nc" not in _CACHE:
        _CACHE["nc"] = build_program()
    nc = _CACHE["nc"]
    in_maps = prep_inputs(inp)
    res = run_bass_kernel_spmd(nc, in_maps, core_ids=list(range(8)))
    return assemble(res.results)


def assemble(R):
    y_prompt = np.concatenate([R[c]["yp"].reshape(2, 256, 1024) for c in range(8)], axis=0)
    y_sample = np.stack([R[0]["ys"], R[4]["ys"]], axis=0)
    cat = lambda k: np.concatenate([R[c][k] for c in range(8)], axis=0)
    ndk = cat("ndk").reshape(16, 4, 256, 4, 2, 64)
    ndv = cat("ndv").reshape(16, 4, 256, 4, 128)
    ngk = cat("ngk").reshape(16, 4, 256, 2, 64)
    ngv = cat("ngv").reshape(16, 4, 256, 2, 64)
    nssm = cat("nssm")
    nmc = cat("nmc")
    nmn = cat("nmn")
    nmm = cat("nmm").reshape(16, 4, 2, 4)
    return (y_prompt, y_sample, ndk, ndv, ngk, ngv, nssm, nmc, nmn, nmm)
```

```python
import os
import math
from contextlib import ExitStack
import numpy as np
import concourse.bass as bass
import concourse.mybir as mybir
from concourse.bass_utils import run_bass_kernel_spmd

F32 = mybir.dt.float32
BF16 = mybir.dt.bfloat16
ALU = mybir.AluOpType
AF = mybir.ActivationFunctionType
AX = mybir.AxisListType

D_MODEL = 1024
DEPTH = 4
IN_COLS = 5664
D_FF = 2816
EPS = 1e-6
NEG = -30000.0

NL = int(os.environ.get("MK_NL", "4"))
MIXERS = os.environ.get("MK_MIX", "abcd")
SSD_PH = int(os.environ.get("MK_SSD_PH", "9"))
SSD_SUB = int(os.environ.get("MK_SSD_SUB", "9"))


class Buf:
    __slots__ = ("t", "w", "r")

    def __init__(self, t):
        self.t = t
        self.w = None
        self.r = []

    def __getitem__(self, idx):
        return self.t[idx]


class _Rec:
    def __init__(self):
        self.call = None

    def __getattr__(self, name):
        def f(*args, **kw):
            self.call = (name, args, kw)
            return self
        return f


class KB:
    NDMA_SEM = 8

    def __init__(self, nc):
        self.nc = nc
        self.engs = {"pe": nc.tensor, "act": nc.scalar, "dve": nc.vector, "pool": nc.gpsimd, "sp": nc.sync}
        self.sems = {}
        self.cnt = {}
        for k in ("pe", "act", "dve", "pool"):
            self.sems[k] = nc.alloc_semaphore(name="s_" + k)
            self.cnt[k] = 0
        self.dq = {}
        for q in ("sp", "pool", "act"):
            lst = []
            for i in range(self.NDMA_SEM):
                key = "d_%s%d" % (q, i)
                self.sems[key] = nc.alloc_semaphore(name=key)
                self.cnt[key] = 0
                lst.append(key)
            self.dq[q] = [lst, 0]
        self.seen = {e: {} for e in self.engs}
        self.ninst = 0
        self.defer = bool(int(os.environ.get('MK_SCHED', '1')))
        self.pending = []

    def _wait(self, eng, k, v):
        seen = self.seen[eng]
        if seen.get(k, 0) >= v:
            return
        self.engs[eng].wait_ge(self.sems[k], v)
        self.ninst += 1
        seen[k] = v

    def _need(self, eng, reads, writes):
        need = {}

        def add(dep):
            if dep is None:
                return
            k, v = dep
            if need.get(k, 0) < v:
                need[k] = v
        for b in reads:
            add(b.w)
        for b in writes:
            add(b.w)
            for d in b.r:
                add(d)
        for k, v in need.items():
            if k == eng and eng == "pe":
                continue
            self._wait(eng, k, v)

    def _record(self, dep, reads, writes):
        for b in reads:
            b.r.append(dep)
            if len(b.r) > 64:
                mx = {}
                for k, v in b.r:
                    if mx.get(k, 0) < v:
                        mx[k] = v
                b.r = list(mx.items())
        for b in writes:
            b.w = dep
            b.r = []

    def op(self, eng, fn, reads=(), writes=(), inc=True):
        if self.defer:
            rec = _Rec()
            fn(rec)
            self.pending.append(("op", eng, rec.call, tuple(reads), tuple(writes), inc))
            return None
        return self._op_now(eng, fn, reads, writes, inc)

    def _op_now(self, eng, fn, reads=(), writes=(), inc=True):
        self._need(eng, reads, writes)
        ins = fn(self.engs[eng])
        self.ninst += 1
        val = self.cnt[eng] + 1
        if inc:
            ins.then_inc(self.sems[eng], 1)
            self.cnt[eng] = val
        self._record((eng, val), reads, writes)
        return ins

    def dma(self, q, out, in_, reads=(), writes=(), **kw):
        if self.defer:
            self.pending.append(("dma", q, (out, in_, kw), tuple(reads), tuple(writes), True))
            return None
        return self._dma_now(q, out, in_, reads, writes, **kw)

    def _dma_now(self, q, out, in_, reads=(), writes=(), **kw):
        self._need(q, reads, writes)
        lst, i = self.dq[q]
        key = lst[i % len(lst)]
        self.dq[q][1] = i + 1
        if self.cnt[key]:
            self._wait(q, key, self.cnt[key])
        ins = self.engs[q].dma_start(out=out, in_=in_, **kw)
        self.ninst += 1
        self.cnt[key] += 16
        ins.then_inc(self.sems[key], 16)
        dep = (key, self.cnt[key])
        self._record(dep, reads, writes)
        return dep

    @staticmethod
    def _cost(kind, eng, call):
        def fsz(ap):
            n = 1
            for d in ap.shape[1:]:
                n *= d
            return n
        if kind == "dma":
            out = call[0]
            nb = fsz(out) * out.shape[0] * (2 if out.dtype == BF16 else 4)
            return 2000.0 + nb / 80.0
        name, args, kw = call
        if name == "matmul":
            n = fsz(kw["rhs"])
            passes = 4 if kw["lhsT"].dtype == F32 else 1
            return 70.0 + n * passes * 0.45
        out = kw.get("out", None)
        if out is None:
            out = kw.get("ap", args[0] if args else None)
        n = fsz(out) if out is not None else 64
        if eng == "act":
            return 230.0 + n * 0.75
        if name == "reciprocal":
            return 70.0 + n * 6.5
        if name == "memset":
            return 70.0 + n * 0.5
        return 70.0 + n * 1.1

    def flush(self):
        pend = self.pending
        self.pending = []
        if not pend:
            return
        import heapq
        units = []
        cur = None
        for it in pend:
            kind, eng, call, rd, wr, inc = it
            if kind == "op" and eng == "pe":
                if cur is None:
                    cur = [eng, [], 0.0, set(), set()]
                cur[1].append(it)
                cur[2] += self._cost(kind, eng, call)
                cur[3].update(rd)
                cur[4].update(wr)
                if inc:
                    units.append(cur)
                    cur = None
            else:
                assert cur is None, "non-PE op inside an open PE group"
                units.append([eng, [it], self._cost(kind, eng, call), set(rd), set(wr)])
        assert cur is None, "PE group without final inc"
        n = len(units)
        lastw = {}
        readers = {}
        deps = [None] * n
        succ = [[] for _ in range(n)]
        for i, u in enumerate(units):
            d = set()
            for b in u[3]:
                if b in lastw:
                    d.add(lastw[b])
            for b in u[4]:
                if b in lastw:
                    d.add(lastw[b])
                for r in readers.get(b, ()):
                    d.add(r)
            d.discard(i)
            deps[i] = d
            for j in d:
                succ[j].append(i)
            for b in u[3]:
                readers.setdefault(b, []).append(i)
            for b in u[4]:
                lastw[b] = i
                readers[b] = []
        ndep = [len(d) for d in deps]
        ready_t = [0.0] * n
        fin = [0.0] * n
        free = {}
        heaps = {}
        for i in range(n):
            if ndep[i] == 0:
                heapq.heappush(heaps.setdefault(units[i][0], []), (0.0, i))
        order = []
        done = 0
        while done < n:
            best = None
            for e, h in heaps.items():
                if not h:
                    continue
                rt, i = h[0]
                st_ = max(rt, free.get(e, 0.0))
                if best is None or (st_, i) < (best[0], best[1]):
                    best = (st_, i, e)
            st_, i, e = best
            heapq.heappop(heaps[e])
            u = units[i]
            if e in ("sp", "pool") or (e == "act" and u[1][0][0] == "dma"):
                free[e] = st_ + 60.0
                fin[i] = st_ + u[2]
            else:
                fin[i] = st_ + u[2]
                free[e] = fin[i]
            order.append((st_, i))
            done += 1
            for j in succ[i]:
                ndep[j] -= 1
                if fin[i] > ready_t[j]:
                    ready_t[j] = fin[i]
                if ndep[j] == 0:
                    heapq.heappush(heaps.setdefault(units[j][0], []), (ready_t[j], j))
        order.sort()
        for _, i in order:
            for kind, eng, call, rd, wr, inc in units[i][1]:
                if kind == "op":
                    name, args, kw = call
                    self._op_now(eng, lambda en: getattr(en, name)(*args, **kw), rd, wr, inc)
                else:
                    out, in_, kw = call
                    self._dma_now(eng, out, in_, rd, wr, **kw)

    def barrier(self, include_pool_dma=False):
        self.flush()
        keys = ["pe", "act", "dve", "pool"] + self.dq["sp"][0] + self.dq["act"][0]
        if include_pool_dma:
            keys += self.dq["pool"][0]
        for e in ("pe", "act", "dve", "pool", "sp"):
            for k in keys:
                if (k == e and e == "pe") or self.cnt[k] == 0:
                    continue
                self._wait(e, k, self.cnt[k])


def bcast(ap, shape, axis):
    return ap.unsqueeze(axis).broadcast_to(list(shape))


IN_SPECS = [
    ("xp", [512, 1024]), ("xs", [2048, 1024]), ("cvec", [2, 1024]),
    ("cdk", [4, 256, 512]), ("cdv", [4, 256, 512]), ("cgk", [4, 256, 128]), ("cgv", [4, 256, 128]),
    ("sssm", [4, 2, 8, 64, 64]), ("smc", [4, 2, 4, 128, 128]), ("smn", [4, 2, 4, 128]), ("smm", [4, 8]),
    ("w_ada", [4, 1024, 6144]), ("b_ada", [4, 6144]), ("norm1", [4, 1024]), ("norm2", [4, 1024]),
    ("w_in", [4, 1024, IN_COLS]), ("w_out", [4, 2048, 1024]),
    ("conv_ssd_w", [4, 5, 768]), ("conv_ssd_b", [4, 768]), ("ssd_a_log", [4, 16]), ("ssd_dt_bias", [4, 16]),
    ("ssd_d", [4, 8]), ("ssd_norm", [4, 512]),
    ("diff_lq1", [4, 64]), ("diff_lk1", [4, 64]), ("diff_lq2", [4, 64]), ("diff_lk2", [4, 64]),
    ("conv_mlstm_w", [4, 5, 1024]), ("conv_mlstm_b", [4, 1024]), ("mlstm_gate_b", [4, 16]), ("mlstm_norm", [4, 512]),
    ("gqa_q_norm", [4, 64]), ("gqa_k_norm", [4, 64]),
    ("w_ffn_in", [4, 1024, 2 * D_FF]), ("w_ffn_out", [4, D_FF, 1024]), ("norm_f", [1024]),
    ("consts", [128, 1152]), ("rope", [128, 2, 2048]),
]
OUT_SPECS = [
    ("yp", [512, 1024]), ("ys", [2048, 1024]),
    ("ndk", [2, 4, 256, 512]), ("ndv", [2, 4, 256, 512]), ("ngk", [2, 4, 256, 128]), ("ngv", [2, 4, 256, 128]),
    ("nssm", [2, 4, 2, 8, 64, 64]), ("nmc", [2, 4, 2, 4, 128, 128]), ("nmn", [2, 4, 2, 4, 128]), ("nmm", [2, 4, 8]),
]


def make_consts():
    c = np.zeros((128, 1152), np.float32)
    k = np.arange(128)
    c[:, 0:128] = np.eye(128)
    c[:, 128:256] = 1.0
    c[:, 256:384] = (k[:, None] <= k[None, :])
    c[:, 384:512] = (k[:, None] >= k[None, :])
    c[:, 512:640] = np.where(k[:, None] <= k[None, :], 0.0, NEG)
    c[:, 640:768] = np.where(k[:, None] >= k[None, :], 0.0, NEG)
    c[:, 768:896] = (k[:, None] // 64 == k[None, :] // 64)
    rm = np.zeros((128, 128), np.float32)
    for dp in range(128):
        half = (dp % 32) // 16
        if half == 0:
            rm[dp + 16, dp] = -1.0
        else:
            rm[dp - 16, dp] = 1.0
    c[:, 896:1024] = rm
    c[64, 1024:1088] = 1.0
    c[0, 1088:1152] = 1.0
    return c


def make_rope():
    t = np.arange(2048)
    r = (t // 64).astype(np.float32)
    cc = (t % 64).astype(np.float32)
    nf = 16
    freqs = (10000.0 ** (-np.arange(nf, dtype=np.float32) / nf)).astype(np.float32)
    ang = np.stack([r[:, None] * freqs, cc[:, None] * freqs], axis=1).astype(np.float32)
    out = np.zeros((128, 2, 2048), np.float32)
    for p in range(128):
        d = p % 64
        a = d // 32
        f = d % 16
        out[p, 0] = np.cos(ang[:, a, f])
        out[p, 1] = np.sin(ang[:, a, f])
    return out


def build_program():
    nc = bass.Bass("TRN2", target_bir_lowering=False)
    kb = KB(nc)
    D = {}
    for name, shape in IN_SPECS:
        if shape[0] == 4 and len(shape) > 1:
            shape = [NL] + list(shape[1:])
        D[name] = nc.dram_tensor(name, shape, F32, kind="ExternalInput").ap()
    for name, shape in OUT_SPECS:
        D[name] = nc.dram_tensor(name, shape, F32, kind="ExternalOutput").ap()
    mod_d = nc.dram_tensor("mod_scr", [4, 2, 6144], F32, kind="Internal").ap()

    top = ExitStack()

    class scope:
        def __enter__(self_):
            self_.st = ExitStack()
            return self_.st

        def __exit__(self_, *a):
            if a[0] is None:
                kb.barrier()
            self_.st.close()
            return False

    uid = [0]

    def alloc(stack, name, shape, dt, psum=False):
        uid[0] += 1
        name = "%s_%d" % (name, uid[0])
        cm = nc.psum_tensor(name, shape, dt) if psum else nc.sbuf_tensor(name, shape, dt)
        return Buf(stack.enter_context(cm))

    xT = [alloc(top, "xT%d" % i, [128, 8, 512], F32) for i in range(4)]
    hT = [alloc(top, "hT%d" % i, [128, 8, 512], BF16) for i in range(4)]
    NW = 2
    wbufs = [alloc(top, "wb%d" % i, [128, 4096], BF16) for i in range(NW)]
    wstate = [0]
    psb = [alloc(top, "ps%d" % i, [128, 512], F32, psum=True) for i in range(8)]
    pstate = [0]
    cf = alloc(top, "cf", [128, 640], F32)
    cb = alloc(top, "cb", [128, 1024], BF16)
    modc = alloc(top, "modc", [128, 6, 8], F32)
    nrm = alloc(top, "nrm", [128, 2, 8], F32)
    AB = alloc(top, "AB", [128, 4, 8], F32)
    nfc = alloc(top, "nfc", [128, 8], F32)
    lnsb = alloc(top, "lnsb", [128, 1], F32)
    lns_col = lnsb.t

    wo_buf = alloc(top, "wo_buf", [128, 4, 1024], BF16)
    accstate = [0]

    def ps():
        b = psb[pstate[0] % 4]
        pstate[0] += 1
        return b

    def ps_acc():
        b = psb[4 + accstate[0] % 4]
        accstate[0] += 1
        return b

    def wload(pieces, kch):
        b = wbufs[wstate[0] % NW]
        wstate[0] += 1
        ntot = sum(n for _, n in pieces)
        assert kch * ntot <= 4096, (kch, ntot)
        view = b.t[:, 0:kch * ntot].rearrange("p (k n) -> p k n", k=kch)
        o = 0
        for ap, n in pieces:
            kb.dma("pool", view[:, :, o:o + n], ap.rearrange("(k p) n -> p k n", p=128), writes=[b])
            o += n
        return b, view

    ident_f = cf.t[:, 0:128]
    ones_f = cf.t[:, 128:256]
    ident_b = cb.t[:, 0:128]
    selm_f = cf.t[:, 512:640]
    ones_b = cb.t[:, 128:256]

    kb.dma("sp", cf[:, 0:512], D["consts"][:, 0:512], writes=[cf])
    kb.dma("sp", cf[:, 512:640], D["consts"][:, 1024:1152], writes=[cf])
    kb.dma("pool", cb[:], D["consts"][:, 0:1024], writes=[cb])
    kb.op("dve", lambda e: e.memset(lnsb[:], math.log(128 ** -0.5)), writes=[lnsb])
    kb.dma("sp", nfc[:], D["norm_f"].rearrange("(c p) -> p c", p=128), writes=[nfc], allow_slow_non_contiguous=True)

    modall = alloc(top, "modall", [128, NL, 48, 2], F32)
    ball = alloc(top, "ball", [128, NL, 48], F32)
    nrmall = alloc(top, "nrmall", [128, NL, 2, 8], F32)
    for l in range(NL):
        kb.dma("sp", ball[:, l, :], D["b_ada"][l].rearrange("(j p) -> p j", p=128), writes=[ball], allow_slow_non_contiguous=True)
        kb.dma("sp", nrmall[:, l, 0, :], D["norm1"][l].rearrange("(c p) -> p c", p=128), writes=[nrmall], allow_slow_non_contiguous=True)
        kb.dma("sp", nrmall[:, l, 1, :], D["norm2"][l].rearrange("(c p) -> p c", p=128), writes=[nrmall], allow_slow_non_contiguous=True)
    with scope() as st:
        cT = alloc(st, "cT", [128, 2, 8], F32)
        cTb = alloc(st, "cTb", [128, 2, 8], BF16)
        sig = alloc(st, "csig", [128, 2, 8], F32)
        for g in range(2):
            kb.dma("sp", cT[:, g, :], D["cvec"][g].rearrange("(c p) -> p c", p=128), writes=[cT], allow_slow_non_contiguous=True)
        kb.op("act", lambda e: e.activation(out=sig[:], in_=cT[:], func=AF.Sigmoid), reads=[cT], writes=[sig])
        kb.op("dve", lambda e: e.tensor_tensor(out=cTb[:], in0=cT[:], in1=sig[:], op=ALU.mult), reads=[cT, sig], writes=[cTb])
        for l in range(NL):
            for blk in range(12):
                c0 = blk * 512
                wb, wv = wload([(D["w_ada"][l][:, c0:c0 + 512], 512)], 8)
                p = ps()
                for cc in range(4):
                    for k in range(8):
                        kb.op("pe", lambda e: e.matmul(p[:, 2 * cc:2 * cc + 2], lhsT=wv[:, k, cc * 128:(cc + 1) * 128], rhs=cTb[:, :, k], start=(k == 0), stop=(k == 7)),
                              reads=[cTb, wb], writes=[p], inc=(k == 7 and cc == 3))
                kb.op("dve", lambda e: e.tensor_tensor(out=modall[:, l, blk * 4:(blk + 1) * 4, :], in0=p[:, 0:8].rearrange("p (j g) -> p j g", g=2),
                                                       in1=bcast(ball[:, l, blk * 4:(blk + 1) * 4], [128, 4, 2], 2), op=ALU.add), reads=[p, ball], writes=[modall])
        kb.barrier()

    def load_x(src, nblk):
        with scope() as st:
            xin = [alloc(st, "xin%d" % i, [128, 1024], F32) for i in range(2)]
            for tb in range(nblk):
                tiles = []
                for tl in range(4):
                    pass
                for tl in range(4):
                    xi = xin[(tb * 4 + tl) % 2]
                    t0 = (tb * 4 + tl) * 128
                    kb.dma("sp", xi[:], src[t0:t0 + 128, :], writes=[xi])
                    for half in range(2):
                        p = ps()
                        for cc in range(4):
                            c = half * 4 + cc
                            kb.op("pe", lambda e: e.matmul(p[:, cc * 128:(cc + 1) * 128], lhsT=xi[:, c * 128:(c + 1) * 128], rhs=ident_f,
                                                           start=True, stop=True), reads=[xi, cf], writes=[p], inc=(cc == 3))
                        kb.op("act", lambda e: e.activation(
                            out=xT[tb][:, half * 4:half * 4 + 4, tl * 128:(tl + 1) * 128],
                            in_=p[:, :].rearrange("p (c t) -> p c t", c=4), func=AF.Copy), reads=[p], writes=[xT[tb]])

    def norm_block(st_tmp, tb, Acol, Bcol, dst, dst_dt_is_bf16=True):
        sq, rstd, tmp2 = st_tmp
        if isinstance(sq, list):
            sq, rstd = sq[tb % 2], rstd[tb % 2]
        kb.op("act", lambda e: e.activation(out=sq[:], in_=xT[tb][:], func=AF.Square), reads=[xT[tb]], writes=[sq])
        p = ps()
        for c in range(8):
            kb.op("pe", lambda e: e.matmul(p[:, :], lhsT=ones_b, rhs=sq[:, c, :], start=(c == 0), stop=(c == 7)),
                  reads=[sq, cb], writes=[p], inc=(c == 7))
        kb.op("dve", lambda e: e.tensor_scalar(out=rstd[:], in0=p[:, :], scalar1=1.0 / D_MODEL, scalar2=EPS, op0=ALU.mult, op1=ALU.add),
              reads=[p], writes=[rstd])
        kb.op("act", lambda e: e.activation(out=rstd[:], in_=rstd[:], func=AF.Sqrt), reads=[rstd], writes=[rstd])
        kb.op("dve", lambda e: e.reciprocal(out=rstd[:], in_=rstd[:]), reads=[rstd], writes=[rstd])
        for c in range(8):
            t2 = tmp2[c % len(tmp2)]
            kb.op("dve", lambda e: e.tensor_tensor(out=t2[:], in0=xT[tb][:, c, :], in1=rstd[:], op=ALU.mult),
                  reads=[xT[tb], rstd], writes=[t2])
            if Bcol is not None:
                kb.op("act", lambda e: e.activation(out=dst[:, c, :], in_=t2[:], func=AF.Identity, bias=Bcol[:, c:c + 1], scale=Acol[:, c:c + 1]),
                      reads=[t2, AB], writes=[dst])
            else:
                kb.op("act", lambda e: e.activation(out=dst[:, c, :], in_=t2[:], func=AF.Identity, scale=Acol[:, c:c + 1]),
                      reads=[t2, nfc], writes=[dst])

    def load_mod(l, g):
        kb.op("dve", lambda e: e.tensor_copy(out=modc[:], in_=modall[:, l, :, g].rearrange("p (v c) -> p v c", v=6)), reads=[modall], writes=[modc])
        kb.op("dve", lambda e: e.tensor_copy(out=nrm[:], in_=nrmall[:, l, :, :]), reads=[nrmall], writes=[nrm])
        for j, (vs, vh) in enumerate(((1, 0), (4, 3))):
            kb.op("dve", lambda e: e.scalar_tensor_tensor(out=AB[:, 2 * j, :], in0=modc[:, vs, :], scalar=1.0, in1=nrm[:, j, :],
                                                          op0=ALU.add, op1=ALU.mult), reads=[modc, nrm], writes=[AB])
            kb.op("dve", lambda e: e.tensor_copy(out=AB[:, 2 * j + 1, :], in_=modc[:, vh, :]), reads=[modc], writes=[AB])

    def ffn(l, blocks):
        nb = len(blocks)
        with scope() as st:
            actT = alloc(st, "actT", [128, 22, nb * 512], BF16)
            sg = [alloc(st, "sg%d" % i, [128, 512], F32) for i in range(2)]
            it = 0
            for jj in range(11):
                c0 = jj * 256
                wb, wv = wload([(D["w_ffn_in"][l][:, c0:c0 + 256], 256), (D["w_ffn_in"][l][:, D_FF + c0:D_FF + c0 + 256], 256)], 8)
                for j2 in range(2):
                    j = jj * 2 + j2
                    for bi, tb in enumerate(blocks):
                        pg = ps()
                        pu = ps()
                        for k in range(8):
                            kb.op("pe", lambda e: e.matmul(pg[:, :], lhsT=wv[:, k, j2 * 128:(j2 + 1) * 128], rhs=hT[tb][:, k, :],
                                                           start=(k == 0), stop=(k == 7)), reads=[wb, hT[tb]], writes=[pg], inc=(k == 7))
                        for k in range(8):
                            kb.op("pe", lambda e: e.matmul(pu[:, :], lhsT=wv[:, k, 256 + j2 * 128:256 + (j2 + 1) * 128], rhs=hT[tb][:, k, :],
                                                           start=(k == 0), stop=(k == 7)), reads=[wb, hT[tb]], writes=[pu], inc=(k == 7))
                        s = sg[it % 2]
                        it += 1
                        kb.op("act", lambda e: e.activation(out=s[:], in_=pg[:, :], func=AF.Silu), reads=[pg], writes=[s])
                        kb.op("dve", lambda e: e.tensor_tensor(out=actT[:, j, bi * 512:(bi + 1) * 512], in0=s[:], in1=pu[:, :], op=ALU.mult),
                              reads=[s, pu], writes=[actT])
            for c in range(8):
                wb, wv = wload([(D["w_ffn_out"][l][:, c * 128:(c + 1) * 128], 128)], 22)
                for bi, tb in enumerate(blocks):
                    p = ps()
                    for k in range(22):
                        kb.op("pe", lambda e: e.matmul(p[:, :], lhsT=wv[:, k, :], rhs=actT[:, k, bi * 512:(bi + 1) * 512],
                                                       start=(k == 0), stop=(k == 21)), reads=[wb, actT], writes=[p], inc=(k == 21))
                    kb.op("dve", lambda e: e.scalar_tensor_tensor(out=xT[tb][:, c, :], in0=p[:, :], scalar=modc[:, 5, c:c + 1], in1=xT[tb][:, c, :],
                                                                  op0=ALU.mult, op1=ALU.add), reads=[p, modc, xT[tb]], writes=[xT[tb]])
        kb.barrier()

    def final_out(nblk, dst):
        with scope() as st:
            sq = alloc(st, "sq", [128, 8, 512], BF16)
            rstd = alloc(st, "rstd", [128, 512], F32)
            tmp2 = [alloc(st, "tmpn%d" % i, [128, 512], F32) for i in range(2)]
            xn = alloc(st, "xn", [128, 8, 512], F32)
            ot = [alloc(st, "ot%d" % i, [128, 1024], F32) for i in range(2)]
            for tb in range(nblk):
                norm_block((sq, rstd, tmp2), tb, nfc, None, xn)
                for tl in range(4):
                    o = ot[tl % 2]
                    for half in range(2):
                        p = ps()
                        for cc in range(4):
                            c = half * 4 + cc
                            kb.op("pe", lambda e: e.matmul(p[:, cc * 128:(cc + 1) * 128], lhsT=xn[:, c, tl * 128:(tl + 1) * 128], rhs=ident_f,
                                                           start=True, stop=True), reads=[xn, cf], writes=[p], inc=(cc == 3))
                        kb.op("act", lambda e: e.activation(out=o[:, half * 512:(half + 1) * 512], in_=p[:, :], func=AF.Copy), reads=[p], writes=[o])
                    t0 = (tb * 4 + tl) * 128
                    kb.dma("sp", dst[t0:t0 + 128, :], o[:], reads=[o])
        kb.barrier()


    bd64_b = cb.t[:, 768:896]
    rm_b = cb.t[:, 896:1024]

    def proj_tm(wb, wv, s0, n, tb, tl, p):
        for k in range(8):
            kb.op("pe", lambda e: e.matmul(p[:, 0:n], lhsT=hT[tb][:, k, tl * 128:(tl + 1) * 128], rhs=wv[:, k, s0:s0 + n],
                                           start=(k == 0), stop=(k == 7)), reads=[wb, hT[tb]], writes=[p], inc=(k == 7))

    def load_wo(l, row0):
        kb.dma("pool", wo_buf[:], D["w_out"][l][row0:row0 + 512, :].rearrange("(k p) n -> p k n", p=128), writes=[wo_buf])

    def mixer_out(st, ytm, tb, yT):
        for c in range(4):
            p = ps()
            for tl in range(4):
                kb.op("pe", lambda e: e.matmul(p[:, tl * 128:(tl + 1) * 128], lhsT=ytm[:, tl, c * 128:(c + 1) * 128], rhs=ident_b,
                                               start=True, stop=True), reads=[ytm, cb], writes=[p], inc=(tl == 3))
            kb.op("act", lambda e: e.activation(out=yT[:, c, :], in_=p[:, :], func=AF.Copy), reads=[p], writes=[yT])
        for c in range(8):
            p = ps()
            for k in range(4):
                kb.op("pe", lambda e: e.matmul(p[:, :], lhsT=wo_buf[:, k, c * 128:(c + 1) * 128], rhs=yT[:, k, :],
                                               start=(k == 0), stop=(k == 3)), reads=[wo_buf, yT], writes=[p], inc=(k == 3))
            kb.op("dve", lambda e: e.scalar_tensor_tensor(out=xT[tb][:, c, :], in0=p[:, :], scalar=modc[:, 2, c:c + 1], in1=xT[tb][:, c, :],
                                                          op0=ALU.mult, op1=ALU.add), reads=[p, modc, xT[tb]], writes=[xT[tb]])

    def attention(l, g, kind, nblk):
        sample = (g == 1)
        L = 2048 if sample else 256
        nseq = 1 if sample else 2
        nctx = 2 if sample else 0
        lt = L // 128
        nkt = lt + nctx
        ntok = nblk * 512
        if kind == "d":
            qc0, kc0, vc0, nkc, nvh, ve, orow0, nheads = 4896, 5408, 5536, 1, 2, 64, 1536, 8
            ck, cv, ok, ov = D["cgk"], D["cgv"], D["ngk"], D["ngv"]
        else:
            qc0, kc0, vc0, nkc, nvh, ve, orow0, nheads = 1296, 1808, 2320, 4, 4, 128, 512, 4
            ck, cv, ok, ov = D["cdk"], D["cdv"], D["ndk"], D["ndv"]
        scale = 64 ** -0.5
        kw = nkc * 128
        vw = nvh * ve
        lam_init = 0.8 - 0.6 * math.exp(-0.3 * l)
        with scope() as st:
            qT = alloc(st, "qT", [128, 4, ntok], BF16)
            kT = alloc(st, "kT", [128, nkc, nseq * nkt * 128], BF16)
            vsw = ve + 1 if kind == "d" else ve
            vaug = alloc(st, "vaug", [128, nseq * nkt, nvh, vsw], BF16)
            vodd = alloc(st, "vodd", [128, nseq * nkt, nvh, 128], BF16) if kind == "d" else None
            yT = alloc(st, "yT", [128, 4, 512], BF16)
            pTs = [alloc(st, "pT%d" % i, [128, 512], BF16) for i in range(3)]
            sqb = alloc(st, "sqb", [128, 512], BF16)
            rs = alloc(st, "rs", [128, 512], F32)
            qn = alloc(st, "qn", [128, 512], BF16)
            t1 = alloc(st, "t1", [128, 512], F32)
            gcol = alloc(st, "gcol", [128, 2], F32)
            osb = alloc(st, "osb", [128, 512], F32)
            t2 = osb
            sm = alloc(st, "sm", [128, 16], F32)
            lamt = alloc(st, "lamt", [128, 4, 64], F32)
            kng = alloc(st, "kng", [128, 64], F32)
            ropeT = alloc(st, "ropeT", [128, 2, 2048], BF16) if sample else None
            load_wo(l, orow0)
            if kind == "d":
                kb.op("dve", lambda e: e.memset(vaug[:, :, :, ve:ve + 1], 1.0), writes=[vaug])
                kb.op("dve", lambda e: e.memset(vodd[:, :, :, 0:1], 1.0), writes=[vodd])
                kb.op("dve", lambda e: e.memset(vodd[:, :, :, 1:64], 0.0), writes=[vodd])
            if sample:
                kb.dma("pool", ropeT[:], D["rope"], writes=[ropeT])
            if kind == "d":
                for j, nm in enumerate(("gqa_q_norm", "gqa_k_norm")):
                    for hh in range(2):
                        kb.dma("sp", gcol[hh * 64:(hh + 1) * 64, j:j + 1], D[nm][l].rearrange("(d o) -> d o", o=1), writes=[gcol])
                kb.dma("sp", kng[:], D["gqa_k_norm"][l].partition_broadcast(128), writes=[kng])
            else:
                for j, nm in enumerate(("diff_lq1", "diff_lk1", "diff_lq2", "diff_lk2")):
                    kb.dma("sp", lamt[:, j, :], D[nm][l].partition_broadcast(128), writes=[lamt])
                kb.op("dve", lambda e: e.tensor_tensor(out=lamt[:, 0, :], in0=lamt[:, 0, :], in1=lamt[:, 1, :], op=ALU.mult), reads=[lamt], writes=[lamt])
                kb.op("dve", lambda e: e.tensor_tensor(out=lamt[:, 2, :], in0=lamt[:, 2, :], in1=lamt[:, 3, :], op=ALU.mult), reads=[lamt], writes=[lamt])
                kb.op("dve", lambda e: e.tensor_reduce(out=sm[:, 2:3], in_=lamt[:, 0, :], axis=AX.X, op=ALU.add), reads=[lamt], writes=[sm])
                kb.op("dve", lambda e: e.tensor_reduce(out=sm[:, 3:4], in_=lamt[:, 2, :], axis=AX.X, op=ALU.add), reads=[lamt], writes=[sm])
                kb.op("act", lambda e: e.activation(out=sm[:, 2:4], in_=sm[:, 2:4], func=AF.Exp), reads=[sm], writes=[sm])
                kb.op("dve", lambda e: e.tensor_tensor(out=sm[:, 0:1], in0=sm[:, 2:3], in1=sm[:, 3:4], op=ALU.subtract), reads=[sm], writes=[sm])
                kb.op("dve", lambda e: e.tensor_scalar(out=sm[:, 1:2], in0=sm[:, 0:1], scalar1=lam_init, scalar2=-1.0, op0=ALU.add, op1=ALU.mult), reads=[sm], writes=[sm])

            def qk_post(p, dst, tb, normj):
                src = p
                if kind == "d":
                    kb.op("act", lambda e: e.activation(out=sqb[:], in_=p[:, :], func=AF.Square), reads=[p], writes=[sqb])
                    pn = ps()
                    kb.op("pe", lambda e: e.matmul(pn[:, :], lhsT=bd64_b, rhs=sqb[:], start=True, stop=True), reads=[cb, sqb], writes=[pn])
                    kb.op("dve", lambda e: e.tensor_scalar(out=rs[:], in0=pn[:, :], scalar1=1.0 / 64, scalar2=EPS, op0=ALU.mult, op1=ALU.add), reads=[pn], writes=[rs])
                    kb.op("act", lambda e: e.activation(out=rs[:], in_=rs[:], func=AF.Sqrt), reads=[rs], writes=[rs])
                    kb.op("dve", lambda e: e.reciprocal(out=rs[:], in_=rs[:]), reads=[rs], writes=[rs])
                    tgt = qn if sample else None
                    o_ap = qn[:] if sample else dst
                    kb.op("dve", lambda e: e.scalar_tensor_tensor(out=o_ap, in0=p[:, :], scalar=gcol[:, normj:normj + 1], in1=rs[:], op0=ALU.mult, op1=ALU.mult),
                          reads=[p, gcol, rs], writes=[qn if sample else dst_buf[0]])
                else:
                    o_ap = qn[:] if sample else dst
                    kb.op("act", lambda e: e.activation(out=o_ap, in_=p[:, :], func=AF.Copy), reads=[p], writes=[qn if sample else dst_buf[0]])
                if sample:
                    pr = ps()
                    kb.op("pe", lambda e: e.matmul(pr[:, :], lhsT=rm_b, rhs=qn[:], start=True, stop=True), reads=[cb, qn], writes=[pr])
                    kb.op("dve", lambda e: e.tensor_tensor(out=t1[:], in0=qn[:], in1=ropeT[:, 0, tb * 512:(tb + 1) * 512], op=ALU.mult), reads=[qn, ropeT], writes=[t1])
                    kb.op("dve", lambda e: e.tensor_tensor(out=t2[:], in0=pr[:, :], in1=ropeT[:, 1, tb * 512:(tb + 1) * 512], op=ALU.mult), reads=[pr, ropeT], writes=[t2])
                    kb.op("dve", lambda e: e.tensor_tensor(out=dst, in0=t1[:], in1=t2[:], op=ALU.add), reads=[t1, t2], writes=[dst_buf[0]])

            dst_buf = [None]
            blocks = list(range(nblk))
            if kind == "d":
                pcs = []
                for j in range(4):
                    for hh in (j, 4 + j):
                        pcs.append((D["w_in"][l][:, qc0 + hh * 64:qc0 + (hh + 1) * 64], 64))
                wb, wv = wload(pcs, 8)
            else:
                wb, wv = wload([(D["w_in"][l][:, qc0:qc0 + 512], 512)], 8)
            dst_buf[0] = qT
            for j in range(4):
                for tb in blocks:
                    p = ps()
                    for k in range(8):
                        lh = wv[:, k, j * 128:(j + 1) * 128]
                        kb.op("pe", lambda e: e.matmul(p[:, :], lhsT=lh, rhs=hT[tb][:, k, :], start=(k == 0), stop=(k == 7)),
                              reads=[wb, hT[tb]], writes=[p], inc=(k == 7))
                    qk_post(p, qT[:, j, tb * 512:(tb + 1) * 512], tb, 0)
            wb, wv = wload([(D["w_in"][l][:, kc0:kc0 + kw], kw)], 8)
            dst_buf[0] = kT
            for j in range(nkc):
                for tb in blocks:
                    p = ps()
                    for k in range(8):
                        kb.op("pe", lambda e: e.matmul(p[:, :], lhsT=wv[:, k, j * 128:(j + 1) * 128], rhs=hT[tb][:, k, :], start=(k == 0), stop=(k == 7)),
                              reads=[wb, hT[tb]], writes=[p], inc=(k == 7))
                    qk_post(p, kT[:, j, nctx * 128 + tb * 512:nctx * 128 + (tb + 1) * 512], tb, 1)
            if not sample:
                for tt in range(ntok // 128):
                    sq_, r0 = divmod(tt, lt)
                    p = ps()
                    proj_tm(wb, wv, 0, kw, tt // 4, tt % 4, p)
                    kb.op("act", lambda e: e.activation(out=osb[:, 0:kw], in_=p[:, 0:kw], func=AF.Copy), reads=[p], writes=[osb])
                    if kind == "d":
                        kb.op("dve", lambda e: e.tensor_tensor(out=t1[:, 0:128], in0=osb[:, 0:128], in1=osb[:, 0:128], op=ALU.mult), reads=[osb], writes=[t1])
                        kb.op("dve", lambda e: e.tensor_reduce(out=sm[:, 8:10], in_=t1[:, 0:128].rearrange("p (h d) -> p h d", d=64), axis=AX.X, op=ALU.add), reads=[t1], writes=[sm])
                        kb.op("dve", lambda e: e.tensor_scalar(out=sm[:, 8:10], in0=sm[:, 8:10], scalar1=1.0 / 64, scalar2=EPS, op0=ALU.mult, op1=ALU.add), reads=[sm], writes=[sm])
                        kb.op("act", lambda e: e.activation(out=sm[:, 8:10], in_=sm[:, 8:10], func=AF.Sqrt), reads=[sm], writes=[sm])
                        kb.op("dve", lambda e: e.reciprocal(out=sm[:, 8:10], in_=sm[:, 8:10]), reads=[sm], writes=[sm])
                        kb.op("dve", lambda e: e.tensor_tensor(out=t1[:, 0:128].rearrange("p (h d) -> p h d", d=64), in0=osb[:, 0:128].rearrange("p (h d) -> p h d", d=64),
                                                               in1=bcast(sm[:, 8:10], [128, 2, 64], 2), op=ALU.mult), reads=[osb, sm], writes=[t1])
                        kb.op("dve", lambda e: e.tensor_tensor(out=t2[:, 0:128].rearrange("p (h d) -> p h d", d=64), in0=t1[:, 0:128].rearrange("p (h d) -> p h d", d=64),
                                                               in1=bcast(kng[:], [128, 2, 64], 1), op=ALU.mult), reads=[t1, kng], writes=[t2])
                        kb.dma("sp", ok[sq_, l, r0 * 128:(r0 + 1) * 128, :], t2[:, 0:128], reads=[t2])
                    else:
                        kb.dma("sp", ok[sq_, l, r0 * 128:(r0 + 1) * 128, :], osb[:, 0:kw], reads=[osb])
            wb, wv = wload([(D["w_in"][l][:, vc0:vc0 + vw], vw)], 8)
            for tt in range(ntok // 128):
                sq_, r0 = divmod(tt, lt)
                ktile = sq_ * nkt + nctx + r0
                p = ps()
                proj_tm(wb, wv, 0, vw, tt // 4, tt % 4, p)
                kb.op("act", lambda e: e.activation(out=vaug[:, ktile, :, 0:ve], in_=p[:, 0:vw].rearrange("p (h e) -> p h e", e=ve), func=AF.Copy), reads=[p], writes=[vaug])
                if kind == "d":
                    kb.op("act", lambda e: e.activation(out=vodd[:, ktile, :, 64:128], in_=p[:, 0:vw].rearrange("p (h e) -> p h e", e=ve), func=AF.Copy), reads=[p], writes=[vodd])
                if not sample:
                    kb.op("dve", lambda e: e.tensor_copy(out=osb[:, 0:vw], in_=p[:, 0:vw]), reads=[p], writes=[osb])
                    kb.dma("sp", ov[sq_, l, r0 * 128:(r0 + 1) * 128, :], osb[:, 0:vw], reads=[osb])
            if sample:
                with scope() as st2:
                    ctxk = alloc(st2, "ctxk", [128, kw], F32)
                    for t in range(2):
                        kb.dma("sp", ctxk[:], ck[l][t * 128:(t + 1) * 128, :], writes=[ctxk])
                        for j in range(nkc):
                            p = ps()
                            kb.op("pe", lambda e: e.matmul(p[:, 0:128], lhsT=ctxk[:, j * 128:(j + 1) * 128], rhs=ident_f, start=True, stop=True),
                                  reads=[ctxk, cf], writes=[p])
                            kb.op("act", lambda e: e.activation(out=kT[:, j, t * 128:(t + 1) * 128], in_=p[:, 0:128], func=AF.Copy), reads=[p], writes=[kT])
                    for t in range(2):
                        kb.dma("pool", vaug[:, t, :, 0:ve], cv[l][t * 128:(t + 1) * 128, :].rearrange("p (h e) -> p h e", e=ve), writes=[vaug])
                        if kind == "d":
                            kb.dma("pool", vodd[:, t, :, 64:128], cv[l][t * 128:(t + 1) * 128, :].rearrange("p (h e) -> p h e", e=ve), writes=[vodd])
                    kb.barrier(include_pool_dma=True)
            qblk = min(L, 512)
            pti = 0
            for s_ in range(nseq):
                for qb in range(L // qblk):
                    q0 = s_ * L + qb * qblk
                    qs = slice(q0, q0 + qblk)
                    yc = slice(q0 % 512, q0 % 512 + qblk)
                    its = []
                    for h in range(nheads):
                        for kt in range(nkt):
                            if kind == "d":
                                its.append((h, kt, 0))
                            else:
                                its.append((h, kt, 0))
                                its.append((h, kt, 1))
                    hstate = {}
                    pts = {}
                    DEPTH = 2

                    def stageA(i):
                        h, kt, r = its[i]
                        kc = (s_ * nkt + kt) * 128
                        if kind == "d":
                            rows, qch = (h // 4) * 64, h % 4
                            lhs = kT[rows:rows + 64, 0, kc:kc + 128]
                            rh = qT[rows:rows + 64, qch, qs]
                        else:
                            lhs = kT[r * 64:(r + 1) * 64, h, kc:kc + 128]
                            rh = qT[r * 64:(r + 1) * 64, h, qs]
                        pS = ps()
                        kb.op("pe", lambda e: e.matmul(pS[:, 0:qblk], lhsT=lhs, rhs=rh, start=True, stop=True), reads=[kT, qT], writes=[pS])
                        pT = pTs[i % 3]
                        kb.op("act", lambda e: e.activation(out=pT[:, 0:qblk], in_=pS[:, 0:qblk], func=AF.Exp, scale=scale), reads=[pS], writes=[pT])
                        pts[i] = pT

                    def stageC(i):
                        h, kt, r = its[i]
                        pT = pts.pop(i)
                        last = (kt == nkt - 1)
                        if kind == "d":
                            vh, odd = h // 4, h % 2
                            if kt == 0:
                                hstate[h] = ps_acc()
                            accO = hstate[h]
                            mo = 128 if odd else ve + 1
                            lh = vodd[:, s_ * nkt + kt, vh, :] if odd else vaug[:, s_ * nkt + kt, vh, :]
                            kb.op("pe", lambda e: e.matmul(accO[0:mo, 0:qblk], lhsT=lh, rhs=pT[:, 0:qblk], start=(kt == 0), stop=last),
                                  reads=[pT, vodd if odd else vaug], writes=[accO], inc=True)
                            if last:
                                drow = 0 if odd else 64
                                orow = 64 if odd else 0
                                kb.op("act", lambda e: e.activation(out=rs[drow:drow + 1, 0:qblk], in_=accO[drow:drow + 1, 0:qblk], func=AF.Copy), reads=[accO], writes=[rs])
                                kb.op("dve", lambda e: e.reciprocal(out=rs[drow:drow + 1, 0:qblk], in_=rs[drow:drow + 1, 0:qblk]), reads=[rs], writes=[rs])
                                pB = ps()
                                kb.op("pe", lambda e: e.matmul(pB[:, 0:qblk], lhsT=selm_f[drow:drow + 1, :], rhs=rs[drow:drow + 1, 0:qblk], start=True, stop=True),
                                      reads=[cf, rs], writes=[pB])
                                kb.op("act", lambda e: e.activation(out=t1[orow:orow + 64, 0:qblk], in_=pB[orow:orow + 64, 0:qblk], func=AF.Copy), reads=[pB], writes=[t1])
                                kb.op("dve", lambda e: e.tensor_tensor(out=yT[orow:orow + 64, h // 2, yc], in0=accO[orow:orow + 64, 0:qblk], in1=t1[orow:orow + 64, 0:qblk], op=ALU.mult),
                                      reads=[accO, t1], writes=[yT])
                        else:
                            if kt == 0 and r == 0:
                                hstate[h] = ([ps_acc(), ps_acc()], [ps_acc(), ps_acc()])
                            accO, accD = hstate[h]
                            kb.op("pe", lambda e: e.matmul(accO[r][:, 0:qblk], lhsT=vaug[:, s_ * nkt + kt, h, :], rhs=pT[:, 0:qblk], start=(kt == 0), stop=last),
                                  reads=[pT, vaug], writes=[accO[r]], inc=False)
                            kb.op("pe", lambda e: e.matmul(accD[r][:, 0:qblk], lhsT=ones_b, rhs=pT[:, 0:qblk], start=(kt == 0), stop=last),
                                  reads=[pT, cb], writes=[accD[r]], inc=True)
                            if last and r == 1:
                                A, Bt, O = rs, t1, osb
                                kb.op("dve", lambda e: e.reciprocal(out=A[:, 0:qblk], in_=accD[0][:, 0:qblk]), reads=[accD[0]], writes=[A])
                                kb.op("dve", lambda e: e.reciprocal(out=Bt[:, 0:qblk], in_=accD[1][:, 0:qblk]), reads=[accD[1]], writes=[Bt])
                                kb.op("dve", lambda e: e.tensor_tensor(out=O[:, 0:qblk], in0=accO[0][:, 0:qblk], in1=A[:, 0:qblk], op=ALU.mult), reads=[accO[0], A], writes=[O])
                                kb.op("dve", lambda e: e.tensor_tensor(out=Bt[:, 0:qblk], in0=accO[1][:, 0:qblk], in1=Bt[:, 0:qblk], op=ALU.mult), reads=[accO[1], Bt], writes=[Bt])
                                kb.op("dve", lambda e: e.scalar_tensor_tensor(out=O[:, 0:qblk], in0=Bt[:, 0:qblk], scalar=sm[:, 1:2], in1=O[:, 0:qblk], op0=ALU.mult, op1=ALU.add),
                                      reads=[Bt, sm, O], writes=[O])
                                kb.op("act", lambda e: e.activation(out=sqb[:, 0:qblk], in_=O[:, 0:qblk], func=AF.Square), reads=[O], writes=[sqb])
                                pn = ps()
                                kb.op("pe", lambda e: e.matmul(pn[:, 0:qblk], lhsT=ones_b, rhs=sqb[:, 0:qblk], start=True, stop=True), reads=[cb, sqb], writes=[pn])
                                kb.op("dve", lambda e: e.tensor_scalar(out=A[:, 0:qblk], in0=pn[:, 0:qblk], scalar1=1.0 / 128, scalar2=EPS, op0=ALU.mult, op1=ALU.add), reads=[pn], writes=[A])
                                kb.op("act", lambda e: e.activation(out=A[:, 0:qblk], in_=A[:, 0:qblk], func=AF.Sqrt), reads=[A], writes=[A])
                                kb.op("dve", lambda e: e.reciprocal(out=A[:, 0:qblk], in_=A[:, 0:qblk]), reads=[A], writes=[A])
                                kb.op("dve", lambda e: e.scalar_tensor_tensor(out=yT[:, h, yc], in0=O[:, 0:qblk], scalar=1.0 - lam_init, in1=A[:, 0:qblk], op0=ALU.mult, op1=ALU.mult),
                                      reads=[O, A], writes=[yT])

                    n_it = len(its)
                    for i in range(n_it + DEPTH):
                        if i < n_it:
                            stageA(i)
                        if i >= DEPTH:
                            stageC(i - DEPTH)
                    if (q0 + qblk) % 512 == 0:
                        tb = (q0 + qblk) // 512 - 1
                        for c in range(8):
                            p = ps()
                            for k in range(4):
                                kb.op("pe", lambda e: e.matmul(p[:, :], lhsT=wo_buf[:, k, c * 128:(c + 1) * 128], rhs=yT[:, k, :],
                                                               start=(k == 0), stop=(k == 3)), reads=[wo_buf, yT], writes=[p], inc=(k == 3))
                            kb.op("dve", lambda e: e.scalar_tensor_tensor(out=xT[tb][:, c, :], in0=p[:, :], scalar=modc[:, 2, c:c + 1], in1=xT[tb][:, c, :],
                                                                          op0=ALU.mult, op1=ALU.add), reads=[p, modc, xT[tb]], writes=[xT[tb]])
        kb.barrier()

    triF_f = cf.t[:, 256:384]
    triB_f = cf.t[:, 384:512]
    maskF_b = cb.t[:, 512:640]
    maskB_b = cb.t[:, 640:768]

    def proj_fm(l, c0, ncol, blocks, fn):
        for t0 in range(0, ncol, 512):
            n = min(512, ncol - t0)
            wb, wv = wload([(D["w_in"][l][:, c0 + t0:c0 + t0 + n], n)], 8)
            for s0 in range(0, n, 128):
                w = min(128, n - s0)
                for tb in blocks:
                    p = ps()
                    for k in range(8):
                        kb.op("pe", lambda e: e.matmul(p[0:w, :], lhsT=wv[:, k, s0:s0 + w], rhs=hT[tb][:, k, :], start=(k == 0), stop=(k == 7)),
                              reads=[wb, hT[tb]], writes=[p], inc=(k == 7))
                    fn((t0 + s0) // 128, tb, p, w)

    def conv_chunk(convin, acc, cwt, cbt, cc, nseq, L, dst_ap_fn, dst_buf):
        for s_ in range(nseq):
            a = acc[:, s_ * L:(s_ + 1) * L]
            kb.op("dve", lambda e: e.tensor_scalar(out=a, in0=convin[:, s_, 0:L], scalar1=cwt[:, cc, 0:1], scalar2=None, op0=ALU.mult),
                  reads=[convin, cwt], writes=[acc])
            for tap in range(1, 5):
                kb.op("dve", lambda e: e.scalar_tensor_tensor(out=a, in0=convin[:, s_, tap:tap + L], scalar=cwt[:, cc, tap:tap + 1], in1=a,
                                                              op0=ALU.mult, op1=ALU.add), reads=[convin, cwt, acc], writes=[acc])
            if isinstance(dst_buf, list):
                for gg in range(2):
                    kb.op("act", lambda e: e.activation(out=dst_buf[gg][gg * 64:(gg + 1) * 64, s_ * L:(s_ + 1) * L], in_=acc[gg * 64:(gg + 1) * 64, s_ * L:(s_ + 1) * L],
                                                        func=AF.Silu, bias=cbt[gg * 64:(gg + 1) * 64, cc:cc + 1]), reads=[acc, cbt], writes=[dst_buf[gg]])
            else:
                kb.op("act", lambda e: e.activation(out=dst_ap_fn(s_), in_=a, func=AF.Silu, bias=cbt[:, cc:cc + 1]), reads=[acc, cbt], writes=[dst_buf])

    def evac_conv_in(convin, p, tb, nseq, L):
        if nseq == 1:
            kb.op("act", lambda e: e.activation(out=convin[:, 0, 2 + tb * 512:2 + (tb + 1) * 512], in_=p[:, :], func=AF.Copy), reads=[p], writes=[convin])
        else:
            for s_ in range(2):
                kb.op("act", lambda e: e.activation(out=convin[:, s_, 2:2 + L], in_=p[:, s_ * L:(s_ + 1) * L], func=AF.Copy), reads=[p], writes=[convin])

    def transpose_to_tm(src, dst_fn, dst_buf, T):
        for t0 in range(0, T, 4):
            p = ps()
            for j in range(4):
                kb.op("pe", lambda e: e.matmul(p[:, j * 128:(j + 1) * 128], lhsT=src[:, (t0 + j) * 128:(t0 + j + 1) * 128], rhs=ident_b, start=True, stop=True),
                      reads=[src, cb], writes=[p], inc=(j == 3))
            kb.op("act", lambda e: e.activation(out=dst_fn(t0, 4), in_=p[:, :].rearrange("p (t c) -> p t c", t=4), func=AF.Copy), reads=[p], writes=[dst_buf])

    def ssd(l, g, nblk):
        sample = (g == 1)
        L = 2048 if sample else 256
        nseq = 1 if sample else 2
        lt = L // 128
        ntok = nblk * 512
        T = ntok // 128
        blocks = list(range(nblk))
        with scope() as st:
            xtm = alloc(st, "xtm", [128, T, 512], BF16)
            Btm = alloc(st, "Btm", [128, T, 128], BF16)
            BT = alloc(st, "BT", [128, ntok], BF16)
            CTz = [alloc(st, "CT%d" % i, [128, ntok], BF16) for i in range(2)]
            for i_ in range(2):
                kb.op("dve", lambda e: e.memset(CTz[i_][:], 0.0), writes=[CTz[i_]])
            dtt = alloc(st, "dtt", [128, T, 16], F32)
            dtA = alloc(st, "dtA", [128, T, 16], F32)
            cum = alloc(st, "cum", [128, T, 16], F32)
            tot = alloc(st, "tot", [128, T, 16], F32)
            ecum = alloc(st, "ecum", [128, T, 16], F32)
            wd = alloc(st, "wd", [128, T, 16], F32)
            bj = alloc(st, "bj", [128, T, 16], F32)
            dec = alloc(st, "dec", [128, T, 2, 4], F32)
            abc = alloc(st, "abc", [128, 16], F32)
            dtb = alloc(st, "dtb", [128, 16], F32)
            dsk = alloc(st, "dsk", [128, 8], F32)
            ng = alloc(st, "ng", [128, 512], F32)
            cwt = alloc(st, "cwt", [128, 6, 5], F32)
            cbt = alloc(st, "cbt", [128, 6], F32)
            load_wo(l, 0)
            kb.dma("sp", abc[:], D["ssd_a_log"][l].partition_broadcast(128), writes=[abc])
            kb.dma("sp", dtb[:], D["ssd_dt_bias"][l].partition_broadcast(128), writes=[dtb])
            kb.dma("sp", dsk[:], D["ssd_d"][l].partition_broadcast(128), writes=[dsk])
            kb.dma("sp", ng[:], D["ssd_norm"][l].partition_broadcast(128), writes=[ng])
            for tap in range(5):
                kb.dma("sp", cwt[:, :, tap], D["conv_ssd_w"][l, tap].rearrange("(c p) -> p c", p=128), writes=[cwt], allow_slow_non_contiguous=True)
            kb.dma("sp", cbt[:], D["conv_ssd_b"][l].rearrange("(c p) -> p c", p=128), writes=[cbt], allow_slow_non_contiguous=True)
            kb.op("act", lambda e: e.activation(out=abc[:], in_=abc[:], func=AF.Exp), reads=[abc], writes=[abc])
            kb.op("dve", lambda e: e.tensor_scalar(out=abc[:], in0=abc[:], scalar1=-1.0, scalar2=None, op0=ALU.mult), reads=[abc], writes=[abc])
            with scope() as st1:
                convins = [alloc(st1, "convin", [128, nseq, L + 4], F32) for _ in range(2)]
                acc = alloc(st1, "cacc", [128, ntok], F32)
                xcT = alloc(st1, "xcT", [128, ntok], BF16)
                for cv_ in convins:
                    kb.op("dve", lambda e: e.memset(cv_[:], 0.0), writes=[cv_])

                def cb_fn(ci, tb, p, w):
                    convin = convins[ci % 2]
                    evac_conv_in(convin, p, tb, nseq, L)
                    if tb != blocks[-1]:
                        return
                    if ci < 4:
                        conv_chunk(convin, acc, cwt, cbt, ci, nseq, L, lambda s_: xcT[:, s_ * L:(s_ + 1) * L], xcT)
                        transpose_to_tm(xcT, lambda t0, n: xtm[:, t0:t0 + n, ci * 128:(ci + 1) * 128], xtm, T)
                    elif ci == 4:
                        conv_chunk(convin, acc, cwt, cbt, ci, nseq, L, lambda s_: BT[:, s_ * L:(s_ + 1) * L], BT)
                        transpose_to_tm(BT, lambda t0, n: Btm[:, t0:t0 + n, :], Btm, T)
                    else:
                        conv_chunk(convin, acc, cwt, cbt, ci, nseq, L, None, CTz)
                proj_fm(l, 512, 768, blocks, cb_fn)
            kb.barrier()
            if SSD_PH < 2:
                return
            wb, wv = wload([(D["w_in"][l][:, 1280:1296], 16)], 8)
            for tt in range(T):
                p = ps()
                proj_tm(wb, wv, 0, 16, tt // 4, tt % 4, p)
                kb.op("dve", lambda e: e.tensor_tensor(out=dtt[:, tt, :], in0=p[:, 0:16], in1=dtb[:], op=ALU.add), reads=[p, dtb], writes=[dtt])
            kb.op("act", lambda e: e.activation(out=dtt[:], in_=dtt[:], func=AF.Exp), reads=[dtt], writes=[dtt])
            kb.op("act", lambda e: e.activation(out=dtt[:], in_=dtt[:], func=AF.Ln, bias=1.0), reads=[dtt], writes=[dtt])
            kb.op("dve", lambda e: e.tensor_tensor(out=dtA[:], in0=dtt[:], in1=bcast(abc[:], [128, T, 16], 1), op=ALU.mult), reads=[dtt, abc], writes=[dtA])
            for tt in range(T):
                p = ps()
                kb.op("pe", lambda e: e.matmul(p[:, 0:16], lhsT=triF_f, rhs=dtA[:, tt, :], start=True, stop=True), reads=[cf, dtA], writes=[p], inc=False)
                kb.op("pe", lambda e: e.matmul(p[:, 16:32], lhsT=triB_f, rhs=dtA[:, tt, :], start=True, stop=True), reads=[cf, dtA], writes=[p], inc=False)
                kb.op("pe", lambda e: e.matmul(p[:, 32:48], lhsT=ones_f, rhs=dtA[:, tt, :], start=True, stop=True), reads=[cf, dtA], writes=[p])
                kb.op("dve", lambda e: e.tensor_copy(out=cum[:, tt, 0:8], in_=p[:, 0:8]), reads=[p], writes=[cum])
                kb.op("dve", lambda e: e.tensor_copy(out=cum[:, tt, 8:16], in_=p[:, 24:32]), reads=[p], writes=[cum])
                kb.op("dve", lambda e: e.tensor_copy(out=tot[:, tt, :], in_=p[:, 32:48]), reads=[p], writes=[tot])
            kb.op("act", lambda e: e.activation(out=ecum[:], in_=cum[:], func=AF.Exp), reads=[cum], writes=[ecum])
            kb.op("dve", lambda e: e.tensor_tensor(out=wd[:], in0=tot[:], in1=cum[:], op=ALU.subtract), reads=[tot, cum], writes=[wd])
            kb.op("act", lambda e: e.activation(out=wd[:], in_=wd[:], func=AF.Exp), reads=[wd], writes=[wd])
            kb.op("dve", lambda e: e.tensor_tensor(out=wd[:], in0=wd[:], in1=dtt[:], op=ALU.mult), reads=[wd, dtt], writes=[wd])
            kb.op("act", lambda e: e.activation(out=bj[:], in_=dtt[:], func=AF.Ln), reads=[dtt], writes=[bj])
            kb.op("dve", lambda e: e.tensor_tensor(out=bj[:], in0=bj[:], in1=cum[:], op=ALU.subtract), reads=[bj, cum], writes=[bj])
            tot4 = tot[:].rearrange("p t (d h) -> p t d h", d=2)
            for gg in range(2):
                kb.op("act", lambda e: e.activation(out=dec[gg * 64:(gg + 1) * 64], in_=tot4[gg * 64:(gg + 1) * 64, :, :, gg * 4:(gg + 1) * 4], func=AF.Exp),
                      reads=[tot], writes=[dec])
            if SSD_PH < 3:
                kb.barrier()
                return
            with scope() as st2:
                Hprev = alloc(st2, "Hprev", [128, T, 2, 256], BF16)
                st3 = ExitStack()
                Hs = [alloc(st3, "Hs%d" % i, [128, 256], F32) for i in range(2)]
                xws = [alloc(st3, "xw%d" % i, [128, 512], BF16) for i in range(2)]
                hx = alloc(st3, "hx", [128, 2, 128], F32)
                ho = alloc(st3, "ho", [128, 128], F32)
                xi = 0
                for dr in range(2):
                    H = Hs[dr]
                    for s_ in range(nseq):
                        if sample:
                            for blk in range(2):
                                for two in range(2):
                                    kb.dma("sp", hx[two * 64:(two + 1) * 64, blk, :].rearrange("p (g n) -> p g n", g=2),
                                           D["sssm"][l, dr].rearrange("(g r) p n -> r p g n", g=2)[blk * 2 + two], writes=[hx])
                            for blk in range(2):
                                p = ps()
                                kb.op("pe", lambda e: e.matmul(p[:, 0:128], lhsT=hx[:, blk, :], rhs=ident_f, start=True, stop=True), reads=[hx, cf], writes=[p])
                                kb.op("act", lambda e: e.activation(out=H[:, blk * 128:(blk + 1) * 128], in_=p[:, 0:128], func=AF.Copy), reads=[p], writes=[H])
                        else:
                            kb.op("dve", lambda e: e.memset(H[:], 0.0), writes=[H])
                        order = range(lt) if dr == 0 else range(lt - 1, -1, -1)
                        for r0 in order:
                            tt = s_ * lt + r0
                            kb.op("act", lambda e: e.activation(out=Hprev[:, tt, dr, :], in_=H[:], func=AF.Copy), reads=[H], writes=[Hprev])
                            xw = xws[xi % 2]
                            xi += 1
                            kb.op("dve", lambda e: e.tensor_tensor(out=xw[:].rearrange("p (h d) -> p h d", d=64), in0=xtm[:, tt, :].rearrange("p (h d) -> p h d", d=64),
                                                                   in1=bcast(wd[:, tt, dr * 8:(dr + 1) * 8], [128, 8, 64], 2), op=ALU.mult), reads=[xtm, wd], writes=[xw])
                            p = ps()
                            kb.op("pe", lambda e: e.matmul(p[:, :], lhsT=Btm[:, tt, :], rhs=xw[:], start=True, stop=True), reads=[Btm, xw], writes=[p])
                            kb.op("dve", lambda e: e.tensor_tensor(out=H[:].rearrange("p (h d) -> p h d", d=64), in0=H[:].rearrange("p (h d) -> p h d", d=64),
                                                                   in1=bcast(dec[:, tt, dr, :], [128, 4, 64], 2), op=ALU.mult), reads=[H, dec], writes=[H])
                            for gg in range(2):
                                kb.op("dve", lambda e: e.tensor_tensor(out=H[gg * 64:(gg + 1) * 64, :], in0=H[gg * 64:(gg + 1) * 64, :],
                                                                       in1=p[gg * 64:(gg + 1) * 64, gg * 256:(gg + 1) * 256], op=ALU.add), reads=[H, p], writes=[H])
                        if not sample:
                            for blk in range(2):
                                p = ps()
                                kb.op("pe", lambda e: e.matmul(p[:, 0:128], lhsT=H[:, blk * 128:(blk + 1) * 128], rhs=ident_f, start=True, stop=True), reads=[H, cf], writes=[p])
                                kb.op("act", lambda e: e.activation(out=ho[:], in_=p[:, 0:128], func=AF.Copy), reads=[p], writes=[ho])
                                for two in range(2):
                                    kb.dma("sp", D["nssm"][s_, l, dr].rearrange("(g r) p n -> r p g n", g=2)[blk * 2 + two],
                                           ho[two * 64:(two + 1) * 64, :].rearrange("p (g n) -> p g n", g=2), reads=[ho])
                kb.barrier()
                st3.close()
                if SSD_PH < 4:
                    return
                Dg = alloc(st2, "Dg", [128, 16, 128], F32)
                Es = [alloc(st2, "E%d" % i, [128, 128], F32) for i in range(3)]
                Ms = [alloc(st2, "M%d" % i, [128, 128], BF16) for i in range(3)]
                ya = alloc(st2, "ya", [128, 512], F32)
                yu = alloc(st2, "yu", [128, 512], F32)
                zs = yu
                ytm = alloc(st2, "ytm", [128, 4, 512], BF16)
                yT = alloc(st2, "yT", [128, 4, 512], BF16)
                ss = alloc(st2, "ss", [128, 4], F32)
                wzb, wzv = wload([(D["w_in"][l][:, 0:512], 512)], 8)
                ei = 0
                for tt in range(T):
                    tk = slice(tt * 128, (tt + 1) * 128)
                    pz = ps_acc()
                    proj_tm(wzb, wzv, 0, 512, tt // 4, tt % 4, pz)
                    pBC = ps_acc()
                    for gg in range(2):
                        kb.op("pe", lambda e: e.matmul(pBC[:, gg * 128:(gg + 1) * 128], lhsT=BT[:, tk], rhs=CTz[gg][:, tk], start=True, stop=True),
                              reads=[BT, CTz[gg]], writes=[pBC], inc=(gg == 1))
                    kb.op("dve", lambda e: e.tensor_tensor(out=Dg[:], in0=bcast(ident_f, [128, 16, 128], 1), in1=bcast(cum[:, tt, :], [128, 16, 128], 2), op=ALU.mult),
                          reads=[cf, cum], writes=[Dg])
                    yint = ps_acc()
                    hd = [(h, dr) for h in range(8) for dr in range(2)]
                    mts = {}

                    def sA(i):
                        h, dr = hd[i]
                        pE = ps()
                        kb.op("pe", lambda e: e.matmul(pE[:, 0:128], lhsT=ones_f, rhs=Dg[:, dr * 8 + h, :], start=True, stop=False), reads=[cf, Dg], writes=[pE], inc=False)
                        kb.op("pe", lambda e: e.matmul(pE[:, 0:128], lhsT=ident_b, rhs=(maskF_b if dr == 0 else maskB_b), start=False, stop=True), reads=[cb], writes=[pE])
                        E = Es[i % 3]
                        M = Ms[i % 3]
                        kb.op("act", lambda e: e.activation(out=E[:], in_=pE[:, 0:128], func=AF.Exp, bias=bj[:, tt, dr * 8 + h:dr * 8 + h + 1]), reads=[pE, bj], writes=[E])
                        gg = h // 4
                        kb.op("dve", lambda e: e.tensor_tensor(out=M[:], in0=E[:], in1=pBC[:, gg * 128:(gg + 1) * 128], op=ALU.mult), reads=[E, pBC], writes=[M])
                        mts[i] = M

                    def sC(i):
                        h, dr = hd[i]
                        M = mts.pop(i)
                        kb.op("pe", lambda e: e.matmul(yint[:, h * 64:(h + 1) * 64], lhsT=M[:], rhs=xtm[:, tt, h * 64:(h + 1) * 64], start=(dr == 0), stop=(dr == 1)),
                              reads=[M, xtm], writes=[yint], inc=True)

                    for i in range(16 + 2):
                        if i < 16:
                            sA(i)
                        if i >= 2:
                            sC(i - 2)
                    if SSD_PH < 5:
                        continue
                    pY = [ps(), ps()]
                    for dr in range(2):
                        for gg in range(2):
                            kb.op("pe", lambda e: e.matmul(pY[dr][:, gg * 256:(gg + 1) * 256], lhsT=CTz[gg][:, tk], rhs=Hprev[:, tt, dr, :], start=True, stop=True),
                                  reads=[CTz[gg], Hprev], writes=[pY[dr]], inc=(gg == 1))
                    v3 = lambda ap: ap.rearrange("p (h d) -> p h d", d=64)
                    kb.op("dve", lambda e: e.tensor_tensor(out=v3(ya[:]), in0=v3(xtm[:, tt, :]), in1=bcast(dsk[:], [128, 8, 64], 2), op=ALU.mult), reads=[xtm, dsk], writes=[ya])
                    kb.op("dve", lambda e: e.tensor_tensor(out=ya[:], in0=ya[:], in1=yint[:, :], op=ALU.add), reads=[ya, yint], writes=[ya])
                    for dr in range(2):
                        kb.op("dve", lambda e: e.tensor_tensor(out=v3(yu[:]), in0=v3(pY[dr][:, :]), in1=bcast(ecum[:, tt, dr * 8:(dr + 1) * 8], [128, 8, 64], 2), op=ALU.mult),
                              reads=[pY[dr], ecum], writes=[yu])
                        kb.op("dve", lambda e: e.tensor_tensor(out=ya[:], in0=ya[:], in1=yu[:], op=ALU.add), reads=[ya, yu], writes=[ya])
                    if SSD_PH < 6:
                        continue
                    kb.op("act", lambda e: e.activation(out=zs[:], in_=pz[:, :], func=AF.Silu), reads=[pz], writes=[zs])
                    kb.op("dve", lambda e: e.tensor_tensor(out=ya[:], in0=ya[:], in1=zs[:], op=ALU.mult), reads=[ya, zs], writes=[ya])
                    kb.op("dve", lambda e: e.memset(ss[:, 0:1], 0.0), writes=[ss])
                    kb.op("act", lambda e: e.activation(out=yu[:], in_=ya[:], func=AF.Square, accum_out=ss[:, 0:1]), reads=[ya, ss], writes=[yu, ss])
                    kb.op("dve", lambda e: e.tensor_scalar(out=ss[:, 1:2], in0=ss[:, 0:1], scalar1=1.0 / 512, scalar2=EPS, op0=ALU.mult, op1=ALU.add), reads=[ss], writes=[ss])
                    kb.op("act", lambda e: e.activation(out=ss[:, 1:2], in_=ss[:, 1:2], func=AF.Sqrt), reads=[ss], writes=[ss])
                    kb.op("dve", lambda e: e.reciprocal(out=ss[:, 2:3], in_=ss[:, 1:2]), reads=[ss], writes=[ss])
                    kb.op("dve", lambda e: e.scalar_tensor_tensor(out=ytm[:, tt % 4, :], in0=ya[:], scalar=ss[:, 2:3], in1=ng[:], op0=ALU.mult, op1=ALU.mult),
                          reads=[ya, ss, ng], writes=[ytm])
                    if SSD_PH < 7:
                        continue
                    if tt % 4 == 3:
                        mixer_out(st2, ytm, tt // 4, yT)
        kb.barrier()


    def mlstm(l, g, nblk):
        sample = (g == 1)
        L = 2048 if sample else 256
        nseq = 1 if sample else 2
        lt = L // 128
        ntok = nblk * 512
        T = ntok // 128
        blocks = list(range(nblk))
        C0 = 2832
        lns = math.log(128 ** -0.5)
        for hg in range(2):
            h0 = hg * 2
            with scope() as st:
                qT = alloc(st, "mqT", [128, 2, ntok], BF16)
                kT = alloc(st, "mkT", [128, 2, ntok], BF16)
                vaug = alloc(st, "mvaug", [128, T, 2, 129], BF16)
                Cpb = alloc(st, "Cpb", [128, T, 2, 129], BF16)
                li = alloc(st, "li", [128, T, 4], F32)
                lf = alloc(st, "lf", [128, T, 4], F32)
                G = alloc(st, "G", [128, T, 4], F32)
                tot = alloc(st, "mtot", [128, T, 4], F32)
                pj = alloc(st, "pj", [128, T, 4], F32)
                pjs = alloc(st, "pjs", [128, T, 4], F32)
                gend = alloc(st, "gend", [128, T, 4], F32)
                mlb = alloc(st, "mlb", [128, T, 4], F32)
                wend = alloc(st, "wend", [128, T, 4], F32)
                mprev = alloc(st, "mprev", [128, T, 4], F32)
                gb = alloc(st, "gb", [128, 8], F32)
                cwt = alloc(st, "mcwt", [128, 4, 5], F32)
                cbt = alloc(st, "mcbt", [128, 4], F32)
                ng = alloc(st, "mng", [128, 256], F32)
                kb.dma("pool", wo_buf[:, 0:2, :], D["w_out"][l][1024 + h0 * 128:1024 + (h0 + 2) * 128, :].rearrange("(k p) n -> p k n", p=128), writes=[wo_buf])
                kb.dma("sp", ng[:], D["mlstm_norm"][l][h0 * 128:(h0 + 2) * 128].partition_broadcast(128), writes=[ng])
                goffs = [0 * 8 + 0 * 4 + h0, 1 * 8 + 0 * 4 + h0, 0 * 8 + 1 * 4 + h0, 1 * 8 + 1 * 4 + h0]
                for i_, go in enumerate(goffs):
                    kb.dma("sp", gb[:, i_ * 2:(i_ + 1) * 2], D["mlstm_gate_b"][l][go:go + 2].partition_broadcast(128), writes=[gb])
                for ci, ch0 in enumerate((h0 * 128, (h0 + 1) * 128, 512 + h0 * 128, 512 + (h0 + 1) * 128)):
                    for tap in range(5):
                        kb.dma("sp", cwt[:, ci, tap:tap + 1], D["conv_mlstm_w"][l, tap][ch0:ch0 + 128].rearrange("(p o) -> p o", o=1), writes=[cwt])
                    kb.dma("sp", cbt[:, ci:ci + 1], D["conv_mlstm_b"][l][ch0:ch0 + 128].rearrange("(p o) -> p o", o=1), writes=[cbt])
                kb.op("dve", lambda e: e.memset(vaug[:, :, :, 128:129], 1.0), writes=[vaug])
                with scope() as st1:
                    convins = [alloc(st1, "mconvin", [128, nseq, L + 4], F32) for _ in range(2)]
                    acc = alloc(st1, "mcacc", [128, ntok], F32)
                    for cv_ in convins:
                        kb.op("dve", lambda e: e.memset(cv_[:], 0.0), writes=[cv_])
                    wb, wv = wload([(D["w_in"][l][:, C0 + h0 * 128:C0 + (h0 + 2) * 128], 256),
                                    (D["w_in"][l][:, C0 + 512 + h0 * 128:C0 + 512 + (h0 + 2) * 128], 256)], 8)
                    for ci in range(4):
                        for tb in blocks:
                            p = ps()
                            for k in range(8):
                                kb.op("pe", lambda e: e.matmul(p[:, :], lhsT=wv[:, k, ci * 128:(ci + 1) * 128], rhs=hT[tb][:, k, :], start=(k == 0), stop=(k == 7)),
                                      reads=[wb, hT[tb]], writes=[p], inc=(k == 7))
                            evac_conv_in(convins[ci % 2], p, tb, nseq, L)
                        convin = convins[ci % 2]
                        dstb = qT if ci < 2 else kT
                        conv_chunk(convin, acc, cwt, cbt, ci, nseq, L, lambda s_: dstb[:, ci % 2, s_ * L:(s_ + 1) * L], dstb)
                kb.barrier()
                wb, wv = wload([(D["w_in"][l][:, C0 + 1024 + h0 * 128:C0 + 1024 + (h0 + 2) * 128], 256)], 8)
                for tt in range(T):
                    p = ps()
                    proj_tm(wb, wv, 0, 256, tt // 4, tt % 4, p)
                    kb.op("act", lambda e: e.activation(out=vaug[:, tt, :, 0:128], in_=p[:, 0:256].rearrange("p (h e) -> p h e", e=128), func=AF.Copy), reads=[p], writes=[vaug])
                gc = C0 + 2048
                wb, wv = wload([(D["w_in"][l][:, gc + go:gc + go + 2], 2) for go in goffs], 8)
                for tt in range(T):
                    p = ps()
                    proj_tm(wb, wv, 0, 8, tt // 4, tt % 4, p)
                    kb.op("dve", lambda e: e.tensor_tensor(out=li[:, tt, :], in0=p[:, 0:4], in1=gb[:, 0:4], op=ALU.add), reads=[p, gb], writes=[li])
                    kb.op("dve", lambda e: e.tensor_tensor(out=lf[:, tt, :], in0=p[:, 4:8], in1=gb[:, 4:8], op=ALU.add), reads=[p, gb], writes=[lf])
                kb.op("act", lambda e: e.activation(out=lf[:], in_=lf[:], func=AF.Exp, scale=-1.0), reads=[lf], writes=[lf])
                kb.op("act", lambda e: e.activation(out=lf[:], in_=lf[:], func=AF.Ln, bias=1.0), reads=[lf], writes=[lf])
                kb.op("dve", lambda e: e.tensor_scalar(out=lf[:], in0=lf[:], scalar1=-1.0, scalar2=None, op0=ALU.mult), reads=[lf], writes=[lf])
                for tt in range(T):
                    p = ps()
                    kb.op("pe", lambda e: e.matmul(p[:, 0:4], lhsT=triF_f, rhs=lf[:, tt, :], start=True, stop=True), reads=[cf, lf], writes=[p], inc=False)
                    kb.op("pe", lambda e: e.matmul(p[:, 4:8], lhsT=triB_f, rhs=lf[:, tt, :], start=True, stop=True), reads=[cf, lf], writes=[p], inc=False)
                    kb.op("pe", lambda e: e.matmul(p[:, 8:12], lhsT=ones_f, rhs=lf[:, tt, :], start=True, stop=True), reads=[cf, lf], writes=[p])
                    kb.op("dve", lambda e: e.tensor_copy(out=G[:, tt, 0:2], in_=p[:, 0:2]), reads=[p], writes=[G])
                    kb.op("dve", lambda e: e.tensor_copy(out=G[:, tt, 2:4], in_=p[:, 6:8]), reads=[p], writes=[G])
                    kb.op("dve", lambda e: e.tensor_copy(out=tot[:, tt, :], in_=p[:, 8:12]), reads=[p], writes=[tot])
                kb.op("dve", lambda e: e.tensor_tensor(out=pj[:], in0=li[:], in1=G[:], op=ALU.subtract), reads=[li, G], writes=[pj])
                kb.op("dve", lambda e: e.tensor_scalar(out=pjs[:], in0=pj[:], scalar1=lns, scalar2=None, op0=ALU.add), reads=[pj], writes=[pjs])
                kb.op("dve", lambda e: e.tensor_tensor(out=gend[:], in0=pj[:], in1=tot[:], op=ALU.add), reads=[pj, tot], writes=[gend])
                with scope() as stt:
                    mrow = alloc(stt, "mrow8", [4, 1], F32)
                    d8 = alloc(stt, "d8", [4, 4], F32)
                    for tt in range(T):
                        p = ps()
                        kb.op("pe", lambda e: e.matmul(p[0:4, 0:128], lhsT=gend[:, tt, :], rhs=ident_f, start=True, stop=True), reads=[gend, cf], writes=[p])
                        kb.op("dve", lambda e: e.tensor_reduce(out=mrow[:], in_=p[0:4, 0:128], axis=AX.X, op=ALU.max), reads=[p], writes=[mrow])
                        kb.op("dve", lambda e: e.tensor_scalar(out=d8[:], in0=ident_f[0:4, 0:4], scalar1=mrow[:, 0:1], scalar2=None, op0=ALU.mult), reads=[cf, mrow], writes=[d8])
                        p2 = ps()
                        kb.op("pe", lambda e: e.matmul(p2[:, 0:4], lhsT=ones_f[0:4, :], rhs=d8[:], start=True, stop=True), reads=[cf, d8], writes=[p2])
                        kb.op("dve", lambda e: e.tensor_copy(out=mlb[:, tt, :], in_=p2[:, 0:4]), reads=[p2], writes=[mlb])
                    kb.barrier()
                kb.op("dve", lambda e: e.tensor_tensor(out=wend[:], in0=gend[:], in1=mlb[:], op=ALU.subtract), reads=[gend, mlb], writes=[wend])
                kb.op("act", lambda e: e.activation(out=wend[:], in_=wend[:], func=AF.Exp), reads=[wend], writes=[wend])
                with scope() as st2:
                    Cst = [alloc(st2, "Cst%d" % i, [128, 129], F32) for i in range(4)]
                    mp = alloc(st2, "mp", [128, 4], F32)
                    mt8 = alloc(st2, "mt8", [128, 16], F32)
                    kwt = alloc(st2, "kwt", [128, 2, 128], BF16)
                    Dg = alloc(st2, "mDg", [128, 4, 128], F32)
                    Dr = alloc(st2, "mDr", [128, 4, 128], F32)
                    sc = alloc(st2, "msc", [128, 40], F32)
                    Es = [alloc(st2, "mE%d" % i, [128, 128], F32) for i in range(3)]
                    Ms = [alloc(st2, "mM%d" % i, [128, 128], BF16) for i in range(3)]
                    nds = [alloc(st2, "nd%d" % i, [128, 129], F32) for i in range(2)]
                    cbf = [alloc(st2, "cbf%d" % i, [128, 129], BF16) for i in range(2)]
                    hsum = alloc(st2, "hsum", [128, 256], F32)
                    ht = alloc(st2, "mht", [128, 256], F32)
                    sg = alloc(st2, "msg", [128, 256], F32)
                    ytm = alloc(st2, "mytm", [128, 4, 256], BF16)
                    yT = alloc(st2, "myT", [128, 2, 512], BF16)
                    ei = 0
                    ei0 = [0]

                    def init_state(dr, s_):
                        for hh in range(2):
                            C = Cst[dr * 2 + hh]
                            if sample:
                                kb.dma("sp", C[:, 0:128], D["smc"][l, dr, h0 + hh], writes=[C])
                                kb.dma("sp", C[:, 128:129], D["smn"][l, dr, h0 + hh].rearrange("(p o) -> p o", o=1), writes=[C])
                            else:
                                kb.op("dve", lambda e: e.memset(C[:], 0.0), writes=[C])
                        if sample:
                            kb.dma("sp", mp[:, dr * 2:dr * 2 + 2], D["smm"][l][dr * 4 + h0:dr * 4 + h0 + 2].partition_broadcast(128), writes=[mp])
                        else:
                            kb.op("dve", lambda e: e.memset(mp[:, dr * 2:dr * 2 + 2], 0.0), writes=[mp])

                    def local_update(dr, tt):
                        cs = slice(dr * 2, dr * 2 + 2)
                        pk = ps()
                        for hh in range(2):
                            kb.op("pe", lambda e: e.matmul(pk[:, hh * 128:(hh + 1) * 128], lhsT=kT[:, hh, tt * 128:(tt + 1) * 128], rhs=ident_b, start=True, stop=True),
                                  reads=[kT, cb], writes=[pk], inc=(hh == 1))
                        kb.op("dve", lambda e: e.tensor_tensor(out=kwt[:], in0=pk[:, 0:256].rearrange("p (h d) -> p h d", d=128),
                                                               in1=bcast(wend[:, tt, cs], [128, 2, 128], 2), op=ALU.mult), reads=[pk, wend], writes=[kwt])
                        a = sc[:, 0:2]
                        mn = sc[:, 2:4]
                        sp_ = sc[:, 4:6]
                        sl_ = sc[:, 6:8]
                        kb.op("dve", lambda e: e.tensor_tensor(out=a, in0=tot[:, tt, cs], in1=mp[:, cs], op=ALU.add), reads=[tot, mp], writes=[sc])
                        kb.op("dve", lambda e: e.tensor_tensor(out=mn, in0=a, in1=mlb[:, tt, cs], op=ALU.max), reads=[sc, mlb], writes=[sc])
                        kb.op("dve", lambda e: e.tensor_tensor(out=sp_, in0=a, in1=mn, op=ALU.subtract), reads=[sc], writes=[sc])
                        kb.op("dve", lambda e: e.tensor_tensor(out=sl_, in0=mlb[:, tt, cs], in1=mn, op=ALU.subtract), reads=[sc, mlb], writes=[sc])
                        kb.op("act", lambda e: e.activation(out=sc[:, 4:8], in_=sc[:, 4:8], func=AF.Exp), reads=[sc], writes=[sc])
                        kb.op("dve", lambda e: e.tensor_copy(out=mp[:, cs], in_=mn), reads=[sc], writes=[mp])
                        for hh in range(2):
                            C = Cst[dr * 2 + hh]
                            pc = ps()
                            kb.op("pe", lambda e: e.matmul(pc[:, 0:129], lhsT=kwt[:, hh, :], rhs=vaug[:, tt, hh, :], start=True, stop=True), reads=[kwt, vaug], writes=[pc])
                            kb.op("dve", lambda e: e.tensor_scalar(out=C[:], in0=C[:], scalar1=sc[:, 4 + hh:5 + hh], scalar2=None, op0=ALU.mult), reads=[C, sc], writes=[C])
                            kb.op("dve", lambda e: e.scalar_tensor_tensor(out=C[:], in0=pc[:, 0:129], scalar=sc[:, 6 + hh:7 + hh], in1=C[:], op0=ALU.mult, op1=ALU.add),
                                  reads=[pc, sc, C], writes=[C])

                    def final_state(dr, s_):
                        for hh in range(2):
                            C = Cst[dr * 2 + hh]
                            kb.dma("sp", D["nmc"][s_, l, dr, h0 + hh], C[:, 0:128], reads=[C])
                            kb.dma("sp", D["nmn"][s_, l, dr, h0 + hh].rearrange("(p o) -> p o", o=1), C[:, 128:129], reads=[C])
                        kb.dma("sp", D["nmm"][s_, l:l + 1, dr * 4 + h0:dr * 4 + h0 + 2], mp[0:1, dr * 2:dr * 2 + 2], reads=[mp])

                    for s_ in range(nseq):
                        init_state(1, s_)
                        for r0 in range(lt - 1, -1, -1):
                            tt = s_ * lt + r0
                            for hh in range(2):
                                kb.op("act", lambda e: e.activation(out=Cpb[:, tt, hh, :], in_=Cst[2 + hh][:], func=AF.Copy), reads=[Cst[2 + hh]], writes=[Cpb])
                            kb.op("dve", lambda e: e.tensor_copy(out=mprev[:, tt, 2:4], in_=mp[:, 2:4]), reads=[mp], writes=[mprev])
                            local_update(1, tt)
                        if not sample:
                            final_state(1, s_)
                    wob, wov = wload([(D["w_in"][l][:, C0 + 1536 + h0 * 128:C0 + 1536 + (h0 + 2) * 128], 256)], 8)
                    for s_ in range(nseq):
                        init_state(0, s_)
                        for r0 in range(lt):
                            tt = s_ * lt + r0
                            tk = slice(tt * 128, (tt + 1) * 128)
                            kb.op("dve", lambda e: e.tensor_copy(out=mprev[:, tt, 0:2], in_=mp[:, 0:2]), reads=[mp], writes=[mprev])
                            kb.op("dve", lambda e: e.tensor_tensor(out=sc[:, 8:12], in0=G[:, tt, :], in1=mprev[:, tt, :], op=ALU.add), reads=[G, mprev], writes=[sc])
                            kb.op("dve", lambda e: e.tensor_tensor(out=Dg[:], in0=bcast(ident_f, [128, 4, 128], 1), in1=bcast(pj[:, tt, :], [128, 4, 128], 2), op=ALU.mult),
                                  reads=[cf, pj], writes=[Dg])
                            po = ps_acc()
                            proj_tm(wob, wov, 0, 256, tt // 4, tt % 4, po)
                            pSs = []
                            for hh in range(2):
                                pS = ps_acc()
                                kb.op("pe", lambda e: e.matmul(pS[:, 0:128], lhsT=kT[:, hh, tk], rhs=qT[:, hh, tk], start=True, stop=True), reads=[kT, qT], writes=[pS])
                                pSs.append(pS)
                            pms = []
                            for c in range(4):
                                dr = c // 2
                                pm = ps()
                                kb.op("pe", lambda e: e.matmul(pm[:, 0:128], lhsT=ones_f, rhs=Dg[:, c, :], start=True, stop=False), reads=[cf, Dg], writes=[pm], inc=False)
                                kb.op("pe", lambda e: e.matmul(pm[:, 0:128], lhsT=ident_b, rhs=(maskB_b if dr == 0 else maskF_b), start=False, stop=True), reads=[cb], writes=[pm])
                                pms.append(pm)
                            for c in range(4):
                                kb.op("dve", lambda e: e.tensor_reduce(out=sc[:, 12 + c:13 + c], in_=pms[c][:, 0:128], axis=AX.X, op=ALU.max), reads=[pms[c]], writes=[sc])
                            kb.op("dve", lambda e: e.tensor_tensor(out=sc[:, 12:16], in0=sc[:, 12:16], in1=G[:, tt, :], op=ALU.add), reads=[sc, G], writes=[sc])
                            kb.op("dve", lambda e: e.tensor_tensor(out=sc[:, 16:20], in0=sc[:, 12:16], in1=sc[:, 8:12], op=ALU.max), reads=[sc], writes=[sc])
                            kb.op("dve", lambda e: e.tensor_tensor(out=sc[:, 20:24], in0=G[:, tt, :], in1=sc[:, 16:20], op=ALU.subtract), reads=[sc, G], writes=[sc])
                            kb.op("dve", lambda e: e.tensor_tensor(out=sc[:, 24:28], in0=sc[:, 8:12], in1=sc[:, 16:20], op=ALU.subtract), reads=[sc], writes=[sc])
                            kb.op("act", lambda e: e.activation(out=sc[:, 24:28], in_=sc[:, 24:28], func=AF.Exp, bias=lns_col[:, 0:1]), reads=[sc, cf], writes=[sc])
                            kb.op("act", lambda e: e.activation(out=sc[:, 28:32], in_=sc[:, 16:20], func=AF.Exp, scale=-1.0), reads=[sc], writes=[sc])
                            kb.op("dve", lambda e: e.tensor_tensor(out=Dr[:], in0=bcast(ident_f, [128, 4, 128], 1), in1=bcast(sc[:, 20:24], [128, 4, 128], 2), op=ALU.mult),
                                  reads=[cf, sc], writes=[Dr])
                            items = [(0, 0), (0, 1), (1, 0), (1, 1)]
                            mts = {}

                            def mA(i):
                                hh, dr = items[i]
                                c = dr * 2 + hh
                                pW = ps()
                                kb.op("pe", lambda e: e.matmul(pW[:, 0:128], lhsT=ones_f, rhs=Dr[:, c, :], start=True, stop=False), reads=[cf, Dr], writes=[pW], inc=False)
                                kb.op("pe", lambda e: e.matmul(pW[:, 0:128], lhsT=ident_b, rhs=(maskF_b if dr == 0 else maskB_b), start=False, stop=True), reads=[cb], writes=[pW])
                                E = Es[(ei0[0] + i) % 3]
                                M = Ms[(ei0[0] + i) % 3]
                                kb.op("act", lambda e: e.activation(out=E[:], in_=pW[:, 0:128], func=AF.Exp, bias=pjs[:, tt, c:c + 1]), reads=[pW, pjs], writes=[E])
                                kb.op("dve", lambda e: e.tensor_tensor(out=M[:], in0=E[:], in1=pSs[hh][:, 0:128], op=ALU.mult), reads=[E, pSs[hh]], writes=[M])
                                if dr == 0:
                                    kb.op("act", lambda e: e.activation(out=cbf[hh][:], in_=Cst[hh][:], func=AF.Copy), reads=[Cst[hh]], writes=[cbf[hh]])
                                mts[i] = M

                            def mC(i):
                                hh, dr = items[i]
                                c = dr * 2 + hh
                                M = mts.pop(i)
                                nd = nds[i % 2]
                                pN = ps()
                                kb.op("pe", lambda e: e.matmul(pN[:, 0:129], lhsT=M[:], rhs=vaug[:, tt, hh, :], start=True, stop=True), reads=[M, vaug], writes=[pN])
                                pI = ps()
                                if dr == 0:
                                    kb.op("pe", lambda e: e.matmul(pI[:, 0:129], lhsT=qT[:, hh, tk], rhs=cbf[hh][:], start=True, stop=True), reads=[qT, cbf[hh]], writes=[pI])
                                else:
                                    kb.op("pe", lambda e: e.matmul(pI[:, 0:129], lhsT=qT[:, hh, tk], rhs=Cpb[:, tt, hh, :], start=True, stop=True), reads=[qT, Cpb], writes=[pI])
                                kb.op("act", lambda e: e.activation(out=nd[:], in_=pN[:, 0:129], func=AF.Copy), reads=[pN], writes=[nd])
                                kb.op("dve", lambda e: e.scalar_tensor_tensor(out=nd[:], in0=pI[:, 0:129], scalar=sc[:, 24 + c:25 + c], in1=nd[:], op0=ALU.mult, op1=ALU.add),
                                      reads=[pI, sc, nd], writes=[nd])
                                kb.op("dve", lambda e: e.tensor_scalar(out=sc[:, 32:33], in0=nd[:, 128:129], scalar1=-1.0, scalar2=None, op0=ALU.mult), reads=[nd], writes=[sc])
                                kb.op("dve", lambda e: e.tensor_tensor(out=sc[:, 32:33], in0=sc[:, 32:33], in1=nd[:, 128:129], op=ALU.max), reads=[nd, sc], writes=[sc])
                                kb.op("dve", lambda e: e.tensor_tensor(out=sc[:, 32:33], in0=sc[:, 32:33], in1=sc[:, 28 + c:29 + c], op=ALU.max), reads=[sc], writes=[sc])
                                kb.op("dve", lambda e: e.reciprocal(out=sc[:, 33:34], in_=sc[:, 32:33]), reads=[sc], writes=[sc])
                                if dr == 0:
                                    kb.op("dve", lambda e: e.tensor_scalar(out=hsum[:, hh * 128:(hh + 1) * 128], in0=nd[:, 0:128], scalar1=sc[:, 33:34], scalar2=None, op0=ALU.mult),
                                          reads=[nd, sc], writes=[hsum])
                                else:
                                    kb.op("dve", lambda e: e.scalar_tensor_tensor(out=hsum[:, hh * 128:(hh + 1) * 128], in0=nd[:, 0:128], scalar=sc[:, 33:34],
                                                                                  in1=hsum[:, hh * 128:(hh + 1) * 128], op0=ALU.mult, op1=ALU.add), reads=[nd, sc, hsum], writes=[hsum])

                            for i in range(4 + 2):
                                if i < 4:
                                    mA(i)
                                if i >= 2:
                                    mC(i - 2)
                            ei0[0] += 4
                            local_update(0, tt)
                            h3 = hsum[:].rearrange("p (h d) -> p h d", d=128)
                            t3 = ht[:].rearrange("p (h d) -> p h d", d=128)
                            kb.op("dve", lambda e: e.tensor_tensor(out=ht[:], in0=hsum[:], in1=hsum[:], op=ALU.mult), reads=[hsum], writes=[ht])
                            kb.op("dve", lambda e: e.tensor_reduce(out=sc[:, 34:36], in_=t3, axis=AX.X, op=ALU.add), reads=[ht], writes=[sc])
                            kb.op("dve", lambda e: e.tensor_scalar(out=sc[:, 34:36], in0=sc[:, 34:36], scalar1=1.0 / 128, scalar2=EPS, op0=ALU.mult, op1=ALU.add), reads=[sc], writes=[sc])
                            kb.op("act", lambda e: e.activation(out=sc[:, 34:36], in_=sc[:, 34:36], func=AF.Sqrt), reads=[sc], writes=[sc])
                            kb.op("dve", lambda e: e.reciprocal(out=sc[:, 36:38], in_=sc[:, 34:36]), reads=[sc], writes=[sc])
                            kb.op("dve", lambda e: e.tensor_tensor(out=t3, in0=h3, in1=bcast(sc[:, 36:38], [128, 2, 128], 2), op=ALU.mult), reads=[hsum, sc], writes=[ht])
                            kb.op("dve", lambda e: e.tensor_tensor(out=ht[:], in0=ht[:], in1=ng[:], op=ALU.mult), reads=[ht, ng], writes=[ht])
                            kb.op("act", lambda e: e.activation(out=sg[:], in_=po[:, 0:256], func=AF.Sigmoid), reads=[po], writes=[sg])
                            kb.op("dve", lambda e: e.tensor_tensor(out=ytm[:, tt % 4, :], in0=ht[:], in1=sg[:], op=ALU.mult), reads=[ht, sg], writes=[ytm])
                            if tt % 4 == 3:
                                tb = tt // 4
                                for c2 in range(2):
                                    p = ps()
                                    for tl in range(4):
                                        kb.op("pe", lambda e: e.matmul(p[:, tl * 128:(tl + 1) * 128], lhsT=ytm[:, tl, c2 * 128:(c2 + 1) * 128], rhs=ident_b, start=True, stop=True),
                                              reads=[ytm, cb], writes=[p], inc=(tl == 3))
                                    kb.op("act", lambda e: e.activation(out=yT[:, c2, :], in_=p[:, :], func=AF.Copy), reads=[p], writes=[yT])
                                for c in range(8):
                                    p = ps()
                                    for k in range(2):
                                        kb.op("pe", lambda e: e.matmul(p[:, :], lhsT=wo_buf[:, k, c * 128:(c + 1) * 128], rhs=yT[:, k, :], start=(k == 0), stop=(k == 1)),
                                              reads=[wo_buf, yT], writes=[p], inc=(k == 1))
                                    kb.op("dve", lambda e: e.scalar_tensor_tensor(out=xT[tb][:, c, :], in0=p[:, :], scalar=modc[:, 2, c:c + 1], in1=xT[tb][:, c, :],
                                                                                  op0=ALU.mult, op1=ALU.add), reads=[p, modc, xT[tb]], writes=[xT[tb]])
                        if not sample:
                            final_state(0, s_)
            kb.barrier()

    def run_pass(g, src, dst, nblk):
        load_x(src, nblk)
        kb.barrier()
        for l in range(NL):
            load_mod(l, g)
            with scope() as st:
                sq = [alloc(st, "sq", [128, 8, 512], BF16) for _ in range(2)]
                rstd = [alloc(st, "rstd", [128, 512], F32) for _ in range(2)]
                tmp2 = [alloc(st, "tmpn%d" % i, [128, 512], F32) for i in range(4)]
                for tb in range(nblk):
                    norm_block((sq, rstd, tmp2), tb, AB.t[:, 0, :], AB.t[:, 1, :], hT[tb])
            kb.barrier()
            for mk in MIXERS:
                if mk in "bd":
                    attention(l, g, mk, nblk)
                elif mk == "a":
                    ssd(l, g, nblk)
                elif mk == "c":
                    mlstm(l, g, nblk)
            with scope() as st:
                sq = [alloc(st, "sq", [128, 8, 512], BF16) for _ in range(2)]
                rstd = [alloc(st, "rstd", [128, 512], F32) for _ in range(2)]
                tmp2 = [alloc(st, "tmpn%d" % i, [128, 512], F32) for i in range(4)]
                for tb in range(nblk):
                    norm_block((sq, rstd, tmp2), tb, AB.t[:, 2, :], AB.t[:, 3, :], hT[tb])
            kb.barrier()
            for b0 in range(0, nblk, 2):
                ffn(l, list(range(b0, min(b0 + 2, nblk))))
        final_out(nblk, dst)

    run_pass(0, D["xp"], D["yp"], 1)
    run_pass(1, D["xs"], D["ys"], 4)

    kb.barrier(include_pool_dma=True)
    top.close()
    print("instructions:", kb.ninst, flush=True)
    return nc


_CACHE = {}


def prep_inputs(inp):
    f = lambda a: np.ascontiguousarray(np.asarray(a, dtype=np.float32))
    consts = make_consts()
    rope = make_rope()
    shared = {}
    for name in ("w_ada", "b_ada", "norm1", "norm2", "w_in", "w_out", "conv_ssd_w", "conv_ssd_b", "ssd_d", "ssd_norm",
                 "diff_lq1", "diff_lk1", "diff_lq2", "diff_lk2", "conv_mlstm_w", "conv_mlstm_b", "mlstm_norm",
                 "gqa_q_norm", "gqa_k_norm", "w_ffn_in", "w_ffn_out", "norm_f"):
        shared[name] = f(inp[name])
    shared["ssd_a_log"] = f(inp["ssd_a_log"]).reshape(4, 16)
    shared["ssd_dt_bias"] = f(inp["ssd_dt_bias"]).reshape(4, 16)
    shared["mlstm_gate_b"] = f(inp["mlstm_gate_b"]).reshape(4, 16)
    shared["consts"] = consts
    shared["rope"] = rope
    xp = f(inp["x_prompt"])
    xs = f(inp["x_sample"])
    in_maps = []
    for c in range(8):
        b = c // 4
        m = dict(shared)
        m["xp"] = xp[2 * c:2 * c + 2].reshape(512, 1024)
        m["xs"] = xs[b]
        m["cvec"] = np.stack([f(inp["c_ctx"]), f(inp["c"])[b]], axis=0)
        m["cdk"] = f(inp["cache_diff_k"])[b].reshape(4, 256, 512)
        m["cdv"] = f(inp["cache_diff_v"])[b].reshape(4, 256, 512)
        m["cgk"] = f(inp["cache_gqa_k"])[b].reshape(4, 256, 128)
        m["cgv"] = f(inp["cache_gqa_v"])[b].reshape(4, 256, 128)
        m["sssm"] = f(inp["state_ssm"])[b]
        m["smc"] = f(inp["state_mlstm_c"])[b]
        m["smn"] = f(inp["state_mlstm_n"])[b]
        m["smm"] = f(inp["state_mlstm_m"])[b].reshape(4, 8)
        in_maps.append(m)
    if NL < 4:
        spec = dict(IN_SPECS)
        for m in in_maps:
            for k_ in list(m.keys()):
                if spec[k_][0] == 4 and len(spec[k_]) > 1:
                    m[k_] = np.ascontiguousarray(m[k_][:NL])
    return in_maps


def kernel(**inp):
    if "nc" not in _CACHE:
        _CACHE["nc"] = build_program()
    nc = _CACHE["nc"]
    in_maps = prep_inputs(inp)
    res = run_bass_kernel_spmd(nc, in_maps, core_ids=list(range(8)))
    return assemble(res.results)


def assemble(R):
    y_prompt = np.concatenate([R[c]["yp"].reshape(2, 256, 1024) for c in range(8)], axis=0)
    y_sample = np.stack([R[0]["ys"], R[4]["ys"]], axis=0)
    cat = lambda k: np.concatenate([R[c][k] for c in range(8)], axis=0)
    ndk = cat("ndk").reshape(16, 4, 256, 4, 2, 64)
    ndv = cat("ndv").reshape(16, 4, 256, 4, 128)
    ngk = cat("ngk").reshape(16, 4, 256, 2, 64)
    ngv = cat("ngv").reshape(16, 4, 256, 2, 64)
    nssm = cat("nssm")
    nmc = cat("nmc")
    nmn = cat("nmn")
    nmm = cat("nmm").reshape(16, 4, 2, 4)
    return (y_prompt, y_sample, ndk, ndv, ngk, ngv, nssm, nmc, nmn, nmm)
```

```python
import os
import math
from contextlib import ExitStack
import numpy as np
import concourse.bass as bass
import concourse.mybir as mybir
from concourse.bass_utils import run_bass_kernel_spmd

F32 = mybir.dt.float32
BF16 = mybir.dt.bfloat16
ALU = mybir.AluOpType
AF = mybir.ActivationFunctionType
AX = mybir.AxisListType

D_MODEL = 1024
DEPTH = 4
IN_COLS = 5664
D_FF = 2816
EPS = 1e-6
NEG = -30000.0

NL = int(os.environ.get("MK_NL", "4"))
MIXERS = os.environ.get("MK_MIX", "abcd")
SSD_PH = int(os.environ.get("MK_SSD_PH", "9"))
SSD_SUB = int(os.environ.get("MK_SSD_SUB", "9"))


class Buf:
    __slots__ = ("t", "w", "r")

    def __init__(self, t):
        self.t = t
        self.w = None
        self.r = []

    def __getitem__(self, idx):
        return self.t[idx]


class _Rec:
    def __init__(self):
        self.call = None

    def __getattr__(self, name):
        def f(*args, **kw):
            self.call = (name, args, kw)
            return self
        return f


class KB:
    NDMA_SEM = 8

    def __init__(self, nc):
        self.nc = nc
        self.engs = {"pe": nc.tensor, "act": nc.scalar, "dve": nc.vector, "pool": nc.gpsimd, "sp": nc.sync}
        self.sems = {}
        self.cnt = {}
        for k in ("pe", "act", "dve", "pool"):
            self.sems[k] = nc.alloc_semaphore(name="s_" + k)
            self.cnt[k] = 0
        self.dq = {}
        for q in ("sp", "pool", "act"):
            lst = []
            for i in range(self.NDMA_SEM):
                key = "d_%s%d" % (q, i)
                self.sems[key] = nc.alloc_semaphore(name=key)
                self.cnt[key] = 0
                lst.append(key)
            self.dq[q] = [lst, 0]
        self.seen = {e: {} for e in self.engs}
        self.ninst = 0
        self.defer = bool(int(os.environ.get('MK_SCHED', '1')))
        self.pending = []

    def _wait(self, eng, k, v):
        seen = self.seen[eng]
        if seen.get(k, 0) >= v:
            return
        self.engs[eng].wait_ge(self.sems[k], v)
        self.ninst += 1
        seen[k] = v

    def _need(self, eng, reads, writes):
        need = {}

        def add(dep):
            if dep is None:
                return
            k, v = dep
            if need.get(k, 0) < v:
                need[k] = v
        for b in reads:
            add(b.w)
        for b in writes:
            add(b.w)
            for d in b.r:
                add(d)
        for k, v in need.items():
            if k == eng and eng == "pe":
                continue
            self._wait(eng, k, v)

    def _record(self, dep, reads, writes):
        for b in reads:
            b.r.append(dep)
            if len(b.r) > 64:
                mx = {}
                for k, v in b.r:
                    if mx.get(k, 0) < v:
                        mx[k] = v
                b.r = list(mx.items())
        for b in writes:
            b.w = dep
            b.r = []

    def op(self, eng, fn, reads=(), writes=(), inc=True):
        if self.defer:
            rec = _Rec()
            fn(rec)
            self.pending.append(("op", eng, rec.call, tuple(reads), tuple(writes), inc))
            return None
        return self._op_now(eng, fn, reads, writes, inc)

    def _op_now(self, eng, fn, reads=(), writes=(), inc=True):
        self._need(eng, reads, writes)
        ins = fn(self.engs[eng])
        self.ninst += 1
        val = self.cnt[eng] + 1
        if inc:
            ins.then_inc(self.sems[eng], 1)
            self.cnt[eng] = val
        self._record((eng, val), reads, writes)
        return ins

    def dma(self, q, out, in_, reads=(), writes=(), **kw):
        if self.defer:
            self.pending.append(("dma", q, (out, in_, kw), tuple(reads), tuple(writes), True))
            return None
        return self._dma_now(q, out, in_, reads, writes, **kw)

    def _dma_now(self, q, out, in_, reads=(), writes=(), **kw):
        self._need(q, reads, writes)
        lst, i = self.dq[q]
        key = lst[i % len(lst)]
        self.dq[q][1] = i + 1
        if self.cnt[key]:
            self._wait(q, key, self.cnt[key])
        ins = self.engs[q].dma_start(out=out, in_=in_, **kw)
        self.ninst += 1
        self.cnt[key] += 16
        ins.then_inc(self.sems[key], 16)
        dep = (key, self.cnt[key])
        self._record(dep, reads, writes)
        return dep

    @staticmethod
    def _cost(kind, eng, call):
        def fsz(ap):
            n = 1
            for d in ap.shape[1:]:
                n *= d
            return n
        if kind == "dma":
            out = call[0]
            nb = fsz(out) * out.shape[0] * (2 if out.dtype == BF16 else 4)
            return 2000.0 + nb / 80.0
        name, args, kw = call
        if name == "matmul":
            n = fsz(kw["rhs"])
            passes = 4 if kw["lhsT"].dtype == F32 else 1
            return 70.0 + n * passes * 0.45
        out = kw.get("out", None)
        if out is None:
            out = kw.get("ap", args[0] if args else None)
        n = fsz(out) if out is not None else 64
        if eng == "act":
            return 230.0 + n * 0.75
        if name == "reciprocal":
            return 70.0 + n * 6.5
        if name == "memset":
            return 70.0 + n * 0.5
        return 70.0 + n * 1.1

    def flush(self):
        pend = self.pending
        self.pending = []
        if not pend:
            return
        import heapq
        units = []
        cur = None
        for it in pend:
            kind, eng, call, rd, wr, inc = it
            if kind == "op" and eng == "pe":
                if cur is None:
                    cur = [eng, [], 0.0, set(), set()]
                cur[1].append(it)
                cur[2] += self._cost(kind, eng, call)
                cur[3].update(rd)
                cur[4].update(wr)
                if inc:
                    units.append(cur)
                    cur = None
            else:
                assert cur is None, "non-PE op inside an open PE group"
                units.append([eng, [it], self._cost(kind, eng, call), set(rd), set(wr)])
        assert cur is None, "PE group without final inc"
        n = len(units)
        lastw = {}
        readers = {}
        deps = [None] * n
        succ = [[] for _ in range(n)]
        for i, u in enumerate(units):
            d = set()
            for b in u[3]:
                if b in lastw:
                    d.add(lastw[b])
            for b in u[4]:
                if b in lastw:
                    d.add(lastw[b])
                for r in readers.get(b, ()):
                    d.add(r)
            d.discard(i)
            deps[i] = d
            for j in d:
                succ[j].append(i)
            for b in u[3]:
                readers.setdefault(b, []).append(i)
            for b in u[4]:
                lastw[b] = i
                readers[b] = []
        ndep = [len(d) for d in deps]
        ready_t = [0.0] * n
        fin = [0.0] * n
        free = {}
        heaps = {}
        for i in range(n):
            if ndep[i] == 0:
                heapq.heappush(heaps.setdefault(units[i][0], []), (0.0, i))
        order = []
        done = 0
        while done < n:
            best = None
            for e, h in heaps.items():
                if not h:
                    continue
                rt, i = h[0]
                st_ = max(rt, free.get(e, 0.0))
                if best is None or (st_, i) < (best[0], best[1]):
                    best = (st_, i, e)
            st_, i, e = best
            heapq.heappop(heaps[e])
            u = units[i]
            if e in ("sp", "pool") or (e == "act" and u[1][0][0] == "dma"):
                free[e] = st_ + 60.0
                fin[i] = st_ + u[2]
            else:
                fin[i] = st_ + u[2]
                free[e] = fin[i]
            order.append((st_, i))
            done += 1
            for j in succ[i]:
                ndep[j] -= 1
                if fin[i] > ready_t[j]:
                    ready_t[j] = fin[i]
                if ndep[j] == 0:
                    heapq.heappush(heaps.setdefault(units[j][0], []), (ready_t[j], j))
        order.sort()
        for _, i in order:
            for kind, eng, call, rd, wr, inc in units[i][1]:
                if kind == "op":
                    name, args, kw = call
                    self._op_now(eng, lambda en: getattr(en, name)(*args, **kw), rd, wr, inc)
                else:
                    out, in_, kw = call
                    self._dma_now(eng, out, in_, rd, wr, **kw)

    def barrier(self, include_pool_dma=False):
        self.flush()
        keys = ["pe", "act", "dve", "pool"] + self.dq["sp"][0] + self.dq["act"][0]
        if include_pool_dma:
            keys += self.dq["pool"][0]
        for e in ("pe", "act", "dve", "pool", "sp"):
            for k in keys:
                if (k == e and e == "pe") or self.cnt[k] == 0:
                    continue
                self._wait(e, k, self.cnt[k])


def bcast(ap, shape, axis):
    return ap.unsqueeze(axis).broadcast_to(list(shape))


IN_SPECS = [
    ("xp", [512, 1024]), ("xs", [2048, 1024]), ("cvec", [2, 1024]),
    ("cdk", [4, 256, 512]), ("cdv", [4, 256, 512]), ("cgk", [4, 256, 128]), ("cgv", [4, 256, 128]),
    ("sssm", [4, 2, 8, 64, 64]), ("smc", [4, 2, 4, 128, 128]), ("smn", [4, 2, 4, 128]), ("smm", [4, 8]),
    ("w_ada", [4, 1024, 6144]), ("b_ada", [4, 6144]), ("norm1", [4, 1024]), ("norm2", [4, 1024]),
    ("w_in", [4, 1024, IN_COLS]), ("w_out", [4, 2048, 1024]),
    ("conv_ssd_w", [4, 5, 768]), ("conv_ssd_b", [4, 768]), ("ssd_a_log", [4, 16]), ("ssd_dt_bias", [4, 16]),
    ("ssd_d", [4, 8]), ("ssd_norm", [4, 512]),
    ("diff_lq1", [4, 64]), ("diff_lk1", [4, 64]), ("diff_lq2", [4, 64]), ("diff_lk2", [4, 64]),
    ("conv_mlstm_w", [4, 5, 1024]), ("conv_mlstm_b", [4, 1024]), ("mlstm_gate_b", [4, 16]), ("mlstm_norm", [4, 512]),
    ("gqa_q_norm", [4, 64]), ("gqa_k_norm", [4, 64]),
    ("w_ffn_in", [4, 1024, 2 * D_FF]), ("w_ffn_out", [4, D_FF, 1024]), ("norm_f", [1024]),
    ("consts", [128, 1152]), ("rope", [128, 2, 2048]),
]
OUT_SPECS = [
    ("yp", [512, 1024]), ("ys", [2048, 1024]),
    ("ndk", [2, 4, 256, 512]), ("ndv", [2, 4, 256, 512]), ("ngk", [2, 4, 256, 128]), ("ngv", [2, 4, 256, 128]),
    ("nssm", [2, 4, 2, 8, 64, 64]), ("nmc", [2, 4, 2, 4, 128, 128]), ("nmn", [2, 4, 2, 4, 128]), ("nmm", [2, 4, 8]),
]


def make_consts():
    c = np.zeros((128, 1152), np.float32)
    k = np.arange(128)
    c[:, 0:128] = np.eye(128)
    c[:, 128:256] = 1.0
    c[:, 256:384] = (k[:, None] <= k[None, :])
    c[:, 384:512] = (k[:, None] >= k[None, :])
    c[:, 512:640] = np.where(k[:, None] <= k[None, :], 0.0, NEG)
    c[:, 640:768] = np.where(k[:, None] >= k[None, :], 0.0, NEG)
    c[:, 768:896] = (k[:, None] // 64 == k[None, :] // 64)
    rm = np.zeros((128, 128), np.float32)
    for dp in range(128):
        half = (dp % 32) // 16
        if half == 0:
            rm[dp + 16, dp] = -1.0
        else:
            rm[dp - 16, dp] = 1.0
    c[:, 896:1024] = rm
    c[64, 1024:1088] = 1.0
    c[0, 1088:1152] = 1.0
    return c


def make_rope():
    t = np.arange(2048)
    r = (t // 64).astype(np.float32)
    cc = (t % 64).astype(np.float32)
    nf = 16
    freqs = (10000.0 ** (-np.arange(nf, dtype=np.float32) / nf)).astype(np.float32)
    ang = np.stack([r[:, None] * freqs, cc[:, None] * freqs], axis=1).astype(np.float32)
    out = np.zeros((128, 2, 2048), np.float32)
    for p in range(128):
        d = p % 64
        a = d // 32
        f = d % 16
        out[p, 0] = np.cos(ang[:, a, f])
        out[p, 1] = np.sin(ang[:, a, f])
    return out


def build_program():
    nc = bass.Bass("TRN2", target_bir_lowering=False)
    kb = KB(nc)
    D = {}
    for name, shape in IN_SPECS:
        if shape[0] == 4 and len(shape) > 1:
            shape = [NL] + list(shape[1:])
        D[name] = nc.dram_tensor(name, shape, F32, kind="ExternalInput").ap()
    for name, shape in OUT_SPECS:
        D[name] = nc.dram_tensor(name, shape, F32, kind="ExternalOutput").ap()
    mod_d = nc.dram_tensor("mod_scr", [4, 2, 6144], F32, kind="Internal").ap()

    top = ExitStack()

    class scope:
        def __enter__(self_):
            self_.st = ExitStack()
            return self_.st

        def __exit__(self_, *a):
            if a[0] is None:
                kb.barrier()
            self_.st.close()
            return False

    uid = [0]

    def alloc(stack, name, shape, dt, psum=False):
        uid[0] += 1
        name = "%s_%d" % (name, uid[0])
        cm = nc.psum_tensor(name, shape, dt) if psum else nc.sbuf_tensor(name, shape, dt)
        return Buf(stack.enter_context(cm))

    xT = [alloc(top, "xT%d" % i, [128, 8, 512], F32) for i in range(4)]
    hT = [alloc(top, "hT%d" % i, [128, 8, 512], BF16) for i in range(4)]
    NW = 2
    wbufs = [alloc(top, "wb%d" % i, [128, 4096], BF16) for i in range(NW)]
    wstate = [0]
    psb = [alloc(top, "ps%d" % i, [128, 512], F32, psum=True) for i in range(8)]
    pstate = [0]
    cf = alloc(top, "cf", [128, 640], F32)
    cb = alloc(top, "cb", [128, 1024], BF16)
    modc = alloc(top, "modc", [128, 6, 8], F32)
    nrm = alloc(top, "nrm", [128, 2, 8], F32)
    AB = alloc(top, "AB", [128, 4, 8], F32)
    nfc = alloc(top, "nfc", [128, 8], F32)
    lnsb = alloc(top, "lnsb", [128, 1], F32)
    lns_col = lnsb.t

    wo_buf = alloc(top, "wo_buf", [128, 4, 1024], BF16)
    accstate = [0]

    def ps():
        b = psb[pstate[0] % 4]
        pstate[0] += 1
        return b

    def ps_acc():
        b = psb[4 + accstate[0] % 4]
        accstate[0] += 1
        return b

    def wload(pieces, kch):
        b = wbufs[wstate[0] % NW]
        wstate[0] += 1
        ntot = sum(n for _, n in pieces)
        assert kch * ntot <= 4096, (kch, ntot)
        view = b.t[:, 0:kch * ntot].rearrange("p (k n) -> p k n", k=kch)
        o = 0
        for ap, n in pieces:
            kb.dma("pool", view[:, :, o:o + n], ap.rearrange("(k p) n -> p k n", p=128), writes=[b])
            o += n
        return b, view

    ident_f = cf.t[:, 0:128]
    ones_f = cf.t[:, 128:256]
    ident_b = cb.t[:, 0:128]
    selm_f = cf.t[:, 512:640]
    ones_b = cb.t[:, 128:256]

    kb.dma("sp", cf[:, 0:512], D["consts"][:, 0:512], writes=[cf])
    kb.dma("sp", cf[:, 512:640], D["consts"][:, 1024:1152], writes=[cf])
    kb.dma("pool", cb[:], D["consts"][:, 0:1024], writes=[cb])
    kb.op("dve", lambda e: e.memset(lnsb[:], math.log(128 ** -0.5)), writes=[lnsb])
    kb.dma("sp", nfc[:], D["norm_f"].rearrange("(c p) -> p c", p=128), writes=[nfc], allow_slow_non_contiguous=True)

    modall = alloc(top, "modall", [128, NL, 48, 2], F32)
    ball = alloc(top, "ball", [128, NL, 48], F32)
    nrmall = alloc(top, "nrmall", [128, NL, 2, 8], F32)
    for l in range(NL):
        kb.dma("sp", ball[:, l, :], D["b_ada"][l].rearrange("(j p) -> p j", p=128), writes=[ball], allow_slow_non_contiguous=True)
        kb.dma("sp", nrmall[:, l, 0, :], D["norm1"][l].rearrange("(c p) -> p c", p=128), writes=[nrmall], allow_slow_non_contiguous=True)
        kb.dma("sp", nrmall[:, l, 1, :], D["norm2"][l].rearrange("(c p) -> p c", p=128), writes=[nrmall], allow_slow_non_contiguous=True)
    with scope() as st:
        cT = alloc(st, "cT", [128, 2, 8], F32)
        cTb = alloc(st, "cTb", [128, 2, 8], BF16)
        sig = alloc(st, "csig", [128, 2, 8], F32)
        for g in range(2):
            kb.dma("sp", cT[:, g, :], D["cvec"][g].rearrange("(c p) -> p c", p=128), writes=[cT], allow_slow_non_contiguous=True)
        kb.op("act", lambda e: e.activation(out=sig[:], in_=cT[:], func=AF.Sigmoid), reads=[cT], writes=[sig])
        kb.op("dve", lambda e: e.tensor_tensor(out=cTb[:], in0=cT[:], in1=sig[:], op=ALU.mult), reads=[cT, sig], writes=[cTb])
        for l in range(NL):
            for blk in range(12):
                c0 = blk * 512
                wb, wv = wload([(D["w_ada"][l][:, c0:c0 + 512], 512)], 8)
                p = ps()
                for cc in range(4):
                    for k in range(8):
                        kb.op("pe", lambda e: e.matmul(p[:, 2 * cc:2 * cc + 2], lhsT=wv[:, k, cc * 128:(cc + 1) * 128], rhs=cTb[:, :, k], start=(k == 0), stop=(k == 7)),
                              reads=[cTb, wb], writes=[p], inc=(k == 7 and cc == 3))
                kb.op("dve", lambda e: e.tensor_tensor(out=modall[:, l, blk * 4:(blk + 1) * 4, :], in0=p[:, 0:8].rearrange("p (j g) -> p j g", g=2),
                                                       in1=bcast(ball[:, l, blk * 4:(blk + 1) * 4], [128, 4, 2], 2), op=ALU.add), reads=[p, ball], writes=[modall])
        kb.barrier()

    def load_x(src, nblk):
        with scope() as st:
            xin = [alloc(st, "xin%d" % i, [128, 1024], F32) for i in range(2)]
            for tb in range(nblk):
                tiles = []
                for tl in range(4):
                    pass
                for tl in range(4):
                    xi = xin[(tb * 4 + tl) % 2]
                    t0 = (tb * 4 + tl) * 128
                    kb.dma("sp", xi[:], src[t0:t0 + 128, :], writes=[xi])
                    for half in range(2):
                        p = ps()
                        for cc in range(4):
                            c = half * 4 + cc
                            kb.op("pe", lambda e: e.matmul(p[:, cc * 128:(cc + 1) * 128], lhsT=xi[:, c * 128:(c + 1) * 128], rhs=ident_f,
                                                           start=True, stop=True), reads=[xi, cf], writes=[p], inc=(cc == 3))
                        kb.op("act", lambda e: e.activation(
                            out=xT[tb][:, half * 4:half * 4 + 4, tl * 128:(tl + 1) * 128],
                            in_=p[:, :].rearrange("p (c t) -> p c t", c=4), func=AF.Copy), reads=[p], writes=[xT[tb]])

    def norm_block(st_tmp, tb, Acol, Bcol, dst, dst_dt_is_bf16=True):
        sq, rstd, tmp2 = st_tmp
        if isinstance(sq, list):
            sq, rstd = sq[tb % 2], rstd[tb % 2]
        kb.op("act", lambda e: e.activation(out=sq[:], in_=xT[tb][:], func=AF.Square), reads=[xT[tb]], writes=[sq])
        p = ps()
        for c in range(8):
            kb.op("pe", lambda e: e.matmul(p[:, :], lhsT=ones_b, rhs=sq[:, c, :], start=(c == 0), stop=(c == 7)),
                  reads=[sq, cb], writes=[p], inc=(c == 7))
        kb.op("dve", lambda e: e.tensor_scalar(out=rstd[:], in0=p[:, :], scalar1=1.0 / D_MODEL, scalar2=EPS, op0=ALU.mult, op1=ALU.add),
              reads=[p], writes=[rstd])
        kb.op("act", lambda e: e.activation(out=rstd[:], in_=rstd[:], func=AF.Sqrt), reads=[rstd], writes=[rstd])
        kb.op("dve", lambda e: e.reciprocal(out=rstd[:], in_=rstd[:]), reads=[rstd], writes=[rstd])
        for c in range(8):
            t2 = tmp2[c % len(tmp2)]
            kb.op("dve", lambda e: e.tensor_tensor(out=t2[:], in0=xT[tb][:, c, :], in1=rstd[:], op=ALU.mult),
                  reads=[xT[tb], rstd], writes=[t2])
            if Bcol is not None:
                kb.op("act", lambda e: e.activation(out=dst[:, c, :], in_=t2[:], func=AF.Identity, bias=Bcol[:, c:c + 1], scale=Acol[:, c:c + 1]),
                      reads=[t2, AB], writes=[dst])
            else:
                kb.op("act", lambda e: e.activation(out=dst[:, c, :], in_=t2[:], func=AF.Identity, scale=Acol[:, c:c + 1]),
                      reads=[t2, nfc], writes=[dst])

    def load_mod(l, g):
        kb.op("dve", lambda e: e.tensor_copy(out=modc[:], in_=modall[:, l, :, g].rearrange("p (v c) -> p v c", v=6)), reads=[modall], writes=[modc])
        kb.op("dve", lambda e: e.tensor_copy(out=nrm[:], in_=nrmall[:, l, :, :]), reads=[nrmall], writes=[nrm])
        for j, (vs, vh) in enumerate(((1, 0), (4, 3))):
            kb.op("dve", lambda e: e.scalar_tensor_tensor(out=AB[:, 2 * j, :], in0=modc[:, vs, :], scalar=1.0, in1=nrm[:, j, :],
                                                          op0=ALU.add, op1=ALU.mult), reads=[modc, nrm], writes=[AB])
            kb.op("dve", lambda e: e.tensor_copy(out=AB[:, 2 * j + 1, :], in_=modc[:, vh, :]), reads=[modc], writes=[AB])

    def ffn(l, blocks):
        nb = len(blocks)
        with scope() as st:
            actT = alloc(st, "actT", [128, 22, nb * 512], BF16)
            sg = [alloc(st, "sg%d" % i, [128, 512], F32) for i in range(2)]
            it = 0
            for jj in range(11):
                c0 = jj * 256
                wb, wv = wload([(D["w_ffn_in"][l][:, c0:c0 + 256], 256), (D["w_ffn_in"][l][:, D_FF + c0:D_FF + c0 + 256], 256)], 8)
                for j2 in range(2):
                    j = jj * 2 + j2
                    for bi, tb in enumerate(blocks):
                        pg = ps()
                        pu = ps()
                        for k in range(8):
                            kb.op("pe", lambda e: e.matmul(pg[:, :], lhsT=wv[:, k, j2 * 128:(j2 + 1) * 128], rhs=hT[tb][:, k, :],
                                                           start=(k == 0), stop=(k == 7)), reads=[wb, hT[tb]], writes=[pg], inc=(k == 7))
                        for k in range(8):
                            kb.op("pe", lambda e: e.matmul(pu[:, :], lhsT=wv[:, k, 256 + j2 * 128:256 + (j2 + 1) * 128], rhs=hT[tb][:, k, :],
                                                           start=(k == 0), stop=(k == 7)), reads=[wb, hT[tb]], writes=[pu], inc=(k == 7))
                        s = sg[it % 2]
                        it += 1
                        kb.op("act", lambda e: e.activation(out=s[:], in_=pg[:, :], func=AF.Silu), reads=[pg], writes=[s])
                        kb.op("dve", lambda e: e.tensor_tensor(out=actT[:, j, bi * 512:(bi + 1) * 512], in0=s[:], in1=pu[:, :], op=ALU.mult),
                              reads=[s, pu], writes=[actT])
            for c in range(8):
                wb, wv = wload([(D["w_ffn_out"][l][:, c * 128:(c + 1) * 128], 128)], 22)
                for bi, tb in enumerate(blocks):
                    p = ps()
                    for k in range(22):
                        kb.op("pe", lambda e: e.matmul(p[:, :], lhsT=wv[:, k, :], rhs=actT[:, k, bi * 512:(bi + 1) * 512],
                                                       start=(k == 0), stop=(k == 21)), reads=[wb, actT], writes=[p], inc=(k == 21))
                    kb.op("dve", lambda e: e.scalar_tensor_tensor(out=xT[tb][:, c, :], in0=p[:, :], scalar=modc[:, 5, c:c + 1], in1=xT[tb][:, c, :],
                                                                  op0=ALU.mult, op1=ALU.add), reads=[p, modc, xT[tb]], writes=[xT[tb]])
        kb.barrier()

    def final_out(nblk, dst):
        with scope() as st:
            sq = alloc(st, "sq", [128, 8, 512], BF16)
            rstd = alloc(st, "rstd", [128, 512], F32)
            tmp2 = [alloc(st, "tmpn%d" % i, [128, 512], F32) for i in range(2)]
            xn = alloc(st, "xn", [128, 8, 512], F32)
            ot = [alloc(st, "ot%d" % i, [128, 1024], F32) for i in range(2)]
            for tb in range(nblk):
                norm_block((sq, rstd, tmp2), tb, nfc, None, xn)
                for tl in range(4):
                    o = ot[tl % 2]
                    for half in range(2):
                        p = ps()
                        for cc in range(4):
                            c = half * 4 + cc
                            kb.op("pe", lambda e: e.matmul(p[:, cc * 128:(cc + 1) * 128], lhsT=xn[:, c, tl * 128:(tl + 1) * 128], rhs=ident_f,
                                                           start=True, stop=True), reads=[xn, cf], writes=[p], inc=(cc == 3))
                        kb.op("act", lambda e: e.activation(out=o[:, half * 512:(half + 1) * 512], in_=p[:, :], func=AF.Copy), reads=[p], writes=[o])
                    t0 = (tb * 4 + tl) * 128
                    kb.dma("sp", dst[t0:t0 + 128, :], o[:], reads=[o])
        kb.barrier()


    bd64_b = cb.t[:, 768:896]
    rm_b = cb.t[:, 896:1024]

    def proj_tm(wb, wv, s0, n, tb, tl, p):
        for k in range(8):
            kb.op("pe", lambda e: e.matmul(p[:, 0:n], lhsT=hT[tb][:, k, tl * 128:(tl + 1) * 128], rhs=wv[:, k, s0:s0 + n],
                                           start=(k == 0), stop=(k == 7)), reads=[wb, hT[tb]], writes=[p], inc=(k == 7))

    def load_wo(l, row0):
        kb.dma("pool", wo_buf[:], D["w_out"][l][row0:row0 + 512, :].rearrange("(k p) n -> p k n", p=128), writes=[wo_buf])

    def mixer_out(st, ytm, tb, yT):
        for c in range(4):
            p = ps()
            for tl in range(4):
                kb.op("pe", lambda e: e.matmul(p[:, tl * 128:(tl + 1) * 128], lhsT=ytm[:, tl, c * 128:(c + 1) * 128], rhs=ident_b,
                                               start=True, stop=True), reads=[ytm, cb], writes=[p], inc=(tl == 3))
            kb.op("act", lambda e: e.activation(out=yT[:, c, :], in_=p[:, :], func=AF.Copy), reads=[p], writes=[yT])
        for c in range(8):
            p = ps()
            for k in range(4):
                kb.op("pe", lambda e: e.matmul(p[:, :], lhsT=wo_buf[:, k, c * 128:(c + 1) * 128], rhs=yT[:, k, :],
                                               start=(k == 0), stop=(k == 3)), reads=[wo_buf, yT], writes=[p], inc=(k == 3))
            kb.op("dve", lambda e: e.scalar_tensor_tensor(out=xT[tb][:, c, :], in0=p[:, :], scalar=modc[:, 2, c:c + 1], in1=xT[tb][:, c, :],
                                                          op0=ALU.mult, op1=ALU.add), reads=[p, modc, xT[tb]], writes=[xT[tb]])

    def attention(l, g, kind, nblk):
        sample = (g == 1)
        L = 2048 if sample else 256
        nseq = 1 if sample else 2
        nctx = 2 if sample else 0
        lt = L // 128
        nkt = lt + nctx
        ntok = nblk * 512
        if kind == "d":
            qc0, kc0, vc0, nkc, nvh, ve, orow0, nheads = 4896, 5408, 5536, 1, 2, 64, 1536, 8
            ck, cv, ok, ov = D["cgk"], D["cgv"], D["ngk"], D["ngv"]
        else:
            qc0, kc0, vc0, nkc, nvh, ve, orow0, nheads = 1296, 1808, 2320, 4, 4, 128, 512, 4
            ck, cv, ok, ov = D["cdk"], D["cdv"], D["ndk"], D["ndv"]
        scale = 64 ** -0.5
        kw = nkc * 128
        vw = nvh * ve
        lam_init = 0.8 - 0.6 * math.exp(-0.3 * l)
        with scope() as st:
            qT = alloc(st, "qT", [128, 4, ntok], BF16)
            kT = alloc(st, "kT", [128, nkc, nseq * nkt * 128], BF16)
            vsw = ve + 1 if kind == "d" else ve
            vaug = alloc(st, "vaug", [128, nseq * nkt, nvh, vsw], BF16)
            vodd = alloc(st, "vodd", [128, nseq * nkt, nvh, 128], BF16) if kind == "d" else None
            yT = alloc(st, "yT", [128, 4, 512], BF16)
            pTs = [alloc(st, "pT%d" % i, [128, 512], BF16) for i in range(3)]
            sqb = alloc(st, "sqb", [128, 512], BF16)
            rs = alloc(st, "rs", [128, 512], F32)
            qn = alloc(st, "qn", [128, 512], BF16)
            t1 = alloc(st, "t1", [128, 512], F32)
            fin_bufs = [(rs, t1, None, sqb)]
            gcol = alloc(st, "gcol", [128, 2], F32)
            osb = alloc(st, "osb", [128, 512], F32)
            t2 = osb
            fin_bufs[0] = (rs, t1, osb, sqb)
            sm = alloc(st, "sm", [128, 16], F32)
            lamt = alloc(st, "lamt", [128, 4, 64], F32)
            kng = alloc(st, "kng", [128, 64], F32)
            ropeT = alloc(st, "ropeT", [128, 2, 2048], BF16) if sample else None
            if sample and kind == "b":
                pass
            else:
                try:
                    fin_bufs.append((alloc(st, "rs2", [128, 512], F32), alloc(st, "t12", [128, 512], F32),
                                     alloc(st, "osb2", [128, 512], F32) if kind == "b" else None, alloc(st, "sqb2", [128, 512], BF16) if kind == "b" else None))
                except AssertionError:
                    pass
            load_wo(l, orow0)
            if kind == "d":
                kb.op("dve", lambda e: e.memset(vaug[:, :, :, ve:ve + 1], 1.0), writes=[vaug])
                kb.op("dve", lambda e: e.memset(vodd[:, :, :, 0:1], 1.0), writes=[vodd])
                kb.op("dve", lambda e: e.memset(vodd[:, :, :, 1:64], 0.0), writes=[vodd])
            if sample:
                kb.dma("pool", ropeT[:], D["rope"], writes=[ropeT])
            if kind == "d":
                for j, nm in enumerate(("gqa_q_norm", "gqa_k_norm")):
                    for hh in range(2):
                        kb.dma("sp", gcol[hh * 64:(hh + 1) * 64, j:j + 1], D[nm][l].rearrange("(d o) -> d o", o=1), writes=[gcol])
                kb.dma("sp", kng[:], D["gqa_k_norm"][l].partition_broadcast(128), writes=[kng])
            else:
                for j, nm in enumerate(("diff_lq1", "diff_lk1", "diff_lq2", "diff_lk2")):
                    kb.dma("sp", lamt[:, j, :], D[nm][l].partition_broadcast(128), writes=[lamt])
                kb.op("dve", lambda e: e.tensor_tensor(out=lamt[:, 0, :], in0=lamt[:, 0, :], in1=lamt[:, 1, :], op=ALU.mult), reads=[lamt], writes=[lamt])
                kb.op("dve", lambda e: e.tensor_tensor(out=lamt[:, 2, :], in0=lamt[:, 2, :], in1=lamt[:, 3, :], op=ALU.mult), reads=[lamt], writes=[lamt])
                kb.op("dve", lambda e: e.tensor_reduce(out=sm[:, 2:3], in_=lamt[:, 0, :], axis=AX.X, op=ALU.add), reads=[lamt], writes=[sm])
                kb.op("dve", lambda e: e.tensor_reduce(out=sm[:, 3:4], in_=lamt[:, 2, :], axis=AX.X, op=ALU.add), reads=[lamt], writes=[sm])
                kb.op("act", lambda e: e.activation(out=sm[:, 2:4], in_=sm[:, 2:4], func=AF.Exp), reads=[sm], writes=[sm])
                kb.op("dve", lambda e: e.tensor_tensor(out=sm[:, 0:1], in0=sm[:, 2:3], in1=sm[:, 3:4], op=ALU.subtract), reads=[sm], writes=[sm])
                kb.op("dve", lambda e: e.tensor_scalar(out=sm[:, 1:2], in0=sm[:, 0:1], scalar1=lam_init, scalar2=-1.0, op0=ALU.add, op1=ALU.mult), reads=[sm], writes=[sm])

            def qk_post(p, dst, tb, normj):
                src = p
                if kind == "d":
                    kb.op("act", lambda e: e.activation(out=sqb[:], in_=p[:, :], func=AF.Square), reads=[p], writes=[sqb])
                    pn = ps()
                    kb.op("pe", lambda e: e.matmul(pn[:, :], lhsT=bd64_b, rhs=sqb[:], start=True, stop=True), reads=[cb, sqb], writes=[pn])
                    kb.op("dve", lambda e: e.tensor_scalar(out=rs[:], in0=pn[:, :], scalar1=1.0 / 64, scalar2=EPS, op0=ALU.mult, op1=ALU.add), reads=[pn], writes=[rs])
                    kb.op("act", lambda e: e.activation(out=rs[:], in_=rs[:], func=AF.Sqrt), reads=[rs], writes=[rs])
                    kb.op("dve", lambda e: e.reciprocal(out=rs[:], in_=rs[:]), reads=[rs], writes=[rs])
                    tgt = qn if sample else None
                    o_ap = qn[:] if sample else dst
                    kb.op("dve", lambda e: e.scalar_tensor_tensor(out=o_ap, in0=p[:, :], scalar=gcol[:, normj:normj + 1], in1=rs[:], op0=ALU.mult, op1=ALU.mult),
                          reads=[p, gcol, rs], writes=[qn if sample else dst_buf[0]])
                else:
                    o_ap = qn[:] if sample else dst
                    kb.op("act", lambda e: e.activation(out=o_ap, in_=p[:, :], func=AF.Copy), reads=[p], writes=[qn if sample else dst_buf[0]])
                if sample:
                    pr = ps()
                    kb.op("pe", lambda e: e.matmul(pr[:, :], lhsT=rm_b, rhs=qn[:], start=True, stop=True), reads=[cb, qn], writes=[pr])
                    kb.op("dve", lambda e: e.tensor_tensor(out=t1[:], in0=qn[:], in1=ropeT[:, 0, tb * 512:(tb + 1) * 512], op=ALU.mult), reads=[qn, ropeT], writes=[t1])
                    kb.op("dve", lambda e: e.tensor_tensor(out=t2[:], in0=pr[:, :], in1=ropeT[:, 1, tb * 512:(tb + 1) * 512], op=ALU.mult), reads=[pr, ropeT], writes=[t2])
                    kb.op("dve", lambda e: e.tensor_tensor(out=dst, in0=t1[:], in1=t2[:], op=ALU.add), reads=[t1, t2], writes=[dst_buf[0]])

            dst_buf = [None]
            blocks = list(range(nblk))
            if kind == "d":
                pcs = []
                for j in range(4):
                    for hh in (j, 4 + j):
                        pcs.append((D["w_in"][l][:, qc0 + hh * 64:qc0 + (hh + 1) * 64], 64))
                wb, wv = wload(pcs, 8)
            else:
                wb, wv = wload([(D["w_in"][l][:, qc0:qc0 + 512], 512)], 8)
            dst_buf[0] = qT
            for j in range(4):
                for tb in blocks:
                    p = ps()
                    for k in range(8):
                        lh = wv[:, k, j * 128:(j + 1) * 128]
                        kb.op("pe", lambda e: e.matmul(p[:, :], lhsT=lh, rhs=hT[tb][:, k, :], start=(k == 0), stop=(k == 7)),
                              reads=[wb, hT[tb]], writes=[p], inc=(k == 7))
                    qk_post(p, qT[:, j, tb * 512:(tb + 1) * 512], tb, 0)
            wb, wv = wload([(D["w_in"][l][:, kc0:kc0 + kw], kw)], 8)
            dst_buf[0] = kT
            for j in range(nkc):
                for tb in blocks:
                    p = ps()
                    for k in range(8):
                        kb.op("pe", lambda e: e.matmul(p[:, :], lhsT=wv[:, k, j * 128:(j + 1) * 128], rhs=hT[tb][:, k, :], start=(k == 0), stop=(k == 7)),
                              reads=[wb, hT[tb]], writes=[p], inc=(k == 7))
                    qk_post(p, kT[:, j, nctx * 128 + tb * 512:nctx * 128 + (tb + 1) * 512], tb, 1)
            if not sample:
                for tt in range(ntok // 128):
                    sq_, r0 = divmod(tt, lt)
                    p = ps()
                    proj_tm(wb, wv, 0, kw, tt // 4, tt % 4, p)
                    kb.op("act", lambda e: e.activation(out=osb[:, 0:kw], in_=p[:, 0:kw], func=AF.Copy), reads=[p], writes=[osb])
                    if kind == "d":
                        kb.op("dve", lambda e: e.tensor_tensor(out=t1[:, 0:128], in0=osb[:, 0:128], in1=osb[:, 0:128], op=ALU.mult), reads=[osb], writes=[t1])
                        kb.op("dve", lambda e: e.tensor_reduce(out=sm[:, 8:10], in_=t1[:, 0:128].rearrange("p (h d) -> p h d", d=64), axis=AX.X, op=ALU.add), reads=[t1], writes=[sm])
                        kb.op("dve", lambda e: e.tensor_scalar(out=sm[:, 8:10], in0=sm[:, 8:10], scalar1=1.0 / 64, scalar2=EPS, op0=ALU.mult, op1=ALU.add), reads=[sm], writes=[sm])
                        kb.op("act", lambda e: e.activation(out=sm[:, 8:10], in_=sm[:, 8:10], func=AF.Sqrt), reads=[sm], writes=[sm])
                        kb.op("dve", lambda e: e.reciprocal(out=sm[:, 8:10], in_=sm[:, 8:10]), reads=[sm], writes=[sm])
                        kb.op("dve", lambda e: e.tensor_tensor(out=t1[:, 0:128].rearrange("p (h d) -> p h d", d=64), in0=osb[:, 0:128].rearrange("p (h d) -> p h d", d=64),
                                                               in1=bcast(sm[:, 8:10], [128, 2, 64], 2), op=ALU.mult), reads=[osb, sm], writes=[t1])
                        kb.op("dve", lambda e: e.tensor_tensor(out=t2[:, 0:128].rearrange("p (h d) -> p h d", d=64), in0=t1[:, 0:128].rearrange("p (h d) -> p h d", d=64),
                                                               in1=bcast(kng[:], [128, 2, 64], 1), op=ALU.mult), reads=[t1, kng], writes=[t2])
                        kb.dma("sp", ok[sq_, l, r0 * 128:(r0 + 1) * 128, :], t2[:, 0:128], reads=[t2])
                    else:
                        kb.dma("sp", ok[sq_, l, r0 * 128:(r0 + 1) * 128, :], osb[:, 0:kw], reads=[osb])
            wb, wv = wload([(D["w_in"][l][:, vc0:vc0 + vw], vw)], 8)
            for tt in range(ntok // 128):
                sq_, r0 = divmod(tt, lt)
                ktile = sq_ * nkt + nctx + r0
                p = ps()
                proj_tm(wb, wv, 0, vw, tt // 4, tt % 4, p)
                kb.op("act", lambda e: e.activation(out=vaug[:, ktile, :, 0:ve], in_=p[:, 0:vw].rearrange("p (h e) -> p h e", e=ve), func=AF.Copy), reads=[p], writes=[vaug])
                if kind == "d":
                    kb.op("act", lambda e: e.activation(out=vodd[:, ktile, :, 64:128], in_=p[:, 0:vw].rearrange("p (h e) -> p h e", e=ve), func=AF.Copy), reads=[p], writes=[vodd])
                if not sample:
                    kb.op("dve", lambda e: e.tensor_copy(out=osb[:, 0:vw], in_=p[:, 0:vw]), reads=[p], writes=[osb])
                    kb.dma("sp", ov[sq_, l, r0 * 128:(r0 + 1) * 128, :], osb[:, 0:vw], reads=[osb])
            if sample:
                with scope() as st2:
                    ctxk = alloc(st2, "ctxk", [128, kw], F32)
                    for t in range(2):
                        kb.dma("sp", ctxk[:], ck[l][t * 128:(t + 1) * 128, :], writes=[ctxk])
                        for j in range(nkc):
                            p = ps()
                            kb.op("pe", lambda e: e.matmul(p[:, 0:128], lhsT=ctxk[:, j * 128:(j + 1) * 128], rhs=ident_f, start=True, stop=True),
                                  reads=[ctxk, cf], writes=[p])
                            kb.op("act", lambda e: e.activation(out=kT[:, j, t * 128:(t + 1) * 128], in_=p[:, 0:128], func=AF.Copy), reads=[p], writes=[kT])
                    for t in range(2):
                        kb.dma("pool", vaug[:, t, :, 0:ve], cv[l][t * 128:(t + 1) * 128, :].rearrange("p (h e) -> p h e", e=ve), writes=[vaug])
                        if kind == "d":
                            kb.dma("pool", vodd[:, t, :, 64:128], cv[l][t * 128:(t + 1) * 128, :].rearrange("p (h e) -> p h e", e=ve), writes=[vodd])
                    kb.barrier(include_pool_dma=True)
            qblk = min(L, 512)
            pti = 0
            for s_ in range(nseq):
                for qb in range(L // qblk):
                    q0 = s_ * L + qb * qblk
                    qs = slice(q0, q0 + qblk)
                    yc = slice(q0 % 512, q0 % 512 + qblk)
                    its = []
                    for h in range(nheads):
                        for kt in range(nkt):
                            if kind == "d":
                                its.append((h, kt, 0))
                            else:
                                its.append((h, kt, 0))
                                its.append((h, kt, 1))
                    hstate = {}
                    pts = {}
                    DEPTH = 2

                    def stageA(i):
                        h, kt, r = its[i]
                        kc = (s_ * nkt + kt) * 128
                        if kind == "d":
                            rows, qch = (h // 4) * 64, h % 4
                            lhs = kT[rows:rows + 64, 0, kc:kc + 128]
                            rh = qT[rows:rows + 64, qch, qs]
                        else:
                            lhs = kT[r * 64:(r + 1) * 64, h, kc:kc + 128]
                            rh = qT[r * 64:(r + 1) * 64, h, qs]
                        pS = ps()
                        kb.op("pe", lambda e: e.matmul(pS[:, 0:qblk], lhsT=lhs, rhs=rh, start=True, stop=True), reads=[kT, qT], writes=[pS])
                        pT = pTs[i % 3]
                        kb.op("act", lambda e: e.activation(out=pT[:, 0:qblk], in_=pS[:, 0:qblk], func=AF.Exp, scale=scale), reads=[pS], writes=[pT])
                        pts[i] = pT

                    def stageC(i):
                        h, kt, r = its[i]
                        pT = pts.pop(i)
                        last = (kt == nkt - 1)
                        rs, t1, osb, sqb = fin_bufs[h % len(fin_bufs)]
                        if kind == "d":
                            vh, odd = h // 4, h % 2
                            if kt == 0:
                                hstate[h] = ps_acc()
                            accO = hstate[h]
                            mo = 128 if odd else ve + 1
                            lh = vodd[:, s_ * nkt + kt, vh, :] if odd else vaug[:, s_ * nkt + kt, vh, :]
                            kb.op("pe", lambda e: e.matmul(accO[0:mo, 0:qblk], lhsT=lh, rhs=pT[:, 0:qblk], start=(kt == 0), stop=last),
                                  reads=[pT, vodd if odd else vaug], writes=[accO], inc=True)
                            if last:
                                drow = 0 if odd else 64
                                orow = 64 if odd else 0
                                kb.op("act", lambda e: e.activation(out=rs[drow:drow + 1, 0:qblk], in_=accO[drow:drow + 1, 0:qblk], func=AF.Copy), reads=[accO], writes=[rs])
                                kb.op("dve", lambda e: e.reciprocal(out=rs[drow:drow + 1, 0:qblk], in_=rs[drow:drow + 1, 0:qblk]), reads=[rs], writes=[rs])
                                pB = ps()
                                kb.op("pe", lambda e: e.matmul(pB[:, 0:qblk], lhsT=selm_f[drow:drow + 1, :], rhs=rs[drow:drow + 1, 0:qblk], start=True, stop=True),
                                      reads=[cf, rs], writes=[pB])
                                kb.op("act", lambda e: e.activation(out=t1[orow:orow + 64, 0:qblk], in_=pB[orow:orow + 64, 0:qblk], func=AF.Copy), reads=[pB], writes=[t1])
                                kb.op("dve", lambda e: e.tensor_tensor(out=yT[orow:orow + 64, h // 2, yc], in0=accO[orow:orow + 64, 0:qblk], in1=t1[orow:orow + 64, 0:qblk], op=ALU.mult),
                                      reads=[accO, t1], writes=[yT])
                        else:
                            if kt == 0 and r == 0:
                                hstate[h] = ([ps_acc(), ps_acc()], [ps_acc(), ps_acc()])
                            accO, accD = hstate[h]
                            kb.op("pe", lambda e: e.matmul(accO[r][:, 0:qblk], lhsT=vaug[:, s_ * nkt + kt, h, :], rhs=pT[:, 0:qblk], start=(kt == 0), stop=last),
                                  reads=[pT, vaug], writes=[accO[r]], inc=False)
                            kb.op("pe", lambda e: e.matmul(accD[r][:, 0:qblk], lhsT=ones_b, rhs=pT[:, 0:qblk], start=(kt == 0), stop=last),
                                  reads=[pT, cb], writes=[accD[r]], inc=True)
                            if last and r == 1:
                                A, Bt, O = rs, t1, osb
                                kb.op("dve", lambda e: e.reciprocal(out=A[:, 0:qblk], in_=accD[0][:, 0:qblk]), reads=[accD[0]], writes=[A])
                                kb.op("dve", lambda e: e.reciprocal(out=Bt[:, 0:qblk], in_=accD[1][:, 0:qblk]), reads=[accD[1]], writes=[Bt])
                                kb.op("dve", lambda e: e.tensor_tensor(out=O[:, 0:qblk], in0=accO[0][:, 0:qblk], in1=A[:, 0:qblk], op=ALU.mult), reads=[accO[0], A], writes=[O])
                                kb.op("dve", lambda e: e.tensor_tensor(out=Bt[:, 0:qblk], in0=accO[1][:, 0:qblk], in1=Bt[:, 0:qblk], op=ALU.mult), reads=[accO[1], Bt], writes=[Bt])
                                kb.op("dve", lambda e: e.scalar_tensor_tensor(out=O[:, 0:qblk], in0=Bt[:, 0:qblk], scalar=sm[:, 1:2], in1=O[:, 0:qblk], op0=ALU.mult, op1=ALU.add),
                                      reads=[Bt, sm, O], writes=[O])
                                kb.op("act", lambda e: e.activation(out=sqb[:, 0:qblk], in_=O[:, 0:qblk], func=AF.Square), reads=[O], writes=[sqb])
                                pn = ps()
                                kb.op("pe", lambda e: e.matmul(pn[:, 0:qblk], lhsT=ones_b, rhs=sqb[:, 0:qblk], start=True, stop=True), reads=[cb, sqb], writes=[pn])
                                kb.op("dve", lambda e: e.tensor_scalar(out=A[:, 0:qblk], in0=pn[:, 0:qblk], scalar1=1.0 / 128, scalar2=EPS, op0=ALU.mult, op1=ALU.add), reads=[pn], writes=[A])
                                kb.op("act", lambda e: e.activation(out=A[:, 0:qblk], in_=A[:, 0:qblk], func=AF.Sqrt), reads=[A], writes=[A])
                                kb.op("dve", lambda e: e.reciprocal(out=A[:, 0:qblk], in_=A[:, 0:qblk]), reads=[A], writes=[A])
                                kb.op("dve", lambda e: e.scalar_tensor_tensor(out=yT[:, h, yc], in0=O[:, 0:qblk], scalar=1.0 - lam_init, in1=A[:, 0:qblk], op0=ALU.mult, op1=ALU.mult),
                                      reads=[O, A], writes=[yT])

                    n_it = len(its)
                    for i in range(n_it + DEPTH):
                        if i < n_it:
                            stageA(i)
                        if i >= DEPTH:
                            stageC(i - DEPTH)
                    if (q0 + qblk) % 512 == 0:
                        tb = (q0 + qblk) // 512 - 1
                        for c in range(8):
                            p = ps()
                            for k in range(4):
                                kb.op("pe", lambda e: e.matmul(p[:, :], lhsT=wo_buf[:, k, c * 128:(c + 1) * 128], rhs=yT[:, k, :],
                                                               start=(k == 0), stop=(k == 3)), reads=[wo_buf, yT], writes=[p], inc=(k == 3))
                            kb.op("dve", lambda e: e.scalar_tensor_tensor(out=xT[tb][:, c, :], in0=p[:, :], scalar=modc[:, 2, c:c + 1], in1=xT[tb][:, c, :],
                                                                          op0=ALU.mult, op1=ALU.add), reads=[p, modc, xT[tb]], writes=[xT[tb]])
        kb.barrier()

    triF_f = cf.t[:, 256:384]
    triB_f = cf.t[:, 384:512]
    maskF_b = cb.t[:, 512:640]
    maskB_b = cb.t[:, 640:768]

    def proj_fm(l, c0, ncol, blocks, fn):
        for t0 in range(0, ncol, 512):
            n = min(512, ncol - t0)
            wb, wv = wload([(D["w_in"][l][:, c0 + t0:c0 + t0 + n], n)], 8)
            for s0 in range(0, n, 128):
                w = min(128, n - s0)
                for tb in blocks:
                    p = ps()
                    for k in range(8):
                        kb.op("pe", lambda e: e.matmul(p[0:w, :], lhsT=wv[:, k, s0:s0 + w], rhs=hT[tb][:, k, :], start=(k == 0), stop=(k == 7)),
                              reads=[wb, hT[tb]], writes=[p], inc=(k == 7))
                    fn((t0 + s0) // 128, tb, p, w)

    def conv_chunk(convin, acc, cwt, cbt, cc, nseq, L, dst_ap_fn, dst_buf):
        for s_ in range(nseq):
            a = acc[:, s_ * L:(s_ + 1) * L]
            kb.op("dve", lambda e: e.tensor_scalar(out=a, in0=convin[:, s_, 0:L], scalar1=cwt[:, cc, 0:1], scalar2=None, op0=ALU.mult),
                  reads=[convin, cwt], writes=[acc])
            for tap in range(1, 5):
                kb.op("dve", lambda e: e.scalar_tensor_tensor(out=a, in0=convin[:, s_, tap:tap + L], scalar=cwt[:, cc, tap:tap + 1], in1=a,
                                                              op0=ALU.mult, op1=ALU.add), reads=[convin, cwt, acc], writes=[acc])
            if isinstance(dst_buf, list):
                for gg in range(2):
                    kb.op("act", lambda e: e.activation(out=dst_buf[gg][gg * 64:(gg + 1) * 64, s_ * L:(s_ + 1) * L], in_=acc[gg * 64:(gg + 1) * 64, s_ * L:(s_ + 1) * L],
                                                        func=AF.Silu, bias=cbt[gg * 64:(gg + 1) * 64, cc:cc + 1]), reads=[acc, cbt], writes=[dst_buf[gg]])
            else:
                kb.op("act", lambda e: e.activation(out=dst_ap_fn(s_), in_=a, func=AF.Silu, bias=cbt[:, cc:cc + 1]), reads=[acc, cbt], writes=[dst_buf])

    def evac_conv_in(convin, p, tb, nseq, L):
        if nseq == 1:
            kb.op("act", lambda e: e.activation(out=convin[:, 0, 2 + tb * 512:2 + (tb + 1) * 512], in_=p[:, :], func=AF.Copy), reads=[p], writes=[convin])
        else:
            for s_ in range(2):
                kb.op("act", lambda e: e.activation(out=convin[:, s_, 2:2 + L], in_=p[:, s_ * L:(s_ + 1) * L], func=AF.Copy), reads=[p], writes=[convin])

    def transpose_to_tm(src, dst_fn, dst_buf, T):
        for t0 in range(0, T, 4):
            p = ps()
            for j in range(4):
                kb.op("pe", lambda e: e.matmul(p[:, j * 128:(j + 1) * 128], lhsT=src[:, (t0 + j) * 128:(t0 + j + 1) * 128], rhs=ident_b, start=True, stop=True),
                      reads=[src, cb], writes=[p], inc=(j == 3))
            kb.op("act", lambda e: e.activation(out=dst_fn(t0, 4), in_=p[:, :].rearrange("p (t c) -> p t c", t=4), func=AF.Copy), reads=[p], writes=[dst_buf])

    def ssd(l, g, nblk):
        sample = (g == 1)
        L = 2048 if sample else 256
        nseq = 1 if sample else 2
        lt = L // 128
        ntok = nblk * 512
        T = ntok // 128
        blocks = list(range(nblk))
        with scope() as st:
            xtm = alloc(st, "xtm", [128, T, 512], BF16)
            Btm = alloc(st, "Btm", [128, T, 128], BF16)
            BT = alloc(st, "BT", [128, ntok], BF16)
            CTz = [alloc(st, "CT%d" % i, [128, ntok], BF16) for i in range(2)]
            for i_ in range(2):
                kb.op("dve", lambda e: e.memset(CTz[i_][:], 0.0), writes=[CTz[i_]])
            dtt = alloc(st, "dtt", [128, T, 16], F32)
            dtA = alloc(st, "dtA", [128, T, 16], F32)
            cum = alloc(st, "cum", [128, T, 16], F32)
            tot = alloc(st, "tot", [128, T, 16], F32)
            ecum = alloc(st, "ecum", [128, T, 16], F32)
            wd = alloc(st, "wd", [128, T, 16], F32)
            bj = alloc(st, "bj", [128, T, 16], F32)
            dec = alloc(st, "dec", [128, T, 2, 4], F32)
            abc = alloc(st, "abc", [128, 16], F32)
            dtb = alloc(st, "dtb", [128, 16], F32)
            dsk = alloc(st, "dsk", [128, 8], F32)
            ng = alloc(st, "ng", [128, 512], F32)
            cwt = alloc(st, "cwt", [128, 6, 5], F32)
            cbt = alloc(st, "cbt", [128, 6], F32)
            load_wo(l, 0)
            kb.dma("sp", abc[:], D["ssd_a_log"][l].partition_broadcast(128), writes=[abc])
            kb.dma("sp", dtb[:], D["ssd_dt_bias"][l].partition_broadcast(128), writes=[dtb])
            kb.dma("sp", dsk[:], D["ssd_d"][l].partition_broadcast(128), writes=[dsk])
            kb.dma("sp", ng[:], D["ssd_norm"][l].partition_broadcast(128), writes=[ng])
            for tap in range(5):
                kb.dma("sp", cwt[:, :, tap], D["conv_ssd_w"][l, tap].rearrange("(c p) -> p c", p=128), writes=[cwt], allow_slow_non_contiguous=True)
            kb.dma("sp", cbt[:], D["conv_ssd_b"][l].rearrange("(c p) -> p c", p=128), writes=[cbt], allow_slow_non_contiguous=True)
            kb.op("act", lambda e: e.activation(out=abc[:], in_=abc[:], func=AF.Exp), reads=[abc], writes=[abc])
            kb.op("dve", lambda e: e.tensor_scalar(out=abc[:], in0=abc[:], scalar1=-1.0, scalar2=None, op0=ALU.mult), reads=[abc], writes=[abc])
            with scope() as st1:
                convins = [alloc(st1, "convin", [128, nseq, L + 4], F32) for _ in range(2)]
                acc = alloc(st1, "cacc", [128, ntok], F32)
                xcT = alloc(st1, "xcT", [128, ntok], BF16)
                for cv_ in convins:
                    kb.op("dve", lambda e: e.memset(cv_[:], 0.0), writes=[cv_])

                def cb_fn(ci, tb, p, w):
                    convin = convins[ci % 2]
                    evac_conv_in(convin, p, tb, nseq, L)
                    if tb != blocks[-1]:
                        return
                    if ci < 4:
                        conv_chunk(convin, acc, cwt, cbt, ci, nseq, L, lambda s_: xcT[:, s_ * L:(s_ + 1) * L], xcT)
                        transpose_to_tm(xcT, lambda t0, n: xtm[:, t0:t0 + n, ci * 128:(ci + 1) * 128], xtm, T)
                    elif ci == 4:
                        conv_chunk(convin, acc, cwt, cbt, ci, nseq, L, lambda s_: BT[:, s_ * L:(s_ + 1) * L], BT)
                        transpose_to_tm(BT, lambda t0, n: Btm[:, t0:t0 + n, :], Btm, T)
                    else:
                        conv_chunk(convin, acc, cwt, cbt, ci, nseq, L, None, CTz)
                proj_fm(l, 512, 768, blocks, cb_fn)
            kb.barrier()
            if SSD_PH < 2:
                return
            wb, wv = wload([(D["w_in"][l][:, 1280:1296], 16)], 8)
            for tt in range(T):
                p = ps()
                proj_tm(wb, wv, 0, 16, tt // 4, tt % 4, p)
                kb.op("dve", lambda e: e.tensor_tensor(out=dtt[:, tt, :], in0=p[:, 0:16], in1=dtb[:], op=ALU.add), reads=[p, dtb], writes=[dtt])
            kb.op("act", lambda e: e.activation(out=dtt[:], in_=dtt[:], func=AF.Exp), reads=[dtt], writes=[dtt])
            kb.op("act", lambda e: e.activation(out=dtt[:], in_=dtt[:], func=AF.Ln, bias=1.0), reads=[dtt], writes=[dtt])
            kb.op("dve", lambda e: e.tensor_tensor(out=dtA[:], in0=dtt[:], in1=bcast(abc[:], [128, T, 16], 1), op=ALU.mult), reads=[dtt, abc], writes=[dtA])
            for tt in range(T):
                p = ps()
                kb.op("pe", lambda e: e.matmul(p[:, 0:16], lhsT=triF_f, rhs=dtA[:, tt, :], start=True, stop=True), reads=[cf, dtA], writes=[p], inc=False)
                kb.op("pe", lambda e: e.matmul(p[:, 16:32], lhsT=triB_f, rhs=dtA[:, tt, :], start=True, stop=True), reads=[cf, dtA], writes=[p], inc=False)
                kb.op("pe", lambda e: e.matmul(p[:, 32:48], lhsT=ones_f, rhs=dtA[:, tt, :], start=True, stop=True), reads=[cf, dtA], writes=[p])
                kb.op("dve", lambda e: e.tensor_copy(out=cum[:, tt, 0:8], in_=p[:, 0:8]), reads=[p], writes=[cum])
                kb.op("dve", lambda e: e.tensor_copy(out=cum[:, tt, 8:16], in_=p[:, 24:32]), reads=[p], writes=[cum])
                kb.op("dve", lambda e: e.tensor_copy(out=tot[:, tt, :], in_=p[:, 32:48]), reads=[p], writes=[tot])
            kb.op("act", lambda e: e.activation(out=ecum[:], in_=cum[:], func=AF.Exp), reads=[cum], writes=[ecum])
            kb.op("dve", lambda e: e.tensor_tensor(out=wd[:], in0=tot[:], in1=cum[:], op=ALU.subtract), reads=[tot, cum], writes=[wd])
            kb.op("act", lambda e: e.activation(out=wd[:], in_=wd[:], func=AF.Exp), reads=[wd], writes=[wd])
            kb.op("dve", lambda e: e.tensor_tensor(out=wd[:], in0=wd[:], in1=dtt[:], op=ALU.mult), reads=[wd, dtt], writes=[wd])
            kb.op("act", lambda e: e.activation(out=bj[:], in_=dtt[:], func=AF.Ln), reads=[dtt], writes=[bj])
            kb.op("dve", lambda e: e.tensor_tensor(out=bj[:], in0=bj[:], in1=cum[:], op=ALU.subtract), reads=[bj, cum], writes=[bj])
            tot4 = tot[:].rearrange("p t (d h) -> p t d h", d=2)
            for gg in range(2):
                kb.op("act", lambda e: e.activation(out=dec[gg * 64:(gg + 1) * 64], in_=tot4[gg * 64:(gg + 1) * 64, :, :, gg * 4:(gg + 1) * 4], func=AF.Exp),
                      reads=[tot], writes=[dec])
            if SSD_PH < 3:
                kb.barrier()
                return
            with scope() as st2:
                Hprev = alloc(st2, "Hprev", [128, T, 2, 256], BF16)
                st3 = ExitStack()
                Hs = [alloc(st3, "Hs%d" % i, [128, 256], F32) for i in range(2)]
                xws = [alloc(st3, "xw%d" % i, [128, 512], BF16) for i in range(2)]
                hx = alloc(st3, "hx", [128, 2, 128], F32)
                ho = alloc(st3, "ho", [128, 128], F32)
                xi = 0
                for dr in range(2):
                    H = Hs[dr]
                    for s_ in range(nseq):
                        if sample:
                            for blk in range(2):
                                for two in range(2):
                                    kb.dma("sp", hx[two * 64:(two + 1) * 64, blk, :].rearrange("p (g n) -> p g n", g=2),
                                           D["sssm"][l, dr].rearrange("(g r) p n -> r p g n", g=2)[blk * 2 + two], writes=[hx])
                            for blk in range(2):
                                p = ps()
                                kb.op("pe", lambda e: e.matmul(p[:, 0:128], lhsT=hx[:, blk, :], rhs=ident_f, start=True, stop=True), reads=[hx, cf], writes=[p])
                                kb.op("act", lambda e: e.activation(out=H[:, blk * 128:(blk + 1) * 128], in_=p[:, 0:128], func=AF.Copy), reads=[p], writes=[H])
                        else:
                            kb.op("dve", lambda e: e.memset(H[:], 0.0), writes=[H])
                        order = range(lt) if dr == 0 else range(lt - 1, -1, -1)
                        for r0 in order:
                            tt = s_ * lt + r0
                            kb.op("act", lambda e: e.activation(out=Hprev[:, tt, dr, :], in_=H[:], func=AF.Copy), reads=[H], writes=[Hprev])
                            xw = xws[xi % 2]
                            xi += 1
                            kb.op("dve", lambda e: e.tensor_tensor(out=xw[:].rearrange("p (h d) -> p h d", d=64), in0=xtm[:, tt, :].rearrange("p (h d) -> p h d", d=64),
                                                                   in1=bcast(wd[:, tt, dr * 8:(dr + 1) * 8], [128, 8, 64], 2), op=ALU.mult), reads=[xtm, wd], writes=[xw])
                            p = ps()
                            kb.op("pe", lambda e: e.matmul(p[:, :], lhsT=Btm[:, tt, :], rhs=xw[:], start=True, stop=True), reads=[Btm, xw], writes=[p])
                            kb.op("dve", lambda e: e.tensor_tensor(out=H[:].rearrange("p (h d) -> p h d", d=64), in0=H[:].rearrange("p (h d) -> p h d", d=64),
                                                                   in1=bcast(dec[:, tt, dr, :], [128, 4, 64], 2), op=ALU.mult), reads=[H, dec], writes=[H])
                            for gg in range(2):
                                kb.op("dve", lambda e: e.tensor_tensor(out=H[gg * 64:(gg + 1) * 64, :], in0=H[gg * 64:(gg + 1) * 64, :],
                                                                       in1=p[gg * 64:(gg + 1) * 64, gg * 256:(gg + 1) * 256], op=ALU.add), reads=[H, p], writes=[H])
                        if not sample:
                            for blk in range(2):
                                p = ps()
                                kb.op("pe", lambda e: e.matmul(p[:, 0:128], lhsT=H[:, blk * 128:(blk + 1) * 128], rhs=ident_f, start=True, stop=True), reads=[H, cf], writes=[p])
                                kb.op("act", lambda e: e.activation(out=ho[:], in_=p[:, 0:128], func=AF.Copy), reads=[p], writes=[ho])
                                for two in range(2):
                                    kb.dma("sp", D["nssm"][s_, l, dr].rearrange("(g r) p n -> r p g n", g=2)[blk * 2 + two],
                                           ho[two * 64:(two + 1) * 64, :].rearrange("p (g n) -> p g n", g=2), reads=[ho])
                kb.barrier()
                st3.close()
                if SSD_PH < 4:
                    return
                Dg = alloc(st2, "Dg", [128, 16, 128], F32)
                Es = [alloc(st2, "E%d" % i, [128, 128], F32) for i in range(3)]
                Ms = [alloc(st2, "M%d" % i, [128, 128], BF16) for i in range(3)]
                ya = alloc(st2, "ya", [128, 512], F32)
                yu = alloc(st2, "yu", [128, 512], F32)
                zs = yu
                ytm = alloc(st2, "ytm", [128, 4, 512], BF16)
                yT = alloc(st2, "yT", [128, 4, 512], BF16)
                ss = alloc(st2, "ss", [128, 4], F32)
                wzb, wzv = wload([(D["w_in"][l][:, 0:512], 512)], 8)
                ei = 0
                for tt in range(T):
                    tk = slice(tt * 128, (tt + 1) * 128)
                    pz = ps_acc()
                    proj_tm(wzb, wzv, 0, 512, tt // 4, tt % 4, pz)
                    pBC = ps_acc()
                    for gg in range(2):
                        kb.op("pe", lambda e: e.matmul(pBC[:, gg * 128:(gg + 1) * 128], lhsT=BT[:, tk], rhs=CTz[gg][:, tk], start=True, stop=True),
                              reads=[BT, CTz[gg]], writes=[pBC], inc=(gg == 1))
                    kb.op("dve", lambda e: e.tensor_tensor(out=Dg[:], in0=bcast(ident_f, [128, 16, 128], 1), in1=bcast(cum[:, tt, :], [128, 16, 128], 2), op=ALU.mult),
                          reads=[cf, cum], writes=[Dg])
                    yint = ps_acc()
                    hd = [(h, dr) for h in range(8) for dr in range(2)]
                    mts = {}

                    def sA(i):
                        h, dr = hd[i]
                        pE = ps()
                        kb.op("pe", lambda e: e.matmul(pE[:, 0:128], lhsT=ones_f, rhs=Dg[:, dr * 8 + h, :], start=True, stop=False), reads=[cf, Dg], writes=[pE], inc=False)
                        kb.op("pe", lambda e: e.matmul(pE[:, 0:128], lhsT=ident_b, rhs=(maskF_b if dr == 0 else maskB_b), start=False, stop=True), reads=[cb], writes=[pE])
                        E = Es[i % 3]
                        M = Ms[i % 3]
                        kb.op("act", lambda e: e.activation(out=E[:], in_=pE[:, 0:128], func=AF.Exp, bias=bj[:, tt, dr * 8 + h:dr * 8 + h + 1]), reads=[pE, bj], writes=[E])
                        gg = h // 4
                        kb.op("dve", lambda e: e.tensor_tensor(out=M[:], in0=E[:], in1=pBC[:, gg * 128:(gg + 1) * 128], op=ALU.mult), reads=[E, pBC], writes=[M])
                        mts[i] = M

                    def sC(i):
                        h, dr = hd[i]
                        M = mts.pop(i)
                        kb.op("pe", lambda e: e.matmul(yint[:, h * 64:(h + 1) * 64], lhsT=M[:], rhs=xtm[:, tt, h * 64:(h + 1) * 64], start=(dr == 0), stop=(dr == 1)),
                              reads=[M, xtm], writes=[yint], inc=True)

                    for i in range(16 + 2):
                        if i < 16:
                            sA(i)
                        if i >= 2:
                            sC(i - 2)
                    if SSD_PH < 5:
                        continue
                    pY = [ps(), ps()]
                    for dr in range(2):
                        for gg in range(2):
                            kb.op("pe", lambda e: e.matmul(pY[dr][:, gg * 256:(gg + 1) * 256], lhsT=CTz[gg][:, tk], rhs=Hprev[:, tt, dr, :], start=True, stop=True),
                                  reads=[CTz[gg], Hprev], writes=[pY[dr]], inc=(gg == 1))
                    v3 = lambda ap: ap.rearrange("p (h d) -> p h d", d=64)
                    kb.op("dve", lambda e: e.tensor_tensor(out=v3(ya[:]), in0=v3(xtm[:, tt, :]), in1=bcast(dsk[:], [128, 8, 64], 2), op=ALU.mult), reads=[xtm, dsk], writes=[ya])
                    kb.op("dve", lambda e: e.tensor_tensor(out=ya[:], in0=ya[:], in1=yint[:, :], op=ALU.add), reads=[ya, yint], writes=[ya])
                    for dr in range(2):
                        kb.op("dve", lambda e: e.tensor_tensor(out=v3(yu[:]), in0=v3(pY[dr][:, :]), in1=bcast(ecum[:, tt, dr * 8:(dr + 1) * 8], [128, 8, 64], 2), op=ALU.mult),
                              reads=[pY[dr], ecum], writes=[yu])
                        kb.op("dve", lambda e: e.tensor_tensor(out=ya[:], in0=ya[:], in1=yu[:], op=ALU.add), reads=[ya, yu], writes=[ya])
                    if SSD_PH < 6:
                        continue
                    kb.op("act", lambda e: e.activation(out=zs[:], in_=pz[:, :], func=AF.Silu), reads=[pz], writes=[zs])
                    kb.op("dve", lambda e: e.tensor_tensor(out=ya[:], in0=ya[:], in1=zs[:], op=ALU.mult), reads=[ya, zs], writes=[ya])
                    kb.op("dve", lambda e: e.memset(ss[:, 0:1], 0.0), writes=[ss])
                    kb.op("act", lambda e: e.activation(out=yu[:], in_=ya[:], func=AF.Square, accum_out=ss[:, 0:1]), reads=[ya, ss], writes=[yu, ss])
                    kb.op("dve", lambda e: e.tensor_scalar(out=ss[:, 1:2], in0=ss[:, 0:1], scalar1=1.0 / 512, scalar2=EPS, op0=ALU.mult, op1=ALU.add), reads=[ss], writes=[ss])
                    kb.op("act", lambda e: e.activation(out=ss[:, 1:2], in_=ss[:, 1:2], func=AF.Sqrt), reads=[ss], writes=[ss])
                    kb.op("dve", lambda e: e.reciprocal(out=ss[:, 2:3], in_=ss[:, 1:2]), reads=[ss], writes=[ss])
                    kb.op("dve", lambda e: e.scalar_tensor_tensor(out=ytm[:, tt % 4, :], in0=ya[:], scalar=ss[:, 2:3], in1=ng[:], op0=ALU.mult, op1=ALU.mult),
                          reads=[ya, ss, ng], writes=[ytm])
                    if SSD_PH < 7:
                        continue
                    if tt % 4 == 3:
                        mixer_out(st2, ytm, tt // 4, yT)
        kb.barrier()


    def mlstm(l, g, nblk):
        sample = (g == 1)
        L = 2048 if sample else 256
        nseq = 1 if sample else 2
        lt = L // 128
        ntok = nblk * 512
        T = ntok // 128
        blocks = list(range(nblk))
        C0 = 2832
        lns = math.log(128 ** -0.5)
        for hg in range(2):
            h0 = hg * 2
            with scope() as st:
                qT = alloc(st, "mqT", [128, 2, ntok], BF16)
                kT = alloc(st, "mkT", [128, 2, ntok], BF16)
                vaug = alloc(st, "mvaug", [128, T, 2, 129], BF16)
                Cpb = alloc(st, "Cpb", [128, T, 2, 129], BF16)
                li = alloc(st, "li", [128, T, 4], F32)
                lf = alloc(st, "lf", [128, T, 4], F32)
                G = alloc(st, "G", [128, T, 4], F32)
                tot = alloc(st, "mtot", [128, T, 4], F32)
                pj = alloc(st, "pj", [128, T, 4], F32)
                pjs = alloc(st, "pjs", [128, T, 4], F32)
                gend = alloc(st, "gend", [128, T, 4], F32)
                mlb = alloc(st, "mlb", [128, T, 4], F32)
                wend = alloc(st, "wend", [128, T, 4], F32)
                mprev = alloc(st, "mprev", [128, T, 4], F32)
                gb = alloc(st, "gb", [128, 8], F32)
                cwt = alloc(st, "mcwt", [128, 4, 5], F32)
                cbt = alloc(st, "mcbt", [128, 4], F32)
                ng = alloc(st, "mng", [128, 256], F32)
                kb.dma("pool", wo_buf[:, 0:2, :], D["w_out"][l][1024 + h0 * 128:1024 + (h0 + 2) * 128, :].rearrange("(k p) n -> p k n", p=128), writes=[wo_buf])
                kb.dma("sp", ng[:], D["mlstm_norm"][l][h0 * 128:(h0 + 2) * 128].partition_broadcast(128), writes=[ng])
                goffs = [0 * 8 + 0 * 4 + h0, 1 * 8 + 0 * 4 + h0, 0 * 8 + 1 * 4 + h0, 1 * 8 + 1 * 4 + h0]
                for i_, go in enumerate(goffs):
                    kb.dma("sp", gb[:, i_ * 2:(i_ + 1) * 2], D["mlstm_gate_b"][l][go:go + 2].partition_broadcast(128), writes=[gb])
                for ci, ch0 in enumerate((h0 * 128, (h0 + 1) * 128, 512 + h0 * 128, 512 + (h0 + 1) * 128)):
                    for tap in range(5):
                        kb.dma("sp", cwt[:, ci, tap:tap + 1], D["conv_mlstm_w"][l, tap][ch0:ch0 + 128].rearrange("(p o) -> p o", o=1), writes=[cwt])
                    kb.dma("sp", cbt[:, ci:ci + 1], D["conv_mlstm_b"][l][ch0:ch0 + 128].rearrange("(p o) -> p o", o=1), writes=[cbt])
                kb.op("dve", lambda e: e.memset(vaug[:, :, :, 128:129], 1.0), writes=[vaug])
                with scope() as st1:
                    convins = [alloc(st1, "mconvin", [128, nseq, L + 4], F32) for _ in range(2)]
                    acc = alloc(st1, "mcacc", [128, ntok], F32)
                    for cv_ in convins:
                        kb.op("dve", lambda e: e.memset(cv_[:], 0.0), writes=[cv_])
                    wb, wv = wload([(D["w_in"][l][:, C0 + h0 * 128:C0 + (h0 + 2) * 128], 256),
                                    (D["w_in"][l][:, C0 + 512 + h0 * 128:C0 + 512 + (h0 + 2) * 128], 256)], 8)
                    for ci in range(4):
                        for tb in blocks:
                            p = ps()
                            for k in range(8):
                                kb.op("pe", lambda e: e.matmul(p[:, :], lhsT=wv[:, k, ci * 128:(ci + 1) * 128], rhs=hT[tb][:, k, :], start=(k == 0), stop=(k == 7)),
                                      reads=[wb, hT[tb]], writes=[p], inc=(k == 7))
                            evac_conv_in(convins[ci % 2], p, tb, nseq, L)
                        convin = convins[ci % 2]
                        dstb = qT if ci < 2 else kT
                        conv_chunk(convin, acc, cwt, cbt, ci, nseq, L, lambda s_: dstb[:, ci % 2, s_ * L:(s_ + 1) * L], dstb)
                kb.barrier()
                wb, wv = wload([(D["w_in"][l][:, C0 + 1024 + h0 * 128:C0 + 1024 + (h0 + 2) * 128], 256)], 8)
                for tt in range(T):
                    p = ps()
                    proj_tm(wb, wv, 0, 256, tt // 4, tt % 4, p)
                    kb.op("act", lambda e: e.activation(out=vaug[:, tt, :, 0:128], in_=p[:, 0:256].rearrange("p (h e) -> p h e", e=128), func=AF.Copy), reads=[p], writes=[vaug])
                gc = C0 + 2048
                wb, wv = wload([(D["w_in"][l][:, gc + go:gc + go + 2], 2) for go in goffs], 8)
                for tt in range(T):
                    p = ps()
                    proj_tm(wb, wv, 0, 8, tt // 4, tt % 4, p)
                    kb.op("dve", lambda e: e.tensor_tensor(out=li[:, tt, :], in0=p[:, 0:4], in1=gb[:, 0:4], op=ALU.add), reads=[p, gb], writes=[li])
                    kb.op("dve", lambda e: e.tensor_tensor(out=lf[:, tt, :], in0=p[:, 4:8], in1=gb[:, 4:8], op=ALU.add), reads=[p, gb], writes=[lf])
                kb.op("act", lambda e: e.activation(out=lf[:], in_=lf[:], func=AF.Exp, scale=-1.0), reads=[lf], writes=[lf])
                kb.op("act", lambda e: e.activation(out=lf[:], in_=lf[:], func=AF.Ln, bias=1.0), reads=[lf], writes=[lf])
                kb.op("dve", lambda e: e.tensor_scalar(out=lf[:], in0=lf[:], scalar1=-1.0, scalar2=None, op0=ALU.mult), reads=[lf], writes=[lf])
                for tt in range(T):
                    p = ps()
                    kb.op("pe", lambda e: e.matmul(p[:, 0:4], lhsT=triF_f, rhs=lf[:, tt, :], start=True, stop=True), reads=[cf, lf], writes=[p], inc=False)
                    kb.op("pe", lambda e: e.matmul(p[:, 4:8], lhsT=triB_f, rhs=lf[:, tt, :], start=True, stop=True), reads=[cf, lf], writes=[p], inc=False)
                    kb.op("pe", lambda e: e.matmul(p[:, 8:12], lhsT=ones_f, rhs=lf[:, tt, :], start=True, stop=True), reads=[cf, lf], writes=[p])
                    kb.op("dve", lambda e: e.tensor_copy(out=G[:, tt, 0:2], in_=p[:, 0:2]), reads=[p], writes=[G])
                    kb.op("dve", lambda e: e.tensor_copy(out=G[:, tt, 2:4], in_=p[:, 6:8]), reads=[p], writes=[G])
                    kb.op("dve", lambda e: e.tensor_copy(out=tot[:, tt, :], in_=p[:, 8:12]), reads=[p], writes=[tot])
                kb.op("dve", lambda e: e.tensor_tensor(out=pj[:], in0=li[:], in1=G[:], op=ALU.subtract), reads=[li, G], writes=[pj])
                kb.op("dve", lambda e: e.tensor_scalar(out=pjs[:], in0=pj[:], scalar1=lns, scalar2=None, op0=ALU.add), reads=[pj], writes=[pjs])
                kb.op("dve", lambda e: e.tensor_tensor(out=gend[:], in0=pj[:], in1=tot[:], op=ALU.add), reads=[pj, tot], writes=[gend])
                with scope() as stt:
                    mrow = alloc(stt, "mrow8", [4, 1], F32)
                    d8 = alloc(stt, "d8", [4, 4], F32)
                    for tt in range(T):
                        p = ps()
                        kb.op("pe", lambda e: e.matmul(p[0:4, 0:128], lhsT=gend[:, tt, :], rhs=ident_f, start=True, stop=True), reads=[gend, cf], writes=[p])
                        kb.op("dve", lambda e: e.tensor_reduce(out=mrow[:], in_=p[0:4, 0:128], axis=AX.X, op=ALU.max), reads=[p], writes=[mrow])
                        kb.op("dve", lambda e: e.tensor_scalar(out=d8[:], in0=ident_f[0:4, 0:4], scalar1=mrow[:, 0:1], scalar2=None, op0=ALU.mult), reads=[cf, mrow], writes=[d8])
                        p2 = ps()
                        kb.op("pe", lambda e: e.matmul(p2[:, 0:4], lhsT=ones_f[0:4, :], rhs=d8[:], start=True, stop=True), reads=[cf, d8], writes=[p2])
                        kb.op("dve", lambda e: e.tensor_copy(out=mlb[:, tt, :], in_=p2[:, 0:4]), reads=[p2], writes=[mlb])
                    kb.barrier()
                kb.op("dve", lambda e: e.tensor_tensor(out=wend[:], in0=gend[:], in1=mlb[:], op=ALU.subtract), reads=[gend, mlb], writes=[wend])
                kb.op("act", lambda e: e.activation(out=wend[:], in_=wend[:], func=AF.Exp), reads=[wend], writes=[wend])
                with scope() as st2:
                    Cst = [alloc(st2, "Cst%d" % i, [128, 129], F32) for i in range(4)]
                    mp = alloc(st2, "mp", [128, 4], F32)
                    mt8 = alloc(st2, "mt8", [128, 16], F32)
                    kwts = [alloc(st2, "kwt", [128, 2, 128], BF16) for _ in range(2)]
                    Dgs = [alloc(st2, "mDg", [128, 4, 128], F32) for _ in range(2)]
                    Drs = [alloc(st2, "mDr", [128, 4, 128], F32) for _ in range(2)]
                    scs = [alloc(st2, "msc", [128, 40], F32) for _ in range(2)]
                    kwt, Dg, Dr, sc = kwts[0], Dgs[0], Drs[0], scs[0]
                    Es = [alloc(st2, "mE%d" % i, [128, 128], F32) for i in range(3)]
                    Ms = [alloc(st2, "mM%d" % i, [128, 128], BF16) for i in range(3)]
                    nds = [alloc(st2, "nd%d" % i, [128, 129], F32) for i in range(2)]
                    cbf = [alloc(st2, "cbf%d" % i, [128, 129], BF16) for i in range(2)]
                    hsums = [alloc(st2, "hsum", [128, 256], F32) for _ in range(2)]
                    hts = [alloc(st2, "mht", [128, 256], F32) for _ in range(2)]
                    sgs = [alloc(st2, "msg", [128, 256], F32) for _ in range(2)]
                    hsum, ht, sg = hsums[0], hts[0], sgs[0]
                    ytm = alloc(st2, "mytm", [128, 4, 256], BF16)
                    yT = alloc(st2, "myT", [128, 2, 512], BF16)
                    ei = 0
                    ei0 = [0]

                    def init_state(dr, s_):
                        for hh in range(2):
                            C = Cst[dr * 2 + hh]
                            if sample:
                                kb.dma("sp", C[:, 0:128], D["smc"][l, dr, h0 + hh], writes=[C])
                                kb.dma("sp", C[:, 128:129], D["smn"][l, dr, h0 + hh].rearrange("(p o) -> p o", o=1), writes=[C])
                            else:
                                kb.op("dve", lambda e: e.memset(C[:], 0.0), writes=[C])
                        if sample:
                            kb.dma("sp", mp[:, dr * 2:dr * 2 + 2], D["smm"][l][dr * 4 + h0:dr * 4 + h0 + 2].partition_broadcast(128), writes=[mp])
                        else:
                            kb.op("dve", lambda e: e.memset(mp[:, dr * 2:dr * 2 + 2], 0.0), writes=[mp])

                    def local_update(dr, tt):
                        cs = slice(dr * 2, dr * 2 + 2)
                        pk = ps()
                        for hh in range(2):
                            kb.op("pe", lambda e: e.matmul(pk[:, hh * 128:(hh + 1) * 128], lhsT=kT[:, hh, tt * 128:(tt + 1) * 128], rhs=ident_b, start=True, stop=True),
                                  reads=[kT, cb], writes=[pk], inc=(hh == 1))
                        kb.op("dve", lambda e: e.tensor_tensor(out=kwt[:], in0=pk[:, 0:256].rearrange("p (h d) -> p h d", d=128),
                                                               in1=bcast(wend[:, tt, cs], [128, 2, 128], 2), op=ALU.mult), reads=[pk, wend], writes=[kwt])
                        a = sc[:, 0:2]
                        mn = sc[:, 2:4]
                        sp_ = sc[:, 4:6]
                        sl_ = sc[:, 6:8]
                        kb.op("dve", lambda e: e.tensor_tensor(out=a, in0=tot[:, tt, cs], in1=mp[:, cs], op=ALU.add), reads=[tot, mp], writes=[sc])
                        kb.op("dve", lambda e: e.tensor_tensor(out=mn, in0=a, in1=mlb[:, tt, cs], op=ALU.max), reads=[sc, mlb], writes=[sc])
                        kb.op("dve", lambda e: e.tensor_tensor(out=sp_, in0=a, in1=mn, op=ALU.subtract), reads=[sc], writes=[sc])
                        kb.op("dve", lambda e: e.tensor_tensor(out=sl_, in0=mlb[:, tt, cs], in1=mn, op=ALU.subtract), reads=[sc, mlb], writes=[sc])
                        kb.op("act", lambda e: e.activation(out=sc[:, 4:8], in_=sc[:, 4:8], func=AF.Exp), reads=[sc], writes=[sc])
                        kb.op("dve", lambda e: e.tensor_copy(out=mp[:, cs], in_=mn), reads=[sc], writes=[mp])
                        for hh in range(2):
                            C = Cst[dr * 2 + hh]
                            pc = ps()
                            kb.op("pe", lambda e: e.matmul(pc[:, 0:129], lhsT=kwt[:, hh, :], rhs=vaug[:, tt, hh, :], start=True, stop=True), reads=[kwt, vaug], writes=[pc])
                            kb.op("dve", lambda e: e.tensor_scalar(out=C[:], in0=C[:], scalar1=sc[:, 4 + hh:5 + hh], scalar2=None, op0=ALU.mult), reads=[C, sc], writes=[C])
                            kb.op("dve", lambda e: e.scalar_tensor_tensor(out=C[:], in0=pc[:, 0:129], scalar=sc[:, 6 + hh:7 + hh], in1=C[:], op0=ALU.mult, op1=ALU.add),
                                  reads=[pc, sc, C], writes=[C])

                    def final_state(dr, s_):
                        for hh in range(2):
                            C = Cst[dr * 2 + hh]
                            kb.dma("sp", D["nmc"][s_, l, dr, h0 + hh], C[:, 0:128], reads=[C])
                            kb.dma("sp", D["nmn"][s_, l, dr, h0 + hh].rearrange("(p o) -> p o", o=1), C[:, 128:129], reads=[C])
                        kb.dma("sp", D["nmm"][s_, l:l + 1, dr * 4 + h0:dr * 4 + h0 + 2], mp[0:1, dr * 2:dr * 2 + 2], reads=[mp])

                    for s_ in range(nseq):
                        init_state(1, s_)
                        for r0 in range(lt - 1, -1, -1):
                            tt = s_ * lt + r0
                            kwt, sc = kwts[tt % 2], scs[tt % 2]
                            for hh in range(2):
                                kb.op("act", lambda e: e.activation(out=Cpb[:, tt, hh, :], in_=Cst[2 + hh][:], func=AF.Copy), reads=[Cst[2 + hh]], writes=[Cpb])
                            kb.op("dve", lambda e: e.tensor_copy(out=mprev[:, tt, 2:4], in_=mp[:, 2:4]), reads=[mp], writes=[mprev])
                            local_update(1, tt)
                        if not sample:
                            final_state(1, s_)
                    wob, wov = wload([(D["w_in"][l][:, C0 + 1536 + h0 * 128:C0 + 1536 + (h0 + 2) * 128], 256)], 8)
                    for s_ in range(nseq):
                        init_state(0, s_)
                        for r0 in range(lt):
                            tt = s_ * lt + r0
                            tk = slice(tt * 128, (tt + 1) * 128)
                            kwt, Dg, Dr, sc = kwts[tt % 2], Dgs[tt % 2], Drs[tt % 2], scs[tt % 2]
                            hsum, ht, sg = hsums[tt % 2], hts[tt % 2], sgs[tt % 2]
                            kb.op("dve", lambda e: e.tensor_copy(out=mprev[:, tt, 0:2], in_=mp[:, 0:2]), reads=[mp], writes=[mprev])
                            kb.op("dve", lambda e: e.tensor_tensor(out=sc[:, 8:12], in0=G[:, tt, :], in1=mprev[:, tt, :], op=ALU.add), reads=[G, mprev], writes=[sc])
                            kb.op("dve", lambda e: e.tensor_tensor(out=Dg[:], in0=bcast(ident_f, [128, 4, 128], 1), in1=bcast(pj[:, tt, :], [128, 4, 128], 2), op=ALU.mult),
                                  reads=[cf, pj], writes=[Dg])
                            po = ps_acc()
                            proj_tm(wob, wov, 0, 256, tt // 4, tt % 4, po)
                            pSs = []
                            for hh in range(2):
                                pS = ps_acc()
                                kb.op("pe", lambda e: e.matmul(pS[:, 0:128], lhsT=kT[:, hh, tk], rhs=qT[:, hh, tk], start=True, stop=True), reads=[kT, qT], writes=[pS])
                                pSs.append(pS)
                            pms = []
                            for c in range(4):
                                dr = c // 2
                                pm = ps()
                                kb.op("pe", lambda e: e.matmul(pm[:, 0:128], lhsT=ones_f, rhs=Dg[:, c, :], start=True, stop=False), reads=[cf, Dg], writes=[pm], inc=False)
                                kb.op("pe", lambda e: e.matmul(pm[:, 0:128], lhsT=ident_b, rhs=(maskB_b if dr == 0 else maskF_b), start=False, stop=True), reads=[cb], writes=[pm])
                                pms.append(pm)
                            for c in range(4):
                                kb.op("dve", lambda e: e.tensor_reduce(out=sc[:, 12 + c:13 + c], in_=pms[c][:, 0:128], axis=AX.X, op=ALU.max), reads=[pms[c]], writes=[sc])
                            kb.op("dve", lambda e: e.tensor_tensor(out=sc[:, 12:16], in0=sc[:, 12:16], in1=G[:, tt, :], op=ALU.add), reads=[sc, G], writes=[sc])
                            kb.op("dve", lambda e: e.tensor_tensor(out=sc[:, 16:20], in0=sc[:, 12:16], in1=sc[:, 8:12], op=ALU.max), reads=[sc], writes=[sc])
                            kb.op("dve", lambda e: e.tensor_tensor(out=sc[:, 20:24], in0=G[:, tt, :], in1=sc[:, 16:20], op=ALU.subtract), reads=[sc, G], writes=[sc])
                            kb.op("dve", lambda e: e.tensor_tensor(out=sc[:, 24:28], in0=sc[:, 8:12], in1=sc[:, 16:20], op=ALU.subtract), reads=[sc], writes=[sc])
                            kb.op("act", lambda e: e.activation(out=sc[:, 24:28], in_=sc[:, 24:28], func=AF.Exp, bias=lns_col[:, 0:1]), reads=[sc, cf], writes=[sc])
                            kb.op("act", lambda e: e.activation(out=sc[:, 28:32], in_=sc[:, 16:20], func=AF.Exp, scale=-1.0), reads=[sc], writes=[sc])
                            kb.op("dve", lambda e: e.tensor_tensor(out=Dr[:], in0=bcast(ident_f, [128, 4, 128], 1), in1=bcast(sc[:, 20:24], [128, 4, 128], 2), op=ALU.mult),
                                  reads=[cf, sc], writes=[Dr])
                            items = [(0, 0), (0, 1), (1, 0), (1, 1)]
                            mts = {}

                            def mA(i):
                                hh, dr = items[i]
                                c = dr * 2 + hh
                                pW = ps()
                                kb.op("pe", lambda e: e.matmul(pW[:, 0:128], lhsT=ones_f, rhs=Dr[:, c, :], start=True, stop=False), reads=[cf, Dr], writes=[pW], inc=False)
                                kb.op("pe", lambda e: e.matmul(pW[:, 0:128], lhsT=ident_b, rhs=(maskF_b if dr == 0 else maskB_b), start=False, stop=True), reads=[cb], writes=[pW])
                                E = Es[(ei0[0] + i) % 3]
                                M = Ms[(ei0[0] + i) % 3]
                                kb.op("act", lambda e: e.activation(out=E[:], in_=pW[:, 0:128], func=AF.Exp, bias=pjs[:, tt, c:c + 1]), reads=[pW, pjs], writes=[E])
                                kb.op("dve", lambda e: e.tensor_tensor(out=M[:], in0=E[:], in1=pSs[hh][:, 0:128], op=ALU.mult), reads=[E, pSs[hh]], writes=[M])
                                if dr == 0:
                                    kb.op("act", lambda e: e.activation(out=cbf[hh][:], in_=Cst[hh][:], func=AF.Copy), reads=[Cst[hh]], writes=[cbf[hh]])
                                mts[i] = M

                            def mC(i):
                                hh, dr = items[i]
                                c = dr * 2 + hh
                                M = mts.pop(i)
                                nd = nds[i % 2]
                                pN = ps()
                                kb.op("pe", lambda e: e.matmul(pN[:, 0:129], lhsT=M[:], rhs=vaug[:, tt, hh, :], start=True, stop=True), reads=[M, vaug], writes=[pN])
                                pI = ps()
                                if dr == 0:
                                    kb.op("pe", lambda e: e.matmul(pI[:, 0:129], lhsT=qT[:, hh, tk], rhs=cbf[hh][:], start=True, stop=True), reads=[qT, cbf[hh]], writes=[pI])
                                else:
                                    kb.op("pe", lambda e: e.matmul(pI[:, 0:129], lhsT=qT[:, hh, tk], rhs=Cpb[:, tt, hh, :], start=True, stop=True), reads=[qT, Cpb], writes=[pI])
                                kb.op("act", lambda e: e.activation(out=nd[:], in_=pN[:, 0:129], func=AF.Copy), reads=[pN], writes=[nd])
                                kb.op("dve", lambda e: e.scalar_tensor_tensor(out=nd[:], in0=pI[:, 0:129], scalar=sc[:, 24 + c:25 + c], in1=nd[:], op0=ALU.mult, op1=ALU.add),
                                      reads=[pI, sc, nd], writes=[nd])
                                kb.op("dve", lambda e: e.tensor_scalar(out=sc[:, 32:33], in0=nd[:, 128:129], scalar1=-1.0, scalar2=None, op0=ALU.mult), reads=[nd], writes=[sc])
                                kb.op("dve", lambda e: e.tensor_tensor(out=sc[:, 32:33], in0=sc[:, 32:33], in1=nd[:, 128:129], op=ALU.max), reads=[nd, sc], writes=[sc])
                                kb.op("dve", lambda e: e.tensor_tensor(out=sc[:, 32:33], in0=sc[:, 32:33], in1=sc[:, 28 + c:29 + c], op=ALU.max), reads=[sc], writes=[sc])
                                kb.op("dve", lambda e: e.reciprocal(out=sc[:, 33:34], in_=sc[:, 32:33]), reads=[sc], writes=[sc])
                                if dr == 0:
                                    kb.op("dve", lambda e: e.tensor_scalar(out=hsum[:, hh * 128:(hh + 1) * 128], in0=nd[:, 0:128], scalar1=sc[:, 33:34], scalar2=None, op0=ALU.mult),
                                          reads=[nd, sc], writes=[hsum])
                                else:
                                    kb.op("dve", lambda e: e.scalar_tensor_tensor(out=hsum[:, hh * 128:(hh + 1) * 128], in0=nd[:, 0:128], scalar=sc[:, 33:34],
                                                                                  in1=hsum[:, hh * 128:(hh + 1) * 128], op0=ALU.mult, op1=ALU.add), reads=[nd, sc, hsum], writes=[hsum])

                            for i in range(4 + 2):
                                if i < 4:
                                    mA(i)
                                if i >= 2:
                                    mC(i - 2)
                            ei0[0] += 4
                            local_update(0, tt)
                            h3 = hsum[:].rearrange("p (h d) -> p h d", d=128)
                            t3 = ht[:].rearrange("p (h d) -> p h d", d=128)
                            kb.op("dve", lambda e: e.tensor_tensor(out=ht[:], in0=hsum[:], in1=hsum[:], op=ALU.mult), reads=[hsum], writes=[ht])
                            kb.op("dve", lambda e: e.tensor_reduce(out=sc[:, 34:36], in_=t3, axis=AX.X, op=ALU.add), reads=[ht], writes=[sc])
                            kb.op("dve", lambda e: e.tensor_scalar(out=sc[:, 34:36], in0=sc[:, 34:36], scalar1=1.0 / 128, scalar2=EPS, op0=ALU.mult, op1=ALU.add), reads=[sc], writes=[sc])
                            kb.op("act", lambda e: e.activation(out=sc[:, 34:36], in_=sc[:, 34:36], func=AF.Sqrt), reads=[sc], writes=[sc])
                            kb.op("dve", lambda e: e.reciprocal(out=sc[:, 36:38], in_=sc[:, 34:36]), reads=[sc], writes=[sc])
                            kb.op("dve", lambda e: e.tensor_tensor(out=t3, in0=h3, in1=bcast(sc[:, 36:38], [128, 2, 128], 2), op=ALU.mult), reads=[hsum, sc], writes=[ht])
                            kb.op("dve", lambda e: e.tensor_tensor(out=ht[:], in0=ht[:], in1=ng[:], op=ALU.mult), reads=[ht, ng], writes=[ht])
                            kb.op("act", lambda e: e.activation(out=sg[:], in_=po[:, 0:256], func=AF.Sigmoid), reads=[po], writes=[sg])
                            kb.op("dve", lambda e: e.tensor_tensor(out=ytm[:, tt % 4, :], in0=ht[:], in1=sg[:], op=ALU.mult), reads=[ht, sg], writes=[ytm])
                            if tt % 4 == 3:
                                tb = tt // 4
                                for c2 in range(2):
                                    p = ps()
                                    for tl in range(4):
                                        kb.op("pe", lambda e: e.matmul(p[:, tl * 128:(tl + 1) * 128], lhsT=ytm[:, tl, c2 * 128:(c2 + 1) * 128], rhs=ident_b, start=True, stop=True),
                                              reads=[ytm, cb], writes=[p], inc=(tl == 3))
                                    kb.op("act", lambda e: e.activation(out=yT[:, c2, :], in_=p[:, :], func=AF.Copy), reads=[p], writes=[yT])
                                for c in range(8):
                                    p = ps()
                                    for k in range(2):
                                        kb.op("pe", lambda e: e.matmul(p[:, :], lhsT=wo_buf[:, k, c * 128:(c + 1) * 128], rhs=yT[:, k, :], start=(k == 0), stop=(k == 1)),
                                              reads=[wo_buf, yT], writes=[p], inc=(k == 1))
                                    kb.op("dve", lambda e: e.scalar_tensor_tensor(out=xT[tb][:, c, :], in0=p[:, :], scalar=modc[:, 2, c:c + 1], in1=xT[tb][:, c, :],
                                                                                  op0=ALU.mult, op1=ALU.add), reads=[p, modc, xT[tb]], writes=[xT[tb]])
                        if not sample:
                            final_state(0, s_)
            kb.barrier()

    def run_pass(g, src, dst, nblk):
        load_x(src, nblk)
        kb.barrier()
        for l in range(NL):
            load_mod(l, g)
            with scope() as st:
                sq = [alloc(st, "sq", [128, 8, 512], BF16) for _ in range(2)]
                rstd = [alloc(st, "rstd", [128, 512], F32) for _ in range(2)]
                tmp2 = [alloc(st, "tmpn%d" % i, [128, 512], F32) for i in range(4)]
                for tb in range(nblk):
                    norm_block((sq, rstd, tmp2), tb, AB.t[:, 0, :], AB.t[:, 1, :], hT[tb])
            kb.barrier()
            for mk in MIXERS:
                if mk in "bd":
                    attention(l, g, mk, nblk)
                elif mk == "a":
                    ssd(l, g, nblk)
                elif mk == "c":
                    mlstm(l, g, nblk)
            with scope() as st:
                sq = [alloc(st, "sq", [128, 8, 512], BF16) for _ in range(2)]
                rstd = [alloc(st, "rstd", [128, 512], F32) for _ in range(2)]
                tmp2 = [alloc(st, "tmpn%d" % i, [128, 512], F32) for i in range(4)]
                for tb in range(nblk):
                    norm_block((sq, rstd, tmp2), tb, AB.t[:, 2, :], AB.t[:, 3, :], hT[tb])
            kb.barrier()
            for b0 in range(0, nblk, 2):
                ffn(l, list(range(b0, min(b0 + 2, nblk))))
        final_out(nblk, dst)

    run_pass(0, D["xp"], D["yp"], 1)
    run_pass(1, D["xs"], D["ys"], 4)

    kb.barrier(include_pool_dma=True)
    top.close()
    print("instructions:", kb.ninst, flush=True)
    return nc


_CACHE = {}


def prep_inputs(inp):
    f = lambda a: np.ascontiguousarray(np.asarray(a, dtype=np.float32))
    consts = make_consts()
    rope = make_rope()
    shared = {}
    for name in ("w_ada", "b_ada", "norm1", "norm2", "w_in", "w_out", "conv_ssd_w", "conv_ssd_b", "ssd_d", "ssd_norm",
                 "diff_lq1", "diff_lk1", "diff_lq2", "diff_lk2", "conv_mlstm_w", "conv_mlstm_b", "mlstm_norm",
                 "gqa_q_norm", "gqa_k_norm", "w_ffn_in", "w_ffn_out", "norm_f"):
        shared[name] = f(inp[name])
    shared["ssd_a_log"] = f(inp["ssd_a_log"]).reshape(4, 16)
    shared["ssd_dt_bias"] = f(inp["ssd_dt_bias"]).reshape(4, 16)
    shared["mlstm_gate_b"] = f(inp["mlstm_gate_b"]).reshape(4, 16)
    shared["consts"] = consts
    shared["rope"] = rope
    xp = f(inp["x_prompt"])
    xs = f(inp["x_sample"])
    in_maps = []
    for c in range(8):
        b = c // 4
        m = dict(shared)
        m["xp"] = xp[2 * c:2 * c + 2].reshape(512, 1024)
        m["xs"] = xs[b]
        m["cvec"] = np.stack([f(inp["c_ctx"]), f(inp["c"])[b]], axis=0)
        m["cdk"] = f(inp["cache_diff_k"])[b].reshape(4, 256, 512)
        m["cdv"] = f(inp["cache_diff_v"])[b].reshape(4, 256, 512)
        m["cgk"] = f(inp["cache_gqa_k"])[b].reshape(4, 256, 128)
        m["cgv"] = f(inp["cache_gqa_v"])[b].reshape(4, 256, 128)
        m["sssm"] = f(inp["state_ssm"])[b]
        m["smc"] = f(inp["state_mlstm_c"])[b]
        m["smn"] = f(inp["state_mlstm_n"])[b]
        m["smm"] = f(inp["state_mlstm_m"])[b].reshape(4, 8)
        in_maps.append(m)
    if NL < 4:
        spec = dict(IN_SPECS)
        for m in in_maps:
            for k_ in list(m.keys()):
                if spec[k_][0] == 4 and len(spec[k_]) > 1:
                    m[k_] = np.ascontiguousarray(m[k_][:NL])
    return in_maps


def kernel(**inp):
    if "nc" not in _CACHE:
        _CACHE["nc"] = build_program()
    nc = _CACHE["nc"]
    in_maps = prep_inputs(inp)
    res = run_bass_kernel_spmd(nc, in_maps, core_ids=list(range(8)))
    return assemble(res.results)


def assemble(R):
    y_prompt = np.concatenate([R[c]["yp"].reshape(2, 256, 1024) for c in range(8)], axis=0)
    y_sample = np.stack([R[0]["ys"], R[4]["ys"]], axis=0)
    cat = lambda k: np.concatenate([R[c][k] for c in range(8)], axis=0)
    ndk = cat("ndk").reshape(16, 4, 256, 4, 2, 64)
    ndv = cat("ndv").reshape(16, 4, 256, 4, 128)
    ngk = cat("ngk").reshape(16, 4, 256, 2, 64)
    ngv = cat("ngv").reshape(16, 4, 256, 2, 64)
    nssm = cat("nssm")
    nmc = cat("nmc")
    nmn = cat("nmn")
    nmm = cat("nmm").reshape(16, 4, 2, 4)
    return (y_prompt, y_sample, ndk, ndv, ngk, ngv, nssm, nmc, nmn, nmm)
```

```python
import os
import math
from contextlib import ExitStack
import numpy as np
import concourse.bass as bass
import concourse.mybir as mybir
from concourse.bass_utils import run_bass_kernel_spmd

F32 = mybir.dt.float32
BF16 = mybir.dt.bfloat16
ALU = mybir.AluOpType
AF = mybir.ActivationFunctionType
AX = mybir.AxisListType

D_MODEL = 1024
DEPTH = 4
IN_COLS = 5664
D_FF = 2816
EPS = 1e-6
NEG = -30000.0

NL = int(os.environ.get("MK_NL", "4"))
MIXERS = os.environ.get("MK_MIX", "abcd")
SSD_PH = int(os.environ.get("MK_SSD_PH", "9"))
SSD_SUB = int(os.environ.get("MK_SSD_SUB", "9"))


class Buf:
    __slots__ = ("t", "w", "r")

    def __init__(self, t):
        self.t = t
        self.w = None
        self.r = []

    def __getitem__(self, idx):
        return self.t[idx]


class _Rec:
    def __init__(self):
        self.call = None

    def __getattr__(self, name):
        def f(*args, **kw):
            self.call = (name, args, kw)
            return self
        return f


class KB:
    NDMA_SEM = 8

    def __init__(self, nc):
        self.nc = nc
        self.engs = {"pe": nc.tensor, "act": nc.scalar, "dve": nc.vector, "pool": nc.gpsimd, "sp": nc.sync}
        self.sems = {}
        self.cnt = {}
        for k in ("pe", "act", "dve", "pool"):
            self.sems[k] = nc.alloc_semaphore(name="s_" + k)
            self.cnt[k] = 0
        self.dq = {}
        for q in ("sp", "pool", "act"):
            lst = []
            for i in range(self.NDMA_SEM):
                key = "d_%s%d" % (q, i)
                self.sems[key] = nc.alloc_semaphore(name=key)
                self.cnt[key] = 0
                lst.append(key)
            self.dq[q] = [lst, 0]
        self.seen = {e: {} for e in self.engs}
        self.ninst = 0
        self.defer = bool(int(os.environ.get('MK_SCHED', '1')))
        self.pending = []
        self.stats = {} if os.environ.get('MK_STATS') else None
        self.label = 'top'

    def _wait(self, eng, k, v):
        seen = self.seen[eng]
        if seen.get(k, 0) >= v:
            return
        self.engs[eng].wait_ge(self.sems[k], v)
        self.ninst += 1
        seen[k] = v

    def _need(self, eng, reads, writes):
        need = {}

        def add(dep):
            if dep is None:
                return
            k, v = dep
            if need.get(k, 0) < v:
                need[k] = v
        for b in reads:
            add(b.w)
        for b in writes:
            add(b.w)
            for d in b.r:
                add(d)
        for k, v in need.items():
            if k == eng and eng == "pe":
                continue
            self._wait(eng, k, v)

    def _record(self, dep, reads, writes):
        for b in reads:
            b.r.append(dep)
            if len(b.r) > 64:
                mx = {}
                for k, v in b.r:
                    if mx.get(k, 0) < v:
                        mx[k] = v
                b.r = list(mx.items())
        for b in writes:
            b.w = dep
            b.r = []

    def op(self, eng, fn, reads=(), writes=(), inc=True):
        if self.defer:
            rec = _Rec()
            fn(rec)
            self.pending.append(("op", eng, rec.call, tuple(reads), tuple(writes), inc))
            return None
        return self._op_now(eng, fn, reads, writes, inc)

    def _op_now(self, eng, fn, reads=(), writes=(), inc=True):
        self._need(eng, reads, writes)
        ins = fn(self.engs[eng])
        self.ninst += 1
        val = self.cnt[eng] + 1
        if inc:
            ins.then_inc(self.sems[eng], 1)
            self.cnt[eng] = val
        self._record((eng, val), reads, writes)
        return ins

    def dma(self, q, out, in_, reads=(), writes=(), **kw):
        if self.defer:
            self.pending.append(("dma", q, (out, in_, kw), tuple(reads), tuple(writes), True))
            return None
        return self._dma_now(q, out, in_, reads, writes, **kw)

    def _dma_now(self, q, out, in_, reads=(), writes=(), **kw):
        self._need(q, reads, writes)
        lst, i = self.dq[q]
        key = lst[i % len(lst)]
        self.dq[q][1] = i + 1
        if self.cnt[key]:
            self._wait(q, key, self.cnt[key])
        ins = self.engs[q].dma_start(out=out, in_=in_, **kw)
        self.ninst += 1
        self.cnt[key] += 16
        ins.then_inc(self.sems[key], 16)
        dep = (key, self.cnt[key])
        self._record(dep, reads, writes)
        return dep

    @staticmethod
    def _cost(kind, eng, call):
        def fsz(ap):
            n = 1
            for d in ap.shape[1:]:
                n *= d
            return n
        if kind == "dma":
            out = call[0]
            nb = fsz(out) * out.shape[0] * (2 if out.dtype == BF16 else 4)
            return 2000.0 + nb / 80.0
        name, args, kw = call
        if name == "matmul":
            n = fsz(kw["rhs"])
            passes = 4 if kw["lhsT"].dtype == F32 else 1
            return 70.0 + n * passes * 0.45
        out = kw.get("out", None)
        if out is None:
            out = kw.get("ap", args[0] if args else None)
        n = fsz(out) if out is not None else 64
        if eng == "act":
            return 230.0 + n * 0.75
        if name == "reciprocal":
            return 70.0 + n * 6.5
        if name == "memset":
            return 70.0 + n * 0.5
        return 70.0 + n * 1.1

    def flush(self):
        pend = self.pending
        self.pending = []
        if not pend:
            return
        import heapq
        units = []
        cur = None
        for it in pend:
            kind, eng, call, rd, wr, inc = it
            if kind == "op" and eng == "pe":
                if cur is None:
                    cur = [eng, [], 0.0, set(), set()]
                cur[1].append(it)
                cur[2] += self._cost(kind, eng, call)
                cur[3].update(rd)
                cur[4].update(wr)
                if inc:
                    units.append(cur)
                    cur = None
            else:
                assert cur is None, "non-PE op inside an open PE group"
                units.append([eng, [it], self._cost(kind, eng, call), set(rd), set(wr)])
        assert cur is None, "PE group without final inc"
        n = len(units)
        lastw = {}
        readers = {}
        deps = [None] * n
        succ = [[] for _ in range(n)]
        for i, u in enumerate(units):
            d = set()
            for b in u[3]:
                if b in lastw:
                    d.add(lastw[b])
            for b in u[4]:
                if b in lastw:
                    d.add(lastw[b])
                for r in readers.get(b, ()):
                    d.add(r)
            d.discard(i)
            deps[i] = d
            for j in d:
                succ[j].append(i)
            for b in u[3]:
                readers.setdefault(b, []).append(i)
            for b in u[4]:
                lastw[b] = i
                readers[b] = []
        ndep = [len(d) for d in deps]
        ready_t = [0.0] * n
        fin = [0.0] * n
        free = {}
        heaps = {}
        for i in range(n):
            if ndep[i] == 0:
                heapq.heappush(heaps.setdefault(units[i][0], []), (0.0, i))
        order = []
        done = 0
        while done < n:
            best = None
            for e, h in heaps.items():
                if not h:
                    continue
                rt, i = h[0]
                st_ = max(rt, free.get(e, 0.0))
                if best is None or (st_, i) < (best[0], best[1]):
                    best = (st_, i, e)
            st_, i, e = best
            heapq.heappop(heaps[e])
            u = units[i]
            if e in ("sp", "pool") or (e == "act" and u[1][0][0] == "dma"):
                free[e] = st_ + 60.0
                fin[i] = st_ + u[2]
            else:
                fin[i] = st_ + u[2]
                free[e] = fin[i]
            order.append((st_, i))
            done += 1
            for j in succ[i]:
                ndep[j] -= 1
                if fin[i] > ready_t[j]:
                    ready_t[j] = fin[i]
                if ndep[j] == 0:
                    heapq.heappush(heaps.setdefault(units[j][0], []), (ready_t[j], j))
        if self.stats is not None:
            mk = max(fin) if fin else 0.0
            busy = {}
            for u in units:
                busy[u[0]] = busy.get(u[0], 0.0) + u[2]
            st = self.stats.setdefault(self.label, [0.0, {}, 0])
            st[0] += mk
            st[2] += n
            for e_, v_ in busy.items():
                st[1][e_] = st[1].get(e_, 0.0) + v_
        order.sort()
        for _, i in order:
            for kind, eng, call, rd, wr, inc in units[i][1]:
                if kind == "op":
                    name, args, kw = call
                    self._op_now(eng, lambda en: getattr(en, name)(*args, **kw), rd, wr, inc)
                else:
                    out, in_, kw = call
                    self._dma_now(eng, out, in_, rd, wr, **kw)

    def barrier(self, include_pool_dma=False):
        self.flush()
        keys = ["pe", "act", "dve", "pool"] + self.dq["sp"][0] + self.dq["act"][0]
        if include_pool_dma:
            keys += self.dq["pool"][0]
        for e in ("pe", "act", "dve", "pool", "sp"):
            for k in keys:
                if (k == e and e == "pe") or self.cnt[k] == 0:
                    continue
                self._wait(e, k, self.cnt[k])


def bcast(ap, shape, axis):
    return ap.unsqueeze(axis).broadcast_to(list(shape))


IN_SPECS = [
    ("xp", [512, 1024]), ("xs", [2048, 1024]), ("cvec", [2, 1024]),
    ("cdk", [4, 256, 512]), ("cdv", [4, 256, 512]), ("cgk", [4, 256, 128]), ("cgv", [4, 256, 128]),
    ("sssm", [4, 2, 8, 64, 64]), ("smc", [4, 2, 4, 128, 128]), ("smn", [4, 2, 4, 128]), ("smm", [4, 8]),
    ("w_ada", [4, 1024, 6144]), ("b_ada", [4, 6144]), ("norm1", [4, 1024]), ("norm2", [4, 1024]),
    ("w_in", [4, 1024, IN_COLS]), ("w_out", [4, 2048, 1024]),
    ("conv_ssd_w", [4, 5, 768]), ("conv_ssd_b", [4, 768]), ("ssd_a_log", [4, 16]), ("ssd_dt_bias", [4, 16]),
    ("ssd_d", [4, 8]), ("ssd_norm", [4, 512]),
    ("diff_lq1", [4, 64]), ("diff_lk1", [4, 64]), ("diff_lq2", [4, 64]), ("diff_lk2", [4, 64]),
    ("conv_mlstm_w", [4, 5, 1024]), ("conv_mlstm_b", [4, 1024]), ("mlstm_gate_b", [4, 16]), ("mlstm_norm", [4, 512]),
    ("gqa_q_norm", [4, 64]), ("gqa_k_norm", [4, 64]),
    ("w_ffn_in", [4, 1024, 2 * D_FF]), ("w_ffn_out", [4, D_FF, 1024]), ("norm_f", [1024]),
    ("consts", [128, 1152]), ("rope", [128, 2, 2048]),
]
OUT_SPECS = [
    ("yp", [512, 1024]), ("ys", [2048, 1024]),
    ("ndk", [2, 4, 256, 512]), ("ndv", [2, 4, 256, 512]), ("ngk", [2, 4, 256, 128]), ("ngv", [2, 4, 256, 128]),
    ("nssm", [2, 4, 2, 8, 64, 64]), ("nmc", [2, 4, 2, 4, 128, 128]), ("nmn", [2, 4, 2, 4, 128]), ("nmm", [2, 4, 8]),
]


def make_consts():
    c = np.zeros((128, 1152), np.float32)
    k = np.arange(128)
    c[:, 0:128] = np.eye(128)
    c[:, 128:256] = 1.0
    c[:, 256:384] = (k[:, None] <= k[None, :])
    c[:, 384:512] = (k[:, None] >= k[None, :])
    c[:, 512:640] = np.where(k[:, None] <= k[None, :], 0.0, NEG)
    c[:, 640:768] = np.where(k[:, None] >= k[None, :], 0.0, NEG)
    c[:, 768:896] = (k[:, None] // 64 == k[None, :] // 64)
    rm = np.zeros((128, 128), np.float32)
    for dp in range(128):
        half = (dp % 32) // 16
        if half == 0:
            rm[dp + 16, dp] = -1.0
        else:
            rm[dp - 16, dp] = 1.0
    c[:, 896:1024] = rm
    c[64, 1024:1088] = 1.0
    c[0, 1088:1152] = 1.0
    return c


def make_rope():
    t = np.arange(2048)
    r = (t // 64).astype(np.float32)
    cc = (t % 64).astype(np.float32)
    nf = 16
    freqs = (10000.0 ** (-np.arange(nf, dtype=np.float32) / nf)).astype(np.float32)
    ang = np.stack([r[:, None] * freqs, cc[:, None] * freqs], axis=1).astype(np.float32)
    out = np.zeros((128, 2, 2048), np.float32)
    for p in range(128):
        d = p % 64
        a = d // 32
        f = d % 16
        out[p, 0] = np.cos(ang[:, a, f])
        out[p, 1] = np.sin(ang[:, a, f])
    return out


def build_program():
    nc = bass.Bass("TRN2", target_bir_lowering=False)
    kb = KB(nc)
    D = {}
    for name, shape in IN_SPECS:
        if shape[0] == 4 and len(shape) > 1:
            shape = [NL] + list(shape[1:])
        D[name] = nc.dram_tensor(name, shape, F32, kind="ExternalInput").ap()
    for name, shape in OUT_SPECS:
        D[name] = nc.dram_tensor(name, shape, F32, kind="ExternalOutput").ap()
    mod_d = nc.dram_tensor("mod_scr", [4, 2, 6144], F32, kind="Internal").ap()

    top = ExitStack()

    class scope:
        def __enter__(self_):
            self_.st = ExitStack()
            return self_.st

        def __exit__(self_, *a):
            if a[0] is None:
                kb.barrier()
            self_.st.close()
            return False

    uid = [0]

    def alloc(stack, name, shape, dt, psum=False):
        uid[0] += 1
        name = "%s_%d" % (name, uid[0])
        cm = nc.psum_tensor(name, shape, dt) if psum else nc.sbuf_tensor(name, shape, dt)
        return Buf(stack.enter_context(cm))

    xT = [alloc(top, "xT%d" % i, [128, 8, 512], F32) for i in range(4)]
    hT = [alloc(top, "hT%d" % i, [128, 8, 512], BF16) for i in range(4)]
    NW = 2
    wbufs = [alloc(top, "wb%d" % i, [128, 4096], BF16) for i in range(NW)]
    wstate = [0]
    psb = [alloc(top, "ps%d" % i, [128, 512], F32, psum=True) for i in range(8)]
    pstate = [0]
    cf = alloc(top, "cf", [128, 640], F32)
    cb = alloc(top, "cb", [128, 1024], BF16)
    modc = alloc(top, "modc", [128, 6, 8], F32)
    nrm = alloc(top, "nrm", [128, 2, 8], F32)
    AB = alloc(top, "AB", [128, 4, 8], F32)
    nfc = alloc(top, "nfc", [128, 8], F32)
    lnsb = alloc(top, "lnsb", [128, 1], F32)
    lns_col = lnsb.t

    wo_buf = alloc(top, "wo_buf", [128, 4, 1024], BF16)
    accstate = [0]

    def ps():
        b = psb[pstate[0] % 4]
        pstate[0] += 1
        return b

    def ps_acc():
        b = psb[4 + accstate[0] % 4]
        accstate[0] += 1
        return b

    def wload(pieces, kch):
        b = wbufs[wstate[0] % NW]
        wstate[0] += 1
        ntot = sum(n for _, n in pieces)
        assert kch * ntot <= 4096, (kch, ntot)
        view = b.t[:, 0:kch * ntot].rearrange("p (k n) -> p k n", k=kch)
        o = 0
        for ap, n in pieces:
            kb.dma("pool", view[:, :, o:o + n], ap.rearrange("(k p) n -> p k n", p=128), writes=[b])
            o += n
        return b, view

    ident_f = cf.t[:, 0:128]
    ones_f = cf.t[:, 128:256]
    ident_b = cb.t[:, 0:128]
    selm_f = cf.t[:, 512:640]
    ones_b = cb.t[:, 128:256]

    kb.dma("sp", cf[:, 0:512], D["consts"][:, 0:512], writes=[cf])
    kb.dma("sp", cf[:, 512:640], D["consts"][:, 1024:1152], writes=[cf])
    kb.dma("pool", cb[:], D["consts"][:, 0:1024], writes=[cb])
    kb.op("dve", lambda e: e.memset(lnsb[:], math.log(128 ** -0.5)), writes=[lnsb])
    kb.dma("sp", nfc[:], D["norm_f"].rearrange("(c p) -> p c", p=128), writes=[nfc], allow_slow_non_contiguous=True)

    modall = alloc(top, "modall", [128, NL, 48, 2], F32)
    ball = alloc(top, "ball", [128, NL, 48], F32)
    nrmall = alloc(top, "nrmall", [128, NL, 2, 8], F32)
    for l in range(NL):
        kb.dma("sp", ball[:, l, :], D["b_ada"][l].rearrange("(j p) -> p j", p=128), writes=[ball], allow_slow_non_contiguous=True)
        kb.dma("sp", nrmall[:, l, 0, :], D["norm1"][l].rearrange("(c p) -> p c", p=128), writes=[nrmall], allow_slow_non_contiguous=True)
        kb.dma("sp", nrmall[:, l, 1, :], D["norm2"][l].rearrange("(c p) -> p c", p=128), writes=[nrmall], allow_slow_non_contiguous=True)
    with scope() as st:
        cT = alloc(st, "cT", [128, 2, 8], F32)
        cTb = alloc(st, "cTb", [128, 2, 8], BF16)
        sig = alloc(st, "csig", [128, 2, 8], F32)
        for g in range(2):
            kb.dma("sp", cT[:, g, :], D["cvec"][g].rearrange("(c p) -> p c", p=128), writes=[cT], allow_slow_non_contiguous=True)
        kb.op("act", lambda e: e.activation(out=sig[:], in_=cT[:], func=AF.Sigmoid), reads=[cT], writes=[sig])
        kb.op("dve", lambda e: e.tensor_tensor(out=cTb[:], in0=cT[:], in1=sig[:], op=ALU.mult), reads=[cT, sig], writes=[cTb])
        for l in range(NL):
            for blk in range(12):
                c0 = blk * 512
                wb, wv = wload([(D["w_ada"][l][:, c0:c0 + 512], 512)], 8)
                p = ps()
                for cc in range(4):
                    for k in range(8):
                        kb.op("pe", lambda e: e.matmul(p[:, 2 * cc:2 * cc + 2], lhsT=wv[:, k, cc * 128:(cc + 1) * 128], rhs=cTb[:, :, k], start=(k == 0), stop=(k == 7)),
                              reads=[cTb, wb], writes=[p], inc=(k == 7 and cc == 3))
                kb.op("dve", lambda e: e.tensor_tensor(out=modall[:, l, blk * 4:(blk + 1) * 4, :], in0=p[:, 0:8].rearrange("p (j g) -> p j g", g=2),
                                                       in1=bcast(ball[:, l, blk * 4:(blk + 1) * 4], [128, 4, 2], 2), op=ALU.add), reads=[p, ball], writes=[modall])
        kb.barrier()

    def load_x(src, nblk):
        with scope() as st:
            xin = [alloc(st, "xin%d" % i, [128, 1024], F32) for i in range(2)]
            for tb in range(nblk):
                tiles = []
                for tl in range(4):
                    pass
                for tl in range(4):
                    xi = xin[(tb * 4 + tl) % 2]
                    t0 = (tb * 4 + tl) * 128
                    kb.dma("sp", xi[:], src[t0:t0 + 128, :], writes=[xi])
                    for half in range(2):
                        p = ps()
                        for cc in range(4):
                            c = half * 4 + cc
                            kb.op("pe", lambda e: e.matmul(p[:, cc * 128:(cc + 1) * 128], lhsT=xi[:, c * 128:(c + 1) * 128], rhs=ident_f,
                                                           start=True, stop=True), reads=[xi, cf], writes=[p], inc=(cc == 3))
                        kb.op("act", lambda e: e.activation(
                            out=xT[tb][:, half * 4:half * 4 + 4, tl * 128:(tl + 1) * 128],
                            in_=p[:, :].rearrange("p (c t) -> p c t", c=4), func=AF.Copy), reads=[p], writes=[xT[tb]])

    def norm_block(st_tmp, tb, Acol, Bcol, dst, dst_dt_is_bf16=True):
        sq, rstd, tmp2 = st_tmp
        if isinstance(sq, list):
            sq, rstd = sq[tb % 2], rstd[tb % 2]
        kb.op("act", lambda e: e.activation(out=sq[:], in_=xT[tb][:], func=AF.Square), reads=[xT[tb]], writes=[sq])
        p = ps()
        for c in range(8):
            kb.op("pe", lambda e: e.matmul(p[:, :], lhsT=ones_b, rhs=sq[:, c, :], start=(c == 0), stop=(c == 7)),
                  reads=[sq, cb], writes=[p], inc=(c == 7))
        kb.op("dve", lambda e: e.tensor_scalar(out=rstd[:], in0=p[:, :], scalar1=1.0 / D_MODEL, scalar2=EPS, op0=ALU.mult, op1=ALU.add),
              reads=[p], writes=[rstd])
        kb.op("act", lambda e: e.activation(out=rstd[:], in_=rstd[:], func=AF.Sqrt), reads=[rstd], writes=[rstd])
        kb.op("dve", lambda e: e.reciprocal(out=rstd[:], in_=rstd[:]), reads=[rstd], writes=[rstd])
        for c in range(8):
            t2 = tmp2[c % len(tmp2)]
            kb.op("dve", lambda e: e.tensor_tensor(out=t2[:], in0=xT[tb][:, c, :], in1=rstd[:], op=ALU.mult),
                  reads=[xT[tb], rstd], writes=[t2])
            if Bcol is not None:
                kb.op("act", lambda e: e.activation(out=dst[:, c, :], in_=t2[:], func=AF.Identity, bias=Bcol[:, c:c + 1], scale=Acol[:, c:c + 1]),
                      reads=[t2, AB], writes=[dst])
            else:
                kb.op("act", lambda e: e.activation(out=dst[:, c, :], in_=t2[:], func=AF.Identity, scale=Acol[:, c:c + 1]),
                      reads=[t2, nfc], writes=[dst])

    def load_mod(l, g):
        kb.op("dve", lambda e: e.tensor_copy(out=modc[:], in_=modall[:, l, :, g].rearrange("p (v c) -> p v c", v=6)), reads=[modall], writes=[modc])
        kb.op("dve", lambda e: e.tensor_copy(out=nrm[:], in_=nrmall[:, l, :, :]), reads=[nrmall], writes=[nrm])
        for j, (vs, vh) in enumerate(((1, 0), (4, 3))):
            kb.op("dve", lambda e: e.scalar_tensor_tensor(out=AB[:, 2 * j, :], in0=modc[:, vs, :], scalar=1.0, in1=nrm[:, j, :],
                                                          op0=ALU.add, op1=ALU.mult), reads=[modc, nrm], writes=[AB])
            kb.op("dve", lambda e: e.tensor_copy(out=AB[:, 2 * j + 1, :], in_=modc[:, vh, :]), reads=[modc], writes=[AB])

    def ffn(l, blocks):
        kb.label = 'ffn'
        nb = len(blocks)
        with scope() as st:
            actT = alloc(st, "actT", [128, 22, nb * 512], BF16)
            sg = [alloc(st, "sg%d" % i, [128, 512], F32) for i in range(2)]
            it = 0
            for jj in range(11):
                c0 = jj * 256
                wb, wv = wload([(D["w_ffn_in"][l][:, c0:c0 + 256], 256), (D["w_ffn_in"][l][:, D_FF + c0:D_FF + c0 + 256], 256)], 8)
                for j2 in range(2):
                    j = jj * 2 + j2
                    for bi, tb in enumerate(blocks):
                        pg = ps()
                        pu = ps()
                        for k in range(8):
                            kb.op("pe", lambda e: e.matmul(pg[:, :], lhsT=wv[:, k, j2 * 128:(j2 + 1) * 128], rhs=hT[tb][:, k, :],
                                                           start=(k == 0), stop=(k == 7)), reads=[wb, hT[tb]], writes=[pg], inc=(k == 7))
                        for k in range(8):
                            kb.op("pe", lambda e: e.matmul(pu[:, :], lhsT=wv[:, k, 256 + j2 * 128:256 + (j2 + 1) * 128], rhs=hT[tb][:, k, :],
                                                           start=(k == 0), stop=(k == 7)), reads=[wb, hT[tb]], writes=[pu], inc=(k == 7))
                        s = sg[it % 2]
                        it += 1
                        kb.op("act", lambda e: e.activation(out=s[:], in_=pg[:, :], func=AF.Silu), reads=[pg], writes=[s])
                        kb.op("dve", lambda e: e.tensor_tensor(out=actT[:, j, bi * 512:(bi + 1) * 512], in0=s[:], in1=pu[:, :], op=ALU.mult),
                              reads=[s, pu], writes=[actT])
            for c in range(8):
                wb, wv = wload([(D["w_ffn_out"][l][:, c * 128:(c + 1) * 128], 128)], 22)
                for bi, tb in enumerate(blocks):
                    p = ps()
                    for k in range(22):
                        kb.op("pe", lambda e: e.matmul(p[:, :], lhsT=wv[:, k, :], rhs=actT[:, k, bi * 512:(bi + 1) * 512],
                                                       start=(k == 0), stop=(k == 21)), reads=[wb, actT], writes=[p], inc=(k == 21))
                    kb.op("dve", lambda e: e.scalar_tensor_tensor(out=xT[tb][:, c, :], in0=p[:, :], scalar=modc[:, 5, c:c + 1], in1=xT[tb][:, c, :],
                                                                  op0=ALU.mult, op1=ALU.add), reads=[p, modc, xT[tb]], writes=[xT[tb]])
        kb.barrier()

    def final_out(nblk, dst):
        kb.label = 'final'
        with scope() as st:
            sq = alloc(st, "sq", [128, 8, 512], BF16)
            rstd = alloc(st, "rstd", [128, 512], F32)
            tmp2 = [alloc(st, "tmpn%d" % i, [128, 512], F32) for i in range(2)]
            xn = alloc(st, "xn", [128, 8, 512], F32)
            ot = [alloc(st, "ot%d" % i, [128, 1024], F32) for i in range(2)]
            for tb in range(nblk):
                norm_block((sq, rstd, tmp2), tb, nfc, None, xn)
                for tl in range(4):
                    o = ot[tl % 2]
                    for half in range(2):
                        p = ps()
                        for cc in range(4):
                            c = half * 4 + cc
                            kb.op("pe", lambda e: e.matmul(p[:, cc * 128:(cc + 1) * 128], lhsT=xn[:, c, tl * 128:(tl + 1) * 128], rhs=ident_f,
                                                           start=True, stop=True), reads=[xn, cf], writes=[p], inc=(cc == 3))
                        kb.op("act", lambda e: e.activation(out=o[:, half * 512:(half + 1) * 512], in_=p[:, :], func=AF.Copy), reads=[p], writes=[o])
                    t0 = (tb * 4 + tl) * 128
                    kb.dma("sp", dst[t0:t0 + 128, :], o[:], reads=[o])
        kb.barrier()


    bd64_b = cb.t[:, 768:896]
    rm_b = cb.t[:, 896:1024]

    def proj_tm(wb, wv, s0, n, tb, tl, p):
        for k in range(8):
            kb.op("pe", lambda e: e.matmul(p[:, 0:n], lhsT=hT[tb][:, k, tl * 128:(tl + 1) * 128], rhs=wv[:, k, s0:s0 + n],
                                           start=(k == 0), stop=(k == 7)), reads=[wb, hT[tb]], writes=[p], inc=(k == 7))

    def load_wo(l, row0):
        kb.dma("pool", wo_buf[:], D["w_out"][l][row0:row0 + 512, :].rearrange("(k p) n -> p k n", p=128), writes=[wo_buf])

    def mixer_out(st, ytm, tb, yT):
        for c in range(4):
            p = ps()
            for tl in range(4):
                kb.op("pe", lambda e: e.matmul(p[:, tl * 128:(tl + 1) * 128], lhsT=ytm[:, tl, c * 128:(c + 1) * 128], rhs=ident_b,
                                               start=True, stop=True), reads=[ytm, cb], writes=[p], inc=(tl == 3))
            kb.op("act", lambda e: e.activation(out=yT[:, c, :], in_=p[:, :], func=AF.Copy), reads=[p], writes=[yT])
        for c in range(8):
            p = ps()
            for k in range(4):
                kb.op("pe", lambda e: e.matmul(p[:, :], lhsT=wo_buf[:, k, c * 128:(c + 1) * 128], rhs=yT[:, k, :],
                                               start=(k == 0), stop=(k == 3)), reads=[wo_buf, yT], writes=[p], inc=(k == 3))
            kb.op("dve", lambda e: e.scalar_tensor_tensor(out=xT[tb][:, c, :], in0=p[:, :], scalar=modc[:, 2, c:c + 1], in1=xT[tb][:, c, :],
                                                          op0=ALU.mult, op1=ALU.add), reads=[p, modc, xT[tb]], writes=[xT[tb]])

    def attention(l, g, kind, nblk):
        kb.label = 'attn_%s_g%d' % (kind, g)
        sample = (g == 1)
        L = 2048 if sample else 256
        nseq = 1 if sample else 2
        nctx = 2 if sample else 0
        lt = L // 128
        nkt = lt + nctx
        ntok = nblk * 512
        if kind == "d":
            qc0, kc0, vc0, nkc, nvh, ve, orow0, nheads = 4896, 5408, 5536, 1, 2, 64, 1536, 8
            ck, cv, ok, ov = D["cgk"], D["cgv"], D["ngk"], D["ngv"]
        else:
            qc0, kc0, vc0, nkc, nvh, ve, orow0, nheads = 1296, 1808, 2320, 4, 4, 128, 512, 4
            ck, cv, ok, ov = D["cdk"], D["cdv"], D["ndk"], D["ndv"]
        scale = 64 ** -0.5
        kw = nkc * 128
        vw = nvh * ve
        lam_init = 0.8 - 0.6 * math.exp(-0.3 * l)
        with scope() as st:
            qT = alloc(st, "qT", [128, 4, ntok], BF16)
            kT = alloc(st, "kT", [128, nkc, nseq * nkt * 128], BF16)
            vsw = ve + 1 if kind == "d" else ve
            vaug = alloc(st, "vaug", [128, nseq * nkt, nvh, vsw], BF16)
            vodd = alloc(st, "vodd", [128, nseq * nkt, nvh, 128], BF16) if kind == "d" else None
            yT = alloc(st, "yT", [128, 4, 512], BF16)
            pTs = [alloc(st, "pT%d" % i, [128, 512], BF16) for i in range(4)]
            sqb = alloc(st, "sqb", [128, 512], BF16)
            rs = alloc(st, "rs", [128, 512], F32)
            qn = alloc(st, "qn", [128, 512], BF16)
            t1 = alloc(st, "t1", [128, 512], F32)
            fin_bufs = [(rs, t1, None, sqb)]
            gcol = alloc(st, "gcol", [128, 2], F32)
            osb = alloc(st, "osb", [128, 512], F32)
            t2 = osb
            fin_bufs[0] = (rs, t1, osb, sqb)
            sm = alloc(st, "sm", [128, 16], F32)
            lamt = alloc(st, "lamt", [128, 4, 64], F32)
            kng = alloc(st, "kng", [128, 64], F32)
            ropeT = alloc(st, "ropeT", [128, 2, 2048], BF16) if sample else None
            if sample and kind == "b":
                pass
            else:
                try:
                    fin_bufs.append((alloc(st, "rs2", [128, 512], F32), alloc(st, "t12", [128, 512], F32),
                                     alloc(st, "osb2", [128, 512], F32) if kind == "b" else None, alloc(st, "sqb2", [128, 512], BF16) if kind == "b" else None))
                except AssertionError:
                    pass
            load_wo(l, orow0)
            if kind == "d":
                kb.op("dve", lambda e: e.memset(vaug[:, :, :, ve:ve + 1], 1.0), writes=[vaug])
                kb.op("dve", lambda e: e.memset(vodd[:, :, :, 0:1], 1.0), writes=[vodd])
                kb.op("dve", lambda e: e.memset(vodd[:, :, :, 1:64], 0.0), writes=[vodd])
            if sample:
                kb.dma("pool", ropeT[:], D["rope"], writes=[ropeT])
            if kind == "d":
                for j, nm in enumerate(("gqa_q_norm", "gqa_k_norm")):
                    for hh in range(2):
                        kb.dma("sp", gcol[hh * 64:(hh + 1) * 64, j:j + 1], D[nm][l].rearrange("(d o) -> d o", o=1), writes=[gcol])
                kb.dma("sp", kng[:], D["gqa_k_norm"][l].partition_broadcast(128), writes=[kng])
            else:
                for j, nm in enumerate(("diff_lq1", "diff_lk1", "diff_lq2", "diff_lk2")):
                    kb.dma("sp", lamt[:, j, :], D[nm][l].partition_broadcast(128), writes=[lamt])
                kb.op("dve", lambda e: e.tensor_tensor(out=lamt[:, 0, :], in0=lamt[:, 0, :], in1=lamt[:, 1, :], op=ALU.mult), reads=[lamt], writes=[lamt])
                kb.op("dve", lambda e: e.tensor_tensor(out=lamt[:, 2, :], in0=lamt[:, 2, :], in1=lamt[:, 3, :], op=ALU.mult), reads=[lamt], writes=[lamt])
                kb.op("dve", lambda e: e.tensor_reduce(out=sm[:, 2:3], in_=lamt[:, 0, :], axis=AX.X, op=ALU.add), reads=[lamt], writes=[sm])
                kb.op("dve", lambda e: e.tensor_reduce(out=sm[:, 3:4], in_=lamt[:, 2, :], axis=AX.X, op=ALU.add), reads=[lamt], writes=[sm])
                kb.op("act", lambda e: e.activation(out=sm[:, 2:4], in_=sm[:, 2:4], func=AF.Exp), reads=[sm], writes=[sm])
                kb.op("dve", lambda e: e.tensor_tensor(out=sm[:, 0:1], in0=sm[:, 2:3], in1=sm[:, 3:4], op=ALU.subtract), reads=[sm], writes=[sm])
                kb.op("dve", lambda e: e.tensor_scalar(out=sm[:, 1:2], in0=sm[:, 0:1], scalar1=lam_init, scalar2=-1.0, op0=ALU.add, op1=ALU.mult), reads=[sm], writes=[sm])

            def qk_post(p, dst, tb, normj):
                src = p
                if kind == "d":
                    kb.op("act", lambda e: e.activation(out=sqb[:], in_=p[:, :], func=AF.Square), reads=[p], writes=[sqb])
                    pn = ps()
                    kb.op("pe", lambda e: e.matmul(pn[:, :], lhsT=bd64_b, rhs=sqb[:], start=True, stop=True), reads=[cb, sqb], writes=[pn])
                    kb.op("dve", lambda e: e.tensor_scalar(out=rs[:], in0=pn[:, :], scalar1=1.0 / 64, scalar2=EPS, op0=ALU.mult, op1=ALU.add), reads=[pn], writes=[rs])
                    kb.op("act", lambda e: e.activation(out=rs[:], in_=rs[:], func=AF.Sqrt), reads=[rs], writes=[rs])
                    kb.op("dve", lambda e: e.reciprocal(out=rs[:], in_=rs[:]), reads=[rs], writes=[rs])
                    tgt = qn if sample else None
                    o_ap = qn[:] if sample else dst
                    kb.op("dve", lambda e: e.scalar_tensor_tensor(out=o_ap, in0=p[:, :], scalar=gcol[:, normj:normj + 1], in1=rs[:], op0=ALU.mult, op1=ALU.mult),
                          reads=[p, gcol, rs], writes=[qn if sample else dst_buf[0]])
                else:
                    o_ap = qn[:] if sample else dst
                    kb.op("act", lambda e: e.activation(out=o_ap, in_=p[:, :], func=AF.Copy), reads=[p], writes=[qn if sample else dst_buf[0]])
                if sample:
                    pr = ps()
                    kb.op("pe", lambda e: e.matmul(pr[:, :], lhsT=rm_b, rhs=qn[:], start=True, stop=True), reads=[cb, qn], writes=[pr])
                    kb.op("dve", lambda e: e.tensor_tensor(out=t1[:], in0=qn[:], in1=ropeT[:, 0, tb * 512:(tb + 1) * 512], op=ALU.mult), reads=[qn, ropeT], writes=[t1])
                    kb.op("dve", lambda e: e.tensor_tensor(out=t2[:], in0=pr[:, :], in1=ropeT[:, 1, tb * 512:(tb + 1) * 512], op=ALU.mult), reads=[pr, ropeT], writes=[t2])
                    kb.op("dve", lambda e: e.tensor_tensor(out=dst, in0=t1[:], in1=t2[:], op=ALU.add), reads=[t1, t2], writes=[dst_buf[0]])

            dst_buf = [None]
            blocks = list(range(nblk))
            if kind == "d":
                pcs = []
                for j in range(4):
                    for hh in (j, 4 + j):
                        pcs.append((D["w_in"][l][:, qc0 + hh * 64:qc0 + (hh + 1) * 64], 64))
                wb, wv = wload(pcs, 8)
            else:
                wb, wv = wload([(D["w_in"][l][:, qc0:qc0 + 512], 512)], 8)
            dst_buf[0] = qT
            for j in range(4):
                for tb in blocks:
                    p = ps()
                    for k in range(8):
                        lh = wv[:, k, j * 128:(j + 1) * 128]
                        kb.op("pe", lambda e: e.matmul(p[:, :], lhsT=lh, rhs=hT[tb][:, k, :], start=(k == 0), stop=(k == 7)),
                              reads=[wb, hT[tb]], writes=[p], inc=(k == 7))
                    qk_post(p, qT[:, j, tb * 512:(tb + 1) * 512], tb, 0)
            wb, wv = wload([(D["w_in"][l][:, kc0:kc0 + kw], kw)], 8)
            dst_buf[0] = kT
            for j in range(nkc):
                for tb in blocks:
                    p = ps()
                    for k in range(8):
                        kb.op("pe", lambda e: e.matmul(p[:, :], lhsT=wv[:, k, j * 128:(j + 1) * 128], rhs=hT[tb][:, k, :], start=(k == 0), stop=(k == 7)),
                              reads=[wb, hT[tb]], writes=[p], inc=(k == 7))
                    qk_post(p, kT[:, j, nctx * 128 + tb * 512:nctx * 128 + (tb + 1) * 512], tb, 1)
            if not sample:
                for tt in range(ntok // 128):
                    sq_, r0 = divmod(tt, lt)
                    p = ps()
                    proj_tm(wb, wv, 0, kw, tt // 4, tt % 4, p)
                    kb.op("act", lambda e: e.activation(out=osb[:, 0:kw], in_=p[:, 0:kw], func=AF.Copy), reads=[p], writes=[osb])
                    if kind == "d":
                        kb.op("dve", lambda e: e.tensor_tensor(out=t1[:, 0:128], in0=osb[:, 0:128], in1=osb[:, 0:128], op=ALU.mult), reads=[osb], writes=[t1])
                        kb.op("dve", lambda e: e.tensor_reduce(out=sm[:, 8:10], in_=t1[:, 0:128].rearrange("p (h d) -> p h d", d=64), axis=AX.X, op=ALU.add), reads=[t1], writes=[sm])
                        kb.op("dve", lambda e: e.tensor_scalar(out=sm[:, 8:10], in0=sm[:, 8:10], scalar1=1.0 / 64, scalar2=EPS, op0=ALU.mult, op1=ALU.add), reads=[sm], writes=[sm])
                        kb.op("act", lambda e: e.activation(out=sm[:, 8:10], in_=sm[:, 8:10], func=AF.Sqrt), reads=[sm], writes=[sm])
                        kb.op("dve", lambda e: e.reciprocal(out=sm[:, 8:10], in_=sm[:, 8:10]), reads=[sm], writes=[sm])
                        kb.op("dve", lambda e: e.tensor_tensor(out=t1[:, 0:128].rearrange("p (h d) -> p h d", d=64), in0=osb[:, 0:128].rearrange("p (h d) -> p h d", d=64),
                                                               in1=bcast(sm[:, 8:10], [128, 2, 64], 2), op=ALU.mult), reads=[osb, sm], writes=[t1])
                        kb.op("dve", lambda e: e.tensor_tensor(out=t2[:, 0:128].rearrange("p (h d) -> p h d", d=64), in0=t1[:, 0:128].rearrange("p (h d) -> p h d", d=64),
                                                               in1=bcast(kng[:], [128, 2, 64], 1), op=ALU.mult), reads=[t1, kng], writes=[t2])
                        kb.dma("sp", ok[sq_, l, r0 * 128:(r0 + 1) * 128, :], t2[:, 0:128], reads=[t2])
                    else:
                        kb.dma("sp", ok[sq_, l, r0 * 128:(r0 + 1) * 128, :], osb[:, 0:kw], reads=[osb])
            wb, wv = wload([(D["w_in"][l][:, vc0:vc0 + vw], vw)], 8)
            for tt in range(ntok // 128):
                sq_, r0 = divmod(tt, lt)
                ktile = sq_ * nkt + nctx + r0
                p = ps()
                proj_tm(wb, wv, 0, vw, tt // 4, tt % 4, p)
                kb.op("act", lambda e: e.activation(out=vaug[:, ktile, :, 0:ve], in_=p[:, 0:vw].rearrange("p (h e) -> p h e", e=ve), func=AF.Copy), reads=[p], writes=[vaug])
                if kind == "d":
                    kb.op("act", lambda e: e.activation(out=vodd[:, ktile, :, 64:128], in_=p[:, 0:vw].rearrange("p (h e) -> p h e", e=ve), func=AF.Copy), reads=[p], writes=[vodd])
                if not sample:
                    kb.op("dve", lambda e: e.tensor_copy(out=osb[:, 0:vw], in_=p[:, 0:vw]), reads=[p], writes=[osb])
                    kb.dma("sp", ov[sq_, l, r0 * 128:(r0 + 1) * 128, :], osb[:, 0:vw], reads=[osb])
            if sample:
                with scope() as st2:
                    ctxk = alloc(st2, "ctxk", [128, kw], F32)
                    for t in range(2):
                        kb.dma("sp", ctxk[:], ck[l][t * 128:(t + 1) * 128, :], writes=[ctxk])
                        for j in range(nkc):
                            p = ps()
                            kb.op("pe", lambda e: e.matmul(p[:, 0:128], lhsT=ctxk[:, j * 128:(j + 1) * 128], rhs=ident_f, start=True, stop=True),
                                  reads=[ctxk, cf], writes=[p])
                            kb.op("act", lambda e: e.activation(out=kT[:, j, t * 128:(t + 1) * 128], in_=p[:, 0:128], func=AF.Copy), reads=[p], writes=[kT])
                    for t in range(2):
                        kb.dma("pool", vaug[:, t, :, 0:ve], cv[l][t * 128:(t + 1) * 128, :].rearrange("p (h e) -> p h e", e=ve), writes=[vaug])
                        if kind == "d":
                            kb.dma("pool", vodd[:, t, :, 64:128], cv[l][t * 128:(t + 1) * 128, :].rearrange("p (h e) -> p h e", e=ve), writes=[vodd])
                    kb.barrier(include_pool_dma=True)
            qblk = min(L, 512)
            pti = 0
            for s_ in range(nseq):
                for qb in range(L // qblk):
                    q0 = s_ * L + qb * qblk
                    qs = slice(q0, q0 + qblk)
                    yc = slice(q0 % 512, q0 % 512 + qblk)
                    its = []
                    if kind == "d":
                        for hp in range(4):
                            for kt in range(nkt):
                                its.append((hp, kt, 0))
                                its.append((hp + 4, kt, 0))
                    else:
                        for h in range(nheads):
                            for kt in range(nkt):
                                its.append((h, kt, 0))
                                its.append((h, kt, 1))
                    hstate = {}
                    pts = {}
                    DEPTH = 2

                    def stageA2(j):
                        pss = []
                        for i in (2 * j, 2 * j + 1):
                            h, kt, r = its[i]
                            kc = (s_ * nkt + kt) * 128
                            if kind == "d":
                                rows, qch = (h // 4) * 64, h % 4
                                lhs = kT[rows:rows + 64, 0, kc:kc + 128]
                                rh = qT[rows:rows + 64, qch, qs]
                            else:
                                lhs = kT[r * 64:(r + 1) * 64, h, kc:kc + 128]
                                rh = qT[r * 64:(r + 1) * 64, h, qs]
                            pS = ps()
                            kb.op("pe", lambda e: e.matmul(pS[:, 0:qblk], lhsT=lhs, rhs=rh, start=True, stop=True), reads=[kT, qT], writes=[pS], inc=(i % 2 == 1))
                            pss.append(pS)
                        for i, pS in zip((2 * j, 2 * j + 1), pss):
                            pT = pTs[i % 4]
                            kb.op("act", lambda e: e.activation(out=pT[:, 0:qblk], in_=pS[:, 0:qblk], func=AF.Exp, scale=scale), reads=[pS], writes=[pT])
                            pts[i] = pT

                    def stageC(i):
                        h, kt, r = its[i]
                        pT = pts.pop(i)
                        last = (kt == nkt - 1)
                        rs, t1, osb, sqb = fin_bufs[h % len(fin_bufs)]
                        if kind == "d":
                            vh, odd = h // 4, h % 2
                            if kt == 0:
                                hstate[h] = ps_acc()
                            accO = hstate[h]
                            mo = 128 if odd else ve + 1
                            lh = vodd[:, s_ * nkt + kt, vh, :] if odd else vaug[:, s_ * nkt + kt, vh, :]
                            kb.op("pe", lambda e: e.matmul(accO[0:mo, 0:qblk], lhsT=lh, rhs=pT[:, 0:qblk], start=(kt == 0), stop=last),
                                  reads=[pT, vodd if odd else vaug], writes=[accO], inc=True)
                            if last:
                                drow = 0 if odd else 64
                                orow = 64 if odd else 0
                                kb.op("act", lambda e: e.activation(out=rs[drow:drow + 1, 0:qblk], in_=accO[drow:drow + 1, 0:qblk], func=AF.Copy), reads=[accO], writes=[rs])
                                kb.op("dve", lambda e: e.reciprocal(out=rs[drow:drow + 1, 0:qblk], in_=rs[drow:drow + 1, 0:qblk]), reads=[rs], writes=[rs])
                                pB = ps()
                                kb.op("pe", lambda e: e.matmul(pB[:, 0:qblk], lhsT=selm_f[drow:drow + 1, :], rhs=rs[drow:drow + 1, 0:qblk], start=True, stop=True),
                                      reads=[cf, rs], writes=[pB])
                                kb.op("act", lambda e: e.activation(out=t1[orow:orow + 64, 0:qblk], in_=pB[orow:orow + 64, 0:qblk], func=AF.Copy), reads=[pB], writes=[t1])
                                kb.op("dve", lambda e: e.tensor_tensor(out=yT[orow:orow + 64, h // 2, yc], in0=accO[orow:orow + 64, 0:qblk], in1=t1[orow:orow + 64, 0:qblk], op=ALU.mult),
                                      reads=[accO, t1], writes=[yT])
                        else:
                            if kt == 0 and r == 0:
                                hstate[h] = ([ps_acc(), ps_acc()], [ps_acc(), ps_acc()])
                            accO, accD = hstate[h]
                            kb.op("pe", lambda e: e.matmul(accO[r][:, 0:qblk], lhsT=vaug[:, s_ * nkt + kt, h, :], rhs=pT[:, 0:qblk], start=(kt == 0), stop=last),
                                  reads=[pT, vaug], writes=[accO[r]], inc=False)
                            kb.op("pe", lambda e: e.matmul(accD[r][:, 0:qblk], lhsT=ones_b, rhs=pT[:, 0:qblk], start=(kt == 0), stop=last),
                                  reads=[pT, cb], writes=[accD[r]], inc=True)
                            if last and r == 1:
                                A, Bt, O = rs, t1, osb
                                kb.op("dve", lambda e: e.reciprocal(out=A[:, 0:qblk], in_=accD[0][:, 0:qblk]), reads=[accD[0]], writes=[A])
                                kb.op("dve", lambda e: e.reciprocal(out=Bt[:, 0:qblk], in_=accD[1][:, 0:qblk]), reads=[accD[1]], writes=[Bt])
                                kb.op("dve", lambda e: e.tensor_tensor(out=O[:, 0:qblk], in0=accO[0][:, 0:qblk], in1=A[:, 0:qblk], op=ALU.mult), reads=[accO[0], A], writes=[O])
                                kb.op("dve", lambda e: e.tensor_tensor(out=Bt[:, 0:qblk], in0=accO[1][:, 0:qblk], in1=Bt[:, 0:qblk], op=ALU.mult), reads=[accO[1], Bt], writes=[Bt])
                                kb.op("dve", lambda e: e.scalar_tensor_tensor(out=O[:, 0:qblk], in0=Bt[:, 0:qblk], scalar=sm[:, 1:2], in1=O[:, 0:qblk], op0=ALU.mult, op1=ALU.add),
                                      reads=[Bt, sm, O], writes=[O])
                                kb.op("act", lambda e: e.activation(out=sqb[:, 0:qblk], in_=O[:, 0:qblk], func=AF.Square), reads=[O], writes=[sqb])
                                pn = ps()
                                kb.op("pe", lambda e: e.matmul(pn[:, 0:qblk], lhsT=ones_b, rhs=sqb[:, 0:qblk], start=True, stop=True), reads=[cb, sqb], writes=[pn])
                                kb.op("dve", lambda e: e.tensor_scalar(out=A[:, 0:qblk], in0=pn[:, 0:qblk], scalar1=1.0 / 128, scalar2=EPS, op0=ALU.mult, op1=ALU.add), reads=[pn], writes=[A])
                                kb.op("act", lambda e: e.activation(out=A[:, 0:qblk], in_=A[:, 0:qblk], func=AF.Sqrt), reads=[A], writes=[A])
                                kb.op("dve", lambda e: e.reciprocal(out=A[:, 0:qblk], in_=A[:, 0:qblk]), reads=[A], writes=[A])
                                kb.op("dve", lambda e: e.scalar_tensor_tensor(out=yT[:, h, yc], in0=O[:, 0:qblk], scalar=1.0 - lam_init, in1=A[:, 0:qblk], op0=ALU.mult, op1=ALU.mult),
                                      reads=[O, A], writes=[yT])

                    n_pairs = len(its) // 2
                    for j in range(n_pairs + 1):
                        if j < n_pairs:
                            stageA2(j)
                        if j >= 1:
                            stageC(2 * (j - 1))
                            stageC(2 * (j - 1) + 1)
                    if (q0 + qblk) % 512 == 0:
                        tb = (q0 + qblk) // 512 - 1
                        for c in range(8):
                            p = ps()
                            for k in range(4):
                                kb.op("pe", lambda e: e.matmul(p[:, :], lhsT=wo_buf[:, k, c * 128:(c + 1) * 128], rhs=yT[:, k, :],
                                                               start=(k == 0), stop=(k == 3)), reads=[wo_buf, yT], writes=[p], inc=(k == 3))
                            kb.op("dve", lambda e: e.scalar_tensor_tensor(out=xT[tb][:, c, :], in0=p[:, :], scalar=modc[:, 2, c:c + 1], in1=xT[tb][:, c, :],
                                                                          op0=ALU.mult, op1=ALU.add), reads=[p, modc, xT[tb]], writes=[xT[tb]])
        kb.barrier()

    triF_f = cf.t[:, 256:384]
    triB_f = cf.t[:, 384:512]
    maskF_b = cb.t[:, 512:640]
    maskB_b = cb.t[:, 640:768]

    def proj_fm(l, c0, ncol, blocks, fn):
        for t0 in range(0, ncol, 512):
            n = min(512, ncol - t0)
            wb, wv = wload([(D["w_in"][l][:, c0 + t0:c0 + t0 + n], n)], 8)
            for s0 in range(0, n, 128):
                w = min(128, n - s0)
                for tb in blocks:
                    p = ps()
                    for k in range(8):
                        kb.op("pe", lambda e: e.matmul(p[0:w, :], lhsT=wv[:, k, s0:s0 + w], rhs=hT[tb][:, k, :], start=(k == 0), stop=(k == 7)),
                              reads=[wb, hT[tb]], writes=[p], inc=(k == 7))
                    fn((t0 + s0) // 128, tb, p, w)

    def conv_chunk(convin, acc, cwt, cbt, cc, nseq, L, dst_ap_fn, dst_buf):
        for s_ in range(nseq):
            a = acc[:, s_ * L:(s_ + 1) * L]
            kb.op("dve", lambda e: e.tensor_scalar(out=a, in0=convin[:, s_, 0:L], scalar1=cwt[:, cc, 0:1], scalar2=None, op0=ALU.mult),
                  reads=[convin, cwt], writes=[acc])
            for tap in range(1, 5):
                kb.op("dve", lambda e: e.scalar_tensor_tensor(out=a, in0=convin[:, s_, tap:tap + L], scalar=cwt[:, cc, tap:tap + 1], in1=a,
                                                              op0=ALU.mult, op1=ALU.add), reads=[convin, cwt, acc], writes=[acc])
            if isinstance(dst_buf, list):
                for gg in range(2):
                    kb.op("act", lambda e: e.activation(out=dst_buf[gg][gg * 64:(gg + 1) * 64, s_ * L:(s_ + 1) * L], in_=acc[gg * 64:(gg + 1) * 64, s_ * L:(s_ + 1) * L],
                                                        func=AF.Silu, bias=cbt[gg * 64:(gg + 1) * 64, cc:cc + 1]), reads=[acc, cbt], writes=[dst_buf[gg]])
            else:
                kb.op("act", lambda e: e.activation(out=dst_ap_fn(s_), in_=a, func=AF.Silu, bias=cbt[:, cc:cc + 1]), reads=[acc, cbt], writes=[dst_buf])

    def evac_conv_in(convin, p, tb, nseq, L):
        if nseq == 1:
            kb.op("act", lambda e: e.activation(out=convin[:, 0, 2 + tb * 512:2 + (tb + 1) * 512], in_=p[:, :], func=AF.Copy), reads=[p], writes=[convin])
        else:
            for s_ in range(2):
                kb.op("act", lambda e: e.activation(out=convin[:, s_, 2:2 + L], in_=p[:, s_ * L:(s_ + 1) * L], func=AF.Copy), reads=[p], writes=[convin])

    def transpose_to_tm(src, dst_fn, dst_buf, T):
        for t0 in range(0, T, 4):
            p = ps()
            for j in range(4):
                kb.op("pe", lambda e: e.matmul(p[:, j * 128:(j + 1) * 128], lhsT=src[:, (t0 + j) * 128:(t0 + j + 1) * 128], rhs=ident_b, start=True, stop=True),
                      reads=[src, cb], writes=[p], inc=(j == 3))
            kb.op("act", lambda e: e.activation(out=dst_fn(t0, 4), in_=p[:, :].rearrange("p (t c) -> p t c", t=4), func=AF.Copy), reads=[p], writes=[dst_buf])

    def ssd(l, g, nblk):
        kb.label = 'ssd_g%d' % g
        sample = (g == 1)
        L = 2048 if sample else 256
        nseq = 1 if sample else 2
        lt = L // 128
        ntok = nblk * 512
        T = ntok // 128
        blocks = list(range(nblk))
        with scope() as st:
            xtm = alloc(st, "xtm", [128, T, 512], BF16)
            Btm = alloc(st, "Btm", [128, T, 128], BF16)
            BT = alloc(st, "BT", [128, ntok], BF16)
            CTz = [alloc(st, "CT%d" % i, [128, ntok], BF16) for i in range(2)]
            for i_ in range(2):
                kb.op("dve", lambda e: e.memset(CTz[i_][:], 0.0), writes=[CTz[i_]])
            dtt = alloc(st, "dtt", [128, T, 16], F32)
            dtA = alloc(st, "dtA", [128, T, 16], F32)
            cum = alloc(st, "cum", [128, T, 16], F32)
            tot = alloc(st, "tot", [128, T, 16], F32)
            ecum = alloc(st, "ecum", [128, T, 16], F32)
            wd = alloc(st, "wd", [128, T, 16], F32)
            bj = alloc(st, "bj", [128, T, 16], F32)
            dec = alloc(st, "dec", [128, T, 2, 4], F32)
            abc = alloc(st, "abc", [128, 16], F32)
            dtb = alloc(st, "dtb", [128, 16], F32)
            dsk = alloc(st, "dsk", [128, 8], F32)
            ng = alloc(st, "ng", [128, 512], F32)
            cwt = alloc(st, "cwt", [128, 6, 5], F32)
            cbt = alloc(st, "cbt", [128, 6], F32)
            load_wo(l, 0)
            kb.dma("sp", abc[:], D["ssd_a_log"][l].partition_broadcast(128), writes=[abc])
            kb.dma("sp", dtb[:], D["ssd_dt_bias"][l].partition_broadcast(128), writes=[dtb])
            kb.dma("sp", dsk[:], D["ssd_d"][l].partition_broadcast(128), writes=[dsk])
            kb.dma("sp", ng[:], D["ssd_norm"][l].partition_broadcast(128), writes=[ng])
            for tap in range(5):
                kb.dma("sp", cwt[:, :, tap], D["conv_ssd_w"][l, tap].rearrange("(c p) -> p c", p=128), writes=[cwt], allow_slow_non_contiguous=True)
            kb.dma("sp", cbt[:], D["conv_ssd_b"][l].rearrange("(c p) -> p c", p=128), writes=[cbt], allow_slow_non_contiguous=True)
            kb.op("act", lambda e: e.activation(out=abc[:], in_=abc[:], func=AF.Exp), reads=[abc], writes=[abc])
            kb.op("dve", lambda e: e.tensor_scalar(out=abc[:], in0=abc[:], scalar1=-1.0, scalar2=None, op0=ALU.mult), reads=[abc], writes=[abc])
            with scope() as st1:
                convins = [alloc(st1, "convin", [128, nseq, L + 4], F32) for _ in range(2)]
                acc = alloc(st1, "cacc", [128, ntok], F32)
                xcT = alloc(st1, "xcT", [128, ntok], BF16)
                for cv_ in convins:
                    kb.op("dve", lambda e: e.memset(cv_[:], 0.0), writes=[cv_])

                def cb_fn(ci, tb, p, w):
                    convin = convins[ci % 2]
                    evac_conv_in(convin, p, tb, nseq, L)
                    if tb != blocks[-1]:
                        return
                    if ci < 4:
                        conv_chunk(convin, acc, cwt, cbt, ci, nseq, L, lambda s_: xcT[:, s_ * L:(s_ + 1) * L], xcT)
                        transpose_to_tm(xcT, lambda t0, n: xtm[:, t0:t0 + n, ci * 128:(ci + 1) * 128], xtm, T)
                    elif ci == 4:
                        conv_chunk(convin, acc, cwt, cbt, ci, nseq, L, lambda s_: BT[:, s_ * L:(s_ + 1) * L], BT)
                        transpose_to_tm(BT, lambda t0, n: Btm[:, t0:t0 + n, :], Btm, T)
                    else:
                        conv_chunk(convin, acc, cwt, cbt, ci, nseq, L, None, CTz)
                proj_fm(l, 512, 768, blocks, cb_fn)
            kb.barrier()
            if SSD_PH < 2:
                return
            wb, wv = wload([(D["w_in"][l][:, 1280:1296], 16)], 8)
            for tt in range(T):
                p = ps()
                proj_tm(wb, wv, 0, 16, tt // 4, tt % 4, p)
                kb.op("dve", lambda e: e.tensor_tensor(out=dtt[:, tt, :], in0=p[:, 0:16], in1=dtb[:], op=ALU.add), reads=[p, dtb], writes=[dtt])
            kb.op("act", lambda e: e.activation(out=dtt[:], in_=dtt[:], func=AF.Exp), reads=[dtt], writes=[dtt])
            kb.op("act", lambda e: e.activation(out=dtt[:], in_=dtt[:], func=AF.Ln, bias=1.0), reads=[dtt], writes=[dtt])
            kb.op("dve", lambda e: e.tensor_tensor(out=dtA[:], in0=dtt[:], in1=bcast(abc[:], [128, T, 16], 1), op=ALU.mult), reads=[dtt, abc], writes=[dtA])
            for tt in range(T):
                p = ps()
                kb.op("pe", lambda e: e.matmul(p[:, 0:16], lhsT=triF_f, rhs=dtA[:, tt, :], start=True, stop=True), reads=[cf, dtA], writes=[p], inc=False)
                kb.op("pe", lambda e: e.matmul(p[:, 16:32], lhsT=triB_f, rhs=dtA[:, tt, :], start=True, stop=True), reads=[cf, dtA], writes=[p], inc=False)
                kb.op("pe", lambda e: e.matmul(p[:, 32:48], lhsT=ones_f, rhs=dtA[:, tt, :], start=True, stop=True), reads=[cf, dtA], writes=[p])
                kb.op("dve", lambda e: e.tensor_copy(out=cum[:, tt, 0:8], in_=p[:, 0:8]), reads=[p], writes=[cum])
                kb.op("dve", lambda e: e.tensor_copy(out=cum[:, tt, 8:16], in_=p[:, 24:32]), reads=[p], writes=[cum])
                kb.op("dve", lambda e: e.tensor_copy(out=tot[:, tt, :], in_=p[:, 32:48]), reads=[p], writes=[tot])
            kb.op("act", lambda e: e.activation(out=ecum[:], in_=cum[:], func=AF.Exp), reads=[cum], writes=[ecum])
            kb.op("dve", lambda e: e.tensor_tensor(out=wd[:], in0=tot[:], in1=cum[:], op=ALU.subtract), reads=[tot, cum], writes=[wd])
            kb.op("act", lambda e: e.activation(out=wd[:], in_=wd[:], func=AF.Exp), reads=[wd], writes=[wd])
            kb.op("dve", lambda e: e.tensor_tensor(out=wd[:], in0=wd[:], in1=dtt[:], op=ALU.mult), reads=[wd, dtt], writes=[wd])
            kb.op("act", lambda e: e.activation(out=bj[:], in_=dtt[:], func=AF.Ln), reads=[dtt], writes=[bj])
            kb.op("dve", lambda e: e.tensor_tensor(out=bj[:], in0=bj[:], in1=cum[:], op=ALU.subtract), reads=[bj, cum], writes=[bj])
            tot4 = tot[:].rearrange("p t (d h) -> p t d h", d=2)
            for gg in range(2):
                kb.op("act", lambda e: e.activation(out=dec[gg * 64:(gg + 1) * 64], in_=tot4[gg * 64:(gg + 1) * 64, :, :, gg * 4:(gg + 1) * 4], func=AF.Exp),
                      reads=[tot], writes=[dec])
            if SSD_PH < 3:
                kb.barrier()
                return
            with scope() as st2:
                Hprev = alloc(st2, "Hprev", [128, T, 2, 256], BF16)
                st3 = ExitStack()
                Hs = [alloc(st3, "Hs%d" % i, [128, 256], F32) for i in range(2)]
                xws = [alloc(st3, "xw%d" % i, [128, 512], BF16) for i in range(2)]
                hx = alloc(st3, "hx", [128, 2, 128], F32)
                ho = alloc(st3, "ho", [128, 128], F32)
                xi = 0
                for dr in range(2):
                    H = Hs[dr]
                    for s_ in range(nseq):
                        if sample:
                            for blk in range(2):
                                for two in range(2):
                                    kb.dma("sp", hx[two * 64:(two + 1) * 64, blk, :].rearrange("p (g n) -> p g n", g=2),
                                           D["sssm"][l, dr].rearrange("(g r) p n -> r p g n", g=2)[blk * 2 + two], writes=[hx])
                            for blk in range(2):
                                p = ps()
                                kb.op("pe", lambda e: e.matmul(p[:, 0:128], lhsT=hx[:, blk, :], rhs=ident_f, start=True, stop=True), reads=[hx, cf], writes=[p])
                                kb.op("act", lambda e: e.activation(out=H[:, blk * 128:(blk + 1) * 128], in_=p[:, 0:128], func=AF.Copy), reads=[p], writes=[H])
                        else:
                            kb.op("dve", lambda e: e.memset(H[:], 0.0), writes=[H])
                        order = range(lt) if dr == 0 else range(lt - 1, -1, -1)
                        for r0 in order:
                            tt = s_ * lt + r0
                            kb.op("act", lambda e: e.activation(out=Hprev[:, tt, dr, :], in_=H[:], func=AF.Copy), reads=[H], writes=[Hprev])
                            xw = xws[xi % 2]
                            xi += 1
                            kb.op("dve", lambda e: e.tensor_tensor(out=xw[:].rearrange("p (h d) -> p h d", d=64), in0=xtm[:, tt, :].rearrange("p (h d) -> p h d", d=64),
                                                                   in1=bcast(wd[:, tt, dr * 8:(dr + 1) * 8], [128, 8, 64], 2), op=ALU.mult), reads=[xtm, wd], writes=[xw])
                            p = ps()
                            kb.op("pe", lambda e: e.matmul(p[:, :], lhsT=Btm[:, tt, :], rhs=xw[:], start=True, stop=True), reads=[Btm, xw], writes=[p])
                            kb.op("dve", lambda e: e.tensor_tensor(out=H[:].rearrange("p (h d) -> p h d", d=64), in0=H[:].rearrange("p (h d) -> p h d", d=64),
                                                                   in1=bcast(dec[:, tt, dr, :], [128, 4, 64], 2), op=ALU.mult), reads=[H, dec], writes=[H])
                            for gg in range(2):
                                kb.op("dve", lambda e: e.tensor_tensor(out=H[gg * 64:(gg + 1) * 64, :], in0=H[gg * 64:(gg + 1) * 64, :],
                                                                       in1=p[gg * 64:(gg + 1) * 64, gg * 256:(gg + 1) * 256], op=ALU.add), reads=[H, p], writes=[H])
                        if not sample:
                            for blk in range(2):
                                p = ps()
                                kb.op("pe", lambda e: e.matmul(p[:, 0:128], lhsT=H[:, blk * 128:(blk + 1) * 128], rhs=ident_f, start=True, stop=True), reads=[H, cf], writes=[p])
                                kb.op("act", lambda e: e.activation(out=ho[:], in_=p[:, 0:128], func=AF.Copy), reads=[p], writes=[ho])
                                for two in range(2):
                                    kb.dma("sp", D["nssm"][s_, l, dr].rearrange("(g r) p n -> r p g n", g=2)[blk * 2 + two],
                                           ho[two * 64:(two + 1) * 64, :].rearrange("p (g n) -> p g n", g=2), reads=[ho])
                kb.barrier()
                st3.close()
                if SSD_PH < 4:
                    return
                Dg = alloc(st2, "Dg", [128, 16, 128], F32)
                Es = [alloc(st2, "E%d" % i, [128, 128], F32) for i in range(3)]
                Ms = [alloc(st2, "M%d" % i, [128, 128], BF16) for i in range(3)]
                ya = alloc(st2, "ya", [128, 512], F32)
                yu = alloc(st2, "yu", [128, 512], F32)
                zs = yu
                ytm = alloc(st2, "ytm", [128, 4, 512], BF16)
                yT = alloc(st2, "yT", [128, 4, 512], BF16)
                ss = alloc(st2, "ss", [128, 4], F32)
                wzb, wzv = wload([(D["w_in"][l][:, 0:512], 512)], 8)
                ei = 0
                for tt in range(T):
                    tk = slice(tt * 128, (tt + 1) * 128)
                    pz = ps_acc()
                    proj_tm(wzb, wzv, 0, 512, tt // 4, tt % 4, pz)
                    pBC = ps_acc()
                    for gg in range(2):
                        kb.op("pe", lambda e: e.matmul(pBC[:, gg * 128:(gg + 1) * 128], lhsT=BT[:, tk], rhs=CTz[gg][:, tk], start=True, stop=True),
                              reads=[BT, CTz[gg]], writes=[pBC], inc=(gg == 1))
                    kb.op("dve", lambda e: e.tensor_tensor(out=Dg[:], in0=bcast(ident_f, [128, 16, 128], 1), in1=bcast(cum[:, tt, :], [128, 16, 128], 2), op=ALU.mult),
                          reads=[cf, cum], writes=[Dg])
                    yint = ps_acc()
                    hd = [(h, dr) for h in range(8) for dr in range(2)]
                    mts = {}

                    def sA(i):
                        h, dr = hd[i]
                        pE = ps()
                        kb.op("pe", lambda e: e.matmul(pE[:, 0:128], lhsT=ones_f, rhs=Dg[:, dr * 8 + h, :], start=True, stop=False), reads=[cf, Dg], writes=[pE], inc=False)
                        kb.op("pe", lambda e: e.matmul(pE[:, 0:128], lhsT=ident_b, rhs=(maskF_b if dr == 0 else maskB_b), start=False, stop=True), reads=[cb], writes=[pE])
                        E = Es[i % 3]
                        M = Ms[i % 3]
                        kb.op("act", lambda e: e.activation(out=E[:], in_=pE[:, 0:128], func=AF.Exp, bias=bj[:, tt, dr * 8 + h:dr * 8 + h + 1]), reads=[pE, bj], writes=[E])
                        gg = h // 4
                        kb.op("dve", lambda e: e.tensor_tensor(out=M[:], in0=E[:], in1=pBC[:, gg * 128:(gg + 1) * 128], op=ALU.mult), reads=[E, pBC], writes=[M])
                        mts[i] = M

                    def sC(i):
                        h, dr = hd[i]
                        M = mts.pop(i)
                        kb.op("pe", lambda e: e.matmul(yint[:, h * 64:(h + 1) * 64], lhsT=M[:], rhs=xtm[:, tt, h * 64:(h + 1) * 64], start=(dr == 0), stop=(dr == 1)),
                              reads=[M, xtm], writes=[yint], inc=True)

                    for i in range(16 + 2):
                        if i < 16:
                            sA(i)
                        if i >= 2:
                            sC(i - 2)
                    if SSD_PH < 5:
                        continue
                    pY = [ps(), ps()]
                    for dr in range(2):
                        for gg in range(2):
                            kb.op("pe", lambda e: e.matmul(pY[dr][:, gg * 256:(gg + 1) * 256], lhsT=CTz[gg][:, tk], rhs=Hprev[:, tt, dr, :], start=True, stop=True),
                                  reads=[CTz[gg], Hprev], writes=[pY[dr]], inc=(gg == 1))
                    v3 = lambda ap: ap.rearrange("p (h d) -> p h d", d=64)
                    kb.op("dve", lambda e: e.tensor_tensor(out=v3(ya[:]), in0=v3(xtm[:, tt, :]), in1=bcast(dsk[:], [128, 8, 64], 2), op=ALU.mult), reads=[xtm, dsk], writes=[ya])
                    kb.op("dve", lambda e: e.tensor_tensor(out=ya[:], in0=ya[:], in1=yint[:, :], op=ALU.add), reads=[ya, yint], writes=[ya])
                    for dr in range(2):
                        kb.op("dve", lambda e: e.tensor_tensor(out=v3(yu[:]), in0=v3(pY[dr][:, :]), in1=bcast(ecum[:, tt, dr * 8:(dr + 1) * 8], [128, 8, 64], 2), op=ALU.mult),
                              reads=[pY[dr], ecum], writes=[yu])
                        kb.op("dve", lambda e: e.tensor_tensor(out=ya[:], in0=ya[:], in1=yu[:], op=ALU.add), reads=[ya, yu], writes=[ya])
                    if SSD_PH < 6:
                        continue
                    kb.op("act", lambda e: e.activation(out=zs[:], in_=pz[:, :], func=AF.Silu), reads=[pz], writes=[zs])
                    kb.op("dve", lambda e: e.tensor_tensor(out=ya[:], in0=ya[:], in1=zs[:], op=ALU.mult), reads=[ya, zs], writes=[ya])
                    kb.op("dve", lambda e: e.memset(ss[:, 0:1], 0.0), writes=[ss])
                    kb.op("act", lambda e: e.activation(out=yu[:], in_=ya[:], func=AF.Square, accum_out=ss[:, 0:1]), reads=[ya, ss], writes=[yu, ss])
                    kb.op("dve", lambda e: e.tensor_scalar(out=ss[:, 1:2], in0=ss[:, 0:1], scalar1=1.0 / 512, scalar2=EPS, op0=ALU.mult, op1=ALU.add), reads=[ss], writes=[ss])
                    kb.op("act", lambda e: e.activation(out=ss[:, 1:2], in_=ss[:, 1:2], func=AF.Sqrt), reads=[ss], writes=[ss])
                    kb.op("dve", lambda e: e.reciprocal(out=ss[:, 2:3], in_=ss[:, 1:2]), reads=[ss], writes=[ss])
                    kb.op("dve", lambda e: e.scalar_tensor_tensor(out=ytm[:, tt % 4, :], in0=ya[:], scalar=ss[:, 2:3], in1=ng[:], op0=ALU.mult, op1=ALU.mult),
                          reads=[ya, ss, ng], writes=[ytm])
                    if SSD_PH < 7:
                        continue
                    if tt % 4 == 3:
                        mixer_out(st2, ytm, tt // 4, yT)
        kb.barrier()


    def mlstm(l, g, nblk):
        kb.label = 'mlstm_g%d' % g
        sample = (g == 1)
        L = 2048 if sample else 256
        nseq = 1 if sample else 2
        lt = L // 128
        ntok = nblk * 512
        T = ntok // 128
        blocks = list(range(nblk))
        C0 = 2832
        lns = math.log(128 ** -0.5)
        for hg in range(2):
            h0 = hg * 2
            with scope() as st:
                qT = alloc(st, "mqT", [128, 2, ntok], BF16)
                kT = alloc(st, "mkT", [128, 2, ntok], BF16)
                vaug = alloc(st, "mvaug", [128, T, 2, 129], BF16)
                Cpb = alloc(st, "Cpb", [128, T, 2, 129], BF16)
                li = alloc(st, "li", [128, T, 4], F32)
                lf = alloc(st, "lf", [128, T, 4], F32)
                G = alloc(st, "G", [128, T, 4], F32)
                tot = alloc(st, "mtot", [128, T, 4], F32)
                pj = alloc(st, "pj", [128, T, 4], F32)
                pjs = alloc(st, "pjs", [128, T, 4], F32)
                gend = alloc(st, "gend", [128, T, 4], F32)
                mlb = alloc(st, "mlb", [128, T, 4], F32)
                wend = alloc(st, "wend", [128, T, 4], F32)
                mprev = alloc(st, "mprev", [128, T, 4], F32)
                gb = alloc(st, "gb", [128, 8], F32)
                cwt = alloc(st, "mcwt", [128, 4, 5], F32)
                cbt = alloc(st, "mcbt", [128, 4], F32)
                ng = alloc(st, "mng", [128, 256], F32)
                kb.dma("pool", wo_buf[:, 0:2, :], D["w_out"][l][1024 + h0 * 128:1024 + (h0 + 2) * 128, :].rearrange("(k p) n -> p k n", p=128), writes=[wo_buf])
                kb.dma("sp", ng[:], D["mlstm_norm"][l][h0 * 128:(h0 + 2) * 128].partition_broadcast(128), writes=[ng])
                goffs = [0 * 8 + 0 * 4 + h0, 1 * 8 + 0 * 4 + h0, 0 * 8 + 1 * 4 + h0, 1 * 8 + 1 * 4 + h0]
                for i_, go in enumerate(goffs):
                    kb.dma("sp", gb[:, i_ * 2:(i_ + 1) * 2], D["mlstm_gate_b"][l][go:go + 2].partition_broadcast(128), writes=[gb])
                for ci, ch0 in enumerate((h0 * 128, (h0 + 1) * 128, 512 + h0 * 128, 512 + (h0 + 1) * 128)):
                    for tap in range(5):
                        kb.dma("sp", cwt[:, ci, tap:tap + 1], D["conv_mlstm_w"][l, tap][ch0:ch0 + 128].rearrange("(p o) -> p o", o=1), writes=[cwt])
                    kb.dma("sp", cbt[:, ci:ci + 1], D["conv_mlstm_b"][l][ch0:ch0 + 128].rearrange("(p o) -> p o", o=1), writes=[cbt])
                kb.op("dve", lambda e: e.memset(vaug[:, :, :, 128:129], 1.0), writes=[vaug])
                with scope() as st1:
                    convins = [alloc(st1, "mconvin", [128, nseq, L + 4], F32) for _ in range(2)]
                    acc = alloc(st1, "mcacc", [128, ntok], F32)
                    for cv_ in convins:
                        kb.op("dve", lambda e: e.memset(cv_[:], 0.0), writes=[cv_])
                    wb, wv = wload([(D["w_in"][l][:, C0 + h0 * 128:C0 + (h0 + 2) * 128], 256),
                                    (D["w_in"][l][:, C0 + 512 + h0 * 128:C0 + 512 + (h0 + 2) * 128], 256)], 8)
                    for ci in range(4):
                        for tb in blocks:
                            p = ps()
                            for k in range(8):
                                kb.op("pe", lambda e: e.matmul(p[:, :], lhsT=wv[:, k, ci * 128:(ci + 1) * 128], rhs=hT[tb][:, k, :], start=(k == 0), stop=(k == 7)),
                                      reads=[wb, hT[tb]], writes=[p], inc=(k == 7))
                            evac_conv_in(convins[ci % 2], p, tb, nseq, L)
                        convin = convins[ci % 2]
                        dstb = qT if ci < 2 else kT
                        conv_chunk(convin, acc, cwt, cbt, ci, nseq, L, lambda s_: dstb[:, ci % 2, s_ * L:(s_ + 1) * L], dstb)
                kb.barrier()
                wb, wv = wload([(D["w_in"][l][:, C0 + 1024 + h0 * 128:C0 + 1024 + (h0 + 2) * 128], 256)], 8)
                for tt in range(T):
                    p = ps()
                    proj_tm(wb, wv, 0, 256, tt // 4, tt % 4, p)
                    kb.op("act", lambda e: e.activation(out=vaug[:, tt, :, 0:128], in_=p[:, 0:256].rearrange("p (h e) -> p h e", e=128), func=AF.Copy), reads=[p], writes=[vaug])
                gc = C0 + 2048
                wb, wv = wload([(D["w_in"][l][:, gc + go:gc + go + 2], 2) for go in goffs], 8)
                for tt in range(T):
                    p = ps()
                    proj_tm(wb, wv, 0, 8, tt // 4, tt % 4, p)
                    kb.op("dve", lambda e: e.tensor_tensor(out=li[:, tt, :], in0=p[:, 0:4], in1=gb[:, 0:4], op=ALU.add), reads=[p, gb], writes=[li])
                    kb.op("dve", lambda e: e.tensor_tensor(out=lf[:, tt, :], in0=p[:, 4:8], in1=gb[:, 4:8], op=ALU.add), reads=[p, gb], writes=[lf])
                kb.op("act", lambda e: e.activation(out=lf[:], in_=lf[:], func=AF.Exp, scale=-1.0), reads=[lf], writes=[lf])
                kb.op("act", lambda e: e.activation(out=lf[:], in_=lf[:], func=AF.Ln, bias=1.0), reads=[lf], writes=[lf])
                kb.op("dve", lambda e: e.tensor_scalar(out=lf[:], in0=lf[:], scalar1=-1.0, scalar2=None, op0=ALU.mult), reads=[lf], writes=[lf])
                for tt in range(T):
                    p = ps()
                    kb.op("pe", lambda e: e.matmul(p[:, 0:4], lhsT=triF_f, rhs=lf[:, tt, :], start=True, stop=True), reads=[cf, lf], writes=[p], inc=False)
                    kb.op("pe", lambda e: e.matmul(p[:, 4:8], lhsT=triB_f, rhs=lf[:, tt, :], start=True, stop=True), reads=[cf, lf], writes=[p], inc=False)
                    kb.op("pe", lambda e: e.matmul(p[:, 8:12], lhsT=ones_f, rhs=lf[:, tt, :], start=True, stop=True), reads=[cf, lf], writes=[p])
                    kb.op("dve", lambda e: e.tensor_copy(out=G[:, tt, 0:2], in_=p[:, 0:2]), reads=[p], writes=[G])
                    kb.op("dve", lambda e: e.tensor_copy(out=G[:, tt, 2:4], in_=p[:, 6:8]), reads=[p], writes=[G])
                    kb.op("dve", lambda e: e.tensor_copy(out=tot[:, tt, :], in_=p[:, 8:12]), reads=[p], writes=[tot])
                kb.op("dve", lambda e: e.tensor_tensor(out=pj[:], in0=li[:], in1=G[:], op=ALU.subtract), reads=[li, G], writes=[pj])
                kb.op("dve", lambda e: e.tensor_scalar(out=pjs[:], in0=pj[:], scalar1=lns, scalar2=None, op0=ALU.add), reads=[pj], writes=[pjs])
                kb.op("dve", lambda e: e.tensor_tensor(out=gend[:], in0=pj[:], in1=tot[:], op=ALU.add), reads=[pj, tot], writes=[gend])
                with scope() as stt:
                    mrow = alloc(stt, "mrow8", [4, 1], F32)
                    d8 = alloc(stt, "d8", [4, 4], F32)
                    for tt in range(T):
                        p = ps()
                        kb.op("pe", lambda e: e.matmul(p[0:4, 0:128], lhsT=gend[:, tt, :], rhs=ident_f, start=True, stop=True), reads=[gend, cf], writes=[p])
                        kb.op("dve", lambda e: e.tensor_reduce(out=mrow[:], in_=p[0:4, 0:128], axis=AX.X, op=ALU.max), reads=[p], writes=[mrow])
                        kb.op("dve", lambda e: e.tensor_scalar(out=d8[:], in0=ident_f[0:4, 0:4], scalar1=mrow[:, 0:1], scalar2=None, op0=ALU.mult), reads=[cf, mrow], writes=[d8])
                        p2 = ps()
                        kb.op("pe", lambda e: e.matmul(p2[:, 0:4], lhsT=ones_f[0:4, :], rhs=d8[:], start=True, stop=True), reads=[cf, d8], writes=[p2])
                        kb.op("dve", lambda e: e.tensor_copy(out=mlb[:, tt, :], in_=p2[:, 0:4]), reads=[p2], writes=[mlb])
                    kb.barrier()
                kb.op("dve", lambda e: e.tensor_tensor(out=wend[:], in0=gend[:], in1=mlb[:], op=ALU.subtract), reads=[gend, mlb], writes=[wend])
                kb.op("act", lambda e: e.activation(out=wend[:], in_=wend[:], func=AF.Exp), reads=[wend], writes=[wend])
                with scope() as st2:
                    Cst = [alloc(st2, "Cst%d" % i, [128, 129], F32) for i in range(4)]
                    mp = alloc(st2, "mp", [128, 4], F32)
                    mt8 = alloc(st2, "mt8", [128, 16], F32)
                    kwts = [alloc(st2, "kwt", [128, 2, 128], BF16) for _ in range(2)]
                    Dgs = [alloc(st2, "mDg", [128, 4, 128], F32) for _ in range(2)]
                    Drs = [alloc(st2, "mDr", [128, 4, 128], F32) for _ in range(2)]
                    scs = [alloc(st2, "msc", [128, 40], F32) for _ in range(2)]
                    kwt, Dg, Dr, sc = kwts[0], Dgs[0], Drs[0], scs[0]
                    Es = [alloc(st2, "mE%d" % i, [128, 128], F32) for i in range(3)]
                    Ms = [alloc(st2, "mM%d" % i, [128, 128], BF16) for i in range(3)]
                    nds = [alloc(st2, "nd%d" % i, [128, 129], F32) for i in range(2)]
                    cbf = [alloc(st2, "cbf%d" % i, [128, 129], BF16) for i in range(2)]
                    hsums = [alloc(st2, "hsum", [128, 256], F32) for _ in range(2)]
                    hts = [alloc(st2, "mht", [128, 256], F32) for _ in range(2)]
                    sgs = [alloc(st2, "msg", [128, 256], F32) for _ in range(2)]
                    hsum, ht, sg = hsums[0], hts[0], sgs[0]
                    ytm = alloc(st2, "mytm", [128, 4, 256], BF16)
                    yT = alloc(st2, "myT", [128, 2, 512], BF16)
                    ei = 0
                    ei0 = [0]

                    def init_state(dr, s_):
                        for hh in range(2):
                            C = Cst[dr * 2 + hh]
                            if sample:
                                kb.dma("sp", C[:, 0:128], D["smc"][l, dr, h0 + hh], writes=[C])
                                kb.dma("sp", C[:, 128:129], D["smn"][l, dr, h0 + hh].rearrange("(p o) -> p o", o=1), writes=[C])
                            else:
                                kb.op("dve", lambda e: e.memset(C[:], 0.0), writes=[C])
                        if sample:
                            kb.dma("sp", mp[:, dr * 2:dr * 2 + 2], D["smm"][l][dr * 4 + h0:dr * 4 + h0 + 2].partition_broadcast(128), writes=[mp])
                        else:
                            kb.op("dve", lambda e: e.memset(mp[:, dr * 2:dr * 2 + 2], 0.0), writes=[mp])

                    def local_update(dr, tt):
                        cs = slice(dr * 2, dr * 2 + 2)
                        pk = ps()
                        for hh in range(2):
                            kb.op("pe", lambda e: e.matmul(pk[:, hh * 128:(hh + 1) * 128], lhsT=kT[:, hh, tt * 128:(tt + 1) * 128], rhs=ident_b, start=True, stop=True),
                                  reads=[kT, cb], writes=[pk], inc=(hh == 1))
                        kb.op("dve", lambda e: e.tensor_tensor(out=kwt[:], in0=pk[:, 0:256].rearrange("p (h d) -> p h d", d=128),
                                                               in1=bcast(wend[:, tt, cs], [128, 2, 128], 2), op=ALU.mult), reads=[pk, wend], writes=[kwt])
                        a = sc[:, 0:2]
                        mn = sc[:, 2:4]
                        sp_ = sc[:, 4:6]
                        sl_ = sc[:, 6:8]
                        kb.op("dve", lambda e: e.tensor_tensor(out=a, in0=tot[:, tt, cs], in1=mp[:, cs], op=ALU.add), reads=[tot, mp], writes=[sc])
                        kb.op("dve", lambda e: e.tensor_tensor(out=mn, in0=a, in1=mlb[:, tt, cs], op=ALU.max), reads=[sc, mlb], writes=[sc])
                        kb.op("dve", lambda e: e.tensor_tensor(out=sp_, in0=a, in1=mn, op=ALU.subtract), reads=[sc], writes=[sc])
                        kb.op("dve", lambda e: e.tensor_tensor(out=sl_, in0=mlb[:, tt, cs], in1=mn, op=ALU.subtract), reads=[sc, mlb], writes=[sc])
                        kb.op("act", lambda e: e.activation(out=sc[:, 4:8], in_=sc[:, 4:8], func=AF.Exp), reads=[sc], writes=[sc])
                        kb.op("dve", lambda e: e.tensor_copy(out=mp[:, cs], in_=mn), reads=[sc], writes=[mp])
                        for hh in range(2):
                            C = Cst[dr * 2 + hh]
                            pc = ps()
                            kb.op("pe", lambda e: e.matmul(pc[:, 0:129], lhsT=kwt[:, hh, :], rhs=vaug[:, tt, hh, :], start=True, stop=True), reads=[kwt, vaug], writes=[pc])
                            kb.op("dve", lambda e: e.tensor_scalar(out=C[:], in0=C[:], scalar1=sc[:, 4 + hh:5 + hh], scalar2=None, op0=ALU.mult), reads=[C, sc], writes=[C])
                            kb.op("dve", lambda e: e.scalar_tensor_tensor(out=C[:], in0=pc[:, 0:129], scalar=sc[:, 6 + hh:7 + hh], in1=C[:], op0=ALU.mult, op1=ALU.add),
                                  reads=[pc, sc, C], writes=[C])

                    def final_state(dr, s_):
                        for hh in range(2):
                            C = Cst[dr * 2 + hh]
                            kb.dma("sp", D["nmc"][s_, l, dr, h0 + hh], C[:, 0:128], reads=[C])
                            kb.dma("sp", D["nmn"][s_, l, dr, h0 + hh].rearrange("(p o) -> p o", o=1), C[:, 128:129], reads=[C])
                        kb.dma("sp", D["nmm"][s_, l:l + 1, dr * 4 + h0:dr * 4 + h0 + 2], mp[0:1, dr * 2:dr * 2 + 2], reads=[mp])

                    for s_ in range(nseq):
                        init_state(1, s_)
                        for r0 in range(lt - 1, -1, -1):
                            tt = s_ * lt + r0
                            kwt, sc = kwts[tt % 2], scs[tt % 2]
                            for hh in range(2):
                                kb.op("act", lambda e: e.activation(out=Cpb[:, tt, hh, :], in_=Cst[2 + hh][:], func=AF.Copy), reads=[Cst[2 + hh]], writes=[Cpb])
                            kb.op("dve", lambda e: e.tensor_copy(out=mprev[:, tt, 2:4], in_=mp[:, 2:4]), reads=[mp], writes=[mprev])
                            local_update(1, tt)
                        if not sample:
                            final_state(1, s_)
                    wob, wov = wload([(D["w_in"][l][:, C0 + 1536 + h0 * 128:C0 + 1536 + (h0 + 2) * 128], 256)], 8)
                    for s_ in range(nseq):
                        init_state(0, s_)
                        for r0 in range(lt):
                            tt = s_ * lt + r0
                            tk = slice(tt * 128, (tt + 1) * 128)
                            kwt, Dg, Dr, sc = kwts[tt % 2], Dgs[tt % 2], Drs[tt % 2], scs[tt % 2]
                            hsum, ht, sg = hsums[tt % 2], hts[tt % 2], sgs[tt % 2]
                            kb.op("dve", lambda e: e.tensor_copy(out=mprev[:, tt, 0:2], in_=mp[:, 0:2]), reads=[mp], writes=[mprev])
                            kb.op("dve", lambda e: e.tensor_tensor(out=sc[:, 8:12], in0=G[:, tt, :], in1=mprev[:, tt, :], op=ALU.add), reads=[G, mprev], writes=[sc])
                            kb.op("dve", lambda e: e.tensor_tensor(out=Dg[:], in0=bcast(ident_f, [128, 4, 128], 1), in1=bcast(pj[:, tt, :], [128, 4, 128], 2), op=ALU.mult),
                                  reads=[cf, pj], writes=[Dg])
                            po = ps_acc()
                            proj_tm(wob, wov, 0, 256, tt // 4, tt % 4, po)
                            pSs = []
                            for hh in range(2):
                                pS = ps_acc()
                                kb.op("pe", lambda e: e.matmul(pS[:, 0:128], lhsT=kT[:, hh, tk], rhs=qT[:, hh, tk], start=True, stop=True), reads=[kT, qT], writes=[pS])
                                pSs.append(pS)
                            pms = []
                            for c in range(4):
                                dr = c // 2
                                pm = ps()
                                kb.op("pe", lambda e: e.matmul(pm[:, 0:128], lhsT=ones_f, rhs=Dg[:, c, :], start=True, stop=False), reads=[cf, Dg], writes=[pm], inc=False)
                                kb.op("pe", lambda e: e.matmul(pm[:, 0:128], lhsT=ident_b, rhs=(maskB_b if dr == 0 else maskF_b), start=False, stop=True), reads=[cb], writes=[pm])
                                pms.append(pm)
                            for c in range(4):
                                kb.op("dve", lambda e: e.tensor_reduce(out=sc[:, 12 + c:13 + c], in_=pms[c][:, 0:128], axis=AX.X, op=ALU.max), reads=[pms[c]], writes=[sc])
                            kb.op("dve", lambda e: e.tensor_tensor(out=sc[:, 12:16], in0=sc[:, 12:16], in1=G[:, tt, :], op=ALU.add), reads=[sc, G], writes=[sc])
                            kb.op("dve", lambda e: e.tensor_tensor(out=sc[:, 16:20], in0=sc[:, 12:16], in1=sc[:, 8:12], op=ALU.max), reads=[sc], writes=[sc])
                            kb.op("dve", lambda e: e.tensor_tensor(out=sc[:, 20:24], in0=G[:, tt, :], in1=sc[:, 16:20], op=ALU.subtract), reads=[sc, G], writes=[sc])
                            kb.op("dve", lambda e: e.tensor_tensor(out=sc[:, 24:28], in0=sc[:, 8:12], in1=sc[:, 16:20], op=ALU.subtract), reads=[sc], writes=[sc])
                            kb.op("act", lambda e: e.activation(out=sc[:, 24:28], in_=sc[:, 24:28], func=AF.Exp, bias=lns_col[:, 0:1]), reads=[sc, cf], writes=[sc])
                            kb.op("act", lambda e: e.activation(out=sc[:, 28:32], in_=sc[:, 16:20], func=AF.Exp, scale=-1.0), reads=[sc], writes=[sc])
                            kb.op("dve", lambda e: e.tensor_tensor(out=Dr[:], in0=bcast(ident_f, [128, 4, 128], 1), in1=bcast(sc[:, 20:24], [128, 4, 128], 2), op=ALU.mult),
                                  reads=[cf, sc], writes=[Dr])
                            items = [(0, 0), (0, 1), (1, 0), (1, 1)]
                            mts = {}

                            def mA(i):
                                hh, dr = items[i]
                                c = dr * 2 + hh
                                pW = ps()
                                kb.op("pe", lambda e: e.matmul(pW[:, 0:128], lhsT=ones_f, rhs=Dr[:, c, :], start=True, stop=False), reads=[cf, Dr], writes=[pW], inc=False)
                                kb.op("pe", lambda e: e.matmul(pW[:, 0:128], lhsT=ident_b, rhs=(maskF_b if dr == 0 else maskB_b), start=False, stop=True), reads=[cb], writes=[pW])
                                E = Es[(ei0[0] + i) % 3]
                                M = Ms[(ei0[0] + i) % 3]
                                kb.op("act", lambda e: e.activation(out=E[:], in_=pW[:, 0:128], func=AF.Exp, bias=pjs[:, tt, c:c + 1]), reads=[pW, pjs], writes=[E])
                                kb.op("dve", lambda e: e.tensor_tensor(out=M[:], in0=E[:], in1=pSs[hh][:, 0:128], op=ALU.mult), reads=[E, pSs[hh]], writes=[M])
                                if dr == 0:
                                    kb.op("act", lambda e: e.activation(out=cbf[hh][:], in_=Cst[hh][:], func=AF.Copy), reads=[Cst[hh]], writes=[cbf[hh]])
                                mts[i] = M

                            def mC(i):
                                hh, dr = items[i]
                                c = dr * 2 + hh
                                M = mts.pop(i)
                                nd = nds[i % 2]
                                pN = ps()
                                kb.op("pe", lambda e: e.matmul(pN[:, 0:129], lhsT=M[:], rhs=vaug[:, tt, hh, :], start=True, stop=True), reads=[M, vaug], writes=[pN])
                                pI = ps()
                                if dr == 0:
                                    kb.op("pe", lambda e: e.matmul(pI[:, 0:129], lhsT=qT[:, hh, tk], rhs=cbf[hh][:], start=True, stop=True), reads=[qT, cbf[hh]], writes=[pI])
                                else:
                                    kb.op("pe", lambda e: e.matmul(pI[:, 0:129], lhsT=qT[:, hh, tk], rhs=Cpb[:, tt, hh, :], start=True, stop=True), reads=[qT, Cpb], writes=[pI])
                                kb.op("act", lambda e: e.activation(out=nd[:], in_=pN[:, 0:129], func=AF.Copy), reads=[pN], writes=[nd])
                                kb.op("dve", lambda e: e.scalar_tensor_tensor(out=nd[:], in0=pI[:, 0:129], scalar=sc[:, 24 + c:25 + c], in1=nd[:], op0=ALU.mult, op1=ALU.add),
                                      reads=[pI, sc, nd], writes=[nd])
                                kb.op("dve", lambda e: e.tensor_scalar(out=sc[:, 32:33], in0=nd[:, 128:129], scalar1=-1.0, scalar2=None, op0=ALU.mult), reads=[nd], writes=[sc])
                                kb.op("dve", lambda e: e.tensor_tensor(out=sc[:, 32:33], in0=sc[:, 32:33], in1=nd[:, 128:129], op=ALU.max), reads=[nd, sc], writes=[sc])
                                kb.op("dve", lambda e: e.tensor_tensor(out=sc[:, 32:33], in0=sc[:, 32:33], in1=sc[:, 28 + c:29 + c], op=ALU.max), reads=[sc], writes=[sc])
                                kb.op("dve", lambda e: e.reciprocal(out=sc[:, 33:34], in_=sc[:, 32:33]), reads=[sc], writes=[sc])
                                if dr == 0:
                                    kb.op("dve", lambda e: e.tensor_scalar(out=hsum[:, hh * 128:(hh + 1) * 128], in0=nd[:, 0:128], scalar1=sc[:, 33:34], scalar2=None, op0=ALU.mult),
                                          reads=[nd, sc], writes=[hsum])
                                else:
                                    kb.op("dve", lambda e: e.scalar_tensor_tensor(out=hsum[:, hh * 128:(hh + 1) * 128], in0=nd[:, 0:128], scalar=sc[:, 33:34],
                                                                                  in1=hsum[:, hh * 128:(hh + 1) * 128], op0=ALU.mult, op1=ALU.add), reads=[nd, sc, hsum], writes=[hsum])

                            for i in range(4 + 2):
                                if i < 4:
                                    mA(i)
                                if i >= 2:
                                    mC(i - 2)
                            ei0[0] += 4
                            local_update(0, tt)
                            h3 = hsum[:].rearrange("p (h d) -> p h d", d=128)
                            t3 = ht[:].rearrange("p (h d) -> p h d", d=128)
                            kb.op("dve", lambda e: e.tensor_tensor(out=ht[:], in0=hsum[:], in1=hsum[:], op=ALU.mult), reads=[hsum], writes=[ht])
                            kb.op("dve", lambda e: e.tensor_reduce(out=sc[:, 34:36], in_=t3, axis=AX.X, op=ALU.add), reads=[ht], writes=[sc])
                            kb.op("dve", lambda e: e.tensor_scalar(out=sc[:, 34:36], in0=sc[:, 34:36], scalar1=1.0 / 128, scalar2=EPS, op0=ALU.mult, op1=ALU.add), reads=[sc], writes=[sc])
                            kb.op("act", lambda e: e.activation(out=sc[:, 34:36], in_=sc[:, 34:36], func=AF.Sqrt), reads=[sc], writes=[sc])
                            kb.op("dve", lambda e: e.reciprocal(out=sc[:, 36:38], in_=sc[:, 34:36]), reads=[sc], writes=[sc])
                            kb.op("dve", lambda e: e.tensor_tensor(out=t3, in0=h3, in1=bcast(sc[:, 36:38], [128, 2, 128], 2), op=ALU.mult), reads=[hsum, sc], writes=[ht])
                            kb.op("dve", lambda e: e.tensor_tensor(out=ht[:], in0=ht[:], in1=ng[:], op=ALU.mult), reads=[ht, ng], writes=[ht])
                            kb.op("act", lambda e: e.activation(out=sg[:], in_=po[:, 0:256], func=AF.Sigmoid), reads=[po], writes=[sg])
                            kb.op("dve", lambda e: e.tensor_tensor(out=ytm[:, tt % 4, :], in0=ht[:], in1=sg[:], op=ALU.mult), reads=[ht, sg], writes=[ytm])
                            if tt % 4 == 3:
                                tb = tt // 4
                                for c2 in range(2):
                                    p = ps()
                                    for tl in range(4):
                                        kb.op("pe", lambda e: e.matmul(p[:, tl * 128:(tl + 1) * 128], lhsT=ytm[:, tl, c2 * 128:(c2 + 1) * 128], rhs=ident_b, start=True, stop=True),
                                              reads=[ytm, cb], writes=[p], inc=(tl == 3))
                                    kb.op("act", lambda e: e.activation(out=yT[:, c2, :], in_=p[:, :], func=AF.Copy), reads=[p], writes=[yT])
                                for c in range(8):
                                    p = ps()
                                    for k in range(2):
                                        kb.op("pe", lambda e: e.matmul(p[:, :], lhsT=wo_buf[:, k, c * 128:(c + 1) * 128], rhs=yT[:, k, :], start=(k == 0), stop=(k == 1)),
                                              reads=[wo_buf, yT], writes=[p], inc=(k == 1))
                                    kb.op("dve", lambda e: e.scalar_tensor_tensor(out=xT[tb][:, c, :], in0=p[:, :], scalar=modc[:, 2, c:c + 1], in1=xT[tb][:, c, :],
                                                                                  op0=ALU.mult, op1=ALU.add), reads=[p, modc, xT[tb]], writes=[xT[tb]])
                        if not sample:
                            final_state(0, s_)
            kb.barrier()

    def run_pass(g, src, dst, nblk):
        load_x(src, nblk)
        kb.barrier()
        for l in range(NL):
            kb.label = 'norm_g%d' % g
            load_mod(l, g)
            with scope() as st:
                sq = [alloc(st, "sq", [128, 8, 512], BF16) for _ in range(2)]
                rstd = [alloc(st, "rstd", [128, 512], F32) for _ in range(2)]
                tmp2 = [alloc(st, "tmpn%d" % i, [128, 512], F32) for i in range(4)]
                for tb in range(nblk):
                    norm_block((sq, rstd, tmp2), tb, AB.t[:, 0, :], AB.t[:, 1, :], hT[tb])
            kb.barrier()
            for mk in MIXERS:
                if mk in "bd":
                    attention(l, g, mk, nblk)
                elif mk == "a":
                    ssd(l, g, nblk)
                elif mk == "c":
                    mlstm(l, g, nblk)
            with scope() as st:
                sq = [alloc(st, "sq", [128, 8, 512], BF16) for _ in range(2)]
                rstd = [alloc(st, "rstd", [128, 512], F32) for _ in range(2)]
                tmp2 = [alloc(st, "tmpn%d" % i, [128, 512], F32) for i in range(4)]
                for tb in range(nblk):
                    norm_block((sq, rstd, tmp2), tb, AB.t[:, 2, :], AB.t[:, 3, :], hT[tb])
            kb.barrier()
            for b0 in range(0, nblk, 2):
                ffn(l, list(range(b0, min(b0 + 2, nblk))))
        final_out(nblk, dst)

    run_pass(0, D["xp"], D["yp"], 1)
    run_pass(1, D["xs"], D["ys"], 4)

    kb.barrier(include_pool_dma=True)
    top.close()
    print("instructions:", kb.ninst, flush=True)
    if kb.stats is not None:
        tot = 0.0
        for lab, (mk, busy, nu) in sorted(kb.stats.items(), key=lambda kv: -kv[1][0]):
            tot += mk
            print("  %-12s est_us=%8.0f units=%6d  busy: %s" % (lab, mk / 1e3, nu, " ".join("%s=%.0f" % (e_, v_ / 1e3) for e_, v_ in sorted(busy.items()))))
        print("  est total us", tot / 1e3)
    return nc


_CACHE = {}


def prep_inputs(inp):
    f = lambda a: np.ascontiguousarray(np.asarray(a, dtype=np.float32))
    consts = make_consts()
    rope = make_rope()
    shared = {}
    for name in ("w_ada", "b_ada", "norm1", "norm2", "w_in", "w_out", "conv_ssd_w", "conv_ssd_b", "ssd_d", "ssd_norm",
                 "diff_lq1", "diff_lk1", "diff_lq2", "diff_lk2", "conv_mlstm_w", "conv_mlstm_b", "mlstm_norm",
                 "gqa_q_norm", "gqa_k_norm", "w_ffn_in", "w_ffn_out", "norm_f"):
        shared[name] = f(inp[name])
    shared["ssd_a_log"] = f(inp["ssd_a_log"]).reshape(4, 16)
    shared["ssd_dt_bias"] = f(inp["ssd_dt_bias"]).reshape(4, 16)
    shared["mlstm_gate_b"] = f(inp["mlstm_gate_b"]).reshape(4, 16)
    shared["consts"] = consts
    shared["rope"] = rope
    xp = f(inp["x_prompt"])
    xs = f(inp["x_sample"])
    in_maps = []
    for c in range(8):
        b = c // 4
        m = dict(shared)
        m["xp"] = xp[2 * c:2 * c + 2].reshape(512, 1024)
        m["xs"] = xs[b]
        m["cvec"] = np.stack([f(inp["c_ctx"]), f(inp["c"])[b]], axis=0)
        m["cdk"] = f(inp["cache_diff_k"])[b].reshape(4, 256, 512)
        m["cdv"] = f(inp["cache_diff_v"])[b].reshape(4, 256, 512)
        m["cgk"] = f(inp["cache_gqa_k"])[b].reshape(4, 256, 128)
        m["cgv"] = f(inp["cache_gqa_v"])[b].reshape(4, 256, 128)
        m["sssm"] = f(inp["state_ssm"])[b]
        m["smc"] = f(inp["state_mlstm_c"])[b]
        m["smn"] = f(inp["state_mlstm_n"])[b]
        m["smm"] = f(inp["state_mlstm_m"])[b].reshape(4, 8)
        in_maps.append(m)
    if NL < 4:
        spec = dict(IN_SPECS)
        for m in in_maps:
            for k_ in list(m.keys()):
                if spec[k_][0] == 4 and len(spec[k_]) > 1:
                    m[k_] = np.ascontiguousarray(m[k_][:NL])
    return in_maps


def kernel(**inp):
    if "nc" not in _CACHE:
        _CACHE["nc"] = build_program()
    nc = _CACHE["nc"]
    in_maps = prep_inputs(inp)
    res = run_bass_kernel_spmd(nc, in_maps, core_ids=list(range(8)))
    return assemble(res.results)


def assemble(R):
    y_prompt = np.concatenate([R[c]["yp"].reshape(2, 256, 1024) for c in range(8)], axis=0)
    y_sample = np.stack([R[0]["ys"], R[4]["ys"]], axis=0)
    cat = lambda k: np.concatenate([R[c][k] for c in range(8)], axis=0)
    ndk = cat("ndk").reshape(16, 4, 256, 4, 2, 64)
    ndv = cat("ndv").reshape(16, 4, 256, 4, 128)
    ngk = cat("ngk").reshape(16, 4, 256, 2, 64)
    ngv = cat("ngv").reshape(16, 4, 256, 2, 64)
    nssm = cat("nssm")
    nmc = cat("nmc")
    nmn = cat("nmn")
    nmm = cat("nmm").reshape(16, 4, 2, 4)
    return (y_prompt, y_sample, ndk, ndv, ngk, ngv, nssm, nmc, nmn, nmm)
```

```python
import os
import math
from contextlib import ExitStack
import numpy as np
import concourse.bass as bass
import concourse.mybir as mybir
from concourse.bass_utils import run_bass_kernel_spmd

F32 = mybir.dt.float32
BF16 = mybir.dt.bfloat16
ALU = mybir.AluOpType
AF = mybir.ActivationFunctionType
AX = mybir.AxisListType

D_MODEL = 1024
DEPTH = 4
IN_COLS = 5664
D_FF = 2816
EPS = 1e-6
NEG = -30000.0

NL = int(os.environ.get("MK_NL", "4"))
MIXERS = os.environ.get("MK_MIX", "abcd")
SSD_PH = int(os.environ.get("MK_SSD_PH", "9"))
SSD_SUB = int(os.environ.get("MK_SSD_SUB", "9"))


class Buf:
    __slots__ = ("t", "w", "r")

    def __init__(self, t):
        self.t = t
        self.w = None
        self.r = []

    def __getitem__(self, idx):
        return self.t[idx]


class _Rec:
    def __init__(self):
        self.call = None

    def __getattr__(self, name):
        def f(*args, **kw):
            self.call = (name, args, kw)
            return self
        return f


class KB:
    NDMA_SEM = 8

    def __init__(self, nc):
        self.nc = nc
        self.engs = {"pe": nc.tensor, "act": nc.scalar, "dve": nc.vector, "pool": nc.gpsimd, "sp": nc.sync}
        self.sems = {}
        self.cnt = {}
        for k in ("pe", "act", "dve", "pool"):
            self.sems[k] = nc.alloc_semaphore(name="s_" + k)
            self.cnt[k] = 0
        self.dq = {}
        for q in ("sp", "pool", "act"):
            lst = []
            for i in range(self.NDMA_SEM):
                key = "d_%s%d" % (q, i)
                self.sems[key] = nc.alloc_semaphore(name=key)
                self.cnt[key] = 0
                lst.append(key)
            self.dq[q] = [lst, 0]
        self.seen = {e: {} for e in self.engs}
        self.ninst = 0
        self.defer = bool(int(os.environ.get('MK_SCHED', '1')))
        self.pending = []
        self.stats = {} if os.environ.get('MK_STATS') else None
        self.label = 'top'

    def _wait(self, eng, k, v):
        seen = self.seen[eng]
        if seen.get(k, 0) >= v:
            return
        self.engs[eng].wait_ge(self.sems[k], v)
        self.ninst += 1
        seen[k] = v

    def _need(self, eng, reads, writes):
        need = {}

        def add(dep):
            if dep is None:
                return
            k, v = dep
            if need.get(k, 0) < v:
                need[k] = v
        for b in reads:
            add(b.w)
        for b in writes:
            add(b.w)
            for d in b.r:
                add(d)
        for k, v in need.items():
            if k == eng and eng == "pe":
                continue
            self._wait(eng, k, v)

    def _record(self, dep, reads, writes):
        for b in reads:
            b.r.append(dep)
            if len(b.r) > 64:
                mx = {}
                for k, v in b.r:
                    if mx.get(k, 0) < v:
                        mx[k] = v
                b.r = list(mx.items())
        for b in writes:
            b.w = dep
            b.r = []

    def op(self, eng, fn, reads=(), writes=(), inc=True):
        if self.defer:
            rec = _Rec()
            fn(rec)
            self.pending.append(("op", eng, rec.call, tuple(reads), tuple(writes), inc))
            return None
        return self._op_now(eng, fn, reads, writes, inc)

    def _op_now(self, eng, fn, reads=(), writes=(), inc=True):
        self._need(eng, reads, writes)
        ins = fn(self.engs[eng])
        self.ninst += 1
        val = self.cnt[eng] + 1
        if inc:
            ins.then_inc(self.sems[eng], 1)
            self.cnt[eng] = val
        self._record((eng, val), reads, writes)
        return ins

    def dma(self, q, out, in_, reads=(), writes=(), **kw):
        if self.defer:
            self.pending.append(("dma", q, (out, in_, kw), tuple(reads), tuple(writes), True))
            return None
        return self._dma_now(q, out, in_, reads, writes, **kw)

    def _dma_now(self, q, out, in_, reads=(), writes=(), **kw):
        self._need(q, reads, writes)
        lst, i = self.dq[q]
        key = lst[i % len(lst)]
        self.dq[q][1] = i + 1
        if self.cnt[key]:
            self._wait(q, key, self.cnt[key])
        ins = self.engs[q].dma_start(out=out, in_=in_, **kw)
        self.ninst += 1
        self.cnt[key] += 16
        ins.then_inc(self.sems[key], 16)
        dep = (key, self.cnt[key])
        self._record(dep, reads, writes)
        return dep

    @staticmethod
    def _cost(kind, eng, call):
        def fsz(ap):
            n = 1
            for d in ap.shape[1:]:
                n *= d
            return n
        if kind == "dma":
            out = call[0]
            nb = fsz(out) * out.shape[0] * (2 if out.dtype == BF16 else 4)
            return 2000.0 + nb / 80.0
        name, args, kw = call
        if name == "matmul":
            n = fsz(kw["rhs"])
            passes = 4 if kw["lhsT"].dtype == F32 else 1
            return 70.0 + n * passes * 0.45
        out = kw.get("out", None)
        if out is None:
            out = kw.get("ap", args[0] if args else None)
        n = fsz(out) if out is not None else 64
        if eng == "act":
            return 230.0 + n * 0.75
        if name == "reciprocal":
            return 70.0 + n * 6.5
        if name == "memset":
            return 70.0 + n * 0.5
        return 70.0 + n * 1.1

    def flush(self):
        pend = self.pending
        self.pending = []
        if not pend:
            return
        import heapq
        units = []
        cur = None
        for it in pend:
            kind, eng, call, rd, wr, inc = it
            if kind == "op" and eng == "pe":
                if cur is None:
                    cur = [eng, [], 0.0, set(), set()]
                cur[1].append(it)
                cur[2] += self._cost(kind, eng, call)
                cur[3].update(rd)
                cur[4].update(wr)
                if inc:
                    units.append(cur)
                    cur = None
            else:
                assert cur is None, "non-PE op inside an open PE group"
                units.append([eng, [it], self._cost(kind, eng, call), set(rd), set(wr)])
        assert cur is None, "PE group without final inc"
        n = len(units)
        lastw = {}
        readers = {}
        deps = [None] * n
        succ = [[] for _ in range(n)]
        for i, u in enumerate(units):
            d = set()
            for b in u[3]:
                if b in lastw:
                    d.add(lastw[b])
            for b in u[4]:
                if b in lastw:
                    d.add(lastw[b])
                for r in readers.get(b, ()):
                    d.add(r)
            d.discard(i)
            deps[i] = d
            for j in d:
                succ[j].append(i)
            for b in u[3]:
                readers.setdefault(b, []).append(i)
            for b in u[4]:
                lastw[b] = i
                readers[b] = []
        ndep = [len(d) for d in deps]
        ready_t = [0.0] * n
        fin = [0.0] * n
        free = {}
        heaps = {}
        for i in range(n):
            if ndep[i] == 0:
                heapq.heappush(heaps.setdefault(units[i][0], []), (0.0, i))
        order = []
        done = 0
        while done < n:
            best = None
            for e, h in heaps.items():
                if not h:
                    continue
                rt, i = h[0]
                st_ = max(rt, free.get(e, 0.0))
                if best is None or (st_, i) < (best[0], best[1]):
                    best = (st_, i, e)
            st_, i, e = best
            heapq.heappop(heaps[e])
            u = units[i]
            if e in ("sp", "pool") or (e == "act" and u[1][0][0] == "dma"):
                free[e] = st_ + 60.0
                fin[i] = st_ + u[2]
            else:
                fin[i] = st_ + u[2]
                free[e] = fin[i]
            order.append((st_, i))
            done += 1
            for j in succ[i]:
                ndep[j] -= 1
                if fin[i] > ready_t[j]:
                    ready_t[j] = fin[i]
                if ndep[j] == 0:
                    heapq.heappush(heaps.setdefault(units[j][0], []), (ready_t[j], j))
        if self.stats is not None:
            mk = max(fin) if fin else 0.0
            busy = {}
            for u in units:
                busy[u[0]] = busy.get(u[0], 0.0) + u[2]
            st = self.stats.setdefault(self.label, [0.0, {}, 0, 0, 0.0])
            st[0] += mk
            st[2] += n
            st[3] += 1
            st[4] += mk - max(busy.values())
            for e_, v_ in busy.items():
                st[1][e_] = st[1].get(e_, 0.0) + v_
        order.sort()
        for _, i in order:
            for kind, eng, call, rd, wr, inc in units[i][1]:
                if kind == "op":
                    name, args, kw = call
                    self._op_now(eng, lambda en: getattr(en, name)(*args, **kw), rd, wr, inc)
                else:
                    out, in_, kw = call
                    self._dma_now(eng, out, in_, rd, wr, **kw)

    def barrier(self, include_pool_dma=False):
        self.flush()
        keys = ["pe", "act", "dve", "pool"] + self.dq["sp"][0] + self.dq["act"][0]
        if include_pool_dma:
            keys += self.dq["pool"][0]
        for e in ("pe", "act", "dve", "pool", "sp"):
            for k in keys:
                if (k == e and e == "pe") or self.cnt[k] == 0:
                    continue
                self._wait(e, k, self.cnt[k])


def bcast(ap, shape, axis):
    return ap.unsqueeze(axis).broadcast_to(list(shape))


IN_SPECS = [
    ("xp", [512, 1024]), ("xs", [2048, 1024]), ("cvec", [2, 1024]),
    ("cdk", [4, 256, 512]), ("cdv", [4, 256, 512]), ("cgk", [4, 256, 128]), ("cgv", [4, 256, 128]),
    ("sssm", [4, 2, 8, 64, 64]), ("smc", [4, 2, 4, 128, 128]), ("smn", [4, 2, 4, 128]), ("smm", [4, 8]),
    ("w_ada", [4, 1024, 6144]), ("b_ada", [4, 6144]), ("norm1", [4, 1024]), ("norm2", [4, 1024]),
    ("w_in", [4, 1024, IN_COLS]), ("w_out", [4, 2048, 1024]),
    ("conv_ssd_w", [4, 5, 768]), ("conv_ssd_b", [4, 768]), ("ssd_a_log", [4, 16]), ("ssd_dt_bias", [4, 16]),
    ("ssd_d", [4, 8]), ("ssd_norm", [4, 512]),
    ("diff_lq1", [4, 64]), ("diff_lk1", [4, 64]), ("diff_lq2", [4, 64]), ("diff_lk2", [4, 64]),
    ("conv_mlstm_w", [4, 5, 1024]), ("conv_mlstm_b", [4, 1024]), ("mlstm_gate_b", [4, 16]), ("mlstm_norm", [4, 512]),
    ("gqa_q_norm", [4, 64]), ("gqa_k_norm", [4, 64]),
    ("w_ffn_in", [4, 1024, 2 * D_FF]), ("w_ffn_out", [4, D_FF, 1024]), ("norm_f", [1024]),
    ("consts", [128, 1152]), ("rope", [128, 2, 2048]),
]
OUT_SPECS = [
    ("yp", [512, 1024]), ("ys", [2048, 1024]),
    ("ndk", [2, 4, 256, 512]), ("ndv", [2, 4, 256, 512]), ("ngk", [2, 4, 256, 128]), ("ngv", [2, 4, 256, 128]),
    ("nssm", [2, 4, 2, 8, 64, 64]), ("nmc", [2, 4, 2, 4, 128, 128]), ("nmn", [2, 4, 2, 4, 128]), ("nmm", [2, 4, 8]),
]


def make_consts():
    c = np.zeros((128, 1152), np.float32)
    k = np.arange(128)
    c[:, 0:128] = np.eye(128)
    c[:, 128:256] = 1.0
    c[:, 256:384] = (k[:, None] <= k[None, :])
    c[:, 384:512] = (k[:, None] >= k[None, :])
    c[:, 512:640] = np.where(k[:, None] <= k[None, :], 0.0, NEG)
    c[:, 640:768] = np.where(k[:, None] >= k[None, :], 0.0, NEG)
    c[:, 768:896] = (k[:, None] // 64 == k[None, :] // 64)
    rm = np.zeros((128, 128), np.float32)
    for dp in range(128):
        half = (dp % 32) // 16
        if half == 0:
            rm[dp + 16, dp] = -1.0
        else:
            rm[dp - 16, dp] = 1.0
    c[:, 896:1024] = rm
    c[64, 1024:1088] = 1.0
    c[0, 1088:1152] = 1.0
    return c


def make_rope():
    t = np.arange(2048)
    r = (t // 64).astype(np.float32)
    cc = (t % 64).astype(np.float32)
    nf = 16
    freqs = (10000.0 ** (-np.arange(nf, dtype=np.float32) / nf)).astype(np.float32)
    ang = np.stack([r[:, None] * freqs, cc[:, None] * freqs], axis=1).astype(np.float32)
    out = np.zeros((128, 2, 2048), np.float32)
    for p in range(128):
        d = p % 64
        a = d // 32
        f = d % 16
        out[p, 0] = np.cos(ang[:, a, f])
        out[p, 1] = np.sin(ang[:, a, f])
    return out


def build_program():
    nc = bass.Bass("TRN2", target_bir_lowering=False)
    kb = KB(nc)
    D = {}
    for name, shape in IN_SPECS:
        if shape[0] == 4 and len(shape) > 1:
            shape = [NL] + list(shape[1:])
        D[name] = nc.dram_tensor(name, shape, F32, kind="ExternalInput").ap()
    for name, shape in OUT_SPECS:
        D[name] = nc.dram_tensor(name, shape, F32, kind="ExternalOutput").ap()
    mod_d = nc.dram_tensor("mod_scr", [4, 2, 6144], F32, kind="Internal").ap()

    top = ExitStack()

    class scope:
        def __enter__(self_):
            self_.st = ExitStack()
            return self_.st

        def __exit__(self_, *a):
            if a[0] is None:
                kb.barrier()
            self_.st.close()
            return False

    uid = [0]

    def alloc(stack, name, shape, dt, psum=False):
        uid[0] += 1
        name = "%s_%d" % (name, uid[0])
        cm = nc.psum_tensor(name, shape, dt) if psum else nc.sbuf_tensor(name, shape, dt)
        return Buf(stack.enter_context(cm))

    xT = [alloc(top, "xT%d" % i, [128, 8, 512], F32) for i in range(4)]
    hT = [alloc(top, "hT%d" % i, [128, 8, 512], BF16) for i in range(4)]
    NW = 2
    wbufs = [alloc(top, "wb%d" % i, [128, 4096], BF16) for i in range(NW)]
    wstate = [0]
    psb = [alloc(top, "ps%d" % i, [128, 512], F32, psum=True) for i in range(8)]
    pstate = [0]
    cf = alloc(top, "cf", [128, 640], F32)
    cb = alloc(top, "cb", [128, 1024], BF16)
    modc = alloc(top, "modc", [128, 6, 8], F32)
    nrm = alloc(top, "nrm", [128, 2, 8], F32)
    AB = alloc(top, "AB", [128, 4, 8], F32)
    nfc = alloc(top, "nfc", [128, 8], F32)
    lnsb = alloc(top, "lnsb", [128, 1], F32)
    lns_col = lnsb.t

    wo_buf = alloc(top, "wo_buf", [128, 4, 1024], BF16)
    accstate = [0]

    def ps():
        b = psb[pstate[0] % 4]
        pstate[0] += 1
        return b

    def ps_acc():
        b = psb[4 + accstate[0] % 4]
        accstate[0] += 1
        return b

    def wload(pieces, kch):
        b = wbufs[wstate[0] % NW]
        wstate[0] += 1
        ntot = sum(n for _, n in pieces)
        assert kch * ntot <= 4096, (kch, ntot)
        view = b.t[:, 0:kch * ntot].rearrange("p (k n) -> p k n", k=kch)
        o = 0
        for ap, n in pieces:
            kb.dma("pool", view[:, :, o:o + n], ap.rearrange("(k p) n -> p k n", p=128), writes=[b])
            o += n
        return b, view

    ident_f = cf.t[:, 0:128]
    ones_f = cf.t[:, 128:256]
    ident_b = cb.t[:, 0:128]
    selm_f = cf.t[:, 512:640]
    ones_b = cb.t[:, 128:256]

    kb.dma("sp", cf[:, 0:512], D["consts"][:, 0:512], writes=[cf])
    kb.dma("sp", cf[:, 512:640], D["consts"][:, 1024:1152], writes=[cf])
    kb.dma("pool", cb[:], D["consts"][:, 0:1024], writes=[cb])
    kb.op("dve", lambda e: e.memset(lnsb[:], math.log(128 ** -0.5)), writes=[lnsb])
    kb.dma("sp", nfc[:], D["norm_f"].rearrange("(c p) -> p c", p=128), writes=[nfc], allow_slow_non_contiguous=True)

    modall = alloc(top, "modall", [128, NL, 48, 2], F32)
    ball = alloc(top, "ball", [128, NL, 48], F32)
    nrmall = alloc(top, "nrmall", [128, NL, 2, 8], F32)
    for l in range(NL):
        kb.dma("sp", ball[:, l, :], D["b_ada"][l].rearrange("(j p) -> p j", p=128), writes=[ball], allow_slow_non_contiguous=True)
        kb.dma("sp", nrmall[:, l, 0, :], D["norm1"][l].rearrange("(c p) -> p c", p=128), writes=[nrmall], allow_slow_non_contiguous=True)
        kb.dma("sp", nrmall[:, l, 1, :], D["norm2"][l].rearrange("(c p) -> p c", p=128), writes=[nrmall], allow_slow_non_contiguous=True)
    with scope() as st:
        cT = alloc(st, "cT", [128, 2, 8], F32)
        cTb = alloc(st, "cTb", [128, 2, 8], BF16)
        sig = alloc(st, "csig", [128, 2, 8], F32)
        for g in range(2):
            kb.dma("sp", cT[:, g, :], D["cvec"][g].rearrange("(c p) -> p c", p=128), writes=[cT], allow_slow_non_contiguous=True)
        kb.op("act", lambda e: e.activation(out=sig[:], in_=cT[:], func=AF.Sigmoid), reads=[cT], writes=[sig])
        kb.op("dve", lambda e: e.tensor_tensor(out=cTb[:], in0=cT[:], in1=sig[:], op=ALU.mult), reads=[cT, sig], writes=[cTb])
        for l in range(NL):
            for blk in range(12):
                c0 = blk * 512
                wb, wv = wload([(D["w_ada"][l][:, c0:c0 + 512], 512)], 8)
                p = ps()
                for cc in range(4):
                    for k in range(8):
                        kb.op("pe", lambda e: e.matmul(p[:, 2 * cc:2 * cc + 2], lhsT=wv[:, k, cc * 128:(cc + 1) * 128], rhs=cTb[:, :, k], start=(k == 0), stop=(k == 7)),
                              reads=[cTb, wb], writes=[p], inc=(k == 7 and cc == 3))
                kb.op("dve", lambda e: e.tensor_tensor(out=modall[:, l, blk * 4:(blk + 1) * 4, :], in0=p[:, 0:8].rearrange("p (j g) -> p j g", g=2),
                                                       in1=bcast(ball[:, l, blk * 4:(blk + 1) * 4], [128, 4, 2], 2), op=ALU.add), reads=[p, ball], writes=[modall])
        kb.barrier()

    def load_x(src, nblk):
        with scope() as st:
            xin = [alloc(st, "xin%d" % i, [128, 1024], F32) for i in range(2)]
            for tb in range(nblk):
                tiles = []
                for tl in range(4):
                    pass
                for tl in range(4):
                    xi = xin[(tb * 4 + tl) % 2]
                    t0 = (tb * 4 + tl) * 128
                    kb.dma("sp", xi[:], src[t0:t0 + 128, :], writes=[xi])
                    for half in range(2):
                        p = ps()
                        for cc in range(4):
                            c = half * 4 + cc
                            kb.op("pe", lambda e: e.matmul(p[:, cc * 128:(cc + 1) * 128], lhsT=xi[:, c * 128:(c + 1) * 128], rhs=ident_f,
                                                           start=True, stop=True), reads=[xi, cf], writes=[p], inc=(cc == 3))
                        kb.op("act", lambda e: e.activation(
                            out=xT[tb][:, half * 4:half * 4 + 4, tl * 128:(tl + 1) * 128],
                            in_=p[:, :].rearrange("p (c t) -> p c t", c=4), func=AF.Copy), reads=[p], writes=[xT[tb]])

    def norm_block(st_tmp, tb, Acol, Bcol, dst, dst_dt_is_bf16=True):
        sq, rstd, tmp2 = st_tmp
        if isinstance(sq, list):
            sq, rstd = sq[tb % 2], rstd[tb % 2]
        kb.op("act", lambda e: e.activation(out=sq[:], in_=xT[tb][:], func=AF.Square), reads=[xT[tb]], writes=[sq])
        p = ps()
        for c in range(8):
            kb.op("pe", lambda e: e.matmul(p[:, :], lhsT=ones_b, rhs=sq[:, c, :], start=(c == 0), stop=(c == 7)),
                  reads=[sq, cb], writes=[p], inc=(c == 7))
        kb.op("dve", lambda e: e.tensor_scalar(out=rstd[:], in0=p[:, :], scalar1=1.0 / D_MODEL, scalar2=EPS, op0=ALU.mult, op1=ALU.add),
              reads=[p], writes=[rstd])
        kb.op("act", lambda e: e.activation(out=rstd[:], in_=rstd[:], func=AF.Ln), reads=[rstd], writes=[rstd])
        kb.op("act", lambda e: e.activation(out=rstd[:], in_=rstd[:], func=AF.Exp, scale=-0.5), reads=[rstd], writes=[rstd])
        for c in range(8):
            t2 = tmp2[c % len(tmp2)]
            kb.op("dve", lambda e: e.tensor_tensor(out=t2[:], in0=xT[tb][:, c, :], in1=rstd[:], op=ALU.mult),
                  reads=[xT[tb], rstd], writes=[t2])
            if Bcol is not None:
                kb.op("act", lambda e: e.activation(out=dst[:, c, :], in_=t2[:], func=AF.Identity, bias=Bcol[:, c:c + 1], scale=Acol[:, c:c + 1]),
                      reads=[t2, AB], writes=[dst])
            else:
                kb.op("act", lambda e: e.activation(out=dst[:, c, :], in_=t2[:], func=AF.Identity, scale=Acol[:, c:c + 1]),
                      reads=[t2, nfc], writes=[dst])

    def load_mod(l, g):
        kb.op("dve", lambda e: e.tensor_copy(out=modc[:], in_=modall[:, l, :, g].rearrange("p (v c) -> p v c", v=6)), reads=[modall], writes=[modc])
        kb.op("dve", lambda e: e.tensor_copy(out=nrm[:], in_=nrmall[:, l, :, :]), reads=[nrmall], writes=[nrm])
        for j, (vs, vh) in enumerate(((1, 0), (4, 3))):
            kb.op("dve", lambda e: e.scalar_tensor_tensor(out=AB[:, 2 * j, :], in0=modc[:, vs, :], scalar=1.0, in1=nrm[:, j, :],
                                                          op0=ALU.add, op1=ALU.mult), reads=[modc, nrm], writes=[AB])
            kb.op("dve", lambda e: e.tensor_copy(out=AB[:, 2 * j + 1, :], in_=modc[:, vh, :]), reads=[modc], writes=[AB])

    def ffn(l, blocks):
        kb.label = 'ffn'
        nb = len(blocks)
        with scope() as st:
            actT = alloc(st, "actT", [128, 22, nb * 512], BF16)
            sg = [alloc(st, "sg%d" % i, [128, 512], F32) for i in range(2)]
            it = 0
            for jj in range(11):
                c0 = jj * 256
                wb, wv = wload([(D["w_ffn_in"][l][:, c0:c0 + 256], 256), (D["w_ffn_in"][l][:, D_FF + c0:D_FF + c0 + 256], 256)], 8)
                for j2 in range(2):
                    j = jj * 2 + j2
                    for bi, tb in enumerate(blocks):
                        pg = ps()
                        pu = ps()
                        for k in range(8):
                            kb.op("pe", lambda e: e.matmul(pg[:, :], lhsT=wv[:, k, j2 * 128:(j2 + 1) * 128], rhs=hT[tb][:, k, :],
                                                           start=(k == 0), stop=(k == 7)), reads=[wb, hT[tb]], writes=[pg], inc=(k == 7))
                        for k in range(8):
                            kb.op("pe", lambda e: e.matmul(pu[:, :], lhsT=wv[:, k, 256 + j2 * 128:256 + (j2 + 1) * 128], rhs=hT[tb][:, k, :],
                                                           start=(k == 0), stop=(k == 7)), reads=[wb, hT[tb]], writes=[pu], inc=(k == 7))
                        s = sg[it % 2]
                        it += 1
                        kb.op("act", lambda e: e.activation(out=s[:], in_=pg[:, :], func=AF.Silu), reads=[pg], writes=[s])
                        kb.op("dve", lambda e: e.tensor_tensor(out=actT[:, j, bi * 512:(bi + 1) * 512], in0=s[:], in1=pu[:, :], op=ALU.mult),
                              reads=[s, pu], writes=[actT])
            for c in range(8):
                wb, wv = wload([(D["w_ffn_out"][l][:, c * 128:(c + 1) * 128], 128)], 22)
                for bi, tb in enumerate(blocks):
                    p = ps()
                    for k in range(22):
                        kb.op("pe", lambda e: e.matmul(p[:, :], lhsT=wv[:, k, :], rhs=actT[:, k, bi * 512:(bi + 1) * 512],
                                                       start=(k == 0), stop=(k == 21)), reads=[wb, actT], writes=[p], inc=(k == 21))
                    kb.op("dve", lambda e: e.scalar_tensor_tensor(out=xT[tb][:, c, :], in0=p[:, :], scalar=modc[:, 5, c:c + 1], in1=xT[tb][:, c, :],
                                                                  op0=ALU.mult, op1=ALU.add), reads=[p, modc, xT[tb]], writes=[xT[tb]])
        kb.barrier()

    def final_out(nblk, dst):
        kb.label = 'final'
        with scope() as st:
            sq = alloc(st, "sq", [128, 8, 512], BF16)
            rstd = alloc(st, "rstd", [128, 512], F32)
            tmp2 = [alloc(st, "tmpn%d" % i, [128, 512], F32) for i in range(2)]
            xn = alloc(st, "xn", [128, 8, 512], F32)
            ot = [alloc(st, "ot%d" % i, [128, 1024], F32) for i in range(2)]
            for tb in range(nblk):
                norm_block((sq, rstd, tmp2), tb, nfc, None, xn)
                for tl in range(4):
                    o = ot[tl % 2]
                    for half in range(2):
                        p = ps()
                        for cc in range(4):
                            c = half * 4 + cc
                            kb.op("pe", lambda e: e.matmul(p[:, cc * 128:(cc + 1) * 128], lhsT=xn[:, c, tl * 128:(tl + 1) * 128], rhs=ident_f,
                                                           start=True, stop=True), reads=[xn, cf], writes=[p], inc=(cc == 3))
                        kb.op("act", lambda e: e.activation(out=o[:, half * 512:(half + 1) * 512], in_=p[:, :], func=AF.Copy), reads=[p], writes=[o])
                    t0 = (tb * 4 + tl) * 128
                    kb.dma("sp", dst[t0:t0 + 128, :], o[:], reads=[o])
        kb.barrier()


    bd64_b = cb.t[:, 768:896]
    rm_b = cb.t[:, 896:1024]

    def proj_tm(wb, wv, s0, n, tb, tl, p):
        for k in range(8):
            kb.op("pe", lambda e: e.matmul(p[:, 0:n], lhsT=hT[tb][:, k, tl * 128:(tl + 1) * 128], rhs=wv[:, k, s0:s0 + n],
                                           start=(k == 0), stop=(k == 7)), reads=[wb, hT[tb]], writes=[p], inc=(k == 7))

    def load_wo(l, row0):
        kb.dma("pool", wo_buf[:], D["w_out"][l][row0:row0 + 512, :].rearrange("(k p) n -> p k n", p=128), writes=[wo_buf])

    def mixer_out(st, ytm, tb, yT):
        for c in range(4):
            p = ps()
            for tl in range(4):
                kb.op("pe", lambda e: e.matmul(p[:, tl * 128:(tl + 1) * 128], lhsT=ytm[:, tl, c * 128:(c + 1) * 128], rhs=ident_b,
                                               start=True, stop=True), reads=[ytm, cb], writes=[p], inc=(tl == 3))
            kb.op("act", lambda e: e.activation(out=yT[:, c, :], in_=p[:, :], func=AF.Copy), reads=[p], writes=[yT])
        for c in range(8):
            p = ps()
            for k in range(4):
                kb.op("pe", lambda e: e.matmul(p[:, :], lhsT=wo_buf[:, k, c * 128:(c + 1) * 128], rhs=yT[:, k, :],
                                               start=(k == 0), stop=(k == 3)), reads=[wo_buf, yT], writes=[p], inc=(k == 3))
            kb.op("dve", lambda e: e.scalar_tensor_tensor(out=xT[tb][:, c, :], in0=p[:, :], scalar=modc[:, 2, c:c + 1], in1=xT[tb][:, c, :],
                                                          op0=ALU.mult, op1=ALU.add), reads=[p, modc, xT[tb]], writes=[xT[tb]])

    def attention(l, g, kind, nblk):
        kb.label = 'attn_%s_g%d' % (kind, g)
        sample = (g == 1)
        L = 2048 if sample else 256
        nseq = 1 if sample else 2
        nctx = 2 if sample else 0
        lt = L // 128
        nkt = lt + nctx
        ntok = nblk * 512
        if kind == "d":
            qc0, kc0, vc0, nkc, nvh, ve, orow0, nheads = 4896, 5408, 5536, 1, 2, 64, 1536, 8
            ck, cv, ok, ov = D["cgk"], D["cgv"], D["ngk"], D["ngv"]
        else:
            qc0, kc0, vc0, nkc, nvh, ve, orow0, nheads = 1296, 1808, 2320, 4, 4, 128, 512, 4
            ck, cv, ok, ov = D["cdk"], D["cdv"], D["ndk"], D["ndv"]
        scale = 64 ** -0.5
        kw = nkc * 128
        vw = nvh * ve
        lam_init = 0.8 - 0.6 * math.exp(-0.3 * l)
        with scope() as st:
            qT = alloc(st, "qT", [128, 4, ntok], BF16)
            kT = alloc(st, "kT", [128, nkc, nseq * nkt * 128], BF16)
            vsw = ve + 1 if kind == "d" else ve
            vaug = alloc(st, "vaug", [128, nseq * nkt, nvh, vsw], BF16)
            vodd = alloc(st, "vodd", [128, nseq * nkt, nvh, 128], BF16) if kind == "d" else None
            yT = alloc(st, "yT", [128, 4, 512], BF16)
            pTs = [alloc(st, "pT%d" % i, [128, 512], BF16) for i in range(4)]
            sqb = alloc(st, "sqb", [128, 512], BF16)
            rs = alloc(st, "rs", [128, 512], F32)
            qn = alloc(st, "qn", [128, 512], BF16)
            t1 = alloc(st, "t1", [128, 512], F32)
            fin_bufs = [(rs, t1, None, sqb)]
            gcol = alloc(st, "gcol", [128, 2], F32)
            osb = alloc(st, "osb", [128, 512], F32)
            t2 = osb
            fin_bufs[0] = (rs, t1, osb, sqb)
            sm = alloc(st, "sm", [128, 16], F32)
            lamt = alloc(st, "lamt", [128, 4, 64], F32)
            kng = alloc(st, "kng", [128, 64], F32)
            ropeT = alloc(st, "ropeT", [128, 2, 2048], BF16) if sample else None
            if sample and kind == "b":
                pass
            else:
                try:
                    fin_bufs.append((alloc(st, "rs2", [128, 512], F32), alloc(st, "t12", [128, 512], F32),
                                     alloc(st, "osb2", [128, 512], F32) if kind == "b" else None, alloc(st, "sqb2", [128, 512], BF16) if kind == "b" else None))
                except AssertionError:
                    pass
            load_wo(l, orow0)
            if kind == "d":
                kb.op("dve", lambda e: e.memset(vaug[:, :, :, ve:ve + 1], 1.0), writes=[vaug])
                kb.op("dve", lambda e: e.memset(vodd[:, :, :, 0:1], 1.0), writes=[vodd])
                kb.op("dve", lambda e: e.memset(vodd[:, :, :, 1:64], 0.0), writes=[vodd])
            if sample:
                kb.dma("pool", ropeT[:], D["rope"], writes=[ropeT])
            if kind == "d":
                for j, nm in enumerate(("gqa_q_norm", "gqa_k_norm")):
                    for hh in range(2):
                        kb.dma("sp", gcol[hh * 64:(hh + 1) * 64, j:j + 1], D[nm][l].rearrange("(d o) -> d o", o=1), writes=[gcol])
                kb.dma("sp", kng[:], D["gqa_k_norm"][l].partition_broadcast(128), writes=[kng])
            else:
                for j, nm in enumerate(("diff_lq1", "diff_lk1", "diff_lq2", "diff_lk2")):
                    kb.dma("sp", lamt[:, j, :], D[nm][l].partition_broadcast(128), writes=[lamt])
                kb.op("dve", lambda e: e.tensor_tensor(out=lamt[:, 0, :], in0=lamt[:, 0, :], in1=lamt[:, 1, :], op=ALU.mult), reads=[lamt], writes=[lamt])
                kb.op("dve", lambda e: e.tensor_tensor(out=lamt[:, 2, :], in0=lamt[:, 2, :], in1=lamt[:, 3, :], op=ALU.mult), reads=[lamt], writes=[lamt])
                kb.op("dve", lambda e: e.tensor_reduce(out=sm[:, 2:3], in_=lamt[:, 0, :], axis=AX.X, op=ALU.add), reads=[lamt], writes=[sm])
                kb.op("dve", lambda e: e.tensor_reduce(out=sm[:, 3:4], in_=lamt[:, 2, :], axis=AX.X, op=ALU.add), reads=[lamt], writes=[sm])
                kb.op("act", lambda e: e.activation(out=sm[:, 2:4], in_=sm[:, 2:4], func=AF.Exp), reads=[sm], writes=[sm])
                kb.op("dve", lambda e: e.tensor_tensor(out=sm[:, 0:1], in0=sm[:, 2:3], in1=sm[:, 3:4], op=ALU.subtract), reads=[sm], writes=[sm])
                kb.op("dve", lambda e: e.tensor_scalar(out=sm[:, 1:2], in0=sm[:, 0:1], scalar1=lam_init, scalar2=-1.0, op0=ALU.add, op1=ALU.mult), reads=[sm], writes=[sm])

            def qk_post(p, dst, tb, normj):
                src = p
                if kind == "d":
                    kb.op("act", lambda e: e.activation(out=sqb[:], in_=p[:, :], func=AF.Square), reads=[p], writes=[sqb])
                    pn = ps()
                    kb.op("pe", lambda e: e.matmul(pn[:, :], lhsT=bd64_b, rhs=sqb[:], start=True, stop=True), reads=[cb, sqb], writes=[pn])
                    kb.op("dve", lambda e: e.tensor_scalar(out=rs[:], in0=pn[:, :], scalar1=1.0 / 64, scalar2=EPS, op0=ALU.mult, op1=ALU.add), reads=[pn], writes=[rs])
                    kb.op("act", lambda e: e.activation(out=rs[:], in_=rs[:], func=AF.Ln), reads=[rs], writes=[rs])
                    kb.op("act", lambda e: e.activation(out=rs[:], in_=rs[:], func=AF.Exp, scale=-0.5), reads=[rs], writes=[rs])
                    tgt = qn if sample else None
                    o_ap = qn[:] if sample else dst
                    kb.op("dve", lambda e: e.scalar_tensor_tensor(out=o_ap, in0=p[:, :], scalar=gcol[:, normj:normj + 1], in1=rs[:], op0=ALU.mult, op1=ALU.mult),
                          reads=[p, gcol, rs], writes=[qn if sample else dst_buf[0]])
                else:
                    o_ap = qn[:] if sample else dst
                    kb.op("act", lambda e: e.activation(out=o_ap, in_=p[:, :], func=AF.Copy), reads=[p], writes=[qn if sample else dst_buf[0]])
                if sample:
                    pr = ps()
                    kb.op("pe", lambda e: e.matmul(pr[:, :], lhsT=rm_b, rhs=qn[:], start=True, stop=True), reads=[cb, qn], writes=[pr])
                    kb.op("dve", lambda e: e.tensor_tensor(out=t1[:], in0=qn[:], in1=ropeT[:, 0, tb * 512:(tb + 1) * 512], op=ALU.mult), reads=[qn, ropeT], writes=[t1])
                    kb.op("dve", lambda e: e.tensor_tensor(out=t2[:], in0=pr[:, :], in1=ropeT[:, 1, tb * 512:(tb + 1) * 512], op=ALU.mult), reads=[pr, ropeT], writes=[t2])
                    kb.op("dve", lambda e: e.tensor_tensor(out=dst, in0=t1[:], in1=t2[:], op=ALU.add), reads=[t1, t2], writes=[dst_buf[0]])

            dst_buf = [None]
            blocks = list(range(nblk))
            if kind == "d":
                pcs = []
                for j in range(4):
                    for hh in (j, 4 + j):
                        pcs.append((D["w_in"][l][:, qc0 + hh * 64:qc0 + (hh + 1) * 64], 64))
                wb, wv = wload(pcs, 8)
            else:
                wb, wv = wload([(D["w_in"][l][:, qc0:qc0 + 512], 512)], 8)
            dst_buf[0] = qT
            for j in range(4):
                for tb in blocks:
                    p = ps()
                    for k in range(8):
                        lh = wv[:, k, j * 128:(j + 1) * 128]
                        kb.op("pe", lambda e: e.matmul(p[:, :], lhsT=lh, rhs=hT[tb][:, k, :], start=(k == 0), stop=(k == 7)),
                              reads=[wb, hT[tb]], writes=[p], inc=(k == 7))
                    qk_post(p, qT[:, j, tb * 512:(tb + 1) * 512], tb, 0)
            wb, wv = wload([(D["w_in"][l][:, kc0:kc0 + kw], kw)], 8)
            dst_buf[0] = kT
            for j in range(nkc):
                for tb in blocks:
                    p = ps()
                    for k in range(8):
                        kb.op("pe", lambda e: e.matmul(p[:, :], lhsT=wv[:, k, j * 128:(j + 1) * 128], rhs=hT[tb][:, k, :], start=(k == 0), stop=(k == 7)),
                              reads=[wb, hT[tb]], writes=[p], inc=(k == 7))
                    qk_post(p, kT[:, j, nctx * 128 + tb * 512:nctx * 128 + (tb + 1) * 512], tb, 1)
            if not sample:
                for tt in range(ntok // 128):
                    sq_, r0 = divmod(tt, lt)
                    p = ps()
                    proj_tm(wb, wv, 0, kw, tt // 4, tt % 4, p)
                    kb.op("act", lambda e: e.activation(out=osb[:, 0:kw], in_=p[:, 0:kw], func=AF.Copy), reads=[p], writes=[osb])
                    if kind == "d":
                        kb.op("dve", lambda e: e.tensor_tensor(out=t1[:, 0:128], in0=osb[:, 0:128], in1=osb[:, 0:128], op=ALU.mult), reads=[osb], writes=[t1])
                        kb.op("dve", lambda e: e.tensor_reduce(out=sm[:, 8:10], in_=t1[:, 0:128].rearrange("p (h d) -> p h d", d=64), axis=AX.X, op=ALU.add), reads=[t1], writes=[sm])
                        kb.op("dve", lambda e: e.tensor_scalar(out=sm[:, 8:10], in0=sm[:, 8:10], scalar1=1.0 / 64, scalar2=EPS, op0=ALU.mult, op1=ALU.add), reads=[sm], writes=[sm])
                        kb.op("act", lambda e: e.activation(out=sm[:, 8:10], in_=sm[:, 8:10], func=AF.Ln), reads=[sm], writes=[sm])
                        kb.op("act", lambda e: e.activation(out=sm[:, 8:10], in_=sm[:, 8:10], func=AF.Exp, scale=-0.5), reads=[sm], writes=[sm])
                        kb.op("dve", lambda e: e.tensor_tensor(out=t1[:, 0:128].rearrange("p (h d) -> p h d", d=64), in0=osb[:, 0:128].rearrange("p (h d) -> p h d", d=64),
                                                               in1=bcast(sm[:, 8:10], [128, 2, 64], 2), op=ALU.mult), reads=[osb, sm], writes=[t1])
                        kb.op("dve", lambda e: e.tensor_tensor(out=t2[:, 0:128].rearrange("p (h d) -> p h d", d=64), in0=t1[:, 0:128].rearrange("p (h d) -> p h d", d=64),
                                                               in1=bcast(kng[:], [128, 2, 64], 1), op=ALU.mult), reads=[t1, kng], writes=[t2])
                        kb.dma("sp", ok[sq_, l, r0 * 128:(r0 + 1) * 128, :], t2[:, 0:128], reads=[t2])
                    else:
                        kb.dma("sp", ok[sq_, l, r0 * 128:(r0 + 1) * 128, :], osb[:, 0:kw], reads=[osb])
            wb, wv = wload([(D["w_in"][l][:, vc0:vc0 + vw], vw)], 8)
            for tt in range(ntok // 128):
                sq_, r0 = divmod(tt, lt)
                ktile = sq_ * nkt + nctx + r0
                p = ps()
                proj_tm(wb, wv, 0, vw, tt // 4, tt % 4, p)
                kb.op("act", lambda e: e.activation(out=vaug[:, ktile, :, 0:ve], in_=p[:, 0:vw].rearrange("p (h e) -> p h e", e=ve), func=AF.Copy), reads=[p], writes=[vaug])
                if kind == "d":
                    kb.op("act", lambda e: e.activation(out=vodd[:, ktile, :, 64:128], in_=p[:, 0:vw].rearrange("p (h e) -> p h e", e=ve), func=AF.Copy), reads=[p], writes=[vodd])
                if not sample:
                    kb.op("dve", lambda e: e.tensor_copy(out=osb[:, 0:vw], in_=p[:, 0:vw]), reads=[p], writes=[osb])
                    kb.dma("sp", ov[sq_, l, r0 * 128:(r0 + 1) * 128, :], osb[:, 0:vw], reads=[osb])
            if sample:
                with scope() as st2:
                    ctxk = alloc(st2, "ctxk", [128, kw], F32)
                    for t in range(2):
                        kb.dma("sp", ctxk[:], ck[l][t * 128:(t + 1) * 128, :], writes=[ctxk])
                        for j in range(nkc):
                            p = ps()
                            kb.op("pe", lambda e: e.matmul(p[:, 0:128], lhsT=ctxk[:, j * 128:(j + 1) * 128], rhs=ident_f, start=True, stop=True),
                                  reads=[ctxk, cf], writes=[p])
                            kb.op("act", lambda e: e.activation(out=kT[:, j, t * 128:(t + 1) * 128], in_=p[:, 0:128], func=AF.Copy), reads=[p], writes=[kT])
                    for t in range(2):
                        kb.dma("pool", vaug[:, t, :, 0:ve], cv[l][t * 128:(t + 1) * 128, :].rearrange("p (h e) -> p h e", e=ve), writes=[vaug])
                        if kind == "d":
                            kb.dma("pool", vodd[:, t, :, 64:128], cv[l][t * 128:(t + 1) * 128, :].rearrange("p (h e) -> p h e", e=ve), writes=[vodd])
                    kb.barrier(include_pool_dma=True)
            qblk = min(L, 512)
            pti = 0
            for s_ in range(nseq):
                for qb in range(L // qblk):
                    q0 = s_ * L + qb * qblk
                    qs = slice(q0, q0 + qblk)
                    yc = slice(q0 % 512, q0 % 512 + qblk)
                    its = []
                    if kind == "d":
                        for hp in range(4):
                            for kt in range(nkt):
                                its.append((hp, kt, 0))
                                its.append((hp + 4, kt, 0))
                    else:
                        for h in range(nheads):
                            for kt in range(nkt):
                                its.append((h, kt, 0))
                                its.append((h, kt, 1))
                    hstate = {}
                    pts = {}
                    DEPTH = 2

                    def stageA2(j):
                        pss = []
                        for i in (2 * j, 2 * j + 1):
                            h, kt, r = its[i]
                            kc = (s_ * nkt + kt) * 128
                            if kind == "d":
                                rows, qch = (h // 4) * 64, h % 4
                                lhs = kT[rows:rows + 64, 0, kc:kc + 128]
                                rh = qT[rows:rows + 64, qch, qs]
                            else:
                                lhs = kT[r * 64:(r + 1) * 64, h, kc:kc + 128]
                                rh = qT[r * 64:(r + 1) * 64, h, qs]
                            pS = ps()
                            kb.op("pe", lambda e: e.matmul(pS[:, 0:qblk], lhsT=lhs, rhs=rh, start=True, stop=True), reads=[kT, qT], writes=[pS], inc=(i % 2 == 1))
                            pss.append(pS)
                        for i, pS in zip((2 * j, 2 * j + 1), pss):
                            pT = pTs[i % 4]
                            kb.op("act", lambda e: e.activation(out=pT[:, 0:qblk], in_=pS[:, 0:qblk], func=AF.Exp, scale=scale), reads=[pS], writes=[pT])
                            pts[i] = pT

                    def stageC(i):
                        h, kt, r = its[i]
                        pT = pts.pop(i)
                        last = (kt == nkt - 1)
                        rs, t1, osb, sqb = fin_bufs[h % len(fin_bufs)]
                        if kind == "d":
                            vh, odd = h // 4, h % 2
                            if kt == 0:
                                hstate[h] = ps_acc()
                            accO = hstate[h]
                            mo = 128 if odd else ve + 1
                            lh = vodd[:, s_ * nkt + kt, vh, :] if odd else vaug[:, s_ * nkt + kt, vh, :]
                            kb.op("pe", lambda e: e.matmul(accO[0:mo, 0:qblk], lhsT=lh, rhs=pT[:, 0:qblk], start=(kt == 0), stop=last),
                                  reads=[pT, vodd if odd else vaug], writes=[accO], inc=True)
                            if last:
                                drow = 0 if odd else 64
                                orow = 64 if odd else 0
                                kb.op("act", lambda e: e.activation(out=rs[drow:drow + 1, 0:qblk], in_=accO[drow:drow + 1, 0:qblk], func=AF.Ln), reads=[accO], writes=[rs])
                                kb.op("act", lambda e: e.activation(out=rs[drow:drow + 1, 0:qblk], in_=rs[drow:drow + 1, 0:qblk], func=AF.Exp, scale=-1.0), reads=[rs], writes=[rs])
                                pB = ps()
                                kb.op("pe", lambda e: e.matmul(pB[:, 0:qblk], lhsT=selm_f[drow:drow + 1, :], rhs=rs[drow:drow + 1, 0:qblk], start=True, stop=True),
                                      reads=[cf, rs], writes=[pB])
                                kb.op("act", lambda e: e.activation(out=t1[orow:orow + 64, 0:qblk], in_=pB[orow:orow + 64, 0:qblk], func=AF.Copy), reads=[pB], writes=[t1])
                                kb.op("dve", lambda e: e.tensor_tensor(out=yT[orow:orow + 64, h // 2, yc], in0=accO[orow:orow + 64, 0:qblk], in1=t1[orow:orow + 64, 0:qblk], op=ALU.mult),
                                      reads=[accO, t1], writes=[yT])
                        else:
                            if kt == 0 and r == 0:
                                hstate[h] = ([ps_acc(), ps_acc()], [ps_acc(), ps_acc()])
                            accO, accD = hstate[h]
                            kb.op("pe", lambda e: e.matmul(accO[r][:, 0:qblk], lhsT=vaug[:, s_ * nkt + kt, h, :], rhs=pT[:, 0:qblk], start=(kt == 0), stop=last),
                                  reads=[pT, vaug], writes=[accO[r]], inc=False)
                            kb.op("pe", lambda e: e.matmul(accD[r][:, 0:qblk], lhsT=ones_b, rhs=pT[:, 0:qblk], start=(kt == 0), stop=last),
                                  reads=[pT, cb], writes=[accD[r]], inc=True)
                            if last and r == 1:
                                A, Bt, O = rs, t1, osb
                                kb.op("act", lambda e: e.activation(out=A[:, 0:qblk], in_=accD[0][:, 0:qblk], func=AF.Ln), reads=[accD[0]], writes=[A])
                                kb.op("act", lambda e: e.activation(out=A[:, 0:qblk], in_=A[:, 0:qblk], func=AF.Exp, scale=-1.0), reads=[A], writes=[A])
                                kb.op("act", lambda e: e.activation(out=Bt[:, 0:qblk], in_=accD[1][:, 0:qblk], func=AF.Ln), reads=[accD[1]], writes=[Bt])
                                kb.op("act", lambda e: e.activation(out=Bt[:, 0:qblk], in_=Bt[:, 0:qblk], func=AF.Exp, scale=-1.0), reads=[Bt], writes=[Bt])
                                kb.op("dve", lambda e: e.tensor_tensor(out=O[:, 0:qblk], in0=accO[0][:, 0:qblk], in1=A[:, 0:qblk], op=ALU.mult), reads=[accO[0], A], writes=[O])
                                kb.op("dve", lambda e: e.tensor_tensor(out=Bt[:, 0:qblk], in0=accO[1][:, 0:qblk], in1=Bt[:, 0:qblk], op=ALU.mult), reads=[accO[1], Bt], writes=[Bt])
                                kb.op("dve", lambda e: e.scalar_tensor_tensor(out=O[:, 0:qblk], in0=Bt[:, 0:qblk], scalar=sm[:, 1:2], in1=O[:, 0:qblk], op0=ALU.mult, op1=ALU.add),
                                      reads=[Bt, sm, O], writes=[O])
                                kb.op("act", lambda e: e.activation(out=sqb[:, 0:qblk], in_=O[:, 0:qblk], func=AF.Square), reads=[O], writes=[sqb])
                                pn = ps()
                                kb.op("pe", lambda e: e.matmul(pn[:, 0:qblk], lhsT=ones_b, rhs=sqb[:, 0:qblk], start=True, stop=True), reads=[cb, sqb], writes=[pn])
                                kb.op("dve", lambda e: e.tensor_scalar(out=A[:, 0:qblk], in0=pn[:, 0:qblk], scalar1=1.0 / 128, scalar2=EPS, op0=ALU.mult, op1=ALU.add), reads=[pn], writes=[A])
                                kb.op("act", lambda e: e.activation(out=A[:, 0:qblk], in_=A[:, 0:qblk], func=AF.Ln), reads=[A], writes=[A])
                                kb.op("act", lambda e: e.activation(out=A[:, 0:qblk], in_=A[:, 0:qblk], func=AF.Exp, scale=-0.5), reads=[A], writes=[A])
                                kb.op("dve", lambda e: e.scalar_tensor_tensor(out=yT[:, h, yc], in0=O[:, 0:qblk], scalar=1.0 - lam_init, in1=A[:, 0:qblk], op0=ALU.mult, op1=ALU.mult),
                                      reads=[O, A], writes=[yT])

                    n_pairs = len(its) // 2
                    for j in range(n_pairs + 1):
                        if j < n_pairs:
                            stageA2(j)
                        if j >= 1:
                            stageC(2 * (j - 1))
                            stageC(2 * (j - 1) + 1)
                    if (q0 + qblk) % 512 == 0:
                        tb = (q0 + qblk) // 512 - 1
                        for c in range(8):
                            p = ps()
                            for k in range(4):
                                kb.op("pe", lambda e: e.matmul(p[:, :], lhsT=wo_buf[:, k, c * 128:(c + 1) * 128], rhs=yT[:, k, :],
                                                               start=(k == 0), stop=(k == 3)), reads=[wo_buf, yT], writes=[p], inc=(k == 3))
                            kb.op("dve", lambda e: e.scalar_tensor_tensor(out=xT[tb][:, c, :], in0=p[:, :], scalar=modc[:, 2, c:c + 1], in1=xT[tb][:, c, :],
                                                                          op0=ALU.mult, op1=ALU.add), reads=[p, modc, xT[tb]], writes=[xT[tb]])
        kb.barrier()

    triF_f = cf.t[:, 256:384]
    triB_f = cf.t[:, 384:512]
    maskF_b = cb.t[:, 512:640]
    maskB_b = cb.t[:, 640:768]

    def proj_fm(l, c0, ncol, blocks, fn):
        for t0 in range(0, ncol, 512):
            n = min(512, ncol - t0)
            wb, wv = wload([(D["w_in"][l][:, c0 + t0:c0 + t0 + n], n)], 8)
            for s0 in range(0, n, 128):
                w = min(128, n - s0)
                for tb in blocks:
                    p = ps()
                    for k in range(8):
                        kb.op("pe", lambda e: e.matmul(p[0:w, :], lhsT=wv[:, k, s0:s0 + w], rhs=hT[tb][:, k, :], start=(k == 0), stop=(k == 7)),
                              reads=[wb, hT[tb]], writes=[p], inc=(k == 7))
                    fn((t0 + s0) // 128, tb, p, w)

    def conv_chunk(convin, acc, cwt, cbt, cc, nseq, L, dst_ap_fn, dst_buf):
        for s_ in range(nseq):
            a = acc[:, s_ * L:(s_ + 1) * L]
            kb.op("dve", lambda e: e.tensor_scalar(out=a, in0=convin[:, s_, 0:L], scalar1=cwt[:, cc, 0:1], scalar2=None, op0=ALU.mult),
                  reads=[convin, cwt], writes=[acc])
            for tap in range(1, 5):
                kb.op("dve", lambda e: e.scalar_tensor_tensor(out=a, in0=convin[:, s_, tap:tap + L], scalar=cwt[:, cc, tap:tap + 1], in1=a,
                                                              op0=ALU.mult, op1=ALU.add), reads=[convin, cwt, acc], writes=[acc])
            if isinstance(dst_buf, list):
                for gg in range(2):
                    kb.op("act", lambda e: e.activation(out=dst_buf[gg][gg * 64:(gg + 1) * 64, s_ * L:(s_ + 1) * L], in_=acc[gg * 64:(gg + 1) * 64, s_ * L:(s_ + 1) * L],
                                                        func=AF.Silu, bias=cbt[gg * 64:(gg + 1) * 64, cc:cc + 1]), reads=[acc, cbt], writes=[dst_buf[gg]])
            else:
                kb.op("act", lambda e: e.activation(out=dst_ap_fn(s_), in_=a, func=AF.Silu, bias=cbt[:, cc:cc + 1]), reads=[acc, cbt], writes=[dst_buf])

    def evac_conv_in(convin, p, tb, nseq, L):
        if nseq == 1:
            kb.op("act", lambda e: e.activation(out=convin[:, 0, 2 + tb * 512:2 + (tb + 1) * 512], in_=p[:, :], func=AF.Copy), reads=[p], writes=[convin])
        else:
            for s_ in range(2):
                kb.op("act", lambda e: e.activation(out=convin[:, s_, 2:2 + L], in_=p[:, s_ * L:(s_ + 1) * L], func=AF.Copy), reads=[p], writes=[convin])

    def transpose_to_tm(src, dst_fn, dst_buf, T):
        for t0 in range(0, T, 4):
            p = ps()
            for j in range(4):
                kb.op("pe", lambda e: e.matmul(p[:, j * 128:(j + 1) * 128], lhsT=src[:, (t0 + j) * 128:(t0 + j + 1) * 128], rhs=ident_b, start=True, stop=True),
                      reads=[src, cb], writes=[p], inc=(j == 3))
            kb.op("act", lambda e: e.activation(out=dst_fn(t0, 4), in_=p[:, :].rearrange("p (t c) -> p t c", t=4), func=AF.Copy), reads=[p], writes=[dst_buf])

    def ssd(l, g, nblk):
        kb.label = 'ssd_g%d' % g
        sample = (g == 1)
        L = 2048 if sample else 256
        nseq = 1 if sample else 2
        lt = L // 128
        ntok = nblk * 512
        T = ntok // 128
        blocks = list(range(nblk))
        with scope() as st:
            xtm = alloc(st, "xtm", [128, T, 512], BF16)
            Btm = alloc(st, "Btm", [128, T, 128], BF16)
            BT = alloc(st, "BT", [128, ntok], BF16)
            CTz = [alloc(st, "CT%d" % i, [128, ntok], BF16) for i in range(2)]
            for i_ in range(2):
                kb.op("dve", lambda e: e.memset(CTz[i_][:], 0.0), writes=[CTz[i_]])
            dtt = alloc(st, "dtt", [128, T, 16], F32)
            dtA = alloc(st, "dtA", [128, T, 16], F32)
            cum = alloc(st, "cum", [128, T, 16], F32)
            tot = alloc(st, "tot", [128, T, 16], F32)
            ecum = alloc(st, "ecum", [128, T, 16], F32)
            wd = alloc(st, "wd", [128, T, 16], F32)
            bj = alloc(st, "bj", [128, T, 16], F32)
            dec = alloc(st, "dec", [128, T, 2, 4], F32)
            abc = alloc(st, "abc", [128, 16], F32)
            dtb = alloc(st, "dtb", [128, 16], F32)
            dsk = alloc(st, "dsk", [128, 8], F32)
            ng = alloc(st, "ng", [128, 512], F32)
            cwt = alloc(st, "cwt", [128, 6, 5], F32)
            cbt = alloc(st, "cbt", [128, 6], F32)
            load_wo(l, 0)
            kb.dma("sp", abc[:], D["ssd_a_log"][l].partition_broadcast(128), writes=[abc])
            kb.dma("sp", dtb[:], D["ssd_dt_bias"][l].partition_broadcast(128), writes=[dtb])
            kb.dma("sp", dsk[:], D["ssd_d"][l].partition_broadcast(128), writes=[dsk])
            kb.dma("sp", ng[:], D["ssd_norm"][l].partition_broadcast(128), writes=[ng])
            for tap in range(5):
                kb.dma("sp", cwt[:, :, tap], D["conv_ssd_w"][l, tap].rearrange("(c p) -> p c", p=128), writes=[cwt], allow_slow_non_contiguous=True)
            kb.dma("sp", cbt[:], D["conv_ssd_b"][l].rearrange("(c p) -> p c", p=128), writes=[cbt], allow_slow_non_contiguous=True)
            kb.op("act", lambda e: e.activation(out=abc[:], in_=abc[:], func=AF.Exp), reads=[abc], writes=[abc])
            kb.op("dve", lambda e: e.tensor_scalar(out=abc[:], in0=abc[:], scalar1=-1.0, scalar2=None, op0=ALU.mult), reads=[abc], writes=[abc])
            with scope() as st1:
                convins = [alloc(st1, "convin", [128, nseq, L + 4], F32) for _ in range(2)]
                acc = alloc(st1, "cacc", [128, ntok], F32)
                xcT = alloc(st1, "xcT", [128, ntok], BF16)
                for cv_ in convins:
                    kb.op("dve", lambda e: e.memset(cv_[:], 0.0), writes=[cv_])

                def cb_fn(ci, tb, p, w):
                    convin = convins[ci % 2]
                    evac_conv_in(convin, p, tb, nseq, L)
                    if tb != blocks[-1]:
                        return
                    if ci < 4:
                        conv_chunk(convin, acc, cwt, cbt, ci, nseq, L, lambda s_: xcT[:, s_ * L:(s_ + 1) * L], xcT)
                        transpose_to_tm(xcT, lambda t0, n: xtm[:, t0:t0 + n, ci * 128:(ci + 1) * 128], xtm, T)
                    elif ci == 4:
                        conv_chunk(convin, acc, cwt, cbt, ci, nseq, L, lambda s_: BT[:, s_ * L:(s_ + 1) * L], BT)
                        transpose_to_tm(BT, lambda t0, n: Btm[:, t0:t0 + n, :], Btm, T)
                    else:
                        conv_chunk(convin, acc, cwt, cbt, ci, nseq, L, None, CTz)
                proj_fm(l, 512, 768, blocks, cb_fn)
            kb.barrier()
            if SSD_PH < 2:
                return
            wb, wv = wload([(D["w_in"][l][:, 1280:1296], 16)], 8)
            for tt in range(T):
                p = ps()
                proj_tm(wb, wv, 0, 16, tt // 4, tt % 4, p)
                kb.op("dve", lambda e: e.tensor_tensor(out=dtt[:, tt, :], in0=p[:, 0:16], in1=dtb[:], op=ALU.add), reads=[p, dtb], writes=[dtt])
            kb.op("act", lambda e: e.activation(out=dtt[:], in_=dtt[:], func=AF.Exp), reads=[dtt], writes=[dtt])
            kb.op("act", lambda e: e.activation(out=dtt[:], in_=dtt[:], func=AF.Ln, bias=1.0), reads=[dtt], writes=[dtt])
            kb.op("dve", lambda e: e.tensor_tensor(out=dtA[:], in0=dtt[:], in1=bcast(abc[:], [128, T, 16], 1), op=ALU.mult), reads=[dtt, abc], writes=[dtA])
            for tt in range(T):
                p = ps()
                kb.op("pe", lambda e: e.matmul(p[:, 0:16], lhsT=triF_f, rhs=dtA[:, tt, :], start=True, stop=True), reads=[cf, dtA], writes=[p], inc=False)
                kb.op("pe", lambda e: e.matmul(p[:, 16:32], lhsT=triB_f, rhs=dtA[:, tt, :], start=True, stop=True), reads=[cf, dtA], writes=[p], inc=False)
                kb.op("pe", lambda e: e.matmul(p[:, 32:48], lhsT=ones_f, rhs=dtA[:, tt, :], start=True, stop=True), reads=[cf, dtA], writes=[p])
                kb.op("dve", lambda e: e.tensor_copy(out=cum[:, tt, 0:8], in_=p[:, 0:8]), reads=[p], writes=[cum])
                kb.op("dve", lambda e: e.tensor_copy(out=cum[:, tt, 8:16], in_=p[:, 24:32]), reads=[p], writes=[cum])
                kb.op("dve", lambda e: e.tensor_copy(out=tot[:, tt, :], in_=p[:, 32:48]), reads=[p], writes=[tot])
            kb.op("act", lambda e: e.activation(out=ecum[:], in_=cum[:], func=AF.Exp), reads=[cum], writes=[ecum])
            kb.op("dve", lambda e: e.tensor_tensor(out=wd[:], in0=tot[:], in1=cum[:], op=ALU.subtract), reads=[tot, cum], writes=[wd])
            kb.op("act", lambda e: e.activation(out=wd[:], in_=wd[:], func=AF.Exp), reads=[wd], writes=[wd])
            kb.op("dve", lambda e: e.tensor_tensor(out=wd[:], in0=wd[:], in1=dtt[:], op=ALU.mult), reads=[wd, dtt], writes=[wd])
            kb.op("act", lambda e: e.activation(out=bj[:], in_=dtt[:], func=AF.Ln), reads=[dtt], writes=[bj])
            kb.op("dve", lambda e: e.tensor_tensor(out=bj[:], in0=bj[:], in1=cum[:], op=ALU.subtract), reads=[bj, cum], writes=[bj])
            tot4 = tot[:].rearrange("p t (d h) -> p t d h", d=2)
            for gg in range(2):
                kb.op("act", lambda e: e.activation(out=dec[gg * 64:(gg + 1) * 64], in_=tot4[gg * 64:(gg + 1) * 64, :, :, gg * 4:(gg + 1) * 4], func=AF.Exp),
                      reads=[tot], writes=[dec])
            if SSD_PH < 3:
                kb.barrier()
                return
            with scope() as st2:
                Hprev = alloc(st2, "Hprev", [128, T, 2, 256], BF16)
                st3 = ExitStack()
                Hs = [alloc(st3, "Hs%d" % i, [128, 256], F32) for i in range(2)]
                xws = [alloc(st3, "xw%d" % i, [128, 512], BF16) for i in range(2)]
                hx = alloc(st3, "hx", [128, 2, 128], F32)
                ho = alloc(st3, "ho", [128, 128], F32)
                xi = 0
                for dr in range(2):
                    H = Hs[dr]
                    for s_ in range(nseq):
                        if sample:
                            for blk in range(2):
                                for two in range(2):
                                    kb.dma("sp", hx[two * 64:(two + 1) * 64, blk, :].rearrange("p (g n) -> p g n", g=2),
                                           D["sssm"][l, dr].rearrange("(g r) p n -> r p g n", g=2)[blk * 2 + two], writes=[hx])
                            for blk in range(2):
                                p = ps()
                                kb.op("pe", lambda e: e.matmul(p[:, 0:128], lhsT=hx[:, blk, :], rhs=ident_f, start=True, stop=True), reads=[hx, cf], writes=[p])
                                kb.op("act", lambda e: e.activation(out=H[:, blk * 128:(blk + 1) * 128], in_=p[:, 0:128], func=AF.Copy), reads=[p], writes=[H])
                        else:
                            kb.op("dve", lambda e: e.memset(H[:], 0.0), writes=[H])
                        order = range(lt) if dr == 0 else range(lt - 1, -1, -1)
                        for r0 in order:
                            tt = s_ * lt + r0
                            kb.op("act", lambda e: e.activation(out=Hprev[:, tt, dr, :], in_=H[:], func=AF.Copy), reads=[H], writes=[Hprev])
                            xw = xws[xi % 2]
                            xi += 1
                            kb.op("dve", lambda e: e.tensor_tensor(out=xw[:].rearrange("p (h d) -> p h d", d=64), in0=xtm[:, tt, :].rearrange("p (h d) -> p h d", d=64),
                                                                   in1=bcast(wd[:, tt, dr * 8:(dr + 1) * 8], [128, 8, 64], 2), op=ALU.mult), reads=[xtm, wd], writes=[xw])
                            p = ps()
                            kb.op("pe", lambda e: e.matmul(p[:, :], lhsT=Btm[:, tt, :], rhs=xw[:], start=True, stop=True), reads=[Btm, xw], writes=[p])
                            kb.op("dve", lambda e: e.tensor_tensor(out=H[:].rearrange("p (h d) -> p h d", d=64), in0=H[:].rearrange("p (h d) -> p h d", d=64),
                                                                   in1=bcast(dec[:, tt, dr, :], [128, 4, 64], 2), op=ALU.mult), reads=[H, dec], writes=[H])
                            for gg in range(2):
                                kb.op("dve", lambda e: e.tensor_tensor(out=H[gg * 64:(gg + 1) * 64, :], in0=H[gg * 64:(gg + 1) * 64, :],
                                                                       in1=p[gg * 64:(gg + 1) * 64, gg * 256:(gg + 1) * 256], op=ALU.add), reads=[H, p], writes=[H])
                        if not sample:
                            for blk in range(2):
                                p = ps()
                                kb.op("pe", lambda e: e.matmul(p[:, 0:128], lhsT=H[:, blk * 128:(blk + 1) * 128], rhs=ident_f, start=True, stop=True), reads=[H, cf], writes=[p])
                                kb.op("act", lambda e: e.activation(out=ho[:], in_=p[:, 0:128], func=AF.Copy), reads=[p], writes=[ho])
                                for two in range(2):
                                    kb.dma("sp", D["nssm"][s_, l, dr].rearrange("(g r) p n -> r p g n", g=2)[blk * 2 + two],
                                           ho[two * 64:(two + 1) * 64, :].rearrange("p (g n) -> p g n", g=2), reads=[ho])
                kb.barrier()
                st3.close()
                if SSD_PH < 4:
                    return
                Dg = alloc(st2, "Dg", [128, 16, 128], F32)
                Es = [alloc(st2, "E%d" % i, [128, 128], F32) for i in range(3)]
                Ms = [alloc(st2, "M%d" % i, [128, 128], BF16) for i in range(3)]
                ya = alloc(st2, "ya", [128, 512], F32)
                yu = alloc(st2, "yu", [128, 512], F32)
                zs = yu
                ytm = alloc(st2, "ytm", [128, 4, 512], BF16)
                yT = alloc(st2, "yT", [128, 4, 512], BF16)
                ss = alloc(st2, "ss", [128, 4], F32)
                wzb, wzv = wload([(D["w_in"][l][:, 0:512], 512)], 8)
                ei = 0
                for tt in range(T):
                    tk = slice(tt * 128, (tt + 1) * 128)
                    pz = ps_acc()
                    proj_tm(wzb, wzv, 0, 512, tt // 4, tt % 4, pz)
                    pBC = ps_acc()
                    for gg in range(2):
                        kb.op("pe", lambda e: e.matmul(pBC[:, gg * 128:(gg + 1) * 128], lhsT=BT[:, tk], rhs=CTz[gg][:, tk], start=True, stop=True),
                              reads=[BT, CTz[gg]], writes=[pBC], inc=(gg == 1))
                    kb.op("dve", lambda e: e.tensor_tensor(out=Dg[:], in0=bcast(ident_f, [128, 16, 128], 1), in1=bcast(cum[:, tt, :], [128, 16, 128], 2), op=ALU.mult),
                          reads=[cf, cum], writes=[Dg])
                    yint = ps_acc()
                    hd = [(h, dr) for h in range(8) for dr in range(2)]
                    mts = {}

                    def sA(i):
                        h, dr = hd[i]
                        pE = ps()
                        kb.op("pe", lambda e: e.matmul(pE[:, 0:128], lhsT=ones_f, rhs=Dg[:, dr * 8 + h, :], start=True, stop=False), reads=[cf, Dg], writes=[pE], inc=False)
                        kb.op("pe", lambda e: e.matmul(pE[:, 0:128], lhsT=ident_b, rhs=(maskF_b if dr == 0 else maskB_b), start=False, stop=True), reads=[cb], writes=[pE])
                        E = Es[i % 3]
                        M = Ms[i % 3]
                        kb.op("act", lambda e: e.activation(out=E[:], in_=pE[:, 0:128], func=AF.Exp, bias=bj[:, tt, dr * 8 + h:dr * 8 + h + 1]), reads=[pE, bj], writes=[E])
                        gg = h // 4
                        kb.op("dve", lambda e: e.tensor_tensor(out=M[:], in0=E[:], in1=pBC[:, gg * 128:(gg + 1) * 128], op=ALU.mult), reads=[E, pBC], writes=[M])
                        mts[i] = M

                    def sC(i):
                        h, dr = hd[i]
                        M = mts.pop(i)
                        kb.op("pe", lambda e: e.matmul(yint[:, h * 64:(h + 1) * 64], lhsT=M[:], rhs=xtm[:, tt, h * 64:(h + 1) * 64], start=(dr == 0), stop=(dr == 1)),
                              reads=[M, xtm], writes=[yint], inc=True)

                    for i in range(16 + 2):
                        if i < 16:
                            sA(i)
                        if i >= 2:
                            sC(i - 2)
                    if SSD_PH < 5:
                        continue
                    pY = [ps(), ps()]
                    for dr in range(2):
                        for gg in range(2):
                            kb.op("pe", lambda e: e.matmul(pY[dr][:, gg * 256:(gg + 1) * 256], lhsT=CTz[gg][:, tk], rhs=Hprev[:, tt, dr, :], start=True, stop=True),
                                  reads=[CTz[gg], Hprev], writes=[pY[dr]], inc=(gg == 1))
                    v3 = lambda ap: ap.rearrange("p (h d) -> p h d", d=64)
                    kb.op("dve", lambda e: e.tensor_tensor(out=v3(ya[:]), in0=v3(xtm[:, tt, :]), in1=bcast(dsk[:], [128, 8, 64], 2), op=ALU.mult), reads=[xtm, dsk], writes=[ya])
                    kb.op("dve", lambda e: e.tensor_tensor(out=ya[:], in0=ya[:], in1=yint[:, :], op=ALU.add), reads=[ya, yint], writes=[ya])
                    for dr in range(2):
                        kb.op("dve", lambda e: e.tensor_tensor(out=v3(yu[:]), in0=v3(pY[dr][:, :]), in1=bcast(ecum[:, tt, dr * 8:(dr + 1) * 8], [128, 8, 64], 2), op=ALU.mult),
                              reads=[pY[dr], ecum], writes=[yu])
                        kb.op("dve", lambda e: e.tensor_tensor(out=ya[:], in0=ya[:], in1=yu[:], op=ALU.add), reads=[ya, yu], writes=[ya])
                    if SSD_PH < 6:
                        continue
                    kb.op("act", lambda e: e.activation(out=zs[:], in_=pz[:, :], func=AF.Silu), reads=[pz], writes=[zs])
                    kb.op("dve", lambda e: e.tensor_tensor(out=ya[:], in0=ya[:], in1=zs[:], op=ALU.mult), reads=[ya, zs], writes=[ya])
                    kb.op("dve", lambda e: e.memset(ss[:, 0:1], 0.0), writes=[ss])
                    kb.op("act", lambda e: e.activation(out=yu[:], in_=ya[:], func=AF.Square, accum_out=ss[:, 0:1]), reads=[ya, ss], writes=[yu, ss])
                    kb.op("dve", lambda e: e.tensor_scalar(out=ss[:, 1:2], in0=ss[:, 0:1], scalar1=1.0 / 512, scalar2=EPS, op0=ALU.mult, op1=ALU.add), reads=[ss], writes=[ss])
                    kb.op("act", lambda e: e.activation(out=ss[:, 1:2], in_=ss[:, 1:2], func=AF.Ln), reads=[ss], writes=[ss])
                    kb.op("act", lambda e: e.activation(out=ss[:, 2:3], in_=ss[:, 1:2], func=AF.Exp, scale=-0.5), reads=[ss], writes=[ss])
                    kb.op("dve", lambda e: e.scalar_tensor_tensor(out=ytm[:, tt % 4, :], in0=ya[:], scalar=ss[:, 2:3], in1=ng[:], op0=ALU.mult, op1=ALU.mult),
                          reads=[ya, ss, ng], writes=[ytm])
                    if SSD_PH < 7:
                        continue
                    if tt % 4 == 3:
                        mixer_out(st2, ytm, tt // 4, yT)
        kb.barrier()


    def mlstm(l, g, nblk):
        kb.label = 'mlstm_g%d' % g
        sample = (g == 1)
        L = 2048 if sample else 256
        nseq = 1 if sample else 2
        lt = L // 128
        ntok = nblk * 512
        T = ntok // 128
        blocks = list(range(nblk))
        C0 = 2832
        lns = math.log(128 ** -0.5)
        for hg in range(2):
            h0 = hg * 2
            with scope() as st:
                qT = alloc(st, "mqT", [128, 2, ntok], BF16)
                kT = alloc(st, "mkT", [128, 2, ntok], BF16)
                vaug = alloc(st, "mvaug", [128, T, 2, 129], BF16)
                Cpb = alloc(st, "Cpb", [128, T, 2, 129], BF16)
                li = alloc(st, "li", [128, T, 4], F32)
                lf = alloc(st, "lf", [128, T, 4], F32)
                G = alloc(st, "G", [128, T, 4], F32)
                tot = alloc(st, "mtot", [128, T, 4], F32)
                pj = alloc(st, "pj", [128, T, 4], F32)
                pjs = alloc(st, "pjs", [128, T, 4], F32)
                gend = alloc(st, "gend", [128, T, 4], F32)
                mlb = alloc(st, "mlb", [128, T, 4], F32)
                wend = alloc(st, "wend", [128, T, 4], F32)
                mprev = alloc(st, "mprev", [128, T, 4], F32)
                gb = alloc(st, "gb", [128, 8], F32)
                cwt = alloc(st, "mcwt", [128, 4, 5], F32)
                cbt = alloc(st, "mcbt", [128, 4], F32)
                ng = alloc(st, "mng", [128, 256], F32)
                kb.dma("pool", wo_buf[:, 0:2, :], D["w_out"][l][1024 + h0 * 128:1024 + (h0 + 2) * 128, :].rearrange("(k p) n -> p k n", p=128), writes=[wo_buf])
                kb.dma("sp", ng[:], D["mlstm_norm"][l][h0 * 128:(h0 + 2) * 128].partition_broadcast(128), writes=[ng])
                goffs = [0 * 8 + 0 * 4 + h0, 1 * 8 + 0 * 4 + h0, 0 * 8 + 1 * 4 + h0, 1 * 8 + 1 * 4 + h0]
                for i_, go in enumerate(goffs):
                    kb.dma("sp", gb[:, i_ * 2:(i_ + 1) * 2], D["mlstm_gate_b"][l][go:go + 2].partition_broadcast(128), writes=[gb])
                for ci, ch0 in enumerate((h0 * 128, (h0 + 1) * 128, 512 + h0 * 128, 512 + (h0 + 1) * 128)):
                    for tap in range(5):
                        kb.dma("sp", cwt[:, ci, tap:tap + 1], D["conv_mlstm_w"][l, tap][ch0:ch0 + 128].rearrange("(p o) -> p o", o=1), writes=[cwt])
                    kb.dma("sp", cbt[:, ci:ci + 1], D["conv_mlstm_b"][l][ch0:ch0 + 128].rearrange("(p o) -> p o", o=1), writes=[cbt])
                kb.op("dve", lambda e: e.memset(vaug[:, :, :, 128:129], 1.0), writes=[vaug])
                with scope() as st1:
                    convins = [alloc(st1, "mconvin", [128, nseq, L + 4], F32) for _ in range(2)]
                    acc = alloc(st1, "mcacc", [128, ntok], F32)
                    for cv_ in convins:
                        kb.op("dve", lambda e: e.memset(cv_[:], 0.0), writes=[cv_])
                    wb, wv = wload([(D["w_in"][l][:, C0 + h0 * 128:C0 + (h0 + 2) * 128], 256),
                                    (D["w_in"][l][:, C0 + 512 + h0 * 128:C0 + 512 + (h0 + 2) * 128], 256)], 8)
                    for ci in range(4):
                        for tb in blocks:
                            p = ps()
                            for k in range(8):
                                kb.op("pe", lambda e: e.matmul(p[:, :], lhsT=wv[:, k, ci * 128:(ci + 1) * 128], rhs=hT[tb][:, k, :], start=(k == 0), stop=(k == 7)),
                                      reads=[wb, hT[tb]], writes=[p], inc=(k == 7))
                            evac_conv_in(convins[ci % 2], p, tb, nseq, L)
                        convin = convins[ci % 2]
                        dstb = qT if ci < 2 else kT
                        conv_chunk(convin, acc, cwt, cbt, ci, nseq, L, lambda s_: dstb[:, ci % 2, s_ * L:(s_ + 1) * L], dstb)
                kb.barrier()
                wb, wv = wload([(D["w_in"][l][:, C0 + 1024 + h0 * 128:C0 + 1024 + (h0 + 2) * 128], 256)], 8)
                for tt in range(T):
                    p = ps()
                    proj_tm(wb, wv, 0, 256, tt // 4, tt % 4, p)
                    kb.op("act", lambda e: e.activation(out=vaug[:, tt, :, 0:128], in_=p[:, 0:256].rearrange("p (h e) -> p h e", e=128), func=AF.Copy), reads=[p], writes=[vaug])
                gc = C0 + 2048
                wb, wv = wload([(D["w_in"][l][:, gc + go:gc + go + 2], 2) for go in goffs], 8)
                for tt in range(T):
                    p = ps()
                    proj_tm(wb, wv, 0, 8, tt // 4, tt % 4, p)
                    kb.op("dve", lambda e: e.tensor_tensor(out=li[:, tt, :], in0=p[:, 0:4], in1=gb[:, 0:4], op=ALU.add), reads=[p, gb], writes=[li])
                    kb.op("dve", lambda e: e.tensor_tensor(out=lf[:, tt, :], in0=p[:, 4:8], in1=gb[:, 4:8], op=ALU.add), reads=[p, gb], writes=[lf])
                kb.op("act", lambda e: e.activation(out=lf[:], in_=lf[:], func=AF.Exp, scale=-1.0), reads=[lf], writes=[lf])
                kb.op("act", lambda e: e.activation(out=lf[:], in_=lf[:], func=AF.Ln, bias=1.0), reads=[lf], writes=[lf])
                kb.op("dve", lambda e: e.tensor_scalar(out=lf[:], in0=lf[:], scalar1=-1.0, scalar2=None, op0=ALU.mult), reads=[lf], writes=[lf])
                for tt in range(T):
                    p = ps()
                    kb.op("pe", lambda e: e.matmul(p[:, 0:4], lhsT=triF_f, rhs=lf[:, tt, :], start=True, stop=True), reads=[cf, lf], writes=[p], inc=False)
                    kb.op("pe", lambda e: e.matmul(p[:, 4:8], lhsT=triB_f, rhs=lf[:, tt, :], start=True, stop=True), reads=[cf, lf], writes=[p], inc=False)
                    kb.op("pe", lambda e: e.matmul(p[:, 8:12], lhsT=ones_f, rhs=lf[:, tt, :], start=True, stop=True), reads=[cf, lf], writes=[p])
                    kb.op("dve", lambda e: e.tensor_copy(out=G[:, tt, 0:2], in_=p[:, 0:2]), reads=[p], writes=[G])
                    kb.op("dve", lambda e: e.tensor_copy(out=G[:, tt, 2:4], in_=p[:, 6:8]), reads=[p], writes=[G])
                    kb.op("dve", lambda e: e.tensor_copy(out=tot[:, tt, :], in_=p[:, 8:12]), reads=[p], writes=[tot])
                kb.op("dve", lambda e: e.tensor_tensor(out=pj[:], in0=li[:], in1=G[:], op=ALU.subtract), reads=[li, G], writes=[pj])
                kb.op("dve", lambda e: e.tensor_scalar(out=pjs[:], in0=pj[:], scalar1=lns, scalar2=None, op0=ALU.add), reads=[pj], writes=[pjs])
                kb.op("dve", lambda e: e.tensor_tensor(out=gend[:], in0=pj[:], in1=tot[:], op=ALU.add), reads=[pj, tot], writes=[gend])
                with scope() as stt:
                    mrow = alloc(stt, "mrow8", [4, 1], F32)
                    d8 = alloc(stt, "d8", [4, 4], F32)
                    for tt in range(T):
                        p = ps()
                        kb.op("pe", lambda e: e.matmul(p[0:4, 0:128], lhsT=gend[:, tt, :], rhs=ident_f, start=True, stop=True), reads=[gend, cf], writes=[p])
                        kb.op("dve", lambda e: e.tensor_reduce(out=mrow[:], in_=p[0:4, 0:128], axis=AX.X, op=ALU.max), reads=[p], writes=[mrow])
                        kb.op("dve", lambda e: e.tensor_scalar(out=d8[:], in0=ident_f[0:4, 0:4], scalar1=mrow[:, 0:1], scalar2=None, op0=ALU.mult), reads=[cf, mrow], writes=[d8])
                        p2 = ps()
                        kb.op("pe", lambda e: e.matmul(p2[:, 0:4], lhsT=ones_f[0:4, :], rhs=d8[:], start=True, stop=True), reads=[cf, d8], writes=[p2])
                        kb.op("dve", lambda e: e.tensor_copy(out=mlb[:, tt, :], in_=p2[:, 0:4]), reads=[p2], writes=[mlb])
                    kb.barrier()
                kb.op("dve", lambda e: e.tensor_tensor(out=wend[:], in0=gend[:], in1=mlb[:], op=ALU.subtract), reads=[gend, mlb], writes=[wend])
                kb.op("act", lambda e: e.activation(out=wend[:], in_=wend[:], func=AF.Exp), reads=[wend], writes=[wend])
                with scope() as st2:
                    Cst = [alloc(st2, "Cst%d" % i, [128, 129], F32) for i in range(4)]
                    mp = alloc(st2, "mp", [128, 4], F32)
                    mt8 = alloc(st2, "mt8", [128, 16], F32)
                    kwts = [alloc(st2, "kwt", [128, 2, 128], BF16) for _ in range(2)]
                    Dgs = [alloc(st2, "mDg", [128, 4, 128], F32) for _ in range(2)]
                    Drs = [alloc(st2, "mDr", [128, 4, 128], F32) for _ in range(2)]
                    scs = [alloc(st2, "msc", [128, 40], F32) for _ in range(2)]
                    kwt, Dg, Dr, sc = kwts[0], Dgs[0], Drs[0], scs[0]
                    Es = [alloc(st2, "mE%d" % i, [128, 128], F32) for i in range(3)]
                    Ms = [alloc(st2, "mM%d" % i, [128, 128], BF16) for i in range(3)]
                    nds = [alloc(st2, "nd%d" % i, [128, 129], F32) for i in range(2)]
                    cbf = [alloc(st2, "cbf%d" % i, [128, 129], BF16) for i in range(2)]
                    hsums = [alloc(st2, "hsum", [128, 256], F32) for _ in range(2)]
                    hts = [alloc(st2, "mht", [128, 256], F32) for _ in range(2)]
                    sgs = [alloc(st2, "msg", [128, 256], F32) for _ in range(2)]
                    hsum, ht, sg = hsums[0], hts[0], sgs[0]
                    ytm = alloc(st2, "mytm", [128, 4, 256], BF16)
                    yT = alloc(st2, "myT", [128, 2, 512], BF16)
                    ei = 0
                    ei0 = [0]

                    def init_state(dr, s_):
                        for hh in range(2):
                            C = Cst[dr * 2 + hh]
                            if sample:
                                kb.dma("sp", C[:, 0:128], D["smc"][l, dr, h0 + hh], writes=[C])
                                kb.dma("sp", C[:, 128:129], D["smn"][l, dr, h0 + hh].rearrange("(p o) -> p o", o=1), writes=[C])
                            else:
                                kb.op("dve", lambda e: e.memset(C[:], 0.0), writes=[C])
                        if sample:
                            kb.dma("sp", mp[:, dr * 2:dr * 2 + 2], D["smm"][l][dr * 4 + h0:dr * 4 + h0 + 2].partition_broadcast(128), writes=[mp])
                        else:
                            kb.op("dve", lambda e: e.memset(mp[:, dr * 2:dr * 2 + 2], 0.0), writes=[mp])

                    def local_update(dr, tt):
                        cs = slice(dr * 2, dr * 2 + 2)
                        pk = ps()
                        for hh in range(2):
                            kb.op("pe", lambda e: e.matmul(pk[:, hh * 128:(hh + 1) * 128], lhsT=kT[:, hh, tt * 128:(tt + 1) * 128], rhs=ident_b, start=True, stop=True),
                                  reads=[kT, cb], writes=[pk], inc=(hh == 1))
                        kb.op("dve", lambda e: e.tensor_tensor(out=kwt[:], in0=pk[:, 0:256].rearrange("p (h d) -> p h d", d=128),
                                                               in1=bcast(wend[:, tt, cs], [128, 2, 128], 2), op=ALU.mult), reads=[pk, wend], writes=[kwt])
                        a = sc[:, 0:2]
                        mn = sc[:, 2:4]
                        sp_ = sc[:, 4:6]
                        sl_ = sc[:, 6:8]
                        kb.op("dve", lambda e: e.tensor_tensor(out=a, in0=tot[:, tt, cs], in1=mp[:, cs], op=ALU.add), reads=[tot, mp], writes=[sc])
                        kb.op("dve", lambda e: e.tensor_tensor(out=mn, in0=a, in1=mlb[:, tt, cs], op=ALU.max), reads=[sc, mlb], writes=[sc])
                        kb.op("dve", lambda e: e.tensor_tensor(out=sp_, in0=a, in1=mn, op=ALU.subtract), reads=[sc], writes=[sc])
                        kb.op("dve", lambda e: e.tensor_tensor(out=sl_, in0=mlb[:, tt, cs], in1=mn, op=ALU.subtract), reads=[sc, mlb], writes=[sc])
                        kb.op("act", lambda e: e.activation(out=sc[:, 4:8], in_=sc[:, 4:8], func=AF.Exp), reads=[sc], writes=[sc])
                        kb.op("dve", lambda e: e.tensor_copy(out=mp[:, cs], in_=mn), reads=[sc], writes=[mp])
                        for hh in range(2):
                            C = Cst[dr * 2 + hh]
                            pc = ps()
                            kb.op("pe", lambda e: e.matmul(pc[:, 0:129], lhsT=kwt[:, hh, :], rhs=vaug[:, tt, hh, :], start=True, stop=True), reads=[kwt, vaug], writes=[pc])
                            kb.op("dve", lambda e: e.tensor_scalar(out=C[:], in0=C[:], scalar1=sc[:, 4 + hh:5 + hh], scalar2=None, op0=ALU.mult), reads=[C, sc], writes=[C])
                            kb.op("dve", lambda e: e.scalar_tensor_tensor(out=C[:], in0=pc[:, 0:129], scalar=sc[:, 6 + hh:7 + hh], in1=C[:], op0=ALU.mult, op1=ALU.add),
                                  reads=[pc, sc, C], writes=[C])

                    def final_state(dr, s_):
                        for hh in range(2):
                            C = Cst[dr * 2 + hh]
                            kb.dma("sp", D["nmc"][s_, l, dr, h0 + hh], C[:, 0:128], reads=[C])
                            kb.dma("sp", D["nmn"][s_, l, dr, h0 + hh].rearrange("(p o) -> p o", o=1), C[:, 128:129], reads=[C])
                        kb.dma("sp", D["nmm"][s_, l:l + 1, dr * 4 + h0:dr * 4 + h0 + 2], mp[0:1, dr * 2:dr * 2 + 2], reads=[mp])

                    for s_ in range(nseq):
                        init_state(1, s_)
                        for r0 in range(lt - 1, -1, -1):
                            tt = s_ * lt + r0
                            kwt, sc = kwts[tt % 2], scs[tt % 2]
                            for hh in range(2):
                                kb.op("act", lambda e: e.activation(out=Cpb[:, tt, hh, :], in_=Cst[2 + hh][:], func=AF.Copy), reads=[Cst[2 + hh]], writes=[Cpb])
                            kb.op("dve", lambda e: e.tensor_copy(out=mprev[:, tt, 2:4], in_=mp[:, 2:4]), reads=[mp], writes=[mprev])
                            local_update(1, tt)
                        if not sample:
                            final_state(1, s_)
                    wob, wov = wload([(D["w_in"][l][:, C0 + 1536 + h0 * 128:C0 + 1536 + (h0 + 2) * 128], 256)], 8)
                    for s_ in range(nseq):
                        init_state(0, s_)
                        for r0 in range(lt):
                            tt = s_ * lt + r0
                            tk = slice(tt * 128, (tt + 1) * 128)
                            kwt, Dg, Dr, sc = kwts[tt % 2], Dgs[tt % 2], Drs[tt % 2], scs[tt % 2]
                            hsum, ht, sg = hsums[tt % 2], hts[tt % 2], sgs[tt % 2]
                            kb.op("dve", lambda e: e.tensor_copy(out=mprev[:, tt, 0:2], in_=mp[:, 0:2]), reads=[mp], writes=[mprev])
                            kb.op("dve", lambda e: e.tensor_tensor(out=sc[:, 8:12], in0=G[:, tt, :], in1=mprev[:, tt, :], op=ALU.add), reads=[G, mprev], writes=[sc])
                            kb.op("dve", lambda e: e.tensor_tensor(out=Dg[:], in0=bcast(ident_f, [128, 4, 128], 1), in1=bcast(pj[:, tt, :], [128, 4, 128], 2), op=ALU.mult),
                                  reads=[cf, pj], writes=[Dg])
                            po = ps_acc()
                            proj_tm(wob, wov, 0, 256, tt // 4, tt % 4, po)
                            pSs = []
                            for hh in range(2):
                                pS = ps_acc()
                                kb.op("pe", lambda e: e.matmul(pS[:, 0:128], lhsT=kT[:, hh, tk], rhs=qT[:, hh, tk], start=True, stop=True), reads=[kT, qT], writes=[pS])
                                pSs.append(pS)
                            pms = []
                            for c in range(4):
                                dr = c // 2
                                pm = ps()
                                kb.op("pe", lambda e: e.matmul(pm[:, 0:128], lhsT=ones_f, rhs=Dg[:, c, :], start=True, stop=False), reads=[cf, Dg], writes=[pm], inc=False)
                                kb.op("pe", lambda e: e.matmul(pm[:, 0:128], lhsT=ident_b, rhs=(maskB_b if dr == 0 else maskF_b), start=False, stop=True), reads=[cb], writes=[pm])
                                pms.append(pm)
                            for c in range(4):
                                kb.op("dve", lambda e: e.tensor_reduce(out=sc[:, 12 + c:13 + c], in_=pms[c][:, 0:128], axis=AX.X, op=ALU.max), reads=[pms[c]], writes=[sc])
                            kb.op("dve", lambda e: e.tensor_tensor(out=sc[:, 12:16], in0=sc[:, 12:16], in1=G[:, tt, :], op=ALU.add), reads=[sc, G], writes=[sc])
                            kb.op("dve", lambda e: e.tensor_tensor(out=sc[:, 16:20], in0=sc[:, 12:16], in1=sc[:, 8:12], op=ALU.max), reads=[sc], writes=[sc])
                            kb.op("dve", lambda e: e.tensor_tensor(out=sc[:, 20:24], in0=G[:, tt, :], in1=sc[:, 16:20], op=ALU.subtract), reads=[sc, G], writes=[sc])
                            kb.op("dve", lambda e: e.tensor_tensor(out=sc[:, 24:28], in0=sc[:, 8:12], in1=sc[:, 16:20], op=ALU.subtract), reads=[sc], writes=[sc])
                            kb.op("act", lambda e: e.activation(out=sc[:, 24:28], in_=sc[:, 24:28], func=AF.Exp, bias=lns_col[:, 0:1]), reads=[sc, cf], writes=[sc])
                            kb.op("act", lambda e: e.activation(out=sc[:, 28:32], in_=sc[:, 16:20], func=AF.Exp, scale=-1.0), reads=[sc], writes=[sc])
                            kb.op("dve", lambda e: e.tensor_tensor(out=Dr[:], in0=bcast(ident_f, [128, 4, 128], 1), in1=bcast(sc[:, 20:24], [128, 4, 128], 2), op=ALU.mult),
                                  reads=[cf, sc], writes=[Dr])
                            items = [(0, 0), (0, 1), (1, 0), (1, 1)]
                            mts = {}

                            def mA(i):
                                hh, dr = items[i]
                                c = dr * 2 + hh
                                pW = ps()
                                kb.op("pe", lambda e: e.matmul(pW[:, 0:128], lhsT=ones_f, rhs=Dr[:, c, :], start=True, stop=False), reads=[cf, Dr], writes=[pW], inc=False)
                                kb.op("pe", lambda e: e.matmul(pW[:, 0:128], lhsT=ident_b, rhs=(maskF_b if dr == 0 else maskB_b), start=False, stop=True), reads=[cb], writes=[pW])
                                E = Es[(ei0[0] + i) % 3]
                                M = Ms[(ei0[0] + i) % 3]
                                kb.op("act", lambda e: e.activation(out=E[:], in_=pW[:, 0:128], func=AF.Exp, bias=pjs[:, tt, c:c + 1]), reads=[pW, pjs], writes=[E])
                                kb.op("dve", lambda e: e.tensor_tensor(out=M[:], in0=E[:], in1=pSs[hh][:, 0:128], op=ALU.mult), reads=[E, pSs[hh]], writes=[M])
                                if dr == 0:
                                    kb.op("act", lambda e: e.activation(out=cbf[hh][:], in_=Cst[hh][:], func=AF.Copy), reads=[Cst[hh]], writes=[cbf[hh]])
                                mts[i] = M

                            def mC(i):
                                hh, dr = items[i]
                                c = dr * 2 + hh
                                M = mts.pop(i)
                                nd = nds[i % 2]
                                pN = ps()
                                kb.op("pe", lambda e: e.matmul(pN[:, 0:129], lhsT=M[:], rhs=vaug[:, tt, hh, :], start=True, stop=True), reads=[M, vaug], writes=[pN])
                                pI = ps()
                                if dr == 0:
                                    kb.op("pe", lambda e: e.matmul(pI[:, 0:129], lhsT=qT[:, hh, tk], rhs=cbf[hh][:], start=True, stop=True), reads=[qT, cbf[hh]], writes=[pI])
                                else:
                                    kb.op("pe", lambda e: e.matmul(pI[:, 0:129], lhsT=qT[:, hh, tk], rhs=Cpb[:, tt, hh, :], start=True, stop=True), reads=[qT, Cpb], writes=[pI])
                                kb.op("act", lambda e: e.activation(out=nd[:], in_=pN[:, 0:129], func=AF.Copy), reads=[pN], writes=[nd])
                                kb.op("dve", lambda e: e.scalar_tensor_tensor(out=nd[:], in0=pI[:, 0:129], scalar=sc[:, 24 + c:25 + c], in1=nd[:], op0=ALU.mult, op1=ALU.add),
                                      reads=[pI, sc, nd], writes=[nd])
                                kb.op("dve", lambda e: e.tensor_scalar(out=sc[:, 32:33], in0=nd[:, 128:129], scalar1=-1.0, scalar2=None, op0=ALU.mult), reads=[nd], writes=[sc])
                                kb.op("dve", lambda e: e.tensor_tensor(out=sc[:, 32:33], in0=sc[:, 32:33], in1=nd[:, 128:129], op=ALU.max), reads=[nd, sc], writes=[sc])
                                kb.op("dve", lambda e: e.tensor_tensor(out=sc[:, 32:33], in0=sc[:, 32:33], in1=sc[:, 28 + c:29 + c], op=ALU.max), reads=[sc], writes=[sc])
                                kb.op("dve", lambda e: e.reciprocal(out=sc[:, 33:34], in_=sc[:, 32:33]), reads=[sc], writes=[sc])
                                if dr == 0:
                                    kb.op("dve", lambda e: e.tensor_scalar(out=hsum[:, hh * 128:(hh + 1) * 128], in0=nd[:, 0:128], scalar1=sc[:, 33:34], scalar2=None, op0=ALU.mult),
                                          reads=[nd, sc], writes=[hsum])
                                else:
                                    kb.op("dve", lambda e: e.scalar_tensor_tensor(out=hsum[:, hh * 128:(hh + 1) * 128], in0=nd[:, 0:128], scalar=sc[:, 33:34],
                                                                                  in1=hsum[:, hh * 128:(hh + 1) * 128], op0=ALU.mult, op1=ALU.add), reads=[nd, sc, hsum], writes=[hsum])

                            for i in range(4 + 2):
                                if i < 4:
                                    mA(i)
                                if i >= 2:
                                    mC(i - 2)
                            ei0[0] += 4
                            local_update(0, tt)
                            h3 = hsum[:].rearrange("p (h d) -> p h d", d=128)
                            t3 = ht[:].rearrange("p (h d) -> p h d", d=128)
                            kb.op("dve", lambda e: e.tensor_tensor(out=ht[:], in0=hsum[:], in1=hsum[:], op=ALU.mult), reads=[hsum], writes=[ht])
                            kb.op("dve", lambda e: e.tensor_reduce(out=sc[:, 34:36], in_=t3, axis=AX.X, op=ALU.add), reads=[ht], writes=[sc])
                            kb.op("dve", lambda e: e.tensor_scalar(out=sc[:, 34:36], in0=sc[:, 34:36], scalar1=1.0 / 128, scalar2=EPS, op0=ALU.mult, op1=ALU.add), reads=[sc], writes=[sc])
                            kb.op("act", lambda e: e.activation(out=sc[:, 34:36], in_=sc[:, 34:36], func=AF.Ln), reads=[sc], writes=[sc])
                            kb.op("act", lambda e: e.activation(out=sc[:, 36:38], in_=sc[:, 34:36], func=AF.Exp, scale=-0.5), reads=[sc], writes=[sc])
                            kb.op("dve", lambda e: e.tensor_tensor(out=t3, in0=h3, in1=bcast(sc[:, 36:38], [128, 2, 128], 2), op=ALU.mult), reads=[hsum, sc], writes=[ht])
                            kb.op("dve", lambda e: e.tensor_tensor(out=ht[:], in0=ht[:], in1=ng[:], op=ALU.mult), reads=[ht, ng], writes=[ht])
                            kb.op("act", lambda e: e.activation(out=sg[:], in_=po[:, 0:256], func=AF.Sigmoid), reads=[po], writes=[sg])
                            kb.op("dve", lambda e: e.tensor_tensor(out=ytm[:, tt % 4, :], in0=ht[:], in1=sg[:], op=ALU.mult), reads=[ht, sg], writes=[ytm])
                            if tt % 4 == 3:
                                tb = tt // 4
                                for c2 in range(2):
                                    p = ps()
                                    for tl in range(4):
                                        kb.op("pe", lambda e: e.matmul(p[:, tl * 128:(tl + 1) * 128], lhsT=ytm[:, tl, c2 * 128:(c2 + 1) * 128], rhs=ident_b, start=True, stop=True),
                                              reads=[ytm, cb], writes=[p], inc=(tl == 3))
                                    kb.op("act", lambda e: e.activation(out=yT[:, c2, :], in_=p[:, :], func=AF.Copy), reads=[p], writes=[yT])
                                for c in range(8):
                                    p = ps()
                                    for k in range(2):
                                        kb.op("pe", lambda e: e.matmul(p[:, :], lhsT=wo_buf[:, k, c * 128:(c + 1) * 128], rhs=yT[:, k, :], start=(k == 0), stop=(k == 1)),
                                              reads=[wo_buf, yT], writes=[p], inc=(k == 1))
                                    kb.op("dve", lambda e: e.scalar_tensor_tensor(out=xT[tb][:, c, :], in0=p[:, :], scalar=modc[:, 2, c:c + 1], in1=xT[tb][:, c, :],
                                                                                  op0=ALU.mult, op1=ALU.add), reads=[p, modc, xT[tb]], writes=[xT[tb]])
                        if not sample:
                            final_state(0, s_)
            kb.barrier()

    def run_pass(g, src, dst, nblk):
        load_x(src, nblk)
        kb.barrier()
        for l in range(NL):
            kb.label = 'norm_g%d' % g
            load_mod(l, g)
            with scope() as st:
                sq = [alloc(st, "sq", [128, 8, 512], BF16) for _ in range(2)]
                rstd = [alloc(st, "rstd", [128, 512], F32) for _ in range(2)]
                tmp2 = [alloc(st, "tmpn%d" % i, [128, 512], F32) for i in range(4)]
                for tb in range(nblk):
                    norm_block((sq, rstd, tmp2), tb, AB.t[:, 0, :], AB.t[:, 1, :], hT[tb])
            kb.barrier()
            for mk in MIXERS:
                if mk in "bd":
                    attention(l, g, mk, nblk)
                elif mk == "a":
                    ssd(l, g, nblk)
                elif mk == "c":
                    mlstm(l, g, nblk)
            with scope() as st:
                sq = [alloc(st, "sq", [128, 8, 512], BF16) for _ in range(2)]
                rstd = [alloc(st, "rstd", [128, 512], F32) for _ in range(2)]
                tmp2 = [alloc(st, "tmpn%d" % i, [128, 512], F32) for i in range(4)]
                for tb in range(nblk):
                    norm_block((sq, rstd, tmp2), tb, AB.t[:, 2, :], AB.t[:, 3, :], hT[tb])
            kb.barrier()
            for b0 in range(0, nblk, 2):
                ffn(l, list(range(b0, min(b0 + 2, nblk))))
        final_out(nblk, dst)

    run_pass(0, D["xp"], D["yp"], 1)
    run_pass(1, D["xs"], D["ys"], 4)

    kb.barrier(include_pool_dma=True)
    top.close()
    print("instructions:", kb.ninst, flush=True)
    if kb.stats is not None:
        tot = 0.0
        for lab, (mk, busy, nu, nfl, bub) in sorted(kb.stats.items(), key=lambda kv: -kv[1][0]):
            tot += mk
            print("  %-12s est_us=%8.0f units=%6d regions=%4d bubble_us=%6.0f busy: %s" % (lab, mk / 1e3, nu, nfl, bub / 1e3, " ".join("%s=%.0f" % (e_, v_ / 1e3) for e_, v_ in sorted(busy.items()))))
        print("  est total us", tot / 1e3)
    return nc


_CACHE = {}


def prep_inputs(inp):
    f = lambda a: np.ascontiguousarray(np.asarray(a, dtype=np.float32))
    consts = make_consts()
    rope = make_rope()
    shared = {}
    for name in ("w_ada", "b_ada", "norm1", "norm2", "w_in", "w_out", "conv_ssd_w", "conv_ssd_b", "ssd_d", "ssd_norm",
                 "diff_lq1", "diff_lk1", "diff_lq2", "diff_lk2", "conv_mlstm_w", "conv_mlstm_b", "mlstm_norm",
                 "gqa_q_norm", "gqa_k_norm", "w_ffn_in", "w_ffn_out", "norm_f"):
        shared[name] = f(inp[name])
    shared["ssd_a_log"] = f(inp["ssd_a_log"]).reshape(4, 16)
    shared["ssd_dt_bias"] = f(inp["ssd_dt_bias"]).reshape(4, 16)
    shared["mlstm_gate_b"] = f(inp["mlstm_gate_b"]).reshape(4, 16)
    shared["consts"] = consts
    shared["rope"] = rope
    xp = f(inp["x_prompt"])
    xs = f(inp["x_sample"])
    in_maps = []
    for c in range(8):
        b = c // 4
        m = dict(shared)
        m["xp"] = xp[2 * c:2 * c + 2].reshape(512, 1024)
        m["xs"] = xs[b]
        m["cvec"] = np.stack([f(inp["c_ctx"]), f(inp["c"])[b]], axis=0)
        m["cdk"] = f(inp["cache_diff_k"])[b].reshape(4, 256, 512)
        m["cdv"] = f(inp["cache_diff_v"])[b].reshape(4, 256, 512)
        m["cgk"] = f(inp["cache_gqa_k"])[b].reshape(4, 256, 128)
        m["cgv"] = f(inp["cache_gqa_v"])[b].reshape(4, 256, 128)
        m["sssm"] = f(inp["state_ssm"])[b]
        m["smc"] = f(inp["state_mlstm_c"])[b]
        m["smn"] = f(inp["state_mlstm_n"])[b]
        m["smm"] = f(inp["state_mlstm_m"])[b].reshape(4, 8)
        in_maps.append(m)
    if NL < 4:
        spec = dict(IN_SPECS)
        for m in in_maps:
            for k_ in list(m.keys()):
                if spec[k_][0] == 4 and len(spec[k_]) > 1:
                    m[k_] = np.ascontiguousarray(m[k_][:NL])
    return in_maps


def kernel(**inp):
    if "nc" not in _CACHE:
        _CACHE["nc"] = build_program()
    nc = _CACHE["nc"]
    in_maps = prep_inputs(inp)
    res = run_bass_kernel_spmd(nc, in_maps, core_ids=list(range(8)))
    return assemble(res.results)


def assemble(R):
    y_prompt = np.concatenate([R[c]["yp"].reshape(2, 256, 1024) for c in range(8)], axis=0)
    y_sample = np.stack([R[0]["ys"], R[4]["ys"]], axis=0)
    cat = lambda k: np.concatenate([R[c][k] for c in range(8)], axis=0)
    ndk = cat("ndk").reshape(16, 4, 256, 4, 2, 64)
    ndv = cat("ndv").reshape(16, 4, 256, 4, 128)
    ngk = cat("ngk").reshape(16, 4, 256, 2, 64)
    ngv = cat("ngv").reshape(16, 4, 256, 2, 64)
    nssm = cat("nssm")
    nmc = cat("nmc")
    nmn = cat("nmn")
    nmm = cat("nmm").reshape(16, 4, 2, 4)
    return (y_prompt, y_sample, ndk, ndv, ngk, ngv, nssm, nmc, nmn, nmm)
```

```python
import os
import math
from contextlib import ExitStack
import numpy as np
import concourse.bass as bass
import concourse.mybir as mybir
from concourse.bass_utils import run_bass_kernel_spmd

F32 = mybir.dt.float32
BF16 = mybir.dt.bfloat16
ALU = mybir.AluOpType
AF = mybir.ActivationFunctionType
AX = mybir.AxisListType

D_MODEL = 1024
DEPTH = 4
IN_COLS = 5664
D_FF = 2816
EPS = 1e-6
NEG = -30000.0

NL = int(os.environ.get("MK_NL", "4"))
MIXERS = os.environ.get("MK_MIX", "abcd")
SSD_PH = int(os.environ.get("MK_SSD_PH", "9"))
SSD_SUB = int(os.environ.get("MK_SSD_SUB", "9"))


class Buf:
    __slots__ = ("t", "w", "r")

    def __init__(self, t):
        self.t = t
        self.w = None
        self.r = []

    def __getitem__(self, idx):
        return self.t[idx]


class _Rec:
    def __init__(self):
        self.call = None

    def __getattr__(self, name):
        def f(*args, **kw):
            self.call = (name, args, kw)
            return self
        return f


class KB:
    NDMA_SEM = 8

    def __init__(self, nc):
        self.nc = nc
        self.engs = {"pe": nc.tensor, "act": nc.scalar, "dve": nc.vector, "pool": nc.gpsimd, "sp": nc.sync}
        self.sems = {}
        self.cnt = {}
        for k in ("pe", "act", "dve", "pool"):
            self.sems[k] = nc.alloc_semaphore(name="s_" + k)
            self.cnt[k] = 0
        self.dq = {}
        for q in ("sp", "pool", "act"):
            lst = []
            for i in range(self.NDMA_SEM):
                key = "d_%s%d" % (q, i)
                self.sems[key] = nc.alloc_semaphore(name=key)
                self.cnt[key] = 0
                lst.append(key)
            self.dq[q] = [lst, 0]
        self.seen = {e: {} for e in self.engs}
        self.ninst = 0
        self.defer = bool(int(os.environ.get('MK_SCHED', '1')))
        self.pending = []
        self.stats = {} if os.environ.get('MK_STATS') else None
        self.label = 'top'

    def _wait(self, eng, k, v):
        seen = self.seen[eng]
        if seen.get(k, 0) >= v:
            return
        self.engs[eng].wait_ge(self.sems[k], v)
        self.ninst += 1
        seen[k] = v

    def _need(self, eng, reads, writes):
        need = {}

        def add(dep):
            if dep is None:
                return
            k, v = dep
            if need.get(k, 0) < v:
                need[k] = v
        for b in reads:
            add(b.w)
        for b in writes:
            add(b.w)
            for d in b.r:
                add(d)
        for k, v in need.items():
            if k == eng and eng == "pe":
                continue
            self._wait(eng, k, v)

    def _record(self, dep, reads, writes):
        for b in reads:
            b.r.append(dep)
            if len(b.r) > 64:
                mx = {}
                for k, v in b.r:
                    if mx.get(k, 0) < v:
                        mx[k] = v
                b.r = list(mx.items())
        for b in writes:
            b.w = dep
            b.r = []

    def op(self, eng, fn, reads=(), writes=(), inc=True):
        if self.defer:
            rec = _Rec()
            fn(rec)
            self.pending.append(("op", eng, rec.call, tuple(reads), tuple(writes), inc))
            return None
        return self._op_now(eng, fn, reads, writes, inc)

    def _op_now(self, eng, fn, reads=(), writes=(), inc=True):
        self._need(eng, reads, writes)
        ins = fn(self.engs[eng])
        self.ninst += 1
        val = self.cnt[eng] + 1
        if inc:
            ins.then_inc(self.sems[eng], 1)
            self.cnt[eng] = val
        self._record((eng, val), reads, writes)
        return ins

    def dma(self, q, out, in_, reads=(), writes=(), **kw):
        if self.defer:
            self.pending.append(("dma", q, (out, in_, kw), tuple(reads), tuple(writes), True))
            return None
        return self._dma_now(q, out, in_, reads, writes, **kw)

    def _dma_now(self, q, out, in_, reads=(), writes=(), **kw):
        self._need(q, reads, writes)
        lst, i = self.dq[q]
        key = lst[i % len(lst)]
        self.dq[q][1] = i + 1
        if self.cnt[key]:
            self._wait(q, key, self.cnt[key])
        ins = self.engs[q].dma_start(out=out, in_=in_, **kw)
        self.ninst += 1
        self.cnt[key] += 16
        ins.then_inc(self.sems[key], 16)
        dep = (key, self.cnt[key])
        self._record(dep, reads, writes)
        return dep

    @staticmethod
    def _cost(kind, eng, call):
        def fsz(ap):
            n = 1
            for d in ap.shape[1:]:
                n *= d
            return n
        if kind == "dma":
            out = call[0]
            nb = fsz(out) * out.shape[0] * (2 if out.dtype == BF16 else 4)
            return 2000.0 + nb / 80.0
        name, args, kw = call
        if name == "matmul":
            n = fsz(kw["rhs"])
            passes = 4 if kw["lhsT"].dtype == F32 else 1
            return 70.0 + n * passes * 0.45
        out = kw.get("out", None)
        if out is None:
            out = kw.get("ap", args[0] if args else None)
        n = fsz(out) if out is not None else 64
        if eng == "act":
            return 230.0 + n * 0.75
        if name == "reciprocal":
            return 70.0 + n * 6.5
        if name == "memset":
            return 70.0 + n * 0.5
        return 70.0 + n * 1.1

    def flush(self):
        pend = self.pending
        self.pending = []
        if not pend:
            return
        import heapq
        units = []
        cur = None
        for it in pend:
            kind, eng, call, rd, wr, inc = it
            if kind == "op" and eng == "pe":
                if cur is None:
                    cur = [eng, [], 0.0, set(), set()]
                cur[1].append(it)
                cur[2] += self._cost(kind, eng, call)
                cur[3].update(rd)
                cur[4].update(wr)
                if inc:
                    units.append(cur)
                    cur = None
            else:
                assert cur is None, "non-PE op inside an open PE group"
                units.append([eng, [it], self._cost(kind, eng, call), set(rd), set(wr)])
        assert cur is None, "PE group without final inc"
        n = len(units)
        lastw = {}
        readers = {}
        deps = [None] * n
        succ = [[] for _ in range(n)]
        for i, u in enumerate(units):
            d = set()
            for b in u[3]:
                if b in lastw:
                    d.add(lastw[b])
            for b in u[4]:
                if b in lastw:
                    d.add(lastw[b])
                for r in readers.get(b, ()):
                    d.add(r)
            d.discard(i)
            deps[i] = d
            for j in d:
                succ[j].append(i)
            for b in u[3]:
                readers.setdefault(b, []).append(i)
            for b in u[4]:
                lastw[b] = i
                readers[b] = []
        ndep = [len(d) for d in deps]
        ready_t = [0.0] * n
        fin = [0.0] * n
        free = {}
        heaps = {}
        for i in range(n):
            if ndep[i] == 0:
                heapq.heappush(heaps.setdefault(units[i][0], []), (0.0, i))
        order = []
        done = 0
        while done < n:
            best = None
            for e, h in heaps.items():
                if not h:
                    continue
                rt, i = h[0]
                st_ = max(rt, free.get(e, 0.0))
                if best is None or (st_, i) < (best[0], best[1]):
                    best = (st_, i, e)
            st_, i, e = best
            heapq.heappop(heaps[e])
            u = units[i]
            if e in ("sp", "pool") or (e == "act" and u[1][0][0] == "dma"):
                free[e] = st_ + 60.0
                fin[i] = st_ + u[2]
            else:
                fin[i] = st_ + u[2]
                free[e] = fin[i]
            order.append((st_, i))
            done += 1
            for j in succ[i]:
                ndep[j] -= 1
                if fin[i] > ready_t[j]:
                    ready_t[j] = fin[i]
                if ndep[j] == 0:
                    heapq.heappush(heaps.setdefault(units[j][0], []), (ready_t[j], j))
        if self.stats is not None:
            mk = max(fin) if fin else 0.0
            busy = {}
            for u in units:
                busy[u[0]] = busy.get(u[0], 0.0) + u[2]
            st = self.stats.setdefault(self.label, [0.0, {}, 0, 0, 0.0])
            st[0] += mk
            st[2] += n
            st[3] += 1
            st[4] += mk - max(busy.values())
            for e_, v_ in busy.items():
                st[1][e_] = st[1].get(e_, 0.0) + v_
        order.sort()
        for _, i in order:
            for kind, eng, call, rd, wr, inc in units[i][1]:
                if kind == "op":
                    name, args, kw = call
                    self._op_now(eng, lambda en: getattr(en, name)(*args, **kw), rd, wr, inc)
                else:
                    out, in_, kw = call
                    self._dma_now(eng, out, in_, rd, wr, **kw)

    def barrier(self, include_pool_dma=False):
        self.flush()
        keys = ["pe", "act", "dve", "pool"] + self.dq["sp"][0] + self.dq["act"][0]
        if include_pool_dma:
            keys += self.dq["pool"][0]
        for e in ("pe", "act", "dve", "pool", "sp"):
            for k in keys:
                if (k == e and e == "pe") or self.cnt[k] == 0:
                    continue
                self._wait(e, k, self.cnt[k])


def bcast(ap, shape, axis):
    return ap.unsqueeze(axis).broadcast_to(list(shape))


IN_SPECS = [
    ("xp", [512, 1024]), ("xs", [2048, 1024]), ("cvec", [2, 1024]),
    ("cdk", [4, 256, 512]), ("cdv", [4, 256, 512]), ("cgk", [4, 256, 128]), ("cgv", [4, 256, 128]),
    ("sssm", [4, 2, 8, 64, 64]), ("smc", [4, 2, 4, 128, 128]), ("smn", [4, 2, 4, 128]), ("smm", [4, 8]),
    ("w_ada", [4, 1024, 6144]), ("b_ada", [4, 6144]), ("norm1", [4, 1024]), ("norm2", [4, 1024]),
    ("w_in", [4, 1024, IN_COLS]), ("w_out", [4, 2048, 1024]),
    ("conv_ssd_w", [4, 5, 768]), ("conv_ssd_b", [4, 768]), ("ssd_a_log", [4, 16]), ("ssd_dt_bias", [4, 16]),
    ("ssd_d", [4, 8]), ("ssd_norm", [4, 512]),
    ("diff_lq1", [4, 64]), ("diff_lk1", [4, 64]), ("diff_lq2", [4, 64]), ("diff_lk2", [4, 64]),
    ("conv_mlstm_w", [4, 5, 1024]), ("conv_mlstm_b", [4, 1024]), ("mlstm_gate_b", [4, 16]), ("mlstm_norm", [4, 512]),
    ("gqa_q_norm", [4, 64]), ("gqa_k_norm", [4, 64]),
    ("w_ffn_in", [4, 1024, 2 * D_FF]), ("w_ffn_out", [4, D_FF, 1024]), ("norm_f", [1024]),
    ("consts", [128, 1152]), ("rope", [128, 2, 2048]),
]
OUT_SPECS = [
    ("yp", [512, 1024]), ("ys", [2048, 1024]),
    ("ndk", [2, 4, 256, 512]), ("ndv", [2, 4, 256, 512]), ("ngk", [2, 4, 256, 128]), ("ngv", [2, 4, 256, 128]),
    ("nssm", [2, 4, 2, 8, 64, 64]), ("nmc", [2, 4, 2, 4, 128, 128]), ("nmn", [2, 4, 2, 4, 128]), ("nmm", [2, 4, 8]),
]


def make_consts():
    c = np.zeros((128, 1152), np.float32)
    k = np.arange(128)
    c[:, 0:128] = np.eye(128)
    c[:, 128:256] = 1.0
    c[:, 256:384] = (k[:, None] <= k[None, :])
    c[:, 384:512] = (k[:, None] >= k[None, :])
    c[:, 512:640] = np.where(k[:, None] <= k[None, :], 0.0, NEG)
    c[:, 640:768] = np.where(k[:, None] >= k[None, :], 0.0, NEG)
    c[:, 768:896] = (k[:, None] // 64 == k[None, :] // 64)
    rm = np.zeros((128, 128), np.float32)
    for dp in range(128):
        half = (dp % 32) // 16
        if half == 0:
            rm[dp + 16, dp] = -1.0
        else:
            rm[dp - 16, dp] = 1.0
    c[:, 896:1024] = rm
    c[64, 1024:1088] = 1.0
    c[0, 1088:1152] = 1.0
    return c


def make_rope():
    t = np.arange(2048)
    r = (t // 64).astype(np.float32)
    cc = (t % 64).astype(np.float32)
    nf = 16
    freqs = (10000.0 ** (-np.arange(nf, dtype=np.float32) / nf)).astype(np.float32)
    ang = np.stack([r[:, None] * freqs, cc[:, None] * freqs], axis=1).astype(np.float32)
    out = np.zeros((128, 2, 2048), np.float32)
    for p in range(128):
        d = p % 64
        a = d // 32
        f = d % 16
        out[p, 0] = np.cos(ang[:, a, f])
        out[p, 1] = np.sin(ang[:, a, f])
    return out


def build_program():
    nc = bass.Bass("TRN2", target_bir_lowering=False)
    kb = KB(nc)
    D = {}
    for name, shape in IN_SPECS:
        if shape[0] == 4 and len(shape) > 1:
            shape = [NL] + list(shape[1:])
        D[name] = nc.dram_tensor(name, shape, F32, kind="ExternalInput").ap()
    for name, shape in OUT_SPECS:
        D[name] = nc.dram_tensor(name, shape, F32, kind="ExternalOutput").ap()
    mod_d = nc.dram_tensor("mod_scr", [4, 2, 6144], F32, kind="Internal").ap()

    top = ExitStack()

    class scope:
        def __enter__(self_):
            self_.st = ExitStack()
            return self_.st

        def __exit__(self_, *a):
            if a[0] is None:
                kb.barrier()
            self_.st.close()
            return False

    uid = [0]

    def alloc(stack, name, shape, dt, psum=False):
        uid[0] += 1
        name = "%s_%d" % (name, uid[0])
        cm = nc.psum_tensor(name, shape, dt) if psum else nc.sbuf_tensor(name, shape, dt)
        return Buf(stack.enter_context(cm))

    xT = [alloc(top, "xT%d" % i, [128, 8, 512], F32) for i in range(4)]
    hT = [alloc(top, "hT%d" % i, [128, 8, 512], BF16) for i in range(4)]
    NW = 2
    wbufs = [alloc(top, "wb%d" % i, [128, 4096], BF16) for i in range(NW)]
    wstate = [0]
    psb = [alloc(top, "ps%d" % i, [128, 512], F32, psum=True) for i in range(8)]
    pstate = [0]
    cf = alloc(top, "cf", [128, 640], F32)
    cb = alloc(top, "cb", [128, 1024], BF16)
    modc = alloc(top, "modc", [128, 6, 8], F32)
    nrm = alloc(top, "nrm", [128, 2, 8], F32)
    AB = alloc(top, "AB", [128, 4, 8], F32)
    nfc = alloc(top, "nfc", [128, 8], F32)
    lnsb = alloc(top, "lnsb", [128, 1], F32)
    lns_col = lnsb.t

    wo_buf = alloc(top, "wo_buf", [128, 4, 1024], BF16)
    accstate = [0]

    def ps():
        b = psb[pstate[0] % 4]
        pstate[0] += 1
        return b

    def ps_acc():
        b = psb[4 + accstate[0] % 4]
        accstate[0] += 1
        return b

    ffn_state = [0]

    def wload(pieces, kch, three=False):
        if three:
            lst = wbufs + [wo_buf]
            b = lst[ffn_state[0] % len(lst)]
            ffn_state[0] += 1
        else:
            b = wbufs[wstate[0] % NW]
            wstate[0] += 1
        ntot = sum(n for _, n in pieces)
        assert kch * ntot <= 4096, (kch, ntot)
        flat = b.t[:].rearrange("p k n -> p (k n)") if b is wo_buf else b.t
        view = flat[:, 0:kch * ntot].rearrange("p (k n) -> p k n", k=kch)
        o = 0
        for ap, n in pieces:
            kb.dma("pool", view[:, :, o:o + n], ap.rearrange("(k p) n -> p k n", p=128), writes=[b])
            o += n
        return b, view

    ident_f = cf.t[:, 0:128]
    ones_f = cf.t[:, 128:256]
    ident_b = cb.t[:, 0:128]
    selm_f = cf.t[:, 512:640]
    ones_b = cb.t[:, 128:256]

    kb.dma("sp", cf[:, 0:512], D["consts"][:, 0:512], writes=[cf])
    kb.dma("sp", cf[:, 512:640], D["consts"][:, 1024:1152], writes=[cf])
    kb.dma("pool", cb[:], D["consts"][:, 0:1024], writes=[cb])
    kb.op("dve", lambda e: e.memset(lnsb[:], math.log(128 ** -0.5)), writes=[lnsb])
    kb.dma("sp", nfc[:], D["norm_f"].rearrange("(c p) -> p c", p=128), writes=[nfc], allow_slow_non_contiguous=True)

    modall = alloc(top, "modall", [128, NL, 48, 2], F32)
    ball = alloc(top, "ball", [128, NL, 48], F32)
    nrmall = alloc(top, "nrmall", [128, NL, 2, 8], F32)
    for l in range(NL):
        kb.dma("sp", ball[:, l, :], D["b_ada"][l].rearrange("(j p) -> p j", p=128), writes=[ball], allow_slow_non_contiguous=True)
        kb.dma("sp", nrmall[:, l, 0, :], D["norm1"][l].rearrange("(c p) -> p c", p=128), writes=[nrmall], allow_slow_non_contiguous=True)
        kb.dma("sp", nrmall[:, l, 1, :], D["norm2"][l].rearrange("(c p) -> p c", p=128), writes=[nrmall], allow_slow_non_contiguous=True)
    cT = alloc(top, "cT", [128, 2, 8], F32)
    cTb = alloc(top, "cTb", [128, 2, 8], BF16)
    sig = alloc(top, "csig", [128, 2, 8], F32)
    for g in range(2):
        kb.dma("sp", cT[:, g, :], D["cvec"][g].rearrange("(c p) -> p c", p=128), writes=[cT], allow_slow_non_contiguous=True)
    kb.op("act", lambda e: e.activation(out=sig[:], in_=cT[:], func=AF.Sigmoid), reads=[cT], writes=[sig])
    kb.op("dve", lambda e: e.tensor_tensor(out=cTb[:], in0=cT[:], in1=sig[:], op=ALU.mult), reads=[cT, sig], writes=[cTb])

    def compute_mod(l):
        for blk in range(12):
            c0 = blk * 512
            wb, wv = wload([(D["w_ada"][l][:, c0:c0 + 512], 512)], 8)
            p = ps()
            for cc in range(4):
                for k in range(8):
                    kb.op("pe", lambda e: e.matmul(p[:, 2 * cc:2 * cc + 2], lhsT=wv[:, k, cc * 128:(cc + 1) * 128], rhs=cTb[:, :, k], start=(k == 0), stop=(k == 7)),
                          reads=[cTb, wb], writes=[p], inc=(k == 7 and cc == 3))
            kb.op("dve", lambda e: e.tensor_tensor(out=modall[:, l, blk * 4:(blk + 1) * 4, :], in0=p[:, 0:8].rearrange("p (j g) -> p j g", g=2),
                                                   in1=bcast(ball[:, l, blk * 4:(blk + 1) * 4], [128, 4, 2], 2), op=ALU.add), reads=[p, ball], writes=[modall])

    compute_mod(0)
    kb.barrier()

    def load_x(src, nblk):
        with scope() as st:
            xin = [alloc(st, "xin%d" % i, [128, 1024], F32) for i in range(2)]
            for tb in range(nblk):
                tiles = []
                for tl in range(4):
                    pass
                for tl in range(4):
                    xi = xin[(tb * 4 + tl) % 2]
                    t0 = (tb * 4 + tl) * 128
                    kb.dma("sp", xi[:], src[t0:t0 + 128, :], writes=[xi])
                    for half in range(2):
                        p = ps()
                        for cc in range(4):
                            c = half * 4 + cc
                            kb.op("pe", lambda e: e.matmul(p[:, cc * 128:(cc + 1) * 128], lhsT=xi[:, c * 128:(c + 1) * 128], rhs=ident_f,
                                                           start=True, stop=True), reads=[xi, cf], writes=[p], inc=(cc == 3))
                        kb.op("act", lambda e: e.activation(
                            out=xT[tb][:, half * 4:half * 4 + 4, tl * 128:(tl + 1) * 128],
                            in_=p[:, :].rearrange("p (c t) -> p c t", c=4), func=AF.Copy), reads=[p], writes=[xT[tb]])

    def norm_block(st_tmp, tb, Acol, Bcol, dst, dst_dt_is_bf16=True):
        sq, rstd, tmp2 = st_tmp
        if isinstance(sq, list):
            sq, rstd = sq[tb % 2], rstd[tb % 2]
        kb.op("act", lambda e: e.activation(out=sq[:], in_=xT[tb][:], func=AF.Square), reads=[xT[tb]], writes=[sq])
        p = ps()
        for c in range(8):
            kb.op("pe", lambda e: e.matmul(p[:, :], lhsT=ones_b, rhs=sq[:, c, :], start=(c == 0), stop=(c == 7)),
                  reads=[sq, cb], writes=[p], inc=(c == 7))
        kb.op("dve", lambda e: e.tensor_scalar(out=rstd[:], in0=p[:, :], scalar1=1.0 / D_MODEL, scalar2=EPS, op0=ALU.mult, op1=ALU.add),
              reads=[p], writes=[rstd])
        kb.op("act", lambda e: e.activation(out=rstd[:], in_=rstd[:], func=AF.Ln), reads=[rstd], writes=[rstd])
        kb.op("act", lambda e: e.activation(out=rstd[:], in_=rstd[:], func=AF.Exp, scale=-0.5), reads=[rstd], writes=[rstd])
        for c in range(8):
            t2 = tmp2[c % len(tmp2)]
            kb.op("dve", lambda e: e.tensor_tensor(out=t2[:], in0=xT[tb][:, c, :], in1=rstd[:], op=ALU.mult),
                  reads=[xT[tb], rstd], writes=[t2])
            if Bcol is not None:
                kb.op("act", lambda e: e.activation(out=dst[:, c, :], in_=t2[:], func=AF.Identity, bias=Bcol[:, c:c + 1], scale=Acol[:, c:c + 1]),
                      reads=[t2, AB], writes=[dst])
            else:
                kb.op("act", lambda e: e.activation(out=dst[:, c, :], in_=t2[:], func=AF.Identity, scale=Acol[:, c:c + 1]),
                      reads=[t2, nfc], writes=[dst])

    def load_mod(l, g):
        kb.op("dve", lambda e: e.tensor_copy(out=modc[:], in_=modall[:, l, :, g].rearrange("p (v c) -> p v c", v=6)), reads=[modall], writes=[modc])
        kb.op("dve", lambda e: e.tensor_copy(out=nrm[:], in_=nrmall[:, l, :, :]), reads=[nrmall], writes=[nrm])
        for j, (vs, vh) in enumerate(((1, 0), (4, 3))):
            kb.op("dve", lambda e: e.scalar_tensor_tensor(out=AB[:, 2 * j, :], in0=modc[:, vs, :], scalar=1.0, in1=nrm[:, j, :],
                                                          op0=ALU.add, op1=ALU.mult), reads=[modc, nrm], writes=[AB])
            kb.op("dve", lambda e: e.tensor_copy(out=AB[:, 2 * j + 1, :], in_=modc[:, vh, :]), reads=[modc], writes=[AB])

    def ffn(l, blocks):
        kb.label = 'ffn'
        nb = len(blocks)
        with scope() as st:
            actT = alloc(st, "actT", [128, 22, nb * 512], BF16)
            sg = [alloc(st, "sg%d" % i, [128, 512], F32) for i in range(2)]
            it = 0
            for jj in range(11):
                c0 = jj * 256
                wb, wv = wload([(D["w_ffn_in"][l][:, c0:c0 + 256], 256), (D["w_ffn_in"][l][:, D_FF + c0:D_FF + c0 + 256], 256)], 8, three=True)
                for j2 in range(2):
                    j = jj * 2 + j2
                    for bi, tb in enumerate(blocks):
                        pg = ps()
                        pu = ps()
                        for k in range(8):
                            kb.op("pe", lambda e: e.matmul(pg[:, :], lhsT=wv[:, k, j2 * 128:(j2 + 1) * 128], rhs=hT[tb][:, k, :],
                                                           start=(k == 0), stop=(k == 7)), reads=[wb, hT[tb]], writes=[pg], inc=(k == 7))
                        for k in range(8):
                            kb.op("pe", lambda e: e.matmul(pu[:, :], lhsT=wv[:, k, 256 + j2 * 128:256 + (j2 + 1) * 128], rhs=hT[tb][:, k, :],
                                                           start=(k == 0), stop=(k == 7)), reads=[wb, hT[tb]], writes=[pu], inc=(k == 7))
                        s = sg[it % 2]
                        it += 1
                        kb.op("act", lambda e: e.activation(out=s[:], in_=pg[:, :], func=AF.Silu), reads=[pg], writes=[s])
                        kb.op("dve", lambda e: e.tensor_tensor(out=actT[:, j, bi * 512:(bi + 1) * 512], in0=s[:], in1=pu[:, :], op=ALU.mult),
                              reads=[s, pu], writes=[actT])
            for c in range(8):
                wb, wv = wload([(D["w_ffn_out"][l][:, c * 128:(c + 1) * 128], 128)], 22, three=True)
                for bi, tb in enumerate(blocks):
                    p = ps()
                    for k in range(22):
                        kb.op("pe", lambda e: e.matmul(p[:, :], lhsT=wv[:, k, :], rhs=actT[:, k, bi * 512:(bi + 1) * 512],
                                                       start=(k == 0), stop=(k == 21)), reads=[wb, actT], writes=[p], inc=(k == 21))
                    kb.op("dve", lambda e: e.scalar_tensor_tensor(out=xT[tb][:, c, :], in0=p[:, :], scalar=modc[:, 5, c:c + 1], in1=xT[tb][:, c, :],
                                                                  op0=ALU.mult, op1=ALU.add), reads=[p, modc, xT[tb]], writes=[xT[tb]])
        kb.barrier()

    def final_out(nblk, dst):
        kb.label = 'final'
        with scope() as st:
            sq = alloc(st, "sq", [128, 8, 512], BF16)
            rstd = alloc(st, "rstd", [128, 512], F32)
            tmp2 = [alloc(st, "tmpn%d" % i, [128, 512], F32) for i in range(2)]
            xn = alloc(st, "xn", [128, 8, 512], F32)
            ot = [alloc(st, "ot%d" % i, [128, 1024], F32) for i in range(2)]
            for tb in range(nblk):
                norm_block((sq, rstd, tmp2), tb, nfc, None, xn)
                for tl in range(4):
                    o = ot[tl % 2]
                    for half in range(2):
                        p = ps()
                        for cc in range(4):
                            c = half * 4 + cc
                            kb.op("pe", lambda e: e.matmul(p[:, cc * 128:(cc + 1) * 128], lhsT=xn[:, c, tl * 128:(tl + 1) * 128], rhs=ident_f,
                                                           start=True, stop=True), reads=[xn, cf], writes=[p], inc=(cc == 3))
                        kb.op("act", lambda e: e.activation(out=o[:, half * 512:(half + 1) * 512], in_=p[:, :], func=AF.Copy), reads=[p], writes=[o])
                    t0 = (tb * 4 + tl) * 128
                    kb.dma("sp", dst[t0:t0 + 128, :], o[:], reads=[o])
        kb.barrier()


    bd64_b = cb.t[:, 768:896]
    rm_b = cb.t[:, 896:1024]

    def proj_tm(wb, wv, s0, n, tb, tl, p):
        for k in range(8):
            kb.op("pe", lambda e: e.matmul(p[:, 0:n], lhsT=hT[tb][:, k, tl * 128:(tl + 1) * 128], rhs=wv[:, k, s0:s0 + n],
                                           start=(k == 0), stop=(k == 7)), reads=[wb, hT[tb]], writes=[p], inc=(k == 7))

    def load_wo(l, row0):
        kb.dma("pool", wo_buf[:], D["w_out"][l][row0:row0 + 512, :].rearrange("(k p) n -> p k n", p=128), writes=[wo_buf])

    def mixer_out(st, ytm, tb, yT):
        for c in range(4):
            p = ps()
            for tl in range(4):
                kb.op("pe", lambda e: e.matmul(p[:, tl * 128:(tl + 1) * 128], lhsT=ytm[:, tl, c * 128:(c + 1) * 128], rhs=ident_b,
                                               start=True, stop=True), reads=[ytm, cb], writes=[p], inc=(tl == 3))
            kb.op("act", lambda e: e.activation(out=yT[:, c, :], in_=p[:, :], func=AF.Copy), reads=[p], writes=[yT])
        for c in range(8):
            p = ps()
            for k in range(4):
                kb.op("pe", lambda e: e.matmul(p[:, :], lhsT=wo_buf[:, k, c * 128:(c + 1) * 128], rhs=yT[:, k, :],
                                               start=(k == 0), stop=(k == 3)), reads=[wo_buf, yT], writes=[p], inc=(k == 3))
            kb.op("dve", lambda e: e.scalar_tensor_tensor(out=xT[tb][:, c, :], in0=p[:, :], scalar=modc[:, 2, c:c + 1], in1=xT[tb][:, c, :],
                                                          op0=ALU.mult, op1=ALU.add), reads=[p, modc, xT[tb]], writes=[xT[tb]])

    def attention(l, g, kind, nblk):
        kb.label = 'attn_%s_g%d' % (kind, g)
        sample = (g == 1)
        L = 2048 if sample else 256
        nseq = 1 if sample else 2
        nctx = 2 if sample else 0
        lt = L // 128
        nkt = lt + nctx
        ntok = nblk * 512
        if kind == "d":
            qc0, kc0, vc0, nkc, nvh, ve, orow0, nheads = 4896, 5408, 5536, 1, 2, 64, 1536, 8
            ck, cv, ok, ov = D["cgk"], D["cgv"], D["ngk"], D["ngv"]
        else:
            qc0, kc0, vc0, nkc, nvh, ve, orow0, nheads = 1296, 1808, 2320, 4, 4, 128, 512, 4
            ck, cv, ok, ov = D["cdk"], D["cdv"], D["ndk"], D["ndv"]
        scale = 64 ** -0.5
        kw = nkc * 128
        vw = nvh * ve
        lam_init = 0.8 - 0.6 * math.exp(-0.3 * l)
        with scope() as st:
            qT = alloc(st, "qT", [128, 4, ntok], BF16)
            kT = alloc(st, "kT", [128, nkc, nseq * nkt * 128], BF16)
            vsw = ve + 1 if kind == "d" else ve
            vaug = alloc(st, "vaug", [128, nseq * nkt, nvh, vsw], BF16)
            vodd = alloc(st, "vodd", [128, nseq * nkt, nvh, 128], BF16) if kind == "d" else None
            yT = alloc(st, "yT", [128, 4, 512], BF16)
            pTs = [alloc(st, "pT%d" % i, [128, 512], BF16) for i in range(4)]
            sqb = alloc(st, "sqb", [128, 512], BF16)
            rs = alloc(st, "rs", [128, 512], F32)
            qn = alloc(st, "qn", [128, 512], BF16)
            t1 = alloc(st, "t1", [128, 512], F32)
            fin_bufs = [(rs, t1, None, sqb)]
            gcol = alloc(st, "gcol", [128, 2], F32)
            osb = alloc(st, "osb", [128, 512], F32)
            t2 = osb
            fin_bufs[0] = (rs, t1, osb, sqb)
            sm = alloc(st, "sm", [128, 16], F32)
            lamt = alloc(st, "lamt", [128, 4, 64], F32)
            kng = alloc(st, "kng", [128, 64], F32)
            ropeT = alloc(st, "ropeT", [128, 2, 2048], BF16) if sample else None
            if sample and kind == "b":
                pass
            else:
                try:
                    fin_bufs.append((alloc(st, "rs2", [128, 512], F32), alloc(st, "t12", [128, 512], F32),
                                     alloc(st, "osb2", [128, 512], F32) if kind == "b" else None, alloc(st, "sqb2", [128, 512], BF16) if kind == "b" else None))
                except AssertionError:
                    pass
            load_wo(l, orow0)
            if kind == "d":
                kb.op("dve", lambda e: e.memset(vaug[:, :, :, ve:ve + 1], 1.0), writes=[vaug])
                kb.op("dve", lambda e: e.memset(vodd[:, :, :, 0:1], 1.0), writes=[vodd])
                kb.op("dve", lambda e: e.memset(vodd[:, :, :, 1:64], 0.0), writes=[vodd])
            if sample:
                kb.dma("pool", ropeT[:], D["rope"], writes=[ropeT])
            if kind == "d":
                for j, nm in enumerate(("gqa_q_norm", "gqa_k_norm")):
                    for hh in range(2):
                        kb.dma("sp", gcol[hh * 64:(hh + 1) * 64, j:j + 1], D[nm][l].rearrange("(d o) -> d o", o=1), writes=[gcol])
                kb.dma("sp", kng[:], D["gqa_k_norm"][l].partition_broadcast(128), writes=[kng])
            else:
                for j, nm in enumerate(("diff_lq1", "diff_lk1", "diff_lq2", "diff_lk2")):
                    kb.dma("sp", lamt[:, j, :], D[nm][l].partition_broadcast(128), writes=[lamt])
                kb.op("dve", lambda e: e.tensor_tensor(out=lamt[:, 0, :], in0=lamt[:, 0, :], in1=lamt[:, 1, :], op=ALU.mult), reads=[lamt], writes=[lamt])
                kb.op("dve", lambda e: e.tensor_tensor(out=lamt[:, 2, :], in0=lamt[:, 2, :], in1=lamt[:, 3, :], op=ALU.mult), reads=[lamt], writes=[lamt])
                kb.op("dve", lambda e: e.tensor_reduce(out=sm[:, 2:3], in_=lamt[:, 0, :], axis=AX.X, op=ALU.add), reads=[lamt], writes=[sm])
                kb.op("dve", lambda e: e.tensor_reduce(out=sm[:, 3:4], in_=lamt[:, 2, :], axis=AX.X, op=ALU.add), reads=[lamt], writes=[sm])
                kb.op("act", lambda e: e.activation(out=sm[:, 2:4], in_=sm[:, 2:4], func=AF.Exp), reads=[sm], writes=[sm])
                kb.op("dve", lambda e: e.tensor_tensor(out=sm[:, 0:1], in0=sm[:, 2:3], in1=sm[:, 3:4], op=ALU.subtract), reads=[sm], writes=[sm])
                kb.op("dve", lambda e: e.tensor_scalar(out=sm[:, 1:2], in0=sm[:, 0:1], scalar1=lam_init, scalar2=-1.0, op0=ALU.add, op1=ALU.mult), reads=[sm], writes=[sm])

            def qk_post(p, dst, tb, normj):
                src = p
                if kind == "d":
                    kb.op("act", lambda e: e.activation(out=sqb[:], in_=p[:, :], func=AF.Square), reads=[p], writes=[sqb])
                    pn = ps()
                    kb.op("pe", lambda e: e.matmul(pn[:, :], lhsT=bd64_b, rhs=sqb[:], start=True, stop=True), reads=[cb, sqb], writes=[pn])
                    kb.op("dve", lambda e: e.tensor_scalar(out=rs[:], in0=pn[:, :], scalar1=1.0 / 64, scalar2=EPS, op0=ALU.mult, op1=ALU.add), reads=[pn], writes=[rs])
                    kb.op("act", lambda e: e.activation(out=rs[:], in_=rs[:], func=AF.Ln), reads=[rs], writes=[rs])
                    kb.op("act", lambda e: e.activation(out=rs[:], in_=rs[:], func=AF.Exp, scale=-0.5), reads=[rs], writes=[rs])
                    tgt = qn if sample else None
                    o_ap = qn[:] if sample else dst
                    kb.op("dve", lambda e: e.scalar_tensor_tensor(out=o_ap, in0=p[:, :], scalar=gcol[:, normj:normj + 1], in1=rs[:], op0=ALU.mult, op1=ALU.mult),
                          reads=[p, gcol, rs], writes=[qn if sample else dst_buf[0]])
                else:
                    o_ap = qn[:] if sample else dst
                    kb.op("act", lambda e: e.activation(out=o_ap, in_=p[:, :], func=AF.Copy), reads=[p], writes=[qn if sample else dst_buf[0]])
                if sample:
                    pr = ps()
                    kb.op("pe", lambda e: e.matmul(pr[:, :], lhsT=rm_b, rhs=qn[:], start=True, stop=True), reads=[cb, qn], writes=[pr])
                    kb.op("dve", lambda e: e.tensor_tensor(out=t1[:], in0=qn[:], in1=ropeT[:, 0, tb * 512:(tb + 1) * 512], op=ALU.mult), reads=[qn, ropeT], writes=[t1])
                    kb.op("dve", lambda e: e.tensor_tensor(out=t2[:], in0=pr[:, :], in1=ropeT[:, 1, tb * 512:(tb + 1) * 512], op=ALU.mult), reads=[pr, ropeT], writes=[t2])
                    kb.op("dve", lambda e: e.tensor_tensor(out=dst, in0=t1[:], in1=t2[:], op=ALU.add), reads=[t1, t2], writes=[dst_buf[0]])

            dst_buf = [None]
            blocks = list(range(nblk))
            if kind == "d":
                pcs = []
                for j in range(4):
                    for hh in (j, 4 + j):
                        pcs.append((D["w_in"][l][:, qc0 + hh * 64:qc0 + (hh + 1) * 64], 64))
                wb, wv = wload(pcs, 8)
            else:
                wb, wv = wload([(D["w_in"][l][:, qc0:qc0 + 512], 512)], 8)
            dst_buf[0] = qT
            for j in range(4):
                for tb in blocks:
                    p = ps()
                    for k in range(8):
                        lh = wv[:, k, j * 128:(j + 1) * 128]
                        kb.op("pe", lambda e: e.matmul(p[:, :], lhsT=lh, rhs=hT[tb][:, k, :], start=(k == 0), stop=(k == 7)),
                              reads=[wb, hT[tb]], writes=[p], inc=(k == 7))
                    qk_post(p, qT[:, j, tb * 512:(tb + 1) * 512], tb, 0)
            wb, wv = wload([(D["w_in"][l][:, kc0:kc0 + kw], kw)], 8)
            dst_buf[0] = kT
            for j in range(nkc):
                for tb in blocks:
                    p = ps()
                    for k in range(8):
                        kb.op("pe", lambda e: e.matmul(p[:, :], lhsT=wv[:, k, j * 128:(j + 1) * 128], rhs=hT[tb][:, k, :], start=(k == 0), stop=(k == 7)),
                              reads=[wb, hT[tb]], writes=[p], inc=(k == 7))
                    qk_post(p, kT[:, j, nctx * 128 + tb * 512:nctx * 128 + (tb + 1) * 512], tb, 1)
            if not sample:
                for tt in range(ntok // 128):
                    sq_, r0 = divmod(tt, lt)
                    p = ps()
                    proj_tm(wb, wv, 0, kw, tt // 4, tt % 4, p)
                    kb.op("act", lambda e: e.activation(out=osb[:, 0:kw], in_=p[:, 0:kw], func=AF.Copy), reads=[p], writes=[osb])
                    if kind == "d":
                        kb.op("dve", lambda e: e.tensor_tensor(out=t1[:, 0:128], in0=osb[:, 0:128], in1=osb[:, 0:128], op=ALU.mult), reads=[osb], writes=[t1])
                        kb.op("dve", lambda e: e.tensor_reduce(out=sm[:, 8:10], in_=t1[:, 0:128].rearrange("p (h d) -> p h d", d=64), axis=AX.X, op=ALU.add), reads=[t1], writes=[sm])
                        kb.op("dve", lambda e: e.tensor_scalar(out=sm[:, 8:10], in0=sm[:, 8:10], scalar1=1.0 / 64, scalar2=EPS, op0=ALU.mult, op1=ALU.add), reads=[sm], writes=[sm])
                        kb.op("act", lambda e: e.activation(out=sm[:, 8:10], in_=sm[:, 8:10], func=AF.Ln), reads=[sm], writes=[sm])
                        kb.op("act", lambda e: e.activation(out=sm[:, 8:10], in_=sm[:, 8:10], func=AF.Exp, scale=-0.5), reads=[sm], writes=[sm])
                        kb.op("dve", lambda e: e.tensor_tensor(out=t1[:, 0:128].rearrange("p (h d) -> p h d", d=64), in0=osb[:, 0:128].rearrange("p (h d) -> p h d", d=64),
                                                               in1=bcast(sm[:, 8:10], [128, 2, 64], 2), op=ALU.mult), reads=[osb, sm], writes=[t1])
                        kb.op("dve", lambda e: e.tensor_tensor(out=t2[:, 0:128].rearrange("p (h d) -> p h d", d=64), in0=t1[:, 0:128].rearrange("p (h d) -> p h d", d=64),
                                                               in1=bcast(kng[:], [128, 2, 64], 1), op=ALU.mult), reads=[t1, kng], writes=[t2])
                        kb.dma("sp", ok[sq_, l, r0 * 128:(r0 + 1) * 128, :], t2[:, 0:128], reads=[t2])
                    else:
                        kb.dma("sp", ok[sq_, l, r0 * 128:(r0 + 1) * 128, :], osb[:, 0:kw], reads=[osb])
            wb, wv = wload([(D["w_in"][l][:, vc0:vc0 + vw], vw)], 8)
            for tt in range(ntok // 128):
                sq_, r0 = divmod(tt, lt)
                ktile = sq_ * nkt + nctx + r0
                p = ps()
                proj_tm(wb, wv, 0, vw, tt // 4, tt % 4, p)
                kb.op("act", lambda e: e.activation(out=vaug[:, ktile, :, 0:ve], in_=p[:, 0:vw].rearrange("p (h e) -> p h e", e=ve), func=AF.Copy), reads=[p], writes=[vaug])
                if kind == "d":
                    kb.op("act", lambda e: e.activation(out=vodd[:, ktile, :, 64:128], in_=p[:, 0:vw].rearrange("p (h e) -> p h e", e=ve), func=AF.Copy), reads=[p], writes=[vodd])
                if not sample:
                    kb.op("dve", lambda e: e.tensor_copy(out=osb[:, 0:vw], in_=p[:, 0:vw]), reads=[p], writes=[osb])
                    kb.dma("sp", ov[sq_, l, r0 * 128:(r0 + 1) * 128, :], osb[:, 0:vw], reads=[osb])
            if sample:
                with scope() as st2:
                    ctxk = alloc(st2, "ctxk", [128, kw], F32)
                    for t in range(2):
                        kb.dma("sp", ctxk[:], ck[l][t * 128:(t + 1) * 128, :], writes=[ctxk])
                        for j in range(nkc):
                            p = ps()
                            kb.op("pe", lambda e: e.matmul(p[:, 0:128], lhsT=ctxk[:, j * 128:(j + 1) * 128], rhs=ident_f, start=True, stop=True),
                                  reads=[ctxk, cf], writes=[p])
                            kb.op("act", lambda e: e.activation(out=kT[:, j, t * 128:(t + 1) * 128], in_=p[:, 0:128], func=AF.Copy), reads=[p], writes=[kT])
                    for t in range(2):
                        kb.dma("pool", vaug[:, t, :, 0:ve], cv[l][t * 128:(t + 1) * 128, :].rearrange("p (h e) -> p h e", e=ve), writes=[vaug])
                        if kind == "d":
                            kb.dma("pool", vodd[:, t, :, 64:128], cv[l][t * 128:(t + 1) * 128, :].rearrange("p (h e) -> p h e", e=ve), writes=[vodd])
                    kb.barrier(include_pool_dma=True)
            qblk = min(L, 512)
            pti = 0
            for s_ in range(nseq):
                for qb in range(L // qblk):
                    q0 = s_ * L + qb * qblk
                    qs = slice(q0, q0 + qblk)
                    yc = slice(q0 % 512, q0 % 512 + qblk)
                    its = []
                    if kind == "d":
                        for hp in range(4):
                            for kt in range(nkt):
                                its.append((hp, kt, 0))
                                its.append((hp + 4, kt, 0))
                    else:
                        for h in range(nheads):
                            for kt in range(nkt):
                                its.append((h, kt, 0))
                                its.append((h, kt, 1))
                    hstate = {}
                    pts = {}
                    DEPTH = 2

                    def stageA2(j):
                        pss = []
                        for i in (2 * j, 2 * j + 1):
                            h, kt, r = its[i]
                            kc = (s_ * nkt + kt) * 128
                            if kind == "d":
                                rows, qch = (h // 4) * 64, h % 4
                                lhs = kT[rows:rows + 64, 0, kc:kc + 128]
                                rh = qT[rows:rows + 64, qch, qs]
                            else:
                                lhs = kT[r * 64:(r + 1) * 64, h, kc:kc + 128]
                                rh = qT[r * 64:(r + 1) * 64, h, qs]
                            pS = ps()
                            kb.op("pe", lambda e: e.matmul(pS[:, 0:qblk], lhsT=lhs, rhs=rh, start=True, stop=True), reads=[kT, qT], writes=[pS], inc=(i % 2 == 1))
                            pss.append(pS)
                        for i, pS in zip((2 * j, 2 * j + 1), pss):
                            pT = pTs[i % 4]
                            kb.op("act", lambda e: e.activation(out=pT[:, 0:qblk], in_=pS[:, 0:qblk], func=AF.Exp, scale=scale), reads=[pS], writes=[pT])
                            pts[i] = pT

                    def stageC(i):
                        h, kt, r = its[i]
                        pT = pts.pop(i)
                        last = (kt == nkt - 1)
                        rs, t1, osb, sqb = fin_bufs[h % len(fin_bufs)]
                        if kind == "d":
                            vh, odd = h // 4, h % 2
                            if kt == 0:
                                hstate[h] = ps_acc()
                            accO = hstate[h]
                            mo = 128 if odd else ve + 1
                            lh = vodd[:, s_ * nkt + kt, vh, :] if odd else vaug[:, s_ * nkt + kt, vh, :]
                            kb.op("pe", lambda e: e.matmul(accO[0:mo, 0:qblk], lhsT=lh, rhs=pT[:, 0:qblk], start=(kt == 0), stop=last),
                                  reads=[pT, vodd if odd else vaug], writes=[accO], inc=True)
                            if last:
                                drow = 0 if odd else 64
                                orow = 64 if odd else 0
                                kb.op("act", lambda e: e.activation(out=rs[drow:drow + 1, 0:qblk], in_=accO[drow:drow + 1, 0:qblk], func=AF.Ln), reads=[accO], writes=[rs])
                                kb.op("act", lambda e: e.activation(out=rs[drow:drow + 1, 0:qblk], in_=rs[drow:drow + 1, 0:qblk], func=AF.Exp, scale=-1.0), reads=[rs], writes=[rs])
                                pB = ps()
                                kb.op("pe", lambda e: e.matmul(pB[:, 0:qblk], lhsT=selm_f[drow:drow + 1, :], rhs=rs[drow:drow + 1, 0:qblk], start=True, stop=True),
                                      reads=[cf, rs], writes=[pB])
                                kb.op("act", lambda e: e.activation(out=t1[orow:orow + 64, 0:qblk], in_=pB[orow:orow + 64, 0:qblk], func=AF.Copy), reads=[pB], writes=[t1])
                                kb.op("dve", lambda e: e.tensor_tensor(out=yT[orow:orow + 64, h // 2, yc], in0=accO[orow:orow + 64, 0:qblk], in1=t1[orow:orow + 64, 0:qblk], op=ALU.mult),
                                      reads=[accO, t1], writes=[yT])
                        else:
                            if kt == 0 and r == 0:
                                hstate[h] = ([ps_acc(), ps_acc()], [ps_acc(), ps_acc()])
                            accO, accD = hstate[h]
                            kb.op("pe", lambda e: e.matmul(accO[r][:, 0:qblk], lhsT=vaug[:, s_ * nkt + kt, h, :], rhs=pT[:, 0:qblk], start=(kt == 0), stop=last),
                                  reads=[pT, vaug], writes=[accO[r]], inc=False)
                            kb.op("pe", lambda e: e.matmul(accD[r][:, 0:qblk], lhsT=ones_b, rhs=pT[:, 0:qblk], start=(kt == 0), stop=last),
                                  reads=[pT, cb], writes=[accD[r]], inc=True)
                            if last and r == 1:
                                A, Bt, O = rs, t1, osb
                                kb.op("act", lambda e: e.activation(out=A[:, 0:qblk], in_=accD[0][:, 0:qblk], func=AF.Ln), reads=[accD[0]], writes=[A])
                                kb.op("act", lambda e: e.activation(out=A[:, 0:qblk], in_=A[:, 0:qblk], func=AF.Exp, scale=-1.0), reads=[A], writes=[A])
                                kb.op("act", lambda e: e.activation(out=Bt[:, 0:qblk], in_=accD[1][:, 0:qblk], func=AF.Ln), reads=[accD[1]], writes=[Bt])
                                kb.op("act", lambda e: e.activation(out=Bt[:, 0:qblk], in_=Bt[:, 0:qblk], func=AF.Exp, scale=-1.0), reads=[Bt], writes=[Bt])
                                kb.op("dve", lambda e: e.tensor_tensor(out=O[:, 0:qblk], in0=accO[0][:, 0:qblk], in1=A[:, 0:qblk], op=ALU.mult), reads=[accO[0], A], writes=[O])
                                kb.op("dve", lambda e: e.tensor_tensor(out=Bt[:, 0:qblk], in0=accO[1][:, 0:qblk], in1=Bt[:, 0:qblk], op=ALU.mult), reads=[accO[1], Bt], writes=[Bt])
                                kb.op("dve", lambda e: e.scalar_tensor_tensor(out=O[:, 0:qblk], in0=Bt[:, 0:qblk], scalar=sm[:, 1:2], in1=O[:, 0:qblk], op0=ALU.mult, op1=ALU.add),
                                      reads=[Bt, sm, O], writes=[O])
                                kb.op("act", lambda e: e.activation(out=sqb[:, 0:qblk], in_=O[:, 0:qblk], func=AF.Square), reads=[O], writes=[sqb])
                                pn = ps()
                                kb.op("pe", lambda e: e.matmul(pn[:, 0:qblk], lhsT=ones_b, rhs=sqb[:, 0:qblk], start=True, stop=True), reads=[cb, sqb], writes=[pn])
                                kb.op("dve", lambda e: e.tensor_scalar(out=A[:, 0:qblk], in0=pn[:, 0:qblk], scalar1=1.0 / 128, scalar2=EPS, op0=ALU.mult, op1=ALU.add), reads=[pn], writes=[A])
                                kb.op("act", lambda e: e.activation(out=A[:, 0:qblk], in_=A[:, 0:qblk], func=AF.Ln), reads=[A], writes=[A])
                                kb.op("act", lambda e: e.activation(out=A[:, 0:qblk], in_=A[:, 0:qblk], func=AF.Exp, scale=-0.5), reads=[A], writes=[A])
                                kb.op("dve", lambda e: e.scalar_tensor_tensor(out=yT[:, h, yc], in0=O[:, 0:qblk], scalar=1.0 - lam_init, in1=A[:, 0:qblk], op0=ALU.mult, op1=ALU.mult),
                                      reads=[O, A], writes=[yT])

                    n_pairs = len(its) // 2
                    for j in range(n_pairs + 1):
                        if j < n_pairs:
                            stageA2(j)
                        if j >= 1:
                            stageC(2 * (j - 1))
                            stageC(2 * (j - 1) + 1)
                    if (q0 + qblk) % 512 == 0:
                        tb = (q0 + qblk) // 512 - 1
                        for c in range(8):
                            p = ps()
                            for k in range(4):
                                kb.op("pe", lambda e: e.matmul(p[:, :], lhsT=wo_buf[:, k, c * 128:(c + 1) * 128], rhs=yT[:, k, :],
                                                               start=(k == 0), stop=(k == 3)), reads=[wo_buf, yT], writes=[p], inc=(k == 3))
                            kb.op("dve", lambda e: e.scalar_tensor_tensor(out=xT[tb][:, c, :], in0=p[:, :], scalar=modc[:, 2, c:c + 1], in1=xT[tb][:, c, :],
                                                                          op0=ALU.mult, op1=ALU.add), reads=[p, modc, xT[tb]], writes=[xT[tb]])
        kb.barrier()

    triF_f = cf.t[:, 256:384]
    triB_f = cf.t[:, 384:512]
    maskF_b = cb.t[:, 512:640]
    maskB_b = cb.t[:, 640:768]

    def proj_fm(l, c0, ncol, blocks, fn):
        for t0 in range(0, ncol, 512):
            n = min(512, ncol - t0)
            wb, wv = wload([(D["w_in"][l][:, c0 + t0:c0 + t0 + n], n)], 8)
            for s0 in range(0, n, 128):
                w = min(128, n - s0)
                for tb in blocks:
                    p = ps()
                    for k in range(8):
                        kb.op("pe", lambda e: e.matmul(p[0:w, :], lhsT=wv[:, k, s0:s0 + w], rhs=hT[tb][:, k, :], start=(k == 0), stop=(k == 7)),
                              reads=[wb, hT[tb]], writes=[p], inc=(k == 7))
                    fn((t0 + s0) // 128, tb, p, w)

    def conv_chunk(convin, acc, cwt, cbt, cc, nseq, L, dst_ap_fn, dst_buf):
        for s_ in range(nseq):
            a = acc[:, s_ * L:(s_ + 1) * L]
            kb.op("dve", lambda e: e.tensor_scalar(out=a, in0=convin[:, s_, 0:L], scalar1=cwt[:, cc, 0:1], scalar2=None, op0=ALU.mult),
                  reads=[convin, cwt], writes=[acc])
            for tap in range(1, 5):
                kb.op("dve", lambda e: e.scalar_tensor_tensor(out=a, in0=convin[:, s_, tap:tap + L], scalar=cwt[:, cc, tap:tap + 1], in1=a,
                                                              op0=ALU.mult, op1=ALU.add), reads=[convin, cwt, acc], writes=[acc])
            if isinstance(dst_buf, list):
                for gg in range(2):
                    kb.op("act", lambda e: e.activation(out=dst_buf[gg][gg * 64:(gg + 1) * 64, s_ * L:(s_ + 1) * L], in_=acc[gg * 64:(gg + 1) * 64, s_ * L:(s_ + 1) * L],
                                                        func=AF.Silu, bias=cbt[gg * 64:(gg + 1) * 64, cc:cc + 1]), reads=[acc, cbt], writes=[dst_buf[gg]])
            else:
                kb.op("act", lambda e: e.activation(out=dst_ap_fn(s_), in_=a, func=AF.Silu, bias=cbt[:, cc:cc + 1]), reads=[acc, cbt], writes=[dst_buf])

    def evac_conv_in(convin, p, tb, nseq, L):
        if nseq == 1:
            kb.op("act", lambda e: e.activation(out=convin[:, 0, 2 + tb * 512:2 + (tb + 1) * 512], in_=p[:, :], func=AF.Copy), reads=[p], writes=[convin])
        else:
            for s_ in range(2):
                kb.op("act", lambda e: e.activation(out=convin[:, s_, 2:2 + L], in_=p[:, s_ * L:(s_ + 1) * L], func=AF.Copy), reads=[p], writes=[convin])

    def transpose_to_tm(src, dst_fn, dst_buf, T):
        for t0 in range(0, T, 4):
            p = ps()
            for j in range(4):
                kb.op("pe", lambda e: e.matmul(p[:, j * 128:(j + 1) * 128], lhsT=src[:, (t0 + j) * 128:(t0 + j + 1) * 128], rhs=ident_b, start=True, stop=True),
                      reads=[src, cb], writes=[p], inc=(j == 3))
            kb.op("act", lambda e: e.activation(out=dst_fn(t0, 4), in_=p[:, :].rearrange("p (t c) -> p t c", t=4), func=AF.Copy), reads=[p], writes=[dst_buf])

    def ssd(l, g, nblk):
        kb.label = 'ssd_g%d' % g
        sample = (g == 1)
        L = 2048 if sample else 256
        nseq = 1 if sample else 2
        lt = L // 128
        ntok = nblk * 512
        T = ntok // 128
        blocks = list(range(nblk))
        with scope() as st:
            xtm = alloc(st, "xtm", [128, T, 512], BF16)
            Btm = alloc(st, "Btm", [128, T, 128], BF16)
            BT = alloc(st, "BT", [128, ntok], BF16)
            CTz = [alloc(st, "CT%d" % i, [128, ntok], BF16) for i in range(2)]
            for i_ in range(2):
                kb.op("dve", lambda e: e.memset(CTz[i_][:], 0.0), writes=[CTz[i_]])
            dtt = alloc(st, "dtt", [128, T, 16], F32)
            dtA = alloc(st, "dtA", [128, T, 16], F32)
            cum = alloc(st, "cum", [128, T, 16], F32)
            tot = alloc(st, "tot", [128, T, 16], F32)
            ecum = alloc(st, "ecum", [128, T, 16], F32)
            wd = alloc(st, "wd", [128, T, 16], F32)
            bj = alloc(st, "bj", [128, T, 16], F32)
            dec = alloc(st, "dec", [128, T, 2, 4], F32)
            abc = alloc(st, "abc", [128, 16], F32)
            dtb = alloc(st, "dtb", [128, 16], F32)
            dsk = alloc(st, "dsk", [128, 8], F32)
            ng = alloc(st, "ng", [128, 512], F32)
            cwt = alloc(st, "cwt", [128, 6, 5], F32)
            cbt = alloc(st, "cbt", [128, 6], F32)
            load_wo(l, 0)
            kb.dma("sp", abc[:], D["ssd_a_log"][l].partition_broadcast(128), writes=[abc])
            kb.dma("sp", dtb[:], D["ssd_dt_bias"][l].partition_broadcast(128), writes=[dtb])
            kb.dma("sp", dsk[:], D["ssd_d"][l].partition_broadcast(128), writes=[dsk])
            kb.dma("sp", ng[:], D["ssd_norm"][l].partition_broadcast(128), writes=[ng])
            for tap in range(5):
                kb.dma("sp", cwt[:, :, tap], D["conv_ssd_w"][l, tap].rearrange("(c p) -> p c", p=128), writes=[cwt], allow_slow_non_contiguous=True)
            kb.dma("sp", cbt[:], D["conv_ssd_b"][l].rearrange("(c p) -> p c", p=128), writes=[cbt], allow_slow_non_contiguous=True)
            kb.op("act", lambda e: e.activation(out=abc[:], in_=abc[:], func=AF.Exp), reads=[abc], writes=[abc])
            kb.op("dve", lambda e: e.tensor_scalar(out=abc[:], in0=abc[:], scalar1=-1.0, scalar2=None, op0=ALU.mult), reads=[abc], writes=[abc])
            with scope() as st1:
                convins = [alloc(st1, "convin", [128, nseq, L + 4], F32) for _ in range(2)]
                acc = alloc(st1, "cacc", [128, ntok], F32)
                xcT = alloc(st1, "xcT", [128, ntok], BF16)
                for cv_ in convins:
                    kb.op("dve", lambda e: e.memset(cv_[:], 0.0), writes=[cv_])

                def cb_fn(ci, tb, p, w):
                    convin = convins[ci % 2]
                    evac_conv_in(convin, p, tb, nseq, L)
                    if tb != blocks[-1]:
                        return
                    if ci < 4:
                        conv_chunk(convin, acc, cwt, cbt, ci, nseq, L, lambda s_: xcT[:, s_ * L:(s_ + 1) * L], xcT)
                        transpose_to_tm(xcT, lambda t0, n: xtm[:, t0:t0 + n, ci * 128:(ci + 1) * 128], xtm, T)
                    elif ci == 4:
                        conv_chunk(convin, acc, cwt, cbt, ci, nseq, L, lambda s_: BT[:, s_ * L:(s_ + 1) * L], BT)
                        transpose_to_tm(BT, lambda t0, n: Btm[:, t0:t0 + n, :], Btm, T)
                    else:
                        conv_chunk(convin, acc, cwt, cbt, ci, nseq, L, None, CTz)
                proj_fm(l, 512, 768, blocks, cb_fn)
            kb.barrier()
            if SSD_PH < 2:
                return
            wb, wv = wload([(D["w_in"][l][:, 1280:1296], 16)], 8)
            for tt in range(T):
                p = ps()
                proj_tm(wb, wv, 0, 16, tt // 4, tt % 4, p)
                kb.op("dve", lambda e: e.tensor_tensor(out=dtt[:, tt, :], in0=p[:, 0:16], in1=dtb[:], op=ALU.add), reads=[p, dtb], writes=[dtt])
            kb.op("act", lambda e: e.activation(out=dtt[:], in_=dtt[:], func=AF.Exp), reads=[dtt], writes=[dtt])
            kb.op("act", lambda e: e.activation(out=dtt[:], in_=dtt[:], func=AF.Ln, bias=1.0), reads=[dtt], writes=[dtt])
            kb.op("dve", lambda e: e.tensor_tensor(out=dtA[:], in0=dtt[:], in1=bcast(abc[:], [128, T, 16], 1), op=ALU.mult), reads=[dtt, abc], writes=[dtA])
            for tt in range(T):
                p = ps()
                kb.op("pe", lambda e: e.matmul(p[:, 0:16], lhsT=triF_f, rhs=dtA[:, tt, :], start=True, stop=True), reads=[cf, dtA], writes=[p], inc=False)
                kb.op("pe", lambda e: e.matmul(p[:, 16:32], lhsT=triB_f, rhs=dtA[:, tt, :], start=True, stop=True), reads=[cf, dtA], writes=[p], inc=False)
                kb.op("pe", lambda e: e.matmul(p[:, 32:48], lhsT=ones_f, rhs=dtA[:, tt, :], start=True, stop=True), reads=[cf, dtA], writes=[p])
                kb.op("dve", lambda e: e.tensor_copy(out=cum[:, tt, 0:8], in_=p[:, 0:8]), reads=[p], writes=[cum])
                kb.op("dve", lambda e: e.tensor_copy(out=cum[:, tt, 8:16], in_=p[:, 24:32]), reads=[p], writes=[cum])
                kb.op("dve", lambda e: e.tensor_copy(out=tot[:, tt, :], in_=p[:, 32:48]), reads=[p], writes=[tot])
            kb.op("act", lambda e: e.activation(out=ecum[:], in_=cum[:], func=AF.Exp), reads=[cum], writes=[ecum])
            kb.op("dve", lambda e: e.tensor_tensor(out=wd[:], in0=tot[:], in1=cum[:], op=ALU.subtract), reads=[tot, cum], writes=[wd])
            kb.op("act", lambda e: e.activation(out=wd[:], in_=wd[:], func=AF.Exp), reads=[wd], writes=[wd])
            kb.op("dve", lambda e: e.tensor_tensor(out=wd[:], in0=wd[:], in1=dtt[:], op=ALU.mult), reads=[wd, dtt], writes=[wd])
            kb.op("act", lambda e: e.activation(out=bj[:], in_=dtt[:], func=AF.Ln), reads=[dtt], writes=[bj])
            kb.op("dve", lambda e: e.tensor_tensor(out=bj[:], in0=bj[:], in1=cum[:], op=ALU.subtract), reads=[bj, cum], writes=[bj])
            tot4 = tot[:].rearrange("p t (d h) -> p t d h", d=2)
            for gg in range(2):
                kb.op("act", lambda e: e.activation(out=dec[gg * 64:(gg + 1) * 64], in_=tot4[gg * 64:(gg + 1) * 64, :, :, gg * 4:(gg + 1) * 4], func=AF.Exp),
                      reads=[tot], writes=[dec])
            if SSD_PH < 3:
                kb.barrier()
                return
            with scope() as st2:
                Hprev = alloc(st2, "Hprev", [128, T, 2, 256], BF16)
                st3 = ExitStack()
                Hs = [alloc(st3, "Hs%d" % i, [128, 256], F32) for i in range(2)]
                xws = [alloc(st3, "xw%d" % i, [128, 512], BF16) for i in range(2)]
                hx = alloc(st3, "hx", [128, 2, 128], F32)
                ho = alloc(st3, "ho", [128, 128], F32)
                xi = 0
                for dr in range(2):
                    H = Hs[dr]
                    for s_ in range(nseq):
                        if sample:
                            for blk in range(2):
                                for two in range(2):
                                    kb.dma("sp", hx[two * 64:(two + 1) * 64, blk, :].rearrange("p (g n) -> p g n", g=2),
                                           D["sssm"][l, dr].rearrange("(g r) p n -> r p g n", g=2)[blk * 2 + two], writes=[hx])
                            for blk in range(2):
                                p = ps()
                                kb.op("pe", lambda e: e.matmul(p[:, 0:128], lhsT=hx[:, blk, :], rhs=ident_f, start=True, stop=True), reads=[hx, cf], writes=[p])
                                kb.op("act", lambda e: e.activation(out=H[:, blk * 128:(blk + 1) * 128], in_=p[:, 0:128], func=AF.Copy), reads=[p], writes=[H])
                        else:
                            kb.op("dve", lambda e: e.memset(H[:], 0.0), writes=[H])
                        order = range(lt) if dr == 0 else range(lt - 1, -1, -1)
                        for r0 in order:
                            tt = s_ * lt + r0
                            kb.op("act", lambda e: e.activation(out=Hprev[:, tt, dr, :], in_=H[:], func=AF.Copy), reads=[H], writes=[Hprev])
                            xw = xws[xi % 2]
                            xi += 1
                            kb.op("dve", lambda e: e.tensor_tensor(out=xw[:].rearrange("p (h d) -> p h d", d=64), in0=xtm[:, tt, :].rearrange("p (h d) -> p h d", d=64),
                                                                   in1=bcast(wd[:, tt, dr * 8:(dr + 1) * 8], [128, 8, 64], 2), op=ALU.mult), reads=[xtm, wd], writes=[xw])
                            p = ps()
                            kb.op("pe", lambda e: e.matmul(p[:, :], lhsT=Btm[:, tt, :], rhs=xw[:], start=True, stop=True), reads=[Btm, xw], writes=[p])
                            kb.op("dve", lambda e: e.tensor_tensor(out=H[:].rearrange("p (h d) -> p h d", d=64), in0=H[:].rearrange("p (h d) -> p h d", d=64),
                                                                   in1=bcast(dec[:, tt, dr, :], [128, 4, 64], 2), op=ALU.mult), reads=[H, dec], writes=[H])
                            for gg in range(2):
                                kb.op("dve", lambda e: e.tensor_tensor(out=H[gg * 64:(gg + 1) * 64, :], in0=H[gg * 64:(gg + 1) * 64, :],
                                                                       in1=p[gg * 64:(gg + 1) * 64, gg * 256:(gg + 1) * 256], op=ALU.add), reads=[H, p], writes=[H])
                        if not sample:
                            for blk in range(2):
                                p = ps()
                                kb.op("pe", lambda e: e.matmul(p[:, 0:128], lhsT=H[:, blk * 128:(blk + 1) * 128], rhs=ident_f, start=True, stop=True), reads=[H, cf], writes=[p])
                                kb.op("act", lambda e: e.activation(out=ho[:], in_=p[:, 0:128], func=AF.Copy), reads=[p], writes=[ho])
                                for two in range(2):
                                    kb.dma("sp", D["nssm"][s_, l, dr].rearrange("(g r) p n -> r p g n", g=2)[blk * 2 + two],
                                           ho[two * 64:(two + 1) * 64, :].rearrange("p (g n) -> p g n", g=2), reads=[ho])
                kb.barrier()
                st3.close()
                if SSD_PH < 4:
                    return
                Dg = alloc(st2, "Dg", [128, 16, 128], F32)
                Es = [alloc(st2, "E%d" % i, [128, 128], F32) for i in range(3)]
                Ms = [alloc(st2, "M%d" % i, [128, 128], BF16) for i in range(3)]
                ya = alloc(st2, "ya", [128, 512], F32)
                yu = alloc(st2, "yu", [128, 512], F32)
                zs = yu
                ytm = alloc(st2, "ytm", [128, 4, 512], BF16)
                yT = alloc(st2, "yT", [128, 4, 512], BF16)
                ss = alloc(st2, "ss", [128, 4], F32)
                wzb, wzv = wload([(D["w_in"][l][:, 0:512], 512)], 8)
                ei = 0
                for tt in range(T):
                    tk = slice(tt * 128, (tt + 1) * 128)
                    pz = ps_acc()
                    proj_tm(wzb, wzv, 0, 512, tt // 4, tt % 4, pz)
                    pBC = ps_acc()
                    for gg in range(2):
                        kb.op("pe", lambda e: e.matmul(pBC[:, gg * 128:(gg + 1) * 128], lhsT=BT[:, tk], rhs=CTz[gg][:, tk], start=True, stop=True),
                              reads=[BT, CTz[gg]], writes=[pBC], inc=(gg == 1))
                    kb.op("dve", lambda e: e.tensor_tensor(out=Dg[:], in0=bcast(ident_f, [128, 16, 128], 1), in1=bcast(cum[:, tt, :], [128, 16, 128], 2), op=ALU.mult),
                          reads=[cf, cum], writes=[Dg])
                    yint = ps_acc()
                    hd = [(h, dr) for h in range(8) for dr in range(2)]
                    mts = {}

                    def sA(i):
                        h, dr = hd[i]
                        pE = ps()
                        kb.op("pe", lambda e: e.matmul(pE[:, 0:128], lhsT=ones_f, rhs=Dg[:, dr * 8 + h, :], start=True, stop=False), reads=[cf, Dg], writes=[pE], inc=False)
                        kb.op("pe", lambda e: e.matmul(pE[:, 0:128], lhsT=ident_b, rhs=(maskF_b if dr == 0 else maskB_b), start=False, stop=True), reads=[cb], writes=[pE])
                        E = Es[i % 3]
                        M = Ms[i % 3]
                        kb.op("act", lambda e: e.activation(out=E[:], in_=pE[:, 0:128], func=AF.Exp, bias=bj[:, tt, dr * 8 + h:dr * 8 + h + 1]), reads=[pE, bj], writes=[E])
                        gg = h // 4
                        kb.op("dve", lambda e: e.tensor_tensor(out=M[:], in0=E[:], in1=pBC[:, gg * 128:(gg + 1) * 128], op=ALU.mult), reads=[E, pBC], writes=[M])
                        mts[i] = M

                    def sC(i):
                        h, dr = hd[i]
                        M = mts.pop(i)
                        kb.op("pe", lambda e: e.matmul(yint[:, h * 64:(h + 1) * 64], lhsT=M[:], rhs=xtm[:, tt, h * 64:(h + 1) * 64], start=(dr == 0), stop=(dr == 1)),
                              reads=[M, xtm], writes=[yint], inc=True)

                    for i in range(16 + 2):
                        if i < 16:
                            sA(i)
                        if i >= 2:
                            sC(i - 2)
                    if SSD_PH < 5:
                        continue
                    pY = [ps(), ps()]
                    for dr in range(2):
                        for gg in range(2):
                            kb.op("pe", lambda e: e.matmul(pY[dr][:, gg * 256:(gg + 1) * 256], lhsT=CTz[gg][:, tk], rhs=Hprev[:, tt, dr, :], start=True, stop=True),
                                  reads=[CTz[gg], Hprev], writes=[pY[dr]], inc=(gg == 1))
                    v3 = lambda ap: ap.rearrange("p (h d) -> p h d", d=64)
                    kb.op("dve", lambda e: e.tensor_tensor(out=v3(ya[:]), in0=v3(xtm[:, tt, :]), in1=bcast(dsk[:], [128, 8, 64], 2), op=ALU.mult), reads=[xtm, dsk], writes=[ya])
                    kb.op("dve", lambda e: e.tensor_tensor(out=ya[:], in0=ya[:], in1=yint[:, :], op=ALU.add), reads=[ya, yint], writes=[ya])
                    for dr in range(2):
                        kb.op("dve", lambda e: e.tensor_tensor(out=v3(yu[:]), in0=v3(pY[dr][:, :]), in1=bcast(ecum[:, tt, dr * 8:(dr + 1) * 8], [128, 8, 64], 2), op=ALU.mult),
                              reads=[pY[dr], ecum], writes=[yu])
                        kb.op("dve", lambda e: e.tensor_tensor(out=ya[:], in0=ya[:], in1=yu[:], op=ALU.add), reads=[ya, yu], writes=[ya])
                    if SSD_PH < 6:
                        continue
                    kb.op("act", lambda e: e.activation(out=zs[:], in_=pz[:, :], func=AF.Silu), reads=[pz], writes=[zs])
                    kb.op("dve", lambda e: e.tensor_tensor(out=ya[:], in0=ya[:], in1=zs[:], op=ALU.mult), reads=[ya, zs], writes=[ya])
                    kb.op("dve", lambda e: e.memset(ss[:, 0:1], 0.0), writes=[ss])
                    kb.op("act", lambda e: e.activation(out=yu[:], in_=ya[:], func=AF.Square, accum_out=ss[:, 0:1]), reads=[ya, ss], writes=[yu, ss])
                    kb.op("dve", lambda e: e.tensor_scalar(out=ss[:, 1:2], in0=ss[:, 0:1], scalar1=1.0 / 512, scalar2=EPS, op0=ALU.mult, op1=ALU.add), reads=[ss], writes=[ss])
                    kb.op("act", lambda e: e.activation(out=ss[:, 1:2], in_=ss[:, 1:2], func=AF.Ln), reads=[ss], writes=[ss])
                    kb.op("act", lambda e: e.activation(out=ss[:, 2:3], in_=ss[:, 1:2], func=AF.Exp, scale=-0.5), reads=[ss], writes=[ss])
                    kb.op("dve", lambda e: e.scalar_tensor_tensor(out=ytm[:, tt % 4, :], in0=ya[:], scalar=ss[:, 2:3], in1=ng[:], op0=ALU.mult, op1=ALU.mult),
                          reads=[ya, ss, ng], writes=[ytm])
                    if SSD_PH < 7:
                        continue
                    if tt % 4 == 3:
                        mixer_out(st2, ytm, tt // 4, yT)
        kb.barrier()


    def mlstm(l, g, nblk):
        kb.label = 'mlstm_g%d' % g
        sample = (g == 1)
        L = 2048 if sample else 256
        nseq = 1 if sample else 2
        lt = L // 128
        ntok = nblk * 512
        T = ntok // 128
        blocks = list(range(nblk))
        C0 = 2832
        lns = math.log(128 ** -0.5)
        for hg in range(2):
            h0 = hg * 2
            with scope() as st:
                qT = alloc(st, "mqT", [128, 2, ntok], BF16)
                kT = alloc(st, "mkT", [128, 2, ntok], BF16)
                vaug = alloc(st, "mvaug", [128, T, 2, 129], BF16)
                Cpb = alloc(st, "Cpb", [128, T, 2, 129], BF16)
                li = alloc(st, "li", [128, T, 4], F32)
                lf = alloc(st, "lf", [128, T, 4], F32)
                G = alloc(st, "G", [128, T, 4], F32)
                tot = alloc(st, "mtot", [128, T, 4], F32)
                pj = alloc(st, "pj", [128, T, 4], F32)
                pjs = alloc(st, "pjs", [128, T, 4], F32)
                gend = alloc(st, "gend", [128, T, 4], F32)
                mlb = alloc(st, "mlb", [128, T, 4], F32)
                wend = alloc(st, "wend", [128, T, 4], F32)
                mprev = alloc(st, "mprev", [128, T, 4], F32)
                gb = alloc(st, "gb", [128, 8], F32)
                cwt = alloc(st, "mcwt", [128, 4, 5], F32)
                cbt = alloc(st, "mcbt", [128, 4], F32)
                ng = alloc(st, "mng", [128, 256], F32)
                kb.dma("pool", wo_buf[:, 0:2, :], D["w_out"][l][1024 + h0 * 128:1024 + (h0 + 2) * 128, :].rearrange("(k p) n -> p k n", p=128), writes=[wo_buf])
                kb.dma("sp", ng[:], D["mlstm_norm"][l][h0 * 128:(h0 + 2) * 128].partition_broadcast(128), writes=[ng])
                goffs = [0 * 8 + 0 * 4 + h0, 1 * 8 + 0 * 4 + h0, 0 * 8 + 1 * 4 + h0, 1 * 8 + 1 * 4 + h0]
                for i_, go in enumerate(goffs):
                    kb.dma("sp", gb[:, i_ * 2:(i_ + 1) * 2], D["mlstm_gate_b"][l][go:go + 2].partition_broadcast(128), writes=[gb])
                for ci, ch0 in enumerate((h0 * 128, (h0 + 1) * 128, 512 + h0 * 128, 512 + (h0 + 1) * 128)):
                    for tap in range(5):
                        kb.dma("sp", cwt[:, ci, tap:tap + 1], D["conv_mlstm_w"][l, tap][ch0:ch0 + 128].rearrange("(p o) -> p o", o=1), writes=[cwt])
                    kb.dma("sp", cbt[:, ci:ci + 1], D["conv_mlstm_b"][l][ch0:ch0 + 128].rearrange("(p o) -> p o", o=1), writes=[cbt])
                kb.op("dve", lambda e: e.memset(vaug[:, :, :, 128:129], 1.0), writes=[vaug])
                with scope() as st1:
                    convins = [alloc(st1, "mconvin", [128, nseq, L + 4], F32) for _ in range(2)]
                    acc = alloc(st1, "mcacc", [128, ntok], F32)
                    for cv_ in convins:
                        kb.op("dve", lambda e: e.memset(cv_[:], 0.0), writes=[cv_])
                    wb, wv = wload([(D["w_in"][l][:, C0 + h0 * 128:C0 + (h0 + 2) * 128], 256),
                                    (D["w_in"][l][:, C0 + 512 + h0 * 128:C0 + 512 + (h0 + 2) * 128], 256)], 8)
                    for ci in range(4):
                        for tb in blocks:
                            p = ps()
                            for k in range(8):
                                kb.op("pe", lambda e: e.matmul(p[:, :], lhsT=wv[:, k, ci * 128:(ci + 1) * 128], rhs=hT[tb][:, k, :], start=(k == 0), stop=(k == 7)),
                                      reads=[wb, hT[tb]], writes=[p], inc=(k == 7))
                            evac_conv_in(convins[ci % 2], p, tb, nseq, L)
                        convin = convins[ci % 2]
                        dstb = qT if ci < 2 else kT
                        conv_chunk(convin, acc, cwt, cbt, ci, nseq, L, lambda s_: dstb[:, ci % 2, s_ * L:(s_ + 1) * L], dstb)
                kb.barrier()
                wb, wv = wload([(D["w_in"][l][:, C0 + 1024 + h0 * 128:C0 + 1024 + (h0 + 2) * 128], 256)], 8)
                for tt in range(T):
                    p = ps()
                    proj_tm(wb, wv, 0, 256, tt // 4, tt % 4, p)
                    kb.op("act", lambda e: e.activation(out=vaug[:, tt, :, 0:128], in_=p[:, 0:256].rearrange("p (h e) -> p h e", e=128), func=AF.Copy), reads=[p], writes=[vaug])
                gc = C0 + 2048
                wb, wv = wload([(D["w_in"][l][:, gc + go:gc + go + 2], 2) for go in goffs], 8)
                for tt in range(T):
                    p = ps()
                    proj_tm(wb, wv, 0, 8, tt // 4, tt % 4, p)
                    kb.op("dve", lambda e: e.tensor_tensor(out=li[:, tt, :], in0=p[:, 0:4], in1=gb[:, 0:4], op=ALU.add), reads=[p, gb], writes=[li])
                    kb.op("dve", lambda e: e.tensor_tensor(out=lf[:, tt, :], in0=p[:, 4:8], in1=gb[:, 4:8], op=ALU.add), reads=[p, gb], writes=[lf])
                kb.op("act", lambda e: e.activation(out=lf[:], in_=lf[:], func=AF.Exp, scale=-1.0), reads=[lf], writes=[lf])
                kb.op("act", lambda e: e.activation(out=lf[:], in_=lf[:], func=AF.Ln, bias=1.0), reads=[lf], writes=[lf])
                kb.op("dve", lambda e: e.tensor_scalar(out=lf[:], in0=lf[:], scalar1=-1.0, scalar2=None, op0=ALU.mult), reads=[lf], writes=[lf])
                for tt in range(T):
                    p = ps()
                    kb.op("pe", lambda e: e.matmul(p[:, 0:4], lhsT=triF_f, rhs=lf[:, tt, :], start=True, stop=True), reads=[cf, lf], writes=[p], inc=False)
                    kb.op("pe", lambda e: e.matmul(p[:, 4:8], lhsT=triB_f, rhs=lf[:, tt, :], start=True, stop=True), reads=[cf, lf], writes=[p], inc=False)
                    kb.op("pe", lambda e: e.matmul(p[:, 8:12], lhsT=ones_f, rhs=lf[:, tt, :], start=True, stop=True), reads=[cf, lf], writes=[p])
                    kb.op("dve", lambda e: e.tensor_copy(out=G[:, tt, 0:2], in_=p[:, 0:2]), reads=[p], writes=[G])
                    kb.op("dve", lambda e: e.tensor_copy(out=G[:, tt, 2:4], in_=p[:, 6:8]), reads=[p], writes=[G])
                    kb.op("dve", lambda e: e.tensor_copy(out=tot[:, tt, :], in_=p[:, 8:12]), reads=[p], writes=[tot])
                kb.op("dve", lambda e: e.tensor_tensor(out=pj[:], in0=li[:], in1=G[:], op=ALU.subtract), reads=[li, G], writes=[pj])
                kb.op("dve", lambda e: e.tensor_scalar(out=pjs[:], in0=pj[:], scalar1=lns, scalar2=None, op0=ALU.add), reads=[pj], writes=[pjs])
                kb.op("dve", lambda e: e.tensor_tensor(out=gend[:], in0=pj[:], in1=tot[:], op=ALU.add), reads=[pj, tot], writes=[gend])
                with scope() as stt:
                    mrow = alloc(stt, "mrow8", [4, 1], F32)
                    d8 = alloc(stt, "d8", [4, 4], F32)
                    for tt in range(T):
                        p = ps()
                        kb.op("pe", lambda e: e.matmul(p[0:4, 0:128], lhsT=gend[:, tt, :], rhs=ident_f, start=True, stop=True), reads=[gend, cf], writes=[p])
                        kb.op("dve", lambda e: e.tensor_reduce(out=mrow[:], in_=p[0:4, 0:128], axis=AX.X, op=ALU.max), reads=[p], writes=[mrow])
                        kb.op("dve", lambda e: e.tensor_scalar(out=d8[:], in0=ident_f[0:4, 0:4], scalar1=mrow[:, 0:1], scalar2=None, op0=ALU.mult), reads=[cf, mrow], writes=[d8])
                        p2 = ps()
                        kb.op("pe", lambda e: e.matmul(p2[:, 0:4], lhsT=ones_f[0:4, :], rhs=d8[:], start=True, stop=True), reads=[cf, d8], writes=[p2])
                        kb.op("dve", lambda e: e.tensor_copy(out=mlb[:, tt, :], in_=p2[:, 0:4]), reads=[p2], writes=[mlb])
                    kb.barrier()
                kb.op("dve", lambda e: e.tensor_tensor(out=wend[:], in0=gend[:], in1=mlb[:], op=ALU.subtract), reads=[gend, mlb], writes=[wend])
                kb.op("act", lambda e: e.activation(out=wend[:], in_=wend[:], func=AF.Exp), reads=[wend], writes=[wend])
                with scope() as st2:
                    Cst = [alloc(st2, "Cst%d" % i, [128, 129], F32) for i in range(4)]
                    mp = alloc(st2, "mp", [128, 4], F32)
                    mt8 = alloc(st2, "mt8", [128, 16], F32)
                    kwts = [alloc(st2, "kwt", [128, 2, 128], BF16) for _ in range(2)]
                    Dgs = [alloc(st2, "mDg", [128, 4, 128], F32) for _ in range(2)]
                    Drs = [alloc(st2, "mDr", [128, 4, 128], F32) for _ in range(2)]
                    scs = [alloc(st2, "msc", [128, 40], F32) for _ in range(2)]
                    kwt, Dg, Dr, sc = kwts[0], Dgs[0], Drs[0], scs[0]
                    Es = [alloc(st2, "mE%d" % i, [128, 128], F32) for i in range(3)]
                    Ms = [alloc(st2, "mM%d" % i, [128, 128], BF16) for i in range(3)]
                    nds = [alloc(st2, "nd%d" % i, [128, 129], F32) for i in range(2)]
                    cbf = [alloc(st2, "cbf%d" % i, [128, 129], BF16) for i in range(2)]
                    hsums = [alloc(st2, "hsum", [128, 256], F32) for _ in range(2)]
                    hts = [alloc(st2, "mht", [128, 256], F32) for _ in range(2)]
                    sgs = [alloc(st2, "msg", [128, 256], F32) for _ in range(2)]
                    hsum, ht, sg = hsums[0], hts[0], sgs[0]
                    ytm = alloc(st2, "mytm", [128, 4, 256], BF16)
                    yT = alloc(st2, "myT", [128, 2, 512], BF16)
                    ei = 0
                    ei0 = [0]

                    def init_state(dr, s_):
                        for hh in range(2):
                            C = Cst[dr * 2 + hh]
                            if sample:
                                kb.dma("sp", C[:, 0:128], D["smc"][l, dr, h0 + hh], writes=[C])
                                kb.dma("sp", C[:, 128:129], D["smn"][l, dr, h0 + hh].rearrange("(p o) -> p o", o=1), writes=[C])
                            else:
                                kb.op("dve", lambda e: e.memset(C[:], 0.0), writes=[C])
                        if sample:
                            kb.dma("sp", mp[:, dr * 2:dr * 2 + 2], D["smm"][l][dr * 4 + h0:dr * 4 + h0 + 2].partition_broadcast(128), writes=[mp])
                        else:
                            kb.op("dve", lambda e: e.memset(mp[:, dr * 2:dr * 2 + 2], 0.0), writes=[mp])

                    def local_update(dr, tt):
                        cs = slice(dr * 2, dr * 2 + 2)
                        pk = ps()
                        for hh in range(2):
                            kb.op("pe", lambda e: e.matmul(pk[:, hh * 128:(hh + 1) * 128], lhsT=kT[:, hh, tt * 128:(tt + 1) * 128], rhs=ident_b, start=True, stop=True),
                                  reads=[kT, cb], writes=[pk], inc=(hh == 1))
                        kb.op("dve", lambda e: e.tensor_tensor(out=kwt[:], in0=pk[:, 0:256].rearrange("p (h d) -> p h d", d=128),
                                                               in1=bcast(wend[:, tt, cs], [128, 2, 128], 2), op=ALU.mult), reads=[pk, wend], writes=[kwt])
                        a = sc[:, 0:2]
                        mn = sc[:, 2:4]
                        sp_ = sc[:, 4:6]
                        sl_ = sc[:, 6:8]
                        kb.op("dve", lambda e: e.tensor_tensor(out=a, in0=tot[:, tt, cs], in1=mp[:, cs], op=ALU.add), reads=[tot, mp], writes=[sc])
                        kb.op("dve", lambda e: e.tensor_tensor(out=mn, in0=a, in1=mlb[:, tt, cs], op=ALU.max), reads=[sc, mlb], writes=[sc])
                        kb.op("dve", lambda e: e.tensor_tensor(out=sp_, in0=a, in1=mn, op=ALU.subtract), reads=[sc], writes=[sc])
                        kb.op("dve", lambda e: e.tensor_tensor(out=sl_, in0=mlb[:, tt, cs], in1=mn, op=ALU.subtract), reads=[sc, mlb], writes=[sc])
                        kb.op("act", lambda e: e.activation(out=sc[:, 4:8], in_=sc[:, 4:8], func=AF.Exp), reads=[sc], writes=[sc])
                        kb.op("dve", lambda e: e.tensor_copy(out=mp[:, cs], in_=mn), reads=[sc], writes=[mp])
                        for hh in range(2):
                            C = Cst[dr * 2 + hh]
                            pc = ps()
                            kb.op("pe", lambda e: e.matmul(pc[:, 0:129], lhsT=kwt[:, hh, :], rhs=vaug[:, tt, hh, :], start=True, stop=True), reads=[kwt, vaug], writes=[pc])
                            kb.op("dve", lambda e: e.tensor_scalar(out=C[:], in0=C[:], scalar1=sc[:, 4 + hh:5 + hh], scalar2=None, op0=ALU.mult), reads=[C, sc], writes=[C])
                            kb.op("dve", lambda e: e.scalar_tensor_tensor(out=C[:], in0=pc[:, 0:129], scalar=sc[:, 6 + hh:7 + hh], in1=C[:], op0=ALU.mult, op1=ALU.add),
                                  reads=[pc, sc, C], writes=[C])

                    def final_state(dr, s_):
                        for hh in range(2):
                            C = Cst[dr * 2 + hh]
                            kb.dma("sp", D["nmc"][s_, l, dr, h0 + hh], C[:, 0:128], reads=[C])
                            kb.dma("sp", D["nmn"][s_, l, dr, h0 + hh].rearrange("(p o) -> p o", o=1), C[:, 128:129], reads=[C])
                        kb.dma("sp", D["nmm"][s_, l:l + 1, dr * 4 + h0:dr * 4 + h0 + 2], mp[0:1, dr * 2:dr * 2 + 2], reads=[mp])

                    for s_ in range(nseq):
                        init_state(1, s_)
                        for r0 in range(lt - 1, -1, -1):
                            tt = s_ * lt + r0
                            kwt, sc = kwts[tt % 2], scs[tt % 2]
                            for hh in range(2):
                                kb.op("act", lambda e: e.activation(out=Cpb[:, tt, hh, :], in_=Cst[2 + hh][:], func=AF.Copy), reads=[Cst[2 + hh]], writes=[Cpb])
                            kb.op("dve", lambda e: e.tensor_copy(out=mprev[:, tt, 2:4], in_=mp[:, 2:4]), reads=[mp], writes=[mprev])
                            local_update(1, tt)
                        if not sample:
                            final_state(1, s_)
                    wob, wov = wload([(D["w_in"][l][:, C0 + 1536 + h0 * 128:C0 + 1536 + (h0 + 2) * 128], 256)], 8)
                    for s_ in range(nseq):
                        init_state(0, s_)
                        for r0 in range(lt):
                            tt = s_ * lt + r0
                            tk = slice(tt * 128, (tt + 1) * 128)
                            kwt, Dg, Dr, sc = kwts[tt % 2], Dgs[tt % 2], Drs[tt % 2], scs[tt % 2]
                            hsum, ht, sg = hsums[tt % 2], hts[tt % 2], sgs[tt % 2]
                            kb.op("dve", lambda e: e.tensor_copy(out=mprev[:, tt, 0:2], in_=mp[:, 0:2]), reads=[mp], writes=[mprev])
                            kb.op("dve", lambda e: e.tensor_tensor(out=sc[:, 8:12], in0=G[:, tt, :], in1=mprev[:, tt, :], op=ALU.add), reads=[G, mprev], writes=[sc])
                            kb.op("dve", lambda e: e.tensor_tensor(out=Dg[:], in0=bcast(ident_f, [128, 4, 128], 1), in1=bcast(pj[:, tt, :], [128, 4, 128], 2), op=ALU.mult),
                                  reads=[cf, pj], writes=[Dg])
                            po = ps_acc()
                            proj_tm(wob, wov, 0, 256, tt // 4, tt % 4, po)
                            pSs = []
                            for hh in range(2):
                                pS = ps_acc()
                                kb.op("pe", lambda e: e.matmul(pS[:, 0:128], lhsT=kT[:, hh, tk], rhs=qT[:, hh, tk], start=True, stop=True), reads=[kT, qT], writes=[pS])
                                pSs.append(pS)
                            pms = []
                            for c in range(4):
                                dr = c // 2
                                pm = ps()
                                kb.op("pe", lambda e: e.matmul(pm[:, 0:128], lhsT=ones_f, rhs=Dg[:, c, :], start=True, stop=False), reads=[cf, Dg], writes=[pm], inc=False)
                                kb.op("pe", lambda e: e.matmul(pm[:, 0:128], lhsT=ident_b, rhs=(maskB_b if dr == 0 else maskF_b), start=False, stop=True), reads=[cb], writes=[pm])
                                pms.append(pm)
                            for c in range(4):
                                kb.op("dve", lambda e: e.tensor_reduce(out=sc[:, 12 + c:13 + c], in_=pms[c][:, 0:128], axis=AX.X, op=ALU.max), reads=[pms[c]], writes=[sc])
                            kb.op("dve", lambda e: e.tensor_tensor(out=sc[:, 12:16], in0=sc[:, 12:16], in1=G[:, tt, :], op=ALU.add), reads=[sc, G], writes=[sc])
                            kb.op("dve", lambda e: e.tensor_tensor(out=sc[:, 16:20], in0=sc[:, 12:16], in1=sc[:, 8:12], op=ALU.max), reads=[sc], writes=[sc])
                            kb.op("dve", lambda e: e.tensor_tensor(out=sc[:, 20:24], in0=G[:, tt, :], in1=sc[:, 16:20], op=ALU.subtract), reads=[sc, G], writes=[sc])
                            kb.op("dve", lambda e: e.tensor_tensor(out=sc[:, 24:28], in0=sc[:, 8:12], in1=sc[:, 16:20], op=ALU.subtract), reads=[sc], writes=[sc])
                            kb.op("act", lambda e: e.activation(out=sc[:, 24:28], in_=sc[:, 24:28], func=AF.Exp, bias=lns_col[:, 0:1]), reads=[sc, cf], writes=[sc])
                            kb.op("act", lambda e: e.activation(out=sc[:, 28:32], in_=sc[:, 16:20], func=AF.Exp, scale=-1.0), reads=[sc], writes=[sc])
                            kb.op("dve", lambda e: e.tensor_tensor(out=Dr[:], in0=bcast(ident_f, [128, 4, 128], 1), in1=bcast(sc[:, 20:24], [128, 4, 128], 2), op=ALU.mult),
                                  reads=[cf, sc], writes=[Dr])
                            items = [(0, 0), (0, 1), (1, 0), (1, 1)]
                            mts = {}

                            def mA(i):
                                hh, dr = items[i]
                                c = dr * 2 + hh
                                pW = ps()
                                kb.op("pe", lambda e: e.matmul(pW[:, 0:128], lhsT=ones_f, rhs=Dr[:, c, :], start=True, stop=False), reads=[cf, Dr], writes=[pW], inc=False)
                                kb.op("pe", lambda e: e.matmul(pW[:, 0:128], lhsT=ident_b, rhs=(maskF_b if dr == 0 else maskB_b), start=False, stop=True), reads=[cb], writes=[pW])
                                E = Es[(ei0[0] + i) % 3]
                                M = Ms[(ei0[0] + i) % 3]
                                kb.op("act", lambda e: e.activation(out=E[:], in_=pW[:, 0:128], func=AF.Exp, bias=pjs[:, tt, c:c + 1]), reads=[pW, pjs], writes=[E])
                                kb.op("dve", lambda e: e.tensor_tensor(out=M[:], in0=E[:], in1=pSs[hh][:, 0:128], op=ALU.mult), reads=[E, pSs[hh]], writes=[M])
                                if dr == 0:
                                    kb.op("act", lambda e: e.activation(out=cbf[hh][:], in_=Cst[hh][:], func=AF.Copy), reads=[Cst[hh]], writes=[cbf[hh]])
                                mts[i] = M

                            def mC(i):
                                hh, dr = items[i]
                                c = dr * 2 + hh
                                M = mts.pop(i)
                                nd = nds[i % 2]
                                pN = ps()
                                kb.op("pe", lambda e: e.matmul(pN[:, 0:129], lhsT=M[:], rhs=vaug[:, tt, hh, :], start=True, stop=True), reads=[M, vaug], writes=[pN])
                                pI = ps()
                                if dr == 0:
                                    kb.op("pe", lambda e: e.matmul(pI[:, 0:129], lhsT=qT[:, hh, tk], rhs=cbf[hh][:], start=True, stop=True), reads=[qT, cbf[hh]], writes=[pI])
                                else:
                                    kb.op("pe", lambda e: e.matmul(pI[:, 0:129], lhsT=qT[:, hh, tk], rhs=Cpb[:, tt, hh, :], start=True, stop=True), reads=[qT, Cpb], writes=[pI])
                                kb.op("act", lambda e: e.activation(out=nd[:], in_=pN[:, 0:129], func=AF.Copy), reads=[pN], writes=[nd])
                                kb.op("dve", lambda e: e.scalar_tensor_tensor(out=nd[:], in0=pI[:, 0:129], scalar=sc[:, 24 + c:25 + c], in1=nd[:], op0=ALU.mult, op1=ALU.add),
                                      reads=[pI, sc, nd], writes=[nd])
                                kb.op("dve", lambda e: e.tensor_scalar(out=sc[:, 32:33], in0=nd[:, 128:129], scalar1=-1.0, scalar2=None, op0=ALU.mult), reads=[nd], writes=[sc])
                                kb.op("dve", lambda e: e.tensor_tensor(out=sc[:, 32:33], in0=sc[:, 32:33], in1=nd[:, 128:129], op=ALU.max), reads=[nd, sc], writes=[sc])
                                kb.op("dve", lambda e: e.tensor_tensor(out=sc[:, 32:33], in0=sc[:, 32:33], in1=sc[:, 28 + c:29 + c], op=ALU.max), reads=[sc], writes=[sc])
                                kb.op("dve", lambda e: e.reciprocal(out=sc[:, 33:34], in_=sc[:, 32:33]), reads=[sc], writes=[sc])
                                if dr == 0:
                                    kb.op("dve", lambda e: e.tensor_scalar(out=hsum[:, hh * 128:(hh + 1) * 128], in0=nd[:, 0:128], scalar1=sc[:, 33:34], scalar2=None, op0=ALU.mult),
                                          reads=[nd, sc], writes=[hsum])
                                else:
                                    kb.op("dve", lambda e: e.scalar_tensor_tensor(out=hsum[:, hh * 128:(hh + 1) * 128], in0=nd[:, 0:128], scalar=sc[:, 33:34],
                                                                                  in1=hsum[:, hh * 128:(hh + 1) * 128], op0=ALU.mult, op1=ALU.add), reads=[nd, sc, hsum], writes=[hsum])

                            for i in range(4 + 2):
                                if i < 4:
                                    mA(i)
                                if i >= 2:
                                    mC(i - 2)
                            ei0[0] += 4
                            local_update(0, tt)
                            h3 = hsum[:].rearrange("p (h d) -> p h d", d=128)
                            t3 = ht[:].rearrange("p (h d) -> p h d", d=128)
                            kb.op("dve", lambda e: e.tensor_tensor(out=ht[:], in0=hsum[:], in1=hsum[:], op=ALU.mult), reads=[hsum], writes=[ht])
                            kb.op("dve", lambda e: e.tensor_reduce(out=sc[:, 34:36], in_=t3, axis=AX.X, op=ALU.add), reads=[ht], writes=[sc])
                            kb.op("dve", lambda e: e.tensor_scalar(out=sc[:, 34:36], in0=sc[:, 34:36], scalar1=1.0 / 128, scalar2=EPS, op0=ALU.mult, op1=ALU.add), reads=[sc], writes=[sc])
                            kb.op("act", lambda e: e.activation(out=sc[:, 34:36], in_=sc[:, 34:36], func=AF.Ln), reads=[sc], writes=[sc])
                            kb.op("act", lambda e: e.activation(out=sc[:, 36:38], in_=sc[:, 34:36], func=AF.Exp, scale=-0.5), reads=[sc], writes=[sc])
                            kb.op("dve", lambda e: e.tensor_tensor(out=t3, in0=h3, in1=bcast(sc[:, 36:38], [128, 2, 128], 2), op=ALU.mult), reads=[hsum, sc], writes=[ht])
                            kb.op("dve", lambda e: e.tensor_tensor(out=ht[:], in0=ht[:], in1=ng[:], op=ALU.mult), reads=[ht, ng], writes=[ht])
                            kb.op("act", lambda e: e.activation(out=sg[:], in_=po[:, 0:256], func=AF.Exp, scale=-1.0), reads=[po], writes=[sg])
                            kb.op("act", lambda e: e.activation(out=sg[:], in_=sg[:], func=AF.Ln, bias=1.0), reads=[sg], writes=[sg])
                            kb.op("act", lambda e: e.activation(out=sg[:], in_=sg[:], func=AF.Exp, scale=-1.0), reads=[sg], writes=[sg])
                            kb.op("dve", lambda e: e.tensor_tensor(out=ytm[:, tt % 4, :], in0=ht[:], in1=sg[:], op=ALU.mult), reads=[ht, sg], writes=[ytm])
                            if tt % 4 == 3:
                                tb = tt // 4
                                for c2 in range(2):
                                    p = ps()
                                    for tl in range(4):
                                        kb.op("pe", lambda e: e.matmul(p[:, tl * 128:(tl + 1) * 128], lhsT=ytm[:, tl, c2 * 128:(c2 + 1) * 128], rhs=ident_b, start=True, stop=True),
                                              reads=[ytm, cb], writes=[p], inc=(tl == 3))
                                    kb.op("act", lambda e: e.activation(out=yT[:, c2, :], in_=p[:, :], func=AF.Copy), reads=[p], writes=[yT])
                                for c in range(8):
                                    p = ps()
                                    for k in range(2):
                                        kb.op("pe", lambda e: e.matmul(p[:, :], lhsT=wo_buf[:, k, c * 128:(c + 1) * 128], rhs=yT[:, k, :], start=(k == 0), stop=(k == 1)),
                                              reads=[wo_buf, yT], writes=[p], inc=(k == 1))
                                    kb.op("dve", lambda e: e.scalar_tensor_tensor(out=xT[tb][:, c, :], in0=p[:, :], scalar=modc[:, 2, c:c + 1], in1=xT[tb][:, c, :],
                                                                                  op0=ALU.mult, op1=ALU.add), reads=[p, modc, xT[tb]], writes=[xT[tb]])
                        if not sample:
                            final_state(0, s_)
            kb.barrier()

    def run_pass(g, src, dst, nblk):
        load_x(src, nblk)
        kb.barrier()
        for l in range(NL):
            kb.label = 'norm_g%d' % g
            load_mod(l, g)
            if g == 0 and l + 1 < NL:
                compute_mod(l + 1)
            with scope() as st:
                sq = [alloc(st, "sq", [128, 8, 512], BF16) for _ in range(2)]
                rstd = [alloc(st, "rstd", [128, 512], F32) for _ in range(2)]
                tmp2 = [alloc(st, "tmpn%d" % i, [128, 512], F32) for i in range(4)]
                for tb in range(nblk):
                    norm_block((sq, rstd, tmp2), tb, AB.t[:, 0, :], AB.t[:, 1, :], hT[tb])
            kb.barrier()
            for mk in MIXERS:
                if mk in "bd":
                    attention(l, g, mk, nblk)
                elif mk == "a":
                    ssd(l, g, nblk)
                elif mk == "c":
                    mlstm(l, g, nblk)
            with scope() as st:
                sq = [alloc(st, "sq", [128, 8, 512], BF16) for _ in range(2)]
                rstd = [alloc(st, "rstd", [128, 512], F32) for _ in range(2)]
                tmp2 = [alloc(st, "tmpn%d" % i, [128, 512], F32) for i in range(4)]
                for tb in range(nblk):
                    norm_block((sq, rstd, tmp2), tb, AB.t[:, 2, :], AB.t[:, 3, :], hT[tb])
            kb.barrier()
            for b0 in range(0, nblk, 2):
                ffn(l, list(range(b0, min(b0 + 2, nblk))))
        final_out(nblk, dst)

    run_pass(0, D["xp"], D["yp"], 1)
    run_pass(1, D["xs"], D["ys"], 4)

    kb.barrier(include_pool_dma=True)
    top.close()
    print("instructions:", kb.ninst, flush=True)
    if kb.stats is not None:
        tot = 0.0
        for lab, (mk, busy, nu, nfl, bub) in sorted(kb.stats.items(), key=lambda kv: -kv[1][0]):
            tot += mk
            print("  %-12s est_us=%8.0f units=%6d regions=%4d bubble_us=%6.0f busy: %s" % (lab, mk / 1e3, nu, nfl, bub / 1e3, " ".join("%s=%.0f" % (e_, v_ / 1e3) for e_, v_ in sorted(busy.items()))))
        print("  est total us", tot / 1e3)
    return nc


_CACHE = {}


def prep_inputs(inp):
    f = lambda a: np.ascontiguousarray(np.asarray(a, dtype=np.float32))
    consts = make_consts()
    rope = make_rope()
    shared = {}
    for name in ("w_ada", "b_ada", "norm1", "norm2", "w_in", "w_out", "conv_ssd_w", "conv_ssd_b", "ssd_d", "ssd_norm",
                 "diff_lq1", "diff_lk1", "diff_lq2", "diff_lk2", "conv_mlstm_w", "conv_mlstm_b", "mlstm_norm",
                 "gqa_q_norm", "gqa_k_norm", "w_ffn_in", "w_ffn_out", "norm_f"):
        shared[name] = f(inp[name])
    shared["ssd_a_log"] = f(inp["ssd_a_log"]).reshape(4, 16)
    shared["ssd_dt_bias"] = f(inp["ssd_dt_bias"]).reshape(4, 16)
    shared["mlstm_gate_b"] = f(inp["mlstm_gate_b"]).reshape(4, 16)
    shared["consts"] = consts
    shared["rope"] = rope
    xp = f(inp["x_prompt"])
    xs = f(inp["x_sample"])
    in_maps = []
    for c in range(8):
        b = c // 4
        m = dict(shared)
        m["xp"] = xp[2 * c:2 * c + 2].reshape(512, 1024)
        m["xs"] = xs[b]
        m["cvec"] = np.stack([f(inp["c_ctx"]), f(inp["c"])[b]], axis=0)
        m["cdk"] = f(inp["cache_diff_k"])[b].reshape(4, 256, 512)
        m["cdv"] = f(inp["cache_diff_v"])[b].reshape(4, 256, 512)
        m["cgk"] = f(inp["cache_gqa_k"])[b].reshape(4, 256, 128)
        m["cgv"] = f(inp["cache_gqa_v"])[b].reshape(4, 256, 128)
        m["sssm"] = f(inp["state_ssm"])[b]
        m["smc"] = f(inp["state_mlstm_c"])[b]
        m["smn"] = f(inp["state_mlstm_n"])[b]
        m["smm"] = f(inp["state_mlstm_m"])[b].reshape(4, 8)
        in_maps.append(m)
    if NL < 4:
        spec = dict(IN_SPECS)
        for m in in_maps:
            for k_ in list(m.keys()):
                if spec[k_][0] == 4 and len(spec[k_]) > 1:
                    m[k_] = np.ascontiguousarray(m[k_][:NL])
    return in_maps


def kernel(**inp):
    if "nc" not in _CACHE:
        _CACHE["nc"] = build_program()
    nc = _CACHE["nc"]
    in_maps = prep_inputs(inp)
    res = run_bass_kernel_spmd(nc, in_maps, core_ids=list(range(8)))
    return assemble(res.results)


def assemble(R):
    y_prompt = np.concatenate([R[c]["yp"].reshape(2, 256, 1024) for c in range(8)], axis=0)
    y_sample = np.stack([R[0]["ys"], R[4]["ys"]], axis=0)
    cat = lambda k: np.concatenate([R[c][k] for c in range(8)], axis=0)
    ndk = cat("ndk").reshape(16, 4, 256, 4, 2, 64)
    ndv = cat("ndv").reshape(16, 4, 256, 4, 128)
    ngk = cat("ngk").reshape(16, 4, 256, 2, 64)
    ngv = cat("ngv").reshape(16, 4, 256, 2, 64)
    nssm = cat("nssm")
    nmc = cat("nmc")
    nmn = cat("nmn")
    nmm = cat("nmm").reshape(16, 4, 2, 4)
    return (y_prompt, y_sample, ndk, ndv, ngk, ngv, nssm, nmc, nmn, nmm)
```

```python
import os
import math
from contextlib import ExitStack
import numpy as np
import concourse.bass as bass
import concourse.mybir as mybir
from concourse.bass_utils import run_bass_kernel_spmd

F32 = mybir.dt.float32
BF16 = mybir.dt.bfloat16
ALU = mybir.AluOpType
AF = mybir.ActivationFunctionType
AX = mybir.AxisListType

D_MODEL = 1024
DEPTH = 4
IN_COLS = 5664
D_FF = 2816
EPS = 1e-6
NEG = -30000.0

NL = int(os.environ.get("MK_NL", "4"))
MIXERS = os.environ.get("MK_MIX", "abcd")
SSD_PH = int(os.environ.get("MK_SSD_PH", "9"))
SSD_SUB = int(os.environ.get("MK_SSD_SUB", "9"))


class Buf:
    __slots__ = ("t", "w", "r")

    def __init__(self, t):
        self.t = t
        self.w = None
        self.r = []

    def __getitem__(self, idx):
        return self.t[idx]


class _Rec:
    def __init__(self):
        self.call = None

    def __getattr__(self, name):
        def f(*args, **kw):
            self.call = (name, args, kw)
            return self
        return f


class KB:
    NDMA_SEM = 8

    def __init__(self, nc):
        self.nc = nc
        self.engs = {"pe": nc.tensor, "act": nc.scalar, "dve": nc.vector, "pool": nc.gpsimd, "sp": nc.sync}
        self.sems = {}
        self.cnt = {}
        for k in ("pe", "act", "dve", "pool"):
            self.sems[k] = nc.alloc_semaphore(name="s_" + k)
            self.cnt[k] = 0
        self.dq = {}
        for q in ("sp", "pool", "act"):
            lst = []
            for i in range(self.NDMA_SEM):
                key = "d_%s%d" % (q, i)
                self.sems[key] = nc.alloc_semaphore(name=key)
                self.cnt[key] = 0
                lst.append(key)
            self.dq[q] = [lst, 0]
        self.seen = {e: {} for e in self.engs}
        self.ninst = 0
        self.defer = bool(int(os.environ.get('MK_SCHED', '1')))
        self.pending = []
        self.stats = {} if os.environ.get('MK_STATS') else None
        self.label = 'top'
        self.sched_mode = os.environ.get('MK_SMODE', 'cpx')

    def _wait(self, eng, k, v):
        seen = self.seen[eng]
        if seen.get(k, 0) >= v:
            return
        self.engs[eng].wait_ge(self.sems[k], v)
        self.ninst += 1
        seen[k] = v

    def _need(self, eng, reads, writes):
        need = {}

        def add(dep):
            if dep is None:
                return
            k, v = dep
            if need.get(k, 0) < v:
                need[k] = v
        for b in reads:
            add(b.w)
        for b in writes:
            add(b.w)
            for d in b.r:
                add(d)
        for k, v in need.items():
            if k == eng and eng == "pe":
                continue
            self._wait(eng, k, v)

    def _record(self, dep, reads, writes):
        for b in reads:
            b.r.append(dep)
            if len(b.r) > 64:
                mx = {}
                for k, v in b.r:
                    if mx.get(k, 0) < v:
                        mx[k] = v
                b.r = list(mx.items())
        for b in writes:
            b.w = dep
            b.r = []

    def op(self, eng, fn, reads=(), writes=(), inc=True):
        if self.defer:
            rec = _Rec()
            fn(rec)
            self.pending.append(("op", eng, rec.call, tuple(reads), tuple(writes), inc))
            return None
        return self._op_now(eng, fn, reads, writes, inc)

    def _op_now(self, eng, fn, reads=(), writes=(), inc=True):
        self._need(eng, reads, writes)
        ins = fn(self.engs[eng])
        self.ninst += 1
        val = self.cnt[eng] + 1
        if inc:
            ins.then_inc(self.sems[eng], 1)
            self.cnt[eng] = val
        self._record((eng, val), reads, writes)
        return ins

    def dma(self, q, out, in_, reads=(), writes=(), **kw):
        if self.defer:
            self.pending.append(("dma", q, (out, in_, kw), tuple(reads), tuple(writes), True))
            return None
        return self._dma_now(q, out, in_, reads, writes, **kw)

    def _dma_now(self, q, out, in_, reads=(), writes=(), **kw):
        self._need(q, reads, writes)
        lst, i = self.dq[q]
        key = lst[i % len(lst)]
        self.dq[q][1] = i + 1
        if self.cnt[key]:
            self._wait(q, key, self.cnt[key])
        ins = self.engs[q].dma_start(out=out, in_=in_, **kw)
        self.ninst += 1
        self.cnt[key] += 16
        ins.then_inc(self.sems[key], 16)
        dep = (key, self.cnt[key])
        self._record(dep, reads, writes)
        return dep

    @staticmethod
    def _cost(kind, eng, call):
        def fsz(ap):
            n = 1
            for d in ap.shape[1:]:
                n *= d
            return n
        if kind == "dma":
            out = call[0]
            nb = fsz(out) * out.shape[0] * (2 if out.dtype == BF16 else 4)
            return 2000.0 + nb / 80.0
        name, args, kw = call
        if name == "matmul":
            n = fsz(kw["rhs"])
            passes = 4 if kw["lhsT"].dtype == F32 else 1
            return 70.0 + n * passes * 0.45
        out = kw.get("out", None)
        if out is None:
            out = kw.get("ap", args[0] if args else None)
        n = fsz(out) if out is not None else 64
        if eng == "act":
            return 230.0 + n * 0.75
        if name == "reciprocal":
            return 70.0 + n * 6.5
        if name == "memset":
            return 70.0 + n * 0.5
        return 70.0 + n * 1.1

    def flush(self):
        pend = self.pending
        self.pending = []
        if not pend:
            return
        import heapq
        units = []
        cur = None
        for it in pend:
            kind, eng, call, rd, wr, inc = it
            if kind == "op" and eng == "pe":
                if cur is None:
                    cur = [eng, [], 0.0, set(), set()]
                cur[1].append(it)
                cur[2] += self._cost(kind, eng, call)
                cur[3].update(rd)
                cur[4].update(wr)
                if inc:
                    units.append(cur)
                    cur = None
            else:
                assert cur is None, "non-PE op inside an open PE group"
                units.append([eng, [it], self._cost(kind, eng, call), set(rd), set(wr)])
        assert cur is None, "PE group without final inc"
        n = len(units)
        lastw = {}
        readers = {}
        deps = [None] * n
        succ = [[] for _ in range(n)]
        for i, u in enumerate(units):
            d = set()
            for b in u[3]:
                if b in lastw:
                    d.add(lastw[b])
            for b in u[4]:
                if b in lastw:
                    d.add(lastw[b])
                for r in readers.get(b, ()):
                    d.add(r)
            d.discard(i)
            deps[i] = d
            for j in d:
                succ[j].append(i)
            for b in u[3]:
                readers.setdefault(b, []).append(i)
            for b in u[4]:
                lastw[b] = i
                readers[b] = []
        ndep = [len(d) for d in deps]
        ready_t = [0.0] * n
        fin = [0.0] * n
        bl = [0.0] * n
        for i in range(n - 1, -1, -1):
            m_ = 0.0
            for j in succ[i]:
                if bl[j] > m_:
                    m_ = bl[j]
            bl[i] = units[i][2] + m_
        mode = self.sched_mode
        free = {}
        fut = {}
        avail = {}
        for i in range(n):
            if ndep[i] == 0:
                heapq.heappush(fut.setdefault(units[i][0], []), (0.0, i))
        order = []
        done = 0
        while done < n:
            best = None
            for e, h in fut.items():
                fe = free.get(e, 0.0)
                av = avail.setdefault(e, [])
                while h and h[0][0] <= fe:
                    rt, i = heapq.heappop(h)
                    heapq.heappush(av, ((-bl[i], i) if (mode == "cp" or (mode == "cpx" and e != "pe")) else (rt, i)))
                if av:
                    cand = (fe, 0, e)
                elif h:
                    cand = (h[0][0], 1, e)
                else:
                    continue
                if best is None or cand < best:
                    best = cand
            st_, fromfut, e = best
            if fromfut:
                rt, i = heapq.heappop(fut[e])
            else:
                _, i = heapq.heappop(avail[e])
            u = units[i]
            if e in ("sp", "pool"):
                free[e] = st_ + 60.0
                fin[i] = st_ + u[2]
            else:
                fin[i] = st_ + u[2]
                free[e] = fin[i]
            order.append((st_, i))
            done += 1
            for j in succ[i]:
                ndep[j] -= 1
                if fin[i] > ready_t[j]:
                    ready_t[j] = fin[i]
                if ndep[j] == 0:
                    heapq.heappush(fut.setdefault(units[j][0], []), (ready_t[j], j))
        if self.stats is not None:
            mk = max(fin) if fin else 0.0
            busy = {}
            for u in units:
                busy[u[0]] = busy.get(u[0], 0.0) + u[2]
            st = self.stats.setdefault(self.label, [0.0, {}, 0, 0, 0.0])
            st[0] += mk
            st[2] += n
            st[3] += 1
            st[4] += mk - max(busy.values())
            for e_, v_ in busy.items():
                st[1][e_] = st[1].get(e_, 0.0) + v_
        order.sort()
        for _, i in order:
            for kind, eng, call, rd, wr, inc in units[i][1]:
                if kind == "op":
                    name, args, kw = call
                    self._op_now(eng, lambda en: getattr(en, name)(*args, **kw), rd, wr, inc)
                else:
                    out, in_, kw = call
                    self._dma_now(eng, out, in_, rd, wr, **kw)

    def barrier(self, include_pool_dma=False):
        self.flush()
        keys = ["pe", "act", "dve", "pool"] + self.dq["sp"][0] + self.dq["act"][0]
        if include_pool_dma:
            keys += self.dq["pool"][0]
        for e in ("pe", "act", "dve", "pool", "sp"):
            for k in keys:
                if (k == e and e == "pe") or self.cnt[k] == 0:
                    continue
                self._wait(e, k, self.cnt[k])


def bcast(ap, shape, axis):
    return ap.unsqueeze(axis).broadcast_to(list(shape))


IN_SPECS = [
    ("xp", [512, 1024]), ("xs", [2048, 1024]), ("cvec", [2, 1024]),
    ("cdk", [4, 256, 512]), ("cdv", [4, 256, 512]), ("cgk", [4, 256, 128]), ("cgv", [4, 256, 128]),
    ("sssm", [4, 2, 8, 64, 64]), ("smc", [4, 2, 4, 128, 128]), ("smn", [4, 2, 4, 128]), ("smm", [4, 8]),
    ("w_ada", [4, 1024, 6144]), ("b_ada", [4, 6144]), ("norm1", [4, 1024]), ("norm2", [4, 1024]),
    ("w_in", [4, 1024, IN_COLS]), ("w_out", [4, 2048, 1024]),
    ("conv_ssd_w", [4, 5, 768]), ("conv_ssd_b", [4, 768]), ("ssd_a_log", [4, 16]), ("ssd_dt_bias", [4, 16]),
    ("ssd_d", [4, 8]), ("ssd_norm", [4, 512]),
    ("diff_lq1", [4, 64]), ("diff_lk1", [4, 64]), ("diff_lq2", [4, 64]), ("diff_lk2", [4, 64]),
    ("conv_mlstm_w", [4, 5, 1024]), ("conv_mlstm_b", [4, 1024]), ("mlstm_gate_b", [4, 16]), ("mlstm_norm", [4, 512]),
    ("gqa_q_norm", [4, 64]), ("gqa_k_norm", [4, 64]),
    ("w_ffn_in", [4, 1024, 2 * D_FF]), ("w_ffn_out", [4, D_FF, 1024]), ("norm_f", [1024]),
    ("consts", [128, 1152]), ("rope", [128, 2, 2048]),
]
OUT_SPECS = [
    ("yp", [512, 1024]), ("ys", [2048, 1024]),
    ("ndk", [2, 4, 256, 512]), ("ndv", [2, 4, 256, 512]), ("ngk", [2, 4, 256, 128]), ("ngv", [2, 4, 256, 128]),
    ("nssm", [2, 4, 2, 8, 64, 64]), ("nmc", [2, 4, 2, 4, 128, 128]), ("nmn", [2, 4, 2, 4, 128]), ("nmm", [2, 4, 8]),
]


def make_consts():
    c = np.zeros((128, 1152), np.float32)
    k = np.arange(128)
    c[:, 0:128] = np.eye(128)
    c[:, 128:256] = 1.0
    c[:, 256:384] = (k[:, None] <= k[None, :])
    c[:, 384:512] = (k[:, None] >= k[None, :])
    c[:, 512:640] = np.where(k[:, None] <= k[None, :], 0.0, NEG)
    c[:, 640:768] = np.where(k[:, None] >= k[None, :], 0.0, NEG)
    c[:, 768:896] = (k[:, None] // 64 == k[None, :] // 64)
    rm = np.zeros((128, 128), np.float32)
    for dp in range(128):
        half = (dp % 32) // 16
        if half == 0:
            rm[dp + 16, dp] = -1.0
        else:
            rm[dp - 16, dp] = 1.0
    c[:, 896:1024] = rm
    c[64, 1024:1088] = 1.0
    c[0, 1088:1152] = 1.0
    return c


def make_rope():
    t = np.arange(2048)
    r = (t // 64).astype(np.float32)
    cc = (t % 64).astype(np.float32)
    nf = 16
    freqs = (10000.0 ** (-np.arange(nf, dtype=np.float32) / nf)).astype(np.float32)
    ang = np.stack([r[:, None] * freqs, cc[:, None] * freqs], axis=1).astype(np.float32)
    out = np.zeros((128, 2, 2048), np.float32)
    for p in range(128):
        d = p % 64
        a = d // 32
        f = d % 16
        out[p, 0] = np.cos(ang[:, a, f])
        out[p, 1] = np.sin(ang[:, a, f])
    return out


def build_program():
    nc = bass.Bass("TRN2", target_bir_lowering=False)
    kb = KB(nc)
    D = {}
    for name, shape in IN_SPECS:
        if shape[0] == 4 and len(shape) > 1:
            shape = [NL] + list(shape[1:])
        D[name] = nc.dram_tensor(name, shape, F32, kind="ExternalInput").ap()
    for name, shape in OUT_SPECS:
        D[name] = nc.dram_tensor(name, shape, F32, kind="ExternalOutput").ap()
    mod_d = nc.dram_tensor("mod_scr", [4, 2, 6144], F32, kind="Internal").ap()

    top = ExitStack()

    class scope:
        def __enter__(self_):
            self_.st = ExitStack()
            return self_.st

        def __exit__(self_, *a):
            if a[0] is None:
                kb.barrier()
            self_.st.close()
            return False

    uid = [0]

    def alloc(stack, name, shape, dt, psum=False):
        uid[0] += 1
        name = "%s_%d" % (name, uid[0])
        cm = nc.psum_tensor(name, shape, dt) if psum else nc.sbuf_tensor(name, shape, dt)
        return Buf(stack.enter_context(cm))

    xT = [alloc(top, "xT%d" % i, [128, 8, 512], F32) for i in range(4)]
    hT = [alloc(top, "hT%d" % i, [128, 8, 512], BF16) for i in range(4)]
    NW = 2
    wbufs = [alloc(top, "wb%d" % i, [128, 4096], BF16) for i in range(NW)]
    wstate = [0]
    psb = [alloc(top, "ps%d" % i, [128, 512], F32, psum=True) for i in range(8)]
    pstate = [0]
    cf = alloc(top, "cf", [128, 640], F32)
    cb = alloc(top, "cb", [128, 1024], BF16)
    modc = alloc(top, "modc", [128, 6, 8], F32)
    nrm = alloc(top, "nrm", [128, 2, 8], F32)
    AB = alloc(top, "AB", [128, 4, 8], F32)
    nfc = alloc(top, "nfc", [128, 8], F32)
    lnsb = alloc(top, "lnsb", [128, 1], F32)
    lns_col = lnsb.t

    wo_buf = alloc(top, "wo_buf", [128, 4, 1024], BF16)
    accstate = [0]

    def ps():
        b = psb[pstate[0] % 4]
        pstate[0] += 1
        return b

    def ps_acc():
        b = psb[4 + accstate[0] % 4]
        accstate[0] += 1
        return b

    ffn_state = [0]

    def wload(pieces, kch, three=False):
        if three:
            lst = wbufs + [wo_buf]
            b = lst[ffn_state[0] % len(lst)]
            ffn_state[0] += 1
        else:
            b = wbufs[wstate[0] % NW]
            wstate[0] += 1
        ntot = sum(n for _, n in pieces)
        assert kch * ntot <= 4096, (kch, ntot)
        flat = b.t[:].rearrange("p k n -> p (k n)") if b is wo_buf else b.t
        view = flat[:, 0:kch * ntot].rearrange("p (k n) -> p k n", k=kch)
        o = 0
        for ap, n in pieces:
            kb.dma("pool", view[:, :, o:o + n], ap.rearrange("(k p) n -> p k n", p=128), writes=[b])
            o += n
        return b, view

    ident_f = cf.t[:, 0:128]
    ones_f = cf.t[:, 128:256]
    ident_b = cb.t[:, 0:128]
    selm_f = cf.t[:, 512:640]
    ones_b = cb.t[:, 128:256]

    kb.dma("sp", cf[:, 0:512], D["consts"][:, 0:512], writes=[cf])
    kb.dma("sp", cf[:, 512:640], D["consts"][:, 1024:1152], writes=[cf])
    kb.dma("pool", cb[:], D["consts"][:, 0:1024], writes=[cb])
    kb.op("dve", lambda e: e.memset(lnsb[:], math.log(128 ** -0.5)), writes=[lnsb])
    kb.dma("sp", nfc[:], D["norm_f"].rearrange("(c p) -> p c", p=128), writes=[nfc], allow_slow_non_contiguous=True)

    modall = alloc(top, "modall", [128, NL, 48, 2], F32)
    ball = alloc(top, "ball", [128, NL, 48], F32)
    nrmall = alloc(top, "nrmall", [128, NL, 2, 8], F32)
    for l in range(NL):
        kb.dma("sp", ball[:, l, :], D["b_ada"][l].rearrange("(j p) -> p j", p=128), writes=[ball], allow_slow_non_contiguous=True)
        kb.dma("sp", nrmall[:, l, 0, :], D["norm1"][l].rearrange("(c p) -> p c", p=128), writes=[nrmall], allow_slow_non_contiguous=True)
        kb.dma("sp", nrmall[:, l, 1, :], D["norm2"][l].rearrange("(c p) -> p c", p=128), writes=[nrmall], allow_slow_non_contiguous=True)
    cT = alloc(top, "cT", [128, 2, 8], F32)
    cTb = alloc(top, "cTb", [128, 2, 8], BF16)
    sig = alloc(top, "csig", [128, 2, 8], F32)
    for g in range(2):
        kb.dma("sp", cT[:, g, :], D["cvec"][g].rearrange("(c p) -> p c", p=128), writes=[cT], allow_slow_non_contiguous=True)
    kb.op("act", lambda e: e.activation(out=sig[:], in_=cT[:], func=AF.Sigmoid), reads=[cT], writes=[sig])
    kb.op("dve", lambda e: e.tensor_tensor(out=cTb[:], in0=cT[:], in1=sig[:], op=ALU.mult), reads=[cT, sig], writes=[cTb])

    def compute_mod(l):
        for blk in range(12):
            c0 = blk * 512
            wb, wv = wload([(D["w_ada"][l][:, c0:c0 + 512], 512)], 8)
            p = ps()
            for cc in range(4):
                for k in range(8):
                    kb.op("pe", lambda e: e.matmul(p[:, 2 * cc:2 * cc + 2], lhsT=wv[:, k, cc * 128:(cc + 1) * 128], rhs=cTb[:, :, k], start=(k == 0), stop=(k == 7)),
                          reads=[cTb, wb], writes=[p], inc=(k == 7 and cc == 3))
            kb.op("dve", lambda e: e.tensor_tensor(out=modall[:, l, blk * 4:(blk + 1) * 4, :], in0=p[:, 0:8].rearrange("p (j g) -> p j g", g=2),
                                                   in1=bcast(ball[:, l, blk * 4:(blk + 1) * 4], [128, 4, 2], 2), op=ALU.add), reads=[p, ball], writes=[modall])

    compute_mod(0)
    kb.barrier()

    def load_x(src, nblk):
        with scope() as st:
            xin = [alloc(st, "xin%d" % i, [128, 1024], F32) for i in range(2)]
            for tb in range(nblk):
                tiles = []
                for tl in range(4):
                    pass
                for tl in range(4):
                    xi = xin[(tb * 4 + tl) % 2]
                    t0 = (tb * 4 + tl) * 128
                    kb.dma("sp", xi[:], src[t0:t0 + 128, :], writes=[xi])
                    for half in range(2):
                        p = ps()
                        for cc in range(4):
                            c = half * 4 + cc
                            kb.op("pe", lambda e: e.matmul(p[:, cc * 128:(cc + 1) * 128], lhsT=xi[:, c * 128:(c + 1) * 128], rhs=ident_f,
                                                           start=True, stop=True), reads=[xi, cf], writes=[p], inc=(cc == 3))
                        kb.op("act", lambda e: e.activation(
                            out=xT[tb][:, half * 4:half * 4 + 4, tl * 128:(tl + 1) * 128],
                            in_=p[:, :].rearrange("p (c t) -> p c t", c=4), func=AF.Copy), reads=[p], writes=[xT[tb]])

    def norm_block(st_tmp, tb, Acol, Bcol, dst, dst_dt_is_bf16=True):
        sq, rstd, tmp2 = st_tmp
        if isinstance(sq, list):
            sq, rstd = sq[tb % 2], rstd[tb % 2]
        kb.op("act", lambda e: e.activation(out=sq[:], in_=xT[tb][:], func=AF.Square), reads=[xT[tb]], writes=[sq])
        p = ps()
        for c in range(8):
            kb.op("pe", lambda e: e.matmul(p[:, :], lhsT=ones_b, rhs=sq[:, c, :], start=(c == 0), stop=(c == 7)),
                  reads=[sq, cb], writes=[p], inc=(c == 7))
        kb.op("dve", lambda e: e.tensor_scalar(out=rstd[:], in0=p[:, :], scalar1=1.0 / D_MODEL, scalar2=EPS, op0=ALU.mult, op1=ALU.add),
              reads=[p], writes=[rstd])
        kb.op("act", lambda e: e.activation(out=rstd[:], in_=rstd[:], func=AF.Ln), reads=[rstd], writes=[rstd])
        kb.op("act", lambda e: e.activation(out=rstd[:], in_=rstd[:], func=AF.Exp, scale=-0.5), reads=[rstd], writes=[rstd])
        for c in range(8):
            t2 = tmp2[c % len(tmp2)]
            kb.op("dve", lambda e: e.tensor_tensor(out=t2[:], in0=xT[tb][:, c, :], in1=rstd[:], op=ALU.mult),
                  reads=[xT[tb], rstd], writes=[t2])
            if Bcol is not None:
                kb.op("act", lambda e: e.activation(out=dst[:, c, :], in_=t2[:], func=AF.Identity, bias=Bcol[:, c:c + 1], scale=Acol[:, c:c + 1]),
                      reads=[t2, AB], writes=[dst])
            else:
                kb.op("act", lambda e: e.activation(out=dst[:, c, :], in_=t2[:], func=AF.Identity, scale=Acol[:, c:c + 1]),
                      reads=[t2, nfc], writes=[dst])

    def load_mod(l, g):
        kb.op("dve", lambda e: e.tensor_copy(out=modc[:], in_=modall[:, l, :, g].rearrange("p (v c) -> p v c", v=6)), reads=[modall], writes=[modc])
        kb.op("dve", lambda e: e.tensor_copy(out=nrm[:], in_=nrmall[:, l, :, :]), reads=[nrmall], writes=[nrm])
        for j, (vs, vh) in enumerate(((1, 0), (4, 3))):
            kb.op("dve", lambda e: e.scalar_tensor_tensor(out=AB[:, 2 * j, :], in0=modc[:, vs, :], scalar=1.0, in1=nrm[:, j, :],
                                                          op0=ALU.add, op1=ALU.mult), reads=[modc, nrm], writes=[AB])
            kb.op("dve", lambda e: e.tensor_copy(out=AB[:, 2 * j + 1, :], in_=modc[:, vh, :]), reads=[modc], writes=[AB])

    def ffn(l, blocks):
        kb.label = 'ffn'
        nb = len(blocks)
        with scope() as st:
            actT = alloc(st, "actT", [128, 22, nb * 512], BF16)
            sg = [alloc(st, "sg%d" % i, [128, 512], F32) for i in range(2)]
            it = 0
            for jj in range(11):
                c0 = jj * 256
                wb, wv = wload([(D["w_ffn_in"][l][:, c0:c0 + 256], 256), (D["w_ffn_in"][l][:, D_FF + c0:D_FF + c0 + 256], 256)], 8, three=True)
                for j2 in range(2):
                    j = jj * 2 + j2
                    for bi, tb in enumerate(blocks):
                        pg = ps()
                        pu = ps()
                        for k in range(8):
                            kb.op("pe", lambda e: e.matmul(pg[:, :], lhsT=wv[:, k, j2 * 128:(j2 + 1) * 128], rhs=hT[tb][:, k, :],
                                                           start=(k == 0), stop=(k == 7)), reads=[wb, hT[tb]], writes=[pg], inc=(k == 7))
                        for k in range(8):
                            kb.op("pe", lambda e: e.matmul(pu[:, :], lhsT=wv[:, k, 256 + j2 * 128:256 + (j2 + 1) * 128], rhs=hT[tb][:, k, :],
                                                           start=(k == 0), stop=(k == 7)), reads=[wb, hT[tb]], writes=[pu], inc=(k == 7))
                        s = sg[it % 2]
                        it += 1
                        kb.op("act", lambda e: e.activation(out=s[:], in_=pg[:, :], func=AF.Silu), reads=[pg], writes=[s])
                        kb.op("dve", lambda e: e.tensor_tensor(out=actT[:, j, bi * 512:(bi + 1) * 512], in0=s[:], in1=pu[:, :], op=ALU.mult),
                              reads=[s, pu], writes=[actT])
            for c in range(8):
                wb, wv = wload([(D["w_ffn_out"][l][:, c * 128:(c + 1) * 128], 128)], 22, three=True)
                for bi, tb in enumerate(blocks):
                    p = ps()
                    for k in range(22):
                        kb.op("pe", lambda e: e.matmul(p[:, :], lhsT=wv[:, k, :], rhs=actT[:, k, bi * 512:(bi + 1) * 512],
                                                       start=(k == 0), stop=(k == 21)), reads=[wb, actT], writes=[p], inc=(k == 21))
                    kb.op("dve", lambda e: e.scalar_tensor_tensor(out=xT[tb][:, c, :], in0=p[:, :], scalar=modc[:, 5, c:c + 1], in1=xT[tb][:, c, :],
                                                                  op0=ALU.mult, op1=ALU.add), reads=[p, modc, xT[tb]], writes=[xT[tb]])
        kb.barrier()

    def final_out(nblk, dst):
        kb.label = 'final'
        with scope() as st:
            sq = alloc(st, "sq", [128, 8, 512], BF16)
            rstd = alloc(st, "rstd", [128, 512], F32)
            tmp2 = [alloc(st, "tmpn%d" % i, [128, 512], F32) for i in range(2)]
            xn = alloc(st, "xn", [128, 8, 512], F32)
            ot = [alloc(st, "ot%d" % i, [128, 1024], F32) for i in range(2)]
            for tb in range(nblk):
                norm_block((sq, rstd, tmp2), tb, nfc, None, xn)
                for tl in range(4):
                    o = ot[tl % 2]
                    for half in range(2):
                        p = ps()
                        for cc in range(4):
                            c = half * 4 + cc
                            kb.op("pe", lambda e: e.matmul(p[:, cc * 128:(cc + 1) * 128], lhsT=xn[:, c, tl * 128:(tl + 1) * 128], rhs=ident_f,
                                                           start=True, stop=True), reads=[xn, cf], writes=[p], inc=(cc == 3))
                        kb.op("act", lambda e: e.activation(out=o[:, half * 512:(half + 1) * 512], in_=p[:, :], func=AF.Copy), reads=[p], writes=[o])
                    t0 = (tb * 4 + tl) * 128
                    kb.dma("sp", dst[t0:t0 + 128, :], o[:], reads=[o])
        kb.barrier()


    bd64_b = cb.t[:, 768:896]
    rm_b = cb.t[:, 896:1024]

    def proj_tm(wb, wv, s0, n, tb, tl, p):
        for k in range(8):
            kb.op("pe", lambda e: e.matmul(p[:, 0:n], lhsT=hT[tb][:, k, tl * 128:(tl + 1) * 128], rhs=wv[:, k, s0:s0 + n],
                                           start=(k == 0), stop=(k == 7)), reads=[wb, hT[tb]], writes=[p], inc=(k == 7))

    def load_wo(l, row0):
        kb.dma("pool", wo_buf[:], D["w_out"][l][row0:row0 + 512, :].rearrange("(k p) n -> p k n", p=128), writes=[wo_buf])

    def mixer_out(st, ytm, tb, yT):
        for c in range(4):
            p = ps()
            for tl in range(4):
                kb.op("pe", lambda e: e.matmul(p[:, tl * 128:(tl + 1) * 128], lhsT=ytm[:, tl, c * 128:(c + 1) * 128], rhs=ident_b,
                                               start=True, stop=True), reads=[ytm, cb], writes=[p], inc=(tl == 3))
            kb.op("act", lambda e: e.activation(out=yT[:, c, :], in_=p[:, :], func=AF.Copy), reads=[p], writes=[yT])
        for c in range(8):
            p = ps()
            for k in range(4):
                kb.op("pe", lambda e: e.matmul(p[:, :], lhsT=wo_buf[:, k, c * 128:(c + 1) * 128], rhs=yT[:, k, :],
                                               start=(k == 0), stop=(k == 3)), reads=[wo_buf, yT], writes=[p], inc=(k == 3))
            kb.op("dve", lambda e: e.scalar_tensor_tensor(out=xT[tb][:, c, :], in0=p[:, :], scalar=modc[:, 2, c:c + 1], in1=xT[tb][:, c, :],
                                                          op0=ALU.mult, op1=ALU.add), reads=[p, modc, xT[tb]], writes=[xT[tb]])

    def attention(l, g, kind, nblk):
        kb.label = 'attn_%s_g%d' % (kind, g)
        sample = (g == 1)
        L = 2048 if sample else 256
        nseq = 1 if sample else 2
        nctx = 2 if sample else 0
        lt = L // 128
        nkt = lt + nctx
        ntok = nblk * 512
        if kind == "d":
            qc0, kc0, vc0, nkc, nvh, ve, orow0, nheads = 4896, 5408, 5536, 1, 2, 64, 1536, 8
            ck, cv, ok, ov = D["cgk"], D["cgv"], D["ngk"], D["ngv"]
        else:
            qc0, kc0, vc0, nkc, nvh, ve, orow0, nheads = 1296, 1808, 2320, 4, 4, 128, 512, 4
            ck, cv, ok, ov = D["cdk"], D["cdv"], D["ndk"], D["ndv"]
        scale = 64 ** -0.5
        kw = nkc * 128
        vw = nvh * ve
        lam_init = 0.8 - 0.6 * math.exp(-0.3 * l)
        with scope() as st:
            qT = alloc(st, "qT", [128, 4, ntok], BF16)
            kT = alloc(st, "kT", [128, nkc, nseq * nkt * 128], BF16)
            vsw = ve + 1 if kind == "d" else ve
            vaug = alloc(st, "vaug", [128, nseq * nkt, nvh, vsw], BF16)
            vodd = alloc(st, "vodd", [128, nseq * nkt, nvh, 128], BF16) if kind == "d" else None
            yT = alloc(st, "yT", [128, 4, 512], BF16)
            pTs = [alloc(st, "pT%d" % i, [128, 512], BF16) for i in range(4)]
            sqb = alloc(st, "sqb", [128, 512], BF16)
            rs = alloc(st, "rs", [128, 512], F32)
            qn = alloc(st, "qn", [128, 512], BF16)
            t1 = alloc(st, "t1", [128, 512], F32)
            fin_bufs = [(rs, t1, None, sqb)]
            gcol = alloc(st, "gcol", [128, 2], F32)
            osb = alloc(st, "osb", [128, 512], F32)
            t2 = osb
            fin_bufs[0] = (rs, t1, osb, sqb)
            sm = alloc(st, "sm", [128, 16], F32)
            lamt = alloc(st, "lamt", [128, 4, 64], F32)
            kng = alloc(st, "kng", [128, 64], F32)
            ropeT = alloc(st, "ropeT", [128, 2, 2048], BF16) if sample else None
            if sample and kind == "b":
                pass
            else:
                try:
                    fin_bufs.append((alloc(st, "rs2", [128, 512], F32), alloc(st, "t12", [128, 512], F32),
                                     alloc(st, "osb2", [128, 512], F32) if kind == "b" else None, alloc(st, "sqb2", [128, 512], BF16) if kind == "b" else None))
                except AssertionError:
                    pass
            load_wo(l, orow0)
            if kind == "d":
                kb.op("dve", lambda e: e.memset(vaug[:, :, :, ve:ve + 1], 1.0), writes=[vaug])
                kb.op("dve", lambda e: e.memset(vodd[:, :, :, 0:1], 1.0), writes=[vodd])
                kb.op("dve", lambda e: e.memset(vodd[:, :, :, 1:64], 0.0), writes=[vodd])
            if sample:
                kb.dma("pool", ropeT[:], D["rope"], writes=[ropeT])
            if kind == "d":
                for j, nm in enumerate(("gqa_q_norm", "gqa_k_norm")):
                    for hh in range(2):
                        kb.dma("sp", gcol[hh * 64:(hh + 1) * 64, j:j + 1], D[nm][l].rearrange("(d o) -> d o", o=1), writes=[gcol])
                kb.dma("sp", kng[:], D["gqa_k_norm"][l].partition_broadcast(128), writes=[kng])
            else:
                for j, nm in enumerate(("diff_lq1", "diff_lk1", "diff_lq2", "diff_lk2")):
                    kb.dma("sp", lamt[:, j, :], D[nm][l].partition_broadcast(128), writes=[lamt])
                kb.op("dve", lambda e: e.tensor_tensor(out=lamt[:, 0, :], in0=lamt[:, 0, :], in1=lamt[:, 1, :], op=ALU.mult), reads=[lamt], writes=[lamt])
                kb.op("dve", lambda e: e.tensor_tensor(out=lamt[:, 2, :], in0=lamt[:, 2, :], in1=lamt[:, 3, :], op=ALU.mult), reads=[lamt], writes=[lamt])
                kb.op("dve", lambda e: e.tensor_reduce(out=sm[:, 2:3], in_=lamt[:, 0, :], axis=AX.X, op=ALU.add), reads=[lamt], writes=[sm])
                kb.op("dve", lambda e: e.tensor_reduce(out=sm[:, 3:4], in_=lamt[:, 2, :], axis=AX.X, op=ALU.add), reads=[lamt], writes=[sm])
                kb.op("act", lambda e: e.activation(out=sm[:, 2:4], in_=sm[:, 2:4], func=AF.Exp), reads=[sm], writes=[sm])
                kb.op("dve", lambda e: e.tensor_tensor(out=sm[:, 0:1], in0=sm[:, 2:3], in1=sm[:, 3:4], op=ALU.subtract), reads=[sm], writes=[sm])
                kb.op("dve", lambda e: e.tensor_scalar(out=sm[:, 1:2], in0=sm[:, 0:1], scalar1=lam_init, scalar2=-1.0, op0=ALU.add, op1=ALU.mult), reads=[sm], writes=[sm])

            def qk_post(p, dst, tb, normj):
                src = p
                if kind == "d":
                    kb.op("act", lambda e: e.activation(out=sqb[:], in_=p[:, :], func=AF.Square), reads=[p], writes=[sqb])
                    pn = ps()
                    kb.op("pe", lambda e: e.matmul(pn[:, :], lhsT=bd64_b, rhs=sqb[:], start=True, stop=True), reads=[cb, sqb], writes=[pn])
                    kb.op("dve", lambda e: e.tensor_scalar(out=rs[:], in0=pn[:, :], scalar1=1.0 / 64, scalar2=EPS, op0=ALU.mult, op1=ALU.add), reads=[pn], writes=[rs])
                    kb.op("act", lambda e: e.activation(out=rs[:], in_=rs[:], func=AF.Ln), reads=[rs], writes=[rs])
                    kb.op("act", lambda e: e.activation(out=rs[:], in_=rs[:], func=AF.Exp, scale=-0.5), reads=[rs], writes=[rs])
                    tgt = qn if sample else None
                    o_ap = qn[:] if sample else dst
                    kb.op("dve", lambda e: e.scalar_tensor_tensor(out=o_ap, in0=p[:, :], scalar=gcol[:, normj:normj + 1], in1=rs[:], op0=ALU.mult, op1=ALU.mult),
                          reads=[p, gcol, rs], writes=[qn if sample else dst_buf[0]])
                else:
                    o_ap = qn[:] if sample else dst
                    kb.op("act", lambda e: e.activation(out=o_ap, in_=p[:, :], func=AF.Copy), reads=[p], writes=[qn if sample else dst_buf[0]])
                if sample:
                    pr = ps()
                    kb.op("pe", lambda e: e.matmul(pr[:, :], lhsT=rm_b, rhs=qn[:], start=True, stop=True), reads=[cb, qn], writes=[pr])
                    kb.op("dve", lambda e: e.tensor_tensor(out=t1[:], in0=qn[:], in1=ropeT[:, 0, tb * 512:(tb + 1) * 512], op=ALU.mult), reads=[qn, ropeT], writes=[t1])
                    kb.op("dve", lambda e: e.tensor_tensor(out=t2[:], in0=pr[:, :], in1=ropeT[:, 1, tb * 512:(tb + 1) * 512], op=ALU.mult), reads=[pr, ropeT], writes=[t2])
                    kb.op("dve", lambda e: e.tensor_tensor(out=dst, in0=t1[:], in1=t2[:], op=ALU.add), reads=[t1, t2], writes=[dst_buf[0]])

            dst_buf = [None]
            blocks = list(range(nblk))
            if kind == "d":
                pcs = []
                for j in range(4):
                    for hh in (j, 4 + j):
                        pcs.append((D["w_in"][l][:, qc0 + hh * 64:qc0 + (hh + 1) * 64], 64))
                wb, wv = wload(pcs, 8)
            else:
                wb, wv = wload([(D["w_in"][l][:, qc0:qc0 + 512], 512)], 8)
            dst_buf[0] = qT
            for j in range(4):
                for tb in blocks:
                    p = ps()
                    for k in range(8):
                        lh = wv[:, k, j * 128:(j + 1) * 128]
                        kb.op("pe", lambda e: e.matmul(p[:, :], lhsT=lh, rhs=hT[tb][:, k, :], start=(k == 0), stop=(k == 7)),
                              reads=[wb, hT[tb]], writes=[p], inc=(k == 7))
                    qk_post(p, qT[:, j, tb * 512:(tb + 1) * 512], tb, 0)
            wb, wv = wload([(D["w_in"][l][:, kc0:kc0 + kw], kw)], 8)
            dst_buf[0] = kT
            for j in range(nkc):
                for tb in blocks:
                    p = ps()
                    for k in range(8):
                        kb.op("pe", lambda e: e.matmul(p[:, :], lhsT=wv[:, k, j * 128:(j + 1) * 128], rhs=hT[tb][:, k, :], start=(k == 0), stop=(k == 7)),
                              reads=[wb, hT[tb]], writes=[p], inc=(k == 7))
                    qk_post(p, kT[:, j, nctx * 128 + tb * 512:nctx * 128 + (tb + 1) * 512], tb, 1)
            if not sample:
                for tt in range(ntok // 128):
                    sq_, r0 = divmod(tt, lt)
                    p = ps()
                    proj_tm(wb, wv, 0, kw, tt // 4, tt % 4, p)
                    kb.op("act", lambda e: e.activation(out=osb[:, 0:kw], in_=p[:, 0:kw], func=AF.Copy), reads=[p], writes=[osb])
                    if kind == "d":
                        kb.op("dve", lambda e: e.tensor_tensor(out=t1[:, 0:128], in0=osb[:, 0:128], in1=osb[:, 0:128], op=ALU.mult), reads=[osb], writes=[t1])
                        kb.op("dve", lambda e: e.tensor_reduce(out=sm[:, 8:10], in_=t1[:, 0:128].rearrange("p (h d) -> p h d", d=64), axis=AX.X, op=ALU.add), reads=[t1], writes=[sm])
                        kb.op("dve", lambda e: e.tensor_scalar(out=sm[:, 8:10], in0=sm[:, 8:10], scalar1=1.0 / 64, scalar2=EPS, op0=ALU.mult, op1=ALU.add), reads=[sm], writes=[sm])
                        kb.op("act", lambda e: e.activation(out=sm[:, 8:10], in_=sm[:, 8:10], func=AF.Ln), reads=[sm], writes=[sm])
                        kb.op("act", lambda e: e.activation(out=sm[:, 8:10], in_=sm[:, 8:10], func=AF.Exp, scale=-0.5), reads=[sm], writes=[sm])
                        kb.op("dve", lambda e: e.tensor_tensor(out=t1[:, 0:128].rearrange("p (h d) -> p h d", d=64), in0=osb[:, 0:128].rearrange("p (h d) -> p h d", d=64),
                                                               in1=bcast(sm[:, 8:10], [128, 2, 64], 2), op=ALU.mult), reads=[osb, sm], writes=[t1])
                        kb.op("dve", lambda e: e.tensor_tensor(out=t2[:, 0:128].rearrange("p (h d) -> p h d", d=64), in0=t1[:, 0:128].rearrange("p (h d) -> p h d", d=64),
                                                               in1=bcast(kng[:], [128, 2, 64], 1), op=ALU.mult), reads=[t1, kng], writes=[t2])
                        kb.dma("sp", ok[sq_, l, r0 * 128:(r0 + 1) * 128, :], t2[:, 0:128], reads=[t2])
                    else:
                        kb.dma("sp", ok[sq_, l, r0 * 128:(r0 + 1) * 128, :], osb[:, 0:kw], reads=[osb])
            wb, wv = wload([(D["w_in"][l][:, vc0:vc0 + vw], vw)], 8)
            for tt in range(ntok // 128):
                sq_, r0 = divmod(tt, lt)
                ktile = sq_ * nkt + nctx + r0
                p = ps()
                proj_tm(wb, wv, 0, vw, tt // 4, tt % 4, p)
                kb.op("act", lambda e: e.activation(out=vaug[:, ktile, :, 0:ve], in_=p[:, 0:vw].rearrange("p (h e) -> p h e", e=ve), func=AF.Copy), reads=[p], writes=[vaug])
                if kind == "d":
                    kb.op("act", lambda e: e.activation(out=vodd[:, ktile, :, 64:128], in_=p[:, 0:vw].rearrange("p (h e) -> p h e", e=ve), func=AF.Copy), reads=[p], writes=[vodd])
                if not sample:
                    kb.op("dve", lambda e: e.tensor_copy(out=osb[:, 0:vw], in_=p[:, 0:vw]), reads=[p], writes=[osb])
                    kb.dma("sp", ov[sq_, l, r0 * 128:(r0 + 1) * 128, :], osb[:, 0:vw], reads=[osb])
            if sample:
                with scope() as st2:
                    ctxk = alloc(st2, "ctxk", [128, kw], F32)
                    for t in range(2):
                        kb.dma("sp", ctxk[:], ck[l][t * 128:(t + 1) * 128, :], writes=[ctxk])
                        for j in range(nkc):
                            p = ps()
                            kb.op("pe", lambda e: e.matmul(p[:, 0:128], lhsT=ctxk[:, j * 128:(j + 1) * 128], rhs=ident_f, start=True, stop=True),
                                  reads=[ctxk, cf], writes=[p])
                            kb.op("act", lambda e: e.activation(out=kT[:, j, t * 128:(t + 1) * 128], in_=p[:, 0:128], func=AF.Copy), reads=[p], writes=[kT])
                    for t in range(2):
                        kb.dma("pool", vaug[:, t, :, 0:ve], cv[l][t * 128:(t + 1) * 128, :].rearrange("p (h e) -> p h e", e=ve), writes=[vaug])
                        if kind == "d":
                            kb.dma("pool", vodd[:, t, :, 64:128], cv[l][t * 128:(t + 1) * 128, :].rearrange("p (h e) -> p h e", e=ve), writes=[vodd])
                    kb.barrier(include_pool_dma=True)
            qblk = min(L, 512)
            pti = 0
            for s_ in range(nseq):
                for qb in range(L // qblk):
                    q0 = s_ * L + qb * qblk
                    qs = slice(q0, q0 + qblk)
                    yc = slice(q0 % 512, q0 % 512 + qblk)
                    its = []
                    if kind == "d":
                        for hp in range(4):
                            for kt in range(nkt):
                                its.append((hp, kt, 0))
                                its.append((hp + 4, kt, 0))
                    else:
                        for h in range(nheads):
                            for kt in range(nkt):
                                its.append((h, kt, 0))
                                its.append((h, kt, 1))
                    hstate = {}
                    pts = {}
                    DEPTH = 2

                    def stageA2(j):
                        pss = []
                        for i in (2 * j, 2 * j + 1):
                            h, kt, r = its[i]
                            kc = (s_ * nkt + kt) * 128
                            if kind == "d":
                                rows, qch = (h // 4) * 64, h % 4
                                lhs = kT[rows:rows + 64, 0, kc:kc + 128]
                                rh = qT[rows:rows + 64, qch, qs]
                            else:
                                lhs = kT[r * 64:(r + 1) * 64, h, kc:kc + 128]
                                rh = qT[r * 64:(r + 1) * 64, h, qs]
                            pS = ps()
                            kb.op("pe", lambda e: e.matmul(pS[:, 0:qblk], lhsT=lhs, rhs=rh, start=True, stop=True), reads=[kT, qT], writes=[pS], inc=(i % 2 == 1))
                            pss.append(pS)
                        for i, pS in zip((2 * j, 2 * j + 1), pss):
                            pT = pTs[i % 4]
                            kb.op("act", lambda e: e.activation(out=pT[:, 0:qblk], in_=pS[:, 0:qblk], func=AF.Exp, scale=scale), reads=[pS], writes=[pT])
                            pts[i] = pT

                    def stageC(i):
                        h, kt, r = its[i]
                        pT = pts.pop(i)
                        last = (kt == nkt - 1)
                        rs, t1, osb, sqb = fin_bufs[h % len(fin_bufs)]
                        if kind == "d":
                            vh, odd = h // 4, h % 2
                            if kt == 0:
                                hstate[h] = ps_acc()
                            accO = hstate[h]
                            mo = 128 if odd else ve + 1
                            lh = vodd[:, s_ * nkt + kt, vh, :] if odd else vaug[:, s_ * nkt + kt, vh, :]
                            kb.op("pe", lambda e: e.matmul(accO[0:mo, 0:qblk], lhsT=lh, rhs=pT[:, 0:qblk], start=(kt == 0), stop=last),
                                  reads=[pT, vodd if odd else vaug], writes=[accO], inc=True)
                            if last:
                                drow = 0 if odd else 64
                                orow = 64 if odd else 0
                                kb.op("act", lambda e: e.activation(out=rs[drow:drow + 1, 0:qblk], in_=accO[drow:drow + 1, 0:qblk], func=AF.Ln), reads=[accO], writes=[rs])
                                kb.op("act", lambda e: e.activation(out=rs[drow:drow + 1, 0:qblk], in_=rs[drow:drow + 1, 0:qblk], func=AF.Exp, scale=-1.0), reads=[rs], writes=[rs])
                                pB = ps()
                                kb.op("pe", lambda e: e.matmul(pB[:, 0:qblk], lhsT=selm_f[drow:drow + 1, :], rhs=rs[drow:drow + 1, 0:qblk], start=True, stop=True),
                                      reads=[cf, rs], writes=[pB])
                                kb.op("act", lambda e: e.activation(out=t1[orow:orow + 64, 0:qblk], in_=pB[orow:orow + 64, 0:qblk], func=AF.Copy), reads=[pB], writes=[t1])
                                kb.op("dve", lambda e: e.tensor_tensor(out=yT[orow:orow + 64, h // 2, yc], in0=accO[orow:orow + 64, 0:qblk], in1=t1[orow:orow + 64, 0:qblk], op=ALU.mult),
                                      reads=[accO, t1], writes=[yT])
                        else:
                            if kt == 0 and r == 0:
                                hstate[h] = ([ps_acc(), ps_acc()], [ps_acc(), ps_acc()])
                            accO, accD = hstate[h]
                            kb.op("pe", lambda e: e.matmul(accO[r][:, 0:qblk], lhsT=vaug[:, s_ * nkt + kt, h, :], rhs=pT[:, 0:qblk], start=(kt == 0), stop=last),
                                  reads=[pT, vaug], writes=[accO[r]], inc=False)
                            kb.op("pe", lambda e: e.matmul(accD[r][:, 0:qblk], lhsT=ones_b, rhs=pT[:, 0:qblk], start=(kt == 0), stop=last),
                                  reads=[pT, cb], writes=[accD[r]], inc=True)
                            if last and r == 1:
                                A, Bt, O = rs, t1, osb
                                kb.op("act", lambda e: e.activation(out=A[:, 0:qblk], in_=accD[0][:, 0:qblk], func=AF.Ln), reads=[accD[0]], writes=[A])
                                kb.op("act", lambda e: e.activation(out=A[:, 0:qblk], in_=A[:, 0:qblk], func=AF.Exp, scale=-1.0), reads=[A], writes=[A])
                                kb.op("act", lambda e: e.activation(out=Bt[:, 0:qblk], in_=accD[1][:, 0:qblk], func=AF.Ln), reads=[accD[1]], writes=[Bt])
                                kb.op("act", lambda e: e.activation(out=Bt[:, 0:qblk], in_=Bt[:, 0:qblk], func=AF.Exp, scale=-1.0), reads=[Bt], writes=[Bt])
                                kb.op("dve", lambda e: e.tensor_tensor(out=O[:, 0:qblk], in0=accO[0][:, 0:qblk], in1=A[:, 0:qblk], op=ALU.mult), reads=[accO[0], A], writes=[O])
                                kb.op("dve", lambda e: e.tensor_tensor(out=Bt[:, 0:qblk], in0=accO[1][:, 0:qblk], in1=Bt[:, 0:qblk], op=ALU.mult), reads=[accO[1], Bt], writes=[Bt])
                                kb.op("dve", lambda e: e.scalar_tensor_tensor(out=O[:, 0:qblk], in0=Bt[:, 0:qblk], scalar=sm[:, 1:2], in1=O[:, 0:qblk], op0=ALU.mult, op1=ALU.add),
                                      reads=[Bt, sm, O], writes=[O])
                                kb.op("act", lambda e: e.activation(out=sqb[:, 0:qblk], in_=O[:, 0:qblk], func=AF.Square), reads=[O], writes=[sqb])
                                pn = ps()
                                kb.op("pe", lambda e: e.matmul(pn[:, 0:qblk], lhsT=ones_b, rhs=sqb[:, 0:qblk], start=True, stop=True), reads=[cb, sqb], writes=[pn])
                                kb.op("dve", lambda e: e.tensor_scalar(out=A[:, 0:qblk], in0=pn[:, 0:qblk], scalar1=1.0 / 128, scalar2=EPS, op0=ALU.mult, op1=ALU.add), reads=[pn], writes=[A])
                                kb.op("act", lambda e: e.activation(out=A[:, 0:qblk], in_=A[:, 0:qblk], func=AF.Ln), reads=[A], writes=[A])
                                kb.op("act", lambda e: e.activation(out=A[:, 0:qblk], in_=A[:, 0:qblk], func=AF.Exp, scale=-0.5), reads=[A], writes=[A])
                                kb.op("dve", lambda e: e.scalar_tensor_tensor(out=yT[:, h, yc], in0=O[:, 0:qblk], scalar=1.0 - lam_init, in1=A[:, 0:qblk], op0=ALU.mult, op1=ALU.mult),
                                      reads=[O, A], writes=[yT])

                    n_pairs = len(its) // 2
                    for j in range(n_pairs + 1):
                        if j < n_pairs:
                            stageA2(j)
                        if j >= 1:
                            stageC(2 * (j - 1))
                            stageC(2 * (j - 1) + 1)
                    if (q0 + qblk) % 512 == 0:
                        tb = (q0 + qblk) // 512 - 1
                        for c in range(8):
                            p = ps()
                            for k in range(4):
                                kb.op("pe", lambda e: e.matmul(p[:, :], lhsT=wo_buf[:, k, c * 128:(c + 1) * 128], rhs=yT[:, k, :],
                                                               start=(k == 0), stop=(k == 3)), reads=[wo_buf, yT], writes=[p], inc=(k == 3))
                            kb.op("dve", lambda e: e.scalar_tensor_tensor(out=xT[tb][:, c, :], in0=p[:, :], scalar=modc[:, 2, c:c + 1], in1=xT[tb][:, c, :],
                                                                          op0=ALU.mult, op1=ALU.add), reads=[p, modc, xT[tb]], writes=[xT[tb]])
        kb.barrier()

    triF_f = cf.t[:, 256:384]
    triB_f = cf.t[:, 384:512]
    maskF_b = cb.t[:, 512:640]
    maskB_b = cb.t[:, 640:768]

    def proj_fm(l, c0, ncol, blocks, fn):
        for t0 in range(0, ncol, 512):
            n = min(512, ncol - t0)
            wb, wv = wload([(D["w_in"][l][:, c0 + t0:c0 + t0 + n], n)], 8)
            for s0 in range(0, n, 128):
                w = min(128, n - s0)
                for tb in blocks:
                    p = ps()
                    for k in range(8):
                        kb.op("pe", lambda e: e.matmul(p[0:w, :], lhsT=wv[:, k, s0:s0 + w], rhs=hT[tb][:, k, :], start=(k == 0), stop=(k == 7)),
                              reads=[wb, hT[tb]], writes=[p], inc=(k == 7))
                    fn((t0 + s0) // 128, tb, p, w)

    def conv_chunk(convin, acc, cwt, cbt, cc, nseq, L, dst_ap_fn, dst_buf):
        for s_ in range(nseq):
            a = acc[:, s_ * L:(s_ + 1) * L]
            kb.op("dve", lambda e: e.tensor_scalar(out=a, in0=convin[:, s_, 0:L], scalar1=cwt[:, cc, 0:1], scalar2=None, op0=ALU.mult),
                  reads=[convin, cwt], writes=[acc])
            for tap in range(1, 5):
                kb.op("dve", lambda e: e.scalar_tensor_tensor(out=a, in0=convin[:, s_, tap:tap + L], scalar=cwt[:, cc, tap:tap + 1], in1=a,
                                                              op0=ALU.mult, op1=ALU.add), reads=[convin, cwt, acc], writes=[acc])
            if isinstance(dst_buf, list):
                for gg in range(2):
                    kb.op("act", lambda e: e.activation(out=dst_buf[gg][gg * 64:(gg + 1) * 64, s_ * L:(s_ + 1) * L], in_=acc[gg * 64:(gg + 1) * 64, s_ * L:(s_ + 1) * L],
                                                        func=AF.Silu, bias=cbt[gg * 64:(gg + 1) * 64, cc:cc + 1]), reads=[acc, cbt], writes=[dst_buf[gg]])
            else:
                kb.op("act", lambda e: e.activation(out=dst_ap_fn(s_), in_=a, func=AF.Silu, bias=cbt[:, cc:cc + 1]), reads=[acc, cbt], writes=[dst_buf])

    def evac_conv_in(convin, p, tb, nseq, L):
        if nseq == 1:
            kb.op("act", lambda e: e.activation(out=convin[:, 0, 2 + tb * 512:2 + (tb + 1) * 512], in_=p[:, :], func=AF.Copy), reads=[p], writes=[convin])
        else:
            for s_ in range(2):
                kb.op("act", lambda e: e.activation(out=convin[:, s_, 2:2 + L], in_=p[:, s_ * L:(s_ + 1) * L], func=AF.Copy), reads=[p], writes=[convin])

    def transpose_to_tm(src, dst_fn, dst_buf, T):
        for t0 in range(0, T, 4):
            p = ps()
            for j in range(4):
                kb.op("pe", lambda e: e.matmul(p[:, j * 128:(j + 1) * 128], lhsT=src[:, (t0 + j) * 128:(t0 + j + 1) * 128], rhs=ident_b, start=True, stop=True),
                      reads=[src, cb], writes=[p], inc=(j == 3))
            kb.op("act", lambda e: e.activation(out=dst_fn(t0, 4), in_=p[:, :].rearrange("p (t c) -> p t c", t=4), func=AF.Copy), reads=[p], writes=[dst_buf])

    def ssd(l, g, nblk):
        kb.label = 'ssd_g%d' % g
        sample = (g == 1)
        L = 2048 if sample else 256
        nseq = 1 if sample else 2
        lt = L // 128
        ntok = nblk * 512
        T = ntok // 128
        blocks = list(range(nblk))
        with scope() as st:
            xtm = alloc(st, "xtm", [128, T, 512], BF16)
            Btm = alloc(st, "Btm", [128, T, 128], BF16)
            BT = alloc(st, "BT", [128, ntok], BF16)
            CTz = [alloc(st, "CT%d" % i, [128, ntok], BF16) for i in range(2)]
            for i_ in range(2):
                kb.op("dve", lambda e: e.memset(CTz[i_][:], 0.0), writes=[CTz[i_]])
            dtt = alloc(st, "dtt", [128, T, 16], F32)
            dtA = alloc(st, "dtA", [128, T, 16], F32)
            cum = alloc(st, "cum", [128, T, 16], F32)
            tot = alloc(st, "tot", [128, T, 16], F32)
            ecum = alloc(st, "ecum", [128, T, 16], F32)
            wd = alloc(st, "wd", [128, T, 16], F32)
            bj = alloc(st, "bj", [128, T, 16], F32)
            dec = alloc(st, "dec", [128, T, 2, 4], F32)
            abc = alloc(st, "abc", [128, 16], F32)
            dtb = alloc(st, "dtb", [128, 16], F32)
            dsk = alloc(st, "dsk", [128, 8], F32)
            ng = alloc(st, "ng", [128, 512], F32)
            cwt = alloc(st, "cwt", [128, 6, 5], F32)
            cbt = alloc(st, "cbt", [128, 6], F32)
            load_wo(l, 0)
            kb.dma("sp", abc[:], D["ssd_a_log"][l].partition_broadcast(128), writes=[abc])
            kb.dma("sp", dtb[:], D["ssd_dt_bias"][l].partition_broadcast(128), writes=[dtb])
            kb.dma("sp", dsk[:], D["ssd_d"][l].partition_broadcast(128), writes=[dsk])
            kb.dma("sp", ng[:], D["ssd_norm"][l].partition_broadcast(128), writes=[ng])
            for tap in range(5):
                kb.dma("sp", cwt[:, :, tap], D["conv_ssd_w"][l, tap].rearrange("(c p) -> p c", p=128), writes=[cwt], allow_slow_non_contiguous=True)
            kb.dma("sp", cbt[:], D["conv_ssd_b"][l].rearrange("(c p) -> p c", p=128), writes=[cbt], allow_slow_non_contiguous=True)
            kb.op("act", lambda e: e.activation(out=abc[:], in_=abc[:], func=AF.Exp), reads=[abc], writes=[abc])
            kb.op("dve", lambda e: e.tensor_scalar(out=abc[:], in0=abc[:], scalar1=-1.0, scalar2=None, op0=ALU.mult), reads=[abc], writes=[abc])
            with scope() as st1:
                convins = [alloc(st1, "convin", [128, nseq, L + 4], F32) for _ in range(2)]
                acc = alloc(st1, "cacc", [128, ntok], F32)
                xcT = alloc(st1, "xcT", [128, ntok], BF16)
                for cv_ in convins:
                    kb.op("dve", lambda e: e.memset(cv_[:], 0.0), writes=[cv_])

                def cb_fn(ci, tb, p, w):
                    convin = convins[ci % 2]
                    evac_conv_in(convin, p, tb, nseq, L)
                    if tb != blocks[-1]:
                        return
                    if ci < 4:
                        conv_chunk(convin, acc, cwt, cbt, ci, nseq, L, lambda s_: xcT[:, s_ * L:(s_ + 1) * L], xcT)
                        transpose_to_tm(xcT, lambda t0, n: xtm[:, t0:t0 + n, ci * 128:(ci + 1) * 128], xtm, T)
                    elif ci == 4:
                        conv_chunk(convin, acc, cwt, cbt, ci, nseq, L, lambda s_: BT[:, s_ * L:(s_ + 1) * L], BT)
                        transpose_to_tm(BT, lambda t0, n: Btm[:, t0:t0 + n, :], Btm, T)
                    else:
                        conv_chunk(convin, acc, cwt, cbt, ci, nseq, L, None, CTz)
                proj_fm(l, 512, 768, blocks, cb_fn)
            kb.barrier()
            if SSD_PH < 2:
                return
            wb, wv = wload([(D["w_in"][l][:, 1280:1296], 16)], 8)
            for tt in range(T):
                p = ps()
                proj_tm(wb, wv, 0, 16, tt // 4, tt % 4, p)
                kb.op("dve", lambda e: e.tensor_tensor(out=dtt[:, tt, :], in0=p[:, 0:16], in1=dtb[:], op=ALU.add), reads=[p, dtb], writes=[dtt])
            kb.op("act", lambda e: e.activation(out=dtt[:], in_=dtt[:], func=AF.Exp), reads=[dtt], writes=[dtt])
            kb.op("act", lambda e: e.activation(out=dtt[:], in_=dtt[:], func=AF.Ln, bias=1.0), reads=[dtt], writes=[dtt])
            kb.op("dve", lambda e: e.tensor_tensor(out=dtA[:], in0=dtt[:], in1=bcast(abc[:], [128, T, 16], 1), op=ALU.mult), reads=[dtt, abc], writes=[dtA])
            for tt in range(T):
                p = ps()
                kb.op("pe", lambda e: e.matmul(p[:, 0:16], lhsT=triF_f, rhs=dtA[:, tt, :], start=True, stop=True), reads=[cf, dtA], writes=[p], inc=False)
                kb.op("pe", lambda e: e.matmul(p[:, 16:32], lhsT=triB_f, rhs=dtA[:, tt, :], start=True, stop=True), reads=[cf, dtA], writes=[p], inc=False)
                kb.op("pe", lambda e: e.matmul(p[:, 32:48], lhsT=ones_f, rhs=dtA[:, tt, :], start=True, stop=True), reads=[cf, dtA], writes=[p])
                kb.op("dve", lambda e: e.tensor_copy(out=cum[:, tt, 0:8], in_=p[:, 0:8]), reads=[p], writes=[cum])
                kb.op("dve", lambda e: e.tensor_copy(out=cum[:, tt, 8:16], in_=p[:, 24:32]), reads=[p], writes=[cum])
                kb.op("dve", lambda e: e.tensor_copy(out=tot[:, tt, :], in_=p[:, 32:48]), reads=[p], writes=[tot])
            kb.op("act", lambda e: e.activation(out=ecum[:], in_=cum[:], func=AF.Exp), reads=[cum], writes=[ecum])
            kb.op("dve", lambda e: e.tensor_tensor(out=wd[:], in0=tot[:], in1=cum[:], op=ALU.subtract), reads=[tot, cum], writes=[wd])
            kb.op("act", lambda e: e.activation(out=wd[:], in_=wd[:], func=AF.Exp), reads=[wd], writes=[wd])
            kb.op("dve", lambda e: e.tensor_tensor(out=wd[:], in0=wd[:], in1=dtt[:], op=ALU.mult), reads=[wd, dtt], writes=[wd])
            kb.op("act", lambda e: e.activation(out=bj[:], in_=dtt[:], func=AF.Ln), reads=[dtt], writes=[bj])
            kb.op("dve", lambda e: e.tensor_tensor(out=bj[:], in0=bj[:], in1=cum[:], op=ALU.subtract), reads=[bj, cum], writes=[bj])
            tot4 = tot[:].rearrange("p t (d h) -> p t d h", d=2)
            for gg in range(2):
                kb.op("act", lambda e: e.activation(out=dec[gg * 64:(gg + 1) * 64], in_=tot4[gg * 64:(gg + 1) * 64, :, :, gg * 4:(gg + 1) * 4], func=AF.Exp),
                      reads=[tot], writes=[dec])
            if SSD_PH < 3:
                kb.barrier()
                return
            with scope() as st2:
                Hprev = alloc(st2, "Hprev", [128, T, 2, 256], BF16)
                st3 = ExitStack()
                Hs = [alloc(st3, "Hs%d" % i, [128, 256], F32) for i in range(2)]
                xws = [alloc(st3, "xw%d" % i, [128, 512], BF16) for i in range(2)]
                hx = alloc(st3, "hx", [128, 2, 128], F32)
                ho = alloc(st3, "ho", [128, 128], F32)
                xi = 0
                for dr in range(2):
                    H = Hs[dr]
                    for s_ in range(nseq):
                        if sample:
                            for blk in range(2):
                                for two in range(2):
                                    kb.dma("sp", hx[two * 64:(two + 1) * 64, blk, :].rearrange("p (g n) -> p g n", g=2),
                                           D["sssm"][l, dr].rearrange("(g r) p n -> r p g n", g=2)[blk * 2 + two], writes=[hx])
                            for blk in range(2):
                                p = ps()
                                kb.op("pe", lambda e: e.matmul(p[:, 0:128], lhsT=hx[:, blk, :], rhs=ident_f, start=True, stop=True), reads=[hx, cf], writes=[p])
                                kb.op("act", lambda e: e.activation(out=H[:, blk * 128:(blk + 1) * 128], in_=p[:, 0:128], func=AF.Copy), reads=[p], writes=[H])
                        else:
                            kb.op("dve", lambda e: e.memset(H[:], 0.0), writes=[H])
                        order = range(lt) if dr == 0 else range(lt - 1, -1, -1)
                        for r0 in order:
                            tt = s_ * lt + r0
                            kb.op("act", lambda e: e.activation(out=Hprev[:, tt, dr, :], in_=H[:], func=AF.Copy), reads=[H], writes=[Hprev])
                            xw = xws[xi % 2]
                            xi += 1
                            kb.op("dve", lambda e: e.tensor_tensor(out=xw[:].rearrange("p (h d) -> p h d", d=64), in0=xtm[:, tt, :].rearrange("p (h d) -> p h d", d=64),
                                                                   in1=bcast(wd[:, tt, dr * 8:(dr + 1) * 8], [128, 8, 64], 2), op=ALU.mult), reads=[xtm, wd], writes=[xw])
                            p = ps()
                            kb.op("pe", lambda e: e.matmul(p[:, :], lhsT=Btm[:, tt, :], rhs=xw[:], start=True, stop=True), reads=[Btm, xw], writes=[p])
                            kb.op("dve", lambda e: e.tensor_tensor(out=H[:].rearrange("p (h d) -> p h d", d=64), in0=H[:].rearrange("p (h d) -> p h d", d=64),
                                                                   in1=bcast(dec[:, tt, dr, :], [128, 4, 64], 2), op=ALU.mult), reads=[H, dec], writes=[H])
                            for gg in range(2):
                                kb.op("dve", lambda e: e.tensor_tensor(out=H[gg * 64:(gg + 1) * 64, :], in0=H[gg * 64:(gg + 1) * 64, :],
                                                                       in1=p[gg * 64:(gg + 1) * 64, gg * 256:(gg + 1) * 256], op=ALU.add), reads=[H, p], writes=[H])
                        if not sample:
                            for blk in range(2):
                                p = ps()
                                kb.op("pe", lambda e: e.matmul(p[:, 0:128], lhsT=H[:, blk * 128:(blk + 1) * 128], rhs=ident_f, start=True, stop=True), reads=[H, cf], writes=[p])
                                kb.op("act", lambda e: e.activation(out=ho[:], in_=p[:, 0:128], func=AF.Copy), reads=[p], writes=[ho])
                                for two in range(2):
                                    kb.dma("sp", D["nssm"][s_, l, dr].rearrange("(g r) p n -> r p g n", g=2)[blk * 2 + two],
                                           ho[two * 64:(two + 1) * 64, :].rearrange("p (g n) -> p g n", g=2), reads=[ho])
                kb.barrier()
                st3.close()
                if SSD_PH < 4:
                    return
                Dg = alloc(st2, "Dg", [128, 16, 128], F32)
                Es = [alloc(st2, "E%d" % i, [128, 128], F32) for i in range(3)]
                Ms = [alloc(st2, "M%d" % i, [128, 128], BF16) for i in range(3)]
                ya = alloc(st2, "ya", [128, 512], F32)
                yu = alloc(st2, "yu", [128, 512], F32)
                zs = yu
                ytm = alloc(st2, "ytm", [128, 4, 512], BF16)
                yT = alloc(st2, "yT", [128, 4, 512], BF16)
                ss = alloc(st2, "ss", [128, 4], F32)
                wzb, wzv = wload([(D["w_in"][l][:, 0:512], 512)], 8)
                ei = 0
                for tt in range(T):
                    tk = slice(tt * 128, (tt + 1) * 128)
                    pz = ps_acc()
                    proj_tm(wzb, wzv, 0, 512, tt // 4, tt % 4, pz)
                    pBC = ps_acc()
                    for gg in range(2):
                        kb.op("pe", lambda e: e.matmul(pBC[:, gg * 128:(gg + 1) * 128], lhsT=BT[:, tk], rhs=CTz[gg][:, tk], start=True, stop=True),
                              reads=[BT, CTz[gg]], writes=[pBC], inc=(gg == 1))
                    kb.op("dve", lambda e: e.tensor_tensor(out=Dg[:], in0=bcast(ident_f, [128, 16, 128], 1), in1=bcast(cum[:, tt, :], [128, 16, 128], 2), op=ALU.mult),
                          reads=[cf, cum], writes=[Dg])
                    yint = ps_acc()
                    hd = [(h, dr) for h in range(8) for dr in range(2)]
                    mts = {}

                    def sA(i):
                        h, dr = hd[i]
                        pE = ps()
                        kb.op("pe", lambda e: e.matmul(pE[:, 0:128], lhsT=ones_f, rhs=Dg[:, dr * 8 + h, :], start=True, stop=False), reads=[cf, Dg], writes=[pE], inc=False)
                        kb.op("pe", lambda e: e.matmul(pE[:, 0:128], lhsT=ident_b, rhs=(maskF_b if dr == 0 else maskB_b), start=False, stop=True), reads=[cb], writes=[pE])
                        E = Es[i % 3]
                        M = Ms[i % 3]
                        kb.op("act", lambda e: e.activation(out=E[:], in_=pE[:, 0:128], func=AF.Exp, bias=bj[:, tt, dr * 8 + h:dr * 8 + h + 1]), reads=[pE, bj], writes=[E])
                        gg = h // 4
                        kb.op("dve", lambda e: e.tensor_tensor(out=M[:], in0=E[:], in1=pBC[:, gg * 128:(gg + 1) * 128], op=ALU.mult), reads=[E, pBC], writes=[M])
                        mts[i] = M

                    def sC(i):
                        h, dr = hd[i]
                        M = mts.pop(i)
                        kb.op("pe", lambda e: e.matmul(yint[:, h * 64:(h + 1) * 64], lhsT=M[:], rhs=xtm[:, tt, h * 64:(h + 1) * 64], start=(dr == 0), stop=(dr == 1)),
                              reads=[M, xtm], writes=[yint], inc=True)

                    for i in range(16 + 2):
                        if i < 16:
                            sA(i)
                        if i >= 2:
                            sC(i - 2)
                    if SSD_PH < 5:
                        continue
                    pY = [ps(), ps()]
                    for dr in range(2):
                        for gg in range(2):
                            kb.op("pe", lambda e: e.matmul(pY[dr][:, gg * 256:(gg + 1) * 256], lhsT=CTz[gg][:, tk], rhs=Hprev[:, tt, dr, :], start=True, stop=True),
                                  reads=[CTz[gg], Hprev], writes=[pY[dr]], inc=(gg == 1))
                    v3 = lambda ap: ap.rearrange("p (h d) -> p h d", d=64)
                    kb.op("dve", lambda e: e.tensor_tensor(out=v3(ya[:]), in0=v3(xtm[:, tt, :]), in1=bcast(dsk[:], [128, 8, 64], 2), op=ALU.mult), reads=[xtm, dsk], writes=[ya])
                    kb.op("dve", lambda e: e.tensor_tensor(out=ya[:], in0=ya[:], in1=yint[:, :], op=ALU.add), reads=[ya, yint], writes=[ya])
                    for dr in range(2):
                        kb.op("dve", lambda e: e.tensor_tensor(out=v3(yu[:]), in0=v3(pY[dr][:, :]), in1=bcast(ecum[:, tt, dr * 8:(dr + 1) * 8], [128, 8, 64], 2), op=ALU.mult),
                              reads=[pY[dr], ecum], writes=[yu])
                        kb.op("dve", lambda e: e.tensor_tensor(out=ya[:], in0=ya[:], in1=yu[:], op=ALU.add), reads=[ya, yu], writes=[ya])
                    if SSD_PH < 6:
                        continue
                    kb.op("act", lambda e: e.activation(out=zs[:], in_=pz[:, :], func=AF.Silu), reads=[pz], writes=[zs])
                    kb.op("dve", lambda e: e.tensor_tensor(out=ya[:], in0=ya[:], in1=zs[:], op=ALU.mult), reads=[ya, zs], writes=[ya])
                    kb.op("dve", lambda e: e.memset(ss[:, 0:1], 0.0), writes=[ss])
                    kb.op("act", lambda e: e.activation(out=yu[:], in_=ya[:], func=AF.Square, accum_out=ss[:, 0:1]), reads=[ya, ss], writes=[yu, ss])
                    kb.op("dve", lambda e: e.tensor_scalar(out=ss[:, 1:2], in0=ss[:, 0:1], scalar1=1.0 / 512, scalar2=EPS, op0=ALU.mult, op1=ALU.add), reads=[ss], writes=[ss])
                    kb.op("act", lambda e: e.activation(out=ss[:, 1:2], in_=ss[:, 1:2], func=AF.Ln), reads=[ss], writes=[ss])
                    kb.op("act", lambda e: e.activation(out=ss[:, 2:3], in_=ss[:, 1:2], func=AF.Exp, scale=-0.5), reads=[ss], writes=[ss])
                    kb.op("dve", lambda e: e.scalar_tensor_tensor(out=ytm[:, tt % 4, :], in0=ya[:], scalar=ss[:, 2:3], in1=ng[:], op0=ALU.mult, op1=ALU.mult),
                          reads=[ya, ss, ng], writes=[ytm])
                    if SSD_PH < 7:
                        continue
                    if tt % 4 == 3:
                        mixer_out(st2, ytm, tt // 4, yT)
        kb.barrier()


    def mlstm(l, g, nblk):
        kb.label = 'mlstm_g%d' % g
        sample = (g == 1)
        L = 2048 if sample else 256
        nseq = 1 if sample else 2
        lt = L // 128
        ntok = nblk * 512
        T = ntok // 128
        blocks = list(range(nblk))
        C0 = 2832
        lns = math.log(128 ** -0.5)
        for hg in range(2):
            h0 = hg * 2
            with scope() as st:
                qT = alloc(st, "mqT", [128, 2, ntok], BF16)
                kT = alloc(st, "mkT", [128, 2, ntok], BF16)
                vaug = alloc(st, "mvaug", [128, T, 2, 129], BF16)
                Cpb = alloc(st, "Cpb", [128, T, 2, 129], BF16)
                li = alloc(st, "li", [128, T, 4], F32)
                lf = alloc(st, "lf", [128, T, 4], F32)
                G = alloc(st, "G", [128, T, 4], F32)
                tot = alloc(st, "mtot", [128, T, 4], F32)
                pj = alloc(st, "pj", [128, T, 4], F32)
                pjs = alloc(st, "pjs", [128, T, 4], F32)
                gend = alloc(st, "gend", [128, T, 4], F32)
                mlb = alloc(st, "mlb", [128, T, 4], F32)
                wend = alloc(st, "wend", [128, T, 4], F32)
                mprev = alloc(st, "mprev", [128, T, 4], F32)
                gb = alloc(st, "gb", [128, 8], F32)
                cwt = alloc(st, "mcwt", [128, 4, 5], F32)
                cbt = alloc(st, "mcbt", [128, 4], F32)
                ng = alloc(st, "mng", [128, 256], F32)
                kb.dma("pool", wo_buf[:, 0:2, :], D["w_out"][l][1024 + h0 * 128:1024 + (h0 + 2) * 128, :].rearrange("(k p) n -> p k n", p=128), writes=[wo_buf])
                kb.dma("sp", ng[:], D["mlstm_norm"][l][h0 * 128:(h0 + 2) * 128].partition_broadcast(128), writes=[ng])
                goffs = [0 * 8 + 0 * 4 + h0, 1 * 8 + 0 * 4 + h0, 0 * 8 + 1 * 4 + h0, 1 * 8 + 1 * 4 + h0]
                for i_, go in enumerate(goffs):
                    kb.dma("sp", gb[:, i_ * 2:(i_ + 1) * 2], D["mlstm_gate_b"][l][go:go + 2].partition_broadcast(128), writes=[gb])
                for ci, ch0 in enumerate((h0 * 128, (h0 + 1) * 128, 512 + h0 * 128, 512 + (h0 + 1) * 128)):
                    for tap in range(5):
                        kb.dma("sp", cwt[:, ci, tap:tap + 1], D["conv_mlstm_w"][l, tap][ch0:ch0 + 128].rearrange("(p o) -> p o", o=1), writes=[cwt])
                    kb.dma("sp", cbt[:, ci:ci + 1], D["conv_mlstm_b"][l][ch0:ch0 + 128].rearrange("(p o) -> p o", o=1), writes=[cbt])
                kb.op("dve", lambda e: e.memset(vaug[:, :, :, 128:129], 1.0), writes=[vaug])
                with scope() as st1:
                    convins = [alloc(st1, "mconvin", [128, nseq, L + 4], F32) for _ in range(2)]
                    acc = alloc(st1, "mcacc", [128, ntok], F32)
                    for cv_ in convins:
                        kb.op("dve", lambda e: e.memset(cv_[:], 0.0), writes=[cv_])
                    wb, wv = wload([(D["w_in"][l][:, C0 + h0 * 128:C0 + (h0 + 2) * 128], 256),
                                    (D["w_in"][l][:, C0 + 512 + h0 * 128:C0 + 512 + (h0 + 2) * 128], 256)], 8)
                    for ci in range(4):
                        for tb in blocks:
                            p = ps()
                            for k in range(8):
                                kb.op("pe", lambda e: e.matmul(p[:, :], lhsT=wv[:, k, ci * 128:(ci + 1) * 128], rhs=hT[tb][:, k, :], start=(k == 0), stop=(k == 7)),
                                      reads=[wb, hT[tb]], writes=[p], inc=(k == 7))
                            evac_conv_in(convins[ci % 2], p, tb, nseq, L)
                        convin = convins[ci % 2]
                        dstb = qT if ci < 2 else kT
                        conv_chunk(convin, acc, cwt, cbt, ci, nseq, L, lambda s_: dstb[:, ci % 2, s_ * L:(s_ + 1) * L], dstb)
                kb.barrier()
                wb, wv = wload([(D["w_in"][l][:, C0 + 1024 + h0 * 128:C0 + 1024 + (h0 + 2) * 128], 256)], 8)
                for tt in range(T):
                    p = ps()
                    proj_tm(wb, wv, 0, 256, tt // 4, tt % 4, p)
                    kb.op("act", lambda e: e.activation(out=vaug[:, tt, :, 0:128], in_=p[:, 0:256].rearrange("p (h e) -> p h e", e=128), func=AF.Copy), reads=[p], writes=[vaug])
                gc = C0 + 2048
                wb, wv = wload([(D["w_in"][l][:, gc + go:gc + go + 2], 2) for go in goffs], 8)
                for tt in range(T):
                    p = ps()
                    proj_tm(wb, wv, 0, 8, tt // 4, tt % 4, p)
                    kb.op("dve", lambda e: e.tensor_tensor(out=li[:, tt, :], in0=p[:, 0:4], in1=gb[:, 0:4], op=ALU.add), reads=[p, gb], writes=[li])
                    kb.op("dve", lambda e: e.tensor_tensor(out=lf[:, tt, :], in0=p[:, 4:8], in1=gb[:, 4:8], op=ALU.add), reads=[p, gb], writes=[lf])
                kb.op("act", lambda e: e.activation(out=lf[:], in_=lf[:], func=AF.Exp, scale=-1.0), reads=[lf], writes=[lf])
                kb.op("act", lambda e: e.activation(out=lf[:], in_=lf[:], func=AF.Ln, bias=1.0), reads=[lf], writes=[lf])
                kb.op("dve", lambda e: e.tensor_scalar(out=lf[:], in0=lf[:], scalar1=-1.0, scalar2=None, op0=ALU.mult), reads=[lf], writes=[lf])
                for tt in range(T):
                    p = ps()
                    kb.op("pe", lambda e: e.matmul(p[:, 0:4], lhsT=triF_f, rhs=lf[:, tt, :], start=True, stop=True), reads=[cf, lf], writes=[p], inc=False)
                    kb.op("pe", lambda e: e.matmul(p[:, 4:8], lhsT=triB_f, rhs=lf[:, tt, :], start=True, stop=True), reads=[cf, lf], writes=[p], inc=False)
                    kb.op("pe", lambda e: e.matmul(p[:, 8:12], lhsT=ones_f, rhs=lf[:, tt, :], start=True, stop=True), reads=[cf, lf], writes=[p])
                    kb.op("dve", lambda e: e.tensor_copy(out=G[:, tt, 0:2], in_=p[:, 0:2]), reads=[p], writes=[G])
                    kb.op("dve", lambda e: e.tensor_copy(out=G[:, tt, 2:4], in_=p[:, 6:8]), reads=[p], writes=[G])
                    kb.op("dve", lambda e: e.tensor_copy(out=tot[:, tt, :], in_=p[:, 8:12]), reads=[p], writes=[tot])
                kb.op("dve", lambda e: e.tensor_tensor(out=pj[:], in0=li[:], in1=G[:], op=ALU.subtract), reads=[li, G], writes=[pj])
                kb.op("dve", lambda e: e.tensor_scalar(out=pjs[:], in0=pj[:], scalar1=lns, scalar2=None, op0=ALU.add), reads=[pj], writes=[pjs])
                kb.op("dve", lambda e: e.tensor_tensor(out=gend[:], in0=pj[:], in1=tot[:], op=ALU.add), reads=[pj, tot], writes=[gend])
                with scope() as stt:
                    mrow = alloc(stt, "mrow8", [4, 1], F32)
                    d8 = alloc(stt, "d8", [4, 4], F32)
                    for tt in range(T):
                        p = ps()
                        kb.op("pe", lambda e: e.matmul(p[0:4, 0:128], lhsT=gend[:, tt, :], rhs=ident_f, start=True, stop=True), reads=[gend, cf], writes=[p])
                        kb.op("dve", lambda e: e.tensor_reduce(out=mrow[:], in_=p[0:4, 0:128], axis=AX.X, op=ALU.max), reads=[p], writes=[mrow])
                        kb.op("dve", lambda e: e.tensor_scalar(out=d8[:], in0=ident_f[0:4, 0:4], scalar1=mrow[:, 0:1], scalar2=None, op0=ALU.mult), reads=[cf, mrow], writes=[d8])
                        p2 = ps()
                        kb.op("pe", lambda e: e.matmul(p2[:, 0:4], lhsT=ones_f[0:4, :], rhs=d8[:], start=True, stop=True), reads=[cf, d8], writes=[p2])
                        kb.op("dve", lambda e: e.tensor_copy(out=mlb[:, tt, :], in_=p2[:, 0:4]), reads=[p2], writes=[mlb])
                    kb.barrier()
                kb.op("dve", lambda e: e.tensor_tensor(out=wend[:], in0=gend[:], in1=mlb[:], op=ALU.subtract), reads=[gend, mlb], writes=[wend])
                kb.op("act", lambda e: e.activation(out=wend[:], in_=wend[:], func=AF.Exp), reads=[wend], writes=[wend])
                with scope() as st2:
                    Cst = [alloc(st2, "Cst%d" % i, [128, 129], F32) for i in range(4)]
                    mp = alloc(st2, "mp", [128, 4], F32)
                    mt8 = alloc(st2, "mt8", [128, 16], F32)
                    kwts = [alloc(st2, "kwt", [128, 2, 128], BF16) for _ in range(2)]
                    Dgs = [alloc(st2, "mDg", [128, 4, 128], F32) for _ in range(2)]
                    Drs = [alloc(st2, "mDr", [128, 4, 128], F32) for _ in range(2)]
                    scs = [alloc(st2, "msc", [128, 40], F32) for _ in range(2)]
                    kwt, Dg, Dr, sc = kwts[0], Dgs[0], Drs[0], scs[0]
                    Es = [alloc(st2, "mE%d" % i, [128, 128], F32) for i in range(3)]
                    Ms = [alloc(st2, "mM%d" % i, [128, 128], BF16) for i in range(3)]
                    nds = [alloc(st2, "nd%d" % i, [128, 129], F32) for i in range(2)]
                    cbf = [alloc(st2, "cbf%d" % i, [128, 129], BF16) for i in range(2)]
                    hsums = [alloc(st2, "hsum", [128, 256], F32) for _ in range(2)]
                    hts = [alloc(st2, "mht", [128, 256], F32) for _ in range(2)]
                    sgs = [alloc(st2, "msg", [128, 256], F32) for _ in range(2)]
                    hsum, ht, sg = hsums[0], hts[0], sgs[0]
                    ytm = alloc(st2, "mytm", [128, 4, 256], BF16)
                    yT = alloc(st2, "myT", [128, 2, 512], BF16)
                    ei = 0
                    ei0 = [0]

                    def init_state(dr, s_):
                        for hh in range(2):
                            C = Cst[dr * 2 + hh]
                            if sample:
                                kb.dma("sp", C[:, 0:128], D["smc"][l, dr, h0 + hh], writes=[C])
                                kb.dma("sp", C[:, 128:129], D["smn"][l, dr, h0 + hh].rearrange("(p o) -> p o", o=1), writes=[C])
                            else:
                                kb.op("dve", lambda e: e.memset(C[:], 0.0), writes=[C])
                        if sample:
                            kb.dma("sp", mp[:, dr * 2:dr * 2 + 2], D["smm"][l][dr * 4 + h0:dr * 4 + h0 + 2].partition_broadcast(128), writes=[mp])
                        else:
                            kb.op("dve", lambda e: e.memset(mp[:, dr * 2:dr * 2 + 2], 0.0), writes=[mp])

                    def local_update(dr, tt):
                        cs = slice(dr * 2, dr * 2 + 2)
                        pk = ps()
                        for hh in range(2):
                            kb.op("pe", lambda e: e.matmul(pk[:, hh * 128:(hh + 1) * 128], lhsT=kT[:, hh, tt * 128:(tt + 1) * 128], rhs=ident_b, start=True, stop=True),
                                  reads=[kT, cb], writes=[pk], inc=(hh == 1))
                        kb.op("dve", lambda e: e.tensor_tensor(out=kwt[:], in0=pk[:, 0:256].rearrange("p (h d) -> p h d", d=128),
                                                               in1=bcast(wend[:, tt, cs], [128, 2, 128], 2), op=ALU.mult), reads=[pk, wend], writes=[kwt])
                        a = sc[:, 0:2]
                        mn = sc[:, 2:4]
                        sp_ = sc[:, 4:6]
                        sl_ = sc[:, 6:8]
                        kb.op("dve", lambda e: e.tensor_tensor(out=a, in0=tot[:, tt, cs], in1=mp[:, cs], op=ALU.add), reads=[tot, mp], writes=[sc])
                        kb.op("dve", lambda e: e.tensor_tensor(out=mn, in0=a, in1=mlb[:, tt, cs], op=ALU.max), reads=[sc, mlb], writes=[sc])
                        kb.op("dve", lambda e: e.tensor_tensor(out=sp_, in0=a, in1=mn, op=ALU.subtract), reads=[sc], writes=[sc])
                        kb.op("dve", lambda e: e.tensor_tensor(out=sl_, in0=mlb[:, tt, cs], in1=mn, op=ALU.subtract), reads=[sc, mlb], writes=[sc])
                        kb.op("act", lambda e: e.activation(out=sc[:, 4:8], in_=sc[:, 4:8], func=AF.Exp), reads=[sc], writes=[sc])
                        kb.op("dve", lambda e: e.tensor_copy(out=mp[:, cs], in_=mn), reads=[sc], writes=[mp])
                        for hh in range(2):
                            C = Cst[dr * 2 + hh]
                            pc = ps()
                            kb.op("pe", lambda e: e.matmul(pc[:, 0:129], lhsT=kwt[:, hh, :], rhs=vaug[:, tt, hh, :], start=True, stop=True), reads=[kwt, vaug], writes=[pc])
                            kb.op("dve", lambda e: e.tensor_scalar(out=C[:], in0=C[:], scalar1=sc[:, 4 + hh:5 + hh], scalar2=None, op0=ALU.mult), reads=[C, sc], writes=[C])
                            kb.op("dve", lambda e: e.scalar_tensor_tensor(out=C[:], in0=pc[:, 0:129], scalar=sc[:, 6 + hh:7 + hh], in1=C[:], op0=ALU.mult, op1=ALU.add),
                                  reads=[pc, sc, C], writes=[C])

                    def final_state(dr, s_):
                        for hh in range(2):
                            C = Cst[dr * 2 + hh]
                            kb.dma("sp", D["nmc"][s_, l, dr, h0 + hh], C[:, 0:128], reads=[C])
                            kb.dma("sp", D["nmn"][s_, l, dr, h0 + hh].rearrange("(p o) -> p o", o=1), C[:, 128:129], reads=[C])
                        kb.dma("sp", D["nmm"][s_, l:l + 1, dr * 4 + h0:dr * 4 + h0 + 2], mp[0:1, dr * 2:dr * 2 + 2], reads=[mp])

                    for s_ in range(nseq):
                        init_state(1, s_)
                        for r0 in range(lt - 1, -1, -1):
                            tt = s_ * lt + r0
                            kwt, sc = kwts[tt % 2], scs[tt % 2]
                            for hh in range(2):
                                kb.op("act", lambda e: e.activation(out=Cpb[:, tt, hh, :], in_=Cst[2 + hh][:], func=AF.Copy), reads=[Cst[2 + hh]], writes=[Cpb])
                            kb.op("dve", lambda e: e.tensor_copy(out=mprev[:, tt, 2:4], in_=mp[:, 2:4]), reads=[mp], writes=[mprev])
                            local_update(1, tt)
                        if not sample:
                            final_state(1, s_)
                    wob, wov = wload([(D["w_in"][l][:, C0 + 1536 + h0 * 128:C0 + 1536 + (h0 + 2) * 128], 256)], 8)
                    for s_ in range(nseq):
                        init_state(0, s_)
                        for r0 in range(lt):
                            tt = s_ * lt + r0
                            tk = slice(tt * 128, (tt + 1) * 128)
                            kwt, Dg, Dr, sc = kwts[tt % 2], Dgs[tt % 2], Drs[tt % 2], scs[tt % 2]
                            hsum, ht, sg = hsums[tt % 2], hts[tt % 2], sgs[tt % 2]
                            kb.op("dve", lambda e: e.tensor_copy(out=mprev[:, tt, 0:2], in_=mp[:, 0:2]), reads=[mp], writes=[mprev])
                            kb.op("dve", lambda e: e.tensor_tensor(out=sc[:, 8:12], in0=G[:, tt, :], in1=mprev[:, tt, :], op=ALU.add), reads=[G, mprev], writes=[sc])
                            kb.op("dve", lambda e: e.tensor_tensor(out=Dg[:], in0=bcast(ident_f, [128, 4, 128], 1), in1=bcast(pj[:, tt, :], [128, 4, 128], 2), op=ALU.mult),
                                  reads=[cf, pj], writes=[Dg])
                            po = ps_acc()
                            proj_tm(wob, wov, 0, 256, tt // 4, tt % 4, po)
                            pSs = []
                            for hh in range(2):
                                pS = ps_acc()
                                kb.op("pe", lambda e: e.matmul(pS[:, 0:128], lhsT=kT[:, hh, tk], rhs=qT[:, hh, tk], start=True, stop=True), reads=[kT, qT], writes=[pS])
                                pSs.append(pS)
                            pms = []
                            for c in range(4):
                                dr = c // 2
                                pm = ps()
                                kb.op("pe", lambda e: e.matmul(pm[:, 0:128], lhsT=ones_f, rhs=Dg[:, c, :], start=True, stop=False), reads=[cf, Dg], writes=[pm], inc=False)
                                kb.op("pe", lambda e: e.matmul(pm[:, 0:128], lhsT=ident_b, rhs=(maskB_b if dr == 0 else maskF_b), start=False, stop=True), reads=[cb], writes=[pm])
                                pms.append(pm)
                            for c in range(4):
                                kb.op("dve", lambda e: e.tensor_reduce(out=sc[:, 12 + c:13 + c], in_=pms[c][:, 0:128], axis=AX.X, op=ALU.max), reads=[pms[c]], writes=[sc])
                            kb.op("dve", lambda e: e.tensor_tensor(out=sc[:, 12:16], in0=sc[:, 12:16], in1=G[:, tt, :], op=ALU.add), reads=[sc, G], writes=[sc])
                            kb.op("dve", lambda e: e.tensor_tensor(out=sc[:, 16:20], in0=sc[:, 12:16], in1=sc[:, 8:12], op=ALU.max), reads=[sc], writes=[sc])
                            kb.op("dve", lambda e: e.tensor_tensor(out=sc[:, 20:24], in0=G[:, tt, :], in1=sc[:, 16:20], op=ALU.subtract), reads=[sc, G], writes=[sc])
                            kb.op("dve", lambda e: e.tensor_tensor(out=sc[:, 24:28], in0=sc[:, 8:12], in1=sc[:, 16:20], op=ALU.subtract), reads=[sc], writes=[sc])
                            kb.op("act", lambda e: e.activation(out=sc[:, 24:28], in_=sc[:, 24:28], func=AF.Exp, bias=lns_col[:, 0:1]), reads=[sc, cf], writes=[sc])
                            kb.op("act", lambda e: e.activation(out=sc[:, 28:32], in_=sc[:, 16:20], func=AF.Exp, scale=-1.0), reads=[sc], writes=[sc])
                            kb.op("dve", lambda e: e.tensor_tensor(out=Dr[:], in0=bcast(ident_f, [128, 4, 128], 1), in1=bcast(sc[:, 20:24], [128, 4, 128], 2), op=ALU.mult),
                                  reads=[cf, sc], writes=[Dr])
                            items = [(0, 0), (0, 1), (1, 0), (1, 1)]
                            mts = {}

                            def mA(i):
                                hh, dr = items[i]
                                c = dr * 2 + hh
                                pW = ps()
                                kb.op("pe", lambda e: e.matmul(pW[:, 0:128], lhsT=ones_f, rhs=Dr[:, c, :], start=True, stop=False), reads=[cf, Dr], writes=[pW], inc=False)
                                kb.op("pe", lambda e: e.matmul(pW[:, 0:128], lhsT=ident_b, rhs=(maskF_b if dr == 0 else maskB_b), start=False, stop=True), reads=[cb], writes=[pW])
                                E = Es[(ei0[0] + i) % 3]
                                M = Ms[(ei0[0] + i) % 3]
                                kb.op("act", lambda e: e.activation(out=E[:], in_=pW[:, 0:128], func=AF.Exp, bias=pjs[:, tt, c:c + 1]), reads=[pW, pjs], writes=[E])
                                kb.op("dve", lambda e: e.tensor_tensor(out=M[:], in0=E[:], in1=pSs[hh][:, 0:128], op=ALU.mult), reads=[E, pSs[hh]], writes=[M])
                                if dr == 0:
                                    kb.op("act", lambda e: e.activation(out=cbf[hh][:], in_=Cst[hh][:], func=AF.Copy), reads=[Cst[hh]], writes=[cbf[hh]])
                                mts[i] = M

                            def mC(i):
                                hh, dr = items[i]
                                c = dr * 2 + hh
                                M = mts.pop(i)
                                nd = nds[i % 2]
                                pN = ps()
                                kb.op("pe", lambda e: e.matmul(pN[:, 0:129], lhsT=M[:], rhs=vaug[:, tt, hh, :], start=True, stop=True), reads=[M, vaug], writes=[pN])
                                pI = ps()
                                if dr == 0:
                                    kb.op("pe", lambda e: e.matmul(pI[:, 0:129], lhsT=qT[:, hh, tk], rhs=cbf[hh][:], start=True, stop=True), reads=[qT, cbf[hh]], writes=[pI])
                                else:
                                    kb.op("pe", lambda e: e.matmul(pI[:, 0:129], lhsT=qT[:, hh, tk], rhs=Cpb[:, tt, hh, :], start=True, stop=True), reads=[qT, Cpb], writes=[pI])
                                kb.op("act", lambda e: e.activation(out=nd[:], in_=pN[:, 0:129], func=AF.Copy), reads=[pN], writes=[nd])
                                kb.op("dve", lambda e: e.scalar_tensor_tensor(out=nd[:], in0=pI[:, 0:129], scalar=sc[:, 24 + c:25 + c], in1=nd[:], op0=ALU.mult, op1=ALU.add),
                                      reads=[pI, sc, nd], writes=[nd])
                                kb.op("dve", lambda e: e.tensor_scalar(out=sc[:, 32:33], in0=nd[:, 128:129], scalar1=-1.0, scalar2=None, op0=ALU.mult), reads=[nd], writes=[sc])
                                kb.op("dve", lambda e: e.tensor_tensor(out=sc[:, 32:33], in0=sc[:, 32:33], in1=nd[:, 128:129], op=ALU.max), reads=[nd, sc], writes=[sc])
                                kb.op("dve", lambda e: e.tensor_tensor(out=sc[:, 32:33], in0=sc[:, 32:33], in1=sc[:, 28 + c:29 + c], op=ALU.max), reads=[sc], writes=[sc])
                                kb.op("dve", lambda e: e.reciprocal(out=sc[:, 33:34], in_=sc[:, 32:33]), reads=[sc], writes=[sc])
                                if dr == 0:
                                    kb.op("dve", lambda e: e.tensor_scalar(out=hsum[:, hh * 128:(hh + 1) * 128], in0=nd[:, 0:128], scalar1=sc[:, 33:34], scalar2=None, op0=ALU.mult),
                                          reads=[nd, sc], writes=[hsum])
                                else:
                                    kb.op("dve", lambda e: e.scalar_tensor_tensor(out=hsum[:, hh * 128:(hh + 1) * 128], in0=nd[:, 0:128], scalar=sc[:, 33:34],
                                                                                  in1=hsum[:, hh * 128:(hh + 1) * 128], op0=ALU.mult, op1=ALU.add), reads=[nd, sc, hsum], writes=[hsum])

                            for i in range(4 + 2):
                                if i < 4:
                                    mA(i)
                                if i >= 2:
                                    mC(i - 2)
                            ei0[0] += 4
                            local_update(0, tt)
                            h3 = hsum[:].rearrange("p (h d) -> p h d", d=128)
                            t3 = ht[:].rearrange("p (h d) -> p h d", d=128)
                            kb.op("dve", lambda e: e.tensor_tensor(out=ht[:], in0=hsum[:], in1=hsum[:], op=ALU.mult), reads=[hsum], writes=[ht])
                            kb.op("dve", lambda e: e.tensor_reduce(out=sc[:, 34:36], in_=t3, axis=AX.X, op=ALU.add), reads=[ht], writes=[sc])
                            kb.op("dve", lambda e: e.tensor_scalar(out=sc[:, 34:36], in0=sc[:, 34:36], scalar1=1.0 / 128, scalar2=EPS, op0=ALU.mult, op1=ALU.add), reads=[sc], writes=[sc])
                            kb.op("act", lambda e: e.activation(out=sc[:, 34:36], in_=sc[:, 34:36], func=AF.Ln), reads=[sc], writes=[sc])
                            kb.op("act", lambda e: e.activation(out=sc[:, 36:38], in_=sc[:, 34:36], func=AF.Exp, scale=-0.5), reads=[sc], writes=[sc])
                            kb.op("dve", lambda e: e.tensor_tensor(out=t3, in0=h3, in1=bcast(sc[:, 36:38], [128, 2, 128], 2), op=ALU.mult), reads=[hsum, sc], writes=[ht])
                            kb.op("dve", lambda e: e.tensor_tensor(out=ht[:], in0=ht[:], in1=ng[:], op=ALU.mult), reads=[ht, ng], writes=[ht])
                            kb.op("act", lambda e: e.activation(out=sg[:], in_=po[:, 0:256], func=AF.Exp, scale=-1.0), reads=[po], writes=[sg])
                            kb.op("act", lambda e: e.activation(out=sg[:], in_=sg[:], func=AF.Ln, bias=1.0), reads=[sg], writes=[sg])
                            kb.op("act", lambda e: e.activation(out=sg[:], in_=sg[:], func=AF.Exp, scale=-1.0), reads=[sg], writes=[sg])
                            kb.op("dve", lambda e: e.tensor_tensor(out=ytm[:, tt % 4, :], in0=ht[:], in1=sg[:], op=ALU.mult), reads=[ht, sg], writes=[ytm])
                            if tt % 4 == 3:
                                tb = tt // 4
                                for c2 in range(2):
                                    p = ps()
                                    for tl in range(4):
                                        kb.op("pe", lambda e: e.matmul(p[:, tl * 128:(tl + 1) * 128], lhsT=ytm[:, tl, c2 * 128:(c2 + 1) * 128], rhs=ident_b, start=True, stop=True),
                                              reads=[ytm, cb], writes=[p], inc=(tl == 3))
                                    kb.op("act", lambda e: e.activation(out=yT[:, c2, :], in_=p[:, :], func=AF.Copy), reads=[p], writes=[yT])
                                for c in range(8):
                                    p = ps()
                                    for k in range(2):
                                        kb.op("pe", lambda e: e.matmul(p[:, :], lhsT=wo_buf[:, k, c * 128:(c + 1) * 128], rhs=yT[:, k, :], start=(k == 0), stop=(k == 1)),
                                              reads=[wo_buf, yT], writes=[p], inc=(k == 1))
                                    kb.op("dve", lambda e: e.scalar_tensor_tensor(out=xT[tb][:, c, :], in0=p[:, :], scalar=modc[:, 2, c:c + 1], in1=xT[tb][:, c, :],
                                                                                  op0=ALU.mult, op1=ALU.add), reads=[p, modc, xT[tb]], writes=[xT[tb]])
                        if not sample:
                            final_state(0, s_)
            kb.barrier()

    def run_pass(g, src, dst, nblk):
        load_x(src, nblk)
        kb.barrier()
        for l in range(NL):
            kb.label = 'norm_g%d' % g
            load_mod(l, g)
            if g == 0 and l + 1 < NL:
                compute_mod(l + 1)
            with scope() as st:
                sq = [alloc(st, "sq", [128, 8, 512], BF16) for _ in range(2)]
                rstd = [alloc(st, "rstd", [128, 512], F32) for _ in range(2)]
                tmp2 = [alloc(st, "tmpn%d" % i, [128, 512], F32) for i in range(4)]
                for tb in range(nblk):
                    norm_block((sq, rstd, tmp2), tb, AB.t[:, 0, :], AB.t[:, 1, :], hT[tb])
            kb.barrier()
            for mk in MIXERS:
                if mk in "bd":
                    attention(l, g, mk, nblk)
                elif mk == "a":
                    ssd(l, g, nblk)
                elif mk == "c":
                    mlstm(l, g, nblk)
            with scope() as st:
                sq = [alloc(st, "sq", [128, 8, 512], BF16) for _ in range(2)]
                rstd = [alloc(st, "rstd", [128, 512], F32) for _ in range(2)]
                tmp2 = [alloc(st, "tmpn%d" % i, [128, 512], F32) for i in range(4)]
                for tb in range(nblk):
                    norm_block((sq, rstd, tmp2), tb, AB.t[:, 2, :], AB.t[:, 3, :], hT[tb])
            kb.barrier()
            for b0 in range(0, nblk, 2):
                ffn(l, list(range(b0, min(b0 + 2, nblk))))
        final_out(nblk, dst)

    run_pass(0, D["xp"], D["yp"], 1)
    run_pass(1, D["xs"], D["ys"], 4)

    kb.barrier(include_pool_dma=True)
    top.close()
    print("instructions:", kb.ninst, flush=True)
    if kb.stats is not None:
        tot = 0.0
        for lab, (mk, busy, nu, nfl, bub) in sorted(kb.stats.items(), key=lambda kv: -kv[1][0]):
            tot += mk
            print("  %-12s est_us=%8.0f units=%6d regions=%4d bubble_us=%6.0f busy: %s" % (lab, mk / 1e3, nu, nfl, bub / 1e3, " ".join("%s=%.0f" % (e_, v_ / 1e3) for e_, v_ in sorted(busy.items()))))
        print("  est total us", tot / 1e3)
    return nc


_CACHE = {}


def prep_inputs(inp):
    f = lambda a: np.ascontiguousarray(np.asarray(a, dtype=np.float32))
    consts = make_consts()
    rope = make_rope()
    shared = {}
    for name in ("w_ada", "b_ada", "norm1", "norm2", "w_in", "w_out", "conv_ssd_w", "conv_ssd_b", "ssd_d", "ssd_norm",
                 "diff_lq1", "diff_lk1", "diff_lq2", "diff_lk2", "conv_mlstm_w", "conv_mlstm_b", "mlstm_norm",
                 "gqa_q_norm", "gqa_k_norm", "w_ffn_in", "w_ffn_out", "norm_f"):
        shared[name] = f(inp[name])
    shared["ssd_a_log"] = f(inp["ssd_a_log"]).reshape(4, 16)
    shared["ssd_dt_bias"] = f(inp["ssd_dt_bias"]).reshape(4, 16)
    shared["mlstm_gate_b"] = f(inp["mlstm_gate_b"]).reshape(4, 16)
    shared["consts"] = consts
    shared["rope"] = rope
    xp = f(inp["x_prompt"])
    xs = f(inp["x_sample"])
    in_maps = []
    for c in range(8):
        b = c // 4
        m = dict(shared)
        m["xp"] = xp[2 * c:2 * c + 2].reshape(512, 1024)
        m["xs"] = xs[b]
        m["cvec"] = np.stack([f(inp["c_ctx"]), f(inp["c"])[b]], axis=0)
        m["cdk"] = f(inp["cache_diff_k"])[b].reshape(4, 256, 512)
        m["cdv"] = f(inp["cache_diff_v"])[b].reshape(4, 256, 512)
        m["cgk"] = f(inp["cache_gqa_k"])[b].reshape(4, 256, 128)
        m["cgv"] = f(inp["cache_gqa_v"])[b].reshape(4, 256, 128)
        m["sssm"] = f(inp["state_ssm"])[b]
        m["smc"] = f(inp["state_mlstm_c"])[b]
        m["smn"] = f(inp["state_mlstm_n"])[b]
        m["smm"] = f(inp["state_mlstm_m"])[b].reshape(4, 8)
        in_maps.append(m)
    if NL < 4:
        spec = dict(IN_SPECS)
        for m in in_maps:
            for k_ in list(m.keys()):
                if spec[k_][0] == 4 and len(spec[k_]) > 1:
                    m[k_] = np.ascontiguousarray(m[k_][:NL])
    return in_maps


def kernel(**inp):
    if "nc" not in _CACHE:
        _CACHE["nc"] = build_program()
    nc = _CACHE["nc"]
    in_maps = prep_inputs(inp)
    res = run_bass_kernel_spmd(nc, in_maps, core_ids=list(range(8)))
    return assemble(res.results)


def assemble(R):
    y_prompt = np.concatenate([R[c]["yp"].reshape(2, 256, 1024) for c in range(8)], axis=0)
    y_sample = np.stack([R[0]["ys"], R[4]["ys"]], axis=0)
    cat = lambda k: np.concatenate([R[c][k] for c in range(8)], axis=0)
    ndk = cat("ndk").reshape(16, 4, 256, 4, 2, 64)
    ndv = cat("ndv").reshape(16, 4, 256, 4, 128)
    ngk = cat("ngk").reshape(16, 4, 256, 2, 64)
    ngv = cat("ngv").reshape(16, 4, 256, 2, 64)
    nssm = cat("nssm")
    nmc = cat("nmc")
    nmn = cat("nmn")
    nmm = cat("nmm").reshape(16, 4, 2, 4)
    return (y_prompt, y_sample, ndk, ndv, ngk, ngv, nssm, nmc, nmn, nmm)
```

```python
import os
import math
from contextlib import ExitStack
import numpy as np
import concourse.bass as bass
import concourse.mybir as mybir
from concourse.bass_utils import run_bass_kernel_spmd

F32 = mybir.dt.float32
BF16 = mybir.dt.bfloat16
ALU = mybir.AluOpType
AF = mybir.ActivationFunctionType
AX = mybir.AxisListType

D_MODEL = 1024
DEPTH = 4
IN_COLS = 5664
D_FF = 2816
EPS = 1e-6
NEG = -30000.0

NL = int(os.environ.get("MK_NL", "4"))
MIXERS = os.environ.get("MK_MIX", "abcd")
SSD_PH = int(os.environ.get("MK_SSD_PH", "9"))
SSD_SUB = int(os.environ.get("MK_SSD_SUB", "9"))


class Buf:
    __slots__ = ("t", "w", "r")

    def __init__(self, t):
        self.t = t
        self.w = None
        self.r = []

    def __getitem__(self, idx):
        return self.t[idx]


class _Rec:
    def __init__(self):
        self.call = None

    def __getattr__(self, name):
        def f(*args, **kw):
            self.call = (name, args, kw)
            return self
        return f


class KB:
    NDMA_SEM = 8

    def __init__(self, nc):
        self.nc = nc
        self.engs = {"pe": nc.tensor, "act": nc.scalar, "dve": nc.vector, "pool": nc.gpsimd, "sp": nc.sync}
        self.sems = {}
        self.cnt = {}
        for k in ("pe", "act", "dve", "pool"):
            self.sems[k] = nc.alloc_semaphore(name="s_" + k)
            self.cnt[k] = 0
        self.dq = {}
        for q in ("sp", "pool", "act"):
            lst = []
            for i in range(self.NDMA_SEM):
                key = "d_%s%d" % (q, i)
                self.sems[key] = nc.alloc_semaphore(name=key)
                self.cnt[key] = 0
                lst.append(key)
            self.dq[q] = [lst, 0]
        self.seen = {e: {} for e in self.engs}
        self.ninst = 0
        self.defer = bool(int(os.environ.get('MK_SCHED', '1')))
        self.pending = []
        self.stats = {} if os.environ.get('MK_STATS') else None
        self.label = 'top'
        self.sched_mode = os.environ.get('MK_SMODE', 'cpx')

    def _wait(self, eng, k, v):
        seen = self.seen[eng]
        if seen.get(k, 0) >= v:
            return
        self.engs[eng].wait_ge(self.sems[k], v)
        self.ninst += 1
        seen[k] = v

    def _need(self, eng, reads, writes):
        need = {}

        def add(dep):
            if dep is None:
                return
            k, v = dep
            if need.get(k, 0) < v:
                need[k] = v
        for b in reads:
            add(b.w)
        for b in writes:
            add(b.w)
            for d in b.r:
                add(d)
        for k, v in need.items():
            if k == eng and eng == "pe":
                continue
            self._wait(eng, k, v)

    def _record(self, dep, reads, writes):
        for b in reads:
            b.r.append(dep)
            if len(b.r) > 64:
                mx = {}
                for k, v in b.r:
                    if mx.get(k, 0) < v:
                        mx[k] = v
                b.r = list(mx.items())
        for b in writes:
            b.w = dep
            b.r = []

    def op(self, eng, fn, reads=(), writes=(), inc=True):
        if self.defer:
            rec = _Rec()
            fn(rec)
            self.pending.append(("op", eng, rec.call, tuple(reads), tuple(writes), inc))
            return None
        return self._op_now(eng, fn, reads, writes, inc)

    def copy(self, out, in_, reads=(), writes=()):
        if not self.defer:
            return self._op_now("act", lambda e: e.activation(out=out, in_=in_, func=AF.Copy), reads, writes, True)
        c_dve = ("tensor_copy", (), {"out": out, "in_": in_})
        c_act = ("activation", (), {"out": out, "in_": in_, "func": AF.Copy})
        self.pending.append(("either", None, (c_dve, c_act), tuple(reads), tuple(writes), True))

    def scale(self, out, in_, col, reads=(), writes=()):
        if not self.defer:
            return self._op_now("dve", lambda e: e.tensor_scalar(out=out, in0=in_, scalar1=col, scalar2=None, op0=ALU.mult), reads, writes, True)
        c_dve = ("tensor_scalar", (), {"out": out, "in0": in_, "scalar1": col, "scalar2": None, "op0": ALU.mult})
        c_act = ("activation", (), {"out": out, "in_": in_, "func": AF.Identity, "scale": col})
        self.pending.append(("either", None, (c_dve, c_act), tuple(reads), tuple(writes), True))

    def _op_now(self, eng, fn, reads=(), writes=(), inc=True):
        self._need(eng, reads, writes)
        ins = fn(self.engs[eng])
        self.ninst += 1
        val = self.cnt[eng] + 1
        if inc:
            ins.then_inc(self.sems[eng], 1)
            self.cnt[eng] = val
        self._record((eng, val), reads, writes)
        return ins

    def dma(self, q, out, in_, reads=(), writes=(), **kw):
        if self.defer:
            self.pending.append(("dma", q, (out, in_, kw), tuple(reads), tuple(writes), True))
            return None
        return self._dma_now(q, out, in_, reads, writes, **kw)

    def _dma_now(self, q, out, in_, reads=(), writes=(), **kw):
        self._need(q, reads, writes)
        lst, i = self.dq[q]
        key = lst[i % len(lst)]
        self.dq[q][1] = i + 1
        if self.cnt[key]:
            self._wait(q, key, self.cnt[key])
        ins = self.engs[q].dma_start(out=out, in_=in_, **kw)
        self.ninst += 1
        self.cnt[key] += 16
        ins.then_inc(self.sems[key], 16)
        dep = (key, self.cnt[key])
        self._record(dep, reads, writes)
        return dep

    @staticmethod
    def _cost(kind, eng, call):
        def fsz(ap):
            n = 1
            for d in ap.shape[1:]:
                n *= d
            return n
        if kind == "dma":
            out = call[0]
            nb = fsz(out) * out.shape[0] * (2 if out.dtype == BF16 else 4)
            return 2000.0 + nb / 80.0
        name, args, kw = call
        if name == "matmul":
            n = fsz(kw["rhs"])
            passes = 4 if kw["lhsT"].dtype == F32 else 1
            return 70.0 + n * passes * 0.45
        out = kw.get("out", None)
        if out is None:
            out = kw.get("ap", args[0] if args else None)
        n = fsz(out) if out is not None else 64
        if eng == "act":
            return 230.0 + n * 0.75
        if name == "reciprocal":
            return 70.0 + n * 6.5
        if name == "memset":
            return 70.0 + n * 0.5
        return 70.0 + n * 1.1

    def flush(self):
        pend = self.pending
        self.pending = []
        if not pend:
            return
        import heapq
        units = []
        cur = None
        for it in pend:
            kind, eng, call, rd, wr, inc = it
            if kind == "op" and eng == "pe":
                if cur is None:
                    cur = [eng, [], 0.0, set(), set()]
                cur[1].append(it)
                cur[2] += self._cost(kind, eng, call)
                cur[3].update(rd)
                cur[4].update(wr)
                if inc:
                    units.append(cur)
                    cur = None
            elif kind == "either":
                assert cur is None, "non-PE op inside an open PE group"
                cd = self._cost("op", "dve", call[0])
                ca = self._cost("op", "act", call[1])
                units.append([None, [it], min(cd, ca), set(rd), set(wr), (cd, ca)])
            else:
                assert cur is None, "non-PE op inside an open PE group"
                units.append([eng, [it], self._cost(kind, eng, call), set(rd), set(wr)])
        assert cur is None, "PE group without final inc"
        n = len(units)
        lastw = {}
        readers = {}
        deps = [None] * n
        succ = [[] for _ in range(n)]
        for i, u in enumerate(units):
            d = set()
            for b in u[3]:
                if b in lastw:
                    d.add(lastw[b])
            for b in u[4]:
                if b in lastw:
                    d.add(lastw[b])
                for r in readers.get(b, ()):
                    d.add(r)
            d.discard(i)
            deps[i] = d
            for j in d:
                succ[j].append(i)
            for b in u[3]:
                readers.setdefault(b, []).append(i)
            for b in u[4]:
                lastw[b] = i
                readers[b] = []
        ndep = [len(d) for d in deps]
        ready_t = [0.0] * n
        fin = [0.0] * n
        bl = [0.0] * n
        for i in range(n - 1, -1, -1):
            m_ = 0.0
            for j in succ[i]:
                if bl[j] > m_:
                    m_ = bl[j]
            bl[i] = units[i][2] + m_
        mode = self.sched_mode
        free = {}
        fut = {}
        avail = {}
        load = {"dve": 0.0, "act": 0.0}
        for u in units:
            if u[0] in load:
                load[u[0]] += u[2]

        def bind(i):
            u = units[i]
            if u[0] is None:
                cd, ca = u[5]
                td = load["dve"] + cd
                ta = load["act"] + ca
                load["dve" if td <= ta else "act"] += (cd if td <= ta else ca)
                kind_, _, calls, rd_, wr_, inc_ = u[1][0]
                if td <= ta:
                    u[0], u[2], u[1] = "dve", cd, [("op", "dve", calls[0], rd_, wr_, inc_)]
                else:
                    u[0], u[2], u[1] = "act", ca, [("op", "act", calls[1], rd_, wr_, inc_)]
            return u[0]

        for i in range(n):
            if ndep[i] == 0:
                heapq.heappush(fut.setdefault(bind(i), []), (0.0, i))
        order = []
        done = 0
        while done < n:
            best = None
            for e, h in fut.items():
                fe = free.get(e, 0.0)
                av = avail.setdefault(e, [])
                while h and h[0][0] <= fe:
                    rt, i = heapq.heappop(h)
                    heapq.heappush(av, ((-bl[i], i) if (mode == "cp" or (mode == "cpx" and e != "pe")) else (rt, i)))
                if av:
                    cand = (fe, 0, e)
                elif h:
                    cand = (h[0][0], 1, e)
                else:
                    continue
                if best is None or cand < best:
                    best = cand
            st_, fromfut, e = best
            if fromfut:
                rt, i = heapq.heappop(fut[e])
            else:
                _, i = heapq.heappop(avail[e])
            u = units[i]
            if e in ("sp", "pool"):
                free[e] = st_ + 60.0
                fin[i] = st_ + u[2]
            else:
                fin[i] = st_ + u[2]
                free[e] = fin[i]
            order.append((st_, i))
            done += 1
            for j in succ[i]:
                ndep[j] -= 1
                if fin[i] > ready_t[j]:
                    ready_t[j] = fin[i]
                if ndep[j] == 0:
                    heapq.heappush(fut.setdefault(bind(j), []), (ready_t[j], j))
        if self.stats is not None:
            mk = max(fin) if fin else 0.0
            busy = {}
            for u in units:
                busy[u[0]] = busy.get(u[0], 0.0) + u[2]
            st = self.stats.setdefault(self.label, [0.0, {}, 0, 0, 0.0])
            st[0] += mk
            st[2] += n
            st[3] += 1
            st[4] += mk - max(busy.values())
            for e_, v_ in busy.items():
                st[1][e_] = st[1].get(e_, 0.0) + v_
        order.sort()
        for _, i in order:
            for kind, eng, call, rd, wr, inc in units[i][1]:
                if kind == "op":
                    name, args, kw = call
                    self._op_now(eng, lambda en: getattr(en, name)(*args, **kw), rd, wr, inc)
                else:
                    out, in_, kw = call
                    self._dma_now(eng, out, in_, rd, wr, **kw)

    def barrier(self, include_pool_dma=False):
        self.flush()
        keys = ["pe", "act", "dve", "pool"] + self.dq["sp"][0] + self.dq["act"][0]
        if include_pool_dma:
            keys += self.dq["pool"][0]
        for e in ("pe", "act", "dve", "pool", "sp"):
            for k in keys:
                if (k == e and e == "pe") or self.cnt[k] == 0:
                    continue
                self._wait(e, k, self.cnt[k])


def bcast(ap, shape, axis):
    return ap.unsqueeze(axis).broadcast_to(list(shape))


IN_SPECS = [
    ("xp", [512, 1024]), ("xs", [2048, 1024]), ("cvec", [2, 1024]),
    ("cdk", [4, 256, 512]), ("cdv", [4, 256, 512]), ("cgk", [4, 256, 128]), ("cgv", [4, 256, 128]),
    ("sssm", [4, 2, 8, 64, 64]), ("smc", [4, 2, 4, 128, 128]), ("smn", [4, 2, 4, 128]), ("smm", [4, 8]),
    ("w_ada", [4, 1024, 6144]), ("b_ada", [4, 6144]), ("norm1", [4, 1024]), ("norm2", [4, 1024]),
    ("w_in", [4, 1024, IN_COLS]), ("w_out", [4, 2048, 1024]),
    ("conv_ssd_w", [4, 5, 768]), ("conv_ssd_b", [4, 768]), ("ssd_a_log", [4, 16]), ("ssd_dt_bias", [4, 16]),
    ("ssd_d", [4, 8]), ("ssd_norm", [4, 512]),
    ("diff_lq1", [4, 64]), ("diff_lk1", [4, 64]), ("diff_lq2", [4, 64]), ("diff_lk2", [4, 64]),
    ("conv_mlstm_w", [4, 5, 1024]), ("conv_mlstm_b", [4, 1024]), ("mlstm_gate_b", [4, 16]), ("mlstm_norm", [4, 512]),
    ("gqa_q_norm", [4, 64]), ("gqa_k_norm", [4, 64]),
    ("w_ffn_in", [4, 1024, 2 * D_FF]), ("w_ffn_out", [4, D_FF, 1024]), ("norm_f", [1024]),
    ("consts", [128, 1152]), ("rope", [128, 2, 2048]),
]
OUT_SPECS = [
    ("yp", [512, 1024]), ("ys", [2048, 1024]),
    ("ndk", [2, 4, 256, 512]), ("ndv", [2, 4, 256, 512]), ("ngk", [2, 4, 256, 128]), ("ngv", [2, 4, 256, 128]),
    ("nssm", [2, 4, 2, 8, 64, 64]), ("nmc", [2, 4, 2, 4, 128, 128]), ("nmn", [2, 4, 2, 4, 128]), ("nmm", [2, 4, 8]),
]


def make_consts():
    c = np.zeros((128, 1152), np.float32)
    k = np.arange(128)
    c[:, 0:128] = np.eye(128)
    c[:, 128:256] = 1.0
    c[:, 256:384] = (k[:, None] <= k[None, :])
    c[:, 384:512] = (k[:, None] >= k[None, :])
    c[:, 512:640] = np.where(k[:, None] <= k[None, :], 0.0, NEG)
    c[:, 640:768] = np.where(k[:, None] >= k[None, :], 0.0, NEG)
    c[:, 768:896] = (k[:, None] // 64 == k[None, :] // 64)
    rm = np.zeros((128, 128), np.float32)
    for dp in range(128):
        half = (dp % 32) // 16
        if half == 0:
            rm[dp + 16, dp] = -1.0
        else:
            rm[dp - 16, dp] = 1.0
    c[:, 896:1024] = rm
    c[64, 1024:1088] = 1.0
    c[0, 1088:1152] = 1.0
    return c


def make_rope():
    t = np.arange(2048)
    r = (t // 64).astype(np.float32)
    cc = (t % 64).astype(np.float32)
    nf = 16
    freqs = (10000.0 ** (-np.arange(nf, dtype=np.float32) / nf)).astype(np.float32)
    ang = np.stack([r[:, None] * freqs, cc[:, None] * freqs], axis=1).astype(np.float32)
    out = np.zeros((128, 2, 2048), np.float32)
    for p in range(128):
        d = p % 64
        a = d // 32
        f = d % 16
        out[p, 0] = np.cos(ang[:, a, f])
        out[p, 1] = np.sin(ang[:, a, f])
    return out


def build_program():
    nc = bass.Bass("TRN2", target_bir_lowering=False)
    kb = KB(nc)
    D = {}
    for name, shape in IN_SPECS:
        if shape[0] == 4 and len(shape) > 1:
            shape = [NL] + list(shape[1:])
        D[name] = nc.dram_tensor(name, shape, F32, kind="ExternalInput").ap()
    for name, shape in OUT_SPECS:
        D[name] = nc.dram_tensor(name, shape, F32, kind="ExternalOutput").ap()
    mod_d = nc.dram_tensor("mod_scr", [4, 2, 6144], F32, kind="Internal").ap()

    top = ExitStack()

    class scope:
        def __enter__(self_):
            self_.st = ExitStack()
            return self_.st

        def __exit__(self_, *a):
            if a[0] is None:
                kb.barrier()
            self_.st.close()
            return False

    uid = [0]

    def alloc(stack, name, shape, dt, psum=False):
        uid[0] += 1
        name = "%s_%d" % (name, uid[0])
        cm = nc.psum_tensor(name, shape, dt) if psum else nc.sbuf_tensor(name, shape, dt)
        return Buf(stack.enter_context(cm))

    xT = [alloc(top, "xT%d" % i, [128, 8, 512], F32) for i in range(4)]
    hT = [alloc(top, "hT%d" % i, [128, 8, 512], BF16) for i in range(4)]
    NW = 2
    wbufs = [alloc(top, "wb%d" % i, [128, 4096], BF16) for i in range(NW)]
    wstate = [0]
    psb = [alloc(top, "ps%d" % i, [128, 512], F32, psum=True) for i in range(8)]
    pstate = [0]
    cf = alloc(top, "cf", [128, 640], F32)
    cb = alloc(top, "cb", [128, 1024], BF16)
    modc = alloc(top, "modc", [128, 6, 8], F32)
    nrm = alloc(top, "nrm", [128, 2, 8], F32)
    AB = alloc(top, "AB", [128, 4, 8], F32)
    nfc = alloc(top, "nfc", [128, 8], F32)
    lnsb = alloc(top, "lnsb", [128, 1], F32)
    lns_col = lnsb.t

    wo_buf = alloc(top, "wo_buf", [128, 4, 1024], BF16)
    accstate = [0]

    def ps():
        b = psb[pstate[0] % 4]
        pstate[0] += 1
        return b

    def ps_acc():
        b = psb[4 + accstate[0] % 4]
        accstate[0] += 1
        return b

    ffn_state = [0]

    def wload(pieces, kch, three=False):
        if three:
            lst = wbufs + [wo_buf]
            b = lst[ffn_state[0] % len(lst)]
            ffn_state[0] += 1
        else:
            b = wbufs[wstate[0] % NW]
            wstate[0] += 1
        ntot = sum(n for _, n in pieces)
        assert kch * ntot <= 4096, (kch, ntot)
        flat = b.t[:].rearrange("p k n -> p (k n)") if b is wo_buf else b.t
        view = flat[:, 0:kch * ntot].rearrange("p (k n) -> p k n", k=kch)
        o = 0
        for ap, n in pieces:
            kb.dma("pool", view[:, :, o:o + n], ap.rearrange("(k p) n -> p k n", p=128), writes=[b])
            o += n
        return b, view

    ident_f = cf.t[:, 0:128]
    ones_f = cf.t[:, 128:256]
    ident_b = cb.t[:, 0:128]
    selm_f = cf.t[:, 512:640]
    ones_b = cb.t[:, 128:256]

    kb.dma("sp", cf[:, 0:512], D["consts"][:, 0:512], writes=[cf])
    kb.dma("sp", cf[:, 512:640], D["consts"][:, 1024:1152], writes=[cf])
    kb.dma("pool", cb[:], D["consts"][:, 0:1024], writes=[cb])
    kb.op("dve", lambda e: e.memset(lnsb[:], math.log(128 ** -0.5)), writes=[lnsb])
    kb.dma("sp", nfc[:], D["norm_f"].rearrange("(c p) -> p c", p=128), writes=[nfc], allow_slow_non_contiguous=True)

    modall = alloc(top, "modall", [128, NL, 48, 2], F32)
    ball = alloc(top, "ball", [128, NL, 48], F32)
    nrmall = alloc(top, "nrmall", [128, NL, 2, 8], F32)
    for l in range(NL):
        kb.dma("sp", ball[:, l, :], D["b_ada"][l].rearrange("(j p) -> p j", p=128), writes=[ball], allow_slow_non_contiguous=True)
        kb.dma("sp", nrmall[:, l, 0, :], D["norm1"][l].rearrange("(c p) -> p c", p=128), writes=[nrmall], allow_slow_non_contiguous=True)
        kb.dma("sp", nrmall[:, l, 1, :], D["norm2"][l].rearrange("(c p) -> p c", p=128), writes=[nrmall], allow_slow_non_contiguous=True)
    cT = alloc(top, "cT", [128, 2, 8], F32)
    cTb = alloc(top, "cTb", [128, 2, 8], BF16)
    sig = alloc(top, "csig", [128, 2, 8], F32)
    for g in range(2):
        kb.dma("sp", cT[:, g, :], D["cvec"][g].rearrange("(c p) -> p c", p=128), writes=[cT], allow_slow_non_contiguous=True)
    kb.op("act", lambda e: e.activation(out=sig[:], in_=cT[:], func=AF.Sigmoid), reads=[cT], writes=[sig])
    kb.op("dve", lambda e: e.tensor_tensor(out=cTb[:], in0=cT[:], in1=sig[:], op=ALU.mult), reads=[cT, sig], writes=[cTb])

    def compute_mod(l):
        for blk in range(12):
            c0 = blk * 512
            wb, wv = wload([(D["w_ada"][l][:, c0:c0 + 512], 512)], 8)
            p = ps()
            for cc in range(4):
                for k in range(8):
                    kb.op("pe", lambda e: e.matmul(p[:, 2 * cc:2 * cc + 2], lhsT=wv[:, k, cc * 128:(cc + 1) * 128], rhs=cTb[:, :, k], start=(k == 0), stop=(k == 7)),
                          reads=[cTb, wb], writes=[p], inc=(k == 7 and cc == 3))
            kb.op("dve", lambda e: e.tensor_tensor(out=modall[:, l, blk * 4:(blk + 1) * 4, :], in0=p[:, 0:8].rearrange("p (j g) -> p j g", g=2),
                                                   in1=bcast(ball[:, l, blk * 4:(blk + 1) * 4], [128, 4, 2], 2), op=ALU.add), reads=[p, ball], writes=[modall])

    compute_mod(0)
    kb.barrier()

    def load_x(src, nblk):
        with scope() as st:
            xin = [alloc(st, "xin%d" % i, [128, 1024], F32) for i in range(2)]
            for tb in range(nblk):
                tiles = []
                for tl in range(4):
                    pass
                for tl in range(4):
                    xi = xin[(tb * 4 + tl) % 2]
                    t0 = (tb * 4 + tl) * 128
                    kb.dma("sp", xi[:], src[t0:t0 + 128, :], writes=[xi])
                    for half in range(2):
                        p = ps()
                        for cc in range(4):
                            c = half * 4 + cc
                            kb.op("pe", lambda e: e.matmul(p[:, cc * 128:(cc + 1) * 128], lhsT=xi[:, c * 128:(c + 1) * 128], rhs=ident_f,
                                                           start=True, stop=True), reads=[xi, cf], writes=[p], inc=(cc == 3))
                        kb.op("act", lambda e: e.activation(
                            out=xT[tb][:, half * 4:half * 4 + 4, tl * 128:(tl + 1) * 128],
                            in_=p[:, :].rearrange("p (c t) -> p c t", c=4), func=AF.Copy), reads=[p], writes=[xT[tb]])

    def norm_block(st_tmp, tb, Acol, Bcol, dst, dst_dt_is_bf16=True):
        sq, rstd, tmp2 = st_tmp
        if isinstance(sq, list):
            sq, rstd = sq[tb % 2], rstd[tb % 2]
        kb.op("act", lambda e: e.activation(out=sq[:], in_=xT[tb][:], func=AF.Square), reads=[xT[tb]], writes=[sq])
        p = ps()
        for c in range(8):
            kb.op("pe", lambda e: e.matmul(p[:, :], lhsT=ones_b, rhs=sq[:, c, :], start=(c == 0), stop=(c == 7)),
                  reads=[sq, cb], writes=[p], inc=(c == 7))
        kb.op("dve", lambda e: e.tensor_scalar(out=rstd[:], in0=p[:, :], scalar1=1.0 / D_MODEL, scalar2=EPS, op0=ALU.mult, op1=ALU.add),
              reads=[p], writes=[rstd])
        kb.op("act", lambda e: e.activation(out=rstd[:], in_=rstd[:], func=AF.Ln), reads=[rstd], writes=[rstd])
        kb.op("act", lambda e: e.activation(out=rstd[:], in_=rstd[:], func=AF.Exp, scale=-0.5), reads=[rstd], writes=[rstd])
        for c in range(8):
            t2 = tmp2[c % len(tmp2)]
            kb.op("dve", lambda e: e.tensor_tensor(out=t2[:], in0=xT[tb][:, c, :], in1=rstd[:], op=ALU.mult),
                  reads=[xT[tb], rstd], writes=[t2])
            if Bcol is not None:
                kb.op("act", lambda e: e.activation(out=dst[:, c, :], in_=t2[:], func=AF.Identity, bias=Bcol[:, c:c + 1], scale=Acol[:, c:c + 1]),
                      reads=[t2, AB], writes=[dst])
            else:
                kb.op("act", lambda e: e.activation(out=dst[:, c, :], in_=t2[:], func=AF.Identity, scale=Acol[:, c:c + 1]),
                      reads=[t2, nfc], writes=[dst])

    def load_mod(l, g):
        kb.op("dve", lambda e: e.tensor_copy(out=modc[:], in_=modall[:, l, :, g].rearrange("p (v c) -> p v c", v=6)), reads=[modall], writes=[modc])
        kb.op("dve", lambda e: e.tensor_copy(out=nrm[:], in_=nrmall[:, l, :, :]), reads=[nrmall], writes=[nrm])
        for j, (vs, vh) in enumerate(((1, 0), (4, 3))):
            kb.op("dve", lambda e: e.scalar_tensor_tensor(out=AB[:, 2 * j, :], in0=modc[:, vs, :], scalar=1.0, in1=nrm[:, j, :],
                                                          op0=ALU.add, op1=ALU.mult), reads=[modc, nrm], writes=[AB])
            kb.op("dve", lambda e: e.tensor_copy(out=AB[:, 2 * j + 1, :], in_=modc[:, vh, :]), reads=[modc], writes=[AB])

    def ffn(l, blocks):
        kb.label = 'ffn'
        nb = len(blocks)
        with scope() as st:
            actT = alloc(st, "actT", [128, 22, nb * 512], BF16)
            sg = [alloc(st, "sg%d" % i, [128, 512], F32) for i in range(2)]
            it = 0
            for jj in range(11):
                c0 = jj * 256
                wb, wv = wload([(D["w_ffn_in"][l][:, c0:c0 + 256], 256), (D["w_ffn_in"][l][:, D_FF + c0:D_FF + c0 + 256], 256)], 8, three=True)
                for j2 in range(2):
                    j = jj * 2 + j2
                    for bi, tb in enumerate(blocks):
                        pg = ps()
                        pu = ps()
                        for k in range(8):
                            kb.op("pe", lambda e: e.matmul(pg[:, :], lhsT=wv[:, k, j2 * 128:(j2 + 1) * 128], rhs=hT[tb][:, k, :],
                                                           start=(k == 0), stop=(k == 7)), reads=[wb, hT[tb]], writes=[pg], inc=(k == 7))
                        for k in range(8):
                            kb.op("pe", lambda e: e.matmul(pu[:, :], lhsT=wv[:, k, 256 + j2 * 128:256 + (j2 + 1) * 128], rhs=hT[tb][:, k, :],
                                                           start=(k == 0), stop=(k == 7)), reads=[wb, hT[tb]], writes=[pu], inc=(k == 7))
                        s = sg[it % 2]
                        it += 1
                        kb.op("act", lambda e: e.activation(out=s[:], in_=pg[:, :], func=AF.Silu), reads=[pg], writes=[s])
                        kb.op("dve", lambda e: e.tensor_tensor(out=actT[:, j, bi * 512:(bi + 1) * 512], in0=s[:], in1=pu[:, :], op=ALU.mult),
                              reads=[s, pu], writes=[actT])
            for c in range(8):
                wb, wv = wload([(D["w_ffn_out"][l][:, c * 128:(c + 1) * 128], 128)], 22, three=True)
                for bi, tb in enumerate(blocks):
                    p = ps()
                    for k in range(22):
                        kb.op("pe", lambda e: e.matmul(p[:, :], lhsT=wv[:, k, :], rhs=actT[:, k, bi * 512:(bi + 1) * 512],
                                                       start=(k == 0), stop=(k == 21)), reads=[wb, actT], writes=[p], inc=(k == 21))
                    kb.op("dve", lambda e: e.scalar_tensor_tensor(out=xT[tb][:, c, :], in0=p[:, :], scalar=modc[:, 5, c:c + 1], in1=xT[tb][:, c, :],
                                                                  op0=ALU.mult, op1=ALU.add), reads=[p, modc, xT[tb]], writes=[xT[tb]])
        kb.barrier()

    def final_out(nblk, dst):
        kb.label = 'final'
        with scope() as st:
            sq = alloc(st, "sq", [128, 8, 512], BF16)
            rstd = alloc(st, "rstd", [128, 512], F32)
            tmp2 = [alloc(st, "tmpn%d" % i, [128, 512], F32) for i in range(2)]
            xn = alloc(st, "xn", [128, 8, 512], F32)
            ot = [alloc(st, "ot%d" % i, [128, 1024], F32) for i in range(2)]
            for tb in range(nblk):
                norm_block((sq, rstd, tmp2), tb, nfc, None, xn)
                for tl in range(4):
                    o = ot[tl % 2]
                    for half in range(2):
                        p = ps()
                        for cc in range(4):
                            c = half * 4 + cc
                            kb.op("pe", lambda e: e.matmul(p[:, cc * 128:(cc + 1) * 128], lhsT=xn[:, c, tl * 128:(tl + 1) * 128], rhs=ident_f,
                                                           start=True, stop=True), reads=[xn, cf], writes=[p], inc=(cc == 3))
                        kb.copy(o[:, half * 512:(half + 1) * 512], p[:, :], reads=[p], writes=[o])
                    t0 = (tb * 4 + tl) * 128
                    kb.dma("sp", dst[t0:t0 + 128, :], o[:], reads=[o])
        kb.barrier()


    bd64_b = cb.t[:, 768:896]
    rm_b = cb.t[:, 896:1024]

    def proj_tm(wb, wv, s0, n, tb, tl, p):
        for k in range(8):
            kb.op("pe", lambda e: e.matmul(p[:, 0:n], lhsT=hT[tb][:, k, tl * 128:(tl + 1) * 128], rhs=wv[:, k, s0:s0 + n],
                                           start=(k == 0), stop=(k == 7)), reads=[wb, hT[tb]], writes=[p], inc=(k == 7))

    def load_wo(l, row0):
        kb.dma("pool", wo_buf[:], D["w_out"][l][row0:row0 + 512, :].rearrange("(k p) n -> p k n", p=128), writes=[wo_buf])

    def mixer_out(st, ytm, tb, yT):
        for c in range(4):
            p = ps()
            for tl in range(4):
                kb.op("pe", lambda e: e.matmul(p[:, tl * 128:(tl + 1) * 128], lhsT=ytm[:, tl, c * 128:(c + 1) * 128], rhs=ident_b,
                                               start=True, stop=True), reads=[ytm, cb], writes=[p], inc=(tl == 3))
            kb.copy(yT[:, c, :], p[:, :], reads=[p], writes=[yT])
        for c in range(8):
            p = ps()
            for k in range(4):
                kb.op("pe", lambda e: e.matmul(p[:, :], lhsT=wo_buf[:, k, c * 128:(c + 1) * 128], rhs=yT[:, k, :],
                                               start=(k == 0), stop=(k == 3)), reads=[wo_buf, yT], writes=[p], inc=(k == 3))
            kb.op("dve", lambda e: e.scalar_tensor_tensor(out=xT[tb][:, c, :], in0=p[:, :], scalar=modc[:, 2, c:c + 1], in1=xT[tb][:, c, :],
                                                          op0=ALU.mult, op1=ALU.add), reads=[p, modc, xT[tb]], writes=[xT[tb]])

    def attention(l, g, kind, nblk):
        kb.label = 'attn_%s_g%d' % (kind, g)
        sample = (g == 1)
        L = 2048 if sample else 256
        nseq = 1 if sample else 2
        nctx = 2 if sample else 0
        lt = L // 128
        nkt = lt + nctx
        ntok = nblk * 512
        if kind == "d":
            qc0, kc0, vc0, nkc, nvh, ve, orow0, nheads = 4896, 5408, 5536, 1, 2, 64, 1536, 8
            ck, cv, ok, ov = D["cgk"], D["cgv"], D["ngk"], D["ngv"]
        else:
            qc0, kc0, vc0, nkc, nvh, ve, orow0, nheads = 1296, 1808, 2320, 4, 4, 128, 512, 4
            ck, cv, ok, ov = D["cdk"], D["cdv"], D["ndk"], D["ndv"]
        scale = 64 ** -0.5
        kw = nkc * 128
        vw = nvh * ve
        lam_init = 0.8 - 0.6 * math.exp(-0.3 * l)
        with scope() as st:
            qT = alloc(st, "qT", [128, 4, ntok], BF16)
            kT = alloc(st, "kT", [128, nkc, nseq * nkt * 128], BF16)
            vsw = ve + 1 if kind == "d" else ve
            vaug = alloc(st, "vaug", [128, nseq * nkt, nvh, vsw], BF16)
            vodd = alloc(st, "vodd", [128, nseq * nkt, nvh, 128], BF16) if kind == "d" else None
            yT = alloc(st, "yT", [128, 4, 512], BF16)
            pTs = [alloc(st, "pT%d" % i, [128, 512], BF16) for i in range(4)]
            sqb = alloc(st, "sqb", [128, 512], BF16)
            rs = alloc(st, "rs", [128, 512], F32)
            qn = alloc(st, "qn", [128, 512], BF16)
            t1 = alloc(st, "t1", [128, 512], F32)
            fin_bufs = [(rs, t1, None, sqb)]
            gcol = alloc(st, "gcol", [128, 2], F32)
            osb = alloc(st, "osb", [128, 512], F32)
            t2 = osb
            fin_bufs[0] = (rs, t1, osb, sqb)
            sm = alloc(st, "sm", [128, 16], F32)
            lamt = alloc(st, "lamt", [128, 4, 64], F32)
            kng = alloc(st, "kng", [128, 64], F32)
            ropeT = alloc(st, "ropeT", [128, 2, 2048], BF16) if sample else None
            if sample and kind == "b":
                pass
            else:
                try:
                    fin_bufs.append((alloc(st, "rs2", [128, 512], F32), alloc(st, "t12", [128, 512], F32),
                                     alloc(st, "osb2", [128, 512], F32) if kind == "b" else None, alloc(st, "sqb2", [128, 512], BF16) if kind == "b" else None))
                except AssertionError:
                    pass
            load_wo(l, orow0)
            if kind == "d":
                kb.op("dve", lambda e: e.memset(vaug[:, :, :, ve:ve + 1], 1.0), writes=[vaug])
                kb.op("dve", lambda e: e.memset(vodd[:, :, :, 0:1], 1.0), writes=[vodd])
                kb.op("dve", lambda e: e.memset(vodd[:, :, :, 1:64], 0.0), writes=[vodd])
            if sample:
                kb.dma("pool", ropeT[:], D["rope"], writes=[ropeT])
            if kind == "d":
                for j, nm in enumerate(("gqa_q_norm", "gqa_k_norm")):
                    for hh in range(2):
                        kb.dma("sp", gcol[hh * 64:(hh + 1) * 64, j:j + 1], D[nm][l].rearrange("(d o) -> d o", o=1), writes=[gcol])
                kb.dma("sp", kng[:], D["gqa_k_norm"][l].partition_broadcast(128), writes=[kng])
            else:
                for j, nm in enumerate(("diff_lq1", "diff_lk1", "diff_lq2", "diff_lk2")):
                    kb.dma("sp", lamt[:, j, :], D[nm][l].partition_broadcast(128), writes=[lamt])
                kb.op("dve", lambda e: e.tensor_tensor(out=lamt[:, 0, :], in0=lamt[:, 0, :], in1=lamt[:, 1, :], op=ALU.mult), reads=[lamt], writes=[lamt])
                kb.op("dve", lambda e: e.tensor_tensor(out=lamt[:, 2, :], in0=lamt[:, 2, :], in1=lamt[:, 3, :], op=ALU.mult), reads=[lamt], writes=[lamt])
                kb.op("dve", lambda e: e.tensor_reduce(out=sm[:, 2:3], in_=lamt[:, 0, :], axis=AX.X, op=ALU.add), reads=[lamt], writes=[sm])
                kb.op("dve", lambda e: e.tensor_reduce(out=sm[:, 3:4], in_=lamt[:, 2, :], axis=AX.X, op=ALU.add), reads=[lamt], writes=[sm])
                kb.op("act", lambda e: e.activation(out=sm[:, 2:4], in_=sm[:, 2:4], func=AF.Exp), reads=[sm], writes=[sm])
                kb.op("dve", lambda e: e.tensor_tensor(out=sm[:, 0:1], in0=sm[:, 2:3], in1=sm[:, 3:4], op=ALU.subtract), reads=[sm], writes=[sm])
                kb.op("dve", lambda e: e.tensor_scalar(out=sm[:, 1:2], in0=sm[:, 0:1], scalar1=lam_init, scalar2=-1.0, op0=ALU.add, op1=ALU.mult), reads=[sm], writes=[sm])

            def qk_post(p, dst, tb, normj):
                src = p
                if kind == "d":
                    kb.op("act", lambda e: e.activation(out=sqb[:], in_=p[:, :], func=AF.Square), reads=[p], writes=[sqb])
                    pn = ps()
                    kb.op("pe", lambda e: e.matmul(pn[:, :], lhsT=bd64_b, rhs=sqb[:], start=True, stop=True), reads=[cb, sqb], writes=[pn])
                    kb.op("dve", lambda e: e.tensor_scalar(out=rs[:], in0=pn[:, :], scalar1=1.0 / 64, scalar2=EPS, op0=ALU.mult, op1=ALU.add), reads=[pn], writes=[rs])
                    kb.op("act", lambda e: e.activation(out=rs[:], in_=rs[:], func=AF.Ln), reads=[rs], writes=[rs])
                    kb.op("act", lambda e: e.activation(out=rs[:], in_=rs[:], func=AF.Exp, scale=-0.5), reads=[rs], writes=[rs])
                    tgt = qn if sample else None
                    o_ap = qn[:] if sample else dst
                    kb.op("dve", lambda e: e.scalar_tensor_tensor(out=o_ap, in0=p[:, :], scalar=gcol[:, normj:normj + 1], in1=rs[:], op0=ALU.mult, op1=ALU.mult),
                          reads=[p, gcol, rs], writes=[qn if sample else dst_buf[0]])
                else:
                    o_ap = qn[:] if sample else dst
                    kb.copy(o_ap, p[:, :], reads=[p], writes=[qn if sample else dst_buf[0]])
                if sample:
                    pr = ps()
                    kb.op("pe", lambda e: e.matmul(pr[:, :], lhsT=rm_b, rhs=qn[:], start=True, stop=True), reads=[cb, qn], writes=[pr])
                    kb.op("dve", lambda e: e.tensor_tensor(out=t1[:], in0=qn[:], in1=ropeT[:, 0, tb * 512:(tb + 1) * 512], op=ALU.mult), reads=[qn, ropeT], writes=[t1])
                    kb.op("dve", lambda e: e.tensor_tensor(out=t2[:], in0=pr[:, :], in1=ropeT[:, 1, tb * 512:(tb + 1) * 512], op=ALU.mult), reads=[pr, ropeT], writes=[t2])
                    kb.op("dve", lambda e: e.tensor_tensor(out=dst, in0=t1[:], in1=t2[:], op=ALU.add), reads=[t1, t2], writes=[dst_buf[0]])

            dst_buf = [None]
            blocks = list(range(nblk))
            if kind == "d":
                pcs = []
                for j in range(4):
                    for hh in (j, 4 + j):
                        pcs.append((D["w_in"][l][:, qc0 + hh * 64:qc0 + (hh + 1) * 64], 64))
                wb, wv = wload(pcs, 8)
            else:
                wb, wv = wload([(D["w_in"][l][:, qc0:qc0 + 512], 512)], 8)
            dst_buf[0] = qT
            for j in range(4):
                for tb in blocks:
                    p = ps()
                    for k in range(8):
                        lh = wv[:, k, j * 128:(j + 1) * 128]
                        kb.op("pe", lambda e: e.matmul(p[:, :], lhsT=lh, rhs=hT[tb][:, k, :], start=(k == 0), stop=(k == 7)),
                              reads=[wb, hT[tb]], writes=[p], inc=(k == 7))
                    qk_post(p, qT[:, j, tb * 512:(tb + 1) * 512], tb, 0)
            wb, wv = wload([(D["w_in"][l][:, kc0:kc0 + kw], kw)], 8)
            dst_buf[0] = kT
            for j in range(nkc):
                for tb in blocks:
                    p = ps()
                    for k in range(8):
                        kb.op("pe", lambda e: e.matmul(p[:, :], lhsT=wv[:, k, j * 128:(j + 1) * 128], rhs=hT[tb][:, k, :], start=(k == 0), stop=(k == 7)),
                              reads=[wb, hT[tb]], writes=[p], inc=(k == 7))
                    qk_post(p, kT[:, j, nctx * 128 + tb * 512:nctx * 128 + (tb + 1) * 512], tb, 1)
            if not sample:
                for tt in range(ntok // 128):
                    sq_, r0 = divmod(tt, lt)
                    p = ps()
                    proj_tm(wb, wv, 0, kw, tt // 4, tt % 4, p)
                    kb.copy(osb[:, 0:kw], p[:, 0:kw], reads=[p], writes=[osb])
                    if kind == "d":
                        kb.op("dve", lambda e: e.tensor_tensor(out=t1[:, 0:128], in0=osb[:, 0:128], in1=osb[:, 0:128], op=ALU.mult), reads=[osb], writes=[t1])
                        kb.op("dve", lambda e: e.tensor_reduce(out=sm[:, 8:10], in_=t1[:, 0:128].rearrange("p (h d) -> p h d", d=64), axis=AX.X, op=ALU.add), reads=[t1], writes=[sm])
                        kb.op("dve", lambda e: e.tensor_scalar(out=sm[:, 8:10], in0=sm[:, 8:10], scalar1=1.0 / 64, scalar2=EPS, op0=ALU.mult, op1=ALU.add), reads=[sm], writes=[sm])
                        kb.op("act", lambda e: e.activation(out=sm[:, 8:10], in_=sm[:, 8:10], func=AF.Ln), reads=[sm], writes=[sm])
                        kb.op("act", lambda e: e.activation(out=sm[:, 8:10], in_=sm[:, 8:10], func=AF.Exp, scale=-0.5), reads=[sm], writes=[sm])
                        kb.op("dve", lambda e: e.tensor_tensor(out=t1[:, 0:128].rearrange("p (h d) -> p h d", d=64), in0=osb[:, 0:128].rearrange("p (h d) -> p h d", d=64),
                                                               in1=bcast(sm[:, 8:10], [128, 2, 64], 2), op=ALU.mult), reads=[osb, sm], writes=[t1])
                        kb.op("dve", lambda e: e.tensor_tensor(out=t2[:, 0:128].rearrange("p (h d) -> p h d", d=64), in0=t1[:, 0:128].rearrange("p (h d) -> p h d", d=64),
                                                               in1=bcast(kng[:], [128, 2, 64], 1), op=ALU.mult), reads=[t1, kng], writes=[t2])
                        kb.dma("sp", ok[sq_, l, r0 * 128:(r0 + 1) * 128, :], t2[:, 0:128], reads=[t2])
                    else:
                        kb.dma("sp", ok[sq_, l, r0 * 128:(r0 + 1) * 128, :], osb[:, 0:kw], reads=[osb])
            wb, wv = wload([(D["w_in"][l][:, vc0:vc0 + vw], vw)], 8)
            for tt in range(ntok // 128):
                sq_, r0 = divmod(tt, lt)
                ktile = sq_ * nkt + nctx + r0
                p = ps()
                proj_tm(wb, wv, 0, vw, tt // 4, tt % 4, p)
                kb.copy(vaug[:, ktile, :, 0:ve], p[:, 0:vw].rearrange("p (h e) -> p h e", e=ve), reads=[p], writes=[vaug])
                if kind == "d":
                    kb.copy(vodd[:, ktile, :, 64:128], p[:, 0:vw].rearrange("p (h e) -> p h e", e=ve), reads=[p], writes=[vodd])
                if not sample:
                    kb.op("dve", lambda e: e.tensor_copy(out=osb[:, 0:vw], in_=p[:, 0:vw]), reads=[p], writes=[osb])
                    kb.dma("sp", ov[sq_, l, r0 * 128:(r0 + 1) * 128, :], osb[:, 0:vw], reads=[osb])
            if sample:
                with scope() as st2:
                    ctxk = alloc(st2, "ctxk", [128, kw], F32)
                    for t in range(2):
                        kb.dma("sp", ctxk[:], ck[l][t * 128:(t + 1) * 128, :], writes=[ctxk])
                        for j in range(nkc):
                            p = ps()
                            kb.op("pe", lambda e: e.matmul(p[:, 0:128], lhsT=ctxk[:, j * 128:(j + 1) * 128], rhs=ident_f, start=True, stop=True),
                                  reads=[ctxk, cf], writes=[p])
                            kb.copy(kT[:, j, t * 128:(t + 1) * 128], p[:, 0:128], reads=[p], writes=[kT])
                    for t in range(2):
                        kb.dma("pool", vaug[:, t, :, 0:ve], cv[l][t * 128:(t + 1) * 128, :].rearrange("p (h e) -> p h e", e=ve), writes=[vaug])
                        if kind == "d":
                            kb.dma("pool", vodd[:, t, :, 64:128], cv[l][t * 128:(t + 1) * 128, :].rearrange("p (h e) -> p h e", e=ve), writes=[vodd])
                    kb.barrier(include_pool_dma=True)
            qblk = min(L, 512)
            pti = 0
            for s_ in range(nseq):
                for qb in range(L // qblk):
                    q0 = s_ * L + qb * qblk
                    qs = slice(q0, q0 + qblk)
                    yc = slice(q0 % 512, q0 % 512 + qblk)
                    its = []
                    if kind == "d":
                        for hp in range(4):
                            for kt in range(nkt):
                                its.append((hp, kt, 0))
                                its.append((hp + 4, kt, 0))
                    else:
                        for h in range(nheads):
                            for kt in range(nkt):
                                its.append((h, kt, 0))
                                its.append((h, kt, 1))
                    hstate = {}
                    pts = {}
                    DEPTH = 2

                    def stageA2(j):
                        pss = []
                        for i in (2 * j, 2 * j + 1):
                            h, kt, r = its[i]
                            kc = (s_ * nkt + kt) * 128
                            if kind == "d":
                                rows, qch = (h // 4) * 64, h % 4
                                lhs = kT[rows:rows + 64, 0, kc:kc + 128]
                                rh = qT[rows:rows + 64, qch, qs]
                            else:
                                lhs = kT[r * 64:(r + 1) * 64, h, kc:kc + 128]
                                rh = qT[r * 64:(r + 1) * 64, h, qs]
                            pS = ps()
                            kb.op("pe", lambda e: e.matmul(pS[:, 0:qblk], lhsT=lhs, rhs=rh, start=True, stop=True), reads=[kT, qT], writes=[pS], inc=(i % 2 == 1))
                            pss.append(pS)
                        for i, pS in zip((2 * j, 2 * j + 1), pss):
                            pT = pTs[i % 4]
                            kb.op("act", lambda e: e.activation(out=pT[:, 0:qblk], in_=pS[:, 0:qblk], func=AF.Exp, scale=scale), reads=[pS], writes=[pT])
                            pts[i] = pT

                    def stageC(i):
                        h, kt, r = its[i]
                        pT = pts.pop(i)
                        last = (kt == nkt - 1)
                        rs, t1, osb, sqb = fin_bufs[h % len(fin_bufs)]
                        if kind == "d":
                            vh, odd = h // 4, h % 2
                            if kt == 0:
                                hstate[h] = ps_acc()
                            accO = hstate[h]
                            mo = 128 if odd else ve + 1
                            lh = vodd[:, s_ * nkt + kt, vh, :] if odd else vaug[:, s_ * nkt + kt, vh, :]
                            kb.op("pe", lambda e: e.matmul(accO[0:mo, 0:qblk], lhsT=lh, rhs=pT[:, 0:qblk], start=(kt == 0), stop=last),
                                  reads=[pT, vodd if odd else vaug], writes=[accO], inc=True)
                            if last:
                                drow = 0 if odd else 64
                                orow = 64 if odd else 0
                                kb.op("act", lambda e: e.activation(out=rs[drow:drow + 1, 0:qblk], in_=accO[drow:drow + 1, 0:qblk], func=AF.Ln), reads=[accO], writes=[rs])
                                kb.op("act", lambda e: e.activation(out=rs[drow:drow + 1, 0:qblk], in_=rs[drow:drow + 1, 0:qblk], func=AF.Exp, scale=-1.0), reads=[rs], writes=[rs])
                                pB = ps()
                                kb.op("pe", lambda e: e.matmul(pB[:, 0:qblk], lhsT=selm_f[drow:drow + 1, :], rhs=rs[drow:drow + 1, 0:qblk], start=True, stop=True),
                                      reads=[cf, rs], writes=[pB])
                                kb.copy(t1[orow:orow + 64, 0:qblk], pB[orow:orow + 64, 0:qblk], reads=[pB], writes=[t1])
                                kb.op("dve", lambda e: e.tensor_tensor(out=yT[orow:orow + 64, h // 2, yc], in0=accO[orow:orow + 64, 0:qblk], in1=t1[orow:orow + 64, 0:qblk], op=ALU.mult),
                                      reads=[accO, t1], writes=[yT])
                        else:
                            if kt == 0 and r == 0:
                                hstate[h] = ([ps_acc(), ps_acc()], [ps_acc(), ps_acc()])
                            accO, accD = hstate[h]
                            kb.op("pe", lambda e: e.matmul(accO[r][:, 0:qblk], lhsT=vaug[:, s_ * nkt + kt, h, :], rhs=pT[:, 0:qblk], start=(kt == 0), stop=last),
                                  reads=[pT, vaug], writes=[accO[r]], inc=False)
                            kb.op("pe", lambda e: e.matmul(accD[r][:, 0:qblk], lhsT=ones_b, rhs=pT[:, 0:qblk], start=(kt == 0), stop=last),
                                  reads=[pT, cb], writes=[accD[r]], inc=True)
                            if last and r == 1:
                                A, Bt, O = rs, t1, osb
                                kb.op("act", lambda e: e.activation(out=A[:, 0:qblk], in_=accD[0][:, 0:qblk], func=AF.Ln), reads=[accD[0]], writes=[A])
                                kb.op("act", lambda e: e.activation(out=A[:, 0:qblk], in_=A[:, 0:qblk], func=AF.Exp, scale=-1.0), reads=[A], writes=[A])
                                kb.op("act", lambda e: e.activation(out=Bt[:, 0:qblk], in_=accD[1][:, 0:qblk], func=AF.Ln), reads=[accD[1]], writes=[Bt])
                                kb.op("act", lambda e: e.activation(out=Bt[:, 0:qblk], in_=Bt[:, 0:qblk], func=AF.Exp, scale=-1.0), reads=[Bt], writes=[Bt])
                                kb.op("dve", lambda e: e.tensor_tensor(out=O[:, 0:qblk], in0=accO[0][:, 0:qblk], in1=A[:, 0:qblk], op=ALU.mult), reads=[accO[0], A], writes=[O])
                                kb.op("dve", lambda e: e.tensor_tensor(out=Bt[:, 0:qblk], in0=accO[1][:, 0:qblk], in1=Bt[:, 0:qblk], op=ALU.mult), reads=[accO[1], Bt], writes=[Bt])
                                kb.op("dve", lambda e: e.scalar_tensor_tensor(out=O[:, 0:qblk], in0=Bt[:, 0:qblk], scalar=sm[:, 1:2], in1=O[:, 0:qblk], op0=ALU.mult, op1=ALU.add),
                                      reads=[Bt, sm, O], writes=[O])
                                kb.op("act", lambda e: e.activation(out=sqb[:, 0:qblk], in_=O[:, 0:qblk], func=AF.Square), reads=[O], writes=[sqb])
                                pn = ps()
                                kb.op("pe", lambda e: e.matmul(pn[:, 0:qblk], lhsT=ones_b, rhs=sqb[:, 0:qblk], start=True, stop=True), reads=[cb, sqb], writes=[pn])
                                kb.op("dve", lambda e: e.tensor_scalar(out=A[:, 0:qblk], in0=pn[:, 0:qblk], scalar1=1.0 / 128, scalar2=EPS, op0=ALU.mult, op1=ALU.add), reads=[pn], writes=[A])
                                kb.op("act", lambda e: e.activation(out=A[:, 0:qblk], in_=A[:, 0:qblk], func=AF.Ln), reads=[A], writes=[A])
                                kb.op("act", lambda e: e.activation(out=A[:, 0:qblk], in_=A[:, 0:qblk], func=AF.Exp, scale=-0.5), reads=[A], writes=[A])
                                kb.op("dve", lambda e: e.scalar_tensor_tensor(out=yT[:, h, yc], in0=O[:, 0:qblk], scalar=1.0 - lam_init, in1=A[:, 0:qblk], op0=ALU.mult, op1=ALU.mult),
                                      reads=[O, A], writes=[yT])

                    n_pairs = len(its) // 2
                    for j in range(n_pairs + 1):
                        if j < n_pairs:
                            stageA2(j)
                        if j >= 1:
                            stageC(2 * (j - 1))
                            stageC(2 * (j - 1) + 1)
                    if (q0 + qblk) % 512 == 0:
                        tb = (q0 + qblk) // 512 - 1
                        for c in range(8):
                            p = ps()
                            for k in range(4):
                                kb.op("pe", lambda e: e.matmul(p[:, :], lhsT=wo_buf[:, k, c * 128:(c + 1) * 128], rhs=yT[:, k, :],
                                                               start=(k == 0), stop=(k == 3)), reads=[wo_buf, yT], writes=[p], inc=(k == 3))
                            kb.op("dve", lambda e: e.scalar_tensor_tensor(out=xT[tb][:, c, :], in0=p[:, :], scalar=modc[:, 2, c:c + 1], in1=xT[tb][:, c, :],
                                                                          op0=ALU.mult, op1=ALU.add), reads=[p, modc, xT[tb]], writes=[xT[tb]])
        kb.barrier()

    triF_f = cf.t[:, 256:384]
    triB_f = cf.t[:, 384:512]
    maskF_b = cb.t[:, 512:640]
    maskB_b = cb.t[:, 640:768]

    def proj_fm(l, c0, ncol, blocks, fn):
        for t0 in range(0, ncol, 512):
            n = min(512, ncol - t0)
            wb, wv = wload([(D["w_in"][l][:, c0 + t0:c0 + t0 + n], n)], 8)
            for s0 in range(0, n, 128):
                w = min(128, n - s0)
                for tb in blocks:
                    p = ps()
                    for k in range(8):
                        kb.op("pe", lambda e: e.matmul(p[0:w, :], lhsT=wv[:, k, s0:s0 + w], rhs=hT[tb][:, k, :], start=(k == 0), stop=(k == 7)),
                              reads=[wb, hT[tb]], writes=[p], inc=(k == 7))
                    fn((t0 + s0) // 128, tb, p, w)

    def conv_chunk(convin, acc, cwt, cbt, cc, nseq, L, dst_ap_fn, dst_buf):
        for s_ in range(nseq):
            a = acc[:, s_ * L:(s_ + 1) * L]
            kb.scale(a, convin[:, s_, 0:L], cwt[:, cc, 0:1], reads=[convin, cwt], writes=[acc])
            for tap in range(1, 5):
                kb.op("dve", lambda e: e.scalar_tensor_tensor(out=a, in0=convin[:, s_, tap:tap + L], scalar=cwt[:, cc, tap:tap + 1], in1=a,
                                                              op0=ALU.mult, op1=ALU.add), reads=[convin, cwt, acc], writes=[acc])
            if isinstance(dst_buf, list):
                for gg in range(2):
                    kb.op("act", lambda e: e.activation(out=dst_buf[gg][gg * 64:(gg + 1) * 64, s_ * L:(s_ + 1) * L], in_=acc[gg * 64:(gg + 1) * 64, s_ * L:(s_ + 1) * L],
                                                        func=AF.Silu, bias=cbt[gg * 64:(gg + 1) * 64, cc:cc + 1]), reads=[acc, cbt], writes=[dst_buf[gg]])
            else:
                kb.op("act", lambda e: e.activation(out=dst_ap_fn(s_), in_=a, func=AF.Silu, bias=cbt[:, cc:cc + 1]), reads=[acc, cbt], writes=[dst_buf])

    def evac_conv_in(convin, p, tb, nseq, L):
        if nseq == 1:
            kb.copy(convin[:, 0, 2 + tb * 512:2 + (tb + 1) * 512], p[:, :], reads=[p], writes=[convin])
        else:
            for s_ in range(2):
                kb.copy(convin[:, s_, 2:2 + L], p[:, s_ * L:(s_ + 1) * L], reads=[p], writes=[convin])

    def transpose_to_tm(src, dst_fn, dst_buf, T):
        for t0 in range(0, T, 4):
            p = ps()
            for j in range(4):
                kb.op("pe", lambda e: e.matmul(p[:, j * 128:(j + 1) * 128], lhsT=src[:, (t0 + j) * 128:(t0 + j + 1) * 128], rhs=ident_b, start=True, stop=True),
                      reads=[src, cb], writes=[p], inc=(j == 3))
            kb.copy(dst_fn(t0, 4), p[:, :].rearrange("p (t c) -> p t c", t=4), reads=[p], writes=[dst_buf])

    def ssd(l, g, nblk):
        kb.label = 'ssd_g%d' % g
        sample = (g == 1)
        L = 2048 if sample else 256
        nseq = 1 if sample else 2
        lt = L // 128
        ntok = nblk * 512
        T = ntok // 128
        blocks = list(range(nblk))
        with scope() as st:
            xtm = alloc(st, "xtm", [128, T, 512], BF16)
            Btm = alloc(st, "Btm", [128, T, 128], BF16)
            BT = alloc(st, "BT", [128, ntok], BF16)
            CTz = [alloc(st, "CT%d" % i, [128, ntok], BF16) for i in range(2)]
            for i_ in range(2):
                kb.op("dve", lambda e: e.memset(CTz[i_][:], 0.0), writes=[CTz[i_]])
            dtt = alloc(st, "dtt", [128, T, 16], F32)
            dtA = alloc(st, "dtA", [128, T, 16], F32)
            cum = alloc(st, "cum", [128, T, 16], F32)
            tot = alloc(st, "tot", [128, T, 16], F32)
            ecum = alloc(st, "ecum", [128, T, 16], F32)
            wd = alloc(st, "wd", [128, T, 16], F32)
            bj = alloc(st, "bj", [128, T, 16], F32)
            dec = alloc(st, "dec", [128, T, 2, 4], F32)
            abc = alloc(st, "abc", [128, 16], F32)
            dtb = alloc(st, "dtb", [128, 16], F32)
            dsk = alloc(st, "dsk", [128, 8], F32)
            ng = alloc(st, "ng", [128, 512], F32)
            cwt = alloc(st, "cwt", [128, 6, 5], F32)
            cbt = alloc(st, "cbt", [128, 6], F32)
            load_wo(l, 0)
            kb.dma("sp", abc[:], D["ssd_a_log"][l].partition_broadcast(128), writes=[abc])
            kb.dma("sp", dtb[:], D["ssd_dt_bias"][l].partition_broadcast(128), writes=[dtb])
            kb.dma("sp", dsk[:], D["ssd_d"][l].partition_broadcast(128), writes=[dsk])
            kb.dma("sp", ng[:], D["ssd_norm"][l].partition_broadcast(128), writes=[ng])
            for tap in range(5):
                kb.dma("sp", cwt[:, :, tap], D["conv_ssd_w"][l, tap].rearrange("(c p) -> p c", p=128), writes=[cwt], allow_slow_non_contiguous=True)
            kb.dma("sp", cbt[:], D["conv_ssd_b"][l].rearrange("(c p) -> p c", p=128), writes=[cbt], allow_slow_non_contiguous=True)
            kb.op("act", lambda e: e.activation(out=abc[:], in_=abc[:], func=AF.Exp), reads=[abc], writes=[abc])
            kb.op("dve", lambda e: e.tensor_scalar(out=abc[:], in0=abc[:], scalar1=-1.0, scalar2=None, op0=ALU.mult), reads=[abc], writes=[abc])
            with scope() as st1:
                convins = [alloc(st1, "convin", [128, nseq, L + 4], F32) for _ in range(2)]
                acc = alloc(st1, "cacc", [128, ntok], F32)
                xcT = alloc(st1, "xcT", [128, ntok], BF16)
                for cv_ in convins:
                    kb.op("dve", lambda e: e.memset(cv_[:], 0.0), writes=[cv_])

                def cb_fn(ci, tb, p, w):
                    convin = convins[ci % 2]
                    evac_conv_in(convin, p, tb, nseq, L)
                    if tb != blocks[-1]:
                        return
                    if ci < 4:
                        conv_chunk(convin, acc, cwt, cbt, ci, nseq, L, lambda s_: xcT[:, s_ * L:(s_ + 1) * L], xcT)
                        transpose_to_tm(xcT, lambda t0, n: xtm[:, t0:t0 + n, ci * 128:(ci + 1) * 128], xtm, T)
                    elif ci == 4:
                        conv_chunk(convin, acc, cwt, cbt, ci, nseq, L, lambda s_: BT[:, s_ * L:(s_ + 1) * L], BT)
                        transpose_to_tm(BT, lambda t0, n: Btm[:, t0:t0 + n, :], Btm, T)
                    else:
                        conv_chunk(convin, acc, cwt, cbt, ci, nseq, L, None, CTz)
                proj_fm(l, 512, 768, blocks, cb_fn)
            kb.barrier()
            if SSD_PH < 2:
                return
            wb, wv = wload([(D["w_in"][l][:, 1280:1296], 16)], 8)
            for tt in range(T):
                p = ps()
                proj_tm(wb, wv, 0, 16, tt // 4, tt % 4, p)
                kb.op("dve", lambda e: e.tensor_tensor(out=dtt[:, tt, :], in0=p[:, 0:16], in1=dtb[:], op=ALU.add), reads=[p, dtb], writes=[dtt])
            kb.op("act", lambda e: e.activation(out=dtt[:], in_=dtt[:], func=AF.Exp), reads=[dtt], writes=[dtt])
            kb.op("act", lambda e: e.activation(out=dtt[:], in_=dtt[:], func=AF.Ln, bias=1.0), reads=[dtt], writes=[dtt])
            kb.op("dve", lambda e: e.tensor_tensor(out=dtA[:], in0=dtt[:], in1=bcast(abc[:], [128, T, 16], 1), op=ALU.mult), reads=[dtt, abc], writes=[dtA])
            for tt in range(T):
                p = ps()
                kb.op("pe", lambda e: e.matmul(p[:, 0:16], lhsT=triF_f, rhs=dtA[:, tt, :], start=True, stop=True), reads=[cf, dtA], writes=[p], inc=False)
                kb.op("pe", lambda e: e.matmul(p[:, 16:32], lhsT=triB_f, rhs=dtA[:, tt, :], start=True, stop=True), reads=[cf, dtA], writes=[p], inc=False)
                kb.op("pe", lambda e: e.matmul(p[:, 32:48], lhsT=ones_f, rhs=dtA[:, tt, :], start=True, stop=True), reads=[cf, dtA], writes=[p])
                kb.op("dve", lambda e: e.tensor_copy(out=cum[:, tt, 0:8], in_=p[:, 0:8]), reads=[p], writes=[cum])
                kb.op("dve", lambda e: e.tensor_copy(out=cum[:, tt, 8:16], in_=p[:, 24:32]), reads=[p], writes=[cum])
                kb.op("dve", lambda e: e.tensor_copy(out=tot[:, tt, :], in_=p[:, 32:48]), reads=[p], writes=[tot])
            kb.op("act", lambda e: e.activation(out=ecum[:], in_=cum[:], func=AF.Exp), reads=[cum], writes=[ecum])
            kb.op("dve", lambda e: e.tensor_tensor(out=wd[:], in0=tot[:], in1=cum[:], op=ALU.subtract), reads=[tot, cum], writes=[wd])
            kb.op("act", lambda e: e.activation(out=wd[:], in_=wd[:], func=AF.Exp), reads=[wd], writes=[wd])
            kb.op("dve", lambda e: e.tensor_tensor(out=wd[:], in0=wd[:], in1=dtt[:], op=ALU.mult), reads=[wd, dtt], writes=[wd])
            kb.op("act", lambda e: e.activation(out=bj[:], in_=dtt[:], func=AF.Ln), reads=[dtt], writes=[bj])
            kb.op("dve", lambda e: e.tensor_tensor(out=bj[:], in0=bj[:], in1=cum[:], op=ALU.subtract), reads=[bj, cum], writes=[bj])
            tot4 = tot[:].rearrange("p t (d h) -> p t d h", d=2)
            for gg in range(2):
                kb.op("act", lambda e: e.activation(out=dec[gg * 64:(gg + 1) * 64], in_=tot4[gg * 64:(gg + 1) * 64, :, :, gg * 4:(gg + 1) * 4], func=AF.Exp),
                      reads=[tot], writes=[dec])
            if SSD_PH < 3:
                kb.barrier()
                return
            with scope() as st2:
                Hprev = alloc(st2, "Hprev", [128, T, 2, 256], BF16)
                st3 = ExitStack()
                Hs = [alloc(st3, "Hs%d" % i, [128, 256], F32) for i in range(2)]
                xws = [alloc(st3, "xw%d" % i, [128, 512], BF16) for i in range(2)]
                hx = alloc(st3, "hx", [128, 2, 128], F32)
                ho = alloc(st3, "ho", [128, 128], F32)
                xi = 0
                for dr in range(2):
                    H = Hs[dr]
                    for s_ in range(nseq):
                        if sample:
                            for blk in range(2):
                                for two in range(2):
                                    kb.dma("sp", hx[two * 64:(two + 1) * 64, blk, :].rearrange("p (g n) -> p g n", g=2),
                                           D["sssm"][l, dr].rearrange("(g r) p n -> r p g n", g=2)[blk * 2 + two], writes=[hx])
                            for blk in range(2):
                                p = ps()
                                kb.op("pe", lambda e: e.matmul(p[:, 0:128], lhsT=hx[:, blk, :], rhs=ident_f, start=True, stop=True), reads=[hx, cf], writes=[p])
                                kb.copy(H[:, blk * 128:(blk + 1) * 128], p[:, 0:128], reads=[p], writes=[H])
                        else:
                            kb.op("dve", lambda e: e.memset(H[:], 0.0), writes=[H])
                        order = range(lt) if dr == 0 else range(lt - 1, -1, -1)
                        for r0 in order:
                            tt = s_ * lt + r0
                            kb.copy(Hprev[:, tt, dr, :], H[:], reads=[H], writes=[Hprev])
                            xw = xws[xi % 2]
                            xi += 1
                            kb.op("dve", lambda e: e.tensor_tensor(out=xw[:].rearrange("p (h d) -> p h d", d=64), in0=xtm[:, tt, :].rearrange("p (h d) -> p h d", d=64),
                                                                   in1=bcast(wd[:, tt, dr * 8:(dr + 1) * 8], [128, 8, 64], 2), op=ALU.mult), reads=[xtm, wd], writes=[xw])
                            p = ps()
                            kb.op("pe", lambda e: e.matmul(p[:, :], lhsT=Btm[:, tt, :], rhs=xw[:], start=True, stop=True), reads=[Btm, xw], writes=[p])
                            kb.op("dve", lambda e: e.tensor_tensor(out=H[:].rearrange("p (h d) -> p h d", d=64), in0=H[:].rearrange("p (h d) -> p h d", d=64),
                                                                   in1=bcast(dec[:, tt, dr, :], [128, 4, 64], 2), op=ALU.mult), reads=[H, dec], writes=[H])
                            for gg in range(2):
                                kb.op("dve", lambda e: e.tensor_tensor(out=H[gg * 64:(gg + 1) * 64, :], in0=H[gg * 64:(gg + 1) * 64, :],
                                                                       in1=p[gg * 64:(gg + 1) * 64, gg * 256:(gg + 1) * 256], op=ALU.add), reads=[H, p], writes=[H])
                        if not sample:
                            for blk in range(2):
                                p = ps()
                                kb.op("pe", lambda e: e.matmul(p[:, 0:128], lhsT=H[:, blk * 128:(blk + 1) * 128], rhs=ident_f, start=True, stop=True), reads=[H, cf], writes=[p])
                                kb.copy(ho[:], p[:, 0:128], reads=[p], writes=[ho])
                                for two in range(2):
                                    kb.dma("sp", D["nssm"][s_, l, dr].rearrange("(g r) p n -> r p g n", g=2)[blk * 2 + two],
                                           ho[two * 64:(two + 1) * 64, :].rearrange("p (g n) -> p g n", g=2), reads=[ho])
                kb.barrier()
                st3.close()
                if SSD_PH < 4:
                    return
                Dg = alloc(st2, "Dg", [128, 16, 128], F32)
                Es = [alloc(st2, "E%d" % i, [128, 128], F32) for i in range(3)]
                Ms = [alloc(st2, "M%d" % i, [128, 128], BF16) for i in range(3)]
                ya = alloc(st2, "ya", [128, 512], F32)
                yu = alloc(st2, "yu", [128, 512], F32)
                zs = yu
                ytm = alloc(st2, "ytm", [128, 4, 512], BF16)
                yT = alloc(st2, "yT", [128, 4, 512], BF16)
                ss = alloc(st2, "ss", [128, 4], F32)
                wzb, wzv = wload([(D["w_in"][l][:, 0:512], 512)], 8)
                ei = 0
                for tt in range(T):
                    tk = slice(tt * 128, (tt + 1) * 128)
                    pz = ps_acc()
                    proj_tm(wzb, wzv, 0, 512, tt // 4, tt % 4, pz)
                    pBC = ps_acc()
                    for gg in range(2):
                        kb.op("pe", lambda e: e.matmul(pBC[:, gg * 128:(gg + 1) * 128], lhsT=BT[:, tk], rhs=CTz[gg][:, tk], start=True, stop=True),
                              reads=[BT, CTz[gg]], writes=[pBC], inc=(gg == 1))
                    kb.op("dve", lambda e: e.tensor_tensor(out=Dg[:], in0=bcast(ident_f, [128, 16, 128], 1), in1=bcast(cum[:, tt, :], [128, 16, 128], 2), op=ALU.mult),
                          reads=[cf, cum], writes=[Dg])
                    yint = ps_acc()
                    hd = [(h, dr) for h in range(8) for dr in range(2)]
                    mts = {}

                    def sA(i):
                        h, dr = hd[i]
                        pE = ps()
                        kb.op("pe", lambda e: e.matmul(pE[:, 0:128], lhsT=ones_f, rhs=Dg[:, dr * 8 + h, :], start=True, stop=False), reads=[cf, Dg], writes=[pE], inc=False)
                        kb.op("pe", lambda e: e.matmul(pE[:, 0:128], lhsT=ident_b, rhs=(maskF_b if dr == 0 else maskB_b), start=False, stop=True), reads=[cb], writes=[pE])
                        E = Es[i % 3]
                        M = Ms[i % 3]
                        kb.op("act", lambda e: e.activation(out=E[:], in_=pE[:, 0:128], func=AF.Exp, bias=bj[:, tt, dr * 8 + h:dr * 8 + h + 1]), reads=[pE, bj], writes=[E])
                        gg = h // 4
                        kb.op("dve", lambda e: e.tensor_tensor(out=M[:], in0=E[:], in1=pBC[:, gg * 128:(gg + 1) * 128], op=ALU.mult), reads=[E, pBC], writes=[M])
                        mts[i] = M

                    def sC(i):
                        h, dr = hd[i]
                        M = mts.pop(i)
                        kb.op("pe", lambda e: e.matmul(yint[:, h * 64:(h + 1) * 64], lhsT=M[:], rhs=xtm[:, tt, h * 64:(h + 1) * 64], start=(dr == 0), stop=(dr == 1)),
                              reads=[M, xtm], writes=[yint], inc=True)

                    for i in range(16 + 2):
                        if i < 16:
                            sA(i)
                        if i >= 2:
                            sC(i - 2)
                    if SSD_PH < 5:
                        continue
                    pY = [ps(), ps()]
                    for dr in range(2):
                        for gg in range(2):
                            kb.op("pe", lambda e: e.matmul(pY[dr][:, gg * 256:(gg + 1) * 256], lhsT=CTz[gg][:, tk], rhs=Hprev[:, tt, dr, :], start=True, stop=True),
                                  reads=[CTz[gg], Hprev], writes=[pY[dr]], inc=(gg == 1))
                    v3 = lambda ap: ap.rearrange("p (h d) -> p h d", d=64)
                    kb.op("dve", lambda e: e.tensor_tensor(out=v3(ya[:]), in0=v3(xtm[:, tt, :]), in1=bcast(dsk[:], [128, 8, 64], 2), op=ALU.mult), reads=[xtm, dsk], writes=[ya])
                    kb.op("dve", lambda e: e.tensor_tensor(out=ya[:], in0=ya[:], in1=yint[:, :], op=ALU.add), reads=[ya, yint], writes=[ya])
                    for dr in range(2):
                        kb.op("dve", lambda e: e.tensor_tensor(out=v3(yu[:]), in0=v3(pY[dr][:, :]), in1=bcast(ecum[:, tt, dr * 8:(dr + 1) * 8], [128, 8, 64], 2), op=ALU.mult),
                              reads=[pY[dr], ecum], writes=[yu])
                        kb.op("dve", lambda e: e.tensor_tensor(out=ya[:], in0=ya[:], in1=yu[:], op=ALU.add), reads=[ya, yu], writes=[ya])
                    if SSD_PH < 6:
                        continue
                    kb.op("act", lambda e: e.activation(out=zs[:], in_=pz[:, :], func=AF.Silu), reads=[pz], writes=[zs])
                    kb.op("dve", lambda e: e.tensor_tensor(out=ya[:], in0=ya[:], in1=zs[:], op=ALU.mult), reads=[ya, zs], writes=[ya])
                    kb.op("dve", lambda e: e.memset(ss[:, 0:1], 0.0), writes=[ss])
                    kb.op("act", lambda e: e.activation(out=yu[:], in_=ya[:], func=AF.Square, accum_out=ss[:, 0:1]), reads=[ya, ss], writes=[yu, ss])
                    kb.op("dve", lambda e: e.tensor_scalar(out=ss[:, 1:2], in0=ss[:, 0:1], scalar1=1.0 / 512, scalar2=EPS, op0=ALU.mult, op1=ALU.add), reads=[ss], writes=[ss])
                    kb.op("act", lambda e: e.activation(out=ss[:, 1:2], in_=ss[:, 1:2], func=AF.Ln), reads=[ss], writes=[ss])
                    kb.op("act", lambda e: e.activation(out=ss[:, 2:3], in_=ss[:, 1:2], func=AF.Exp, scale=-0.5), reads=[ss], writes=[ss])
                    kb.op("dve", lambda e: e.scalar_tensor_tensor(out=ytm[:, tt % 4, :], in0=ya[:], scalar=ss[:, 2:3], in1=ng[:], op0=ALU.mult, op1=ALU.mult),
                          reads=[ya, ss, ng], writes=[ytm])
                    if SSD_PH < 7:
                        continue
                    if tt % 4 == 3:
                        mixer_out(st2, ytm, tt // 4, yT)
        kb.barrier()


    def mlstm(l, g, nblk):
        kb.label = 'mlstm_g%d' % g
        sample = (g == 1)
        L = 2048 if sample else 256
        nseq = 1 if sample else 2
        lt = L // 128
        ntok = nblk * 512
        T = ntok // 128
        blocks = list(range(nblk))
        C0 = 2832
        lns = math.log(128 ** -0.5)
        for hg in range(2):
            h0 = hg * 2
            with scope() as st:
                qT = alloc(st, "mqT", [128, 2, ntok], BF16)
                kT = alloc(st, "mkT", [128, 2, ntok], BF16)
                vaug = alloc(st, "mvaug", [128, T, 2, 129], BF16)
                Cpb = alloc(st, "Cpb", [128, T, 2, 129], BF16)
                li = alloc(st, "li", [128, T, 4], F32)
                lf = alloc(st, "lf", [128, T, 4], F32)
                G = alloc(st, "G", [128, T, 4], F32)
                tot = alloc(st, "mtot", [128, T, 4], F32)
                pj = alloc(st, "pj", [128, T, 4], F32)
                pjs = alloc(st, "pjs", [128, T, 4], F32)
                gend = alloc(st, "gend", [128, T, 4], F32)
                mlb = alloc(st, "mlb", [128, T, 4], F32)
                wend = alloc(st, "wend", [128, T, 4], F32)
                mprev = alloc(st, "mprev", [128, T, 4], F32)
                gb = alloc(st, "gb", [128, 8], F32)
                cwt = alloc(st, "mcwt", [128, 4, 5], F32)
                cbt = alloc(st, "mcbt", [128, 4], F32)
                ng = alloc(st, "mng", [128, 256], F32)
                kb.dma("pool", wo_buf[:, 0:2, :], D["w_out"][l][1024 + h0 * 128:1024 + (h0 + 2) * 128, :].rearrange("(k p) n -> p k n", p=128), writes=[wo_buf])
                kb.dma("sp", ng[:], D["mlstm_norm"][l][h0 * 128:(h0 + 2) * 128].partition_broadcast(128), writes=[ng])
                goffs = [0 * 8 + 0 * 4 + h0, 1 * 8 + 0 * 4 + h0, 0 * 8 + 1 * 4 + h0, 1 * 8 + 1 * 4 + h0]
                for i_, go in enumerate(goffs):
                    kb.dma("sp", gb[:, i_ * 2:(i_ + 1) * 2], D["mlstm_gate_b"][l][go:go + 2].partition_broadcast(128), writes=[gb])
                for ci, ch0 in enumerate((h0 * 128, (h0 + 1) * 128, 512 + h0 * 128, 512 + (h0 + 1) * 128)):
                    for tap in range(5):
                        kb.dma("sp", cwt[:, ci, tap:tap + 1], D["conv_mlstm_w"][l, tap][ch0:ch0 + 128].rearrange("(p o) -> p o", o=1), writes=[cwt])
                    kb.dma("sp", cbt[:, ci:ci + 1], D["conv_mlstm_b"][l][ch0:ch0 + 128].rearrange("(p o) -> p o", o=1), writes=[cbt])
                kb.op("dve", lambda e: e.memset(vaug[:, :, :, 128:129], 1.0), writes=[vaug])
                with scope() as st1:
                    convins = [alloc(st1, "mconvin", [128, nseq, L + 4], F32) for _ in range(2)]
                    acc = alloc(st1, "mcacc", [128, ntok], F32)
                    for cv_ in convins:
                        kb.op("dve", lambda e: e.memset(cv_[:], 0.0), writes=[cv_])
                    wb, wv = wload([(D["w_in"][l][:, C0 + h0 * 128:C0 + (h0 + 2) * 128], 256),
                                    (D["w_in"][l][:, C0 + 512 + h0 * 128:C0 + 512 + (h0 + 2) * 128], 256)], 8)
                    for ci in range(4):
                        for tb in blocks:
                            p = ps()
                            for k in range(8):
                                kb.op("pe", lambda e: e.matmul(p[:, :], lhsT=wv[:, k, ci * 128:(ci + 1) * 128], rhs=hT[tb][:, k, :], start=(k == 0), stop=(k == 7)),
                                      reads=[wb, hT[tb]], writes=[p], inc=(k == 7))
                            evac_conv_in(convins[ci % 2], p, tb, nseq, L)
                        convin = convins[ci % 2]
                        dstb = qT if ci < 2 else kT
                        conv_chunk(convin, acc, cwt, cbt, ci, nseq, L, lambda s_: dstb[:, ci % 2, s_ * L:(s_ + 1) * L], dstb)
                kb.barrier()
                wb, wv = wload([(D["w_in"][l][:, C0 + 1024 + h0 * 128:C0 + 1024 + (h0 + 2) * 128], 256)], 8)
                for tt in range(T):
                    p = ps()
                    proj_tm(wb, wv, 0, 256, tt // 4, tt % 4, p)
                    kb.copy(vaug[:, tt, :, 0:128], p[:, 0:256].rearrange("p (h e) -> p h e", e=128), reads=[p], writes=[vaug])
                gc = C0 + 2048
                wb, wv = wload([(D["w_in"][l][:, gc + go:gc + go + 2], 2) for go in goffs], 8)
                for tt in range(T):
                    p = ps()
                    proj_tm(wb, wv, 0, 8, tt // 4, tt % 4, p)
                    kb.op("dve", lambda e: e.tensor_tensor(out=li[:, tt, :], in0=p[:, 0:4], in1=gb[:, 0:4], op=ALU.add), reads=[p, gb], writes=[li])
                    kb.op("dve", lambda e: e.tensor_tensor(out=lf[:, tt, :], in0=p[:, 4:8], in1=gb[:, 4:8], op=ALU.add), reads=[p, gb], writes=[lf])
                kb.op("act", lambda e: e.activation(out=lf[:], in_=lf[:], func=AF.Exp, scale=-1.0), reads=[lf], writes=[lf])
                kb.op("act", lambda e: e.activation(out=lf[:], in_=lf[:], func=AF.Ln, bias=1.0), reads=[lf], writes=[lf])
                kb.op("dve", lambda e: e.tensor_scalar(out=lf[:], in0=lf[:], scalar1=-1.0, scalar2=None, op0=ALU.mult), reads=[lf], writes=[lf])
                for tt in range(T):
                    p = ps()
                    kb.op("pe", lambda e: e.matmul(p[:, 0:4], lhsT=triF_f, rhs=lf[:, tt, :], start=True, stop=True), reads=[cf, lf], writes=[p], inc=False)
                    kb.op("pe", lambda e: e.matmul(p[:, 4:8], lhsT=triB_f, rhs=lf[:, tt, :], start=True, stop=True), reads=[cf, lf], writes=[p], inc=False)
                    kb.op("pe", lambda e: e.matmul(p[:, 8:12], lhsT=ones_f, rhs=lf[:, tt, :], start=True, stop=True), reads=[cf, lf], writes=[p])
                    kb.op("dve", lambda e: e.tensor_copy(out=G[:, tt, 0:2], in_=p[:, 0:2]), reads=[p], writes=[G])
                    kb.op("dve", lambda e: e.tensor_copy(out=G[:, tt, 2:4], in_=p[:, 6:8]), reads=[p], writes=[G])
                    kb.op("dve", lambda e: e.tensor_copy(out=tot[:, tt, :], in_=p[:, 8:12]), reads=[p], writes=[tot])
                kb.op("dve", lambda e: e.tensor_tensor(out=pj[:], in0=li[:], in1=G[:], op=ALU.subtract), reads=[li, G], writes=[pj])
                kb.op("dve", lambda e: e.tensor_scalar(out=pjs[:], in0=pj[:], scalar1=lns, scalar2=None, op0=ALU.add), reads=[pj], writes=[pjs])
                kb.op("dve", lambda e: e.tensor_tensor(out=gend[:], in0=pj[:], in1=tot[:], op=ALU.add), reads=[pj, tot], writes=[gend])
                with scope() as stt:
                    mrow = alloc(stt, "mrow8", [4, 1], F32)
                    d8 = alloc(stt, "d8", [4, 4], F32)
                    for tt in range(T):
                        p = ps()
                        kb.op("pe", lambda e: e.matmul(p[0:4, 0:128], lhsT=gend[:, tt, :], rhs=ident_f, start=True, stop=True), reads=[gend, cf], writes=[p])
                        kb.op("dve", lambda e: e.tensor_reduce(out=mrow[:], in_=p[0:4, 0:128], axis=AX.X, op=ALU.max), reads=[p], writes=[mrow])
                        kb.op("dve", lambda e: e.tensor_scalar(out=d8[:], in0=ident_f[0:4, 0:4], scalar1=mrow[:, 0:1], scalar2=None, op0=ALU.mult), reads=[cf, mrow], writes=[d8])
                        p2 = ps()
                        kb.op("pe", lambda e: e.matmul(p2[:, 0:4], lhsT=ones_f[0:4, :], rhs=d8[:], start=True, stop=True), reads=[cf, d8], writes=[p2])
                        kb.op("dve", lambda e: e.tensor_copy(out=mlb[:, tt, :], in_=p2[:, 0:4]), reads=[p2], writes=[mlb])
                    kb.barrier()
                kb.op("dve", lambda e: e.tensor_tensor(out=wend[:], in0=gend[:], in1=mlb[:], op=ALU.subtract), reads=[gend, mlb], writes=[wend])
                kb.op("act", lambda e: e.activation(out=wend[:], in_=wend[:], func=AF.Exp), reads=[wend], writes=[wend])
                with scope() as st2:
                    Cst = [alloc(st2, "Cst%d" % i, [128, 129], F32) for i in range(4)]
                    mp = alloc(st2, "mp", [128, 4], F32)
                    mt8 = alloc(st2, "mt8", [128, 16], F32)
                    kwts = [alloc(st2, "kwt", [128, 2, 128], BF16) for _ in range(2)]
                    Dgs = [alloc(st2, "mDg", [128, 4, 128], F32) for _ in range(2)]
                    Drs = [alloc(st2, "mDr", [128, 4, 128], F32) for _ in range(2)]
                    scs = [alloc(st2, "msc", [128, 40], F32) for _ in range(2)]
                    kwt, Dg, Dr, sc = kwts[0], Dgs[0], Drs[0], scs[0]
                    Es = [alloc(st2, "mE%d" % i, [128, 128], F32) for i in range(3)]
                    Ms = [alloc(st2, "mM%d" % i, [128, 128], BF16) for i in range(3)]
                    nds = [alloc(st2, "nd%d" % i, [128, 129], F32) for i in range(2)]
                    cbf = [alloc(st2, "cbf%d" % i, [128, 129], BF16) for i in range(2)]
                    hsums = [alloc(st2, "hsum", [128, 256], F32) for _ in range(2)]
                    hts = [alloc(st2, "mht", [128, 256], F32) for _ in range(2)]
                    sgs = [alloc(st2, "msg", [128, 256], F32) for _ in range(2)]
                    hsum, ht, sg = hsums[0], hts[0], sgs[0]
                    ytm = alloc(st2, "mytm", [128, 4, 256], BF16)
                    yT = alloc(st2, "myT", [128, 2, 512], BF16)
                    ei = 0
                    ei0 = [0]

                    def init_state(dr, s_):
                        for hh in range(2):
                            C = Cst[dr * 2 + hh]
                            if sample:
                                kb.dma("sp", C[:, 0:128], D["smc"][l, dr, h0 + hh], writes=[C])
                                kb.dma("sp", C[:, 128:129], D["smn"][l, dr, h0 + hh].rearrange("(p o) -> p o", o=1), writes=[C])
                            else:
                                kb.op("dve", lambda e: e.memset(C[:], 0.0), writes=[C])
                        if sample:
                            kb.dma("sp", mp[:, dr * 2:dr * 2 + 2], D["smm"][l][dr * 4 + h0:dr * 4 + h0 + 2].partition_broadcast(128), writes=[mp])
                        else:
                            kb.op("dve", lambda e: e.memset(mp[:, dr * 2:dr * 2 + 2], 0.0), writes=[mp])

                    def local_update(dr, tt):
                        cs = slice(dr * 2, dr * 2 + 2)
                        pk = ps()
                        for hh in range(2):
                            kb.op("pe", lambda e: e.matmul(pk[:, hh * 128:(hh + 1) * 128], lhsT=kT[:, hh, tt * 128:(tt + 1) * 128], rhs=ident_b, start=True, stop=True),
                                  reads=[kT, cb], writes=[pk], inc=(hh == 1))
                        kb.op("dve", lambda e: e.tensor_tensor(out=kwt[:], in0=pk[:, 0:256].rearrange("p (h d) -> p h d", d=128),
                                                               in1=bcast(wend[:, tt, cs], [128, 2, 128], 2), op=ALU.mult), reads=[pk, wend], writes=[kwt])
                        a = sc[:, 0:2]
                        mn = sc[:, 2:4]
                        sp_ = sc[:, 4:6]
                        sl_ = sc[:, 6:8]
                        kb.op("dve", lambda e: e.tensor_tensor(out=a, in0=tot[:, tt, cs], in1=mp[:, cs], op=ALU.add), reads=[tot, mp], writes=[sc])
                        kb.op("dve", lambda e: e.tensor_tensor(out=mn, in0=a, in1=mlb[:, tt, cs], op=ALU.max), reads=[sc, mlb], writes=[sc])
                        kb.op("dve", lambda e: e.tensor_tensor(out=sp_, in0=a, in1=mn, op=ALU.subtract), reads=[sc], writes=[sc])
                        kb.op("dve", lambda e: e.tensor_tensor(out=sl_, in0=mlb[:, tt, cs], in1=mn, op=ALU.subtract), reads=[sc, mlb], writes=[sc])
                        kb.op("act", lambda e: e.activation(out=sc[:, 4:8], in_=sc[:, 4:8], func=AF.Exp), reads=[sc], writes=[sc])
                        kb.op("dve", lambda e: e.tensor_copy(out=mp[:, cs], in_=mn), reads=[sc], writes=[mp])
                        for hh in range(2):
                            C = Cst[dr * 2 + hh]
                            pc = ps()
                            kb.op("pe", lambda e: e.matmul(pc[:, 0:129], lhsT=kwt[:, hh, :], rhs=vaug[:, tt, hh, :], start=True, stop=True), reads=[kwt, vaug], writes=[pc])
                            kb.scale(C[:], C[:], sc[:, 4 + hh:5 + hh], reads=[C, sc], writes=[C])
                            kb.op("dve", lambda e: e.scalar_tensor_tensor(out=C[:], in0=pc[:, 0:129], scalar=sc[:, 6 + hh:7 + hh], in1=C[:], op0=ALU.mult, op1=ALU.add),
                                  reads=[pc, sc, C], writes=[C])

                    def final_state(dr, s_):
                        for hh in range(2):
                            C = Cst[dr * 2 + hh]
                            kb.dma("sp", D["nmc"][s_, l, dr, h0 + hh], C[:, 0:128], reads=[C])
                            kb.dma("sp", D["nmn"][s_, l, dr, h0 + hh].rearrange("(p o) -> p o", o=1), C[:, 128:129], reads=[C])
                        kb.dma("sp", D["nmm"][s_, l:l + 1, dr * 4 + h0:dr * 4 + h0 + 2], mp[0:1, dr * 2:dr * 2 + 2], reads=[mp])

                    for s_ in range(nseq):
                        init_state(1, s_)
                        for r0 in range(lt - 1, -1, -1):
                            tt = s_ * lt + r0
                            kwt, sc = kwts[tt % 2], scs[tt % 2]
                            for hh in range(2):
                                kb.copy(Cpb[:, tt, hh, :], Cst[2 + hh][:], reads=[Cst[2 + hh]], writes=[Cpb])
                            kb.op("dve", lambda e: e.tensor_copy(out=mprev[:, tt, 2:4], in_=mp[:, 2:4]), reads=[mp], writes=[mprev])
                            local_update(1, tt)
                        if not sample:
                            final_state(1, s_)
                    wob, wov = wload([(D["w_in"][l][:, C0 + 1536 + h0 * 128:C0 + 1536 + (h0 + 2) * 128], 256)], 8)
                    for s_ in range(nseq):
                        init_state(0, s_)
                        for r0 in range(lt):
                            tt = s_ * lt + r0
                            tk = slice(tt * 128, (tt + 1) * 128)
                            kwt, Dg, Dr, sc = kwts[tt % 2], Dgs[tt % 2], Drs[tt % 2], scs[tt % 2]
                            hsum, ht, sg = hsums[tt % 2], hts[tt % 2], sgs[tt % 2]
                            kb.op("dve", lambda e: e.tensor_copy(out=mprev[:, tt, 0:2], in_=mp[:, 0:2]), reads=[mp], writes=[mprev])
                            kb.op("dve", lambda e: e.tensor_tensor(out=sc[:, 8:12], in0=G[:, tt, :], in1=mprev[:, tt, :], op=ALU.add), reads=[G, mprev], writes=[sc])
                            kb.op("dve", lambda e: e.tensor_tensor(out=Dg[:], in0=bcast(ident_f, [128, 4, 128], 1), in1=bcast(pj[:, tt, :], [128, 4, 128], 2), op=ALU.mult),
                                  reads=[cf, pj], writes=[Dg])
                            po = ps_acc()
                            proj_tm(wob, wov, 0, 256, tt // 4, tt % 4, po)
                            pSs = []
                            for hh in range(2):
                                pS = ps_acc()
                                kb.op("pe", lambda e: e.matmul(pS[:, 0:128], lhsT=kT[:, hh, tk], rhs=qT[:, hh, tk], start=True, stop=True), reads=[kT, qT], writes=[pS])
                                pSs.append(pS)
                            pms = []
                            for c in range(4):
                                dr = c // 2
                                pm = ps()
                                kb.op("pe", lambda e: e.matmul(pm[:, 0:128], lhsT=ones_f, rhs=Dg[:, c, :], start=True, stop=False), reads=[cf, Dg], writes=[pm], inc=False)
                                kb.op("pe", lambda e: e.matmul(pm[:, 0:128], lhsT=ident_b, rhs=(maskB_b if dr == 0 else maskF_b), start=False, stop=True), reads=[cb], writes=[pm])
                                pms.append(pm)
                            for c in range(4):
                                kb.op("dve", lambda e: e.tensor_reduce(out=sc[:, 12 + c:13 + c], in_=pms[c][:, 0:128], axis=AX.X, op=ALU.max), reads=[pms[c]], writes=[sc])
                            kb.op("dve", lambda e: e.tensor_tensor(out=sc[:, 12:16], in0=sc[:, 12:16], in1=G[:, tt, :], op=ALU.add), reads=[sc, G], writes=[sc])
                            kb.op("dve", lambda e: e.tensor_tensor(out=sc[:, 16:20], in0=sc[:, 12:16], in1=sc[:, 8:12], op=ALU.max), reads=[sc], writes=[sc])
                            kb.op("dve", lambda e: e.tensor_tensor(out=sc[:, 20:24], in0=G[:, tt, :], in1=sc[:, 16:20], op=ALU.subtract), reads=[sc, G], writes=[sc])
                            kb.op("dve", lambda e: e.tensor_tensor(out=sc[:, 24:28], in0=sc[:, 8:12], in1=sc[:, 16:20], op=ALU.subtract), reads=[sc], writes=[sc])
                            kb.op("act", lambda e: e.activation(out=sc[:, 24:28], in_=sc[:, 24:28], func=AF.Exp, bias=lns_col[:, 0:1]), reads=[sc, cf], writes=[sc])
                            kb.op("act", lambda e: e.activation(out=sc[:, 28:32], in_=sc[:, 16:20], func=AF.Exp, scale=-1.0), reads=[sc], writes=[sc])
                            kb.op("dve", lambda e: e.tensor_tensor(out=Dr[:], in0=bcast(ident_f, [128, 4, 128], 1), in1=bcast(sc[:, 20:24], [128, 4, 128], 2), op=ALU.mult),
                                  reads=[cf, sc], writes=[Dr])
                            items = [(0, 0), (0, 1), (1, 0), (1, 1)]
                            mts = {}

                            def mA(i):
                                hh, dr = items[i]
                                c = dr * 2 + hh
                                pW = ps()
                                kb.op("pe", lambda e: e.matmul(pW[:, 0:128], lhsT=ones_f, rhs=Dr[:, c, :], start=True, stop=False), reads=[cf, Dr], writes=[pW], inc=False)
                                kb.op("pe", lambda e: e.matmul(pW[:, 0:128], lhsT=ident_b, rhs=(maskF_b if dr == 0 else maskB_b), start=False, stop=True), reads=[cb], writes=[pW])
                                E = Es[(ei0[0] + i) % 3]
                                M = Ms[(ei0[0] + i) % 3]
                                kb.op("act", lambda e: e.activation(out=E[:], in_=pW[:, 0:128], func=AF.Exp, bias=pjs[:, tt, c:c + 1]), reads=[pW, pjs], writes=[E])
                                kb.op("dve", lambda e: e.tensor_tensor(out=M[:], in0=E[:], in1=pSs[hh][:, 0:128], op=ALU.mult), reads=[E, pSs[hh]], writes=[M])
                                if dr == 0:
                                    kb.copy(cbf[hh][:], Cst[hh][:], reads=[Cst[hh]], writes=[cbf[hh]])
                                mts[i] = M

                            def mC(i):
                                hh, dr = items[i]
                                c = dr * 2 + hh
                                M = mts.pop(i)
                                nd = nds[i % 2]
                                pN = ps()
                                kb.op("pe", lambda e: e.matmul(pN[:, 0:129], lhsT=M[:], rhs=vaug[:, tt, hh, :], start=True, stop=True), reads=[M, vaug], writes=[pN])
                                pI = ps()
                                if dr == 0:
                                    kb.op("pe", lambda e: e.matmul(pI[:, 0:129], lhsT=qT[:, hh, tk], rhs=cbf[hh][:], start=True, stop=True), reads=[qT, cbf[hh]], writes=[pI])
                                else:
                                    kb.op("pe", lambda e: e.matmul(pI[:, 0:129], lhsT=qT[:, hh, tk], rhs=Cpb[:, tt, hh, :], start=True, stop=True), reads=[qT, Cpb], writes=[pI])
                                kb.copy(nd[:], pN[:, 0:129], reads=[pN], writes=[nd])
                                kb.op("dve", lambda e: e.scalar_tensor_tensor(out=nd[:], in0=pI[:, 0:129], scalar=sc[:, 24 + c:25 + c], in1=nd[:], op0=ALU.mult, op1=ALU.add),
                                      reads=[pI, sc, nd], writes=[nd])
                                kb.op("dve", lambda e: e.tensor_scalar(out=sc[:, 32:33], in0=nd[:, 128:129], scalar1=-1.0, scalar2=None, op0=ALU.mult), reads=[nd], writes=[sc])
                                kb.op("dve", lambda e: e.tensor_tensor(out=sc[:, 32:33], in0=sc[:, 32:33], in1=nd[:, 128:129], op=ALU.max), reads=[nd, sc], writes=[sc])
                                kb.op("dve", lambda e: e.tensor_tensor(out=sc[:, 32:33], in0=sc[:, 32:33], in1=sc[:, 28 + c:29 + c], op=ALU.max), reads=[sc], writes=[sc])
                                kb.op("dve", lambda e: e.reciprocal(out=sc[:, 33:34], in_=sc[:, 32:33]), reads=[sc], writes=[sc])
                                if dr == 0:
                                    kb.scale(hsum[:, hh * 128:(hh + 1) * 128], nd[:, 0:128], sc[:, 33:34], reads=[nd, sc], writes=[hsum])
                                else:
                                    kb.op("dve", lambda e: e.scalar_tensor_tensor(out=hsum[:, hh * 128:(hh + 1) * 128], in0=nd[:, 0:128], scalar=sc[:, 33:34],
                                                                                  in1=hsum[:, hh * 128:(hh + 1) * 128], op0=ALU.mult, op1=ALU.add), reads=[nd, sc, hsum], writes=[hsum])

                            for i in range(4 + 2):
                                if i < 4:
                                    mA(i)
                                if i >= 2:
                                    mC(i - 2)
                            ei0[0] += 4
                            local_update(0, tt)
                            h3 = hsum[:].rearrange("p (h d) -> p h d", d=128)
                            t3 = ht[:].rearrange("p (h d) -> p h d", d=128)
                            kb.op("dve", lambda e: e.tensor_tensor(out=ht[:], in0=hsum[:], in1=hsum[:], op=ALU.mult), reads=[hsum], writes=[ht])
                            kb.op("dve", lambda e: e.tensor_reduce(out=sc[:, 34:36], in_=t3, axis=AX.X, op=ALU.add), reads=[ht], writes=[sc])
                            kb.op("dve", lambda e: e.tensor_scalar(out=sc[:, 34:36], in0=sc[:, 34:36], scalar1=1.0 / 128, scalar2=EPS, op0=ALU.mult, op1=ALU.add), reads=[sc], writes=[sc])
                            kb.op("act", lambda e: e.activation(out=sc[:, 34:36], in_=sc[:, 34:36], func=AF.Ln), reads=[sc], writes=[sc])
                            kb.op("act", lambda e: e.activation(out=sc[:, 36:38], in_=sc[:, 34:36], func=AF.Exp, scale=-0.5), reads=[sc], writes=[sc])
                            kb.op("dve", lambda e: e.tensor_tensor(out=t3, in0=h3, in1=bcast(sc[:, 36:38], [128, 2, 128], 2), op=ALU.mult), reads=[hsum, sc], writes=[ht])
                            kb.op("dve", lambda e: e.tensor_tensor(out=ht[:], in0=ht[:], in1=ng[:], op=ALU.mult), reads=[ht, ng], writes=[ht])
                            kb.op("act", lambda e: e.activation(out=sg[:], in_=po[:, 0:256], func=AF.Exp, scale=-1.0), reads=[po], writes=[sg])
                            kb.op("act", lambda e: e.activation(out=sg[:], in_=sg[:], func=AF.Ln, bias=1.0), reads=[sg], writes=[sg])
                            kb.op("act", lambda e: e.activation(out=sg[:], in_=sg[:], func=AF.Exp, scale=-1.0), reads=[sg], writes=[sg])
                            kb.op("dve", lambda e: e.tensor_tensor(out=ytm[:, tt % 4, :], in0=ht[:], in1=sg[:], op=ALU.mult), reads=[ht, sg], writes=[ytm])
                            if tt % 4 == 3:
                                tb = tt // 4
                                for c2 in range(2):
                                    p = ps()
                                    for tl in range(4):
                                        kb.op("pe", lambda e: e.matmul(p[:, tl * 128:(tl + 1) * 128], lhsT=ytm[:, tl, c2 * 128:(c2 + 1) * 128], rhs=ident_b, start=True, stop=True),
                                              reads=[ytm, cb], writes=[p], inc=(tl == 3))
                                    kb.copy(yT[:, c2, :], p[:, :], reads=[p], writes=[yT])
                                for c in range(8):
                                    p = ps()
                                    for k in range(2):
                                        kb.op("pe", lambda e: e.matmul(p[:, :], lhsT=wo_buf[:, k, c * 128:(c + 1) * 128], rhs=yT[:, k, :], start=(k == 0), stop=(k == 1)),
                                              reads=[wo_buf, yT], writes=[p], inc=(k == 1))
                                    kb.op("dve", lambda e: e.scalar_tensor_tensor(out=xT[tb][:, c, :], in0=p[:, :], scalar=modc[:, 2, c:c + 1], in1=xT[tb][:, c, :],
                                                                                  op0=ALU.mult, op1=ALU.add), reads=[p, modc, xT[tb]], writes=[xT[tb]])
                        if not sample:
                            final_state(0, s_)
            kb.barrier()

    def run_pass(g, src, dst, nblk):
        load_x(src, nblk)
        kb.barrier()
        for l in range(NL):
            kb.label = 'norm_g%d' % g
            load_mod(l, g)
            if g == 0 and l + 1 < NL:
                compute_mod(l + 1)
            with scope() as st:
                sq = [alloc(st, "sq", [128, 8, 512], BF16) for _ in range(2)]
                rstd = [alloc(st, "rstd", [128, 512], F32) for _ in range(2)]
                tmp2 = [alloc(st, "tmpn%d" % i, [128, 512], F32) for i in range(4)]
                for tb in range(nblk):
                    norm_block((sq, rstd, tmp2), tb, AB.t[:, 0, :], AB.t[:, 1, :], hT[tb])
            kb.barrier()
            for mk in MIXERS:
                if mk in "bd":
                    attention(l, g, mk, nblk)
                elif mk == "a":
                    ssd(l, g, nblk)
                elif mk == "c":
                    mlstm(l, g, nblk)
            with scope() as st:
                sq = [alloc(st, "sq", [128, 8, 512], BF16) for _ in range(2)]
                rstd = [alloc(st, "rstd", [128, 512], F32) for _ in range(2)]
                tmp2 = [alloc(st, "tmpn%d" % i, [128, 512], F32) for i in range(4)]
                for tb in range(nblk):
                    norm_block((sq, rstd, tmp2), tb, AB.t[:, 2, :], AB.t[:, 3, :], hT[tb])
            kb.barrier()
            for b0 in range(0, nblk, 2):
                ffn(l, list(range(b0, min(b0 + 2, nblk))))
        final_out(nblk, dst)

    run_pass(0, D["xp"], D["yp"], 1)
    run_pass(1, D["xs"], D["ys"], 4)

    kb.barrier(include_pool_dma=True)
    top.close()
    print("instructions:", kb.ninst, flush=True)
    if kb.stats is not None:
        tot = 0.0
        for lab, (mk, busy, nu, nfl, bub) in sorted(kb.stats.items(), key=lambda kv: -kv[1][0]):
            tot += mk
            print("  %-12s est_us=%8.0f units=%6d regions=%4d bubble_us=%6.0f busy: %s" % (lab, mk / 1e3, nu, nfl, bub / 1e3, " ".join("%s=%.0f" % (e_, v_ / 1e3) for e_, v_ in sorted(busy.items()))))
        print("  est total us", tot / 1e3)
    return nc


_CACHE = {}


def prep_inputs(inp):
    f = lambda a: np.ascontiguousarray(np.asarray(a, dtype=np.float32))
    consts = make_consts()
    rope = make_rope()
    shared = {}
    for name in ("w_ada", "b_ada", "norm1", "norm2", "w_in", "w_out", "conv_ssd_w", "conv_ssd_b", "ssd_d", "ssd_norm",
                 "diff_lq1", "diff_lk1", "diff_lq2", "diff_lk2", "conv_mlstm_w", "conv_mlstm_b", "mlstm_norm",
                 "gqa_q_norm", "gqa_k_norm", "w_ffn_in", "w_ffn_out", "norm_f"):
        shared[name] = f(inp[name])
    shared["ssd_a_log"] = f(inp["ssd_a_log"]).reshape(4, 16)
    shared["ssd_dt_bias"] = f(inp["ssd_dt_bias"]).reshape(4, 16)
    shared["mlstm_gate_b"] = f(inp["mlstm_gate_b"]).reshape(4, 16)
    shared["consts"] = consts
    shared["rope"] = rope
    xp = f(inp["x_prompt"])
    xs = f(inp["x_sample"])
    in_maps = []
    for c in range(8):
        b = c // 4
        m = dict(shared)
        m["xp"] = xp[2 * c:2 * c + 2].reshape(512, 1024)
        m["xs"] = xs[b]
        m["cvec"] = np.stack([f(inp["c_ctx"]), f(inp["c"])[b]], axis=0)
        m["cdk"] = f(inp["cache_diff_k"])[b].reshape(4, 256, 512)
        m["cdv"] = f(inp["cache_diff_v"])[b].reshape(4, 256, 512)
        m["cgk"] = f(inp["cache_gqa_k"])[b].reshape(4, 256, 128)
        m["cgv"] = f(inp["cache_gqa_v"])[b].reshape(4, 256, 128)
        m["sssm"] = f(inp["state_ssm"])[b]
        m["smc"] = f(inp["state_mlstm_c"])[b]
        m["smn"] = f(inp["state_mlstm_n"])[b]
        m["smm"] = f(inp["state_mlstm_m"])[b].reshape(4, 8)
        in_maps.append(m)
    if NL < 4:
        spec = dict(IN_SPECS)
        for m in in_maps:
            for k_ in list(m.keys()):
                if spec[k_][0] == 4 and len(spec[k_]) > 1:
                    m[k_] = np.ascontiguousarray(m[k_][:NL])
    return in_maps


def kernel(**inp):
    if "nc" not in _CACHE:
        _CACHE["nc"] = build_program()
    nc = _CACHE["nc"]
    in_maps = prep_inputs(inp)
    res = run_bass_kernel_spmd(nc, in_maps, core_ids=list(range(8)))
    return assemble(res.results)


def assemble(R):
    y_prompt = np.concatenate([R[c]["yp"].reshape(2, 256, 1024) for c in range(8)], axis=0)
    y_sample = np.stack([R[0]["ys"], R[4]["ys"]], axis=0)
    cat = lambda k: np.concatenate([R[c][k] for c in range(8)], axis=0)
    ndk = cat("ndk").reshape(16, 4, 256, 4, 2, 64)
    ndv = cat("ndv").reshape(16, 4, 256, 4, 128)
    ngk = cat("ngk").reshape(16, 4, 256, 2, 64)
    ngv = cat("ngv").reshape(16, 4, 256, 2, 64)
    nssm = cat("nssm")
    nmc = cat("nmc")
    nmn = cat("nmn")
    nmm = cat("nmm").reshape(16, 4, 2, 4)
    return (y_prompt, y_sample, ndk, ndv, ngk, ngv, nssm, nmc, nmn, nmm)
```

```python
import os
import math
from contextlib import ExitStack
import numpy as np
import concourse.bass as bass
import concourse.mybir as mybir
from concourse.bass_utils import run_bass_kernel_spmd

F32 = mybir.dt.float32
BF16 = mybir.dt.bfloat16
ALU = mybir.AluOpType
AF = mybir.ActivationFunctionType
AX = mybir.AxisListType

D_MODEL = 1024
DEPTH = 4
IN_COLS = 5664
D_FF = 2816
EPS = 1e-6
NEG = -30000.0

NL = int(os.environ.get("MK_NL", "4"))
MIXERS = os.environ.get("MK_MIX", "abcd")
SSD_PH = int(os.environ.get("MK_SSD_PH", "9"))
SSD_SUB = int(os.environ.get("MK_SSD_SUB", "9"))


class Buf:
    __slots__ = ("t", "w", "r")

    def __init__(self, t):
        self.t = t
        self.w = None
        self.r = []

    def __getitem__(self, idx):
        return self.t[idx]


class _Rec:
    def __init__(self):
        self.call = None

    def __getattr__(self, name):
        def f(*args, **kw):
            self.call = (name, args, kw)
            return self
        return f


class KB:
    NDMA_SEM = 8

    def __init__(self, nc):
        self.nc = nc
        self.engs = {"pe": nc.tensor, "act": nc.scalar, "dve": nc.vector, "pool": nc.gpsimd, "sp": nc.sync}
        self.sems = {}
        self.cnt = {}
        for k in ("pe", "act", "dve", "pool"):
            self.sems[k] = nc.alloc_semaphore(name="s_" + k)
            self.cnt[k] = 0
        self.dq = {}
        for q in ("sp", "pool", "act"):
            lst = []
            for i in range(self.NDMA_SEM):
                key = "d_%s%d" % (q, i)
                self.sems[key] = nc.alloc_semaphore(name=key)
                self.cnt[key] = 0
                lst.append(key)
            self.dq[q] = [lst, 0]
        self.seen = {e: {} for e in self.engs}
        self.ninst = 0
        self.defer = bool(int(os.environ.get('MK_SCHED', '1')))
        self.pending = []
        self.stats = {} if os.environ.get('MK_STATS') else None
        self.label = 'top'
        self.sched_mode = os.environ.get('MK_SMODE', 'cpx')

    def _wait(self, eng, k, v):
        seen = self.seen[eng]
        if seen.get(k, 0) >= v:
            return
        self.engs[eng].wait_ge(self.sems[k], v)
        self.ninst += 1
        seen[k] = v

    def _need(self, eng, reads, writes):
        need = {}

        def add(dep):
            if dep is None:
                return
            k, v = dep
            if need.get(k, 0) < v:
                need[k] = v
        for b in reads:
            add(b.w)
        for b in writes:
            add(b.w)
            for d in b.r:
                add(d)
        for k, v in need.items():
            if k == eng and eng == "pe":
                continue
            self._wait(eng, k, v)

    def _record(self, dep, reads, writes):
        for b in reads:
            b.r.append(dep)
            if len(b.r) > 64:
                mx = {}
                for k, v in b.r:
                    if mx.get(k, 0) < v:
                        mx[k] = v
                b.r = list(mx.items())
        for b in writes:
            b.w = dep
            b.r = []

    def op(self, eng, fn, reads=(), writes=(), inc=True):
        if self.defer:
            rec = _Rec()
            fn(rec)
            self.pending.append(("op", eng, rec.call, tuple(reads), tuple(writes), inc))
            return None
        return self._op_now(eng, fn, reads, writes, inc)

    def copy(self, out, in_, reads=(), writes=()):
        if not self.defer:
            return self._op_now("act", lambda e: e.activation(out=out, in_=in_, func=AF.Copy), reads, writes, True)
        c_dve = ("tensor_copy", (), {"out": out, "in_": in_})
        c_act = ("activation", (), {"out": out, "in_": in_, "func": AF.Copy})
        self.pending.append(("either", None, (c_dve, c_act), tuple(reads), tuple(writes), True))

    def scale(self, out, in_, col, reads=(), writes=()):
        if not self.defer:
            return self._op_now("dve", lambda e: e.tensor_scalar(out=out, in0=in_, scalar1=col, scalar2=None, op0=ALU.mult), reads, writes, True)
        c_dve = ("tensor_scalar", (), {"out": out, "in0": in_, "scalar1": col, "scalar2": None, "op0": ALU.mult})
        c_act = ("activation", (), {"out": out, "in_": in_, "func": AF.Identity, "scale": col})
        self.pending.append(("either", None, (c_dve, c_act), tuple(reads), tuple(writes), True))

    def _op_now(self, eng, fn, reads=(), writes=(), inc=True):
        self._need(eng, reads, writes)
        ins = fn(self.engs[eng])
        self.ninst += 1
        val = self.cnt[eng] + 1
        if inc:
            ins.then_inc(self.sems[eng], 1)
            self.cnt[eng] = val
        self._record((eng, val), reads, writes)
        return ins

    def dma(self, q, out, in_, reads=(), writes=(), **kw):
        if self.defer:
            self.pending.append(("dma", q, (out, in_, kw), tuple(reads), tuple(writes), True))
            return None
        return self._dma_now(q, out, in_, reads, writes, **kw)

    def _dma_now(self, q, out, in_, reads=(), writes=(), **kw):
        self._need(q, reads, writes)
        lst, i = self.dq[q]
        key = lst[i % len(lst)]
        self.dq[q][1] = i + 1
        if self.cnt[key]:
            self._wait(q, key, self.cnt[key])
        ins = self.engs[q].dma_start(out=out, in_=in_, **kw)
        self.ninst += 1
        self.cnt[key] += 16
        ins.then_inc(self.sems[key], 16)
        dep = (key, self.cnt[key])
        self._record(dep, reads, writes)
        return dep

    @staticmethod
    def _cost(kind, eng, call):
        def fsz(ap):
            n = 1
            for d in ap.shape[1:]:
                n *= d
            return n
        if kind == "dma":
            out = call[0]
            nb = fsz(out) * out.shape[0] * (2 if out.dtype == BF16 else 4)
            return 2000.0 + nb / 80.0
        name, args, kw = call
        if name == "matmul":
            n = fsz(kw["rhs"])
            passes = 4 if kw["lhsT"].dtype == F32 else 1
            return 70.0 + n * passes * 0.45
        out = kw.get("out", None)
        if out is None:
            out = kw.get("ap", args[0] if args else None)
        n = fsz(out) if out is not None else 64
        if eng == "act":
            return 230.0 + n * 0.75
        if name == "reciprocal":
            return 70.0 + n * 6.5
        if name == "memset":
            return 70.0 + n * 0.5
        return 70.0 + n * 1.1

    def flush(self):
        pend = self.pending
        self.pending = []
        if not pend:
            return
        import heapq
        units = []
        cur = None
        for it in pend:
            kind, eng, call, rd, wr, inc = it
            if kind == "op" and eng == "pe":
                if cur is None:
                    cur = [eng, [], 0.0, set(), set()]
                cur[1].append(it)
                cur[2] += self._cost(kind, eng, call)
                cur[3].update(rd)
                cur[4].update(wr)
                if inc:
                    units.append(cur)
                    cur = None
            elif kind == "either":
                assert cur is None, "non-PE op inside an open PE group"
                cd = self._cost("op", "dve", call[0])
                ca = self._cost("op", "act", call[1])
                units.append([None, [it], min(cd, ca), set(rd), set(wr), (cd, ca)])
            else:
                assert cur is None, "non-PE op inside an open PE group"
                units.append([eng, [it], self._cost(kind, eng, call), set(rd), set(wr)])
        assert cur is None, "PE group without final inc"
        n = len(units)
        lastw = {}
        readers = {}
        deps = [None] * n
        succ = [[] for _ in range(n)]
        for i, u in enumerate(units):
            d = set()
            for b in u[3]:
                if b in lastw:
                    d.add(lastw[b])
            for b in u[4]:
                if b in lastw:
                    d.add(lastw[b])
                for r in readers.get(b, ()):
                    d.add(r)
            d.discard(i)
            deps[i] = d
            for j in d:
                succ[j].append(i)
            for b in u[3]:
                readers.setdefault(b, []).append(i)
            for b in u[4]:
                lastw[b] = i
                readers[b] = []
        ndep = [len(d) for d in deps]
        ready_t = [0.0] * n
        fin = [0.0] * n
        bl = [0.0] * n
        for i in range(n - 1, -1, -1):
            m_ = 0.0
            for j in succ[i]:
                if bl[j] > m_:
                    m_ = bl[j]
            bl[i] = units[i][2] + m_
        mode = self.sched_mode
        free = {}
        fut = {}
        avail = {}
        load = {"dve": 0.0, "act": 0.0}
        for u in units:
            if u[0] in load:
                load[u[0]] += u[2]

        def bind(i):
            u = units[i]
            if u[0] is None:
                cd, ca = u[5]
                td = load["dve"] + cd
                ta = load["act"] + ca
                load["dve" if td <= ta else "act"] += (cd if td <= ta else ca)
                kind_, _, calls, rd_, wr_, inc_ = u[1][0]
                if td <= ta:
                    u[0], u[2], u[1] = "dve", cd, [("op", "dve", calls[0], rd_, wr_, inc_)]
                else:
                    u[0], u[2], u[1] = "act", ca, [("op", "act", calls[1], rd_, wr_, inc_)]
            return u[0]

        for i in range(n):
            if ndep[i] == 0:
                heapq.heappush(fut.setdefault(bind(i), []), (0.0, i))
        order = []
        done = 0
        while done < n:
            best = None
            for e, h in fut.items():
                fe = free.get(e, 0.0)
                av = avail.setdefault(e, [])
                while h and h[0][0] <= fe:
                    rt, i = heapq.heappop(h)
                    heapq.heappush(av, ((-bl[i], i) if (mode == "cp" or (mode == "cpx" and e != "pe")) else (rt, i)))
                if av:
                    cand = (fe, 0, e)
                elif h:
                    cand = (h[0][0], 1, e)
                else:
                    continue
                if best is None or cand < best:
                    best = cand
            st_, fromfut, e = best
            if fromfut:
                rt, i = heapq.heappop(fut[e])
            else:
                _, i = heapq.heappop(avail[e])
            u = units[i]
            if e in ("sp", "pool"):
                free[e] = st_ + 60.0
                fin[i] = st_ + u[2]
            else:
                fin[i] = st_ + u[2]
                free[e] = fin[i]
            order.append((st_, i))
            done += 1
            for j in succ[i]:
                ndep[j] -= 1
                if fin[i] > ready_t[j]:
                    ready_t[j] = fin[i]
                if ndep[j] == 0:
                    heapq.heappush(fut.setdefault(bind(j), []), (ready_t[j], j))
        if self.stats is not None:
            mk = max(fin) if fin else 0.0
            busy = {}
            for u in units:
                busy[u[0]] = busy.get(u[0], 0.0) + u[2]
            st = self.stats.setdefault(self.label, [0.0, {}, 0, 0, 0.0])
            st[0] += mk
            st[2] += n
            st[3] += 1
            st[4] += mk - max(busy.values())
            for e_, v_ in busy.items():
                st[1][e_] = st[1].get(e_, 0.0) + v_
        order.sort()
        for _, i in order:
            for kind, eng, call, rd, wr, inc in units[i][1]:
                if kind == "op":
                    name, args, kw = call
                    self._op_now(eng, lambda en: getattr(en, name)(*args, **kw), rd, wr, inc)
                else:
                    out, in_, kw = call
                    self._dma_now(eng, out, in_, rd, wr, **kw)

    def barrier(self, include_pool_dma=False):
        self.flush()
        keys = ["pe", "act", "dve", "pool"] + self.dq["sp"][0] + self.dq["act"][0]
        if include_pool_dma:
            keys += self.dq["pool"][0]
        for e in ("pe", "act", "dve", "pool", "sp"):
            for k in keys:
                if (k == e and e == "pe") or self.cnt[k] == 0:
                    continue
                self._wait(e, k, self.cnt[k])


def bcast(ap, shape, axis):
    return ap.unsqueeze(axis).broadcast_to(list(shape))


IN_SPECS = [
    ("xp", [512, 1024]), ("xs", [2048, 1024]), ("cvec", [2, 1024]),
    ("cdk", [4, 256, 512]), ("cdv", [4, 256, 512]), ("cgk", [4, 256, 128]), ("cgv", [4, 256, 128]),
    ("sssm", [4, 2, 8, 64, 64]), ("smc", [4, 2, 4, 128, 128]), ("smn", [4, 2, 4, 128]), ("smm", [4, 8]),
    ("w_ada", [4, 1024, 6144]), ("b_ada", [4, 6144]), ("norm1", [4, 1024]), ("norm2", [4, 1024]),
    ("w_in", [4, 1024, IN_COLS]), ("w_out", [4, 2048, 1024]),
    ("conv_ssd_w", [4, 5, 768]), ("conv_ssd_b", [4, 768]), ("ssd_a_log", [4, 16]), ("ssd_dt_bias", [4, 16]),
    ("ssd_d", [4, 8]), ("ssd_norm", [4, 512]),
    ("diff_lq1", [4, 64]), ("diff_lk1", [4, 64]), ("diff_lq2", [4, 64]), ("diff_lk2", [4, 64]),
    ("conv_mlstm_w", [4, 5, 1024]), ("conv_mlstm_b", [4, 1024]), ("mlstm_gate_b", [4, 16]), ("mlstm_norm", [4, 512]),
    ("gqa_q_norm", [4, 64]), ("gqa_k_norm", [4, 64]),
    ("w_ffn_in", [4, 1024, 2 * D_FF]), ("w_ffn_out", [4, D_FF, 1024]), ("norm_f", [1024]),
    ("consts", [128, 1152]), ("rope", [128, 2, 2048]),
]
OUT_SPECS = [
    ("yp", [512, 1024]), ("ys", [2048, 1024]),
    ("ndk", [2, 4, 256, 512]), ("ndv", [2, 4, 256, 512]), ("ngk", [2, 4, 256, 128]), ("ngv", [2, 4, 256, 128]),
    ("nssm", [2, 4, 2, 8, 64, 64]), ("nmc", [2, 4, 2, 4, 128, 128]), ("nmn", [2, 4, 2, 4, 128]), ("nmm", [2, 4, 8]),
]


def make_consts():
    c = np.zeros((128, 1152), np.float32)
    k = np.arange(128)
    c[:, 0:128] = np.eye(128)
    c[:, 128:256] = 1.0
    c[:, 256:384] = (k[:, None] <= k[None, :])
    c[:, 384:512] = (k[:, None] >= k[None, :])
    c[:, 512:640] = np.where(k[:, None] <= k[None, :], 0.0, NEG)
    c[:, 640:768] = np.where(k[:, None] >= k[None, :], 0.0, NEG)
    c[:, 768:896] = (k[:, None] // 64 == k[None, :] // 64)
    rm = np.zeros((128, 128), np.float32)
    for dp in range(128):
        half = (dp % 32) // 16
        if half == 0:
            rm[dp + 16, dp] = -1.0
        else:
            rm[dp - 16, dp] = 1.0
    c[:, 896:1024] = rm
    c[64, 1024:1088] = 1.0
    c[0, 1088:1152] = 1.0
    return c


def make_rope():
    t = np.arange(2048)
    r = (t // 64).astype(np.float32)
    cc = (t % 64).astype(np.float32)
    nf = 16
    freqs = (10000.0 ** (-np.arange(nf, dtype=np.float32) / nf)).astype(np.float32)
    ang = np.stack([r[:, None] * freqs, cc[:, None] * freqs], axis=1).astype(np.float32)
    out = np.zeros((128, 2, 2048), np.float32)
    for p in range(128):
        d = p % 64
        a = d // 32
        f = d % 16
        out[p, 0] = np.cos(ang[:, a, f])
        out[p, 1] = np.sin(ang[:, a, f])
    return out


def build_program():
    nc = bass.Bass("TRN2", target_bir_lowering=False)
    kb = KB(nc)
    D = {}
    for name, shape in IN_SPECS:
        if shape[0] == 4 and len(shape) > 1:
            shape = [NL] + list(shape[1:])
        D[name] = nc.dram_tensor(name, shape, F32, kind="ExternalInput").ap()
    for name, shape in OUT_SPECS:
        D[name] = nc.dram_tensor(name, shape, F32, kind="ExternalOutput").ap()
    mod_d = nc.dram_tensor("mod_scr", [4, 2, 6144], F32, kind="Internal").ap()

    top = ExitStack()

    class scope:
        def __enter__(self_):
            self_.st = ExitStack()
            return self_.st

        def __exit__(self_, *a):
            if a[0] is None:
                kb.barrier()
            self_.st.close()
            return False

    uid = [0]

    def alloc(stack, name, shape, dt, psum=False):
        uid[0] += 1
        name = "%s_%d" % (name, uid[0])
        cm = nc.psum_tensor(name, shape, dt) if psum else nc.sbuf_tensor(name, shape, dt)
        return Buf(stack.enter_context(cm))

    xT = [alloc(top, "xT%d" % i, [128, 8, 512], F32) for i in range(4)]
    hT = [alloc(top, "hT%d" % i, [128, 8, 512], BF16) for i in range(4)]
    NW = 2
    wbufs = [alloc(top, "wb%d" % i, [128, 4096], BF16) for i in range(NW)]
    wstate = [0]
    psb = [alloc(top, "ps%d" % i, [128, 512], F32, psum=True) for i in range(8)]
    pstate = [0]
    cf = alloc(top, "cf", [128, 640], F32)
    cb = alloc(top, "cb", [128, 1024], BF16)
    modc = alloc(top, "modc", [128, 6, 8], F32)
    nrm = alloc(top, "nrm", [128, 2, 8], F32)
    AB = alloc(top, "AB", [128, 4, 8], F32)
    nfc = alloc(top, "nfc", [128, 8], F32)
    lnsb = alloc(top, "lnsb", [128, 1], F32)
    lns_col = lnsb.t

    wo_buf = alloc(top, "wo_buf", [128, 4, 1024], BF16)
    accstate = [0]

    def ps():
        b = psb[pstate[0] % 4]
        pstate[0] += 1
        return b

    def ps_acc():
        b = psb[4 + accstate[0] % 4]
        accstate[0] += 1
        return b

    ffn_state = [0]

    def wload(pieces, kch, three=False):
        if three:
            lst = wbufs + [wo_buf]
            b = lst[ffn_state[0] % len(lst)]
            ffn_state[0] += 1
        else:
            b = wbufs[wstate[0] % NW]
            wstate[0] += 1
        ntot = sum(n for _, n in pieces)
        assert kch * ntot <= 4096, (kch, ntot)
        flat = b.t[:].rearrange("p k n -> p (k n)") if b is wo_buf else b.t
        view = flat[:, 0:kch * ntot].rearrange("p (k n) -> p k n", k=kch)
        o = 0
        for ap, n in pieces:
            kb.dma("pool", view[:, :, o:o + n], ap.rearrange("(k p) n -> p k n", p=128), writes=[b])
            o += n
        return b, view

    ident_f = cf.t[:, 0:128]
    ones_f = cf.t[:, 128:256]
    ident_b = cb.t[:, 0:128]
    selm_f = cf.t[:, 512:640]
    ones_b = cb.t[:, 128:256]

    kb.dma("sp", cf[:, 0:512], D["consts"][:, 0:512], writes=[cf])
    kb.dma("sp", cf[:, 512:640], D["consts"][:, 1024:1152], writes=[cf])
    kb.dma("pool", cb[:], D["consts"][:, 0:1024], writes=[cb])
    kb.op("dve", lambda e: e.memset(lnsb[:], math.log(128 ** -0.5)), writes=[lnsb])
    kb.dma("sp", nfc[:], D["norm_f"].rearrange("(c p) -> p c", p=128), writes=[nfc], allow_slow_non_contiguous=True)

    modall = alloc(top, "modall", [128, NL, 48, 2], F32)
    ball = alloc(top, "ball", [128, NL, 48], F32)
    nrmall = alloc(top, "nrmall", [128, NL, 2, 8], F32)
    for l in range(NL):
        kb.dma("sp", ball[:, l, :], D["b_ada"][l].rearrange("(j p) -> p j", p=128), writes=[ball], allow_slow_non_contiguous=True)
        kb.dma("sp", nrmall[:, l, 0, :], D["norm1"][l].rearrange("(c p) -> p c", p=128), writes=[nrmall], allow_slow_non_contiguous=True)
        kb.dma("sp", nrmall[:, l, 1, :], D["norm2"][l].rearrange("(c p) -> p c", p=128), writes=[nrmall], allow_slow_non_contiguous=True)
    cT = alloc(top, "cT", [128, 2, 8], F32)
    cTb = alloc(top, "cTb", [128, 2, 8], BF16)
    sig = alloc(top, "csig", [128, 2, 8], F32)
    for g in range(2):
        kb.dma("sp", cT[:, g, :], D["cvec"][g].rearrange("(c p) -> p c", p=128), writes=[cT], allow_slow_non_contiguous=True)
    kb.op("act", lambda e: e.activation(out=sig[:], in_=cT[:], func=AF.Sigmoid), reads=[cT], writes=[sig])
    kb.op("dve", lambda e: e.tensor_tensor(out=cTb[:], in0=cT[:], in1=sig[:], op=ALU.mult), reads=[cT, sig], writes=[cTb])

    def compute_mod(l):
        for blk in range(12):
            c0 = blk * 512
            wb, wv = wload([(D["w_ada"][l][:, c0:c0 + 512], 512)], 8)
            p = ps()
            for cc in range(4):
                for k in range(8):
                    kb.op("pe", lambda e: e.matmul(p[:, 2 * cc:2 * cc + 2], lhsT=wv[:, k, cc * 128:(cc + 1) * 128], rhs=cTb[:, :, k], start=(k == 0), stop=(k == 7)),
                          reads=[cTb, wb], writes=[p], inc=(k == 7 and cc == 3))
            kb.op("dve", lambda e: e.tensor_tensor(out=modall[:, l, blk * 4:(blk + 1) * 4, :], in0=p[:, 0:8].rearrange("p (j g) -> p j g", g=2),
                                                   in1=bcast(ball[:, l, blk * 4:(blk + 1) * 4], [128, 4, 2], 2), op=ALU.add), reads=[p, ball], writes=[modall])

    compute_mod(0)
    kb.barrier()

    def load_x(src, nblk):
        with scope() as st:
            xin = [alloc(st, "xin%d" % i, [128, 1024], F32) for i in range(2)]
            for tb in range(nblk):
                tiles = []
                for tl in range(4):
                    pass
                for tl in range(4):
                    xi = xin[(tb * 4 + tl) % 2]
                    t0 = (tb * 4 + tl) * 128
                    kb.dma("sp", xi[:], src[t0:t0 + 128, :], writes=[xi])
                    for half in range(2):
                        p = ps()
                        for cc in range(4):
                            c = half * 4 + cc
                            kb.op("pe", lambda e: e.matmul(p[:, cc * 128:(cc + 1) * 128], lhsT=xi[:, c * 128:(c + 1) * 128], rhs=ident_f,
                                                           start=True, stop=True), reads=[xi, cf], writes=[p], inc=(cc == 3))
                        kb.op("act", lambda e: e.activation(
                            out=xT[tb][:, half * 4:half * 4 + 4, tl * 128:(tl + 1) * 128],
                            in_=p[:, :].rearrange("p (c t) -> p c t", c=4), func=AF.Copy), reads=[p], writes=[xT[tb]])

    def norm_block(st_tmp, tb, Acol, Bcol, dst, dst_dt_is_bf16=True):
        sq, rstd, tmp2 = st_tmp
        if isinstance(sq, list):
            sq, rstd = sq[tb % 2], rstd[tb % 2]
        kb.op("act", lambda e: e.activation(out=sq[:], in_=xT[tb][:], func=AF.Square), reads=[xT[tb]], writes=[sq])
        p = ps()
        for c in range(8):
            kb.op("pe", lambda e: e.matmul(p[:, :], lhsT=ones_b, rhs=sq[:, c, :], start=(c == 0), stop=(c == 7)),
                  reads=[sq, cb], writes=[p], inc=(c == 7))
        kb.op("dve", lambda e: e.tensor_scalar(out=rstd[:], in0=p[:, :], scalar1=1.0 / D_MODEL, scalar2=EPS, op0=ALU.mult, op1=ALU.add),
              reads=[p], writes=[rstd])
        kb.op("act", lambda e: e.activation(out=rstd[:], in_=rstd[:], func=AF.Ln), reads=[rstd], writes=[rstd])
        kb.op("act", lambda e: e.activation(out=rstd[:], in_=rstd[:], func=AF.Exp, scale=-0.5), reads=[rstd], writes=[rstd])
        for c in range(8):
            t2 = tmp2[c % len(tmp2)]
            kb.op("dve", lambda e: e.tensor_tensor(out=t2[:], in0=xT[tb][:, c, :], in1=rstd[:], op=ALU.mult),
                  reads=[xT[tb], rstd], writes=[t2])
            if Bcol is not None:
                kb.op("act", lambda e: e.activation(out=dst[:, c, :], in_=t2[:], func=AF.Identity, bias=Bcol[:, c:c + 1], scale=Acol[:, c:c + 1]),
                      reads=[t2, AB], writes=[dst])
            else:
                kb.op("act", lambda e: e.activation(out=dst[:, c, :], in_=t2[:], func=AF.Identity, scale=Acol[:, c:c + 1]),
                      reads=[t2, nfc], writes=[dst])

    def load_mod(l, g):
        kb.op("dve", lambda e: e.tensor_copy(out=modc[:], in_=modall[:, l, :, g].rearrange("p (v c) -> p v c", v=6)), reads=[modall], writes=[modc])
        kb.op("dve", lambda e: e.tensor_copy(out=nrm[:], in_=nrmall[:, l, :, :]), reads=[nrmall], writes=[nrm])
        for j, (vs, vh) in enumerate(((1, 0), (4, 3))):
            kb.op("dve", lambda e: e.scalar_tensor_tensor(out=AB[:, 2 * j, :], in0=modc[:, vs, :], scalar=1.0, in1=nrm[:, j, :],
                                                          op0=ALU.add, op1=ALU.mult), reads=[modc, nrm], writes=[AB])
            kb.op("dve", lambda e: e.tensor_copy(out=AB[:, 2 * j + 1, :], in_=modc[:, vh, :]), reads=[modc], writes=[AB])

    def ffn(l, blocks):
        kb.label = 'ffn'
        nb = len(blocks)
        with scope() as st:
            actT = alloc(st, "actT", [128, 22, nb * 512], BF16)
            sg = [alloc(st, "sg%d" % i, [128, 512], F32) for i in range(2)]
            it = 0
            for jj in range(11):
                c0 = jj * 256
                wb, wv = wload([(D["w_ffn_in"][l][:, c0:c0 + 256], 256), (D["w_ffn_in"][l][:, D_FF + c0:D_FF + c0 + 256], 256)], 8, three=True)
                for j2 in range(2):
                    j = jj * 2 + j2
                    for bi, tb in enumerate(blocks):
                        pg = ps()
                        pu = ps()
                        for k in range(8):
                            kb.op("pe", lambda e: e.matmul(pg[:, :], lhsT=wv[:, k, j2 * 128:(j2 + 1) * 128], rhs=hT[tb][:, k, :],
                                                           start=(k == 0), stop=(k == 7)), reads=[wb, hT[tb]], writes=[pg], inc=(k == 7))
                        for k in range(8):
                            kb.op("pe", lambda e: e.matmul(pu[:, :], lhsT=wv[:, k, 256 + j2 * 128:256 + (j2 + 1) * 128], rhs=hT[tb][:, k, :],
                                                           start=(k == 0), stop=(k == 7)), reads=[wb, hT[tb]], writes=[pu], inc=(k == 7))
                        s = sg[it % 2]
                        it += 1
                        kb.op("act", lambda e: e.activation(out=s[:], in_=pg[:, :], func=AF.Silu), reads=[pg], writes=[s])
                        kb.op("dve", lambda e: e.tensor_tensor(out=actT[:, j, bi * 512:(bi + 1) * 512], in0=s[:], in1=pu[:, :], op=ALU.mult),
                              reads=[s, pu], writes=[actT])
            for c in range(8):
                wb, wv = wload([(D["w_ffn_out"][l][:, c * 128:(c + 1) * 128], 128)], 22, three=True)
                for bi, tb in enumerate(blocks):
                    p = ps()
                    for k in range(22):
                        kb.op("pe", lambda e: e.matmul(p[:, :], lhsT=wv[:, k, :], rhs=actT[:, k, bi * 512:(bi + 1) * 512],
                                                       start=(k == 0), stop=(k == 21)), reads=[wb, actT], writes=[p], inc=(k == 21))
                    kb.op("dve", lambda e: e.scalar_tensor_tensor(out=xT[tb][:, c, :], in0=p[:, :], scalar=modc[:, 5, c:c + 1], in1=xT[tb][:, c, :],
                                                                  op0=ALU.mult, op1=ALU.add), reads=[p, modc, xT[tb]], writes=[xT[tb]])
        kb.barrier()

    def final_out(nblk, dst):
        kb.label = 'final'
        with scope() as st:
            sq = alloc(st, "sq", [128, 8, 512], BF16)
            rstd = alloc(st, "rstd", [128, 512], F32)
            tmp2 = [alloc(st, "tmpn%d" % i, [128, 512], F32) for i in range(2)]
            xn = alloc(st, "xn", [128, 8, 512], F32)
            ot = [alloc(st, "ot%d" % i, [128, 1024], F32) for i in range(2)]
            for tb in range(nblk):
                norm_block((sq, rstd, tmp2), tb, nfc, None, xn)
                for tl in range(4):
                    o = ot[tl % 2]
                    for half in range(2):
                        p = ps()
                        for cc in range(4):
                            c = half * 4 + cc
                            kb.op("pe", lambda e: e.matmul(p[:, cc * 128:(cc + 1) * 128], lhsT=xn[:, c, tl * 128:(tl + 1) * 128], rhs=ident_f,
                                                           start=True, stop=True), reads=[xn, cf], writes=[p], inc=(cc == 3))
                        kb.copy(o[:, half * 512:(half + 1) * 512], p[:, :], reads=[p], writes=[o])
                    t0 = (tb * 4 + tl) * 128
                    kb.dma("sp", dst[t0:t0 + 128, :], o[:], reads=[o])
        kb.barrier()


    bd64_b = cb.t[:, 768:896]
    rm_b = cb.t[:, 896:1024]

    def proj_tm(wb, wv, s0, n, tb, tl, p):
        for k in range(8):
            kb.op("pe", lambda e: e.matmul(p[:, 0:n], lhsT=hT[tb][:, k, tl * 128:(tl + 1) * 128], rhs=wv[:, k, s0:s0 + n],
                                           start=(k == 0), stop=(k == 7)), reads=[wb, hT[tb]], writes=[p], inc=(k == 7))

    def load_wo(l, row0):
        kb.dma("pool", wo_buf[:], D["w_out"][l][row0:row0 + 512, :].rearrange("(k p) n -> p k n", p=128), writes=[wo_buf])

    def mixer_out(st, ytm, tb, yT):
        for c in range(4):
            p = ps()
            for tl in range(4):
                kb.op("pe", lambda e: e.matmul(p[:, tl * 128:(tl + 1) * 128], lhsT=ytm[:, tl, c * 128:(c + 1) * 128], rhs=ident_b,
                                               start=True, stop=True), reads=[ytm, cb], writes=[p], inc=(tl == 3))
            kb.copy(yT[:, c, :], p[:, :], reads=[p], writes=[yT])
        for c in range(8):
            p = ps()
            for k in range(4):
                kb.op("pe", lambda e: e.matmul(p[:, :], lhsT=wo_buf[:, k, c * 128:(c + 1) * 128], rhs=yT[:, k, :],
                                               start=(k == 0), stop=(k == 3)), reads=[wo_buf, yT], writes=[p], inc=(k == 3))
            kb.op("dve", lambda e: e.scalar_tensor_tensor(out=xT[tb][:, c, :], in0=p[:, :], scalar=modc[:, 2, c:c + 1], in1=xT[tb][:, c, :],
                                                          op0=ALU.mult, op1=ALU.add), reads=[p, modc, xT[tb]], writes=[xT[tb]])

    def attention(l, g, kind, nblk):
        kb.label = 'attn_%s_g%d' % (kind, g)
        sample = (g == 1)
        L = 2048 if sample else 256
        nseq = 1 if sample else 2
        nctx = 2 if sample else 0
        lt = L // 128
        nkt = lt + nctx
        ntok = nblk * 512
        if kind == "d":
            qc0, kc0, vc0, nkc, nvh, ve, orow0, nheads = 4896, 5408, 5536, 1, 2, 64, 1536, 8
            ck, cv, ok, ov = D["cgk"], D["cgv"], D["ngk"], D["ngv"]
        else:
            qc0, kc0, vc0, nkc, nvh, ve, orow0, nheads = 1296, 1808, 2320, 4, 4, 128, 512, 4
            ck, cv, ok, ov = D["cdk"], D["cdv"], D["ndk"], D["ndv"]
        scale = 64 ** -0.5
        kw = nkc * 128
        vw = nvh * ve
        lam_init = 0.8 - 0.6 * math.exp(-0.3 * l)
        with scope() as st:
            qT = alloc(st, "qT", [128, 4, ntok], BF16)
            kT = alloc(st, "kT", [128, nkc, nseq * nkt * 128], BF16)
            vsw = ve + 1 if kind == "d" else ve
            vaug = alloc(st, "vaug", [128, nseq * nkt, nvh, vsw], BF16)
            vodd = alloc(st, "vodd", [128, nseq * nkt, nvh, 128], BF16) if kind == "d" else None
            yT = alloc(st, "yT", [128, 4, 512], BF16)
            pTs = [alloc(st, "pT%d" % i, [128, 512], BF16) for i in range(4)]
            sqb = alloc(st, "sqb", [128, 512], BF16)
            rs = alloc(st, "rs", [128, 512], F32)
            qn = alloc(st, "qn", [128, 512], BF16)
            t1 = alloc(st, "t1", [128, 512], F32)
            fin_bufs = [(rs, t1, None, sqb)]
            gcol = alloc(st, "gcol", [128, 2], F32)
            osb = alloc(st, "osb", [128, 512], F32)
            t2 = osb
            fin_bufs[0] = (rs, t1, osb, sqb)
            sm = alloc(st, "sm", [128, 16], F32)
            lamt = alloc(st, "lamt", [128, 4, 64], F32)
            kng = alloc(st, "kng", [128, 64], F32)
            ropeT = alloc(st, "ropeT", [128, 2, 2048], BF16) if sample else None
            if sample and kind == "b":
                pass
            else:
                try:
                    fin_bufs.append((alloc(st, "rs2", [128, 512], F32), alloc(st, "t12", [128, 512], F32),
                                     alloc(st, "osb2", [128, 512], F32) if kind == "b" else None, alloc(st, "sqb2", [128, 512], BF16) if kind == "b" else None))
                except AssertionError:
                    pass
            load_wo(l, orow0)
            if kind == "d":
                kb.op("dve", lambda e: e.memset(vaug[:, :, :, ve:ve + 1], 1.0), writes=[vaug])
                kb.op("dve", lambda e: e.memset(vodd[:, :, :, 0:1], 1.0), writes=[vodd])
                kb.op("dve", lambda e: e.memset(vodd[:, :, :, 1:64], 0.0), writes=[vodd])
            if sample:
                kb.dma("pool", ropeT[:], D["rope"], writes=[ropeT])
            if kind == "d":
                for j, nm in enumerate(("gqa_q_norm", "gqa_k_norm")):
                    for hh in range(2):
                        kb.dma("sp", gcol[hh * 64:(hh + 1) * 64, j:j + 1], D[nm][l].rearrange("(d o) -> d o", o=1), writes=[gcol])
                kb.dma("sp", kng[:], D["gqa_k_norm"][l].partition_broadcast(128), writes=[kng])
            else:
                for j, nm in enumerate(("diff_lq1", "diff_lk1", "diff_lq2", "diff_lk2")):
                    kb.dma("sp", lamt[:, j, :], D[nm][l].partition_broadcast(128), writes=[lamt])
                kb.op("dve", lambda e: e.tensor_tensor(out=lamt[:, 0, :], in0=lamt[:, 0, :], in1=lamt[:, 1, :], op=ALU.mult), reads=[lamt], writes=[lamt])
                kb.op("dve", lambda e: e.tensor_tensor(out=lamt[:, 2, :], in0=lamt[:, 2, :], in1=lamt[:, 3, :], op=ALU.mult), reads=[lamt], writes=[lamt])
                kb.op("dve", lambda e: e.tensor_reduce(out=sm[:, 2:3], in_=lamt[:, 0, :], axis=AX.X, op=ALU.add), reads=[lamt], writes=[sm])
                kb.op("dve", lambda e: e.tensor_reduce(out=sm[:, 3:4], in_=lamt[:, 2, :], axis=AX.X, op=ALU.add), reads=[lamt], writes=[sm])
                kb.op("act", lambda e: e.activation(out=sm[:, 2:4], in_=sm[:, 2:4], func=AF.Exp), reads=[sm], writes=[sm])
                kb.op("dve", lambda e: e.tensor_tensor(out=sm[:, 0:1], in0=sm[:, 2:3], in1=sm[:, 3:4], op=ALU.subtract), reads=[sm], writes=[sm])
                kb.op("dve", lambda e: e.tensor_scalar(out=sm[:, 1:2], in0=sm[:, 0:1], scalar1=lam_init, scalar2=-1.0, op0=ALU.add, op1=ALU.mult), reads=[sm], writes=[sm])

            def qk_post(p, dst, tb, normj):
                src = p
                if kind == "d":
                    kb.op("act", lambda e: e.activation(out=sqb[:], in_=p[:, :], func=AF.Square), reads=[p], writes=[sqb])
                    pn = ps()
                    kb.op("pe", lambda e: e.matmul(pn[:, :], lhsT=bd64_b, rhs=sqb[:], start=True, stop=True), reads=[cb, sqb], writes=[pn])
                    kb.op("dve", lambda e: e.tensor_scalar(out=rs[:], in0=pn[:, :], scalar1=1.0 / 64, scalar2=EPS, op0=ALU.mult, op1=ALU.add), reads=[pn], writes=[rs])
                    kb.op("act", lambda e: e.activation(out=rs[:], in_=rs[:], func=AF.Ln), reads=[rs], writes=[rs])
                    kb.op("act", lambda e: e.activation(out=rs[:], in_=rs[:], func=AF.Exp, scale=-0.5), reads=[rs], writes=[rs])
                    tgt = qn if sample else None
                    o_ap = qn[:] if sample else dst
                    kb.op("dve", lambda e: e.scalar_tensor_tensor(out=o_ap, in0=p[:, :], scalar=gcol[:, normj:normj + 1], in1=rs[:], op0=ALU.mult, op1=ALU.mult),
                          reads=[p, gcol, rs], writes=[qn if sample else dst_buf[0]])
                else:
                    o_ap = qn[:] if sample else dst
                    kb.copy(o_ap, p[:, :], reads=[p], writes=[qn if sample else dst_buf[0]])
                if sample:
                    pr = ps()
                    kb.op("pe", lambda e: e.matmul(pr[:, :], lhsT=rm_b, rhs=qn[:], start=True, stop=True), reads=[cb, qn], writes=[pr])
                    kb.op("dve", lambda e: e.tensor_tensor(out=t1[:], in0=qn[:], in1=ropeT[:, 0, tb * 512:(tb + 1) * 512], op=ALU.mult), reads=[qn, ropeT], writes=[t1])
                    kb.op("dve", lambda e: e.tensor_tensor(out=t2[:], in0=pr[:, :], in1=ropeT[:, 1, tb * 512:(tb + 1) * 512], op=ALU.mult), reads=[pr, ropeT], writes=[t2])
                    kb.op("dve", lambda e: e.tensor_tensor(out=dst, in0=t1[:], in1=t2[:], op=ALU.add), reads=[t1, t2], writes=[dst_buf[0]])

            dst_buf = [None]
            blocks = list(range(nblk))
            if kind == "d":
                pcs = []
                for j in range(4):
                    for hh in (j, 4 + j):
                        pcs.append((D["w_in"][l][:, qc0 + hh * 64:qc0 + (hh + 1) * 64], 64))
                wb, wv = wload(pcs, 8)
            else:
                wb, wv = wload([(D["w_in"][l][:, qc0:qc0 + 512], 512)], 8)
            dst_buf[0] = qT
            for j in range(4):
                for tb in blocks:
                    p = ps()
                    for k in range(8):
                        lh = wv[:, k, j * 128:(j + 1) * 128]
                        kb.op("pe", lambda e: e.matmul(p[:, :], lhsT=lh, rhs=hT[tb][:, k, :], start=(k == 0), stop=(k == 7)),
                              reads=[wb, hT[tb]], writes=[p], inc=(k == 7))
                    qk_post(p, qT[:, j, tb * 512:(tb + 1) * 512], tb, 0)
            wb, wv = wload([(D["w_in"][l][:, kc0:kc0 + kw], kw)], 8)
            dst_buf[0] = kT
            for j in range(nkc):
                for tb in blocks:
                    p = ps()
                    for k in range(8):
                        kb.op("pe", lambda e: e.matmul(p[:, :], lhsT=wv[:, k, j * 128:(j + 1) * 128], rhs=hT[tb][:, k, :], start=(k == 0), stop=(k == 7)),
                              reads=[wb, hT[tb]], writes=[p], inc=(k == 7))
                    qk_post(p, kT[:, j, nctx * 128 + tb * 512:nctx * 128 + (tb + 1) * 512], tb, 1)
            if not sample:
                for tt in range(ntok // 128):
                    sq_, r0 = divmod(tt, lt)
                    p = ps()
                    proj_tm(wb, wv, 0, kw, tt // 4, tt % 4, p)
                    kb.copy(osb[:, 0:kw], p[:, 0:kw], reads=[p], writes=[osb])
                    if kind == "d":
                        kb.op("dve", lambda e: e.tensor_tensor(out=t1[:, 0:128], in0=osb[:, 0:128], in1=osb[:, 0:128], op=ALU.mult), reads=[osb], writes=[t1])
                        kb.op("dve", lambda e: e.tensor_reduce(out=sm[:, 8:10], in_=t1[:, 0:128].rearrange("p (h d) -> p h d", d=64), axis=AX.X, op=ALU.add), reads=[t1], writes=[sm])
                        kb.op("dve", lambda e: e.tensor_scalar(out=sm[:, 8:10], in0=sm[:, 8:10], scalar1=1.0 / 64, scalar2=EPS, op0=ALU.mult, op1=ALU.add), reads=[sm], writes=[sm])
                        kb.op("act", lambda e: e.activation(out=sm[:, 8:10], in_=sm[:, 8:10], func=AF.Ln), reads=[sm], writes=[sm])
                        kb.op("act", lambda e: e.activation(out=sm[:, 8:10], in_=sm[:, 8:10], func=AF.Exp, scale=-0.5), reads=[sm], writes=[sm])
                        kb.op("dve", lambda e: e.tensor_tensor(out=t1[:, 0:128].rearrange("p (h d) -> p h d", d=64), in0=osb[:, 0:128].rearrange("p (h d) -> p h d", d=64),
                                                               in1=bcast(sm[:, 8:10], [128, 2, 64], 2), op=ALU.mult), reads=[osb, sm], writes=[t1])
                        kb.op("dve", lambda e: e.tensor_tensor(out=t2[:, 0:128].rearrange("p (h d) -> p h d", d=64), in0=t1[:, 0:128].rearrange("p (h d) -> p h d", d=64),
                                                               in1=bcast(kng[:], [128, 2, 64], 1), op=ALU.mult), reads=[t1, kng], writes=[t2])
                        kb.dma("sp", ok[sq_, l, r0 * 128:(r0 + 1) * 128, :], t2[:, 0:128], reads=[t2])
                    else:
                        kb.dma("sp", ok[sq_, l, r0 * 128:(r0 + 1) * 128, :], osb[:, 0:kw], reads=[osb])
            wb, wv = wload([(D["w_in"][l][:, vc0:vc0 + vw], vw)], 8)
            for tt in range(ntok // 128):
                sq_, r0 = divmod(tt, lt)
                ktile = sq_ * nkt + nctx + r0
                p = ps()
                proj_tm(wb, wv, 0, vw, tt // 4, tt % 4, p)
                kb.copy(vaug[:, ktile, :, 0:ve], p[:, 0:vw].rearrange("p (h e) -> p h e", e=ve), reads=[p], writes=[vaug])
                if kind == "d":
                    kb.copy(vodd[:, ktile, :, 64:128], p[:, 0:vw].rearrange("p (h e) -> p h e", e=ve), reads=[p], writes=[vodd])
                if not sample:
                    kb.op("dve", lambda e: e.tensor_copy(out=osb[:, 0:vw], in_=p[:, 0:vw]), reads=[p], writes=[osb])
                    kb.dma("sp", ov[sq_, l, r0 * 128:(r0 + 1) * 128, :], osb[:, 0:vw], reads=[osb])
            if sample:
                with scope() as st2:
                    ctxk = alloc(st2, "ctxk", [128, kw], F32)
                    for t in range(2):
                        kb.dma("sp", ctxk[:], ck[l][t * 128:(t + 1) * 128, :], writes=[ctxk])
                        for j in range(nkc):
                            p = ps()
                            kb.op("pe", lambda e: e.matmul(p[:, 0:128], lhsT=ctxk[:, j * 128:(j + 1) * 128], rhs=ident_f, start=True, stop=True),
                                  reads=[ctxk, cf], writes=[p])
                            kb.copy(kT[:, j, t * 128:(t + 1) * 128], p[:, 0:128], reads=[p], writes=[kT])
                    for t in range(2):
                        kb.dma("pool", vaug[:, t, :, 0:ve], cv[l][t * 128:(t + 1) * 128, :].rearrange("p (h e) -> p h e", e=ve), writes=[vaug])
                        if kind == "d":
                            kb.dma("pool", vodd[:, t, :, 64:128], cv[l][t * 128:(t + 1) * 128, :].rearrange("p (h e) -> p h e", e=ve), writes=[vodd])
                    kb.barrier(include_pool_dma=True)
            qblk = min(L, 512)
            pti = 0
            for s_ in range(nseq):
                for qb in range(L // qblk):
                    q0 = s_ * L + qb * qblk
                    qs = slice(q0, q0 + qblk)
                    yc = slice(q0 % 512, q0 % 512 + qblk)
                    its = []
                    if kind == "d":
                        for hp in range(4):
                            for kt in range(nkt):
                                its.append((hp, kt, 0))
                                its.append((hp + 4, kt, 0))
                    else:
                        for h in range(nheads):
                            for kt in range(nkt):
                                its.append((h, kt, 0))
                                its.append((h, kt, 1))
                    hstate = {}
                    pts = {}
                    DEPTH = 2

                    def stageA2(j):
                        pss = []
                        for i in (2 * j, 2 * j + 1):
                            h, kt, r = its[i]
                            kc = (s_ * nkt + kt) * 128
                            if kind == "d":
                                rows, qch = (h // 4) * 64, h % 4
                                lhs = kT[rows:rows + 64, 0, kc:kc + 128]
                                rh = qT[rows:rows + 64, qch, qs]
                            else:
                                lhs = kT[r * 64:(r + 1) * 64, h, kc:kc + 128]
                                rh = qT[r * 64:(r + 1) * 64, h, qs]
                            pS = ps()
                            kb.op("pe", lambda e: e.matmul(pS[:, 0:qblk], lhsT=lhs, rhs=rh, start=True, stop=True), reads=[kT, qT], writes=[pS], inc=(i % 2 == 1))
                            pss.append(pS)
                        for i, pS in zip((2 * j, 2 * j + 1), pss):
                            pT = pTs[i % 4]
                            kb.op("act", lambda e: e.activation(out=pT[:, 0:qblk], in_=pS[:, 0:qblk], func=AF.Exp, scale=scale), reads=[pS], writes=[pT])
                            pts[i] = pT

                    def stageC(i):
                        h, kt, r = its[i]
                        pT = pts.pop(i)
                        last = (kt == nkt - 1)
                        rs, t1, osb, sqb = fin_bufs[h % len(fin_bufs)]
                        if kind == "d":
                            vh, odd = h // 4, h % 2
                            if kt == 0:
                                hstate[h] = ps_acc()
                            accO = hstate[h]
                            mo = 128 if odd else ve + 1
                            lh = vodd[:, s_ * nkt + kt, vh, :] if odd else vaug[:, s_ * nkt + kt, vh, :]
                            kb.op("pe", lambda e: e.matmul(accO[0:mo, 0:qblk], lhsT=lh, rhs=pT[:, 0:qblk], start=(kt == 0), stop=last),
                                  reads=[pT, vodd if odd else vaug], writes=[accO], inc=True)
                            if last:
                                drow = 0 if odd else 64
                                orow = 64 if odd else 0
                                kb.op("act", lambda e: e.activation(out=rs[drow:drow + 1, 0:qblk], in_=accO[drow:drow + 1, 0:qblk], func=AF.Ln), reads=[accO], writes=[rs])
                                kb.op("act", lambda e: e.activation(out=rs[drow:drow + 1, 0:qblk], in_=rs[drow:drow + 1, 0:qblk], func=AF.Exp, scale=-1.0), reads=[rs], writes=[rs])
                                pB = ps()
                                kb.op("pe", lambda e: e.matmul(pB[:, 0:qblk], lhsT=selm_f[drow:drow + 1, :], rhs=rs[drow:drow + 1, 0:qblk], start=True, stop=True),
                                      reads=[cf, rs], writes=[pB])
                                kb.copy(t1[orow:orow + 64, 0:qblk], pB[orow:orow + 64, 0:qblk], reads=[pB], writes=[t1])
                                kb.op("dve", lambda e: e.tensor_tensor(out=yT[orow:orow + 64, h // 2, yc], in0=accO[orow:orow + 64, 0:qblk], in1=t1[orow:orow + 64, 0:qblk], op=ALU.mult),
                                      reads=[accO, t1], writes=[yT])
                        else:
                            if kt == 0 and r == 0:
                                hstate[h] = ([ps_acc(), ps_acc()], [ps_acc(), ps_acc()])
                            accO, accD = hstate[h]
                            kb.op("pe", lambda e: e.matmul(accO[r][:, 0:qblk], lhsT=vaug[:, s_ * nkt + kt, h, :], rhs=pT[:, 0:qblk], start=(kt == 0), stop=last),
                                  reads=[pT, vaug], writes=[accO[r]], inc=False)
                            kb.op("pe", lambda e: e.matmul(accD[r][:, 0:qblk], lhsT=ones_b, rhs=pT[:, 0:qblk], start=(kt == 0), stop=last),
                                  reads=[pT, cb], writes=[accD[r]], inc=True)
                            if last and r == 1:
                                A, Bt, O = rs, t1, osb
                                kb.op("act", lambda e: e.activation(out=A[:, 0:qblk], in_=accD[0][:, 0:qblk], func=AF.Ln), reads=[accD[0]], writes=[A])
                                kb.op("act", lambda e: e.activation(out=A[:, 0:qblk], in_=A[:, 0:qblk], func=AF.Exp, scale=-1.0), reads=[A], writes=[A])
                                kb.op("act", lambda e: e.activation(out=Bt[:, 0:qblk], in_=accD[1][:, 0:qblk], func=AF.Ln), reads=[accD[1]], writes=[Bt])
                                kb.op("act", lambda e: e.activation(out=Bt[:, 0:qblk], in_=Bt[:, 0:qblk], func=AF.Exp, scale=-1.0), reads=[Bt], writes=[Bt])
                                kb.op("dve", lambda e: e.tensor_tensor(out=O[:, 0:qblk], in0=accO[0][:, 0:qblk], in1=A[:, 0:qblk], op=ALU.mult), reads=[accO[0], A], writes=[O])
                                kb.op("dve", lambda e: e.tensor_tensor(out=Bt[:, 0:qblk], in0=accO[1][:, 0:qblk], in1=Bt[:, 0:qblk], op=ALU.mult), reads=[accO[1], Bt], writes=[Bt])
                                kb.op("dve", lambda e: e.scalar_tensor_tensor(out=O[:, 0:qblk], in0=Bt[:, 0:qblk], scalar=sm[:, 1:2], in1=O[:, 0:qblk], op0=ALU.mult, op1=ALU.add),
                                      reads=[Bt, sm, O], writes=[O])
                                kb.op("act", lambda e: e.activation(out=sqb[:, 0:qblk], in_=O[:, 0:qblk], func=AF.Square), reads=[O], writes=[sqb])
                                pn = ps()
                                kb.op("pe", lambda e: e.matmul(pn[:, 0:qblk], lhsT=ones_b, rhs=sqb[:, 0:qblk], start=True, stop=True), reads=[cb, sqb], writes=[pn])
                                kb.op("dve", lambda e: e.tensor_scalar(out=A[:, 0:qblk], in0=pn[:, 0:qblk], scalar1=1.0 / 128, scalar2=EPS, op0=ALU.mult, op1=ALU.add), reads=[pn], writes=[A])
                                kb.op("act", lambda e: e.activation(out=A[:, 0:qblk], in_=A[:, 0:qblk], func=AF.Ln), reads=[A], writes=[A])
                                kb.op("act", lambda e: e.activation(out=A[:, 0:qblk], in_=A[:, 0:qblk], func=AF.Exp, scale=-0.5), reads=[A], writes=[A])
                                kb.op("dve", lambda e: e.scalar_tensor_tensor(out=yT[:, h, yc], in0=O[:, 0:qblk], scalar=1.0 - lam_init, in1=A[:, 0:qblk], op0=ALU.mult, op1=ALU.mult),
                                      reads=[O, A], writes=[yT])

                    n_pairs = len(its) // 2
                    for j in range(n_pairs + 1):
                        if j < n_pairs:
                            stageA2(j)
                        if j >= 1:
                            stageC(2 * (j - 1))
                            stageC(2 * (j - 1) + 1)
                    if (q0 + qblk) % 512 == 0:
                        tb = (q0 + qblk) // 512 - 1
                        for c in range(8):
                            p = ps()
                            for k in range(4):
                                kb.op("pe", lambda e: e.matmul(p[:, :], lhsT=wo_buf[:, k, c * 128:(c + 1) * 128], rhs=yT[:, k, :],
                                                               start=(k == 0), stop=(k == 3)), reads=[wo_buf, yT], writes=[p], inc=(k == 3))
                            kb.op("dve", lambda e: e.scalar_tensor_tensor(out=xT[tb][:, c, :], in0=p[:, :], scalar=modc[:, 2, c:c + 1], in1=xT[tb][:, c, :],
                                                                          op0=ALU.mult, op1=ALU.add), reads=[p, modc, xT[tb]], writes=[xT[tb]])
        kb.barrier()

    triF_f = cf.t[:, 256:384]
    triB_f = cf.t[:, 384:512]
    maskF_b = cb.t[:, 512:640]
    maskB_b = cb.t[:, 640:768]

    def proj_fm(l, c0, ncol, blocks, fn):
        for t0 in range(0, ncol, 512):
            n = min(512, ncol - t0)
            wb, wv = wload([(D["w_in"][l][:, c0 + t0:c0 + t0 + n], n)], 8)
            for s0 in range(0, n, 128):
                w = min(128, n - s0)
                for tb in blocks:
                    p = ps()
                    for k in range(8):
                        kb.op("pe", lambda e: e.matmul(p[0:w, :], lhsT=wv[:, k, s0:s0 + w], rhs=hT[tb][:, k, :], start=(k == 0), stop=(k == 7)),
                              reads=[wb, hT[tb]], writes=[p], inc=(k == 7))
                    fn((t0 + s0) // 128, tb, p, w)

    def conv_chunk(convin, acc, cwt, cbt, cc, nseq, L, dst_ap_fn, dst_buf):
        for s_ in range(nseq):
            a = acc[:, s_ * L:(s_ + 1) * L]
            kb.scale(a, convin[:, s_, 0:L], cwt[:, cc, 0:1], reads=[convin, cwt], writes=[acc])
            for tap in range(1, 5):
                kb.op("dve", lambda e: e.scalar_tensor_tensor(out=a, in0=convin[:, s_, tap:tap + L], scalar=cwt[:, cc, tap:tap + 1], in1=a,
                                                              op0=ALU.mult, op1=ALU.add), reads=[convin, cwt, acc], writes=[acc])
            if isinstance(dst_buf, list):
                for gg in range(2):
                    kb.op("act", lambda e: e.activation(out=dst_buf[gg][gg * 64:(gg + 1) * 64, s_ * L:(s_ + 1) * L], in_=acc[gg * 64:(gg + 1) * 64, s_ * L:(s_ + 1) * L],
                                                        func=AF.Silu, bias=cbt[gg * 64:(gg + 1) * 64, cc:cc + 1]), reads=[acc, cbt], writes=[dst_buf[gg]])
            else:
                kb.op("act", lambda e: e.activation(out=dst_ap_fn(s_), in_=a, func=AF.Silu, bias=cbt[:, cc:cc + 1]), reads=[acc, cbt], writes=[dst_buf])

    def evac_conv_in(convin, p, tb, nseq, L):
        if nseq == 1:
            kb.copy(convin[:, 0, 2 + tb * 512:2 + (tb + 1) * 512], p[:, :], reads=[p], writes=[convin])
        else:
            for s_ in range(2):
                kb.copy(convin[:, s_, 2:2 + L], p[:, s_ * L:(s_ + 1) * L], reads=[p], writes=[convin])

    def transpose_to_tm(src, dst_fn, dst_buf, T):
        for t0 in range(0, T, 4):
            p = ps()
            for j in range(4):
                kb.op("pe", lambda e: e.matmul(p[:, j * 128:(j + 1) * 128], lhsT=src[:, (t0 + j) * 128:(t0 + j + 1) * 128], rhs=ident_b, start=True, stop=True),
                      reads=[src, cb], writes=[p], inc=(j == 3))
            kb.copy(dst_fn(t0, 4), p[:, :].rearrange("p (t c) -> p t c", t=4), reads=[p], writes=[dst_buf])

    def ssd(l, g, nblk):
        kb.label = 'ssd_g%d' % g
        sample = (g == 1)
        L = 2048 if sample else 256
        nseq = 1 if sample else 2
        lt = L // 128
        ntok = nblk * 512
        T = ntok // 128
        blocks = list(range(nblk))
        with scope() as st:
            xtm = alloc(st, "xtm", [128, T, 512], BF16)
            Btm = alloc(st, "Btm", [128, T, 128], BF16)
            BT = alloc(st, "BT", [128, ntok], BF16)
            CTz = [alloc(st, "CT%d" % i, [128, ntok], BF16) for i in range(2)]
            for i_ in range(2):
                kb.op("dve", lambda e: e.memset(CTz[i_][:], 0.0), writes=[CTz[i_]])
            dtt = alloc(st, "dtt", [128, T, 16], F32)
            dtA = alloc(st, "dtA", [128, T, 16], F32)
            cum = alloc(st, "cum", [128, T, 16], F32)
            tot = alloc(st, "tot", [128, T, 16], F32)
            ecum = alloc(st, "ecum", [128, T, 16], F32)
            wd = alloc(st, "wd", [128, T, 16], F32)
            bj = alloc(st, "bj", [128, T, 16], F32)
            dec = alloc(st, "dec", [128, T, 2, 4], F32)
            abc = alloc(st, "abc", [128, 16], F32)
            dtb = alloc(st, "dtb", [128, 16], F32)
            dsk = alloc(st, "dsk", [128, 8], F32)
            ng = alloc(st, "ng", [128, 512], F32)
            cwt = alloc(st, "cwt", [128, 6, 5], F32)
            cbt = alloc(st, "cbt", [128, 6], F32)
            load_wo(l, 0)
            kb.dma("sp", abc[:], D["ssd_a_log"][l].partition_broadcast(128), writes=[abc])
            kb.dma("sp", dtb[:], D["ssd_dt_bias"][l].partition_broadcast(128), writes=[dtb])
            kb.dma("sp", dsk[:], D["ssd_d"][l].partition_broadcast(128), writes=[dsk])
            kb.dma("sp", ng[:], D["ssd_norm"][l].partition_broadcast(128), writes=[ng])
            for tap in range(5):
                kb.dma("sp", cwt[:, :, tap], D["conv_ssd_w"][l, tap].rearrange("(c p) -> p c", p=128), writes=[cwt], allow_slow_non_contiguous=True)
            kb.dma("sp", cbt[:], D["conv_ssd_b"][l].rearrange("(c p) -> p c", p=128), writes=[cbt], allow_slow_non_contiguous=True)
            kb.op("act", lambda e: e.activation(out=abc[:], in_=abc[:], func=AF.Exp), reads=[abc], writes=[abc])
            kb.op("dve", lambda e: e.tensor_scalar(out=abc[:], in0=abc[:], scalar1=-1.0, scalar2=None, op0=ALU.mult), reads=[abc], writes=[abc])
            with scope() as st1:
                convins = [alloc(st1, "convin", [128, nseq, L + 4], F32) for _ in range(2)]
                acc = alloc(st1, "cacc", [128, ntok], F32)
                xcT = alloc(st1, "xcT", [128, ntok], BF16)
                for cv_ in convins:
                    kb.op("dve", lambda e: e.memset(cv_[:], 0.0), writes=[cv_])

                def cb_fn(ci, tb, p, w):
                    convin = convins[ci % 2]
                    evac_conv_in(convin, p, tb, nseq, L)
                    if tb != blocks[-1]:
                        return
                    if ci < 4:
                        conv_chunk(convin, acc, cwt, cbt, ci, nseq, L, lambda s_: xcT[:, s_ * L:(s_ + 1) * L], xcT)
                        transpose_to_tm(xcT, lambda t0, n: xtm[:, t0:t0 + n, ci * 128:(ci + 1) * 128], xtm, T)
                    elif ci == 4:
                        conv_chunk(convin, acc, cwt, cbt, ci, nseq, L, lambda s_: BT[:, s_ * L:(s_ + 1) * L], BT)
                        transpose_to_tm(BT, lambda t0, n: Btm[:, t0:t0 + n, :], Btm, T)
                    else:
                        conv_chunk(convin, acc, cwt, cbt, ci, nseq, L, None, CTz)
                proj_fm(l, 512, 768, blocks, cb_fn)
            kb.barrier()
            if SSD_PH < 2:
                return
            wb, wv = wload([(D["w_in"][l][:, 1280:1296], 16)], 8)
            for tt in range(T):
                p = ps()
                proj_tm(wb, wv, 0, 16, tt // 4, tt % 4, p)
                kb.op("dve", lambda e: e.tensor_tensor(out=dtt[:, tt, :], in0=p[:, 0:16], in1=dtb[:], op=ALU.add), reads=[p, dtb], writes=[dtt])
            kb.op("act", lambda e: e.activation(out=dtt[:], in_=dtt[:], func=AF.Exp), reads=[dtt], writes=[dtt])
            kb.op("act", lambda e: e.activation(out=dtt[:], in_=dtt[:], func=AF.Ln, bias=1.0), reads=[dtt], writes=[dtt])
            kb.op("dve", lambda e: e.tensor_tensor(out=dtA[:], in0=dtt[:], in1=bcast(abc[:], [128, T, 16], 1), op=ALU.mult), reads=[dtt, abc], writes=[dtA])
            for tt in range(T):
                p = ps()
                kb.op("pe", lambda e: e.matmul(p[:, 0:16], lhsT=triF_f, rhs=dtA[:, tt, :], start=True, stop=True), reads=[cf, dtA], writes=[p], inc=False)
                kb.op("pe", lambda e: e.matmul(p[:, 16:32], lhsT=triB_f, rhs=dtA[:, tt, :], start=True, stop=True), reads=[cf, dtA], writes=[p], inc=False)
                kb.op("pe", lambda e: e.matmul(p[:, 32:48], lhsT=ones_f, rhs=dtA[:, tt, :], start=True, stop=True), reads=[cf, dtA], writes=[p])
                kb.op("dve", lambda e: e.tensor_copy(out=cum[:, tt, 0:8], in_=p[:, 0:8]), reads=[p], writes=[cum])
                kb.op("dve", lambda e: e.tensor_copy(out=cum[:, tt, 8:16], in_=p[:, 24:32]), reads=[p], writes=[cum])
                kb.op("dve", lambda e: e.tensor_copy(out=tot[:, tt, :], in_=p[:, 32:48]), reads=[p], writes=[tot])
            kb.op("act", lambda e: e.activation(out=ecum[:], in_=cum[:], func=AF.Exp), reads=[cum], writes=[ecum])
            kb.op("dve", lambda e: e.tensor_tensor(out=wd[:], in0=tot[:], in1=cum[:], op=ALU.subtract), reads=[tot, cum], writes=[wd])
            kb.op("act", lambda e: e.activation(out=wd[:], in_=wd[:], func=AF.Exp), reads=[wd], writes=[wd])
            kb.op("dve", lambda e: e.tensor_tensor(out=wd[:], in0=wd[:], in1=dtt[:], op=ALU.mult), reads=[wd, dtt], writes=[wd])
            kb.op("act", lambda e: e.activation(out=bj[:], in_=dtt[:], func=AF.Ln), reads=[dtt], writes=[bj])
            kb.op("dve", lambda e: e.tensor_tensor(out=bj[:], in0=bj[:], in1=cum[:], op=ALU.subtract), reads=[bj, cum], writes=[bj])
            tot4 = tot[:].rearrange("p t (d h) -> p t d h", d=2)
            for gg in range(2):
                kb.op("act", lambda e: e.activation(out=dec[gg * 64:(gg + 1) * 64], in_=tot4[gg * 64:(gg + 1) * 64, :, :, gg * 4:(gg + 1) * 4], func=AF.Exp),
                      reads=[tot], writes=[dec])
            if SSD_PH < 3:
                kb.barrier()
                return
            with scope() as st2:
                Hprev = alloc(st2, "Hprev", [128, T, 2, 256], BF16)
                st3 = ExitStack()
                Hs = [alloc(st3, "Hs%d" % i, [128, 256], F32) for i in range(2)]
                xws = [alloc(st3, "xw%d" % i, [128, 512], BF16) for i in range(2)]
                hx = alloc(st3, "hx", [128, 2, 128], F32)
                ho = alloc(st3, "ho", [128, 128], F32)
                xi = 0
                for dr in range(2):
                    H = Hs[dr]
                    for s_ in range(nseq):
                        if sample:
                            for blk in range(2):
                                for two in range(2):
                                    kb.dma("sp", hx[two * 64:(two + 1) * 64, blk, :].rearrange("p (g n) -> p g n", g=2),
                                           D["sssm"][l, dr].rearrange("(g r) p n -> r p g n", g=2)[blk * 2 + two], writes=[hx])
                            for blk in range(2):
                                p = ps()
                                kb.op("pe", lambda e: e.matmul(p[:, 0:128], lhsT=hx[:, blk, :], rhs=ident_f, start=True, stop=True), reads=[hx, cf], writes=[p])
                                kb.copy(H[:, blk * 128:(blk + 1) * 128], p[:, 0:128], reads=[p], writes=[H])
                        else:
                            kb.op("dve", lambda e: e.memset(H[:], 0.0), writes=[H])
                        order = range(lt) if dr == 0 else range(lt - 1, -1, -1)
                        for r0 in order:
                            tt = s_ * lt + r0
                            kb.copy(Hprev[:, tt, dr, :], H[:], reads=[H], writes=[Hprev])
                            xw = xws[xi % 2]
                            xi += 1
                            kb.op("dve", lambda e: e.tensor_tensor(out=xw[:].rearrange("p (h d) -> p h d", d=64), in0=xtm[:, tt, :].rearrange("p (h d) -> p h d", d=64),
                                                                   in1=bcast(wd[:, tt, dr * 8:(dr + 1) * 8], [128, 8, 64], 2), op=ALU.mult), reads=[xtm, wd], writes=[xw])
                            p = ps()
                            kb.op("pe", lambda e: e.matmul(p[:, :], lhsT=Btm[:, tt, :], rhs=xw[:], start=True, stop=True), reads=[Btm, xw], writes=[p])
                            kb.op("dve", lambda e: e.tensor_tensor(out=H[:].rearrange("p (h d) -> p h d", d=64), in0=H[:].rearrange("p (h d) -> p h d", d=64),
                                                                   in1=bcast(dec[:, tt, dr, :], [128, 4, 64], 2), op=ALU.mult), reads=[H, dec], writes=[H])
                            for gg in range(2):
                                kb.op("dve", lambda e: e.tensor_tensor(out=H[gg * 64:(gg + 1) * 64, :], in0=H[gg * 64:(gg + 1) * 64, :],
                                                                       in1=p[gg * 64:(gg + 1) * 64, gg * 256:(gg + 1) * 256], op=ALU.add), reads=[H, p], writes=[H])
                        if not sample:
                            for blk in range(2):
                                p = ps()
                                kb.op("pe", lambda e: e.matmul(p[:, 0:128], lhsT=H[:, blk * 128:(blk + 1) * 128], rhs=ident_f, start=True, stop=True), reads=[H, cf], writes=[p])
                                kb.copy(ho[:], p[:, 0:128], reads=[p], writes=[ho])
                                for two in range(2):
                                    kb.dma("sp", D["nssm"][s_, l, dr].rearrange("(g r) p n -> r p g n", g=2)[blk * 2 + two],
                                           ho[two * 64:(two + 1) * 64, :].rearrange("p (g n) -> p g n", g=2), reads=[ho])
                kb.barrier()
                st3.close()
                if SSD_PH < 4:
                    return
                Dg = alloc(st2, "Dg", [128, 16, 128], F32)
                Es = [alloc(st2, "E%d" % i, [128, 128], F32) for i in range(3)]
                Ms = [alloc(st2, "M%d" % i, [128, 128], BF16) for i in range(3)]
                ya = alloc(st2, "ya", [128, 512], F32)
                yu = alloc(st2, "yu", [128, 512], F32)
                zs = yu
                ytm = alloc(st2, "ytm", [128, 4, 512], BF16)
                yT = alloc(st2, "yT", [128, 4, 512], BF16)
                ss = alloc(st2, "ss", [128, 4], F32)
                wzb, wzv = wload([(D["w_in"][l][:, 0:512], 512)], 8)
                ei = 0
                for tt in range(T):
                    tk = slice(tt * 128, (tt + 1) * 128)
                    pz = ps_acc()
                    proj_tm(wzb, wzv, 0, 512, tt // 4, tt % 4, pz)
                    pBC = ps_acc()
                    for gg in range(2):
                        kb.op("pe", lambda e: e.matmul(pBC[:, gg * 128:(gg + 1) * 128], lhsT=BT[:, tk], rhs=CTz[gg][:, tk], start=True, stop=True),
                              reads=[BT, CTz[gg]], writes=[pBC], inc=(gg == 1))
                    kb.op("dve", lambda e: e.tensor_tensor(out=Dg[:], in0=bcast(ident_f, [128, 16, 128], 1), in1=bcast(cum[:, tt, :], [128, 16, 128], 2), op=ALU.mult),
                          reads=[cf, cum], writes=[Dg])
                    yint = ps_acc()
                    hd = [(h, dr) for h in range(8) for dr in range(2)]
                    mts = {}

                    def sA(i):
                        h, dr = hd[i]
                        pE = ps()
                        kb.op("pe", lambda e: e.matmul(pE[:, 0:128], lhsT=ones_f, rhs=Dg[:, dr * 8 + h, :], start=True, stop=False), reads=[cf, Dg], writes=[pE], inc=False)
                        kb.op("pe", lambda e: e.matmul(pE[:, 0:128], lhsT=ident_b, rhs=(maskF_b if dr == 0 else maskB_b), start=False, stop=True), reads=[cb], writes=[pE])
                        E = Es[i % 3]
                        M = Ms[i % 3]
                        kb.op("act", lambda e: e.activation(out=E[:], in_=pE[:, 0:128], func=AF.Exp, bias=bj[:, tt, dr * 8 + h:dr * 8 + h + 1]), reads=[pE, bj], writes=[E])
                        gg = h // 4
                        kb.op("dve", lambda e: e.tensor_tensor(out=M[:], in0=E[:], in1=pBC[:, gg * 128:(gg + 1) * 128], op=ALU.mult), reads=[E, pBC], writes=[M])
                        mts[i] = M

                    def sC(i):
                        h, dr = hd[i]
                        M = mts.pop(i)
                        kb.op("pe", lambda e: e.matmul(yint[:, h * 64:(h + 1) * 64], lhsT=M[:], rhs=xtm[:, tt, h * 64:(h + 1) * 64], start=(dr == 0), stop=(dr == 1)),
                              reads=[M, xtm], writes=[yint], inc=True)

                    for i in range(16 + 2):
                        if i < 16:
                            sA(i)
                        if i >= 2:
                            sC(i - 2)
                    if SSD_PH < 5:
                        continue
                    pY = [ps(), ps()]
                    for dr in range(2):
                        for gg in range(2):
                            kb.op("pe", lambda e: e.matmul(pY[dr][:, gg * 256:(gg + 1) * 256], lhsT=CTz[gg][:, tk], rhs=Hprev[:, tt, dr, :], start=True, stop=True),
                                  reads=[CTz[gg], Hprev], writes=[pY[dr]], inc=(gg == 1))
                    v3 = lambda ap: ap.rearrange("p (h d) -> p h d", d=64)
                    kb.op("dve", lambda e: e.tensor_tensor(out=v3(ya[:]), in0=v3(xtm[:, tt, :]), in1=bcast(dsk[:], [128, 8, 64], 2), op=ALU.mult), reads=[xtm, dsk], writes=[ya])
                    kb.op("dve", lambda e: e.tensor_tensor(out=ya[:], in0=ya[:], in1=yint[:, :], op=ALU.add), reads=[ya, yint], writes=[ya])
                    for dr in range(2):
                        kb.op("dve", lambda e: e.tensor_tensor(out=v3(yu[:]), in0=v3(pY[dr][:, :]), in1=bcast(ecum[:, tt, dr * 8:(dr + 1) * 8], [128, 8, 64], 2), op=ALU.mult),
                              reads=[pY[dr], ecum], writes=[yu])
                        kb.op("dve", lambda e: e.tensor_tensor(out=ya[:], in0=ya[:], in1=yu[:], op=ALU.add), reads=[ya, yu], writes=[ya])
                    if SSD_PH < 6:
                        continue
                    kb.op("act", lambda e: e.activation(out=zs[:], in_=pz[:, :], func=AF.Silu), reads=[pz], writes=[zs])
                    kb.op("dve", lambda e: e.tensor_tensor(out=ya[:], in0=ya[:], in1=zs[:], op=ALU.mult), reads=[ya, zs], writes=[ya])
                    kb.op("dve", lambda e: e.memset(ss[:, 0:1], 0.0), writes=[ss])
                    kb.op("act", lambda e: e.activation(out=yu[:], in_=ya[:], func=AF.Square, accum_out=ss[:, 0:1]), reads=[ya, ss], writes=[yu, ss])
                    kb.op("dve", lambda e: e.tensor_scalar(out=ss[:, 1:2], in0=ss[:, 0:1], scalar1=1.0 / 512, scalar2=EPS, op0=ALU.mult, op1=ALU.add), reads=[ss], writes=[ss])
                    kb.op("act", lambda e: e.activation(out=ss[:, 1:2], in_=ss[:, 1:2], func=AF.Ln), reads=[ss], writes=[ss])
                    kb.op("act", lambda e: e.activation(out=ss[:, 2:3], in_=ss[:, 1:2], func=AF.Exp, scale=-0.5), reads=[ss], writes=[ss])
                    kb.op("dve", lambda e: e.scalar_tensor_tensor(out=ytm[:, tt % 4, :], in0=ya[:], scalar=ss[:, 2:3], in1=ng[:], op0=ALU.mult, op1=ALU.mult),
                          reads=[ya, ss, ng], writes=[ytm])
                    if SSD_PH < 7:
                        continue
                    if tt % 4 == 3:
                        mixer_out(st2, ytm, tt // 4, yT)
        kb.barrier()


    def mlstm(l, g, nblk):
        kb.label = 'mlstm_g%d' % g
        sample = (g == 1)
        L = 2048 if sample else 256
        nseq = 1 if sample else 2
        lt = L // 128
        ntok = nblk * 512
        T = ntok // 128
        blocks = list(range(nblk))
        C0 = 2832
        lns = math.log(128 ** -0.5)
        for hg in range(2):
            h0 = hg * 2
            with scope() as st:
                qT = alloc(st, "mqT", [128, 2, ntok], BF16)
                kT = alloc(st, "mkT", [128, 2, ntok], BF16)
                vaug = alloc(st, "mvaug", [128, T, 2, 129], BF16)
                Cpb = alloc(st, "Cpb", [128, T, 2, 129], BF16)
                li = alloc(st, "li", [128, T, 4], F32)
                lf = alloc(st, "lf", [128, T, 4], F32)
                G = alloc(st, "G", [128, T, 4], F32)
                tot = alloc(st, "mtot", [128, T, 4], F32)
                pj = alloc(st, "pj", [128, T, 4], F32)
                pjs = alloc(st, "pjs", [128, T, 4], F32)
                gend = alloc(st, "gend", [128, T, 4], F32)
                mlb = alloc(st, "mlb", [128, T, 4], F32)
                wend = alloc(st, "wend", [128, T, 4], F32)
                mprev = alloc(st, "mprev", [128, T, 4], F32)
                gb = alloc(st, "gb", [128, 8], F32)
                cwt = alloc(st, "mcwt", [128, 4, 5], F32)
                cbt = alloc(st, "mcbt", [128, 4], F32)
                ng = alloc(st, "mng", [128, 256], F32)
                kb.dma("pool", wo_buf[:, 0:2, :], D["w_out"][l][1024 + h0 * 128:1024 + (h0 + 2) * 128, :].rearrange("(k p) n -> p k n", p=128), writes=[wo_buf])
                kb.dma("sp", ng[:], D["mlstm_norm"][l][h0 * 128:(h0 + 2) * 128].partition_broadcast(128), writes=[ng])
                goffs = [0 * 8 + 0 * 4 + h0, 1 * 8 + 0 * 4 + h0, 0 * 8 + 1 * 4 + h0, 1 * 8 + 1 * 4 + h0]
                for i_, go in enumerate(goffs):
                    kb.dma("sp", gb[:, i_ * 2:(i_ + 1) * 2], D["mlstm_gate_b"][l][go:go + 2].partition_broadcast(128), writes=[gb])
                for ci, ch0 in enumerate((h0 * 128, (h0 + 1) * 128, 512 + h0 * 128, 512 + (h0 + 1) * 128)):
                    for tap in range(5):
                        kb.dma("sp", cwt[:, ci, tap:tap + 1], D["conv_mlstm_w"][l, tap][ch0:ch0 + 128].rearrange("(p o) -> p o", o=1), writes=[cwt])
                    kb.dma("sp", cbt[:, ci:ci + 1], D["conv_mlstm_b"][l][ch0:ch0 + 128].rearrange("(p o) -> p o", o=1), writes=[cbt])
                kb.op("dve", lambda e: e.memset(vaug[:, :, :, 128:129], 1.0), writes=[vaug])
                with scope() as st1:
                    convins = [alloc(st1, "mconvin", [128, nseq, L + 4], F32) for _ in range(2)]
                    acc = alloc(st1, "mcacc", [128, ntok], F32)
                    for cv_ in convins:
                        kb.op("dve", lambda e: e.memset(cv_[:], 0.0), writes=[cv_])
                    wb, wv = wload([(D["w_in"][l][:, C0 + h0 * 128:C0 + (h0 + 2) * 128], 256),
                                    (D["w_in"][l][:, C0 + 512 + h0 * 128:C0 + 512 + (h0 + 2) * 128], 256)], 8)
                    for ci in range(4):
                        for tb in blocks:
                            p = ps()
                            for k in range(8):
                                kb.op("pe", lambda e: e.matmul(p[:, :], lhsT=wv[:, k, ci * 128:(ci + 1) * 128], rhs=hT[tb][:, k, :], start=(k == 0), stop=(k == 7)),
                                      reads=[wb, hT[tb]], writes=[p], inc=(k == 7))
                            evac_conv_in(convins[ci % 2], p, tb, nseq, L)
                        convin = convins[ci % 2]
                        dstb = qT if ci < 2 else kT
                        conv_chunk(convin, acc, cwt, cbt, ci, nseq, L, lambda s_: dstb[:, ci % 2, s_ * L:(s_ + 1) * L], dstb)
                kb.barrier()
                wb, wv = wload([(D["w_in"][l][:, C0 + 1024 + h0 * 128:C0 + 1024 + (h0 + 2) * 128], 256)], 8)
                for tt in range(T):
                    p = ps()
                    proj_tm(wb, wv, 0, 256, tt // 4, tt % 4, p)
                    kb.copy(vaug[:, tt, :, 0:128], p[:, 0:256].rearrange("p (h e) -> p h e", e=128), reads=[p], writes=[vaug])
                gc = C0 + 2048
                wb, wv = wload([(D["w_in"][l][:, gc + go:gc + go + 2], 2) for go in goffs], 8)
                for tt in range(T):
                    p = ps()
                    proj_tm(wb, wv, 0, 8, tt // 4, tt % 4, p)
                    kb.op("dve", lambda e: e.tensor_tensor(out=li[:, tt, :], in0=p[:, 0:4], in1=gb[:, 0:4], op=ALU.add), reads=[p, gb], writes=[li])
                    kb.op("dve", lambda e: e.tensor_tensor(out=lf[:, tt, :], in0=p[:, 4:8], in1=gb[:, 4:8], op=ALU.add), reads=[p, gb], writes=[lf])
                kb.op("act", lambda e: e.activation(out=lf[:], in_=lf[:], func=AF.Exp, scale=-1.0), reads=[lf], writes=[lf])
                kb.op("act", lambda e: e.activation(out=lf[:], in_=lf[:], func=AF.Ln, bias=1.0), reads=[lf], writes=[lf])
                kb.op("dve", lambda e: e.tensor_scalar(out=lf[:], in0=lf[:], scalar1=-1.0, scalar2=None, op0=ALU.mult), reads=[lf], writes=[lf])
                for tt in range(T):
                    p = ps()
                    kb.op("pe", lambda e: e.matmul(p[:, 0:4], lhsT=triF_f, rhs=lf[:, tt, :], start=True, stop=True), reads=[cf, lf], writes=[p], inc=False)
                    kb.op("pe", lambda e: e.matmul(p[:, 4:8], lhsT=triB_f, rhs=lf[:, tt, :], start=True, stop=True), reads=[cf, lf], writes=[p], inc=False)
                    kb.op("pe", lambda e: e.matmul(p[:, 8:12], lhsT=ones_f, rhs=lf[:, tt, :], start=True, stop=True), reads=[cf, lf], writes=[p])
                    kb.op("dve", lambda e: e.tensor_copy(out=G[:, tt, 0:2], in_=p[:, 0:2]), reads=[p], writes=[G])
                    kb.op("dve", lambda e: e.tensor_copy(out=G[:, tt, 2:4], in_=p[:, 6:8]), reads=[p], writes=[G])
                    kb.op("dve", lambda e: e.tensor_copy(out=tot[:, tt, :], in_=p[:, 8:12]), reads=[p], writes=[tot])
                kb.op("dve", lambda e: e.tensor_tensor(out=pj[:], in0=li[:], in1=G[:], op=ALU.subtract), reads=[li, G], writes=[pj])
                kb.op("dve", lambda e: e.tensor_scalar(out=pjs[:], in0=pj[:], scalar1=lns, scalar2=None, op0=ALU.add), reads=[pj], writes=[pjs])
                kb.op("dve", lambda e: e.tensor_tensor(out=gend[:], in0=pj[:], in1=tot[:], op=ALU.add), reads=[pj, tot], writes=[gend])
                if True:
                    mrows = [alloc(st, "mrow8", [4, 1], F32) for _ in range(2)]
                    d8s = [alloc(st, "d8", [4, 4], F32) for _ in range(2)]
                    for tt in range(T):
                        mrow, d8 = mrows[tt % 2], d8s[tt % 2]
                        p = ps()
                        kb.op("pe", lambda e: e.matmul(p[0:4, 0:128], lhsT=gend[:, tt, :], rhs=ident_f, start=True, stop=True), reads=[gend, cf], writes=[p])
                        kb.op("dve", lambda e: e.tensor_reduce(out=mrow[:], in_=p[0:4, 0:128], axis=AX.X, op=ALU.max), reads=[p], writes=[mrow])
                        kb.op("dve", lambda e: e.tensor_scalar(out=d8[:], in0=ident_f[0:4, 0:4], scalar1=mrow[:, 0:1], scalar2=None, op0=ALU.mult), reads=[cf, mrow], writes=[d8])
                        p2 = ps()
                        kb.op("pe", lambda e: e.matmul(p2[:, 0:4], lhsT=ones_f[0:4, :], rhs=d8[:], start=True, stop=True), reads=[cf, d8], writes=[p2])
                        kb.op("dve", lambda e: e.tensor_copy(out=mlb[:, tt, :], in_=p2[:, 0:4]), reads=[p2], writes=[mlb])
                kb.op("dve", lambda e: e.tensor_tensor(out=wend[:], in0=gend[:], in1=mlb[:], op=ALU.subtract), reads=[gend, mlb], writes=[wend])
                kb.op("act", lambda e: e.activation(out=wend[:], in_=wend[:], func=AF.Exp), reads=[wend], writes=[wend])
                with scope() as st2:
                    Cst = [alloc(st2, "Cst%d" % i, [128, 129], F32) for i in range(4)]
                    mp = alloc(st2, "mp", [128, 4], F32)
                    mt8 = alloc(st2, "mt8", [128, 16], F32)
                    kwts = [alloc(st2, "kwt", [128, 2, 128], BF16) for _ in range(2)]
                    Dgs = [alloc(st2, "mDg", [128, 4, 128], F32) for _ in range(2)]
                    Drs = [alloc(st2, "mDr", [128, 4, 128], F32) for _ in range(2)]
                    scs = [alloc(st2, "msc", [128, 40], F32) for _ in range(2)]
                    kwt, Dg, Dr, sc = kwts[0], Dgs[0], Drs[0], scs[0]
                    Es = [alloc(st2, "mE%d" % i, [128, 128], F32) for i in range(3)]
                    Ms = [alloc(st2, "mM%d" % i, [128, 128], BF16) for i in range(3)]
                    nds = [alloc(st2, "nd%d" % i, [128, 129], F32) for i in range(2)]
                    cbf = [alloc(st2, "cbf%d" % i, [128, 129], BF16) for i in range(2)]
                    hsums = [alloc(st2, "hsum", [128, 256], F32) for _ in range(2)]
                    hts = [alloc(st2, "mht", [128, 256], F32) for _ in range(2)]
                    sgs = [alloc(st2, "msg", [128, 256], F32) for _ in range(2)]
                    hsum, ht, sg = hsums[0], hts[0], sgs[0]
                    ytm = alloc(st2, "mytm", [128, 4, 256], BF16)
                    yT = alloc(st2, "myT", [128, 2, 512], BF16)
                    ei = 0
                    ei0 = [0]

                    def init_state(dr, s_):
                        for hh in range(2):
                            C = Cst[dr * 2 + hh]
                            if sample:
                                kb.dma("sp", C[:, 0:128], D["smc"][l, dr, h0 + hh], writes=[C])
                                kb.dma("sp", C[:, 128:129], D["smn"][l, dr, h0 + hh].rearrange("(p o) -> p o", o=1), writes=[C])
                            else:
                                kb.op("dve", lambda e: e.memset(C[:], 0.0), writes=[C])
                        if sample:
                            kb.dma("sp", mp[:, dr * 2:dr * 2 + 2], D["smm"][l][dr * 4 + h0:dr * 4 + h0 + 2].partition_broadcast(128), writes=[mp])
                        else:
                            kb.op("dve", lambda e: e.memset(mp[:, dr * 2:dr * 2 + 2], 0.0), writes=[mp])

                    def local_update(dr, tt):
                        cs = slice(dr * 2, dr * 2 + 2)
                        pk = ps()
                        for hh in range(2):
                            kb.op("pe", lambda e: e.matmul(pk[:, hh * 128:(hh + 1) * 128], lhsT=kT[:, hh, tt * 128:(tt + 1) * 128], rhs=ident_b, start=True, stop=True),
                                  reads=[kT, cb], writes=[pk], inc=(hh == 1))
                        kb.op("dve", lambda e: e.tensor_tensor(out=kwt[:], in0=pk[:, 0:256].rearrange("p (h d) -> p h d", d=128),
                                                               in1=bcast(wend[:, tt, cs], [128, 2, 128], 2), op=ALU.mult), reads=[pk, wend], writes=[kwt])
                        a = sc[:, 0:2]
                        mn = sc[:, 2:4]
                        sp_ = sc[:, 4:6]
                        sl_ = sc[:, 6:8]
                        kb.op("dve", lambda e: e.tensor_tensor(out=a, in0=tot[:, tt, cs], in1=mp[:, cs], op=ALU.add), reads=[tot, mp], writes=[sc])
                        kb.op("dve", lambda e: e.tensor_tensor(out=mn, in0=a, in1=mlb[:, tt, cs], op=ALU.max), reads=[sc, mlb], writes=[sc])
                        kb.op("dve", lambda e: e.tensor_tensor(out=sp_, in0=a, in1=mn, op=ALU.subtract), reads=[sc], writes=[sc])
                        kb.op("dve", lambda e: e.tensor_tensor(out=sl_, in0=mlb[:, tt, cs], in1=mn, op=ALU.subtract), reads=[sc, mlb], writes=[sc])
                        kb.op("act", lambda e: e.activation(out=sc[:, 4:8], in_=sc[:, 4:8], func=AF.Exp), reads=[sc], writes=[sc])
                        kb.op("dve", lambda e: e.tensor_copy(out=mp[:, cs], in_=mn), reads=[sc], writes=[mp])
                        for hh in range(2):
                            C = Cst[dr * 2 + hh]
                            pc = ps()
                            kb.op("pe", lambda e: e.matmul(pc[:, 0:129], lhsT=kwt[:, hh, :], rhs=vaug[:, tt, hh, :], start=True, stop=True), reads=[kwt, vaug], writes=[pc])
                            kb.scale(C[:], C[:], sc[:, 4 + hh:5 + hh], reads=[C, sc], writes=[C])
                            kb.op("dve", lambda e: e.scalar_tensor_tensor(out=C[:], in0=pc[:, 0:129], scalar=sc[:, 6 + hh:7 + hh], in1=C[:], op0=ALU.mult, op1=ALU.add),
                                  reads=[pc, sc, C], writes=[C])

                    def final_state(dr, s_):
                        for hh in range(2):
                            C = Cst[dr * 2 + hh]
                            kb.dma("sp", D["nmc"][s_, l, dr, h0 + hh], C[:, 0:128], reads=[C])
                            kb.dma("sp", D["nmn"][s_, l, dr, h0 + hh].rearrange("(p o) -> p o", o=1), C[:, 128:129], reads=[C])
                        kb.dma("sp", D["nmm"][s_, l:l + 1, dr * 4 + h0:dr * 4 + h0 + 2], mp[0:1, dr * 2:dr * 2 + 2], reads=[mp])

                    for s_ in range(nseq):
                        init_state(1, s_)
                        for r0 in range(lt - 1, -1, -1):
                            tt = s_ * lt + r0
                            kwt, sc = kwts[tt % 2], scs[tt % 2]
                            for hh in range(2):
                                kb.copy(Cpb[:, tt, hh, :], Cst[2 + hh][:], reads=[Cst[2 + hh]], writes=[Cpb])
                            kb.op("dve", lambda e: e.tensor_copy(out=mprev[:, tt, 2:4], in_=mp[:, 2:4]), reads=[mp], writes=[mprev])
                            local_update(1, tt)
                        if not sample:
                            final_state(1, s_)
                    wob, wov = wload([(D["w_in"][l][:, C0 + 1536 + h0 * 128:C0 + 1536 + (h0 + 2) * 128], 256)], 8)
                    for s_ in range(nseq):
                        init_state(0, s_)
                        for r0 in range(lt):
                            tt = s_ * lt + r0
                            tk = slice(tt * 128, (tt + 1) * 128)
                            kwt, Dg, Dr, sc = kwts[tt % 2], Dgs[tt % 2], Drs[tt % 2], scs[tt % 2]
                            hsum, ht, sg = hsums[tt % 2], hts[tt % 2], sgs[tt % 2]
                            kb.op("dve", lambda e: e.tensor_copy(out=mprev[:, tt, 0:2], in_=mp[:, 0:2]), reads=[mp], writes=[mprev])
                            kb.op("dve", lambda e: e.tensor_tensor(out=sc[:, 8:12], in0=G[:, tt, :], in1=mprev[:, tt, :], op=ALU.add), reads=[G, mprev], writes=[sc])
                            kb.op("dve", lambda e: e.tensor_tensor(out=Dg[:], in0=bcast(ident_f, [128, 4, 128], 1), in1=bcast(pj[:, tt, :], [128, 4, 128], 2), op=ALU.mult),
                                  reads=[cf, pj], writes=[Dg])
                            po = ps_acc()
                            proj_tm(wob, wov, 0, 256, tt // 4, tt % 4, po)
                            pSs = []
                            for hh in range(2):
                                pS = ps_acc()
                                kb.op("pe", lambda e: e.matmul(pS[:, 0:128], lhsT=kT[:, hh, tk], rhs=qT[:, hh, tk], start=True, stop=True), reads=[kT, qT], writes=[pS])
                                pSs.append(pS)
                            pms = []
                            for c in range(4):
                                dr = c // 2
                                pm = ps()
                                kb.op("pe", lambda e: e.matmul(pm[:, 0:128], lhsT=ones_f, rhs=Dg[:, c, :], start=True, stop=False), reads=[cf, Dg], writes=[pm], inc=False)
                                kb.op("pe", lambda e: e.matmul(pm[:, 0:128], lhsT=ident_b, rhs=(maskB_b if dr == 0 else maskF_b), start=False, stop=True), reads=[cb], writes=[pm])
                                pms.append(pm)
                            for c in range(4):
                                kb.op("dve", lambda e: e.tensor_reduce(out=sc[:, 12 + c:13 + c], in_=pms[c][:, 0:128], axis=AX.X, op=ALU.max), reads=[pms[c]], writes=[sc])
                            kb.op("dve", lambda e: e.tensor_tensor(out=sc[:, 12:16], in0=sc[:, 12:16], in1=G[:, tt, :], op=ALU.add), reads=[sc, G], writes=[sc])
                            kb.op("dve", lambda e: e.tensor_tensor(out=sc[:, 16:20], in0=sc[:, 12:16], in1=sc[:, 8:12], op=ALU.max), reads=[sc], writes=[sc])
                            kb.op("dve", lambda e: e.tensor_tensor(out=sc[:, 20:24], in0=G[:, tt, :], in1=sc[:, 16:20], op=ALU.subtract), reads=[sc, G], writes=[sc])
                            kb.op("dve", lambda e: e.tensor_tensor(out=sc[:, 24:28], in0=sc[:, 8:12], in1=sc[:, 16:20], op=ALU.subtract), reads=[sc], writes=[sc])
                            kb.op("act", lambda e: e.activation(out=sc[:, 24:28], in_=sc[:, 24:28], func=AF.Exp, bias=lns_col[:, 0:1]), reads=[sc, cf], writes=[sc])
                            kb.op("act", lambda e: e.activation(out=sc[:, 28:32], in_=sc[:, 16:20], func=AF.Exp, scale=-1.0), reads=[sc], writes=[sc])
                            kb.op("dve", lambda e: e.tensor_tensor(out=Dr[:], in0=bcast(ident_f, [128, 4, 128], 1), in1=bcast(sc[:, 20:24], [128, 4, 128], 2), op=ALU.mult),
                                  reads=[cf, sc], writes=[Dr])
                            items = [(0, 0), (0, 1), (1, 0), (1, 1)]
                            mts = {}

                            def mA(i):
                                hh, dr = items[i]
                                c = dr * 2 + hh
                                pW = ps()
                                kb.op("pe", lambda e: e.matmul(pW[:, 0:128], lhsT=ones_f, rhs=Dr[:, c, :], start=True, stop=False), reads=[cf, Dr], writes=[pW], inc=False)
                                kb.op("pe", lambda e: e.matmul(pW[:, 0:128], lhsT=ident_b, rhs=(maskF_b if dr == 0 else maskB_b), start=False, stop=True), reads=[cb], writes=[pW])
                                E = Es[(ei0[0] + i) % 3]
                                M = Ms[(ei0[0] + i) % 3]
                                kb.op("act", lambda e: e.activation(out=E[:], in_=pW[:, 0:128], func=AF.Exp, bias=pjs[:, tt, c:c + 1]), reads=[pW, pjs], writes=[E])
                                kb.op("dve", lambda e: e.tensor_tensor(out=M[:], in0=E[:], in1=pSs[hh][:, 0:128], op=ALU.mult), reads=[E, pSs[hh]], writes=[M])
                                if dr == 0:
                                    kb.copy(cbf[hh][:], Cst[hh][:], reads=[Cst[hh]], writes=[cbf[hh]])
                                mts[i] = M

                            def mC(i):
                                hh, dr = items[i]
                                c = dr * 2 + hh
                                M = mts.pop(i)
                                nd = nds[i % 2]
                                pN = ps()
                                kb.op("pe", lambda e: e.matmul(pN[:, 0:129], lhsT=M[:], rhs=vaug[:, tt, hh, :], start=True, stop=True), reads=[M, vaug], writes=[pN])
                                pI = ps()
                                if dr == 0:
                                    kb.op("pe", lambda e: e.matmul(pI[:, 0:129], lhsT=qT[:, hh, tk], rhs=cbf[hh][:], start=True, stop=True), reads=[qT, cbf[hh]], writes=[pI])
                                else:
                                    kb.op("pe", lambda e: e.matmul(pI[:, 0:129], lhsT=qT[:, hh, tk], rhs=Cpb[:, tt, hh, :], start=True, stop=True), reads=[qT, Cpb], writes=[pI])
                                kb.copy(nd[:], pN[:, 0:129], reads=[pN], writes=[nd])
                                kb.op("dve", lambda e: e.scalar_tensor_tensor(out=nd[:], in0=pI[:, 0:129], scalar=sc[:, 24 + c:25 + c], in1=nd[:], op0=ALU.mult, op1=ALU.add),
                                      reads=[pI, sc, nd], writes=[nd])
                                kb.op("dve", lambda e: e.tensor_scalar(out=sc[:, 32:33], in0=nd[:, 128:129], scalar1=-1.0, scalar2=None, op0=ALU.mult), reads=[nd], writes=[sc])
                                kb.op("dve", lambda e: e.tensor_tensor(out=sc[:, 32:33], in0=sc[:, 32:33], in1=nd[:, 128:129], op=ALU.max), reads=[nd, sc], writes=[sc])
                                kb.op("dve", lambda e: e.tensor_tensor(out=sc[:, 32:33], in0=sc[:, 32:33], in1=sc[:, 28 + c:29 + c], op=ALU.max), reads=[sc], writes=[sc])
                                kb.op("dve", lambda e: e.reciprocal(out=sc[:, 33:34], in_=sc[:, 32:33]), reads=[sc], writes=[sc])
                                if dr == 0:
                                    kb.scale(hsum[:, hh * 128:(hh + 1) * 128], nd[:, 0:128], sc[:, 33:34], reads=[nd, sc], writes=[hsum])
                                else:
                                    kb.op("dve", lambda e: e.scalar_tensor_tensor(out=hsum[:, hh * 128:(hh + 1) * 128], in0=nd[:, 0:128], scalar=sc[:, 33:34],
                                                                                  in1=hsum[:, hh * 128:(hh + 1) * 128], op0=ALU.mult, op1=ALU.add), reads=[nd, sc, hsum], writes=[hsum])

                            for i in range(4 + 2):
                                if i < 4:
                                    mA(i)
                                if i >= 2:
                                    mC(i - 2)
                            ei0[0] += 4
                            local_update(0, tt)
                            h3 = hsum[:].rearrange("p (h d) -> p h d", d=128)
                            t3 = ht[:].rearrange("p (h d) -> p h d", d=128)
                            kb.op("dve", lambda e: e.tensor_tensor(out=ht[:], in0=hsum[:], in1=hsum[:], op=ALU.mult), reads=[hsum], writes=[ht])
                            kb.op("dve", lambda e: e.tensor_reduce(out=sc[:, 34:36], in_=t3, axis=AX.X, op=ALU.add), reads=[ht], writes=[sc])
                            kb.op("dve", lambda e: e.tensor_scalar(out=sc[:, 34:36], in0=sc[:, 34:36], scalar1=1.0 / 128, scalar2=EPS, op0=ALU.mult, op1=ALU.add), reads=[sc], writes=[sc])
                            kb.op("act", lambda e: e.activation(out=sc[:, 34:36], in_=sc[:, 34:36], func=AF.Ln), reads=[sc], writes=[sc])
                            kb.op("act", lambda e: e.activation(out=sc[:, 36:38], in_=sc[:, 34:36], func=AF.Exp, scale=-0.5), reads=[sc], writes=[sc])
                            kb.op("dve", lambda e: e.tensor_tensor(out=t3, in0=h3, in1=bcast(sc[:, 36:38], [128, 2, 128], 2), op=ALU.mult), reads=[hsum, sc], writes=[ht])
                            kb.op("dve", lambda e: e.tensor_tensor(out=ht[:], in0=ht[:], in1=ng[:], op=ALU.mult), reads=[ht, ng], writes=[ht])
                            kb.op("act", lambda e: e.activation(out=sg[:], in_=po[:, 0:256], func=AF.Exp, scale=-1.0), reads=[po], writes=[sg])
                            kb.op("act", lambda e: e.activation(out=sg[:], in_=sg[:], func=AF.Ln, bias=1.0), reads=[sg], writes=[sg])
                            kb.op("act", lambda e: e.activation(out=sg[:], in_=sg[:], func=AF.Exp, scale=-1.0), reads=[sg], writes=[sg])
                            kb.op("dve", lambda e: e.tensor_tensor(out=ytm[:, tt % 4, :], in0=ht[:], in1=sg[:], op=ALU.mult), reads=[ht, sg], writes=[ytm])
                            if tt % 4 == 3:
                                tb = tt // 4
                                for c2 in range(2):
                                    p = ps()
                                    for tl in range(4):
                                        kb.op("pe", lambda e: e.matmul(p[:, tl * 128:(tl + 1) * 128], lhsT=ytm[:, tl, c2 * 128:(c2 + 1) * 128], rhs=ident_b, start=True, stop=True),
                                              reads=[ytm, cb], writes=[p], inc=(tl == 3))
                                    kb.copy(yT[:, c2, :], p[:, :], reads=[p], writes=[yT])
                                for c in range(8):
                                    p = ps()
                                    for k in range(2):
                                        kb.op("pe", lambda e: e.matmul(p[:, :], lhsT=wo_buf[:, k, c * 128:(c + 1) * 128], rhs=yT[:, k, :], start=(k == 0), stop=(k == 1)),
                                              reads=[wo_buf, yT], writes=[p], inc=(k == 1))
                                    kb.op("dve", lambda e: e.scalar_tensor_tensor(out=xT[tb][:, c, :], in0=p[:, :], scalar=modc[:, 2, c:c + 1], in1=xT[tb][:, c, :],
                                                                                  op0=ALU.mult, op1=ALU.add), reads=[p, modc, xT[tb]], writes=[xT[tb]])
                        if not sample:
                            final_state(0, s_)
            kb.barrier()

    def run_pass(g, src, dst, nblk):
        load_x(src, nblk)
        kb.barrier()
        for l in range(NL):
            kb.label = 'norm_g%d' % g
            load_mod(l, g)
            if g == 0 and l + 1 < NL:
                compute_mod(l + 1)
            with scope() as st:
                sq = [alloc(st, "sq", [128, 8, 512], BF16) for _ in range(2)]
                rstd = [alloc(st, "rstd", [128, 512], F32) for _ in range(2)]
                tmp2 = [alloc(st, "tmpn%d" % i, [128, 512], F32) for i in range(4)]
                for tb in range(nblk):
                    norm_block((sq, rstd, tmp2), tb, AB.t[:, 0, :], AB.t[:, 1, :], hT[tb])
            kb.barrier()
            for mk in MIXERS:
                if mk in "bd":
                    attention(l, g, mk, nblk)
                elif mk == "a":
                    ssd(l, g, nblk)
                elif mk == "c":
                    mlstm(l, g, nblk)
            with scope() as st:
                sq = [alloc(st, "sq", [128, 8, 512], BF16) for _ in range(2)]
                rstd = [alloc(st, "rstd", [128, 512], F32) for _ in range(2)]
                tmp2 = [alloc(st, "tmpn%d" % i, [128, 512], F32) for i in range(4)]
                for tb in range(nblk):
                    norm_block((sq, rstd, tmp2), tb, AB.t[:, 2, :], AB.t[:, 3, :], hT[tb])
            kb.barrier()
            for b0 in range(0, nblk, 2):
                ffn(l, list(range(b0, min(b0 + 2, nblk))))
        final_out(nblk, dst)

    run_pass(0, D["xp"], D["yp"], 1)
    run_pass(1, D["xs"], D["ys"], 4)

    kb.barrier(include_pool_dma=True)
    top.close()
    print("instructions:", kb.ninst, flush=True)
    if kb.stats is not None:
        tot = 0.0
        for lab, (mk, busy, nu, nfl, bub) in sorted(kb.stats.items(), key=lambda kv: -kv[1][0]):
            tot += mk
            print("  %-12s est_us=%8.0f units=%6d regions=%4d bubble_us=%6.0f busy: %s" % (lab, mk / 1e3, nu, nfl, bub / 1e3, " ".join("%s=%.0f" % (e_, v_ / 1e3) for e_, v_ in sorted(busy.items()))))
        print("  est total us", tot / 1e3)
    return nc


_CACHE = {}


def prep_inputs(inp):
    f = lambda a: np.ascontiguousarray(np.asarray(a, dtype=np.float32))
    consts = make_consts()
    rope = make_rope()
    shared = {}
    for name in ("w_ada", "b_ada", "norm1", "norm2", "w_in", "w_out", "conv_ssd_w", "conv_ssd_b", "ssd_d", "ssd_norm",
                 "diff_lq1", "diff_lk1", "diff_lq2", "diff_lk2", "conv_mlstm_w", "conv_mlstm_b", "mlstm_norm",
                 "gqa_q_norm", "gqa_k_norm", "w_ffn_in", "w_ffn_out", "norm_f"):
        shared[name] = f(inp[name])
    shared["ssd_a_log"] = f(inp["ssd_a_log"]).reshape(4, 16)
    shared["ssd_dt_bias"] = f(inp["ssd_dt_bias"]).reshape(4, 16)
    shared["mlstm_gate_b"] = f(inp["mlstm_gate_b"]).reshape(4, 16)
    shared["consts"] = consts
    shared["rope"] = rope
    xp = f(inp["x_prompt"])
    xs = f(inp["x_sample"])
    in_maps = []
    for c in range(8):
        b = c // 4
        m = dict(shared)
        m["xp"] = xp[2 * c:2 * c + 2].reshape(512, 1024)
        m["xs"] = xs[b]
        m["cvec"] = np.stack([f(inp["c_ctx"]), f(inp["c"])[b]], axis=0)
        m["cdk"] = f(inp["cache_diff_k"])[b].reshape(4, 256, 512)
        m["cdv"] = f(inp["cache_diff_v"])[b].reshape(4, 256, 512)
        m["cgk"] = f(inp["cache_gqa_k"])[b].reshape(4, 256, 128)
        m["cgv"] = f(inp["cache_gqa_v"])[b].reshape(4, 256, 128)
        m["sssm"] = f(inp["state_ssm"])[b]
        m["smc"] = f(inp["state_mlstm_c"])[b]
        m["smn"] = f(inp["state_mlstm_n"])[b]
        m["smm"] = f(inp["state_mlstm_m"])[b].reshape(4, 8)
        in_maps.append(m)
    if NL < 4:
        spec = dict(IN_SPECS)
        for m in in_maps:
            for k_ in list(m.keys()):
                if spec[k_][0] == 4 and len(spec[k_]) > 1:
                    m[k_] = np.ascontiguousarray(m[k_][:NL])
    return in_maps


def kernel(**inp):
    if "nc" not in _CACHE:
        _CACHE["nc"] = build_program()
    nc = _CACHE["nc"]
    in_maps = prep_inputs(inp)
    res = run_bass_kernel_spmd(nc, in_maps, core_ids=list(range(8)))
    return assemble(res.results)


def assemble(R):
    y_prompt = np.concatenate([R[c]["yp"].reshape(2, 256, 1024) for c in range(8)], axis=0)
    y_sample = np.stack([R[0]["ys"], R[4]["ys"]], axis=0)
    cat = lambda k: np.concatenate([R[c][k] for c in range(8)], axis=0)
    ndk = cat("ndk").reshape(16, 4, 256, 4, 2, 64)
    ndv = cat("ndv").reshape(16, 4, 256, 4, 128)
    ngk = cat("ngk").reshape(16, 4, 256, 2, 64)
    ngv = cat("ngv").reshape(16, 4, 256, 2, 64)
    nssm = cat("nssm")
    nmc = cat("nmc")
    nmn = cat("nmn")
    nmm = cat("nmm").reshape(16, 4, 2, 4)
    return (y_prompt, y_sample, ndk, ndv, ngk, ngv, nssm, nmc, nmn, nmm)
```
